# Optimizing a Trainium2 kernel written in Bass

```python
import math
import jax
import jax.numpy as jnp
from jax import lax
import numpy as np

D_MODEL = 1024
BATCH = 32
SEQ = 256
DEPTH = 4
DEC_BATCH = 2
DEC_SEQ = 2048
PAST_LEN = 256

GRID_W = 64
N_MIXERS = 4
EXPAND = 2
E = EXPAND * D_MODEL
EPS = 1e-6
SSD_HEAD_DIM = 64
SSD_HEADS = E // SSD_HEAD_DIM
SSD_GROUPS = 8
SSD_STATE = 128
SSD_CHUNK = 128
CONV_W = 5
SSD_GN = SSD_GROUPS * SSD_STATE
SSD_CONV_CH = E + 2 * SSD_GN
SSD_IN = E + SSD_CONV_CH + 2 * SSD_HEADS
MLP_CHUNK = 128
MLP_GROUPS = 8
S5_GROUP = 16
S5_GROUPS = E // S5_GROUP
S5_STATE = 64
ATT_HEAD_DIM = 64
ATT_HEADS = E // ATT_HEAD_DIM
WIN_ROWS = 8
WIN_COLS = 16
ATT_BLOCK = 128
N_SSD = (DEPTH + 3) // 4
N_MLP = (DEPTH + 2) // 4
N_S5 = (DEPTH + 1) // 4
N_NAT = DEPTH // 4

kernel_name = 'hybrid_ssd_gmlp_s5_natten_diffusion_step'


def rms_norm(x, g):
    xf = x.astype(jnp.float32)
    y = xf * lax.rsqrt(jnp.mean(xf * xf, axis=-1, keepdims=True) + EPS)
    return (y * g.astype(jnp.float32)).astype(x.dtype)


def layer_norm(x, g, b):
    xf = x.astype(jnp.float32)
    xc = xf - jnp.mean(xf, axis=-1, keepdims=True)
    y = xc * lax.rsqrt(jnp.mean(xc * xc, axis=-1, keepdims=True) + EPS)
    return (y * g.astype(jnp.float32) + b.astype(jnp.float32)).astype(x.dtype)


def adaln(x, g, mod):
    shift, scale, gate = jnp.split(mod, 3, axis=-1)
    return rms_norm(x, g) * (1.0 + scale) + shift, gate


def dw_conv(x, w, bias):
    k = w.shape[0]
    l = x.shape[1]
    pad = k // 2
    xp = jnp.pad(x, ((0, 0), (pad, k - 1 - pad), (0, 0)))
    acc = xp[:, 0:l] * w[0]
    for j in range(1, k):
        acc = acc + xp[:, j:j + l] * w[j]
    return acc + bias


def segsum(x):
    t = x.shape[-1]
    xr = jnp.broadcast_to(x[..., :, None], x.shape + (t,))
    xr = jnp.where(jnp.tril(jnp.ones((t, t), bool), -1), xr, 0.0)
    ss = jnp.cumsum(xr, axis=-2)
    return jnp.where(jnp.tril(jnp.ones((t, t), bool)), ss, -jnp.inf)


def ssd_scan(x, dt, a, bmat, cmat, h0):
    b, l, h, p = x.shape
    g, n = bmat.shape[2], bmat.shape[3]
    r = h // g
    nc = l // SSD_CHUNK
    xd = (x * dt[..., None]).reshape(b, nc, SSD_CHUNK, g, r, p)
    da = (dt * a).astype(jnp.float32).reshape(b, nc, SSD_CHUNK, g, r)
    da = jnp.transpose(da, (0, 3, 4, 1, 2))
    bm = bmat.reshape(b, nc, SSD_CHUNK, g, n)
    cm = cmat.reshape(b, nc, SSD_CHUNK, g, n)
    a_cum = jnp.cumsum(da, axis=-1)
    decay_in = jnp.exp(segsum(da))
    y_diag = jnp.einsum('bclgn,bcsgn,bgrcls,bcsgrp->bclgrp', cm, bm, decay_in, xd)
    decay_st = jnp.exp(a_cum[..., -1:] - a_cum)
    states = jnp.einsum('bclgn,bgrcl,bclgrp->bcgrpn', bm, decay_st, xd)
    h0r = h0.reshape(b, g, r, p, n).astype(states.dtype)
    states = jnp.concatenate([h0r[:, None], states], axis=1)
    chunk_tot = jnp.pad(a_cum[..., -1], ((0, 0), (0, 0), (0, 0), (1, 0)))
    decay_ch = jnp.exp(segsum(chunk_tot))
    new_states = jnp.einsum('bgrzc,bcgrpn->bzgrpn', decay_ch, states)
    states, final = new_states[:, :-1], new_states[:, -1]
    y_off = jnp.einsum('bclgn,bcgrpn,bgrcl->bclgrp', cm, states, jnp.exp(a_cum))
    y = (y_diag + y_off).reshape(b, l, h, p)
    return y, final.reshape(b, h, p, n)


def ssd_mixer(u, conv_w, conv_b, dt_bias, a_log, d_skip, norm_g, h0):
    b, l, _ = u.shape
    z = u[..., :E]
    xbc = jax.nn.silu(dw_conv(u[..., E:E + SSD_CONV_CH], conv_w, conv_b))
    dt_raw = u[..., E + SSD_CONV_CH:].reshape(b, l, 2, SSD_HEADS)
    xs = xbc[..., :E].reshape(b, l, SSD_HEADS, SSD_HEAD_DIM)
    bm = xbc[..., E:E + SSD_GN].reshape(b, l, SSD_GROUPS, SSD_STATE)
    cm = xbc[..., E + SSD_GN:].reshape(b, l, SSD_GROUPS, SSD_STATE)
    dt = jax.nn.softplus((dt_raw + dt_bias).astype(jnp.float32))
    a = -jnp.exp(a_log.astype(jnp.float32))
    flip = lambda t: jnp.flip(t, axis=1)
    y_f, h_f = ssd_scan(xs, dt[:, :, 0], a[0], bm, cm, h0[:, 0])
    y_b, h_b = ssd_scan(flip(xs), flip(dt[:, :, 1]), a[1], flip(bm), flip(cm), h0[:, 1])
    y = y_f + flip(y_b) + d_skip[:, None] * xs
    y = rms_norm(y.reshape(b, l, E) * jax.nn.silu(z), norm_g)
    return y.astype(u.dtype), jnp.stack([h_f, h_b], axis=1)


def gmlp_mixer(u, ln_g, ln_b, w_s, b_s):
    b, l, _ = u.shape
    uu = jax.nn.gelu(u[..., :E])
    vv = layer_norm(jax.nn.gelu(u[..., E:2 * E]), ln_g, ln_b)
    z = u[..., 2 * E:]
    nc = l // MLP_CHUNK
    vv = vv.reshape(b, nc, MLP_CHUNK, MLP_GROUPS, E // MLP_GROUPS)
    s = jnp.einsum('gij,bcjgd->bcigd', w_s, vv) + b_s.T[:, :, None]
    return uu * s.reshape(b, l, E) * jax.nn.silu(z)


def complex_combine(e1, e2):
    a1r, a1i, b1r, b1i = e1
    a2r, a2i, b2r, b2i = e2
    return (a2r * a1r - a2i * a1i, a2r * a1i + a2i * a1r,
            a2r * b1r - a2i * b1i + b2r, a2r * b1i + a2i * b1r + b2i)


def s5_mixer(u, lam_re, lam_im, log_step, b_re, b_im, c_re, c_im, d_skip, w_glu, b_glu, h0):
    b, l, _ = u.shape
    uu = u[..., :E]
    z = u[..., E:]
    ug = uu.reshape(b, l, S5_GROUPS, S5_GROUP).astype(jnp.float32)
    ys = []
    finals = []
    for d in range(2):
        lr = lam_re[d].astype(jnp.float32)
        li = lam_im[d].astype(jnp.float32)
        step = jnp.exp(log_step[d].astype(jnp.float32))[:, None]
        mag = jnp.exp(lr * step)
        ang = li * step
        ab_re = mag * jnp.cos(ang)
        ab_im = mag * jnp.sin(ang)
        den = lr * lr + li * li
        nr = ab_re - 1.0
        cr = (nr * lr + ab_im * li) / den
        ci = (ab_im * lr - nr * li) / den
        br = b_re[d].astype(jnp.float32)
        bi = b_im[d].astype(jnp.float32)
        bb_re = cr[..., None] * br - ci[..., None] * bi
        bb_im = cr[..., None] * bi + ci[..., None] * br
        seq = ug if d == 0 else jnp.flip(ug, axis=1)
        drv_re = jnp.einsum('gnj,blgj->blgn', bb_re, seq)
        drv_im = jnp.einsum('gnj,blgj->blgn', bb_im, seq)
        h_re = h0[:, d, 0].astype(jnp.float32)
        h_im = h0[:, d, 1].astype(jnp.float32)
        drv_re = drv_re.at[:, 0].add(ab_re * h_re - ab_im * h_im)
        drv_im = drv_im.at[:, 0].add(ab_re * h_im + ab_im * h_re)
        a_re = jnp.broadcast_to(ab_re, (1, l, S5_GROUPS, S5_STATE))
        a_im = jnp.broadcast_to(ab_im, (1, l, S5_GROUPS, S5_STATE))
        _, _, s_re, s_im = lax.associative_scan(complex_combine, (a_re, a_im, drv_re, drv_im), axis=1)
        y_d = (jnp.einsum('gjn,blgn->blgj', c_re[d].astype(jnp.float32), s_re)
               - jnp.einsum('gjn,blgn->blgj', c_im[d].astype(jnp.float32), s_im))
        if d == 1:
            y_d = jnp.flip(y_d, axis=1)
        ys.append(y_d)
        finals.append(jnp.stack([s_re[:, -1], s_im[:, -1]], axis=1))
    y = ((ys[0] + ys[1]).reshape(b, l, E) + d_skip * uu).astype(u.dtype)
    y = jax.nn.gelu(y)
    y = y * jax.nn.sigmoid(y @ w_glu + b_glu)
    return y * jax.nn.silu(z), jnp.stack(finals, axis=1)


def split_heads(t):
    return t.reshape(t.shape[0], t.shape[1], ATT_HEADS, ATT_HEAD_DIM)


def context_attention(q, k, v):
    b, s = q.shape[0], q.shape[1]
    nb = s // ATT_BLOCK
    scale = ATT_HEAD_DIM ** -0.5
    qb = jnp.moveaxis(q.reshape(b, nb, ATT_BLOCK, ATT_HEADS, ATT_HEAD_DIM), 1, 0)

    def one_block(qi):
        logits = jnp.einsum('bqhd,bhkd->bhqk', qi, k).astype(jnp.float32) * scale
        p = jax.nn.softmax(logits, axis=-1).astype(v.dtype)
        return jnp.einsum('bhqk,bhkd->bqhd', p, v)

    out = lax.map(one_block, qb)
    return jnp.moveaxis(out, 0, 1).reshape(b, s, E)


def neighbourhood_attention(q, k, v, ck, cv, rpb):
    b, l = q.shape[0], q.shape[1]
    rows = l // GRID_W
    wr = min(WIN_ROWS, rows)
    nw = wr * GRID_W
    scale = ATT_HEAD_DIM ** -0.5
    qg = q.reshape(b, rows, GRID_W, ATT_HEADS, ATT_HEAD_DIM)
    kg = k.reshape(b, rows, GRID_W, ATT_HEADS, ATT_HEAD_DIM)
    vg = v.reshape(b, rows, GRID_W, ATT_HEADS, ATT_HEAD_DIM)
    r_idx = jnp.arange(rows)
    r_start = jnp.clip(r_idx - wr // 2, 0, rows - wr)
    key_rows = r_start[:, None] + jnp.arange(wr)[None, :]
    kb = kg[:, key_rows].reshape(b, rows, nw, ATT_HEADS, ATT_HEAD_DIM)
    vb = vg[:, key_rows].reshape(b, rows, nw, ATT_HEADS, ATT_HEAD_DIM)
    c_idx = jnp.arange(GRID_W)
    c_start = jnp.clip(c_idx - WIN_COLS // 2, 0, GRID_W - WIN_COLS)
    col_ok = (c_idx[None, :] >= c_start[:, None]) & (c_idx[None, :] < c_start[:, None] + WIN_COLS)
    mask = jnp.broadcast_to(col_ok[:, None, :], (GRID_W, wr, GRID_W)).reshape(GRID_W, nw)
    idx_r = key_rows - r_idx[:, None] + (WIN_ROWS - 1)
    idx_c = jnp.clip(c_idx[None, :] - c_idx[:, None] + (WIN_COLS - 1), 0, 2 * WIN_COLS - 2)
    bias = rpb[:, idx_r[:, None, :, None], idx_c[None, :, None, :]].reshape(ATT_HEADS, rows, GRID_W, nw)
    s_win = jnp.einsum('brqhd,brkhd->bhrqk', qg, kb).astype(jnp.float32) * scale + bias.astype(jnp.float32)
    s_win = jnp.where(mask, s_win, -jnp.inf)
    s_ctx = jnp.einsum('brqhd,bhpd->bhrqp', qg, ck).astype(jnp.float32) * scale
    p = jax.nn.softmax(jnp.concatenate([s_win, s_ctx], axis=-1), axis=-1).astype(v.dtype)
    out = (jnp.einsum('bhrqk,brkhd->brqhd', p[..., :nw], vb)
           + jnp.einsum('bhrqp,bhpd->brqhd', p[..., nw:], cv))
    return out.reshape(b, l, E)


def setup_inputs(seed: int = 0) -> dict:
    key = jax.random.key(seed)
    keys = iter(jax.random.split(key, 64))

    def nrm(shape, scale):
        return jax.random.normal(next(keys), shape, jnp.float32) * scale

    def unif(shape, lo, hi):
        return jax.random.uniform(next(keys), shape, jnp.float32, minval=lo, maxval=hi)

    dt0 = jnp.exp(unif((N_SSD, 2, SSD_HEADS), math.log(1e-3), math.log(1e-1)))
    inp = {}
    inp['x_prompt'] = nrm((BATCH, SEQ, D_MODEL), 1.0)
    inp['x_sample'] = nrm((DEC_BATCH, DEC_SEQ, D_MODEL), 1.0)
    inp['state_ssd'] = nrm((DEC_BATCH, N_SSD, 2, SSD_HEADS, SSD_HEAD_DIM, SSD_STATE), 0.1)
    inp['state_s5'] = nrm((DEC_BATCH, N_S5, 2, 2, S5_GROUPS, S5_STATE), 0.1)
    inp['cache_k'] = nrm((DEC_BATCH, N_NAT, ATT_HEADS, PAST_LEN, ATT_HEAD_DIM), 1.0)
    inp['cache_v'] = nrm((DEC_BATCH, N_NAT, ATT_HEADS, PAST_LEN, ATT_HEAD_DIM), 1.0)
    inp['c'] = nrm((DEC_BATCH, D_MODEL), 1.0)
    inp['c_ctx'] = nrm((D_MODEL,), 1.0)
    inp['norm_g'] = 1.0 + nrm((DEPTH, D_MODEL), 0.02)
    inp['w_mod'] = nrm((DEPTH, D_MODEL, 3 * D_MODEL), 0.5 * D_MODEL ** -0.5)
    inp['b_mod'] = nrm((DEPTH, 3 * D_MODEL), 0.02)
    inp['w_out'] = nrm((DEPTH, E, D_MODEL), E ** -0.5)
    inp['final_g'] = 1.0 + nrm((D_MODEL,), 0.02)
    inp['ssd_w_in'] = nrm((N_SSD, D_MODEL, SSD_IN), D_MODEL ** -0.5)
    inp['ssd_conv_w'] = nrm((N_SSD, CONV_W, SSD_CONV_CH), CONV_W ** -0.5)
    inp['ssd_conv_b'] = nrm((N_SSD, SSD_CONV_CH), 0.02)
    inp['ssd_dt_bias'] = dt0 + jnp.log(-jnp.expm1(-dt0))
    inp['ssd_a_log'] = jnp.log(unif((N_SSD, 2, SSD_HEADS), 1.0, 16.0))
    inp['ssd_d'] = 1.0 + nrm((N_SSD, SSD_HEADS), 0.02)
    inp['ssd_norm_g'] = 1.0 + nrm((N_SSD, E), 0.02)
    inp['mlp_w_in'] = nrm((N_MLP, D_MODEL, 3 * E), D_MODEL ** -0.5)
    inp['mlp_ln_g'] = 1.0 + nrm((N_MLP, E), 0.02)
    inp['mlp_ln_b'] = nrm((N_MLP, E), 0.02)
    inp['mlp_w_s'] = nrm((N_MLP, MLP_GROUPS, MLP_CHUNK, MLP_CHUNK), MLP_CHUNK ** -0.5)
    inp['mlp_b_s'] = 1.0 + nrm((N_MLP, MLP_GROUPS, MLP_CHUNK), 0.02)
    inp['s5_w_in'] = nrm((N_S5, D_MODEL, 2 * E), D_MODEL ** -0.5)
    inp['s5_lam_re'] = -0.5 + nrm((N_S5, 2, S5_GROUPS, S5_STATE), 0.01)
    inp['s5_lam_im'] = jnp.pi * jnp.arange(S5_STATE, dtype=jnp.float32) + nrm((N_S5, 2, S5_GROUPS, S5_STATE), 0.01)
    inp['s5_log_step'] = unif((N_S5, 2, S5_GROUPS), math.log(1e-3), math.log(1e-1))
    inp['s5_b_re'] = nrm((N_S5, 2, S5_GROUPS, S5_STATE, S5_GROUP), (2 * S5_GROUP) ** -0.5)
    inp['s5_b_im'] = nrm((N_S5, 2, S5_GROUPS, S5_STATE, S5_GROUP), (2 * S5_GROUP) ** -0.5)
    inp['s5_c_re'] = nrm((N_S5, 2, S5_GROUPS, S5_GROUP, S5_STATE), S5_STATE ** -0.5)
    inp['s5_c_im'] = nrm((N_S5, 2, S5_GROUPS, S5_GROUP, S5_STATE), S5_STATE ** -0.5)
    inp['s5_d'] = nrm((N_S5, E), 1.0)
    inp['s5_w_glu'] = nrm((N_S5, E, E), E ** -0.5)
    inp['s5_b_glu'] = nrm((N_S5, E), 0.02)
    inp['nat_w_in'] = nrm((N_NAT, D_MODEL, 4 * E), D_MODEL ** -0.5)
    inp['nat_rpb'] = nrm((N_NAT, ATT_HEADS, 2 * WIN_ROWS - 1, 2 * WIN_COLS - 1), 0.1)
    return inp


def reference(x_prompt, x_sample, state_ssd, state_s5, cache_k, cache_v, c, c_ctx,
              norm_g, w_mod, b_mod, w_out, final_g,
              ssd_w_in, ssd_conv_w, ssd_conv_b, ssd_dt_bias, ssd_a_log, ssd_d, ssd_norm_g,
              mlp_w_in, mlp_ln_g, mlp_ln_b, mlp_w_s, mlp_b_s,
              s5_w_in, s5_lam_re, s5_lam_im, s5_log_step, s5_b_re, s5_b_im, s5_c_re, s5_c_im,
              s5_d, s5_w_glu, s5_b_glu,
              nat_w_in, nat_rpb):
    xc = x_prompt
    xl = x_sample
    bc, sc_len = xc.shape[0], xc.shape[1]
    cond_ctx = jax.nn.silu(c_ctx)
    cond_lat = jax.nn.silu(c)[:, None, :]
    new_ssd, new_s5, new_k, new_v = [], [], [], []
    for i in range(DEPTH):
        kind = i % N_MIXERS
        j = i // N_MIXERS
        hc, gate_c = adaln(xc, norm_g[i], cond_ctx @ w_mod[i] + b_mod[i])
        hl, gate_l = adaln(xl, norm_g[i], cond_lat @ w_mod[i] + b_mod[i])
        if kind == 0:
            h0 = jnp.zeros((bc, 2, SSD_HEADS, SSD_HEAD_DIM, SSD_STATE), jnp.float32)
            yc, st = ssd_mixer(hc @ ssd_w_in[j], ssd_conv_w[j], ssd_conv_b[j], ssd_dt_bias[j],
                               ssd_a_log[j], ssd_d[j], ssd_norm_g[j], h0)
            yl, _ = ssd_mixer(hl @ ssd_w_in[j], ssd_conv_w[j], ssd_conv_b[j], ssd_dt_bias[j],
                              ssd_a_log[j], ssd_d[j], ssd_norm_g[j], state_ssd[:, j])
            new_ssd.append(st.astype(x_prompt.dtype))
        elif kind == 1:
            yc = gmlp_mixer(hc @ mlp_w_in[j], mlp_ln_g[j], mlp_ln_b[j], mlp_w_s[j], mlp_b_s[j])
            yl = gmlp_mixer(hl @ mlp_w_in[j], mlp_ln_g[j], mlp_ln_b[j], mlp_w_s[j], mlp_b_s[j])
        elif kind == 2:
            h0 = jnp.zeros((bc, 2, 2, S5_GROUPS, S5_STATE), jnp.float32)
            yc, st = s5_mixer(hc @ s5_w_in[j], s5_lam_re[j], s5_lam_im[j], s5_log_step[j], s5_b_re[j],
                              s5_b_im[j], s5_c_re[j], s5_c_im[j], s5_d[j], s5_w_glu[j], s5_b_glu[j], h0)
            yl, _ = s5_mixer(hl @ s5_w_in[j], s5_lam_re[j], s5_lam_im[j], s5_log_step[j], s5_b_re[j],
                             s5_b_im[j], s5_c_re[j], s5_c_im[j], s5_d[j], s5_w_glu[j], s5_b_glu[j],
                             state_s5[:, j])
            new_s5.append(st.astype(x_prompt.dtype))
        else:
            uc = hc @ nat_w_in[j]
            k_ctx = jnp.transpose(split_heads(uc[..., E:2 * E]), (0, 2, 1, 3))
            v_ctx = jnp.transpose(split_heads(uc[..., 2 * E:3 * E]), (0, 2, 1, 3))
            yc = context_attention(split_heads(uc[..., :E]), k_ctx, v_ctx) * jax.nn.silu(uc[..., 3 * E:])
            ul = hl @ nat_w_in[j]
            yl = neighbourhood_attention(split_heads(ul[..., :E]), split_heads(ul[..., E:2 * E]),
                                         split_heads(ul[..., 2 * E:3 * E]), cache_k[:, j], cache_v[:, j],
                                         nat_rpb[j]) * jax.nn.silu(ul[..., 3 * E:])
            new_k.append(k_ctx)
            new_v.append(v_ctx)
        xc = xc + gate_c * (yc @ w_out[i])
        xl = xl + gate_l * (yl @ w_out[i])
    y_prompt = rms_norm(xc, final_g)
    y_sample = rms_norm(xl, final_g)
    return (y_prompt, y_sample, jnp.stack(new_ssd, axis=1), jnp.stack(new_s5, axis=1),
            jnp.stack(new_k, axis=1), jnp.stack(new_v, axis=1))
```

```python
import numpy as np
from contextlib import ExitStack
import concourse.bass as bass
import concourse.mybir as mybir
from concourse.bass_utils import run_bass_kernel_spmd

F32 = mybir.dt.float32
BF16 = mybir.dt.bfloat16
AF = mybir.ActivationFunctionType
ALU = mybir.AluOpType
AX = mybir.AxisListType

D = 1024
E = 2048
NP_TOK = 1024
NS_TOK = 2048
NTOK = NP_TOK + NS_TOK
EPS = 1e-6
COMPUTE = ("pe", "act", "dve", "pool")
NDMASEM = 12
SAME_ENGINE_SYNC = True


class Buf:
    __slots__ = ("lw", "rd")

    def __init__(self):
        self.lw = None
        self.rd = {}


class Sched:
    def __init__(self, nc):
        self.nc = nc
        self.ops = {e: [] for e in COMPUTE + ("sp",)}
        self.cnt = {e: 0 for e in COMPUTE}
        self.seen = {e: {} for e in COMPUTE + ("sp",)}
        self.dma_slot = {}
        self.dma_val = {}
        self.sems = {}

    def _deps(self, eng, reads, writes):
        deps = {}

        def add(tok):
            if tok is None:
                return
            k, v = tok
            if deps.get(k, 0) < v:
                deps[k] = v

        for r in reads:
            add(r.lw)
        for w in writes:
            add(w.lw)
            for k, v in w.rd.items():
                add((k, v))
        out = []
        seen = self.seen[eng]
        for k, v in deps.items():
            if k == eng and (eng == "pe" or not SAME_ENGINE_SYNC):
                continue
            if seen.get(k, 0) >= v:
                continue
            seen[k] = v
            out.append((k, v))
        return out

    def _mark(self, tok, reads, writes):
        k, v = tok
        for r in reads:
            if r.rd.get(k, 0) < v:
                r.rd[k] = v
        for w in writes:
            w.lw = tok
            w.rd = {}

    def op(self, eng, fn, reads=(), writes=()):
        waits = self._deps(eng, reads, writes)
        self.cnt[eng] += 1
        tok = (eng, self.cnt[eng])
        self.ops[eng].append((waits, fn, tok, 1))
        self._mark(tok, reads, writes)

    def dma(self, q, out, in_, reads=(), writes=()):
        slot = self.dma_slot.get(q, 0)
        self.dma_slot[q] = (slot + 1) % NDMASEM
        key = ("dma", q, slot)
        prev = self.dma_val.get(key, 0)
        waits = self._deps(q, reads, writes)
        if prev > 0 and self.seen[q].get(key, 0) < prev:
            self.seen[q][key] = prev
            waits.append((key, prev))
        val = prev + 16
        self.dma_val[key] = val
        tok = (key, val)

        def fn(e, out=out, in_=in_):
            return e.dma_start(out=out, in_=in_, allow_slow_non_contiguous=True)

        self.ops[q].append((waits, fn, tok, 16))
        self._mark(tok, reads, writes)

    def barrier(self):
        targets = [(e, self.cnt[e]) for e in COMPUTE if self.cnt[e] > 0]
        targets += [(key, v) for key, v in self.dma_val.items()]
        for eng in COMPUTE + ("sp",):
            waits = []
            for key, v in targets:
                if key == eng:
                    continue
                if self.seen[eng].get(key, 0) < v:
                    self.seen[eng][key] = v
                    waits.append((key, v))
            if waits:
                self.ops[eng].append((waits, None, None, 0))

    def emit(self, es, final_wait_engine="sp"):
        nc = self.nc
        keys = list(COMPUTE)
        for q in self.dma_slot:
            for s in range(NDMASEM):
                if ("dma", q, s) in self.dma_val:
                    keys.append(("dma", q, s))
        for k in keys:
            nm = k if isinstance(k, str) else "d_%s_%d" % (k[1], k[2])
            self.sems[k] = es.enter_context(nc.semaphore("s_" + nm))
        fin = []
        for k in keys:
            v = self.cnt[k] if isinstance(k, str) else self.dma_val[k]
            if v > 0 and k != final_wait_engine:
                fin.append((k, v))
        block = es.enter_context(nc.Block())

        def run(e, name):
            for waits, fn, tok, inc in self.ops[name]:
                if fn is None:
                    for k, v in waits:
                        e.wait_ge(self.sems[k], v)
                    continue
                NW = 1
                for k, v in waits[NW:]:
                    e.wait_ge(self.sems[k], v)
                ins = fn(e)
                for k, v in waits[:NW]:
                    ins._wait_ge(self.sems[k], v)
                ins.then_inc(self.sems[tok[0]], inc)
            if name == final_wait_engine:
                for k, v in fin:
                    e.wait_ge(self.sems[k], v)

        @block.tensor
        def _(e):
            run(e, "pe")

        @block.scalar
        def _(e):
            run(e, "act")

        @block.vector
        def _(e):
            run(e, "dve")

        @block.gpsimd
        def _(e):
            run(e, "pool")

        @block.sync
        def _(e):
            run(e, "sp")


class T:
    __slots__ = ("t", "b")

    def __init__(self, t):
        self.t = t
        self.b = Buf()

    def __getitem__(self, k):
        return self.t[k]


class Ring:
    def __init__(self, tiles):
        self.tiles = tiles
        self.i = 0

    def next(self):
        t = self.tiles[self.i]
        self.i = (self.i + 1) % len(self.tiles)
        return t


class K:
    def __init__(self, nc, es):
        self.nc = nc
        self.es = es
        self.S = Sched(nc)
        self.n = 0

    def sb(self, shape, dt, name=None):
        self.n += 1
        return T(self.es.enter_context(self.nc.sbuf_tensor(name or "sb%d" % self.n, list(shape), dt)))

    def ring(self, n, shape, dt):
        return Ring([self.sb(shape, dt) for _ in range(n)])

    def psb(self, shape, dt):
        self.n += 1
        return T(self.es.enter_context(self.nc.psum_tensor("ps%d" % self.n, list(shape), dt)))

    def init_arena(self, nbytes):
        self.arena = self.es.enter_context(self.nc.sbuf_tensor("arena", [128, nbytes // 2], BF16))
        self.asize = nbytes
        self.aoff = 0
        self.alog = []

    def areset(self):
        self.S.barrier()
        self.aoff = 0

    def at(self, shape, dt):
        esz = 4 if dt == F32 else 2
        n = 1
        for d_ in shape[1:]:
            n *= d_
        nb = (n * esz + 63) // 64 * 64
        assert self.aoff + nb <= self.asize, ("arena overflow", self.aoff, nb, self.asize)
        ap = self.arena[0:shape[0], self.aoff // 2:(self.aoff + n * esz) // 2]
        if dt == F32:
            ap = ap.bitcast(F32)
        if len(shape) > 2:
            names = ["d%d" % i for i in range(len(shape) - 1)]
            kw = {names[i]: shape[i + 1] for i in range(len(names) - 1)}
            ap = ap.rearrange("p (%s) -> p %s" % (" ".join(names), " ".join(names)), **kw)
        self.alog.append((self.aoff, tuple(shape), dt))
        self.aoff += nb
        return T(ap)

    def amark(self):
        return self.aoff

    def arestore(self, m):
        self.S.barrier()
        self.aoff = m

    def aring(self, n, shape, dt):
        return Ring([self.at(shape, dt) for _ in range(n)])

    def dram(self, name, shape, dt, kind="Internal"):
        return self.nc.dram_tensor(name, list(shape), dt, kind=kind).ap()


def bufs(*ts):
    return [t.b for t in ts]


def build(cfg):
    layers = cfg.get("layers", [0, 1, 2, 3])
    final = cfg.get("final", True)
    nc = bass.Bass("TRN2", target_bir_lowering=False)
    es = ExitStack()
    k = K(nc, es)
    S = k.S
    I = {}

    def inp(name, shape):
        I[name] = k.dram(name, shape, F32, kind="ExternalInput")
        return I[name]

    xin = inp("xin", [NTOK, D])
    cvec = inp("cvec", [2, D])
    norm_g = inp("norm_g", [4, D])
    w_mod = inp("w_mod", [4, D, 3 * D])
    b_mod = inp("b_mod", [4, 3 * D])
    w_out = inp("w_out", [4, E, D])
    final_g = inp("final_g", [D])
    mlp_w_in = inp("mlp_w_in", [D, 3 * E])
    mlp_ln_g = inp("mlp_ln_g", [E])
    mlp_ln_b = inp("mlp_ln_b", [E])
    mlp_w_sT = inp("mlp_w_sT", [8, 128, 128])
    mlp_b_s = inp("mlp_b_s", [8, 128])
    ssd_w_in = inp("ssd_w_in", [D, 6208])
    ssd_conv_w = inp("ssd_conv_w", [5, 4096])
    ssd_conv_b = inp("ssd_conv_b", [4096])
    ssd_dt_bias = inp("ssd_dt_bias", [64])
    ssd_a_log = inp("ssd_a_log", [64])
    ssd_d = inp("ssd_d", [32])
    ssd_norm_g = inp("ssd_norm_g", [E])
    state_ssd = inp("state_ssd", [2, 32, 64, 128])
    new_ssd = k.dram("new_ssd", [4, 2, 32, 64, 128], F32, kind="ExternalOutput")
    yscr = k.dram("yscr", [16, 128, NTOK], BF16)
    s5_w_in = inp("s5_w_in", [D, 2 * E])
    s5_lam = inp("s5_lam", [2, 2, 128, 64])
    s5_lstep = inp("s5_lstep", [2, 128, 64])
    s5_B = inp("s5_B", [2, 2, 128, 64, 16])
    s5_C = inp("s5_C", [2, 2, 128, 64, 16])
    s5_h0 = inp("s5_h0", [2, 2, 128, 64])
    s5_d = inp("s5_d", [E])
    s5_w_glu = inp("s5_w_glu", [E, E])
    s5_b_glu = inp("s5_b_glu", [E])
    new_s5 = k.dram("new_s5", [4, 2, 2, 128, 64], F32, kind="ExternalOutput")
    Tscr = k.dram("Tscr", [8, 128, 16 * 128], BF16)
    VTscr = k.dram("VTscr", [8, 128, 8 * 4 * 128], BF16)
    W2scr = k.dram("W2scr", [8, 128, 8 * 4 * 128], BF16)
    nat_w_in = inp("nat_w_in", [D, 4 * E])
    rpbg = inp("rpbg", [32, 128, 1024])
    natmask = inp("natmask", [3, 128, 576])
    cache_k = inp("cache_k", [32, 256, 64])
    cache_v = inp("cache_v", [32, 256, 64])
    new_k = k.dram("new_k", [4, 32, 256, 64], F32, kind="ExternalOutput")
    new_v = k.dram("new_v", [4, 32, 256, 64], F32, kind="ExternalOutput")
    y_out = k.dram("y_out", [NTOK, D], F32, kind="ExternalOutput")
    xres = k.dram("xres", [NTOK, D], F32)
    dma_done = Buf()

    identf = k.sb([128, 128], F32)
    identb = k.sb([128, 128], BF16)
    onesf = k.sb([128, 128], F32)
    S.op("pool", lambda e: e.memset(identf[:], 0.0), writes=bufs(identf))
    S.op("pool", lambda e: e.affine_select(out=identf[:], in_=identf[:], compare_op=ALU.not_equal, fill=1.0,
                                           base=0, pattern=[[-1, 128]], channel_multiplier=1),
         reads=bufs(identf), writes=bufs(identf))
    S.op("dve", lambda e: e.tensor_copy(out=identb[:], in_=identf[:]), reads=bufs(identf), writes=bufs(identb))
    S.op("pool", lambda e: e.memset(onesf[:], 1.0), writes=bufs(onesf))

    banks = Ring([k.psb([128, 512], F32) for _ in range(8)])

    hT = k.sb([128, 8, 2048], BF16, "hT")
    wo = T(hT.t)
    wo.b = hT.b
    wo_view = hT.t[:].rearrange("p k t -> p (k t)").rearrange("p (k n) -> p k n", k=16)
    wring = k.ring(3, [128, 8, 512], BF16)
    junk = k.sb([128, D], BF16)
    small = k.ring(8, [128, 8], F32)
    rstd_keep = k.sb([128, 16], F32)
    k.init_arena(136 * 1024)
    L = {}

    cf = k.sb([128, 8, 2], F32)
    cb = k.sb([128, 8, 2], BF16)
    for c_ in range(2):
        S.dma("sp", cf[:, :, c_], cvec[c_].rearrange("(k p) -> p k", p=128), writes=bufs(cf))
    S.op("act", lambda e: e.activation(out=cb[:], in_=cf[:], func=AF.Silu), reads=bufs(cf), writes=bufs(cb))

    modT = k.sb([128, 16, 2], F32)
    bmodT = k.sb([128, 16], F32)
    ngT = k.sb([128, 8], F32)
    Asc = k.sb([128, 8, 2], F32)
    gate_bc = [k.sb([128, D], F32), k.sb([128, D], F32)]
    sel = [k.sb([2, 128], F32), k.sb([2, 128], F32)]
    for c in range(2):
        S.op("pool", lambda e, c=c: e.memset(sel[c][:], 0.0), writes=bufs(sel[c]))
        S.op("pool", lambda e, c=c: e.affine_select(out=sel[c][:], in_=sel[c][:], compare_op=ALU.not_equal, fill=1.0,
                                                     base=-c, pattern=[[0, 128]], channel_multiplier=1),
             reads=bufs(sel[c]), writes=bufs(sel[c]))

    def load_w(wap, c0, n, q="pool"):
        wt = wring.next()
        S.dma(q, wt[:, :, 0:n], wap.rearrange("(k p) n -> p k n", p=128)[:, :, c0:c0 + n], writes=bufs(wt))
        return wt

    def phase_a(li):
        ma_ = k.amark()
        gate2 = k.at([2, D], F32)
        bgate2 = k.at([2, D], F32)
        S.dma("sp", bmodT[:], b_mod[li, 0:2 * D].rearrange("(c p) -> p c", p=128), writes=bufs(bmodT))
        S.dma("sp", ngT[:], norm_g[li].rearrange("(c p) -> p c", p=128), writes=bufs(ngT))
        S.dma("sp", bgate2[:], b_mod[li, 2 * D:3 * D].partition_broadcast(2), writes=bufs(bgate2))
        for blk in range(4):
            wt = load_w(w_mod[li], blk * 512, 512)
            for cc in range(4):
                ch = blk * 4 + cc
                pb = banks.next()
                for kk in range(8):
                    S.op("pe", lambda e, pb=pb, wt=wt, cc=cc, kk=kk: e.matmul(
                        pb[:, 0:2], lhsT=wt[:, kk, cc * 128:(cc + 1) * 128], rhs=cb[:, kk, :],
                        start=(kk == 0), stop=(kk == 7)), reads=bufs(wt, cb), writes=bufs(pb))
                S.op("dve", lambda e, pb=pb, ch=ch: e.tensor_scalar(
                    out=modT[:, ch, :], in0=pb[:, 0:2], scalar1=bmodT[:, ch:ch + 1], scalar2=None, op0=ALU.add),
                    reads=bufs(pb, bmodT), writes=bufs(modT))
        S.op("dve", lambda e: e.tensor_scalar(out=Asc[:], in0=modT[:, 8:16, :], scalar1=1.0, scalar2=None, op0=ALU.add),
             reads=bufs(modT), writes=bufs(Asc))
        S.op("dve", lambda e: e.tensor_tensor(out=Asc[:], in0=Asc[:], in1=ngT[:].unsqueeze(2).to_broadcast([128, 8, 2]),
                                              op=ALU.mult), reads=bufs(Asc, ngT), writes=bufs(Asc))
        for blk in range(2):
            wt = load_w(w_mod[li], 2 * D + blk * 512, 512)
            pb = banks.next()
            for kk in range(8):
                S.op("pe", lambda e, pb=pb, wt=wt, kk=kk: e.matmul(
                    pb[0:2, :], lhsT=cb[:, kk, :], rhs=wt[:, kk, :], start=(kk == 0), stop=(kk == 7)),
                    reads=bufs(wt, cb), writes=bufs(pb))
            S.op("dve", lambda e, pb=pb, blk=blk: e.tensor_tensor(
                out=gate2[:, blk * 512:(blk + 1) * 512], in0=pb[0:2, :], in1=bgate2[:, blk * 512:(blk + 1) * 512],
                op=ALU.add), reads=bufs(pb, bgate2), writes=bufs(gate2))
        for c in range(2):
            for blk in range(2):
                pb = banks.next()
                S.op("pe", lambda e, pb=pb, c=c, blk=blk: e.matmul(
                    pb[:], lhsT=sel[c][:], rhs=gate2[:, blk * 512:(blk + 1) * 512], start=True, stop=True),
                    reads=bufs(sel[c], gate2), writes=bufs(pb))
                S.op("act", lambda e, pb=pb, c=c, blk=blk: e.activation(
                    out=gate_bc[c][:, blk * 512:(blk + 1) * 512], in_=pb[:], func=AF.Copy),
                    reads=bufs(pb), writes=bufs(gate_bc[c]))
        k.arestore(ma_)

    def rows_std(tok0):
        return lambda src: src[tok0:tok0 + 128, :]

    def rms_stats(xt):
        st = small.next()
        S.op("act", lambda e: e.activation(out=junk[:], in_=xt[:], func=AF.Square, accum_out=st[:, 0:1]),
             reads=bufs(xt), writes=bufs(junk, st))
        S.op("dve", lambda e: e.tensor_scalar(out=st[:, 0:1], in0=st[:, 0:1], scalar1=1.0 / D, scalar2=EPS,
                                              op0=ALU.mult, op1=ALU.add), reads=bufs(st), writes=bufs(st))
        S.op("act", lambda e: e.activation(out=st[:, 0:1], in_=st[:, 0:1], func=AF.Sqrt), reads=bufs(st), writes=bufs(st))
        S.op("dve", lambda e: e.reciprocal(out=st[:, 0:1], in_=st[:, 0:1]), reads=bufs(st), writes=bufs(st))
        return st

    def phase_b(src, tiles, cond):
        m_ = k.amark()
        xring = k.aring(3, [128, D], F32)
        xnring = k.aring(2, [128, D], BF16)

        def load(i):
            xt = xring.next()
            S.dma("sp", xt[:], tiles[i][0](src), writes=bufs(xt))
            return xt
        nxt = load(0)
        for i in range(len(tiles)):
            xt = nxt
            if i + 1 < len(tiles):
                nxt = load(i + 1)
            col0 = tiles[i][1]
            st = rms_stats(xt)
            xn = xnring.next()
            S.op("dve", lambda e, xn=xn, xt=xt, st=st: e.tensor_scalar(out=xn[:], in0=xt[:], scalar1=st[:, 0:1],
                                                                   scalar2=None, op0=ALU.mult),
                 reads=bufs(xt, st), writes=bufs(xn))
            pb = banks.next()
            pv = pb[:].bitcast(BF16).rearrange("p (k t) -> p k t", k=8)
            for kk in range(8):
                S.op("pe", lambda e, pv=pv, xn=xn, kk=kk: e.transpose(out=pv[:, kk, :], in_=xn[:, kk * 128:(kk + 1) * 128],
                                                                    identity=identb[:]),
                     reads=bufs(xn, identb), writes=bufs(pb))
            for kk in range(8):
                S.op("act", lambda e, pv=pv, kk=kk, col0=col0: e.activation(
                    out=hT[:, kk, col0:col0 + 128], in_=pv[:, kk, :], func=AF.Identity,
                    scale=Asc[:, kk, cond:cond + 1], bias=modT[:, kk, cond:cond + 1]),
                    reads=bufs(pb, Asc, modT), writes=bufs(hT))
        k.arestore(m_)


    def load_wout(li):
        for h in range(2):
            S.dma("pool", wo_view[:, h * 8:(h + 1) * 8, :],
                  w_out[li].rearrange("(k p) n -> p k n", p=128)[:, h * 8:(h + 1) * 8, :], writes=bufs(wo))

    def phase_d(src, dst, tiles, cond, last, scale_t=None, ytok0=None):
        if cfg.get("skip_d"):
            return
        m_ = k.amark()
        xring = k.aring(3, [128, D], F32)
        tring = k.aring(2, [128, D], F32)
        if last:
            fg_bc = k.at([128, D], F32)
            S.dma("sp", fg_bc[:], final_g.partition_broadcast(128), writes=bufs(fg_bc))
        if ytok0 is None:
            yT = L["yT"]
        else:
            yring = k.aring(2, [128, 16, 512], BF16)
            yT = None

        def load(i):
            xt = xring.next()
            S.dma("sp", xt[:], tiles[i][0](src), writes=bufs(xt))
            return xt
        nxt = load(0)
        for i in range(len(tiles)):
            xt = nxt
            if i + 1 < len(tiles):
                nxt = load(i + 1)
            col0 = tiles[i][1]
            if ytok0 is not None:
                if i % 4 == 0:
                    yT = yring.next()
                    S.dma("sp", yT[:], yscr[:, :, ytok0 + tiles[i][1]:ytok0 + tiles[i][1] + 512].rearrange("b p t -> p b t"),
                          writes=bufs(yT))
                col0 = (i % 4) * 128
            tt = tring.next()
            for h in range(2):
                pb = banks.next()
                for kk in range(16):
                    S.op("pe", lambda e, pb=pb, kk=kk, h=h, col0=col0, yT=yT: e.matmul(
                        pb[:], lhsT=yT[:, kk, col0:col0 + 128], rhs=wo_view[:, kk, h * 512:(h + 1) * 512],
                        start=(kk == 0), stop=(kk == 15)), reads=bufs(yT, wo), writes=bufs(pb))
                if scale_t is None:
                    S.op("dve", lambda e, pb=pb, tt=tt, h=h: e.tensor_tensor(
                        out=tt[:, h * 512:(h + 1) * 512], in0=pb[:], in1=gate_bc[cond][:, h * 512:(h + 1) * 512],
                        op=ALU.mult), reads=bufs(pb, gate_bc[cond]), writes=bufs(tt))
                else:
                    sc = scale_t(i)
                    S.op("dve", lambda e, pb=pb, tt=tt, h=h, sc=sc: e.scalar_tensor_tensor(
                        out=tt[:, h * 512:(h + 1) * 512], in0=pb[:], scalar=sc[0], in1=gate_bc[cond][:, h * 512:(h + 1) * 512],
                        op0=ALU.mult, op1=ALU.mult), reads=bufs(pb, gate_bc[cond]) + [sc[1]], writes=bufs(tt))
            S.op("pool", lambda e, tt=tt, xt=xt: e.tensor_tensor(out=xt[:], in0=tt[:], in1=xt[:], op=ALU.add),
                 reads=bufs(tt, xt), writes=bufs(xt))
            if not last:
                S.dma("sp", tiles[i][0](dst), xt[:], reads=bufs(xt))
            else:
                st = rms_stats(xt)
                S.op("dve", lambda e, tt=tt, xt=xt, st=st: e.scalar_tensor_tensor(
                    out=tt[:], in0=xt[:], scalar=st[:, 0:1], in1=fg_bc[:], op0=ALU.mult, op1=ALU.mult),
                    reads=bufs(xt, st, fg_bc), writes=bufs(tt))
                S.dma("sp", tiles[i][0](y_out), tt[:], reads=bufs(tt))
        k.arestore(m_)

    def gmlp_consts():
        c = {}
        c["lngT"] = k.at([128, 16], F32)
        c["lnbT"] = k.at([128, 16], F32)
        c["wsT"] = k.at([128, 8, 128], BF16)
        c["wsTf"] = k.at([128, 8, 128], F32)
        c["bs_bc"] = k.at([128, 8, 128], F32)
        c["Bt"] = k.at([128, 16, 128], F32)
        S.dma("sp", c["lngT"][:], mlp_ln_g.rearrange("(c p) -> p c", p=128), writes=bufs(c["lngT"]))
        S.dma("sp", c["lnbT"][:], mlp_ln_b.rearrange("(c p) -> p c", p=128), writes=bufs(c["lnbT"]))
        S.dma("sp", c["wsTf"][:], mlp_w_sT.rearrange("g j i -> j g i"), writes=bufs(c["wsTf"]))
        S.dma("sp", c["bs_bc"][:].rearrange("p g i -> p (g i)"), mlp_b_s.rearrange("g i -> (g i)").partition_broadcast(128),
              writes=bufs(c["bs_bc"]))
        S.op("dve", lambda e: e.tensor_copy(out=c["wsT"][:], in_=c["wsTf"][:]), reads=bufs(c["wsTf"]), writes=bufs(c["wsT"]))
        for half in range(2):
            pb = banks.next()
            S.op("pe", lambda e, pb=pb, half=half: e.matmul(
                pb[:], lhsT=onesf[:], rhs=c["wsTf"][:, half * 4:(half + 1) * 4, :].rearrange("p g i -> p (g i)"),
                start=True, stop=True), reads=bufs(onesf, c["wsTf"]), writes=bufs(pb))
            for gg in range(4):
                g = half * 4 + gg
                for bb in range(2):
                    blk = g * 2 + bb
                    S.op("dve", lambda e, pb=pb, gg=gg, g=g, blk=blk: e.scalar_tensor_tensor(
                        out=c["Bt"][:, blk, :], in0=pb[:, gg * 128:(gg + 1) * 128], scalar=c["lnbT"][:, blk:blk + 1],
                        in1=c["bs_bc"][:, g, :], op0=ALU.mult, op1=ALU.add),
                        reads=bufs(pb, c["lnbT"], c["bs_bc"]), writes=bufs(c["Bt"]))
        c["vv"] = k.at([128, 8, E], BF16)
        c["gtmp"] = k.aring(2, [128, 512], F32)
        c["ug"] = k.aring(2, [128, 512], F32)
        c["zs"] = k.aring(2, [128, 512], F32)
        c["sg"] = k.aring(2, [128, 512], F32)
        c["st"] = k.at([128, 8, 8], F32)
        return c

    def gmlp_unit(c, ntile):
        yT = L["yT"]
        vv = c["vv"]
        stt = c["st"]
        for b in range(4):
            wv = load_w(mlp_w_in, E + b * 512, 512)
            for t in range(ntile):
                pb = banks.next()
                for kk in range(8):
                    S.op("pe", lambda e, pb=pb, t=t, wv=wv, kk=kk: e.matmul(
                        pb[:], lhsT=hT[:, kk, t * 128:(t + 1) * 128], rhs=wv[:, kk, :], start=(kk == 0), stop=(kk == 7)),
                        reads=bufs(hT, wv), writes=bufs(pb))
                gt = c["gtmp"].next()
                S.op("act", lambda e, pb=pb, b=b, t=t, gt=gt: e.activation(
                    out=gt[:], in_=pb[:], func=AF.Gelu, accum_out=stt[:, t, b:b + 1]),
                    reads=bufs(pb), writes=bufs(gt, stt))
                S.op("act", lambda e, b=b, t=t, gt=gt: e.activation(
                    out=junk[:, 0:512], in_=gt[:], func=AF.Square, accum_out=stt[:, t, 4 + b:5 + b]),
                    reads=bufs(gt), writes=bufs(junk, stt))
                S.op("pool", lambda e, b=b, t=t, gt=gt: e.tensor_copy(out=vv[:, t, b * 512:(b + 1) * 512], in_=gt[:]),
                     reads=bufs(gt), writes=bufs(vv))
        for t in range(ntile):
            st2 = small.next()
            S.op("dve", lambda e, t=t, st2=st2: e.tensor_reduce(
                out=st2[:, 0:2], in_=stt[:, t, :].rearrange("p (a b) -> p a b", a=2), axis=AX.X, op=ALU.add),
                reads=bufs(stt), writes=bufs(st2))
            S.op("dve", lambda e, st2=st2: e.tensor_scalar(out=st2[:, 0:2], in0=st2[:, 0:2], scalar1=1.0 / E, scalar2=None,
                                                           op0=ALU.mult), reads=bufs(st2), writes=bufs(st2))
            S.op("dve", lambda e, st2=st2: e.tensor_tensor(out=st2[:, 2:3], in0=st2[:, 0:1], in1=st2[:, 0:1], op=ALU.mult),
                 reads=bufs(st2), writes=bufs(st2))
            S.op("dve", lambda e, st2=st2: e.scalar_tensor_tensor(out=st2[:, 2:3], in0=st2[:, 2:3], scalar=-1.0, in1=st2[:, 1:2],
                                                                  op0=ALU.mult, op1=ALU.add), reads=bufs(st2), writes=bufs(st2))
            S.op("dve", lambda e, st2=st2: e.tensor_scalar(out=st2[:, 2:3], in0=st2[:, 2:3], scalar1=EPS, scalar2=None,
                                                           op0=ALU.add), reads=bufs(st2), writes=bufs(st2))
            S.op("act", lambda e, st2=st2: e.activation(out=st2[:, 2:3], in_=st2[:, 2:3], func=AF.Sqrt),
                 reads=bufs(st2), writes=bufs(st2))
            S.op("dve", lambda e, st2=st2: e.reciprocal(out=st2[:, 2:3], in_=st2[:, 2:3]), reads=bufs(st2), writes=bufs(st2))
            S.op("dve", lambda e, st2=st2, t=t: e.tensor_scalar(
                out=vv[:, t, :], in0=vv[:, t, :], scalar1=st2[:, 0:1], scalar2=st2[:, 2:3], op0=ALU.subtract, op1=ALU.mult),
                reads=bufs(vv, st2), writes=bufs(vv))
        nq = ntile // 4
        for blk in range(16):
            g = blk // 2
            if blk % 4 == 0:
                wu = load_w(mlp_w_in, blk * 128, 512)
                wz = load_w(mlp_w_in, 2 * E + blk * 128, 512)
            co = (blk % 4) * 128
            for q in range(nq):
                ug = c["ug"].next()
                zs = c["zs"].next()
                sg = c["sg"].next()
                pu = banks.next()
                for kk in range(8):
                    S.op("pe", lambda e, pu=pu, kk=kk, q=q, wu=wu, co=co: e.matmul(
                        pu[:], lhsT=wu[:, kk, co:co + 128], rhs=hT[:, kk, q * 512:(q + 1) * 512],
                        start=(kk == 0), stop=(kk == 7)), reads=bufs(wu, hT), writes=bufs(pu))
                S.op("act", lambda e, pu=pu, ug=ug: e.activation(out=ug[:], in_=pu[:], func=AF.Gelu),
                     reads=bufs(pu), writes=bufs(ug))
                pz = banks.next()
                for kk in range(8):
                    S.op("pe", lambda e, pz=pz, kk=kk, q=q, wz=wz, co=co: e.matmul(
                        pz[:], lhsT=wz[:, kk, co:co + 128], rhs=hT[:, kk, q * 512:(q + 1) * 512],
                        start=(kk == 0), stop=(kk == 7)), reads=bufs(wz, hT), writes=bufs(pz))
                S.op("act", lambda e, pz=pz, zs=zs: e.activation(out=zs[:], in_=pz[:], func=AF.Silu),
                     reads=bufs(pz), writes=bufs(zs))
                ps_ = banks.next()
                for cc in range(4):
                    t = q * 4 + cc
                    S.op("pe", lambda e, ps_=ps_, t=t, cc=cc, blk=blk, g=g: e.matmul(
                        ps_[:, cc * 128:(cc + 1) * 128], lhsT=vv[:, t, blk * 128:(blk + 1) * 128], rhs=c["wsT"][:, g, :],
                        start=True, stop=True), reads=bufs(vv, c["wsT"]), writes=bufs(ps_))
                S.op("dve", lambda e, ps_=ps_, blk=blk, sg=sg: e.scalar_tensor_tensor(
                    out=sg[:].rearrange("p (c i) -> p c i", c=4),
                    in0=ps_[:].rearrange("p (c i) -> p c i", c=4),
                    scalar=c["lngT"][:, blk:blk + 1],
                    in1=c["Bt"][:, blk:blk + 1, :].to_broadcast([128, 4, 128]), op0=ALU.mult, op1=ALU.add),
                    reads=bufs(ps_, c["lngT"], c["Bt"]), writes=bufs(sg))
                S.op("pool", lambda e, sg=sg, ug=ug: e.tensor_tensor(out=sg[:], in0=sg[:], in1=ug[:], op=ALU.mult),
                     reads=bufs(sg, ug), writes=bufs(sg))
                S.op("dve", lambda e, q=q, sg=sg, zs=zs, blk=blk: e.tensor_tensor(
                    out=yT[:, blk, q * 512:(q + 1) * 512], in0=sg[:], in1=zs[:], op=ALU.mult),
                    reads=bufs(sg, zs), writes=bufs(yT))

    SCALE = 0.125

    def nat_proj(hp, ntok, c, with_ktm):
        wt = wring.next()
        for j in range(4):
            S.dma("pool", wt[:, :, j * 128:(j + 1) * 128],
                  nat_w_in.rearrange("(k p) n -> p k n", p=128)[:, :, j * E + hp * 128:j * E + (hp + 1) * 128],
                  writes=bufs(wt))
        qT, kT, gT, vb = c["qT"], c["kT"], c["gT"], c["vb"]
        for q in range(ntok // 512):
            for j, dst, fn in ((0, qT, AF.Copy), (1, kT, AF.Copy), (3, gT, AF.Silu)):
                pb = banks.next()
                for kk in range(8):
                    S.op("pe", lambda e, pb=pb, kk=kk, q=q, j=j, wt=wt: e.matmul(
                        pb[:], lhsT=wt[:, kk, j * 128:(j + 1) * 128], rhs=hT[:, kk, q * 512:(q + 1) * 512],
                        start=(kk == 0), stop=(kk == 7)), reads=bufs(wt, hT), writes=bufs(pb))
                S.op("act", lambda e, pb=pb, q=q, dst=dst, fn=fn: e.activation(
                    out=dst[:, q * 512:(q + 1) * 512], in_=pb[:], func=fn), reads=bufs(pb), writes=bufs(dst))
        for t4 in range(ntok // 512):
            pv_ = banks.next()
            for tt in range(4):
                t = t4 * 4 + tt
                for kk in range(8):
                    S.op("pe", lambda e, pv_=pv_, kk=kk, t=t, tt=tt, wt=wt: e.matmul(
                        pv_[:, tt * 128:(tt + 1) * 128], lhsT=hT[:, kk, t * 128:(t + 1) * 128], rhs=wt[:, kk, 256:384],
                        start=(kk == 0), stop=(kk == 7)), reads=bufs(wt, hT), writes=bufs(pv_))
            if not with_ktm:
                S.op("act", lambda e, pv_=pv_, t4=t4: e.activation(
                    out=vb[:, t4 * 4:(t4 + 1) * 4, :].rearrange("p a b -> p (a b)"), in_=pv_[:], func=AF.Copy),
                    reads=bufs(pv_), writes=bufs(vb))
            else:
                vst, kst = c["vst"], c["kst"]
                S.op("act", lambda e, pv_=pv_, t4=t4, vst=vst: e.activation(
                    out=vst[:, t4 * 4:(t4 + 1) * 4, :].rearrange("p a b -> p (a b)"), in_=pv_[:], func=AF.Copy),
                    reads=bufs(pv_), writes=bufs(vst))
                S.op("pool", lambda e, t4=t4, vst=vst: e.tensor_copy(
                    out=vb[:, t4 * 4:(t4 + 1) * 4, :].rearrange("p a b -> p (a b)"),
                    in_=vst[:, t4 * 4:(t4 + 1) * 4, :].rearrange("p a b -> p (a b)")),
                    reads=bufs(vst), writes=bufs(vb))
                pk_ = banks.next()
                for tt in range(4):
                    t = t4 * 4 + tt
                    for kk in range(8):
                        S.op("pe", lambda e, pk_=pk_, kk=kk, t=t, tt=tt, wt=wt: e.matmul(
                            pk_[:, tt * 128:(tt + 1) * 128], lhsT=hT[:, kk, t * 128:(t + 1) * 128], rhs=wt[:, kk, 128:256],
                            start=(kk == 0), stop=(kk == 7)), reads=bufs(wt, hT), writes=bufs(pk_))
                S.op("act", lambda e, pk_=pk_, t4=t4, kst=kst: e.activation(
                    out=kst[:, t4 * 4:(t4 + 1) * 4, :].rearrange("p a b -> p (a b)"), in_=pk_[:], func=AF.Copy),
                    reads=bufs(pk_), writes=bufs(kst))

    def nat_ctx_unit():
        yT = L["yT"]
        c = {"qT": k.at([128, 1024], BF16), "kT": k.at([128, 1024], BF16), "gT": k.at([128, 1024], BF16),
             "vb": k.at([128, 8, 128], BF16)}
        vstr = k.aring(2, [128, 8, 128], F32)
        kstr = k.aring(2, [128, 8, 128], F32)
        er = k.aring(2, [128, 512], F32)
        pbr = k.aring(2, [128, 512], BF16)
        ptr_ = k.aring(2, [128, 512], BF16)
        for hp in range(cfg.get("ctx_hp", 16)):
            c["vst"] = vstr.next()
            c["kst"] = kstr.next()
            nat_proj(hp, 1024, c, True)
            for hd in range(2):
                h = hp * 2 + hd
                for sq in range(0 if cfg.get("no_kv") else 4):
                    S.dma("sp", new_k[sq, h, :, :].rearrange("(t p) d -> p t d", p=128),
                          c["kst"][:, sq * 2:(sq + 1) * 2, hd * 64:(hd + 1) * 64], reads=bufs(c["kst"]))
                    S.dma("sp", new_v[sq, h, :, :].rearrange("(t p) d -> p t d", p=128),
                          c["vst"][:, sq * 2:(sq + 1) * 2, hd * 64:(hd + 1) * 64], reads=bufs(c["vst"]))
            qT, kT, gT, vb = c["qT"], c["kT"], c["gT"], c["vb"]
            cb_ = Ring(banks.tiles[2:8])
            pob_ = Ring(banks.tiles[0:2])

            def c_qk(sq, hd):
                rows = slice(hd * 64, (hd + 1) * 64)
                tok0 = sq * 256
                ps_ = cb_.next()
                for qt in range(2):
                    S.op("pe", lambda e, ps_=ps_, qt=qt, rows=rows, tok0=tok0: e.matmul(
                        ps_[:, qt * 256:(qt + 1) * 256], lhsT=qT[rows, tok0 + qt * 128:tok0 + (qt + 1) * 128],
                        rhs=kT[rows, tok0:tok0 + 256], start=True, stop=True), reads=bufs(qT, kT), writes=bufs(ps_))
                return ps_

            def c_softmax(ps_):
                mx = small.next()
                S.op("dve", lambda e, ps_=ps_, mx=mx: e.tensor_reduce(
                    out=mx[:, 0:2], in_=ps_[:].rearrange("p (a b) -> p a b", a=2), axis=AX.X, op=ALU.max),
                    reads=bufs(ps_), writes=bufs(mx))
                S.op("dve", lambda e, mx=mx: e.tensor_scalar(out=mx[:, 2:4], in0=mx[:, 0:2], scalar1=-SCALE, scalar2=None,
                                                             op0=ALU.mult), reads=bufs(mx), writes=bufs(mx))
                et = er.next()
                for qt in range(2):
                    S.op("act", lambda e, ps_=ps_, mx=mx, et=et, qt=qt: e.activation(
                        out=et[:, qt * 256:(qt + 1) * 256], in_=ps_[:, qt * 256:(qt + 1) * 256], func=AF.Exp, scale=SCALE,
                        bias=mx[:, 2 + qt:3 + qt], accum_out=mx[:, 4 + qt:5 + qt]), reads=bufs(ps_, mx), writes=bufs(et, mx))
                S.op("dve", lambda e, mx=mx: e.reciprocal(out=mx[:, 6:8], in_=mx[:, 4:6]), reads=bufs(mx), writes=bufs(mx))
                pbt = pbr.next()
                S.op("dve", lambda e, mx=mx, et=et, pbt=pbt: e.tensor_tensor(
                    out=pbt[:].rearrange("p (a b) -> p a b", a=2), in0=et[:].rearrange("p (a b) -> p a b", a=2),
                    in1=mx[:, 6:8].unsqueeze(2).to_broadcast([128, 2, 256]), op=ALU.mult),
                    reads=bufs(mx, et), writes=bufs(pbt))
                return pbt

            def c_tpv(sq, hd, pbt, po):
                rows = slice(hd * 64, (hd + 1) * 64)
                ptb = cb_.next()
                ptv = ptb[:].bitcast(BF16)
                for j in range(4):
                    S.op("pe", lambda e, ptv=ptv, pbt=pbt, j=j: e.transpose(
                        out=ptv[:, j * 128:(j + 1) * 128], in_=pbt[:, j * 128:(j + 1) * 128], identity=identb[:]),
                        reads=bufs(pbt, identb), writes=bufs(ptb))
                pts = ptr_.next()
                S.op("act", lambda e, ptv=ptv, pts=pts: e.activation(out=pts[:], in_=ptv[:, 0:512], func=AF.Copy),
                     reads=bufs(ptb), writes=bufs(pts))
                for qt in range(2):
                    for kb in range(2):
                        S.op("pe", lambda e, po=po, rows=rows, qt=qt, kb=kb, sq=sq, hd=hd, pts=pts: e.matmul(
                            po[rows, qt * 128:(qt + 1) * 128], lhsT=vb[:, sq * 2 + kb, hd * 64:(hd + 1) * 64],
                            rhs=pts[:, (qt * 2 + kb) * 128:(qt * 2 + kb + 1) * 128], start=(kb == 0), stop=(kb == 1)),
                            reads=bufs(vb, pts), writes=bufs(po))

            its = [(sq, hd) for sq in range(0 if cfg.get("ctx_stage", 9) < 1 else 4) for hd in range(2)]
            nxt = c_qk(*its[0]) if its else None
            po = None
            for ii, (sq, hd) in enumerate(its):
                if hd == 0:
                    po = pob_.next()
                pbt = c_softmax(nxt)
                if ii + 1 < len(its):
                    nxt = c_qk(*its[ii + 1])
                c_tpv(sq, hd, pbt, po)
                if hd == 1:
                    tok0 = sq * 256
                    S.op("dve", lambda e, po=po, hp=hp, tok0=tok0: e.tensor_tensor(
                        out=yT[:, hp, tok0:tok0 + 256], in0=po[:, 0:256], in1=gT[:, tok0:tok0 + 256], op=ALU.mult),
                        reads=bufs(po, gT), writes=bufs(yT))

    def nat_lat_unit():
        yT = L["yT"]
        c = {"qT": k.at([128, 2048], BF16), "kT": k.at([128, 2048], BF16), "gT": k.at([128, 2048], BF16),
             "vb": k.at([128, 16, 128], BF16)}
        maskf = k.at([128, 3, 576], F32)
        maskb = k.at([128, 3, 576], BF16)
        for j in range(3):
            S.dma("sp", maskf[:, j, :], natmask[j], writes=bufs(maskf))
        S.op("dve", lambda e: e.tensor_copy(out=maskb[:], in_=maskf[:]), reads=bufs(maskf), writes=bufs(maskb))
        ckr = k.aring(2, [128, 2, 2, 64], BF16)
        cvr = k.aring(2, [128, 2, 2, 64], BF16)
        cktr = k.aring(2, [128, 256], BF16)
        rpr = k.aring(2, [128, 1024], F32)
        scr = k.aring(2, [128, 832], F32)
        pbr = k.aring(2, [128, 832], BF16)
        ptr_ = k.aring(2, [128, 896], BF16)
        cfg["alog"] = k.alog
        pobanks = Ring(banks.tiles[0:2])
        wbanks = Ring(banks.tiles[2:8])
        for hp in range(cfg.get("nat_hp", 16)):
            nat_proj(hp, 2048, c, False)
            qT, kT, gT, vb = c["qT"], c["kT"], c["gT"], c["vb"]
            ck = ckr.next()
            cv = cvr.next()
            for hd in range(2):
                S.dma("pool", ck[:, :, hd, :], cache_k[hp * 2 + hd].rearrange("(kb p) d -> p kb d", p=128), writes=bufs(ck))
                S.dma("pool", cv[:, :, hd, :], cache_v[hp * 2 + hd].rearrange("(kb p) d -> p kb d", p=128), writes=bufs(cv))
            ckT = cktr.next()
            ptb = banks.next()
            ptv = ptb[:].bitcast(BF16)
            for kb in range(2):
                S.op("pe", lambda e, ptv=ptv, ck=ck, kb=kb: e.transpose(
                    out=ptv[:, kb * 128:(kb + 1) * 128], in_=ck[:, kb, :, :].rearrange("p a b -> p (a b)"), identity=identb[:]),
                    reads=bufs(ck, identb), writes=bufs(ptb))
            S.op("act", lambda e, ptv=ptv, ckT=ckT: e.activation(out=ckT[:], in_=ptv[:, 0:256], func=AF.Copy),
                 reads=bufs(ptb), writes=bufs(ckT))
            rps = []
            for hd in range(2):
                rp = rpr.next()
                S.dma("sp", rp[:], rpbg[hp * 2 + hd], writes=bufs(rp))
                rps.append(rp)
            items = []
            for pg in range(cfg.get("nat_pg", 4)):
                for hd in range(cfg.get("nat_hd", 2)):
                    for pi in range(cfg.get("nat_pi", 4)):
                        items.append((pg, hd, pi))

            def geom(pg, hd, pi):
                pr = pg * 4 + pi
                r = 2 * pr
                if pr <= 1:
                    r0, nrow, a0, mi = 0, 9, 7 - r, 1
                elif pr >= 14:
                    r0, nrow, a0, mi = 24, 8, (3 if pr == 14 else 1), 2
                else:
                    r0, nrow, a0, mi = r - 4, 9, 3, 0
                return r, r0, nrow, a0, mi

            def st_qk(it):
                pg, hd, pi = it
                r, r0, nrow, a0, mi = geom(*it)
                rows = slice(hd * 64, (hd + 1) * 64)
                q0, k0 = r * 64, r0 * 64
                ps1 = wbanks.next()
                ps2 = wbanks.next()
                S.op("pe", lambda e, ps1=ps1, rows=rows, q0=q0, k0=k0: e.matmul(
                    ps1[:], lhsT=qT[rows, q0:q0 + 128], rhs=kT[rows, k0:k0 + 512], start=True, stop=False),
                    reads=bufs(qT, kT), writes=bufs(ps1))
                S.op("pe", lambda e, ps1=ps1, mi=mi: e.matmul(
                    ps1[:], lhsT=identb[:], rhs=maskb[:, mi, 0:512], start=False, stop=True),
                    reads=bufs(identb, maskb), writes=bufs(ps1))
                if nrow == 9:
                    S.op("pe", lambda e, ps2=ps2, rows=rows, q0=q0, k0=k0: e.matmul(
                        ps2[:, 0:64], lhsT=qT[rows, q0:q0 + 128], rhs=kT[rows, k0 + 512:k0 + 576], start=True, stop=False),
                        reads=bufs(qT, kT), writes=bufs(ps2))
                    S.op("pe", lambda e, ps2=ps2, mi=mi: e.matmul(
                        ps2[:, 0:64], lhsT=identb[:], rhs=maskb[:, mi, 512:576], start=False, stop=True),
                        reads=bufs(identb, maskb), writes=bufs(ps2))
                S.op("pe", lambda e, ps2=ps2, rows=rows, q0=q0, ckT=ckT: e.matmul(
                    ps2[:, 64:320], lhsT=qT[rows, q0:q0 + 128], rhs=ckT[rows, :], start=True, stop=True),
                    reads=bufs(qT, ckT), writes=bufs(ps2))
                return ps1, ps2

            def st_softmax(it, ps1, ps2):
                pg, hd, pi = it
                r, r0, nrow, a0, mi = geom(*it)
                rp = rps[hd]
                nk = nrow * 64
                sc = scr.next()
                S.op("dve", lambda e, ps1=ps1, sc=sc, rp=rp, a0=a0: e.scalar_tensor_tensor(
                    out=sc[:, 0:512], in0=ps1[:], scalar=SCALE, in1=rp[:, a0 * 64:a0 * 64 + 512],
                    op0=ALU.mult, op1=ALU.add), reads=bufs(ps1, rp), writes=bufs(sc))
                if nrow == 9:
                    S.op("dve", lambda e, ps2=ps2, sc=sc, rp=rp, a0=a0: e.scalar_tensor_tensor(
                        out=sc[:, 512:576], in0=ps2[:, 0:64], scalar=SCALE, in1=rp[:, a0 * 64 + 512:a0 * 64 + 576],
                        op0=ALU.mult, op1=ALU.add), reads=bufs(ps2, rp), writes=bufs(sc))
                S.op("act", lambda e, ps2=ps2, sc=sc, nk=nk: e.activation(
                    out=sc[:, nk:nk + 256], in_=ps2[:, 64:320], func=AF.Copy, scale=SCALE),
                    reads=bufs(ps2), writes=bufs(sc))
                ntot = nk + 256
                mx = small.next()
                S.op("dve", lambda e, sc=sc, mx=mx, ntot=ntot: e.tensor_reduce(
                    out=mx[:, 0:1], in_=sc[:, 0:ntot], axis=AX.X, op=ALU.max), reads=bufs(sc), writes=bufs(mx))
                S.op("dve", lambda e, mx=mx: e.tensor_scalar(out=mx[:, 1:2], in0=mx[:, 0:1], scalar1=-1.0, scalar2=None,
                                                             op0=ALU.mult), reads=bufs(mx), writes=bufs(mx))
                S.op("act", lambda e, sc=sc, mx=mx, ntot=ntot: e.activation(
                    out=sc[:, 0:ntot], in_=sc[:, 0:ntot], func=AF.Exp, bias=mx[:, 1:2], accum_out=mx[:, 2:3]),
                    reads=bufs(sc, mx), writes=bufs(sc, mx))
                S.op("dve", lambda e, mx=mx: e.reciprocal(out=mx[:, 3:4], in_=mx[:, 2:3]), reads=bufs(mx), writes=bufs(mx))
                pbt = pbr.next()
                S.op("dve", lambda e, sc=sc, mx=mx, pbt=pbt, ntot=ntot: e.tensor_scalar(
                    out=pbt[:, 0:ntot], in0=sc[:, 0:ntot], scalar1=mx[:, 3:4], scalar2=None, op0=ALU.mult),
                    reads=bufs(sc, mx), writes=bufs(pbt))
                return pbt

            def st_tpv(it, pbt, po):
                pg, hd, pi = it
                r, r0, nrow, a0, mi = geom(*it)
                rows = slice(hd * 64, (hd + 1) * 64)
                nk = nrow * 64
                ptb = wbanks.next()
                ptv = ptb[:].bitcast(BF16)
                blocks = [(j * 128, 128) for j in range(4)]
                blocks += [(nk, 128), (nk + 128, 128)]
                if nrow == 9:
                    blocks.append((512, 64))
                for j, (c0, w) in enumerate(blocks):
                    S.op("pe", lambda e, ptv=ptv, pbt=pbt, j=j, c0=c0, w=w: e.transpose(
                        out=ptv[0:w, j * 128:(j + 1) * 128], in_=pbt[:, c0:c0 + w], identity=identb[:]),
                        reads=bufs(pbt, identb), writes=bufs(ptb))
                nb = len(blocks)
                pts = ptr_.next()
                S.op("act", lambda e, ptv=ptv, pts=pts: e.activation(
                    out=pts[:, 0:768], in_=ptv[:, 0:768], func=AF.Copy), reads=bufs(ptb), writes=bufs(pts))
                if nrow == 9:
                    S.op("act", lambda e, ptv=ptv, pts=pts: e.activation(
                        out=pts[0:64, 768:896], in_=ptv[0:64, 768:896], func=AF.Copy), reads=bufs(ptb), writes=bufs(pts))
                t0 = r0 // 2
                for j, (c0, w) in enumerate(blocks):
                    if j < 4:
                        lhs = vb[:, t0 + j, hd * 64:(hd + 1) * 64]
                        rhs = pts[:, j * 128:(j + 1) * 128]
                        rd = bufs(vb, pts)
                    elif w == 64:
                        lhs = vb[0:64, t0 + 4, hd * 64:(hd + 1) * 64]
                        rhs = pts[0:64, j * 128:(j + 1) * 128]
                        rd = bufs(vb, pts)
                    else:
                        kb = j - 4
                        lhs = cv[:, kb, hd, :]
                        rhs = pts[:, j * 128:(j + 1) * 128]
                        rd = bufs(cv, pts)
                    S.op("pe", lambda e, po=po, rows=rows, pi=pi, lhs=lhs, rhs=rhs, j=j, nb=nb: e.matmul(
                        po[rows, pi * 128:(pi + 1) * 128], lhsT=lhs, rhs=rhs, start=(j == 0), stop=(j == nb - 1)),
                        reads=rd, writes=bufs(po))

            pos_ = {}
            nxt = st_qk(items[0]) if items else None
            for ii, it in enumerate(items):
                pg = it[0]
                if pg not in pos_:
                    pos_[pg] = pobanks.next()
                po = pos_[pg]
                ps1, ps2 = nxt
                pbt = st_softmax(it, ps1, ps2)
                if ii + 1 < len(items):
                    nxt = st_qk(items[ii + 1])
                st_tpv(it, pbt, po)
                if ii + 1 == len(items) or items[ii + 1][0] != pg:
                    S.op("dve", lambda e, po=po, hp=hp, pg=pg: e.tensor_tensor(
                        out=yT[:, hp, pg * 512:(pg + 1) * 512], in0=po[:], in1=gT[:, pg * 512:(pg + 1) * 512], op=ALU.mult),
                        reads=bufs(po, gT), writes=bufs(yT))

    def ssd_consts():
        c = {}
        c["tri"] = [k.at([128, 128], F32), k.at([128, 128], F32)]
        c["mneg"] = [k.at([128, 128], F32), k.at([128, 128], F32)]
        c["negones"] = k.at([128, 128], F32)
        for d_ in range(2):
            sgn = 1 if d_ == 0 else -1
            S.op("pool", lambda e, d_=d_: e.memset(c["tri"][d_][:], 1.0), writes=bufs(c["tri"][d_]))
            S.op("pool", lambda e, d_=d_, sgn=sgn: e.affine_select(
                out=c["tri"][d_][:], in_=c["tri"][d_][:], compare_op=ALU.is_ge, fill=0.0, base=0,
                pattern=[[sgn, 128]], channel_multiplier=-sgn), reads=bufs(c["tri"][d_]), writes=bufs(c["tri"][d_]))
            S.op("pool", lambda e, d_=d_: e.memset(c["mneg"][d_][:], 0.0), writes=bufs(c["mneg"][d_]))
            S.op("pool", lambda e, d_=d_, sgn=sgn: e.affine_select(
                out=c["mneg"][d_][:], in_=c["mneg"][d_][:], compare_op=ALU.is_ge, fill=-30000.0, base=0,
                pattern=[[sgn, 128]], channel_multiplier=-sgn), reads=bufs(c["mneg"][d_]), writes=bufs(c["mneg"][d_]))
        S.op("pool", lambda e: e.memset(c["negones"][:], -1.0), writes=bufs(c["negones"]))
        c["ntri"] = [k.at([128, 128], F32), k.at([128, 128], F32)]
        for d_ in range(2):
            S.op("pool", lambda e, d_=d_: e.tensor_scalar(out=c["ntri"][d_][:], in0=c["tri"][d_][:], scalar1=-1.0, scalar2=None,
                                                          op0=ALU.mult), reads=bufs(c["tri"][d_]), writes=bufs(c["ntri"][d_]))
        c["cwT"] = k.at([128, 32, 5], F32)
        c["cbT"] = k.at([128, 32], F32)
        for j in range(5):
            S.dma("sp", c["cwT"][:, :, j], ssd_conv_w[j].rearrange("(b p) -> p b", p=128), writes=bufs(c["cwT"]))
        S.dma("sp", c["cbT"][:], ssd_conv_b.rearrange("(b p) -> p b", p=128), writes=bufs(c["cbT"]))
        c["dtb"] = k.at([128, 64], F32)
        c["abc"] = k.at([128, 64], F32)
        c["dsk"] = k.at([128, 32], F32)
        c["ngT"] = k.at([128, 16], F32)
        S.dma("sp", c["dtb"][:], ssd_dt_bias.partition_broadcast(128), writes=bufs(c["dtb"]))
        S.dma("sp", c["abc"][:], ssd_a_log.partition_broadcast(128), writes=bufs(c["abc"]))
        S.dma("sp", c["dsk"][:], ssd_d.partition_broadcast(128), writes=bufs(c["dsk"]))
        S.dma("sp", c["ngT"][:], ssd_norm_g.rearrange("(b p) -> p b", p=128), writes=bufs(c["ngT"]))
        S.op("act", lambda e: e.activation(out=c["abc"][:], in_=c["abc"][:], func=AF.Exp), reads=bufs(c["abc"]), writes=bufs(c["abc"]))
        S.op("dve", lambda e: e.tensor_scalar(out=c["abc"][:], in0=c["abc"][:], scalar1=-1.0, scalar2=None, op0=ALU.mult),
             reads=bufs(c["abc"]), writes=bufs(c["abc"]))
        return c

    def ssd_unit(c, tok0, ntile, nseq, is_lat):
        T_ = ntile * 128
        nch = ntile // nseq
        Lq = nch * 128
        dt_ = k.at([128, ntile, 64], F32)
        da = k.at([128, ntile, 64], F32)
        ecum = k.at([128, ntile, 64], F32)
        dtd = k.at([128, ntile, 64], F32)
        etot = k.at([128, ntile, 64], F32)
        ssq = k.at([128, ntile, 8], F32)
        rstd = k.at([128, ntile], F32)
        tmpr = k.aring(2, [128, 64], F32)
        wdt = wring.next()
        S.dma("pool", wdt[:, :, 0:64], ssd_w_in.rearrange("(k p) n -> p k n", p=128)[:, :, 6144:6208], writes=bufs(wdt))
        for t in range(ntile):
            pb = banks.next()
            for kk in range(8):
                S.op("pe", lambda e, pb=pb, kk=kk, t=t: e.matmul(
                    pb[:, 0:64], lhsT=hT[:, kk, t * 128:(t + 1) * 128], rhs=wdt[:, kk, 0:64], start=(kk == 0), stop=(kk == 7)),
                    reads=bufs(hT, wdt), writes=bufs(pb))
            S.op("dve", lambda e, pb=pb, t=t: e.tensor_tensor(out=dt_[:, t, :], in0=pb[:, 0:64], in1=c["dtb"][:], op=ALU.add),
                 reads=bufs(pb, c["dtb"]), writes=bufs(dt_))
        S.op("act", lambda e: e.activation(out=dt_[:], in_=dt_[:], func=AF.Exp), reads=bufs(dt_), writes=bufs(dt_))
        S.op("act", lambda e: e.activation(out=dt_[:], in_=dt_[:], func=AF.Ln, bias=1.0), reads=bufs(dt_), writes=bufs(dt_))
        S.op("dve", lambda e: e.tensor_tensor(out=da[:], in0=dt_[:], in1=c["abc"][:].unsqueeze(1).to_broadcast([128, ntile, 64]),
                                              op=ALU.mult), reads=bufs(dt_, c["abc"]), writes=bufs(da))
        for t in range(ntile):
            pc = banks.next()
            S.op("pe", lambda e, pc=pc, t=t: e.matmul(pc[:, 0:32], lhsT=c["tri"][0][:], rhs=da[:, t, 0:32], start=True, stop=True),
                 reads=bufs(c["tri"][0], da), writes=bufs(pc))
            S.op("pe", lambda e, pc=pc, t=t: e.matmul(pc[:, 32:64], lhsT=c["tri"][1][:], rhs=da[:, t, 32:64], start=True, stop=True),
                 reads=bufs(c["tri"][1], da), writes=bufs(pc))
            S.op("pe", lambda e, pc=pc, t=t: e.matmul(pc[:, 64:128], lhsT=onesf[:], rhs=da[:, t, :], start=True, stop=True),
                 reads=bufs(onesf, da), writes=bufs(pc))
            cumt = tmpr.next()
            S.op("act", lambda e, pc=pc, cumt=cumt: e.activation(out=cumt[:], in_=pc[:, 0:64], func=AF.Identity),
                 reads=bufs(pc), writes=bufs(cumt))
            S.op("act", lambda e, pc=pc, t=t: e.activation(out=ecum[:, t, :], in_=pc[:, 0:64], func=AF.Exp),
                 reads=bufs(pc), writes=bufs(ecum))
            S.op("act", lambda e, pc=pc, t=t: e.activation(out=etot[:, t, :], in_=pc[:, 64:128], func=AF.Exp),
                 reads=bufs(pc), writes=bufs(etot))
            S.op("dve", lambda e, pc=pc, cumt=cumt: e.tensor_tensor(out=cumt[:], in0=pc[:, 64:128], in1=cumt[:], op=ALU.subtract),
                 reads=bufs(pc, cumt), writes=bufs(cumt))
            S.op("act", lambda e, cumt=cumt: e.activation(out=cumt[:], in_=cumt[:], func=AF.Exp), reads=bufs(cumt), writes=bufs(cumt))
            S.op("dve", lambda e, cumt=cumt, t=t: e.tensor_tensor(out=dtd[:, t, :], in0=dt_[:, t, :], in1=cumt[:], op=ALU.mult),
                 reads=bufs(cumt, dt_), writes=bufs(dtd))
        raw = k.at([128, T_], F32)
        acc = k.at([128, T_], F32)
        fm = [k.at([128, T_], BF16) for _ in range(4)]
        XB = k.at([128, ntile, 384], BF16)
        SIN = k.at([128, ntile, 2, 256], BF16)
        stf = k.at([128, 2, 256], F32)
        vTg = k.at([128, 2, T_], BF16)
        h0r = k.aring(2, [128, 2, 128], F32)
        fir = k.aring(2, [128, 2, 128], F32)
        GTr = k.aring(2, [128, 128], F32)
        Dr = k.aring(2, [128, 4, 128], F32)
        Lr = k.aring(2, [128, 4, 128], F32)
        Mr = k.aring(2, [128, 4, 128], BF16)
        xdr = k.aring(3, [128, 4, 64], BF16)
        yr = k.aring(4, [128, 256], F32)
        szr = k.aring(2, [128, 256], F32)
        vbr = k.aring(2, [128, 256], BF16)
        tmp4 = k.aring(2, [128, 4, 64], F32)
        for g in range(cfg.get("ssd_g", 8)):
            wA = wring.next()
            wap = ssd_w_in.rearrange("(k p) n -> p k n", p=128)
            S.dma("pool", wA[:, :, 0:256], wap[:, :, g * 256:(g + 1) * 256], writes=bufs(wA))
            S.dma("pool", wA[:, :, 256:512], wap[:, :, E + g * 256:E + (g + 1) * 256], writes=bufs(wA))
            wB = wring.next()
            S.dma("pool", wB[:, :, 0:128], wap[:, :, 2 * E + g * 128:2 * E + (g + 1) * 128], writes=bufs(wB))
            S.dma("pool", wB[:, :, 128:256], wap[:, :, 2 * E + 1024 + g * 128:2 * E + 1024 + (g + 1) * 128], writes=bufs(wB))
            for bi in range(4):
                wt, co, cblk = ((wA, 256, 2 * g), (wA, 384, 2 * g + 1), (wB, 0, 16 + g), (wB, 128, 24 + g))[bi]
                for q in range(T_ // 512):
                    pb = banks.next()
                    for kk in range(8):
                        S.op("pe", lambda e, pb=pb, kk=kk, q=q, wt=wt, co=co: e.matmul(
                            pb[:], lhsT=wt[:, kk, co:co + 128], rhs=hT[:, kk, q * 512:(q + 1) * 512],
                            start=(kk == 0), stop=(kk == 7)), reads=bufs(wt, hT), writes=bufs(pb))
                    S.op("act", lambda e, pb=pb, q=q: e.activation(out=raw[:, q * 512:(q + 1) * 512], in_=pb[:], func=AF.Copy),
                         reads=bufs(pb), writes=bufs(raw))
                cw = c["cwT"]
                rv = raw[:].rearrange("p (s l) -> p s l", s=nseq)
                av = acc[:].rearrange("p (s l) -> p s l", s=nseq)
                S.op("dve", lambda e, cblk=cblk: e.tensor_scalar(out=acc[:], in0=raw[:], scalar1=cw[:, cblk, 2:3], scalar2=None,
                                                                 op0=ALU.mult), reads=bufs(raw, cw), writes=bufs(acc))
                taps = ((0, "dve", slice(2, Lq), slice(0, Lq - 2)), (1, "dve", slice(1, Lq), slice(0, Lq - 1)),
                        (3, "dve", slice(0, Lq - 1), slice(1, Lq)), (4, "dve", slice(0, Lq - 2), slice(2, Lq)))
                for j, eng, osl, isl in taps:
                    S.op(eng, lambda e, j=j, osl=osl, isl=isl, cblk=cblk, rv=rv, av=av: e.scalar_tensor_tensor(
                        out=av[:, :, osl], in0=rv[:, :, isl], scalar=cw[:, cblk, j:j + 1], in1=av[:, :, osl],
                        op0=ALU.mult, op1=ALU.add), reads=bufs(raw, acc, cw), writes=bufs(acc))
                S.op("act", lambda e, bi=bi, cblk=cblk: e.activation(out=fm[bi][:], in_=acc[:], func=AF.Silu,
                                                                      bias=c["cbT"][:, cblk:cblk + 1]),
                     reads=bufs(acc, c["cbT"]), writes=bufs(fm[bi]))
            for t in range(ntile):
                ptb = banks.next()
                ptv = ptb[:].bitcast(BF16)
                for bi in range(3):
                    S.op("pe", lambda e, ptv=ptv, bi=bi, t=t: e.transpose(
                        out=ptv[:, bi * 128:(bi + 1) * 128], in_=fm[bi][:, t * 128:(t + 1) * 128], identity=identb[:]),
                        reads=bufs(fm[bi], identb), writes=bufs(ptb))
                S.op("act", lambda e, ptv=ptv, t=t: e.activation(out=XB[:, t, :], in_=ptv[:, 0:384], func=AF.Copy),
                     reads=bufs(ptb), writes=bufs(XB))
            BT, CT = fm[2], fm[3]
            for sq in range(nseq):
                for d_ in range(2):
                    hs = slice(d_ * 32 + 4 * g, d_ * 32 + 4 * g + 4)
                    if is_lat:
                        h0 = h0r.next()
                        for half in range(2):
                            S.dma("sp", h0[:, half, :], state_ssd[d_, 4 * g + 2 * half:4 * g + 2 * half + 2].rearrange("h p n -> (h p) n"),
                                  writes=bufs(h0))
                        ph = banks.next()
                        for half in range(2):
                            S.op("pe", lambda e, ph=ph, h0=h0, half=half: e.transpose(
                                out=ph[:, half * 128:(half + 1) * 128], in_=h0[:, half, :], identity=identf[:]),
                                reads=bufs(h0, identf), writes=bufs(ph))
                        S.op("act", lambda e, ph=ph, d_=d_: e.activation(out=stf[:, d_, :], in_=ph[:, 0:256], func=AF.Copy),
                             reads=bufs(ph), writes=bufs(stf))
                    else:
                        S.op("pool", lambda e, d_=d_: e.memset(stf[:, d_, :], 0.0), writes=bufs(stf))
                    order = range(nch) if d_ == 0 else range(nch - 1, -1, -1)
                    for ci in order:
                        t = sq * nch + ci
                        S.op("act", lambda e, t=t, d_=d_: e.activation(out=SIN[:, t, d_, :], in_=stf[:, d_, :], func=AF.Copy),
                             reads=bufs(stf), writes=bufs(SIN))
                        xdd = xdr.next()
                        S.op("pool", lambda e, xdd=xdd, t=t, hs=hs: e.tensor_tensor(
                            out=xdd[:], in0=XB[:, t, 0:256].rearrange("p (h d) -> p h d", h=4),
                            in1=dtd[:, t, hs].unsqueeze(2).to_broadcast([128, 4, 64]), op=ALU.mult),
                            reads=bufs(XB, dtd), writes=bufs(xdd))
                        psl = banks.next()
                        S.op("pe", lambda e, psl=psl, t=t, xdd=xdd: e.matmul(
                            psl[:, 0:256], lhsT=XB[:, t, 256:384], rhs=xdd[:].rearrange("p h d -> p (h d)"), start=True, stop=True),
                            reads=bufs(XB, xdd), writes=bufs(psl))
                        S.op("pool", lambda e, t=t, d_=d_, hs=hs: e.tensor_tensor(
                            out=stf[:, d_, :].rearrange("p (h d) -> p h d", h=4), in0=stf[:, d_, :].rearrange("p (h d) -> p h d", h=4),
                            in1=etot[:, t, hs].unsqueeze(2).to_broadcast([128, 4, 64]), op=ALU.mult),
                            reads=bufs(stf, etot), writes=bufs(stf))
                        S.op("dve", lambda e, psl=psl, d_=d_: e.tensor_tensor(out=stf[:, d_, :], in0=psl[:, 0:256], in1=stf[:, d_, :],
                                                                            op=ALU.add), reads=bufs(psl, stf), writes=bufs(stf))
                    if not is_lat:
                        pf = banks.next()
                        for half in range(2):
                            S.op("pe", lambda e, pf=pf, d_=d_, half=half: e.transpose(
                                out=pf[:, half * 128:(half + 1) * 128], in_=stf[:, d_, half * 128:(half + 1) * 128], identity=identf[:]),
                                reads=bufs(stf, identf), writes=bufs(pf))
                        fi = fir.next()
                        S.op("act", lambda e, pf=pf, fi=fi: e.activation(out=fi[:].rearrange("p a b -> p (a b)"), in_=pf[:, 0:256],
                                                                        func=AF.Copy), reads=bufs(pf), writes=bufs(fi))
                        for half in range(2):
                            S.dma("sp", new_ssd[sq, d_, 4 * g + 2 * half:4 * g + 2 * half + 2].rearrange("h p n -> (h p) n"),
                                  fi[:, half, :], reads=bufs(fi))
            for t in range(ntile):
                tsl = slice(t * 128, (t + 1) * 128)
                pg_ = banks.next()
                S.op("pe", lambda e, pg_=pg_, tsl=tsl: e.matmul(pg_[:, 0:128], lhsT=BT[:, tsl], rhs=CT[:, tsl], start=True, stop=True),
                     reads=bufs(BT, CT), writes=bufs(pg_))
                GT = GTr.next()
                S.op("act", lambda e, pg_=pg_, GT=GT: e.activation(out=GT[:], in_=pg_[:, 0:128], func=AF.Copy),
                     reads=bufs(pg_), writes=bufs(GT))
                py = banks.next()
                pos = []
                for d_ in range(2):
                    hs = slice(d_ * 32 + 4 * g, d_ * 32 + 4 * g + 4)
                    Dt = Dr.next()
                    S.op("pool", lambda e, Dt=Dt, d_=d_, t=t, hs=hs: e.tensor_tensor(
                        out=Dt[:], in0=c["tri"][d_][:].unsqueeze(1).to_broadcast([128, 4, 128]),
                        in1=da[:, t, hs].unsqueeze(2).to_broadcast([128, 4, 128]), op=ALU.mult),
                        reads=bufs(c["tri"][d_], da), writes=bufs(Dt))
                    pz_ = banks.next()
                    S.op("pe", lambda e, pz_=pz_, Dt=Dt: e.matmul(
                        pz_[:], lhsT=onesf[:], rhs=Dt[:].rearrange("p h t -> p (h t)"), start=True, stop=False),
                        reads=bufs(onesf, Dt), writes=bufs(pz_))
                    S.op("pe", lambda e, pz_=pz_, d_=d_, t=t, hs=hs: e.matmul(
                        pz_[:], lhsT=c["ntri"][d_][:], rhs=da[:, t, hs].unsqueeze(2).to_broadcast([128, 4, 128]), start=False, stop=False),
                        reads=bufs(c["ntri"][d_], da), writes=bufs(pz_))
                    S.op("pe", lambda e, pz_=pz_, d_=d_: e.matmul(
                        pz_[:], lhsT=identf[:], rhs=c["mneg"][d_][:].unsqueeze(1).to_broadcast([128, 4, 128]), start=False, stop=True),
                        reads=bufs(identf, c["mneg"][d_]), writes=bufs(pz_))
                    Lt = Lr.next()
                    S.op("act", lambda e, pz_=pz_, Lt=Lt: e.activation(out=Lt[:].rearrange("p h t -> p (h t)"), in_=pz_[:], func=AF.Exp),
                         reads=bufs(pz_), writes=bufs(Lt))
                    Mt = Mr.next()
                    S.op("dve", lambda e, Lt=Lt, Mt=Mt, GT=GT: e.tensor_tensor(
                        out=Mt[:], in0=Lt[:], in1=GT[:].unsqueeze(1).to_broadcast([128, 4, 128]), op=ALU.mult),
                        reads=bufs(Lt, GT), writes=bufs(Mt))
                    xd = xdr.next()
                    S.op("pool", lambda e, xd=xd, t=t, hs=hs: e.tensor_tensor(
                        out=xd[:], in0=XB[:, t, 0:256].rearrange("p (h d) -> p h d", h=4),
                        in1=dt_[:, t, hs].unsqueeze(2).to_broadcast([128, 4, 64]), op=ALU.mult),
                        reads=bufs(XB, dt_), writes=bufs(xd))
                    for h in range(4):
                        S.op("pe", lambda e, py=py, Mt=Mt, xd=xd, h=h, d_=d_: e.matmul(
                            py[:, h * 64:(h + 1) * 64], lhsT=Mt[:, h, :], rhs=xd[:, h, :], start=(d_ == 0 and h == 0), stop=(d_ == 1 and h == 3)),
                            reads=bufs(Mt, xd), writes=bufs(py))
                    po_ = banks.next()
                    S.op("pe", lambda e, po_=po_, tsl=tsl, t=t, d_=d_: e.matmul(
                        po_[:, 0:256], lhsT=CT[:, tsl], rhs=SIN[:, t, d_, :], start=True, stop=True),
                        reads=bufs(CT, SIN), writes=bufs(po_))
                    pos.append((po_, hs))
                y1 = yr.next()
                y2 = yr.next()
                for (po_, hs), yy in zip(pos, (y1, y2)):
                    S.op("dve", lambda e, po_=po_, hs=hs, yy=yy, t=t: e.tensor_tensor(
                        out=yy[:].rearrange("p (h d) -> p h d", h=4), in0=po_[:, 0:256].rearrange("p (h d) -> p h d", h=4),
                        in1=ecum[:, t, hs].unsqueeze(2).to_broadcast([128, 4, 64]), op=ALU.mult),
                        reads=bufs(po_, ecum), writes=bufs(yy))
                S.op("pool", lambda e, y1=y1, y2=y2: e.tensor_tensor(out=y1[:], in0=y1[:], in1=y2[:], op=ALU.add),
                     reads=bufs(y1, y2), writes=bufs(y1))
                S.op("pool", lambda e, y2=y2, t=t, g=g: e.tensor_tensor(
                    out=y2[:].rearrange("p (h d) -> p h d", h=4), in0=XB[:, t, 0:256].rearrange("p (h d) -> p h d", h=4),
                    in1=c["dsk"][:, 4 * g:4 * g + 4].unsqueeze(2).to_broadcast([128, 4, 64]), op=ALU.mult),
                    reads=bufs(XB, c["dsk"]), writes=bufs(y2))
                S.op("pool", lambda e, y1=y1, y2=y2: e.tensor_tensor(out=y1[:], in0=y1[:], in1=y2[:], op=ALU.add),
                     reads=bufs(y1, y2), writes=bufs(y1))
                S.op("dve", lambda e, py=py, y1=y1: e.tensor_tensor(out=y1[:], in0=py[:, 0:256], in1=y1[:], op=ALU.add),
                     reads=bufs(py, y1), writes=bufs(y1))
                pzz = banks.next()
                for kk in range(8):
                    S.op("pe", lambda e, pzz=pzz, kk=kk, tsl=tsl, wA=wA: e.matmul(
                        pzz[:, 0:256], lhsT=hT[:, kk, tsl], rhs=wA[:, kk, 0:256], start=(kk == 0), stop=(kk == 7)),
                        reads=bufs(hT, wA), writes=bufs(pzz))
                sz = szr.next()
                S.op("act", lambda e, pzz=pzz, sz=sz: e.activation(out=sz[:], in_=pzz[:, 0:256], func=AF.Silu),
                     reads=bufs(pzz), writes=bufs(sz))
                vb_ = vbr.next()
                S.op("pool", lambda e, vb_=vb_, y1=y1, sz=sz: e.tensor_tensor(out=vb_[:], in0=y1[:], in1=sz[:], op=ALU.mult),
                     reads=bufs(y1, sz), writes=bufs(vb_))
                S.op("act", lambda e, vb_=vb_, t=t, g=g: e.activation(out=junk[:, 0:256], in_=vb_[:], func=AF.Square,
                                                                     accum_out=ssq[:, t, g:g + 1]),
                     reads=bufs(vb_), writes=bufs(junk, ssq))
                ptb = banks.next()
                ptv = ptb[:].bitcast(BF16)
                for bb in range(2):
                    S.op("pe", lambda e, ptv=ptv, vb_=vb_, bb=bb: e.transpose(
                        out=ptv[:, bb * 128:(bb + 1) * 128], in_=vb_[:, bb * 128:(bb + 1) * 128], identity=identb[:]),
                        reads=bufs(vb_, identb), writes=bufs(ptb))
                for bb in range(2):
                    S.op("act", lambda e, ptv=ptv, bb=bb, tsl=tsl, g=g: e.activation(
                        out=vTg[:, bb, tsl], in_=ptv[:, bb * 128:(bb + 1) * 128], func=AF.Identity,
                        scale=c["ngT"][:, 2 * g + bb:2 * g + bb + 1]), reads=bufs(ptb, c["ngT"]), writes=bufs(vTg))
            S.dma("sp", yscr[2 * g:2 * g + 2, :, tok0:tok0 + T_].rearrange("b p t -> p b t"), vTg[:], reads=bufs(vTg))
        S.op("dve", lambda e: e.tensor_reduce(out=rstd[:], in_=ssq[:], axis=AX.X, op=ALU.add), reads=bufs(ssq), writes=bufs(rstd))
        S.op("dve", lambda e: e.tensor_scalar(out=rstd[:], in0=rstd[:], scalar1=1.0 / E, scalar2=EPS, op0=ALU.mult, op1=ALU.add),
             reads=bufs(rstd), writes=bufs(rstd))
        S.op("act", lambda e: e.activation(out=rstd[:], in_=rstd[:], func=AF.Sqrt), reads=bufs(rstd), writes=bufs(rstd))
        S.op("dve", lambda e: e.reciprocal(out=rstd[:], in_=rstd[:]), reads=bufs(rstd), writes=bufs(rstd))
        return rstd

    TWO_PI = float(2 * np.pi)
    MAGIC = 12582912.0

    def s5_prep():
        c = {}
        c["AA"] = k.at([128, 64, 2, 2], F32)
        c["BB"] = k.at([128, 64, 2, 2], F32)
        c["Wsel"] = k.at([128, 8, 240], BF16)
        c["dT"] = k.at([128, 16], F32)
        c["bgT"] = k.at([128, 16], F32)
        c["h0"] = k.at([128, 2, 2, 64], F32)
        S.dma("sp", c["dT"][:], s5_d.rearrange("(b p) -> p b", p=128), writes=bufs(c["dT"]))
        S.dma("sp", c["bgT"][:], s5_b_glu.rearrange("(b p) -> p b", p=128), writes=bufs(c["bgT"]))
        for d_ in range(2):
            S.dma("sp", c["h0"][:, d_, :, :], s5_h0[d_].rearrange("r p g -> p r g"), writes=bufs(c["h0"]))
        mm = k.amark()
        wself = k.at([128, 8, 240], F32)
        S.op("pool", lambda e: e.memset(wself[:], 0.0), writes=bufs(wself))
        S.op("pool", lambda e: e.affine_select(out=wself[:, :, 112:128], in_=wself[:, :, 112:128], compare_op=ALU.not_equal,
                                               fill=1.0, base=0, pattern=[[-16, 8], [-1, 16]], channel_multiplier=1),
             reads=bufs(wself), writes=bufs(wself))
        S.op("pool", lambda e: e.tensor_copy(out=c["Wsel"][:], in_=wself[:]), reads=bufs(wself), writes=bufs(c["Wsel"]))
        maskT = [k.at([128, 8, 16], F32), k.at([128, 8, 16], F32)]
        for d_ in range(2):
            S.op("pool", lambda e, d_=d_: e.memset(maskT[d_][:], 1.0), writes=bufs(maskT[d_]))
        S.op("pool", lambda e: e.affine_select(out=maskT[0][:], in_=maskT[0][:], compare_op=ALU.is_ge, fill=0.0, base=15,
                                               pattern=[[16, 8], [0, 16]], channel_multiplier=-1),
             reads=bufs(maskT[0]), writes=bufs(maskT[0]))
        S.op("pool", lambda e: e.affine_select(out=maskT[1][:], in_=maskT[1][:], compare_op=ALU.is_ge, fill=0.0, base=0,
                                               pattern=[[-16, 8], [0, 16]], channel_multiplier=1),
             reads=bufs(maskT[1]), writes=bufs(maskT[1]))
        pw = [[k.at([128, 64, 16], F32), k.at([128, 64, 16], F32)] for _ in range(2)]
        coef = [[k.at([128, 64], F32), k.at([128, 64], F32)] for _ in range(2)]
        lr, li, ls = k.at([128, 64], F32), k.at([128, 64], F32), k.at([128, 64], F32)
        xx, ang = k.at([128, 64], F32), k.at([128, 64], F32)
        tr = k.aring(6, [128, 64], F32)
        for d_ in range(2):
            S.dma("sp", lr[:], s5_lam[0, d_], writes=bufs(lr))
            S.dma("sp", li[:], s5_lam[1, d_], writes=bufs(li))
            S.dma("sp", ls[:], s5_lstep[d_], writes=bufs(ls))
            S.op("act", lambda e: e.activation(out=ls[:], in_=ls[:], func=AF.Exp), reads=bufs(ls), writes=bufs(ls))
            S.op("dve", lambda e: e.tensor_tensor(out=xx[:], in0=lr[:], in1=ls[:], op=ALU.mult), reads=bufs(lr, ls), writes=bufs(xx))
            S.op("dve", lambda e: e.tensor_tensor(out=ang[:], in0=li[:], in1=ls[:], op=ALU.mult), reads=bufs(li, ls), writes=bufs(ang))
            pre, pim = pw[d_]
            for kq in range(1, 9):
                mp, mn, sn, cs, t1, t2 = [tr.next() for _ in range(6)]
                S.op("act", lambda e, mp=mp, kq=kq: e.activation(out=mp[:], in_=xx[:], func=AF.Exp, scale=float(kq)),
                     reads=bufs(xx), writes=bufs(mp))
                S.op("act", lambda e, mn=mn, kq=kq: e.activation(out=mn[:], in_=xx[:], func=AF.Exp, scale=float(-kq)),
                     reads=bufs(xx), writes=bufs(mn))
                for dst, shift in ((sn, 0.0), (cs, 0.25)):
                    if shift:
                        S.op("dve", lambda e, t1=t1, kq=kq, shift=shift: e.tensor_scalar(
                            out=t1[:], in0=ang[:], scalar1=float(kq / TWO_PI), scalar2=shift, op0=ALU.mult, op1=ALU.add),
                            reads=bufs(ang), writes=bufs(t1))
                        S.op("dve", lambda e, t1=t1: e.tensor_scalar(out=t1[:], in0=t1[:], scalar1=MAGIC, scalar2=None, op0=ALU.add),
                             reads=bufs(t1), writes=bufs(t1))
                    else:
                        S.op("dve", lambda e, t1=t1, kq=kq: e.tensor_scalar(
                            out=t1[:], in0=ang[:], scalar1=float(kq / TWO_PI), scalar2=MAGIC, op0=ALU.mult, op1=ALU.add),
                            reads=bufs(ang), writes=bufs(t1))
                    S.op("dve", lambda e, t1=t1: e.tensor_scalar(out=t1[:], in0=t1[:], scalar1=-MAGIC, scalar2=-TWO_PI,
                                                                 op0=ALU.add, op1=ALU.mult), reads=bufs(t1), writes=bufs(t1))
                    S.op("dve", lambda e, t1=t1, kq=kq: e.scalar_tensor_tensor(
                        out=t1[:], in0=ang[:], scalar=float(kq), in1=t1[:], op0=ALU.mult, op1=ALU.add),
                        reads=bufs(ang, t1), writes=bufs(t1))
                    if shift:
                        S.op("dve", lambda e, t1=t1: e.tensor_scalar(out=t1[:], in0=t1[:], scalar1=float(np.pi / 2), scalar2=None,
                                                                     op0=ALU.add), reads=bufs(t1), writes=bufs(t1))
                    S.op("act", lambda e, t1=t1, dst=dst: e.activation(out=dst[:], in_=t1[:], func=AF.Sin),
                         reads=bufs(t1), writes=bufs(dst))
                S.op("dve", lambda e, kq=kq, mp=mp, cs=cs, pre=pre: e.tensor_tensor(out=pre[:, :, kq - 1], in0=mp[:], in1=cs[:], op=ALU.mult),
                     reads=bufs(mp, cs), writes=bufs(pre))
                S.op("dve", lambda e, kq=kq, mp=mp, sn=sn, pim=pim: e.tensor_tensor(out=pim[:, :, kq - 1], in0=mp[:], in1=sn[:], op=ALU.mult),
                     reads=bufs(mp, sn), writes=bufs(pim))
                S.op("dve", lambda e, kq=kq, mn=mn, cs=cs, pre=pre: e.tensor_tensor(out=pre[:, :, 7 + kq], in0=mn[:], in1=cs[:], op=ALU.mult),
                     reads=bufs(mn, cs), writes=bufs(pre))
                S.op("dve", lambda e, kq=kq, mn=mn, sn=sn, pim=pim: e.scalar_tensor_tensor(
                    out=pim[:, :, 7 + kq], in0=mn[:], scalar=-1.0, in1=sn[:], op0=ALU.mult, op1=ALU.mult),
                    reads=bufs(mn, sn), writes=bufs(pim))
            for r_ in range(2):
                S.op("act", lambda e, d_=d_, r_=r_, pre=pre: e.activation(out=c["AA"][:, :, d_, r_], in_=pre[:, :, 7], func=AF.Copy),
                     reads=bufs(pre), writes=bufs(c["AA"]))
            S.op("dve", lambda e, d_=d_, pim=pim: e.tensor_scalar(out=c["BB"][:, :, d_, 0], in0=pim[:, :, 7], scalar1=-1.0, scalar2=None,
                                                         op0=ALU.mult), reads=bufs(pim), writes=bufs(c["BB"]))
            S.op("act", lambda e, d_=d_, pim=pim: e.activation(out=c["BB"][:, :, d_, 1], in_=pim[:, :, 7], func=AF.Copy),
                 reads=bufs(pim), writes=bufs(c["BB"]))
            den, nr, t1, t2 = [tr.next() for _ in range(4)]
            S.op("dve", lambda e, den=den: e.tensor_tensor(out=den[:], in0=lr[:], in1=lr[:], op=ALU.mult), reads=bufs(lr), writes=bufs(den))
            S.op("dve", lambda e, t1=t1: e.tensor_tensor(out=t1[:], in0=li[:], in1=li[:], op=ALU.mult), reads=bufs(li), writes=bufs(t1))
            S.op("dve", lambda e, den=den, t1=t1: e.tensor_tensor(out=den[:], in0=den[:], in1=t1[:], op=ALU.add),
                 reads=bufs(den, t1), writes=bufs(den))
            S.op("dve", lambda e, den=den: e.reciprocal(out=den[:], in_=den[:]), reads=bufs(den), writes=bufs(den))
            S.op("dve", lambda e, nr=nr, pre=pre: e.tensor_scalar(out=nr[:], in0=pre[:, :, 0], scalar1=-1.0, scalar2=None, op0=ALU.add),
                 reads=bufs(pre), writes=bufs(nr))
            cr_, ci_ = coef[d_]
            S.op("dve", lambda e, nr=nr, t1=t1: e.tensor_tensor(out=t1[:], in0=nr[:], in1=lr[:], op=ALU.mult), reads=bufs(nr, lr), writes=bufs(t1))
            S.op("dve", lambda e, t2=t2, pim=pim: e.tensor_tensor(out=t2[:], in0=pim[:, :, 0], in1=li[:], op=ALU.mult), reads=bufs(pim, li), writes=bufs(t2))
            S.op("dve", lambda e, t1=t1, t2=t2: e.tensor_tensor(out=t1[:], in0=t1[:], in1=t2[:], op=ALU.add), reads=bufs(t1, t2), writes=bufs(t1))
            S.op("dve", lambda e, t1=t1, den=den, cr_=cr_: e.tensor_tensor(out=cr_[:], in0=t1[:], in1=den[:], op=ALU.mult),
                 reads=bufs(t1, den), writes=bufs(cr_))
            S.op("dve", lambda e, t1=t1, pim=pim: e.tensor_tensor(out=t1[:], in0=pim[:, :, 0], in1=lr[:], op=ALU.mult), reads=bufs(pim, lr), writes=bufs(t1))
            S.op("dve", lambda e, nr=nr, t2=t2: e.tensor_tensor(out=t2[:], in0=nr[:], in1=li[:], op=ALU.mult), reads=bufs(nr, li), writes=bufs(t2))
            S.op("dve", lambda e, t1=t1, t2=t2: e.tensor_tensor(out=t1[:], in0=t1[:], in1=t2[:], op=ALU.subtract), reads=bufs(t1, t2), writes=bufs(t1))
            S.op("dve", lambda e, t1=t1, den=den, ci_=ci_: e.tensor_tensor(out=ci_[:], in0=t1[:], in1=den[:], op=ALU.mult),
                 reads=bufs(t1, den), writes=bufs(ci_))
        Braw = [k.at([128, 8, 16], F32), k.at([128, 8, 16], F32)]
        Craw = [k.at([128, 8, 16], F32), k.at([128, 8, 16], F32)]
        Bb = [k.at([128, 8, 16], F32), k.at([128, 8, 16], F32)]
        V = [[k.at([128, 8, 8, 16], F32), k.at([128, 8, 8, 16], F32)] for _ in range(2)]
        W2 = [[k.at([128, 8, 8, 16], F32), k.at([128, 8, 8, 16], F32)] for _ in range(2)]
        t8 = k.aring(4, [128, 8, 16], F32)
        t8e = {"pool": k.aring(4, [128, 8, 16], F32), "dve": k.aring(4, [128, 8, 16], F32)}
        T16 = k.aring(2, [128, 16, 128], BF16)
        VT16 = k.aring(2, [128, 8, 2, 2, 128], BF16)
        W216 = k.aring(2, [128, 8, 2, 2, 128], BF16)
        Ttmp = k.aring(2, [128, 128], F32)
        Ttmp2 = k.aring(2, [128, 128], F32)

        def bc_j(ap2):
            return ap2.unsqueeze(2).to_broadcast([128, 8, 16])

        for b in range(8):
            gs = slice(8 * b, 8 * b + 8)
            t16, vt16, w216 = T16.next(), VT16.next(), W216.next()
            for d_ in range(2):
                pre, pim = pw[d_]
                cr_, ci_ = coef[d_]
                for r_ in range(2):
                    S.dma("sp", Braw[r_][:], s5_B[r_, d_, :, gs, :], writes=bufs(Braw[r_]))
                    S.dma("sp", Craw[r_][:], s5_C[r_, d_, :, gs, :], writes=bufs(Craw[r_]))
                ta, tb = t8.next(), t8.next()
                S.op("dve", lambda e, ta=ta, cr_=cr_, gs=gs: e.tensor_tensor(out=ta[:], in0=Braw[0][:], in1=bc_j(cr_[:, gs]), op=ALU.mult),
                     reads=bufs(Braw[0], cr_), writes=bufs(ta))
                S.op("dve", lambda e, tb=tb, ci_=ci_, gs=gs: e.tensor_tensor(out=tb[:], in0=Braw[1][:], in1=bc_j(ci_[:, gs]), op=ALU.mult),
                     reads=bufs(Braw[1], ci_), writes=bufs(tb))
                S.op("dve", lambda e, ta=ta, tb=tb: e.tensor_tensor(out=Bb[0][:], in0=ta[:], in1=tb[:], op=ALU.subtract),
                     reads=bufs(ta, tb), writes=bufs(Bb[0]))
                ta, tb = t8.next(), t8.next()
                S.op("dve", lambda e, ta=ta, cr_=cr_, gs=gs: e.tensor_tensor(out=ta[:], in0=Braw[1][:], in1=bc_j(cr_[:, gs]), op=ALU.mult),
                     reads=bufs(Braw[1], cr_), writes=bufs(ta))
                S.op("dve", lambda e, tb=tb, ci_=ci_, gs=gs: e.tensor_tensor(out=tb[:], in0=Braw[0][:], in1=bc_j(ci_[:, gs]), op=ALU.mult),
                     reads=bufs(Braw[0], ci_), writes=bufs(tb))
                S.op("dve", lambda e, ta=ta, tb=tb: e.tensor_tensor(out=Bb[1][:], in0=ta[:], in1=tb[:], op=ALU.add),
                     reads=bufs(ta, tb), writes=bufs(Bb[1]))
                for s_ in range(8):
                    kv = 8 + (s_ if d_ == 0 else 7 - s_)
                    kw = s_ if d_ == 0 else 7 - s_
                    for (eng, P_idx, X_, out_, neg_im) in (("pool", kv, Bb, V[d_], False), ("dve", kw, Craw, W2[d_], True)):
                        Pr = bc_j(pre[:, gs, P_idx])
                        Pi = bc_j(pim[:, gs, P_idx])
                        ta, tb = t8e[eng].next(), t8e[eng].next()
                        S.op(eng, lambda e, ta=ta, X_=X_, Pr=Pr: e.tensor_tensor(out=ta[:], in0=X_[0][:], in1=Pr, op=ALU.mult),
                             reads=bufs(X_[0], pre), writes=bufs(ta))
                        S.op(eng, lambda e, tb=tb, X_=X_, Pi=Pi: e.tensor_tensor(out=tb[:], in0=X_[1][:], in1=Pi, op=ALU.mult),
                             reads=bufs(X_[1], pim), writes=bufs(tb))
                        S.op(eng, lambda e, ta=ta, tb=tb, out_=out_, s_=s_: e.tensor_tensor(
                            out=out_[0][:, :, s_, :], in0=ta[:], in1=tb[:], op=ALU.subtract), reads=bufs(ta, tb), writes=bufs(out_[0]))
                        ta, tb = t8e[eng].next(), t8e[eng].next()
                        S.op(eng, lambda e, ta=ta, X_=X_, Pi=Pi: e.tensor_tensor(out=ta[:], in0=X_[0][:], in1=Pi, op=ALU.mult),
                             reads=bufs(X_[0], pim), writes=bufs(ta))
                        S.op(eng, lambda e, tb=tb, X_=X_, Pr=Pr: e.tensor_tensor(out=tb[:], in0=X_[1][:], in1=Pr, op=ALU.mult),
                             reads=bufs(X_[1], pre), writes=bufs(tb))
                        if not neg_im:
                            S.op(eng, lambda e, ta=ta, tb=tb, out_=out_, s_=s_: e.tensor_tensor(
                                out=out_[1][:, :, s_, :], in0=ta[:], in1=tb[:], op=ALU.add), reads=bufs(ta, tb), writes=bufs(out_[1]))
                        else:
                            S.op(eng, lambda e, ta=ta, tb=tb: e.tensor_tensor(out=ta[:], in0=ta[:], in1=tb[:], op=ALU.add),
                                 reads=bufs(ta, tb), writes=bufs(ta))
                            S.op(eng, lambda e, ta=ta, out_=out_, s_=s_: e.tensor_scalar(
                                out=out_[1][:, :, s_, :], in0=ta[:], scalar1=-1.0, scalar2=None, op0=ALU.mult),
                                reads=bufs(ta), writes=bufs(out_[1]))
                for r_ in range(2):
                    S.op("act", lambda e, d_=d_, r_=r_, w216=w216: e.activation(
                        out=w216[:, :, d_, r_, :], in_=W2[d_][r_][:].rearrange("p g s j -> p g (s j)"), func=AF.Copy),
                        reads=bufs(W2[d_][r_]), writes=bufs(w216))
                for gl in range(8):
                    pv_ = banks.next()
                    for r_ in range(2):
                        S.op("pe", lambda e, pv_=pv_, d_=d_, r_=r_, gl=gl: e.transpose(
                            out=pv_[:, r_ * 128:(r_ + 1) * 128], in_=V[d_][r_][:, gl, :, :].rearrange("p s j -> p (s j)"),
                            identity=identf[:]), reads=bufs(V[d_][r_], identf), writes=bufs(pv_))
                    S.op("act", lambda e, pv_=pv_, d_=d_, gl=gl, vt16=vt16: e.activation(
                        out=vt16[:, gl, d_, :, :].rearrange("p r m -> p (r m)"), in_=pv_[:, 0:256], func=AF.Copy),
                        reads=bufs(pv_), writes=bufs(vt16))
            for gl in range(8):
                for par in range(2):
                    rows = slice(par * 64, (par + 1) * 64)
                    gi = gl * 2 + par
                    pT = banks.next()
                    for d_ in range(2):
                        for r_ in range(2):
                            S.op("pe", lambda e, pT=pT, d_=d_, r_=r_, gl=gl, rows=rows: e.matmul(
                                pT[:, d_ * 128:(d_ + 1) * 128], lhsT=V[d_][r_][rows, gl, :, :].rearrange("p s j -> p (s j)"),
                                rhs=W2[d_][r_][rows, gl, :, :].rearrange("p s j -> p (s j)"), start=(r_ == 0), stop=(r_ == 1)),
                                reads=bufs(V[d_][r_], W2[d_][r_]), writes=bufs(pT))
                    ta, tb = Ttmp.next(), Ttmp2.next()
                    S.op("dve", lambda e, pT=pT, ta=ta: e.tensor_tensor(
                        out=ta[:], in0=pT[:, 0:128], in1=maskT[0][:].rearrange("p t j -> p (t j)"), op=ALU.mult),
                        reads=bufs(pT, maskT[0]), writes=bufs(ta))
                    S.op("dve", lambda e, pT=pT, tb=tb: e.tensor_tensor(
                        out=tb[:], in0=pT[:, 128:256], in1=maskT[1][:].rearrange("p t j -> p (t j)"), op=ALU.mult),
                        reads=bufs(pT, maskT[1]), writes=bufs(tb))
                    S.op("pool", lambda e, ta=ta, tb=tb, t16=t16, gi=gi: e.tensor_tensor(out=t16[:, gi, :], in0=ta[:], in1=tb[:], op=ALU.add),
                         reads=bufs(ta, tb), writes=bufs(t16))
            S.dma("sp", Tscr[b], t16[:].rearrange("p g m -> p (g m)"), reads=bufs(t16))
            S.dma("sp", VTscr[b], vt16[:].rearrange("p g d r m -> p (g d r m)"), reads=bufs(vt16))
            S.dma("sp", W2scr[b], w216[:].rearrange("p g d r m -> p (g d r m)"), reads=bufs(w216))
        k.arestore(mm)
        return c

    def s5_tiles(tok0, sub, is_lat):
        tiles = []
        for i in range(8):
            I_ = sub * 8 + i
            if not is_lat:
                s_, c0 = I_, 0
            else:
                s_, c0 = I_ // 2, (I_ % 2) * 128
            base = tok0 + 8 * c0 + s_
            tiles.append(((lambda src_, base=base: src_[base:base + 8 * 127 + 1:8, :]), I_ * 128))
        return tiles

    def s5_unit(c, tok0, is_lat):
        C_ = 256 if is_lat else 128
        nseq = 1 if is_lat else 4
        nch = C_ // nseq
        nct = C_ // 128
        Tn = 8 * C_
        U = k.at([128, nct, 16, 8, 16], BF16)
        X = k.at([128, 16, C_], BF16)
        arr = k.at([128, 16, 2, nseq, nch + 1], F32)
        Hb = k.at([128, 16, 2, nseq, nch + 1], BF16)
        Ysb = T(U.t[:].rearrange("p a g s j -> p (a g s j)").rearrange("p (g c) -> p g c", g=16))
        Ysb.b = U.b
        Tw = k.at([128, 16, 128], BF16)
        VTw = k.at([128, 8, 2, 2, 128], BF16)
        W2w = k.at([128, 8, 2, 2, 128], BF16)
        ygst = k.at([128, 2, Tn], BF16)
        uur = k.aring(2, [128, 512], F32)
        ysr = k.aring(2, [128, 512], F32)
        tmps = {eng: [k.at([128, 8, 2, nseq], F32) for _ in range(3)] for eng in ("dve", "pool")}
        GPB = 512 // C_
        bl = {}
        if is_lat:
            for eng in ("dve", "pool"):
                bl[eng] = {"PR": k.at([128, 8, 16], F32), "PI": k.at([128, 8, 16], F32),
                           "AAp": k.at([128, 8, 2, 16], F32), "BBp": k.at([128, 8, 2, 16], F32),
                           "cc": k.at([128, 8, 2, 17], F32),
                           "t": [k.at([128, 8, 8], F32) for _ in range(4)],
                           "l": [k.at([128, 8, 2, 16], F32) for _ in range(3)],
                           "c": [k.at([128, 8, 2], F32) for _ in range(2)],
                           "f": [[k.at([128, 8, 2, 16], F32) for _ in range(2)] for _ in range(2)]}
        for b in range(cfg.get("s5_nb", 8)):
            gs = slice(8 * b, 8 * b + 8)
            S.dma("sp", Tw[:].rearrange("p g m -> p (g m)"), Tscr[b], writes=bufs(Tw))
            S.dma("sp", VTw[:].rearrange("p g d r m -> p (g d r m)"), VTscr[b], writes=bufs(VTw))
            S.dma("sp", W2w[:].rearrange("p g d r m -> p (g d r m)"), W2scr[b], writes=bufs(W2w))
            wu = wring.next()
            S.dma("pool", wu[:, :, 0:256], s5_w_in.rearrange("(k p) n -> p k n", p=128)[:, :, 256 * b:256 * (b + 1)], writes=bufs(wu))
            for ct in range(nct):
                for s2 in range(4):
                    pb = banks.next()
                    for si in range(2):
                        s_ = s2 * 2 + si
                        p0 = s_ * C_ + ct * 128
                        for kk in range(8):
                            S.op("pe", lambda e, pb=pb, kk=kk, si=si, p0=p0, wu=wu: e.matmul(
                                pb[:, si * 256:(si + 1) * 256], lhsT=hT[:, kk, p0:p0 + 128], rhs=wu[:, kk, 0:256],
                                start=(kk == 0), stop=(kk == 7)), reads=bufs(hT, wu), writes=bufs(pb))
                    S.op("act", lambda e, pb=pb, ct=ct, s2=s2: e.activation(
                        out=U[:, ct, :, s2 * 2:s2 * 2 + 2, :], in_=pb[:].rearrange("p (s g j) -> p g s j", s=2, g=16), func=AF.Copy),
                        reads=bufs(pb), writes=bufs(U))
            for ct in range(nct):
                for g4 in range(4):
                    pb = banks.next()
                    for gg in range(4):
                        gi = g4 * 4 + gg
                        S.op("pe", lambda e, pb=pb, gg=gg, gi=gi, ct=ct: e.matmul(
                            pb[:, gg * 128:(gg + 1) * 128], lhsT=U[:, ct, gi, :, :].rearrange("p s j -> p (s j)"), rhs=identb[:], start=True, stop=True),
                            reads=bufs(U, identb), writes=bufs(pb))
                    S.op("act", lambda e, pb=pb, g4=g4, ct=ct: e.activation(
                        out=X[:, g4 * 4:g4 * 4 + 4, ct * 128:(ct + 1) * 128], in_=pb[:].rearrange("p (g c) -> p g c", g=4), func=AF.Copy),
                        reads=bufs(pb), writes=bufs(X))
            if is_lat:
                S.op("act", lambda e, gs=gs: e.activation(
                    out=arr[:, :, :, 0, 0].rearrange("p (g d) r -> p g d r", d=2),
                    in_=c["h0"][:, :, :, gs].rearrange("p d r g -> p g d r"), func=AF.Copy), reads=bufs(c["h0"]), writes=bufs(arr))
            else:
                S.op("pool", lambda e: e.memset(arr[:, :, :, :, 0:1], 0.0), writes=bufs(arr))
            for gl in range(8):
                pGs = [banks.next() for _ in range(nct)]
                for par in range(2):
                    gi = 2 * gl + par
                    rows = slice(par * 64, (par + 1) * 64)
                    for d_ in range(2):
                        if d_ == 0:
                            rhs = X[:, gi, :]
                        else:
                            rhs = X[:, gi, :].rearrange("p (s c) -> p s c", s=nseq)[:, :, ::-1]
                        for r_ in range(2):
                            if is_lat:
                                outp = pGs[d_][rows, r_ * 256:(r_ + 1) * 256]
                                pgb = pGs[d_]
                            else:
                                outp = pGs[0][rows, (d_ * 2 + r_) * 128:(d_ * 2 + r_ + 1) * 128]
                                pgb = pGs[0]
                            S.op("pe", lambda e, outp=outp, gl=gl, d_=d_, r_=r_, par=par, rhs=rhs: e.matmul(
                                outp, lhsT=VTw[:, gl, d_, r_, par * 64:(par + 1) * 64], rhs=rhs, start=True, stop=True),
                                reads=bufs(VTw, X), writes=bufs(pgb))
                for d_ in range(2):
                    if is_lat:
                        src_ = pGs[d_][:].rearrange("p (r s c) -> p r s c", r=2, s=1)
                        pgb = pGs[d_]
                    else:
                        src_ = pGs[0][:, d_ * 256:(d_ + 1) * 256].rearrange("p (r s c) -> p r s c", r=2, s=nseq)
                        pgb = pGs[0]
                    S.op("act", lambda e, src_=src_, gl=gl, d_=d_: e.activation(
                        out=arr[:, gl * 2 + d_, :, :, 1:nch + 1], in_=src_, func=AF.Copy), reads=bufs(pgb), writes=bufs(arr))
            AAb = c["AA"][:, gs, :, :].rearrange("p g d r -> p (g d) r")
            BBb = c["BB"][:, gs, :, :].rearrange("p g d r -> p (g d) r")
            if not is_lat:
                for kq in range(nch):
                    for eng, qs in (("dve", slice(0, 8)), ("pool", slice(8, 16))):
                        tt, p1, p2 = tmps[eng]
                        S.op(eng, lambda e, tt=tt, qs=qs, kq=kq: e.tensor_tensor(
                            out=tt[:], in0=arr[:, qs, :, :, kq], in1=arr[:, qs, :, :, kq + 1], op=ALU.add),
                            reads=bufs(arr), writes=bufs(tt))
                        S.op(eng, lambda e, tt=tt, p1=p1, qs=qs, AAb=AAb: e.tensor_tensor(
                            out=p1[:], in0=tt[:], in1=AAb[:, qs, :].unsqueeze(3).to_broadcast([128, 8, 2, nseq]), op=ALU.mult),
                            reads=bufs(tt, c["AA"]), writes=bufs(p1))
                        S.op(eng, lambda e, tt=tt, p2=p2, qs=qs, BBb=BBb: e.tensor_tensor(
                            out=p2[:], in0=tt[:, :, ::-1, :], in1=BBb[:, qs, :].unsqueeze(3).to_broadcast([128, 8, 2, nseq]), op=ALU.mult),
                            reads=bufs(tt, c["BB"]), writes=bufs(p2))
                        S.op(eng, lambda e, p1=p1, p2=p2, qs=qs, kq=kq: e.tensor_tensor(
                            out=arr[:, qs, :, :, kq + 1], in0=p1[:], in1=p2[:], op=ALU.add), reads=bufs(p1, p2), writes=bufs(arr))
                S.op("act", lambda e: e.activation(out=Hb[:].rearrange("p q r s c -> p (q r s c)"),
                                                   in_=arr[:].rearrange("p q r s c -> p (q r s c)"), func=AF.Copy),
                     reads=bufs(arr), writes=bufs(Hb))
            else:
                NB_, BL_ = 16, 16
                for eng, qs in (("dve", slice(0, 8)), ("pool", slice(8, 16))):
                    B_ = bl[eng]
                    PR, PI, AAp, BBp, cc = B_["PR"], B_["PI"], B_["AAp"], B_["BBp"], B_["cc"]
                    tA, tB, tC, tD = B_["t"]
                    AAh = AAb[:, qs, :]
                    BBh = BBb[:, qs, :]
                    S.op(eng, lambda e, PR=PR, AAh=AAh: e.tensor_copy(out=PR[:, :, 0], in_=AAh[:, :, 0]), reads=bufs(c["AA"]), writes=bufs(PR))
                    S.op(eng, lambda e, PI=PI, BBh=BBh: e.tensor_copy(out=PI[:, :, 0], in_=BBh[:, :, 1]), reads=bufs(c["BB"]), writes=bufs(PI))
                    m_ = 1
                    while m_ < 16:
                        ar = PR[:, :, m_ - 1:m_].to_broadcast([128, 8, m_])
                        ai = PI[:, :, m_ - 1:m_].to_broadcast([128, 8, m_])
                        src_r, src_i = PR[:, :, 0:m_], PI[:, :, 0:m_]
                        dst_r, dst_i = PR[:, :, m_:2 * m_], PI[:, :, m_:2 * m_]
                        ta, tb = tA[:, :, 0:m_], tB[:, :, 0:m_]
                        tc_, td = tC[:, :, 0:m_], tD[:, :, 0:m_]
                        S.op(eng, lambda e, ta=ta, src_r=src_r, ar=ar: e.tensor_tensor(out=ta, in0=src_r, in1=ar, op=ALU.mult), reads=bufs(PR), writes=bufs(tA))
                        S.op(eng, lambda e, tb=tb, src_i=src_i, ai=ai: e.tensor_tensor(out=tb, in0=src_i, in1=ai, op=ALU.mult), reads=bufs(PI), writes=bufs(tB))
                        S.op(eng, lambda e, tc_=tc_, src_r=src_r, ai=ai: e.tensor_tensor(out=tc_, in0=src_r, in1=ai, op=ALU.mult), reads=bufs(PR, PI), writes=bufs(tC))
                        S.op(eng, lambda e, td=td, src_i=src_i, ar=ar: e.tensor_tensor(out=td, in0=src_i, in1=ar, op=ALU.mult), reads=bufs(PR, PI), writes=bufs(tD))
                        S.op(eng, lambda e, dst_r=dst_r, ta=ta, tb=tb: e.tensor_tensor(out=dst_r, in0=ta, in1=tb, op=ALU.subtract), reads=bufs(tA, tB), writes=bufs(PR))
                        S.op(eng, lambda e, dst_i=dst_i, tc_=tc_, td=td: e.tensor_tensor(out=dst_i, in0=tc_, in1=td, op=ALU.add), reads=bufs(tC, tD), writes=bufs(PI))
                        m_ *= 2
                    for r_ in range(2):
                        S.op(eng, lambda e, AAp=AAp, PR=PR, r_=r_: e.tensor_copy(out=AAp[:, :, r_, :], in_=PR[:]), reads=bufs(PR), writes=bufs(AAp))
                    S.op(eng, lambda e, BBp=BBp, PI=PI: e.tensor_scalar(out=BBp[:, :, 0, :], in0=PI[:], scalar1=-1.0, scalar2=None, op0=ALU.mult),
                         reads=bufs(PI), writes=bufs(BBp))
                    S.op(eng, lambda e, BBp=BBp, PI=PI: e.tensor_copy(out=BBp[:, :, 1, :], in_=PI[:]), reads=bufs(PI), writes=bufs(BBp))
                for eng, qs in (("dve", slice(0, 8)), ("pool", slice(8, 16))):
                    B_ = bl[eng]
                    AAp, BBp, cc = B_["AAp"], B_["BBp"], B_["cc"]
                    t3, p13, p23 = B_["l"]
                    AAh = AAb[:, qs, :].unsqueeze(3).to_broadcast([128, 8, 2, NB_])
                    BBh = BBb[:, qs, :].unsqueeze(3).to_broadcast([128, 8, 2, NB_])
                    xv = arr[:, qs, :, 0, 1:257].rearrange("p q r (b i) -> p q r b i", i=BL_)
                    for i_ in range(BL_):
                        if i_ == 0:
                            src_t = xv[:, :, :, :, 0]
                        else:
                            S.op(eng, lambda e, t3=t3, xv=xv, i_=i_: e.tensor_tensor(
                                out=t3[:], in0=xv[:, :, :, :, i_ - 1], in1=xv[:, :, :, :, i_], op=ALU.add), reads=bufs(arr), writes=bufs(t3))
                            src_t = t3[:]
                        rd = bufs(arr) if i_ == 0 else bufs(t3)
                        src_sw = src_t[:, :, ::-1, :]
                        S.op(eng, lambda e, p13=p13, src_t=src_t, AAh=AAh: e.tensor_tensor(out=p13[:], in0=src_t, in1=AAh, op=ALU.mult),
                             reads=rd + bufs(c["AA"]), writes=bufs(p13))
                        S.op(eng, lambda e, p23=p23, src_sw=src_sw, BBh=BBh: e.tensor_tensor(out=p23[:], in0=src_sw, in1=BBh, op=ALU.mult),
                             reads=rd + bufs(c["BB"]), writes=bufs(p23))
                        S.op(eng, lambda e, p13=p13, p23=p23, xv=xv, i_=i_: e.tensor_tensor(
                            out=xv[:, :, :, :, i_], in0=p13[:], in1=p23[:], op=ALU.add), reads=bufs(p13, p23), writes=bufs(arr))
                for eng, qs in (("dve", slice(0, 8)), ("pool", slice(8, 16))):
                    B_ = bl[eng]
                    AAp, BBp, cc = B_["AAp"], B_["BBp"], B_["cc"]
                    c1, c2 = B_["c"]
                    xv = arr[:, qs, :, 0, 1:257].rearrange("p q r (b i) -> p q r b i", i=BL_)
                    S.op(eng, lambda e, cc=cc, qs=qs: e.tensor_copy(out=cc[:, :, :, 0], in_=arr[:, qs, :, 0, 0]), reads=bufs(arr), writes=bufs(cc))
                    for Bk in range(NB_):
                        S.op(eng, lambda e, c1=c1, cc=cc, AAp=AAp, Bk=Bk: e.tensor_tensor(
                            out=c1[:], in0=cc[:, :, :, Bk], in1=AAp[:, :, :, 15], op=ALU.mult), reads=bufs(cc, AAp), writes=bufs(c1))
                        S.op(eng, lambda e, c2=c2, cc=cc, BBp=BBp, Bk=Bk: e.tensor_tensor(
                            out=c2[:], in0=cc[:, :, ::-1, Bk], in1=BBp[:, :, :, 15], op=ALU.mult), reads=bufs(cc, BBp), writes=bufs(c2))
                        S.op(eng, lambda e, c1=c1, c2=c2: e.tensor_tensor(out=c1[:], in0=c1[:], in1=c2[:], op=ALU.add),
                             reads=bufs(c1, c2), writes=bufs(c1))
                        S.op(eng, lambda e, c1=c1, cc=cc, xv=xv, Bk=Bk: e.tensor_tensor(
                            out=cc[:, :, :, Bk + 1], in0=c1[:], in1=xv[:, :, :, Bk, 15], op=ALU.add), reads=bufs(c1, arr), writes=bufs(cc))
                for eng, qs in (("dve", slice(0, 8)), ("pool", slice(8, 16))):
                    B_ = bl[eng]
                    AAp, BBp, cc = B_["AAp"], B_["BBp"], B_["cc"]
                    xv = arr[:, qs, :, 0, 1:257].rearrange("p q r (b i) -> p q r b i", i=BL_)
                    hv = Hb[:, qs, :, 0, 1:257].rearrange("p q r (b i) -> p q r b i", i=BL_)
                    fr = B_["f"]
                    S.op(eng, lambda e, qs=qs: e.tensor_copy(out=Hb[:, qs, :, 0, 0], in_=arr[:, qs, :, 0, 0]), reads=bufs(arr), writes=bufs(Hb))
                    pend = []
                    for i_ in range(BL_ + 1):
                        if i_ < BL_:
                            f1, f2 = fr[i_ % 2]
                            S.op(eng, lambda e, f1=f1, cc=cc, AAp=AAp, i_=i_: e.tensor_tensor(
                                out=f1[:], in0=cc[:, :, :, 0:NB_], in1=AAp[:, :, :, i_:i_ + 1].to_broadcast([128, 8, 2, NB_]), op=ALU.mult),
                                reads=bufs(cc, AAp), writes=bufs(f1))
                            S.op(eng, lambda e, f2=f2, cc=cc, BBp=BBp, i_=i_: e.tensor_tensor(
                                out=f2[:], in0=cc[:, :, ::-1, 0:NB_], in1=BBp[:, :, :, i_:i_ + 1].to_broadcast([128, 8, 2, NB_]), op=ALU.mult),
                                reads=bufs(cc, BBp), writes=bufs(f2))
                        if i_ >= 1:
                            j_ = i_ - 1
                            f1, f2 = fr[j_ % 2]
                            S.op(eng, lambda e, f1=f1, f2=f2: e.tensor_tensor(out=f1[:], in0=f1[:], in1=f2[:], op=ALU.add),
                                 reads=bufs(f1, f2), writes=bufs(f1))
                            S.op(eng, lambda e, f1=f1, xv=xv, hv=hv, j_=j_: e.tensor_tensor(
                                out=hv[:, :, :, :, j_], in0=f1[:], in1=xv[:, :, :, :, j_], op=ALU.add), reads=bufs(f1, arr), writes=bufs(Hb))
            if not is_lat:
                for sq in range(4):
                    for d_ in range(2):
                        for r_ in range(2):
                            S.dma("sp", new_s5[sq, d_, r_].rearrange("(gp two) n -> (two n) gp", two=2)[:, gs],
                                  arr[:, d_:16:2, r_, sq, nch], reads=bufs(arr))
            for g0 in range(0, 16, GPB):
                pb = banks.next()
                for gg in range(GPB):
                    gi = g0 + gg
                    gl, par = gi // 2, gi % 2
                    rows = slice(par * 64, (par + 1) * 64)
                    yreg = pb[:, gg * C_:(gg + 1) * C_]
                    S.op("pe", lambda e, yreg=yreg, gi=gi: e.matmul(yreg, lhsT=Tw[:, gi, :], rhs=X[:, gi, :], start=True, stop=False),
                         reads=bufs(Tw, X), writes=bufs(pb))
                    for d_ in range(2):
                        for r_ in range(2):
                            hsl = Hb[rows, gl * 2 + d_, r_, :, 0:nch]
                            if d_ == 1:
                                hsl = hsl[:, :, ::-1]
                            S.op("pe", lambda e, yreg=yreg, gl=gl, d_=d_, r_=r_, rows=rows, hsl=hsl: e.matmul(
                                yreg, lhsT=W2w[rows, gl, d_, r_, :], rhs=hsl, start=False, stop=(d_ == 1 and r_ == 1)),
                                reads=bufs(W2w, Hb), writes=bufs(pb))
                S.op("act", lambda e, pb=pb, g0=g0: e.activation(
                    out=Ysb[:, g0:g0 + GPB, :].rearrange("p g c -> p (g c)"), in_=pb[:], func=AF.Copy), reads=bufs(pb), writes=bufs(Ysb))
            for blk in range(2):
                for t0 in range(0, 8, GPB):
                    psel = banks.next()
                    puu = banks.next()
                    for tt_ in range(GPB):
                        t = t0 + tt_
                        for g_ in range(8):
                            S.op("pe", lambda e, psel=psel, tt_=tt_, t=t, g_=g_, blk=blk: e.matmul(
                                psel[:, tt_ * C_:(tt_ + 1) * C_], lhsT=c["Wsel"][:, t, 112 - 16 * g_:240 - 16 * g_],
                                rhs=Ysb[:, blk * 8 + g_, :], start=(g_ == 0), stop=(g_ == 7)),
                                reads=bufs(c["Wsel"], Ysb), writes=bufs(psel))
                        for kk in range(8):
                            S.op("pe", lambda e, puu=puu, tt_=tt_, t=t, kk=kk, blk=blk, wu=wu: e.matmul(
                                puu[:, tt_ * C_:(tt_ + 1) * C_], lhsT=wu[:, kk, blk * 128:(blk + 1) * 128],
                                rhs=hT[:, kk, t * C_:(t + 1) * C_], start=(kk == 0), stop=(kk == 7)),
                                reads=bufs(wu, hT), writes=bufs(puu))
                    uus = uur.next()
                    ysm = ysr.next()
                    S.op("act", lambda e, puu=puu, uus=uus: e.activation(out=uus[:], in_=puu[:], func=AF.Copy),
                         reads=bufs(puu), writes=bufs(uus))
                    S.op("dve", lambda e, psel=psel, uus=uus, ysm=ysm, b=b, blk=blk: e.scalar_tensor_tensor(
                        out=ysm[:], in0=uus[:], scalar=c["dT"][:, 2 * b + blk:2 * b + blk + 1], in1=psel[:], op0=ALU.mult, op1=ALU.add),
                        reads=bufs(psel, uus, c["dT"]), writes=bufs(ysm))
                    S.op("act", lambda e, ysm=ysm, blk=blk, t0=t0: e.activation(
                        out=ygst[:, blk, t0 * C_:t0 * C_ + 512], in_=ysm[:], func=AF.Gelu), reads=bufs(ysm), writes=bufs(ygst))
            S.dma("sp", yscr[2 * b:2 * b + 2, :, tok0:tok0 + Tn].rearrange("b p t -> p b t"), ygst[:], reads=bufs(ygst))

    def s5_glu(c, tok0, sub, yT):
        ygT = k.at([128, 16, 1024], BF16)
        S.dma("sp", ygT[:], yscr[:, :, tok0 + sub * 1024:tok0 + (sub + 1) * 1024].rearrange("b p t -> p b t"), writes=bufs(ygT))
        wgr = k.aring(2, [128, 16, 128], BF16)
        sgr = k.aring(2, [128, 512], F32)
        szr = k.aring(2, [128, 512], F32)
        for blk in range(16):
            wg = wgr.next()
            S.dma("pool", wg[:], s5_w_glu.rearrange("(k p) n -> p k n", p=128)[:, :, blk * 128:(blk + 1) * 128], writes=bufs(wg))
            if blk % 4 == 0:
                wz = load_w(s5_w_in, E + blk * 128, 512)
            co = (blk % 4) * 128
            for q in range(2):
                p0 = sub * 1024 + q * 512
                pg_ = banks.next()
                for kk in range(16):
                    S.op("pe", lambda e, pg_=pg_, kk=kk, wg=wg, q=q: e.matmul(
                        pg_[:], lhsT=wg[:, kk, :], rhs=ygT[:, kk, q * 512:(q + 1) * 512], start=(kk == 0), stop=(kk == 15)),
                        reads=bufs(wg, ygT), writes=bufs(pg_))
                sg = sgr.next()
                S.op("act", lambda e, pg_=pg_, sg=sg, blk=blk: e.activation(
                    out=sg[:], in_=pg_[:], func=AF.Sigmoid, bias=c["bgT"][:, blk:blk + 1]), reads=bufs(pg_, c["bgT"]), writes=bufs(sg))
                pz = banks.next()
                for kk in range(8):
                    S.op("pe", lambda e, pz=pz, kk=kk, wz=wz, co=co, p0=p0: e.matmul(
                        pz[:], lhsT=wz[:, kk, co:co + 128], rhs=hT[:, kk, p0:p0 + 512], start=(kk == 0), stop=(kk == 7)),
                        reads=bufs(wz, hT), writes=bufs(pz))
                sz = szr.next()
                S.op("act", lambda e, pz=pz, sz=sz: e.activation(out=sz[:], in_=pz[:], func=AF.Silu), reads=bufs(pz), writes=bufs(sz))
                S.op("pool", lambda e, sg=sg, blk=blk, q=q: e.tensor_tensor(
                    out=sg[:], in0=sg[:], in1=ygT[:, blk, q * 512:(q + 1) * 512], op=ALU.mult), reads=bufs(sg, ygT), writes=bufs(sg))
                S.op("dve", lambda e, sg=sg, sz=sz, blk=blk, p0=p0: e.tensor_tensor(
                    out=yT[:, blk, p0:p0 + 512], in0=sg[:], in1=sz[:], op=ALU.mult), reads=bufs(sg, sz), writes=bufs(yT))

    def std_tiles(tok0, n):
        return [(rows_std(tok0 + i * 128), i * 128) for i in range(n)]

    units = [(0, 8, 0), (1024, 8, 1), (2048, 8, 1)]
    src = cfg.get("src", None) and inp("xsrc", [NTOK, D]) or xin
    for li in layers:
        last = final and (li == layers[-1])
        dst = xres
        k.areset()
        phase_a(li)
        if li == 1:
            L["yT"] = k.at([128, 16, 1024], BF16)
            c = gmlp_consts()
            for (tok0, nt, cond) in units:
                tiles = std_tiles(tok0, nt)
                phase_b(src, tiles, cond)
                gmlp_unit(c, nt)
                load_wout(li)
                phase_d(src, dst, tiles, cond, last)
        if li == 0:
            c = ssd_consts()
            m0 = k.amark()
            for (tok0, nt, nseq, cond) in ((0, 8, 4, 0), (1024, 16, 1, 1)):
                tiles = std_tiles(tok0, nt)
                phase_b(src, tiles, cond)
                rstd = ssd_unit(c, tok0, nt, nseq, cond == 1)
                S.op("act", lambda e, rstd=rstd, nt=nt: e.activation(out=rstd_keep[:, 0:nt], in_=rstd[:], func=AF.Copy),
                     reads=bufs(rstd), writes=bufs(rstd_keep))
                k.arestore(m0)
                load_wout(li)
                phase_d(src, dst, tiles, cond, last, scale_t=lambda i: (rstd_keep[:, i:i + 1], rstd_keep.b), ytok0=tok0)
                S.barrier()
        if li == 2:
            c = s5_prep()
            m0 = k.amark()
            for (tok0, is_lat, cond) in ((0, False, 0), (1024, True, 1)):
                nsub = 2 if is_lat else 1
                tiles = []
                for sub in range(nsub):
                    tiles += s5_tiles(tok0, sub, is_lat)
                phase_b(src, tiles, cond)
                s5_unit(c, tok0, is_lat)
                k.arestore(m0)
                L["yT"] = k.at([128, 16, 1024 * nsub], BF16)
                m1 = k.amark()
                for sub in range(nsub):
                    s5_glu(c, tok0, sub, L["yT"])
                    k.arestore(m1)
                load_wout(li)
                phase_d(src, dst, tiles, cond, last)
                k.arestore(m0)
        if li == 3:
            L["yT"] = k.at([128, 16, 2048], BF16)
            tiles = std_tiles(0, 8)
            phase_b(src, tiles, 0)
            m_ = k.amark()
            if not cfg.get("skip_ctx"):
                nat_ctx_unit()
            k.arestore(m_)
            load_wout(li)
            phase_d(src, dst, tiles, 0, last)
            S.barrier()
            tiles = std_tiles(1024, 16)
            phase_b(src, tiles, 1)
            m_ = k.amark()
            nat_lat_unit()
            if not cfg.get("skip_d"):
                k.arestore(m_)
            load_wout(li)
            phase_d(src, dst, tiles, 1, last)
        src = xres
    if not final and not cfg.get("skip_d"):
        S.barrier()
        xring = k.aring(3, [128, D], F32)
        for i in range(NTOK // 128):
            xt = xring.next()
            S.dma("sp", xt[:], xres[i * 128:(i + 1) * 128, :], writes=bufs(xt))
            S.dma("sp", y_out[i * 128:(i + 1) * 128, :], xt[:], reads=bufs(xt))
    S.emit(es)
    return nc, es


def host_inputs(inputs, core):
    f = np.ascontiguousarray
    m = {}
    m["xin"] = f(np.concatenate([inputs["x_prompt"][4 * core:4 * core + 4].reshape(NP_TOK, D),
                                 inputs["x_sample"][core % 2]], axis=0))
    m["cvec"] = f(np.stack([inputs["c_ctx"], inputs["c"][core % 2]], axis=0))
    for nm in ["norm_g", "w_mod", "b_mod", "w_out", "final_g"]:
        m[nm] = f(inputs[nm])
    m["mlp_w_in"] = f(inputs["mlp_w_in"][0])
    m["mlp_ln_g"] = f(inputs["mlp_ln_g"][0])
    m["mlp_ln_b"] = f(inputs["mlp_ln_b"][0])
    m["mlp_w_sT"] = f(np.transpose(inputs["mlp_w_s"][0], (0, 2, 1)))
    m["mlp_b_s"] = f(inputs["mlp_b_s"][0])
    m["ssd_w_in"] = f(inputs["ssd_w_in"][0])
    m["ssd_conv_w"] = f(inputs["ssd_conv_w"][0])
    m["ssd_conv_b"] = f(inputs["ssd_conv_b"][0])
    m["ssd_dt_bias"] = f(inputs["ssd_dt_bias"][0].reshape(64))
    m["ssd_a_log"] = f(inputs["ssd_a_log"][0].reshape(64))
    m["ssd_d"] = f(inputs["ssd_d"][0])
    m["ssd_norm_g"] = f(inputs["ssd_norm_g"][0])
    m["state_ssd"] = f(inputs["state_ssd"][core % 2, 0])
    m["s5_w_in"] = f(inputs["s5_w_in"][0])

    def pl(a):
        sh = a.shape[:-2]
        a = a.reshape(sh + (64, 2, 64))
        return np.moveaxis(a, -3, -1).reshape(sh + (128, 64))
    m["s5_lam"] = f(np.stack([pl(inputs["s5_lam_re"][0]), pl(inputs["s5_lam_im"][0])], 0))
    m["s5_lstep"] = f(pl(np.broadcast_to(inputs["s5_log_step"][0][:, :, None], (2, 128, 64))))

    def plj(a):
        a = a.reshape(2, 64, 2, 64, 16)
        return np.transpose(a, (0, 2, 3, 1, 4)).reshape(2, 128, 64, 16)
    m["s5_B"] = f(np.stack([plj(inputs["s5_b_re"][0]), plj(inputs["s5_b_im"][0])], 0))
    m["s5_C"] = f(np.stack([plj(np.transpose(inputs["s5_c_re"][0], (0, 1, 3, 2))),
                            plj(np.transpose(inputs["s5_c_im"][0], (0, 1, 3, 2)))], 0))
    m["s5_h0"] = f(pl(inputs["state_s5"][core % 2, 0]))
    m["s5_d"] = f(inputs["s5_d"][0])
    m["s5_w_glu"] = f(inputs["s5_w_glu"][0])
    m["s5_b_glu"] = f(inputs["s5_b_glu"][0])
    m["nat_w_in"] = f(inputs["nat_w_in"][0])
    m["rpbg"] = rpb_gather(inputs["nat_rpb"][0])
    m["natmask"] = nat_masks()
    m["cache_k"] = f(inputs["cache_k"][core % 2, 0])
    m["cache_v"] = f(inputs["cache_v"][core % 2, 0])
    return m


def rpb_gather(rpb):
    qc = np.arange(64)[:, None]
    kc = np.arange(64)[None, :]
    ci = np.clip(kc - qc + 15, 0, 30)
    out = np.zeros((32, 128, 16, 64), np.float32)
    g = rpb[:, :, ci]
    g = np.transpose(g, (0, 2, 1, 3))
    out[:, 0:64, 0:15, :] = g
    out[:, 64:128, 1:16, :] = g
    return np.ascontiguousarray(out.reshape(32, 128, 1024))


def nat_masks():
    NEG = -30000.0 * 8.0
    qc = np.arange(64)
    cs = np.clip(qc - 8, 0, 48)
    kc = np.arange(64)
    col_ok = (kc[None, :] >= cs[:, None]) & (kc[None, :] < cs[:, None] + 16)
    m = np.zeros((3, 128, 9, 64), np.float32)
    colm = np.where(col_ok, 0.0, NEG).astype(np.float32)
    m[:, 0:64] += colm[None, :, None, :]
    m[:, 64:128] += colm[None, :, None, :]
    m[0, 0:64, 8, :] = NEG
    m[0, 64:128, 0, :] = NEG
    m[1, :, 8, :] = NEG
    return np.ascontiguousarray(m.reshape(3, 128, 576))


def kernel(**inputs):
    inputs = {k_: np.asarray(v) for k_, v in inputs.items()}
    nc, es = build({})
    with es:
        in_maps = [host_inputs(inputs, c) for c in range(8)]
        res = run_bass_kernel_spmd(nc, in_maps, core_ids=list(range(8)))
    r = res.results
    y_prompt = np.concatenate([r[c]["y_out"][:NP_TOK].reshape(4, 256, D) for c in range(8)], axis=0)
    y_sample = np.stack([r[c]["y_out"][NP_TOK:] for c in range(2)], axis=0)
    new_ssd = np.concatenate([r[c]["new_ssd"] for c in range(8)], axis=0)[:, None]
    new_s5 = np.concatenate([r[c]["new_s5"] for c in range(8)], axis=0)[:, None]
    new_k = np.concatenate([r[c]["new_k"] for c in range(8)], axis=0)[:, None]
    new_v = np.concatenate([r[c]["new_v"] for c in range(8)], axis=0)[:, None]
    return (y_prompt.astype(np.float32), y_sample.astype(np.float32), np.ascontiguousarray(new_ssd, dtype=np.float32),
            np.ascontiguousarray(new_s5, dtype=np.float32), np.ascontiguousarray(new_k, dtype=np.float32),
            np.ascontiguousarray(new_v, dtype=np.float32))
```

```python
import numpy as np
from contextlib import ExitStack
import concourse.bass as bass
import concourse.mybir as mybir
from concourse.bass_utils import run_bass_kernel_spmd

F32 = mybir.dt.float32
BF16 = mybir.dt.bfloat16
AF = mybir.ActivationFunctionType
ALU = mybir.AluOpType
AX = mybir.AxisListType

D = 1024
E = 2048
NP_TOK = 1024
NS_TOK = 2048
NTOK = NP_TOK + NS_TOK
EPS = 1e-6
COMPUTE = ("pe", "act", "dve", "pool")
NDMASEM = 12
SAME_ENGINE_SYNC = True


class Buf:
    __slots__ = ("lw", "rd")

    def __init__(self):
        self.lw = None
        self.rd = {}


class Sched:
    def __init__(self, nc):
        self.nc = nc
        self.ops = {e: [] for e in COMPUTE + ("sp",)}
        self.cnt = {e: 0 for e in COMPUTE}
        self.seen = {e: {} for e in COMPUTE + ("sp",)}
        self.dma_slot = {}
        self.dma_val = {}
        self.sems = {}
        self.refd = {e: set() for e in COMPUTE}

    def _deps(self, eng, reads, writes):
        deps = {}

        def add(tok):
            if tok is None:
                return
            k, v = tok
            if deps.get(k, 0) < v:
                deps[k] = v

        for r in reads:
            add(r.lw)
        for w in writes:
            add(w.lw)
            for k, v in w.rd.items():
                add((k, v))
        out = []
        seen = self.seen[eng]
        for k, v in deps.items():
            if k == eng and (eng == "pe" or not SAME_ENGINE_SYNC):
                continue
            if seen.get(k, 0) >= v:
                continue
            seen[k] = v
            out.append((k, v))
            if isinstance(k, str):
                self.refd[k].add(v)
        return out

    def _mark(self, tok, reads, writes):
        k, v = tok
        for r in reads:
            if r.rd.get(k, 0) < v:
                r.rd[k] = v
        for w in writes:
            w.lw = tok
            w.rd = {}

    def op(self, eng, fn, reads=(), writes=()):
        waits = self._deps(eng, reads, writes)
        self.cnt[eng] += 1
        tok = (eng, self.cnt[eng])
        self.ops[eng].append((waits, fn, tok, 1))
        self._mark(tok, reads, writes)

    def dma(self, q, out, in_, reads=(), writes=()):
        slot = self.dma_slot.get(q, 0)
        self.dma_slot[q] = (slot + 1) % NDMASEM
        key = ("dma", q, slot)
        prev = self.dma_val.get(key, 0)
        waits = self._deps(q, reads, writes)
        if prev > 0 and self.seen[q].get(key, 0) < prev:
            self.seen[q][key] = prev
            waits.append((key, prev))
        val = prev + 16
        self.dma_val[key] = val
        tok = (key, val)

        def fn(e, out=out, in_=in_):
            return e.dma_start(out=out, in_=in_, allow_slow_non_contiguous=True)

        self.ops[q].append((waits, fn, tok, 16))
        self._mark(tok, reads, writes)

    def barrier(self):
        targets = [(e, self.cnt[e]) for e in COMPUTE if self.cnt[e] > 0]
        targets += [(key, v) for key, v in self.dma_val.items()]
        for eng in COMPUTE + ("sp",):
            waits = []
            for key, v in targets:
                if key == eng:
                    continue
                if self.seen[eng].get(key, 0) < v:
                    self.seen[eng][key] = v
                    waits.append((key, v))
                    if isinstance(key, str):
                        self.refd[key].add(v)
            if waits:
                self.ops[eng].append((waits, None, None, 0))

    def emit(self, es, final_wait_engine="sp"):
        nc = self.nc
        keys = list(COMPUTE)
        for q in self.dma_slot:
            for s in range(NDMASEM):
                if ("dma", q, s) in self.dma_val:
                    keys.append(("dma", q, s))
        for k in keys:
            nm = k if isinstance(k, str) else "d_%s_%d" % (k[1], k[2])
            self.sems[k] = es.enter_context(nc.semaphore("s_" + nm))
        fin = []
        for k in keys:
            v = self.cnt[k] if isinstance(k, str) else self.dma_val[k]
            if v > 0 and k != final_wait_engine:
                fin.append((k, v))
                if isinstance(k, str):
                    self.refd[k].add(v)
        rank = {}
        for e_ in COMPUTE:
            r_ = {}
            for n_, idx in enumerate(sorted(self.refd[e_])):
                r_[idx] = n_ + 1
            rank[e_] = r_

        def semval(k, v):
            return rank[k][v] if isinstance(k, str) else v
        block = es.enter_context(nc.Block())

        def run(e, name):
            for waits, fn, tok, inc in self.ops[name]:
                if fn is None:
                    for k, v in waits:
                        e.wait_ge(self.sems[k], semval(k, v))
                    continue
                NW = 1
                for k, v in waits[NW:]:
                    e.wait_ge(self.sems[k], semval(k, v))
                ins = fn(e)
                for k, v in waits[:NW]:
                    ins._wait_ge(self.sems[k], semval(k, v))
                if not isinstance(tok[0], str) or tok[1] in self.refd[tok[0]]:
                    ins.then_inc(self.sems[tok[0]], inc)
            if name == final_wait_engine:
                for k, v in fin:
                    e.wait_ge(self.sems[k], semval(k, v))

        @block.tensor
        def _(e):
            run(e, "pe")

        @block.scalar
        def _(e):
            run(e, "act")

        @block.vector
        def _(e):
            run(e, "dve")

        @block.gpsimd
        def _(e):
            run(e, "pool")

        @block.sync
        def _(e):
            run(e, "sp")


class T:
    __slots__ = ("t", "b")

    def __init__(self, t):
        self.t = t
        self.b = Buf()

    def __getitem__(self, k):
        return self.t[k]


class Ring:
    def __init__(self, tiles):
        self.tiles = tiles
        self.i = 0

    def next(self):
        t = self.tiles[self.i]
        self.i = (self.i + 1) % len(self.tiles)
        return t


class K:
    def __init__(self, nc, es):
        self.nc = nc
        self.es = es
        self.S = Sched(nc)
        self.n = 0

    def sb(self, shape, dt, name=None):
        self.n += 1
        return T(self.es.enter_context(self.nc.sbuf_tensor(name or "sb%d" % self.n, list(shape), dt)))

    def ring(self, n, shape, dt):
        return Ring([self.sb(shape, dt) for _ in range(n)])

    def psb(self, shape, dt):
        self.n += 1
        return T(self.es.enter_context(self.nc.psum_tensor("ps%d" % self.n, list(shape), dt)))

    def init_arena(self, nbytes):
        self.arena = self.es.enter_context(self.nc.sbuf_tensor("arena", [128, nbytes // 2], BF16))
        self.asize = nbytes
        self.aoff = 0
        self.alog = []

    def areset(self):
        self.S.barrier()
        self.aoff = 0

    def at(self, shape, dt):
        esz = 4 if dt == F32 else 2
        n = 1
        for d_ in shape[1:]:
            n *= d_
        nb = (n * esz + 63) // 64 * 64
        assert self.aoff + nb <= self.asize, ("arena overflow", self.aoff, nb, self.asize)
        ap = self.arena[0:shape[0], self.aoff // 2:(self.aoff + n * esz) // 2]
        if dt == F32:
            ap = ap.bitcast(F32)
        if len(shape) > 2:
            names = ["d%d" % i for i in range(len(shape) - 1)]
            kw = {names[i]: shape[i + 1] for i in range(len(names) - 1)}
            ap = ap.rearrange("p (%s) -> p %s" % (" ".join(names), " ".join(names)), **kw)
        self.alog.append((self.aoff, tuple(shape), dt))
        self.aoff += nb
        return T(ap)

    def amark(self):
        return self.aoff

    def arestore(self, m):
        self.S.barrier()
        self.aoff = m

    def aring(self, n, shape, dt):
        return Ring([self.at(shape, dt) for _ in range(n)])

    def dram(self, name, shape, dt, kind="Internal"):
        return self.nc.dram_tensor(name, list(shape), dt, kind=kind).ap()


def bufs(*ts):
    return [t.b for t in ts]


def build(cfg):
    layers = cfg.get("layers", [0, 1, 2, 3])
    final = cfg.get("final", True)
    nc = bass.Bass("TRN2", target_bir_lowering=False)
    es = ExitStack()
    k = K(nc, es)
    S = k.S
    I = {}

    def inp(name, shape):
        I[name] = k.dram(name, shape, F32, kind="ExternalInput")
        return I[name]

    xin = inp("xin", [NTOK, D])
    cvec = inp("cvec", [2, D])
    norm_g = inp("norm_g", [4, D])
    w_mod = inp("w_mod", [4, D, 3 * D])
    b_mod = inp("b_mod", [4, 3 * D])
    w_out = inp("w_out", [4, E, D])
    final_g = inp("final_g", [D])
    mlp_w_in = inp("mlp_w_in", [D, 3 * E])
    mlp_ln_g = inp("mlp_ln_g", [E])
    mlp_ln_b = inp("mlp_ln_b", [E])
    mlp_w_sT = inp("mlp_w_sT", [8, 128, 128])
    mlp_b_s = inp("mlp_b_s", [8, 128])
    ssd_w_in = inp("ssd_w_in", [D, 6208])
    ssd_conv_w = inp("ssd_conv_w", [5, 4096])
    ssd_conv_b = inp("ssd_conv_b", [4096])
    ssd_dt_bias = inp("ssd_dt_bias", [64])
    ssd_a_log = inp("ssd_a_log", [64])
    ssd_d = inp("ssd_d", [32])
    ssd_norm_g = inp("ssd_norm_g", [E])
    state_ssd = inp("state_ssd", [2, 32, 64, 128])
    new_ssd = k.dram("new_ssd", [4, 2, 32, 64, 128], F32, kind="ExternalOutput")
    yscr = k.dram("yscr", [16, 128, NTOK], BF16)
    s5_w_in = inp("s5_w_in", [D, 2 * E])
    s5_lam = inp("s5_lam", [2, 2, 128, 64])
    s5_lstep = inp("s5_lstep", [2, 128, 64])
    s5_B = inp("s5_B", [2, 2, 128, 64, 16])
    s5_C = inp("s5_C", [2, 2, 128, 64, 16])
    s5_h0 = inp("s5_h0", [2, 2, 128, 64])
    s5_d = inp("s5_d", [E])
    s5_w_glu = inp("s5_w_glu", [E, E])
    s5_b_glu = inp("s5_b_glu", [E])
    new_s5 = k.dram("new_s5", [4, 2, 2, 128, 64], F32, kind="ExternalOutput")
    Tscr = k.dram("Tscr", [8, 128, 16 * 128], BF16)
    VTscr = k.dram("VTscr", [8, 128, 8 * 4 * 128], BF16)
    W2scr = k.dram("W2scr", [8, 128, 8 * 4 * 128], BF16)
    nat_w_in = inp("nat_w_in", [D, 4 * E])
    rpbg = inp("rpbg", [32, 128, 1024])
    natmask = inp("natmask", [3, 128, 576])
    cache_k = inp("cache_k", [32, 256, 64])
    cache_v = inp("cache_v", [32, 256, 64])
    new_k = k.dram("new_k", [4, 32, 256, 64], F32, kind="ExternalOutput")
    new_v = k.dram("new_v", [4, 32, 256, 64], F32, kind="ExternalOutput")
    y_out = k.dram("y_out", [NTOK, D], F32, kind="ExternalOutput")
    xres = k.dram("xres", [NTOK, D], F32)
    dma_done = Buf()

    identf = k.sb([128, 128], F32)
    identb = k.sb([128, 128], BF16)
    onesf = k.sb([128, 128], F32)
    S.op("pool", lambda e: e.memset(identf[:], 0.0), writes=bufs(identf))
    S.op("pool", lambda e: e.affine_select(out=identf[:], in_=identf[:], compare_op=ALU.not_equal, fill=1.0,
                                           base=0, pattern=[[-1, 128]], channel_multiplier=1),
         reads=bufs(identf), writes=bufs(identf))
    S.op("dve", lambda e: e.tensor_copy(out=identb[:], in_=identf[:]), reads=bufs(identf), writes=bufs(identb))
    S.op("pool", lambda e: e.memset(onesf[:], 1.0), writes=bufs(onesf))

    banks = Ring([k.psb([128, 512], F32) for _ in range(8)])

    hT = k.sb([128, 8, 2048], BF16, "hT")
    wo = T(hT.t)
    wo.b = hT.b
    wo_view = hT.t[:].rearrange("p k t -> p (k t)").rearrange("p (k n) -> p k n", k=16)
    wring = k.ring(3, [128, 8, 512], BF16)
    junk = k.sb([128, D], BF16)
    small = k.ring(8, [128, 8], F32)
    rstd_keep = k.sb([128, 16], F32)
    k.init_arena(136 * 1024)
    L = {}

    cf = k.sb([128, 8, 2], F32)
    cb = k.sb([128, 8, 2], BF16)
    for c_ in range(2):
        S.dma("sp", cf[:, :, c_], cvec[c_].rearrange("(k p) -> p k", p=128), writes=bufs(cf))
    S.op("act", lambda e: e.activation(out=cb[:], in_=cf[:], func=AF.Silu), reads=bufs(cf), writes=bufs(cb))

    modT = k.sb([128, 16, 2], F32)
    bmodT = k.sb([128, 16], F32)
    ngT = k.sb([128, 8], F32)
    Asc = k.sb([128, 8, 2], F32)
    gate_bc = [k.sb([128, D], F32), k.sb([128, D], F32)]
    sel = [k.sb([2, 128], F32), k.sb([2, 128], F32)]
    for c in range(2):
        S.op("pool", lambda e, c=c: e.memset(sel[c][:], 0.0), writes=bufs(sel[c]))
        S.op("pool", lambda e, c=c: e.affine_select(out=sel[c][:], in_=sel[c][:], compare_op=ALU.not_equal, fill=1.0,
                                                     base=-c, pattern=[[0, 128]], channel_multiplier=1),
             reads=bufs(sel[c]), writes=bufs(sel[c]))

    def load_w(wap, c0, n, q="pool"):
        wt = wring.next()
        S.dma(q, wt[:, :, 0:n], wap.rearrange("(k p) n -> p k n", p=128)[:, :, c0:c0 + n], writes=bufs(wt))
        return wt

    def phase_a(li):
        ma_ = k.amark()
        gate2 = k.at([2, D], F32)
        bgate2 = k.at([2, D], F32)
        S.dma("sp", bmodT[:], b_mod[li, 0:2 * D].rearrange("(c p) -> p c", p=128), writes=bufs(bmodT))
        S.dma("sp", ngT[:], norm_g[li].rearrange("(c p) -> p c", p=128), writes=bufs(ngT))
        S.dma("sp", bgate2[:], b_mod[li, 2 * D:3 * D].partition_broadcast(2), writes=bufs(bgate2))
        for blk in range(4):
            wt = load_w(w_mod[li], blk * 512, 512)
            for cc in range(4):
                ch = blk * 4 + cc
                pb = banks.next()
                for kk in range(8):
                    S.op("pe", lambda e, pb=pb, wt=wt, cc=cc, kk=kk: e.matmul(
                        pb[:, 0:2], lhsT=wt[:, kk, cc * 128:(cc + 1) * 128], rhs=cb[:, kk, :],
                        start=(kk == 0), stop=(kk == 7)), reads=bufs(wt, cb), writes=bufs(pb))
                S.op("dve", lambda e, pb=pb, ch=ch: e.tensor_scalar(
                    out=modT[:, ch, :], in0=pb[:, 0:2], scalar1=bmodT[:, ch:ch + 1], scalar2=None, op0=ALU.add),
                    reads=bufs(pb, bmodT), writes=bufs(modT))
        S.op("dve", lambda e: e.tensor_scalar(out=Asc[:], in0=modT[:, 8:16, :], scalar1=1.0, scalar2=None, op0=ALU.add),
             reads=bufs(modT), writes=bufs(Asc))
        S.op("dve", lambda e: e.tensor_tensor(out=Asc[:], in0=Asc[:], in1=ngT[:].unsqueeze(2).to_broadcast([128, 8, 2]),
                                              op=ALU.mult), reads=bufs(Asc, ngT), writes=bufs(Asc))
        for blk in range(2):
            wt = load_w(w_mod[li], 2 * D + blk * 512, 512)
            pb = banks.next()
            for kk in range(8):
                S.op("pe", lambda e, pb=pb, wt=wt, kk=kk: e.matmul(
                    pb[0:2, :], lhsT=cb[:, kk, :], rhs=wt[:, kk, :], start=(kk == 0), stop=(kk == 7)),
                    reads=bufs(wt, cb), writes=bufs(pb))
            S.op("dve", lambda e, pb=pb, blk=blk: e.tensor_tensor(
                out=gate2[:, blk * 512:(blk + 1) * 512], in0=pb[0:2, :], in1=bgate2[:, blk * 512:(blk + 1) * 512],
                op=ALU.add), reads=bufs(pb, bgate2), writes=bufs(gate2))
        for c in range(2):
            for blk in range(2):
                pb = banks.next()
                S.op("pe", lambda e, pb=pb, c=c, blk=blk: e.matmul(
                    pb[:], lhsT=sel[c][:], rhs=gate2[:, blk * 512:(blk + 1) * 512], start=True, stop=True),
                    reads=bufs(sel[c], gate2), writes=bufs(pb))
                S.op("act", lambda e, pb=pb, c=c, blk=blk: e.activation(
                    out=gate_bc[c][:, blk * 512:(blk + 1) * 512], in_=pb[:], func=AF.Copy),
                    reads=bufs(pb), writes=bufs(gate_bc[c]))
        k.arestore(ma_)

    def rows_std(tok0):
        return lambda src: src[tok0:tok0 + 128, :]

    def rms_stats(xt):
        st = small.next()
        S.op("act", lambda e: e.activation(out=junk[:], in_=xt[:], func=AF.Square, accum_out=st[:, 0:1]),
             reads=bufs(xt), writes=bufs(junk, st))
        S.op("dve", lambda e: e.tensor_scalar(out=st[:, 0:1], in0=st[:, 0:1], scalar1=1.0 / D, scalar2=EPS,
                                              op0=ALU.mult, op1=ALU.add), reads=bufs(st), writes=bufs(st))
        S.op("act", lambda e: e.activation(out=st[:, 0:1], in_=st[:, 0:1], func=AF.Sqrt), reads=bufs(st), writes=bufs(st))
        S.op("dve", lambda e: e.reciprocal(out=st[:, 0:1], in_=st[:, 0:1]), reads=bufs(st), writes=bufs(st))
        return st

    def phase_b(src, tiles, cond):
        m_ = k.amark()
        xring = k.aring(3, [128, D], F32)
        xnring = k.aring(2, [128, D], BF16)

        def load(i):
            xt = xring.next()
            S.dma("sp", xt[:], tiles[i][0](src), writes=bufs(xt))
            return xt
        nxt = load(0)
        for i in range(len(tiles)):
            xt = nxt
            if i + 1 < len(tiles):
                nxt = load(i + 1)
            col0 = tiles[i][1]
            st = rms_stats(xt)
            xn = xnring.next()
            S.op("dve", lambda e, xn=xn, xt=xt, st=st: e.tensor_scalar(out=xn[:], in0=xt[:], scalar1=st[:, 0:1],
                                                                   scalar2=None, op0=ALU.mult),
                 reads=bufs(xt, st), writes=bufs(xn))
            pb = banks.next()
            pv = pb[:].bitcast(BF16).rearrange("p (k t) -> p k t", k=8)
            for kk in range(8):
                S.op("pe", lambda e, pv=pv, xn=xn, kk=kk: e.transpose(out=pv[:, kk, :], in_=xn[:, kk * 128:(kk + 1) * 128],
                                                                    identity=identb[:]),
                     reads=bufs(xn, identb), writes=bufs(pb))
            for kk in range(8):
                S.op("act", lambda e, pv=pv, kk=kk, col0=col0: e.activation(
                    out=hT[:, kk, col0:col0 + 128], in_=pv[:, kk, :], func=AF.Identity,
                    scale=Asc[:, kk, cond:cond + 1], bias=modT[:, kk, cond:cond + 1]),
                    reads=bufs(pb, Asc, modT), writes=bufs(hT))
        k.arestore(m_)


    def load_wout(li):
        for h in range(2):
            S.dma("pool", wo_view[:, h * 8:(h + 1) * 8, :],
                  w_out[li].rearrange("(k p) n -> p k n", p=128)[:, h * 8:(h + 1) * 8, :], writes=bufs(wo))

    def phase_d(src, dst, tiles, cond, last, scale_t=None, ytok0=None):
        if cfg.get("skip_d"):
            return
        m_ = k.amark()
        xring = k.aring(3, [128, D], F32)
        tring = k.aring(2, [128, D], F32)
        if last:
            fg_bc = k.at([128, D], F32)
            S.dma("sp", fg_bc[:], final_g.partition_broadcast(128), writes=bufs(fg_bc))
        if ytok0 is None:
            yT = L["yT"]
        else:
            yring = k.aring(2, [128, 16, 512], BF16)
            yT = None

        def load(i):
            xt = xring.next()
            S.dma("sp", xt[:], tiles[i][0](src), writes=bufs(xt))
            return xt
        nxt = load(0)
        for i in range(len(tiles)):
            xt = nxt
            if i + 1 < len(tiles):
                nxt = load(i + 1)
            col0 = tiles[i][1]
            if ytok0 is not None:
                if i % 4 == 0:
                    yT = yring.next()
                    S.dma("sp", yT[:], yscr[:, :, ytok0 + tiles[i][1]:ytok0 + tiles[i][1] + 512].rearrange("b p t -> p b t"),
                          writes=bufs(yT))
                col0 = (i % 4) * 128
            tt = tring.next()
            for h in range(2):
                pb = banks.next()
                for kk in range(16):
                    S.op("pe", lambda e, pb=pb, kk=kk, h=h, col0=col0, yT=yT: e.matmul(
                        pb[:], lhsT=yT[:, kk, col0:col0 + 128], rhs=wo_view[:, kk, h * 512:(h + 1) * 512],
                        start=(kk == 0), stop=(kk == 15)), reads=bufs(yT, wo), writes=bufs(pb))
                if scale_t is None:
                    S.op("dve", lambda e, pb=pb, tt=tt, h=h: e.tensor_tensor(
                        out=tt[:, h * 512:(h + 1) * 512], in0=pb[:], in1=gate_bc[cond][:, h * 512:(h + 1) * 512],
                        op=ALU.mult), reads=bufs(pb, gate_bc[cond]), writes=bufs(tt))
                else:
                    sc = scale_t(i)
                    S.op("dve", lambda e, pb=pb, tt=tt, h=h, sc=sc: e.scalar_tensor_tensor(
                        out=tt[:, h * 512:(h + 1) * 512], in0=pb[:], scalar=sc[0], in1=gate_bc[cond][:, h * 512:(h + 1) * 512],
                        op0=ALU.mult, op1=ALU.mult), reads=bufs(pb, gate_bc[cond]) + [sc[1]], writes=bufs(tt))
            S.op("pool", lambda e, tt=tt, xt=xt: e.tensor_tensor(out=xt[:], in0=tt[:], in1=xt[:], op=ALU.add),
                 reads=bufs(tt, xt), writes=bufs(xt))
            if not last:
                S.dma("sp", tiles[i][0](dst), xt[:], reads=bufs(xt))
            else:
                st = rms_stats(xt)
                S.op("dve", lambda e, tt=tt, xt=xt, st=st: e.scalar_tensor_tensor(
                    out=tt[:], in0=xt[:], scalar=st[:, 0:1], in1=fg_bc[:], op0=ALU.mult, op1=ALU.mult),
                    reads=bufs(xt, st, fg_bc), writes=bufs(tt))
                S.dma("sp", tiles[i][0](y_out), tt[:], reads=bufs(tt))
        k.arestore(m_)

    def gmlp_consts():
        c = {}
        c["lngT"] = k.at([128, 16], F32)
        c["lnbT"] = k.at([128, 16], F32)
        c["wsT"] = k.at([128, 8, 128], BF16)
        c["wsTf"] = k.at([128, 8, 128], F32)
        c["bs_bc"] = k.at([128, 8, 128], F32)
        c["Bt"] = k.at([128, 16, 128], F32)
        S.dma("sp", c["lngT"][:], mlp_ln_g.rearrange("(c p) -> p c", p=128), writes=bufs(c["lngT"]))
        S.dma("sp", c["lnbT"][:], mlp_ln_b.rearrange("(c p) -> p c", p=128), writes=bufs(c["lnbT"]))
        S.dma("sp", c["wsTf"][:], mlp_w_sT.rearrange("g j i -> j g i"), writes=bufs(c["wsTf"]))
        S.dma("sp", c["bs_bc"][:].rearrange("p g i -> p (g i)"), mlp_b_s.rearrange("g i -> (g i)").partition_broadcast(128),
              writes=bufs(c["bs_bc"]))
        S.op("dve", lambda e: e.tensor_copy(out=c["wsT"][:], in_=c["wsTf"][:]), reads=bufs(c["wsTf"]), writes=bufs(c["wsT"]))
        for half in range(2):
            pb = banks.next()
            S.op("pe", lambda e, pb=pb, half=half: e.matmul(
                pb[:], lhsT=onesf[:], rhs=c["wsTf"][:, half * 4:(half + 1) * 4, :].rearrange("p g i -> p (g i)"),
                start=True, stop=True), reads=bufs(onesf, c["wsTf"]), writes=bufs(pb))
            for gg in range(4):
                g = half * 4 + gg
                for bb in range(2):
                    blk = g * 2 + bb
                    S.op("dve", lambda e, pb=pb, gg=gg, g=g, blk=blk: e.scalar_tensor_tensor(
                        out=c["Bt"][:, blk, :], in0=pb[:, gg * 128:(gg + 1) * 128], scalar=c["lnbT"][:, blk:blk + 1],
                        in1=c["bs_bc"][:, g, :], op0=ALU.mult, op1=ALU.add),
                        reads=bufs(pb, c["lnbT"], c["bs_bc"]), writes=bufs(c["Bt"]))
        c["vv"] = k.at([128, 8, E], BF16)
        c["gtmp"] = k.aring(2, [128, 512], F32)
        c["ug"] = k.aring(2, [128, 512], F32)
        c["zs"] = k.aring(2, [128, 512], F32)
        c["sg"] = k.aring(2, [128, 512], F32)
        c["st"] = k.at([128, 8, 8], F32)
        return c

    def gmlp_unit(c, ntile):
        yT = L["yT"]
        vv = c["vv"]
        stt = c["st"]
        for b in range(4):
            wv = load_w(mlp_w_in, E + b * 512, 512)
            for t in range(ntile):
                pb = banks.next()
                for kk in range(8):
                    S.op("pe", lambda e, pb=pb, t=t, wv=wv, kk=kk: e.matmul(
                        pb[:], lhsT=hT[:, kk, t * 128:(t + 1) * 128], rhs=wv[:, kk, :], start=(kk == 0), stop=(kk == 7)),
                        reads=bufs(hT, wv), writes=bufs(pb))
                gt = c["gtmp"].next()
                S.op("act", lambda e, pb=pb, b=b, t=t, gt=gt: e.activation(
                    out=gt[:], in_=pb[:], func=AF.Gelu, accum_out=stt[:, t, b:b + 1]),
                    reads=bufs(pb), writes=bufs(gt, stt))
                S.op("act", lambda e, b=b, t=t, gt=gt: e.activation(
                    out=junk[:, 0:512], in_=gt[:], func=AF.Square, accum_out=stt[:, t, 4 + b:5 + b]),
                    reads=bufs(gt), writes=bufs(junk, stt))
                S.op("pool", lambda e, b=b, t=t, gt=gt: e.tensor_copy(out=vv[:, t, b * 512:(b + 1) * 512], in_=gt[:]),
                     reads=bufs(gt), writes=bufs(vv))
        for t in range(ntile):
            st2 = small.next()
            S.op("dve", lambda e, t=t, st2=st2: e.tensor_reduce(
                out=st2[:, 0:2], in_=stt[:, t, :].rearrange("p (a b) -> p a b", a=2), axis=AX.X, op=ALU.add),
                reads=bufs(stt), writes=bufs(st2))
            S.op("dve", lambda e, st2=st2: e.tensor_scalar(out=st2[:, 0:2], in0=st2[:, 0:2], scalar1=1.0 / E, scalar2=None,
                                                           op0=ALU.mult), reads=bufs(st2), writes=bufs(st2))
            S.op("dve", lambda e, st2=st2: e.tensor_tensor(out=st2[:, 2:3], in0=st2[:, 0:1], in1=st2[:, 0:1], op=ALU.mult),
                 reads=bufs(st2), writes=bufs(st2))
            S.op("dve", lambda e, st2=st2: e.scalar_tensor_tensor(out=st2[:, 2:3], in0=st2[:, 2:3], scalar=-1.0, in1=st2[:, 1:2],
                                                                  op0=ALU.mult, op1=ALU.add), reads=bufs(st2), writes=bufs(st2))
            S.op("dve", lambda e, st2=st2: e.tensor_scalar(out=st2[:, 2:3], in0=st2[:, 2:3], scalar1=EPS, scalar2=None,
                                                           op0=ALU.add), reads=bufs(st2), writes=bufs(st2))
            S.op("act", lambda e, st2=st2: e.activation(out=st2[:, 2:3], in_=st2[:, 2:3], func=AF.Sqrt),
                 reads=bufs(st2), writes=bufs(st2))
            S.op("dve", lambda e, st2=st2: e.reciprocal(out=st2[:, 2:3], in_=st2[:, 2:3]), reads=bufs(st2), writes=bufs(st2))
            S.op("dve", lambda e, st2=st2, t=t: e.tensor_scalar(
                out=vv[:, t, :], in0=vv[:, t, :], scalar1=st2[:, 0:1], scalar2=st2[:, 2:3], op0=ALU.subtract, op1=ALU.mult),
                reads=bufs(vv, st2), writes=bufs(vv))
        nq = ntile // 4
        for blk in range(16):
            g = blk // 2
            if blk % 4 == 0:
                wu = load_w(mlp_w_in, blk * 128, 512)
                wz = load_w(mlp_w_in, 2 * E + blk * 128, 512)
            co = (blk % 4) * 128
            for q in range(nq):
                ug = c["ug"].next()
                zs = c["zs"].next()
                sg = c["sg"].next()
                pu = banks.next()
                for kk in range(8):
                    S.op("pe", lambda e, pu=pu, kk=kk, q=q, wu=wu, co=co: e.matmul(
                        pu[:], lhsT=wu[:, kk, co:co + 128], rhs=hT[:, kk, q * 512:(q + 1) * 512],
                        start=(kk == 0), stop=(kk == 7)), reads=bufs(wu, hT), writes=bufs(pu))
                S.op("act", lambda e, pu=pu, ug=ug: e.activation(out=ug[:], in_=pu[:], func=AF.Gelu),
                     reads=bufs(pu), writes=bufs(ug))
                pz = banks.next()
                for kk in range(8):
                    S.op("pe", lambda e, pz=pz, kk=kk, q=q, wz=wz, co=co: e.matmul(
                        pz[:], lhsT=wz[:, kk, co:co + 128], rhs=hT[:, kk, q * 512:(q + 1) * 512],
                        start=(kk == 0), stop=(kk == 7)), reads=bufs(wz, hT), writes=bufs(pz))
                S.op("act", lambda e, pz=pz, zs=zs: e.activation(out=zs[:], in_=pz[:], func=AF.Silu),
                     reads=bufs(pz), writes=bufs(zs))
                ps_ = banks.next()
                for cc in range(4):
                    t = q * 4 + cc
                    S.op("pe", lambda e, ps_=ps_, t=t, cc=cc, blk=blk, g=g: e.matmul(
                        ps_[:, cc * 128:(cc + 1) * 128], lhsT=vv[:, t, blk * 128:(blk + 1) * 128], rhs=c["wsT"][:, g, :],
                        start=True, stop=True), reads=bufs(vv, c["wsT"]), writes=bufs(ps_))
                S.op("dve", lambda e, ps_=ps_, blk=blk, sg=sg: e.scalar_tensor_tensor(
                    out=sg[:].rearrange("p (c i) -> p c i", c=4),
                    in0=ps_[:].rearrange("p (c i) -> p c i", c=4),
                    scalar=c["lngT"][:, blk:blk + 1],
                    in1=c["Bt"][:, blk:blk + 1, :].to_broadcast([128, 4, 128]), op0=ALU.mult, op1=ALU.add),
                    reads=bufs(ps_, c["lngT"], c["Bt"]), writes=bufs(sg))
                S.op("pool", lambda e, sg=sg, ug=ug: e.tensor_tensor(out=sg[:], in0=sg[:], in1=ug[:], op=ALU.mult),
                     reads=bufs(sg, ug), writes=bufs(sg))
                S.op("dve", lambda e, q=q, sg=sg, zs=zs, blk=blk: e.tensor_tensor(
                    out=yT[:, blk, q * 512:(q + 1) * 512], in0=sg[:], in1=zs[:], op=ALU.mult),
                    reads=bufs(sg, zs), writes=bufs(yT))

    SCALE = 0.125

    def nat_proj(hp, ntok, c, with_ktm):
        wt = wring.next()
        for j in (0, 1, 3):
            S.dma("pool", wt[:, :, j * 128:(j + 1) * 128],
                  nat_w_in.rearrange("(k p) n -> p k n", p=128)[:, :, j * E + hp * 128:j * E + (hp + 1) * 128],
                  writes=bufs(wt))
        qT, kT, gT = c["qT"], c["kT"], c["gT"]
        for q in range(ntok // 512):
            for j, dst, fn in ((0, qT, AF.Copy), (1, kT, AF.Copy), (3, gT, AF.Silu)):
                pb = banks.next()
                for kk in range(8):
                    S.op("pe", lambda e, pb=pb, kk=kk, q=q, j=j, wt=wt: e.matmul(
                        pb[:], lhsT=wt[:, kk, j * 128:(j + 1) * 128], rhs=hT[:, kk, q * 512:(q + 1) * 512],
                        start=(kk == 0), stop=(kk == 7)), reads=bufs(wt, hT), writes=bufs(pb))
                S.op("act", lambda e, pb=pb, q=q, dst=dst, fn=fn: e.activation(
                    out=dst[:, q * 512:(q + 1) * 512], in_=pb[:], func=fn), reads=bufs(pb), writes=bufs(dst))

    def nat_proj_v4(hp4, ntok, c, with_ktm):
        vb = c["vb"]
        wv = load_w(nat_w_in, 2 * E + hp4 * 512, 512)
        wk = load_w(nat_w_in, E + hp4 * 512, 512) if with_ktm else None
        for t in range(ntok // 128):
            pv_ = banks.next()
            for kk in range(8):
                S.op("pe", lambda e, pv_=pv_, kk=kk, t=t, wv=wv: e.matmul(
                    pv_[:], lhsT=hT[:, kk, t * 128:(t + 1) * 128], rhs=wv[:, kk, :], start=(kk == 0), stop=(kk == 7)),
                    reads=bufs(wv, hT), writes=bufs(pv_))
            if not with_ktm:
                S.op("act", lambda e, pv_=pv_, t=t: e.activation(out=vb[:, t, :], in_=pv_[:], func=AF.Copy),
                     reads=bufs(pv_), writes=bufs(vb))
            else:
                vst, kst = c["vst"], c["kst"]
                S.op("act", lambda e, pv_=pv_, t=t: e.activation(out=vst[:, t, :], in_=pv_[:], func=AF.Copy),
                     reads=bufs(pv_), writes=bufs(vst))
                S.op("pool", lambda e, t=t: e.tensor_copy(out=vb[:, t, :], in_=vst[:, t, :]), reads=bufs(vst), writes=bufs(vb))
                pk_ = banks.next()
                for kk in range(8):
                    S.op("pe", lambda e, pk_=pk_, kk=kk, t=t, wk=wk: e.matmul(
                        pk_[:], lhsT=hT[:, kk, t * 128:(t + 1) * 128], rhs=wk[:, kk, :], start=(kk == 0), stop=(kk == 7)),
                        reads=bufs(wk, hT), writes=bufs(pk_))
                S.op("act", lambda e, pk_=pk_, t=t: e.activation(out=kst[:, t, :], in_=pk_[:], func=AF.Copy),
                     reads=bufs(pk_), writes=bufs(kst))

    def nat_ctx_unit():
        yT = L["yT"]
        c = {"qT": k.at([128, 1024], BF16), "kT": k.at([128, 1024], BF16), "gT": k.at([128, 1024], BF16),
             "vb": k.at([128, 8, 512], BF16), "vst": k.at([128, 8, 512], F32), "kst": k.at([128, 8, 512], F32)}
        er = k.aring(2, [128, 512], F32)
        pbr = k.aring(2, [128, 512], BF16)
        ptr_ = k.aring(2, [128, 512], BF16)
        for hp in range(cfg.get("ctx_hp", 16)):
            if hp % 4 == 0:
                nat_proj_v4(hp // 4, 1024, c, True)
            nat_proj(hp, 1024, c, True)
            for hd in range(2):
                h = hp * 2 + hd
                for sq in range(0 if cfg.get("no_kv") else 4):
                    S.dma("sp", new_k[sq, h, :, :].rearrange("(t p) d -> p t d", p=128),
                          c["kst"][:, sq * 2:(sq + 1) * 2, (hp % 4) * 128 + hd * 64:(hp % 4) * 128 + (hd + 1) * 64], reads=bufs(c["kst"]))
                    S.dma("sp", new_v[sq, h, :, :].rearrange("(t p) d -> p t d", p=128),
                          c["vst"][:, sq * 2:(sq + 1) * 2, (hp % 4) * 128 + hd * 64:(hp % 4) * 128 + (hd + 1) * 64], reads=bufs(c["vst"]))
            qT, kT, gT, vb = c["qT"], c["kT"], c["gT"], c["vb"]
            cb_ = Ring(banks.tiles[2:8])
            pob_ = Ring(banks.tiles[0:2])

            vo = (hp % 4) * 128

            def c_qk(sq, hd):
                rows = slice(hd * 64, (hd + 1) * 64)
                tok0 = sq * 256
                ps_ = cb_.next()
                for qt in range(2):
                    S.op("pe", lambda e, ps_=ps_, qt=qt, rows=rows, tok0=tok0: e.matmul(
                        ps_[:, qt * 256:(qt + 1) * 256], lhsT=qT[rows, tok0 + qt * 128:tok0 + (qt + 1) * 128],
                        rhs=kT[rows, tok0:tok0 + 256], start=True, stop=True), reads=bufs(qT, kT), writes=bufs(ps_))
                return ps_

            def c_softmax(ps_):
                mx = small.next()
                S.op("dve", lambda e, ps_=ps_, mx=mx: e.tensor_reduce(
                    out=mx[:, 0:2], in_=ps_[:].rearrange("p (a b) -> p a b", a=2), axis=AX.X, op=ALU.max),
                    reads=bufs(ps_), writes=bufs(mx))
                S.op("dve", lambda e, mx=mx: e.tensor_scalar(out=mx[:, 2:4], in0=mx[:, 0:2], scalar1=-SCALE, scalar2=None,
                                                             op0=ALU.mult), reads=bufs(mx), writes=bufs(mx))
                et = er.next()
                for qt in range(2):
                    S.op("act", lambda e, ps_=ps_, mx=mx, et=et, qt=qt: e.activation(
                        out=et[:, qt * 256:(qt + 1) * 256], in_=ps_[:, qt * 256:(qt + 1) * 256], func=AF.Exp, scale=SCALE,
                        bias=mx[:, 2 + qt:3 + qt], accum_out=mx[:, 4 + qt:5 + qt]), reads=bufs(ps_, mx), writes=bufs(et, mx))
                S.op("dve", lambda e, mx=mx: e.reciprocal(out=mx[:, 6:8], in_=mx[:, 4:6]), reads=bufs(mx), writes=bufs(mx))
                pbt = pbr.next()
                S.op("dve", lambda e, mx=mx, et=et, pbt=pbt: e.tensor_tensor(
                    out=pbt[:].rearrange("p (a b) -> p a b", a=2), in0=et[:].rearrange("p (a b) -> p a b", a=2),
                    in1=mx[:, 6:8].unsqueeze(2).to_broadcast([128, 2, 256]), op=ALU.mult),
                    reads=bufs(mx, et), writes=bufs(pbt))
                return pbt

            def c_tpv(sq, hd, pbt, po):
                rows = slice(hd * 64, (hd + 1) * 64)
                ptb = cb_.next()
                ptv = ptb[:].bitcast(BF16)
                for j in range(4):
                    S.op("pe", lambda e, ptv=ptv, pbt=pbt, j=j: e.transpose(
                        out=ptv[:, j * 128:(j + 1) * 128], in_=pbt[:, j * 128:(j + 1) * 128], identity=identb[:]),
                        reads=bufs(pbt, identb), writes=bufs(ptb))
                pts = ptr_.next()
                S.op("act", lambda e, ptv=ptv, pts=pts: e.activation(out=pts[:], in_=ptv[:, 0:512], func=AF.Copy),
                     reads=bufs(ptb), writes=bufs(pts))
                for qt in range(2):
                    for kb in range(2):
                        S.op("pe", lambda e, po=po, rows=rows, qt=qt, kb=kb, sq=sq, hd=hd, pts=pts, vo=vo: e.matmul(
                            po[rows, qt * 128:(qt + 1) * 128], lhsT=vb[:, sq * 2 + kb, vo + hd * 64:vo + (hd + 1) * 64],
                            rhs=pts[:, (qt * 2 + kb) * 128:(qt * 2 + kb + 1) * 128], start=(kb == 0), stop=(kb == 1)),
                            reads=bufs(vb, pts), writes=bufs(po))

            its = [(sq, hd) for sq in range(0 if cfg.get("ctx_stage", 9) < 1 else 4) for hd in range(2)]
            nxt = c_qk(*its[0]) if its else None
            po = None
            for ii, (sq, hd) in enumerate(its):
                if hd == 0:
                    po = pob_.next()
                pbt = c_softmax(nxt)
                if ii + 1 < len(its):
                    nxt = c_qk(*its[ii + 1])
                c_tpv(sq, hd, pbt, po)
                if hd == 1:
                    tok0 = sq * 256
                    S.op("dve", lambda e, po=po, hp=hp, tok0=tok0: e.tensor_tensor(
                        out=yT[:, hp, tok0:tok0 + 256], in0=po[:, 0:256], in1=gT[:, tok0:tok0 + 256], op=ALU.mult),
                        reads=bufs(po, gT), writes=bufs(yT))

    def nat_lat_unit():
        yT = L["yT"]
        c = {"qT": k.at([128, 2048], BF16), "kT": k.at([128, 2048], BF16), "gT": k.at([128, 2048], BF16),
             "vb": k.at([128, 16, 512], BF16)}
        maskf = k.at([128, 3, 576], F32)
        maskb = k.at([128, 3, 576], BF16)
        for j in range(3):
            S.dma("sp", maskf[:, j, :], natmask[j], writes=bufs(maskf))
        S.op("dve", lambda e: e.tensor_copy(out=maskb[:], in_=maskf[:]), reads=bufs(maskf), writes=bufs(maskb))
        ckr = k.aring(2, [128, 2, 2, 64], BF16)
        cvr = k.aring(2, [128, 2, 2, 64], BF16)
        cktr = k.aring(2, [128, 256], BF16)
        rpr = k.aring(2, [128, 1024], F32)
        scr = k.aring(2, [128, 832], F32)
        pbr = k.aring(2, [128, 832], BF16)
        ptr_ = k.aring(2, [128, 896], BF16)
        cfg["alog"] = k.alog
        pobanks = Ring(banks.tiles[0:2])
        wbanks = Ring(banks.tiles[2:8])
        for hp in range(cfg.get("nat_hp", 16)):
            if hp % 4 == 0:
                nat_proj_v4(hp // 4, 2048, c, False)
            nat_proj(hp, 2048, c, False)
            vo = (hp % 4) * 128
            qT, kT, gT, vb = c["qT"], c["kT"], c["gT"], c["vb"]
            ck = ckr.next()
            cv = cvr.next()
            for hd in range(2):
                S.dma("pool", ck[:, :, hd, :], cache_k[hp * 2 + hd].rearrange("(kb p) d -> p kb d", p=128), writes=bufs(ck))
                S.dma("pool", cv[:, :, hd, :], cache_v[hp * 2 + hd].rearrange("(kb p) d -> p kb d", p=128), writes=bufs(cv))
            ckT = cktr.next()
            ptb = banks.next()
            ptv = ptb[:].bitcast(BF16)
            for kb in range(2):
                S.op("pe", lambda e, ptv=ptv, ck=ck, kb=kb: e.transpose(
                    out=ptv[:, kb * 128:(kb + 1) * 128], in_=ck[:, kb, :, :].rearrange("p a b -> p (a b)"), identity=identb[:]),
                    reads=bufs(ck, identb), writes=bufs(ptb))
            S.op("act", lambda e, ptv=ptv, ckT=ckT: e.activation(out=ckT[:], in_=ptv[:, 0:256], func=AF.Copy),
                 reads=bufs(ptb), writes=bufs(ckT))
            rps = []
            for hd in range(2):
                rp = rpr.next()
                S.dma("sp", rp[:], rpbg[hp * 2 + hd], writes=bufs(rp))
                rps.append(rp)
            items = []
            for pg in range(cfg.get("nat_pg", 4)):
                for hd in range(cfg.get("nat_hd", 2)):
                    for pi in range(cfg.get("nat_pi", 4)):
                        items.append((pg, hd, pi))

            def geom(pg, hd, pi):
                pr = pg * 4 + pi
                r = 2 * pr
                if pr <= 1:
                    r0, nrow, a0, mi = 0, 9, 7 - r, 1
                elif pr >= 14:
                    r0, nrow, a0, mi = 24, 8, (3 if pr == 14 else 1), 2
                else:
                    r0, nrow, a0, mi = r - 4, 9, 3, 0
                return r, r0, nrow, a0, mi

            def st_qk(it):
                pg, hd, pi = it
                r, r0, nrow, a0, mi = geom(*it)
                rows = slice(hd * 64, (hd + 1) * 64)
                q0, k0 = r * 64, r0 * 64
                ps1 = wbanks.next()
                ps2 = wbanks.next()
                S.op("pe", lambda e, ps1=ps1, rows=rows, q0=q0, k0=k0: e.matmul(
                    ps1[:], lhsT=qT[rows, q0:q0 + 128], rhs=kT[rows, k0:k0 + 512], start=True, stop=False),
                    reads=bufs(qT, kT), writes=bufs(ps1))
                S.op("pe", lambda e, ps1=ps1, mi=mi: e.matmul(
                    ps1[:], lhsT=identb[:], rhs=maskb[:, mi, 0:512], start=False, stop=True),
                    reads=bufs(identb, maskb), writes=bufs(ps1))
                if nrow == 9:
                    S.op("pe", lambda e, ps2=ps2, rows=rows, q0=q0, k0=k0: e.matmul(
                        ps2[:, 0:64], lhsT=qT[rows, q0:q0 + 128], rhs=kT[rows, k0 + 512:k0 + 576], start=True, stop=False),
                        reads=bufs(qT, kT), writes=bufs(ps2))
                    S.op("pe", lambda e, ps2=ps2, mi=mi: e.matmul(
                        ps2[:, 0:64], lhsT=identb[:], rhs=maskb[:, mi, 512:576], start=False, stop=True),
                        reads=bufs(identb, maskb), writes=bufs(ps2))
                S.op("pe", lambda e, ps2=ps2, rows=rows, q0=q0, ckT=ckT: e.matmul(
                    ps2[:, 64:320], lhsT=qT[rows, q0:q0 + 128], rhs=ckT[rows, :], start=True, stop=True),
                    reads=bufs(qT, ckT), writes=bufs(ps2))
                return ps1, ps2

            def st_softmax(it, ps1, ps2):
                pg, hd, pi = it
                r, r0, nrow, a0, mi = geom(*it)
                rp = rps[hd]
                nk = nrow * 64
                sc = scr.next()
                S.op("dve", lambda e, ps1=ps1, sc=sc, rp=rp, a0=a0: e.scalar_tensor_tensor(
                    out=sc[:, 0:512], in0=ps1[:], scalar=SCALE, in1=rp[:, a0 * 64:a0 * 64 + 512],
                    op0=ALU.mult, op1=ALU.add), reads=bufs(ps1, rp), writes=bufs(sc))
                if nrow == 9:
                    S.op("dve", lambda e, ps2=ps2, sc=sc, rp=rp, a0=a0: e.scalar_tensor_tensor(
                        out=sc[:, 512:576], in0=ps2[:, 0:64], scalar=SCALE, in1=rp[:, a0 * 64 + 512:a0 * 64 + 576],
                        op0=ALU.mult, op1=ALU.add), reads=bufs(ps2, rp), writes=bufs(sc))
                S.op("act", lambda e, ps2=ps2, sc=sc, nk=nk: e.activation(
                    out=sc[:, nk:nk + 256], in_=ps2[:, 64:320], func=AF.Copy, scale=SCALE),
                    reads=bufs(ps2), writes=bufs(sc))
                ntot = nk + 256
                mx = small.next()
                S.op("dve", lambda e, sc=sc, mx=mx, ntot=ntot: e.tensor_reduce(
                    out=mx[:, 0:1], in_=sc[:, 0:ntot], axis=AX.X, op=ALU.max), reads=bufs(sc), writes=bufs(mx))
                S.op("dve", lambda e, mx=mx: e.tensor_scalar(out=mx[:, 1:2], in0=mx[:, 0:1], scalar1=-1.0, scalar2=None,
                                                             op0=ALU.mult), reads=bufs(mx), writes=bufs(mx))
                S.op("act", lambda e, sc=sc, mx=mx, ntot=ntot: e.activation(
                    out=sc[:, 0:ntot], in_=sc[:, 0:ntot], func=AF.Exp, bias=mx[:, 1:2], accum_out=mx[:, 2:3]),
                    reads=bufs(sc, mx), writes=bufs(sc, mx))
                S.op("dve", lambda e, mx=mx: e.reciprocal(out=mx[:, 3:4], in_=mx[:, 2:3]), reads=bufs(mx), writes=bufs(mx))
                pbt = pbr.next()
                S.op("dve", lambda e, sc=sc, mx=mx, pbt=pbt, ntot=ntot: e.tensor_scalar(
                    out=pbt[:, 0:ntot], in0=sc[:, 0:ntot], scalar1=mx[:, 3:4], scalar2=None, op0=ALU.mult),
                    reads=bufs(sc, mx), writes=bufs(pbt))
                return pbt

            def st_tpv(it, pbt, po):
                pg, hd, pi = it
                r, r0, nrow, a0, mi = geom(*it)
                rows = slice(hd * 64, (hd + 1) * 64)
                nk = nrow * 64
                ptb = wbanks.next()
                ptv = ptb[:].bitcast(BF16)
                blocks = [(j * 128, 128) for j in range(4)]
                blocks += [(nk, 128), (nk + 128, 128)]
                if nrow == 9:
                    blocks.append((512, 64))
                for j, (c0, w) in enumerate(blocks):
                    S.op("pe", lambda e, ptv=ptv, pbt=pbt, j=j, c0=c0, w=w: e.transpose(
                        out=ptv[0:w, j * 128:(j + 1) * 128], in_=pbt[:, c0:c0 + w], identity=identb[:]),
                        reads=bufs(pbt, identb), writes=bufs(ptb))
                nb = len(blocks)
                pts = ptr_.next()
                S.op("act", lambda e, ptv=ptv, pts=pts: e.activation(
                    out=pts[:, 0:768], in_=ptv[:, 0:768], func=AF.Copy), reads=bufs(ptb), writes=bufs(pts))
                if nrow == 9:
                    S.op("act", lambda e, ptv=ptv, pts=pts: e.activation(
                        out=pts[0:64, 768:896], in_=ptv[0:64, 768:896], func=AF.Copy), reads=bufs(ptb), writes=bufs(pts))
                t0 = r0 // 2
                for j, (c0, w) in enumerate(blocks):
                    if j < 4:
                        lhs = vb[:, t0 + j, vo + hd * 64:vo + (hd + 1) * 64]
                        rhs = pts[:, j * 128:(j + 1) * 128]
                        rd = bufs(vb, pts)
                    elif w == 64:
                        lhs = vb[0:64, t0 + 4, vo + hd * 64:vo + (hd + 1) * 64]
                        rhs = pts[0:64, j * 128:(j + 1) * 128]
                        rd = bufs(vb, pts)
                    else:
                        kb = j - 4
                        lhs = cv[:, kb, hd, :]
                        rhs = pts[:, j * 128:(j + 1) * 128]
                        rd = bufs(cv, pts)
                    S.op("pe", lambda e, po=po, rows=rows, pi=pi, lhs=lhs, rhs=rhs, j=j, nb=nb: e.matmul(
                        po[rows, pi * 128:(pi + 1) * 128], lhsT=lhs, rhs=rhs, start=(j == 0), stop=(j == nb - 1)),
                        reads=rd, writes=bufs(po))

            pos_ = {}
            n_it = len(items)
            qk_res = {}
            sm_res = {}
            for j_ in range(min(2, n_it)):
                qk_res[j_] = st_qk(items[j_])
            if n_it:
                sm_res[0] = st_softmax(items[0], *qk_res.pop(0))
            for ii, it in enumerate(items):
                pg = it[0]
                if pg not in pos_:
                    pos_[pg] = pobanks.next()
                po = pos_[pg]
                if ii + 2 < n_it:
                    qk_res[ii + 2] = st_qk(items[ii + 2])
                if ii + 1 < n_it:
                    sm_res[ii + 1] = st_softmax(items[ii + 1], *qk_res.pop(ii + 1))
                pbt = sm_res.pop(ii)
                st_tpv(it, pbt, po)
                if ii + 1 == len(items) or items[ii + 1][0] != pg:
                    S.op("dve", lambda e, po=po, hp=hp, pg=pg: e.tensor_tensor(
                        out=yT[:, hp, pg * 512:(pg + 1) * 512], in0=po[:], in1=gT[:, pg * 512:(pg + 1) * 512], op=ALU.mult),
                        reads=bufs(po, gT), writes=bufs(yT))

    def ssd_consts():
        c = {}
        c["tri"] = [k.at([128, 128], F32), k.at([128, 128], F32)]
        c["mneg"] = [k.at([128, 128], F32), k.at([128, 128], F32)]
        c["negones"] = k.at([128, 128], F32)
        for d_ in range(2):
            sgn = 1 if d_ == 0 else -1
            S.op("pool", lambda e, d_=d_: e.memset(c["tri"][d_][:], 1.0), writes=bufs(c["tri"][d_]))
            S.op("pool", lambda e, d_=d_, sgn=sgn: e.affine_select(
                out=c["tri"][d_][:], in_=c["tri"][d_][:], compare_op=ALU.is_ge, fill=0.0, base=0,
                pattern=[[sgn, 128]], channel_multiplier=-sgn), reads=bufs(c["tri"][d_]), writes=bufs(c["tri"][d_]))
            S.op("pool", lambda e, d_=d_: e.memset(c["mneg"][d_][:], 0.0), writes=bufs(c["mneg"][d_]))
            S.op("pool", lambda e, d_=d_, sgn=sgn: e.affine_select(
                out=c["mneg"][d_][:], in_=c["mneg"][d_][:], compare_op=ALU.is_ge, fill=-30000.0, base=0,
                pattern=[[sgn, 128]], channel_multiplier=-sgn), reads=bufs(c["mneg"][d_]), writes=bufs(c["mneg"][d_]))
        S.op("pool", lambda e: e.memset(c["negones"][:], -1.0), writes=bufs(c["negones"]))
        c["ntri"] = [k.at([128, 128], F32), k.at([128, 128], F32)]
        for d_ in range(2):
            S.op("pool", lambda e, d_=d_: e.tensor_scalar(out=c["ntri"][d_][:], in0=c["tri"][d_][:], scalar1=-1.0, scalar2=None,
                                                          op0=ALU.mult), reads=bufs(c["tri"][d_]), writes=bufs(c["ntri"][d_]))
        c["cwT"] = k.at([128, 32, 5], F32)
        c["cbT"] = k.at([128, 32], F32)
        for j in range(5):
            S.dma("sp", c["cwT"][:, :, j], ssd_conv_w[j].rearrange("(b p) -> p b", p=128), writes=bufs(c["cwT"]))
        S.dma("sp", c["cbT"][:], ssd_conv_b.rearrange("(b p) -> p b", p=128), writes=bufs(c["cbT"]))
        c["dtb"] = k.at([128, 64], F32)
        c["abc"] = k.at([128, 64], F32)
        c["dsk"] = k.at([128, 32], F32)
        c["ngT"] = k.at([128, 16], F32)
        S.dma("sp", c["dtb"][:], ssd_dt_bias.partition_broadcast(128), writes=bufs(c["dtb"]))
        S.dma("sp", c["abc"][:], ssd_a_log.partition_broadcast(128), writes=bufs(c["abc"]))
        S.dma("sp", c["dsk"][:], ssd_d.partition_broadcast(128), writes=bufs(c["dsk"]))
        S.dma("sp", c["ngT"][:], ssd_norm_g.rearrange("(b p) -> p b", p=128), writes=bufs(c["ngT"]))
        S.op("act", lambda e: e.activation(out=c["abc"][:], in_=c["abc"][:], func=AF.Exp), reads=bufs(c["abc"]), writes=bufs(c["abc"]))
        S.op("dve", lambda e: e.tensor_scalar(out=c["abc"][:], in0=c["abc"][:], scalar1=-1.0, scalar2=None, op0=ALU.mult),
             reads=bufs(c["abc"]), writes=bufs(c["abc"]))
        return c

    def ssd_unit(c, tok0, ntile, nseq, is_lat):
        T_ = ntile * 128
        nch = ntile // nseq
        Lq = nch * 128
        dt_ = k.at([128, ntile, 64], F32)
        da = k.at([128, ntile, 64], F32)
        ecum = k.at([128, ntile, 64], F32)
        dtd = k.at([128, ntile, 64], F32)
        etot = k.at([128, ntile, 64], F32)
        ssq = k.at([128, ntile, 8], F32)
        rstd = k.at([128, ntile], F32)
        tmpr = k.aring(2, [128, 64], F32)
        wdt = wring.next()
        S.dma("pool", wdt[:, :, 0:64], ssd_w_in.rearrange("(k p) n -> p k n", p=128)[:, :, 6144:6208], writes=bufs(wdt))
        for t in range(ntile):
            pb = banks.next()
            for kk in range(8):
                S.op("pe", lambda e, pb=pb, kk=kk, t=t: e.matmul(
                    pb[:, 0:64], lhsT=hT[:, kk, t * 128:(t + 1) * 128], rhs=wdt[:, kk, 0:64], start=(kk == 0), stop=(kk == 7)),
                    reads=bufs(hT, wdt), writes=bufs(pb))
            S.op("dve", lambda e, pb=pb, t=t: e.tensor_tensor(out=dt_[:, t, :], in0=pb[:, 0:64], in1=c["dtb"][:], op=ALU.add),
                 reads=bufs(pb, c["dtb"]), writes=bufs(dt_))
        S.op("act", lambda e: e.activation(out=dt_[:], in_=dt_[:], func=AF.Exp), reads=bufs(dt_), writes=bufs(dt_))
        S.op("act", lambda e: e.activation(out=dt_[:], in_=dt_[:], func=AF.Ln, bias=1.0), reads=bufs(dt_), writes=bufs(dt_))
        S.op("dve", lambda e: e.tensor_tensor(out=da[:], in0=dt_[:], in1=c["abc"][:].unsqueeze(1).to_broadcast([128, ntile, 64]),
                                              op=ALU.mult), reads=bufs(dt_, c["abc"]), writes=bufs(da))
        for t in range(ntile):
            pc = banks.next()
            S.op("pe", lambda e, pc=pc, t=t: e.matmul(pc[:, 0:32], lhsT=c["tri"][0][:], rhs=da[:, t, 0:32], start=True, stop=True),
                 reads=bufs(c["tri"][0], da), writes=bufs(pc))
            S.op("pe", lambda e, pc=pc, t=t: e.matmul(pc[:, 32:64], lhsT=c["tri"][1][:], rhs=da[:, t, 32:64], start=True, stop=True),
                 reads=bufs(c["tri"][1], da), writes=bufs(pc))
            S.op("pe", lambda e, pc=pc, t=t: e.matmul(pc[:, 64:128], lhsT=onesf[:], rhs=da[:, t, :], start=True, stop=True),
                 reads=bufs(onesf, da), writes=bufs(pc))
            cumt = tmpr.next()
            S.op("act", lambda e, pc=pc, cumt=cumt: e.activation(out=cumt[:], in_=pc[:, 0:64], func=AF.Identity),
                 reads=bufs(pc), writes=bufs(cumt))
            S.op("act", lambda e, pc=pc, t=t: e.activation(out=ecum[:, t, :], in_=pc[:, 0:64], func=AF.Exp),
                 reads=bufs(pc), writes=bufs(ecum))
            S.op("act", lambda e, pc=pc, t=t: e.activation(out=etot[:, t, :], in_=pc[:, 64:128], func=AF.Exp),
                 reads=bufs(pc), writes=bufs(etot))
            S.op("dve", lambda e, pc=pc, cumt=cumt: e.tensor_tensor(out=cumt[:], in0=pc[:, 64:128], in1=cumt[:], op=ALU.subtract),
                 reads=bufs(pc, cumt), writes=bufs(cumt))
            S.op("act", lambda e, cumt=cumt: e.activation(out=cumt[:], in_=cumt[:], func=AF.Exp), reads=bufs(cumt), writes=bufs(cumt))
            S.op("dve", lambda e, cumt=cumt, t=t: e.tensor_tensor(out=dtd[:, t, :], in0=dt_[:, t, :], in1=cumt[:], op=ALU.mult),
                 reads=bufs(cumt, dt_), writes=bufs(dtd))
        raw = k.at([128, T_], F32)
        acc = k.at([128, T_], F32)
        fm = [k.at([128, T_], BF16) for _ in range(4)]
        XB = k.at([128, ntile, 384], BF16)
        SIN = k.at([128, ntile, 2, 256], BF16)
        stf = k.at([128, 2, 256], F32)
        vTg = k.at([128, 2, T_], BF16)
        h0r = k.aring(2, [128, 2, 128], F32)
        fir = k.aring(2, [128, 2, 128], F32)
        GTr = k.aring(2, [128, 128], F32)
        Dr = k.aring(2, [128, 4, 128], F32)
        Lr = k.aring(2, [128, 4, 128], F32)
        Mr = k.aring(2, [128, 4, 128], BF16)
        xdr = k.aring(3, [128, 4, 64], BF16)
        yr = k.aring(4, [128, 256], F32)
        szr = k.aring(2, [128, 256], F32)
        vbr = k.aring(2, [128, 256], BF16)
        tmp4 = k.aring(2, [128, 4, 64], F32)
        for g in range(cfg.get("ssd_g", 8)):
            wA = wring.next()
            wap = ssd_w_in.rearrange("(k p) n -> p k n", p=128)
            S.dma("pool", wA[:, :, 0:256], wap[:, :, g * 256:(g + 1) * 256], writes=bufs(wA))
            S.dma("pool", wA[:, :, 256:512], wap[:, :, E + g * 256:E + (g + 1) * 256], writes=bufs(wA))
            wB = wring.next()
            S.dma("pool", wB[:, :, 0:128], wap[:, :, 2 * E + g * 128:2 * E + (g + 1) * 128], writes=bufs(wB))
            S.dma("pool", wB[:, :, 128:256], wap[:, :, 2 * E + 1024 + g * 128:2 * E + 1024 + (g + 1) * 128], writes=bufs(wB))
            for bi in range(4):
                wt, co, cblk = ((wA, 256, 2 * g), (wA, 384, 2 * g + 1), (wB, 0, 16 + g), (wB, 128, 24 + g))[bi]
                for q in range(T_ // 512):
                    pb = banks.next()
                    for kk in range(8):
                        S.op("pe", lambda e, pb=pb, kk=kk, q=q, wt=wt, co=co: e.matmul(
                            pb[:], lhsT=wt[:, kk, co:co + 128], rhs=hT[:, kk, q * 512:(q + 1) * 512],
                            start=(kk == 0), stop=(kk == 7)), reads=bufs(wt, hT), writes=bufs(pb))
                    S.op("act", lambda e, pb=pb, q=q: e.activation(out=raw[:, q * 512:(q + 1) * 512], in_=pb[:], func=AF.Copy),
                         reads=bufs(pb), writes=bufs(raw))
                cw = c["cwT"]
                rv = raw[:].rearrange("p (s l) -> p s l", s=nseq)
                av = acc[:].rearrange("p (s l) -> p s l", s=nseq)
                S.op("dve", lambda e, cblk=cblk: e.tensor_scalar(out=acc[:], in0=raw[:], scalar1=cw[:, cblk, 2:3], scalar2=None,
                                                                 op0=ALU.mult), reads=bufs(raw, cw), writes=bufs(acc))
                taps = ((0, "dve", slice(2, Lq), slice(0, Lq - 2)), (1, "dve", slice(1, Lq), slice(0, Lq - 1)),
                        (3, "dve", slice(0, Lq - 1), slice(1, Lq)), (4, "dve", slice(0, Lq - 2), slice(2, Lq)))
                for j, eng, osl, isl in taps:
                    S.op(eng, lambda e, j=j, osl=osl, isl=isl, cblk=cblk, rv=rv, av=av: e.scalar_tensor_tensor(
                        out=av[:, :, osl], in0=rv[:, :, isl], scalar=cw[:, cblk, j:j + 1], in1=av[:, :, osl],
                        op0=ALU.mult, op1=ALU.add), reads=bufs(raw, acc, cw), writes=bufs(acc))
                S.op("act", lambda e, bi=bi, cblk=cblk: e.activation(out=fm[bi][:], in_=acc[:], func=AF.Silu,
                                                                      bias=c["cbT"][:, cblk:cblk + 1]),
                     reads=bufs(acc, c["cbT"]), writes=bufs(fm[bi]))
            for t in range(ntile):
                ptb = banks.next()
                ptv = ptb[:].bitcast(BF16)
                for bi in range(3):
                    S.op("pe", lambda e, ptv=ptv, bi=bi, t=t: e.transpose(
                        out=ptv[:, bi * 128:(bi + 1) * 128], in_=fm[bi][:, t * 128:(t + 1) * 128], identity=identb[:]),
                        reads=bufs(fm[bi], identb), writes=bufs(ptb))
                S.op("act", lambda e, ptv=ptv, t=t: e.activation(out=XB[:, t, :], in_=ptv[:, 0:384], func=AF.Copy),
                     reads=bufs(ptb), writes=bufs(XB))
            BT, CT = fm[2], fm[3]
            for sq in range(nseq):
                for d_ in range(2):
                    hs = slice(d_ * 32 + 4 * g, d_ * 32 + 4 * g + 4)
                    if is_lat:
                        h0 = h0r.next()
                        for half in range(2):
                            S.dma("sp", h0[:, half, :], state_ssd[d_, 4 * g + 2 * half:4 * g + 2 * half + 2].rearrange("h p n -> (h p) n"),
                                  writes=bufs(h0))
                        ph = banks.next()
                        for half in range(2):
                            S.op("pe", lambda e, ph=ph, h0=h0, half=half: e.transpose(
                                out=ph[:, half * 128:(half + 1) * 128], in_=h0[:, half, :], identity=identf[:]),
                                reads=bufs(h0, identf), writes=bufs(ph))
                        S.op("act", lambda e, ph=ph, d_=d_: e.activation(out=stf[:, d_, :], in_=ph[:, 0:256], func=AF.Copy),
                             reads=bufs(ph), writes=bufs(stf))
                    else:
                        S.op("pool", lambda e, d_=d_: e.memset(stf[:, d_, :], 0.0), writes=bufs(stf))
                    order = range(nch) if d_ == 0 else range(nch - 1, -1, -1)
                    for ci in order:
                        t = sq * nch + ci
                        S.op("act", lambda e, t=t, d_=d_: e.activation(out=SIN[:, t, d_, :], in_=stf[:, d_, :], func=AF.Copy),
                             reads=bufs(stf), writes=bufs(SIN))
                        xdd = xdr.next()
                        S.op("pool", lambda e, xdd=xdd, t=t, hs=hs: e.tensor_tensor(
                            out=xdd[:], in0=XB[:, t, 0:256].rearrange("p (h d) -> p h d", h=4),
                            in1=dtd[:, t, hs].unsqueeze(2).to_broadcast([128, 4, 64]), op=ALU.mult),
                            reads=bufs(XB, dtd), writes=bufs(xdd))
                        psl = banks.next()
                        S.op("pe", lambda e, psl=psl, t=t, xdd=xdd: e.matmul(
                            psl[:, 0:256], lhsT=XB[:, t, 256:384], rhs=xdd[:].rearrange("p h d -> p (h d)"), start=True, stop=True),
                            reads=bufs(XB, xdd), writes=bufs(psl))
                        S.op("pool", lambda e, t=t, d_=d_, hs=hs: e.tensor_tensor(
                            out=stf[:, d_, :].rearrange("p (h d) -> p h d", h=4), in0=stf[:, d_, :].rearrange("p (h d) -> p h d", h=4),
                            in1=etot[:, t, hs].unsqueeze(2).to_broadcast([128, 4, 64]), op=ALU.mult),
                            reads=bufs(stf, etot), writes=bufs(stf))
                        S.op("dve", lambda e, psl=psl, d_=d_: e.tensor_tensor(out=stf[:, d_, :], in0=psl[:, 0:256], in1=stf[:, d_, :],
                                                                            op=ALU.add), reads=bufs(psl, stf), writes=bufs(stf))
                    if not is_lat:
                        pf = banks.next()
                        for half in range(2):
                            S.op("pe", lambda e, pf=pf, d_=d_, half=half: e.transpose(
                                out=pf[:, half * 128:(half + 1) * 128], in_=stf[:, d_, half * 128:(half + 1) * 128], identity=identf[:]),
                                reads=bufs(stf, identf), writes=bufs(pf))
                        fi = fir.next()
                        S.op("act", lambda e, pf=pf, fi=fi: e.activation(out=fi[:].rearrange("p a b -> p (a b)"), in_=pf[:, 0:256],
                                                                        func=AF.Copy), reads=bufs(pf), writes=bufs(fi))
                        for half in range(2):
                            S.dma("sp", new_ssd[sq, d_, 4 * g + 2 * half:4 * g + 2 * half + 2].rearrange("h p n -> (h p) n"),
                                  fi[:, half, :], reads=bufs(fi))
            for t in range(ntile):
                tsl = slice(t * 128, (t + 1) * 128)
                pg_ = banks.next()
                S.op("pe", lambda e, pg_=pg_, tsl=tsl: e.matmul(pg_[:, 0:128], lhsT=BT[:, tsl], rhs=CT[:, tsl], start=True, stop=True),
                     reads=bufs(BT, CT), writes=bufs(pg_))
                GT = GTr.next()
                S.op("act", lambda e, pg_=pg_, GT=GT: e.activation(out=GT[:], in_=pg_[:, 0:128], func=AF.Copy),
                     reads=bufs(pg_), writes=bufs(GT))
                py = banks.next()
                pos = []
                for d_ in range(2):
                    hs = slice(d_ * 32 + 4 * g, d_ * 32 + 4 * g + 4)
                    Dt = Dr.next()
                    S.op("pool", lambda e, Dt=Dt, d_=d_, t=t, hs=hs: e.tensor_tensor(
                        out=Dt[:], in0=c["tri"][d_][:].unsqueeze(1).to_broadcast([128, 4, 128]),
                        in1=da[:, t, hs].unsqueeze(2).to_broadcast([128, 4, 128]), op=ALU.mult),
                        reads=bufs(c["tri"][d_], da), writes=bufs(Dt))
                    pz_ = banks.next()
                    S.op("pe", lambda e, pz_=pz_, Dt=Dt: e.matmul(
                        pz_[:], lhsT=onesf[:], rhs=Dt[:].rearrange("p h t -> p (h t)"), start=True, stop=False),
                        reads=bufs(onesf, Dt), writes=bufs(pz_))
                    S.op("pe", lambda e, pz_=pz_, d_=d_, t=t, hs=hs: e.matmul(
                        pz_[:], lhsT=c["ntri"][d_][:], rhs=da[:, t, hs].unsqueeze(2).to_broadcast([128, 4, 128]), start=False, stop=False),
                        reads=bufs(c["ntri"][d_], da), writes=bufs(pz_))
                    S.op("pe", lambda e, pz_=pz_, d_=d_: e.matmul(
                        pz_[:], lhsT=identf[:], rhs=c["mneg"][d_][:].unsqueeze(1).to_broadcast([128, 4, 128]), start=False, stop=True),
                        reads=bufs(identf, c["mneg"][d_]), writes=bufs(pz_))
                    Lt = Lr.next()
                    S.op("act", lambda e, pz_=pz_, Lt=Lt: e.activation(out=Lt[:].rearrange("p h t -> p (h t)"), in_=pz_[:], func=AF.Exp),
                         reads=bufs(pz_), writes=bufs(Lt))
                    Mt = Mr.next()
                    S.op("dve", lambda e, Lt=Lt, Mt=Mt, GT=GT: e.tensor_tensor(
                        out=Mt[:], in0=Lt[:], in1=GT[:].unsqueeze(1).to_broadcast([128, 4, 128]), op=ALU.mult),
                        reads=bufs(Lt, GT), writes=bufs(Mt))
                    xd = xdr.next()
                    S.op("pool", lambda e, xd=xd, t=t, hs=hs: e.tensor_tensor(
                        out=xd[:], in0=XB[:, t, 0:256].rearrange("p (h d) -> p h d", h=4),
                        in1=dt_[:, t, hs].unsqueeze(2).to_broadcast([128, 4, 64]), op=ALU.mult),
                        reads=bufs(XB, dt_), writes=bufs(xd))
                    for h in range(4):
                        S.op("pe", lambda e, py=py, Mt=Mt, xd=xd, h=h, d_=d_: e.matmul(
                            py[:, h * 64:(h + 1) * 64], lhsT=Mt[:, h, :], rhs=xd[:, h, :], start=(d_ == 0 and h == 0), stop=(d_ == 1 and h == 3)),
                            reads=bufs(Mt, xd), writes=bufs(py))
                    po_ = banks.next()
                    S.op("pe", lambda e, po_=po_, tsl=tsl, t=t, d_=d_: e.matmul(
                        po_[:, 0:256], lhsT=CT[:, tsl], rhs=SIN[:, t, d_, :], start=True, stop=True),
                        reads=bufs(CT, SIN), writes=bufs(po_))
                    pos.append((po_, hs))
                y1 = yr.next()
                y2 = yr.next()
                for (po_, hs), yy in zip(pos, (y1, y2)):
                    S.op("dve", lambda e, po_=po_, hs=hs, yy=yy, t=t: e.tensor_tensor(
                        out=yy[:].rearrange("p (h d) -> p h d", h=4), in0=po_[:, 0:256].rearrange("p (h d) -> p h d", h=4),
                        in1=ecum[:, t, hs].unsqueeze(2).to_broadcast([128, 4, 64]), op=ALU.mult),
                        reads=bufs(po_, ecum), writes=bufs(yy))
                S.op("pool", lambda e, y1=y1, y2=y2: e.tensor_tensor(out=y1[:], in0=y1[:], in1=y2[:], op=ALU.add),
                     reads=bufs(y1, y2), writes=bufs(y1))
                S.op("pool", lambda e, y2=y2, t=t, g=g: e.tensor_tensor(
                    out=y2[:].rearrange("p (h d) -> p h d", h=4), in0=XB[:, t, 0:256].rearrange("p (h d) -> p h d", h=4),
                    in1=c["dsk"][:, 4 * g:4 * g + 4].unsqueeze(2).to_broadcast([128, 4, 64]), op=ALU.mult),
                    reads=bufs(XB, c["dsk"]), writes=bufs(y2))
                S.op("pool", lambda e, y1=y1, y2=y2: e.tensor_tensor(out=y1[:], in0=y1[:], in1=y2[:], op=ALU.add),
                     reads=bufs(y1, y2), writes=bufs(y1))
                S.op("dve", lambda e, py=py, y1=y1: e.tensor_tensor(out=y1[:], in0=py[:, 0:256], in1=y1[:], op=ALU.add),
                     reads=bufs(py, y1), writes=bufs(y1))
                pzz = banks.next()
                for kk in range(8):
                    S.op("pe", lambda e, pzz=pzz, kk=kk, tsl=tsl, wA=wA: e.matmul(
                        pzz[:, 0:256], lhsT=hT[:, kk, tsl], rhs=wA[:, kk, 0:256], start=(kk == 0), stop=(kk == 7)),
                        reads=bufs(hT, wA), writes=bufs(pzz))
                sz = szr.next()
                S.op("act", lambda e, pzz=pzz, sz=sz: e.activation(out=sz[:], in_=pzz[:, 0:256], func=AF.Silu),
                     reads=bufs(pzz), writes=bufs(sz))
                vb_ = vbr.next()
                S.op("pool", lambda e, vb_=vb_, y1=y1, sz=sz: e.tensor_tensor(out=vb_[:], in0=y1[:], in1=sz[:], op=ALU.mult),
                     reads=bufs(y1, sz), writes=bufs(vb_))
                S.op("act", lambda e, vb_=vb_, t=t, g=g: e.activation(out=junk[:, 0:256], in_=vb_[:], func=AF.Square,
                                                                     accum_out=ssq[:, t, g:g + 1]),
                     reads=bufs(vb_), writes=bufs(junk, ssq))
                ptb = banks.next()
                ptv = ptb[:].bitcast(BF16)
                for bb in range(2):
                    S.op("pe", lambda e, ptv=ptv, vb_=vb_, bb=bb: e.transpose(
                        out=ptv[:, bb * 128:(bb + 1) * 128], in_=vb_[:, bb * 128:(bb + 1) * 128], identity=identb[:]),
                        reads=bufs(vb_, identb), writes=bufs(ptb))
                for bb in range(2):
                    S.op("act", lambda e, ptv=ptv, bb=bb, tsl=tsl, g=g: e.activation(
                        out=vTg[:, bb, tsl], in_=ptv[:, bb * 128:(bb + 1) * 128], func=AF.Identity,
                        scale=c["ngT"][:, 2 * g + bb:2 * g + bb + 1]), reads=bufs(ptb, c["ngT"]), writes=bufs(vTg))
            S.dma("sp", yscr[2 * g:2 * g + 2, :, tok0:tok0 + T_].rearrange("b p t -> p b t"), vTg[:], reads=bufs(vTg))
        S.op("dve", lambda e: e.tensor_reduce(out=rstd[:], in_=ssq[:], axis=AX.X, op=ALU.add), reads=bufs(ssq), writes=bufs(rstd))
        S.op("dve", lambda e: e.tensor_scalar(out=rstd[:], in0=rstd[:], scalar1=1.0 / E, scalar2=EPS, op0=ALU.mult, op1=ALU.add),
             reads=bufs(rstd), writes=bufs(rstd))
        S.op("act", lambda e: e.activation(out=rstd[:], in_=rstd[:], func=AF.Sqrt), reads=bufs(rstd), writes=bufs(rstd))
        S.op("dve", lambda e: e.reciprocal(out=rstd[:], in_=rstd[:]), reads=bufs(rstd), writes=bufs(rstd))
        return rstd

    TWO_PI = float(2 * np.pi)
    MAGIC = 12582912.0

    def s5_prep():
        c = {}
        c["AA"] = k.at([128, 64, 2, 2], F32)
        c["BB"] = k.at([128, 64, 2, 2], F32)
        c["Wsel"] = k.at([128, 8, 240], BF16)
        c["dT"] = k.at([128, 16], F32)
        c["bgT"] = k.at([128, 16], F32)
        c["h0"] = k.at([128, 2, 2, 64], F32)
        S.dma("sp", c["dT"][:], s5_d.rearrange("(b p) -> p b", p=128), writes=bufs(c["dT"]))
        S.dma("sp", c["bgT"][:], s5_b_glu.rearrange("(b p) -> p b", p=128), writes=bufs(c["bgT"]))
        for d_ in range(2):
            S.dma("sp", c["h0"][:, d_, :, :], s5_h0[d_].rearrange("r p g -> p r g"), writes=bufs(c["h0"]))
        mm = k.amark()
        wself = k.at([128, 8, 240], F32)
        S.op("pool", lambda e: e.memset(wself[:], 0.0), writes=bufs(wself))
        S.op("pool", lambda e: e.affine_select(out=wself[:, :, 112:128], in_=wself[:, :, 112:128], compare_op=ALU.not_equal,
                                               fill=1.0, base=0, pattern=[[-16, 8], [-1, 16]], channel_multiplier=1),
             reads=bufs(wself), writes=bufs(wself))
        S.op("pool", lambda e: e.tensor_copy(out=c["Wsel"][:], in_=wself[:]), reads=bufs(wself), writes=bufs(c["Wsel"]))
        maskT = [k.at([128, 8, 16], F32), k.at([128, 8, 16], F32)]
        for d_ in range(2):
            S.op("pool", lambda e, d_=d_: e.memset(maskT[d_][:], 1.0), writes=bufs(maskT[d_]))
        S.op("pool", lambda e: e.affine_select(out=maskT[0][:], in_=maskT[0][:], compare_op=ALU.is_ge, fill=0.0, base=15,
                                               pattern=[[16, 8], [0, 16]], channel_multiplier=-1),
             reads=bufs(maskT[0]), writes=bufs(maskT[0]))
        S.op("pool", lambda e: e.affine_select(out=maskT[1][:], in_=maskT[1][:], compare_op=ALU.is_ge, fill=0.0, base=0,
                                               pattern=[[-16, 8], [0, 16]], channel_multiplier=1),
             reads=bufs(maskT[1]), writes=bufs(maskT[1]))
        pw = [[k.at([128, 64, 16], F32), k.at([128, 64, 16], F32)] for _ in range(2)]
        coef = [[k.at([128, 64], F32), k.at([128, 64], F32)] for _ in range(2)]
        lr, li, ls = k.at([128, 64], F32), k.at([128, 64], F32), k.at([128, 64], F32)
        xx, ang = k.at([128, 64], F32), k.at([128, 64], F32)
        tr = k.aring(6, [128, 64], F32)
        for d_ in range(2):
            S.dma("sp", lr[:], s5_lam[0, d_], writes=bufs(lr))
            S.dma("sp", li[:], s5_lam[1, d_], writes=bufs(li))
            S.dma("sp", ls[:], s5_lstep[d_], writes=bufs(ls))
            S.op("act", lambda e: e.activation(out=ls[:], in_=ls[:], func=AF.Exp), reads=bufs(ls), writes=bufs(ls))
            S.op("dve", lambda e: e.tensor_tensor(out=xx[:], in0=lr[:], in1=ls[:], op=ALU.mult), reads=bufs(lr, ls), writes=bufs(xx))
            S.op("dve", lambda e: e.tensor_tensor(out=ang[:], in0=li[:], in1=ls[:], op=ALU.mult), reads=bufs(li, ls), writes=bufs(ang))
            pre, pim = pw[d_]
            for kq in range(1, 9):
                mp, mn, sn, cs, t1, t2 = [tr.next() for _ in range(6)]
                S.op("act", lambda e, mp=mp, kq=kq: e.activation(out=mp[:], in_=xx[:], func=AF.Exp, scale=float(kq)),
                     reads=bufs(xx), writes=bufs(mp))
                S.op("act", lambda e, mn=mn, kq=kq: e.activation(out=mn[:], in_=xx[:], func=AF.Exp, scale=float(-kq)),
                     reads=bufs(xx), writes=bufs(mn))
                for dst, shift in ((sn, 0.0), (cs, 0.25)):
                    if shift:
                        S.op("dve", lambda e, t1=t1, kq=kq, shift=shift: e.tensor_scalar(
                            out=t1[:], in0=ang[:], scalar1=float(kq / TWO_PI), scalar2=shift, op0=ALU.mult, op1=ALU.add),
                            reads=bufs(ang), writes=bufs(t1))
                        S.op("dve", lambda e, t1=t1: e.tensor_scalar(out=t1[:], in0=t1[:], scalar1=MAGIC, scalar2=None, op0=ALU.add),
                             reads=bufs(t1), writes=bufs(t1))
                    else:
                        S.op("dve", lambda e, t1=t1, kq=kq: e.tensor_scalar(
                            out=t1[:], in0=ang[:], scalar1=float(kq / TWO_PI), scalar2=MAGIC, op0=ALU.mult, op1=ALU.add),
                            reads=bufs(ang), writes=bufs(t1))
                    S.op("dve", lambda e, t1=t1: e.tensor_scalar(out=t1[:], in0=t1[:], scalar1=-MAGIC, scalar2=-TWO_PI,
                                                                 op0=ALU.add, op1=ALU.mult), reads=bufs(t1), writes=bufs(t1))
                    S.op("dve", lambda e, t1=t1, kq=kq: e.scalar_tensor_tensor(
                        out=t1[:], in0=ang[:], scalar=float(kq), in1=t1[:], op0=ALU.mult, op1=ALU.add),
                        reads=bufs(ang, t1), writes=bufs(t1))
                    if shift:
                        S.op("dve", lambda e, t1=t1: e.tensor_scalar(out=t1[:], in0=t1[:], scalar1=float(np.pi / 2), scalar2=None,
                                                                     op0=ALU.add), reads=bufs(t1), writes=bufs(t1))
                    S.op("act", lambda e, t1=t1, dst=dst: e.activation(out=dst[:], in_=t1[:], func=AF.Sin),
                         reads=bufs(t1), writes=bufs(dst))
                S.op("dve", lambda e, kq=kq, mp=mp, cs=cs, pre=pre: e.tensor_tensor(out=pre[:, :, kq - 1], in0=mp[:], in1=cs[:], op=ALU.mult),
                     reads=bufs(mp, cs), writes=bufs(pre))
                S.op("dve", lambda e, kq=kq, mp=mp, sn=sn, pim=pim: e.tensor_tensor(out=pim[:, :, kq - 1], in0=mp[:], in1=sn[:], op=ALU.mult),
                     reads=bufs(mp, sn), writes=bufs(pim))
                S.op("dve", lambda e, kq=kq, mn=mn, cs=cs, pre=pre: e.tensor_tensor(out=pre[:, :, 7 + kq], in0=mn[:], in1=cs[:], op=ALU.mult),
                     reads=bufs(mn, cs), writes=bufs(pre))
                S.op("dve", lambda e, kq=kq, mn=mn, sn=sn, pim=pim: e.scalar_tensor_tensor(
                    out=pim[:, :, 7 + kq], in0=mn[:], scalar=-1.0, in1=sn[:], op0=ALU.mult, op1=ALU.mult),
                    reads=bufs(mn, sn), writes=bufs(pim))
            for r_ in range(2):
                S.op("act", lambda e, d_=d_, r_=r_, pre=pre: e.activation(out=c["AA"][:, :, d_, r_], in_=pre[:, :, 7], func=AF.Copy),
                     reads=bufs(pre), writes=bufs(c["AA"]))
            S.op("dve", lambda e, d_=d_, pim=pim: e.tensor_scalar(out=c["BB"][:, :, d_, 0], in0=pim[:, :, 7], scalar1=-1.0, scalar2=None,
                                                         op0=ALU.mult), reads=bufs(pim), writes=bufs(c["BB"]))
            S.op("act", lambda e, d_=d_, pim=pim: e.activation(out=c["BB"][:, :, d_, 1], in_=pim[:, :, 7], func=AF.Copy),
                 reads=bufs(pim), writes=bufs(c["BB"]))
            den, nr, t1, t2 = [tr.next() for _ in range(4)]
            S.op("dve", lambda e, den=den: e.tensor_tensor(out=den[:], in0=lr[:], in1=lr[:], op=ALU.mult), reads=bufs(lr), writes=bufs(den))
            S.op("dve", lambda e, t1=t1: e.tensor_tensor(out=t1[:], in0=li[:], in1=li[:], op=ALU.mult), reads=bufs(li), writes=bufs(t1))
            S.op("dve", lambda e, den=den, t1=t1: e.tensor_tensor(out=den[:], in0=den[:], in1=t1[:], op=ALU.add),
                 reads=bufs(den, t1), writes=bufs(den))
            S.op("dve", lambda e, den=den: e.reciprocal(out=den[:], in_=den[:]), reads=bufs(den), writes=bufs(den))
            S.op("dve", lambda e, nr=nr, pre=pre: e.tensor_scalar(out=nr[:], in0=pre[:, :, 0], scalar1=-1.0, scalar2=None, op0=ALU.add),
                 reads=bufs(pre), writes=bufs(nr))
            cr_, ci_ = coef[d_]
            S.op("dve", lambda e, nr=nr, t1=t1: e.tensor_tensor(out=t1[:], in0=nr[:], in1=lr[:], op=ALU.mult), reads=bufs(nr, lr), writes=bufs(t1))
            S.op("dve", lambda e, t2=t2, pim=pim: e.tensor_tensor(out=t2[:], in0=pim[:, :, 0], in1=li[:], op=ALU.mult), reads=bufs(pim, li), writes=bufs(t2))
            S.op("dve", lambda e, t1=t1, t2=t2: e.tensor_tensor(out=t1[:], in0=t1[:], in1=t2[:], op=ALU.add), reads=bufs(t1, t2), writes=bufs(t1))
            S.op("dve", lambda e, t1=t1, den=den, cr_=cr_: e.tensor_tensor(out=cr_[:], in0=t1[:], in1=den[:], op=ALU.mult),
                 reads=bufs(t1, den), writes=bufs(cr_))
            S.op("dve", lambda e, t1=t1, pim=pim: e.tensor_tensor(out=t1[:], in0=pim[:, :, 0], in1=lr[:], op=ALU.mult), reads=bufs(pim, lr), writes=bufs(t1))
            S.op("dve", lambda e, nr=nr, t2=t2: e.tensor_tensor(out=t2[:], in0=nr[:], in1=li[:], op=ALU.mult), reads=bufs(nr, li), writes=bufs(t2))
            S.op("dve", lambda e, t1=t1, t2=t2: e.tensor_tensor(out=t1[:], in0=t1[:], in1=t2[:], op=ALU.subtract), reads=bufs(t1, t2), writes=bufs(t1))
            S.op("dve", lambda e, t1=t1, den=den, ci_=ci_: e.tensor_tensor(out=ci_[:], in0=t1[:], in1=den[:], op=ALU.mult),
                 reads=bufs(t1, den), writes=bufs(ci_))
        Braw = [k.at([128, 8, 16], F32), k.at([128, 8, 16], F32)]
        Craw = [k.at([128, 8, 16], F32), k.at([128, 8, 16], F32)]
        Bb = [k.at([128, 8, 16], F32), k.at([128, 8, 16], F32)]
        V = [[k.at([128, 8, 8, 16], F32), k.at([128, 8, 8, 16], F32)] for _ in range(2)]
        W2 = [[k.at([128, 8, 8, 16], F32), k.at([128, 8, 8, 16], F32)] for _ in range(2)]
        t8 = k.aring(4, [128, 8, 16], F32)
        t8e = {"pool": k.aring(4, [128, 8, 16], F32), "dve": k.aring(4, [128, 8, 16], F32)}
        T16 = k.aring(2, [128, 16, 128], BF16)
        VT16 = k.aring(2, [128, 8, 2, 2, 128], BF16)
        W216 = k.aring(2, [128, 8, 2, 2, 128], BF16)
        Ttmp = k.aring(2, [128, 128], F32)
        Ttmp2 = k.aring(2, [128, 128], F32)

        def bc_j(ap2):
            return ap2.unsqueeze(2).to_broadcast([128, 8, 16])

        for b in range(8):
            gs = slice(8 * b, 8 * b + 8)
            t16, vt16, w216 = T16.next(), VT16.next(), W216.next()
            for d_ in range(2):
                pre, pim = pw[d_]
                cr_, ci_ = coef[d_]
                for r_ in range(2):
                    S.dma("sp", Braw[r_][:], s5_B[r_, d_, :, gs, :], writes=bufs(Braw[r_]))
                    S.dma("sp", Craw[r_][:], s5_C[r_, d_, :, gs, :], writes=bufs(Craw[r_]))
                ta, tb = t8.next(), t8.next()
                S.op("dve", lambda e, ta=ta, cr_=cr_, gs=gs: e.tensor_tensor(out=ta[:], in0=Braw[0][:], in1=bc_j(cr_[:, gs]), op=ALU.mult),
                     reads=bufs(Braw[0], cr_), writes=bufs(ta))
                S.op("dve", lambda e, tb=tb, ci_=ci_, gs=gs: e.tensor_tensor(out=tb[:], in0=Braw[1][:], in1=bc_j(ci_[:, gs]), op=ALU.mult),
                     reads=bufs(Braw[1], ci_), writes=bufs(tb))
                S.op("dve", lambda e, ta=ta, tb=tb: e.tensor_tensor(out=Bb[0][:], in0=ta[:], in1=tb[:], op=ALU.subtract),
                     reads=bufs(ta, tb), writes=bufs(Bb[0]))
                ta, tb = t8.next(), t8.next()
                S.op("dve", lambda e, ta=ta, cr_=cr_, gs=gs: e.tensor_tensor(out=ta[:], in0=Braw[1][:], in1=bc_j(cr_[:, gs]), op=ALU.mult),
                     reads=bufs(Braw[1], cr_), writes=bufs(ta))
                S.op("dve", lambda e, tb=tb, ci_=ci_, gs=gs: e.tensor_tensor(out=tb[:], in0=Braw[0][:], in1=bc_j(ci_[:, gs]), op=ALU.mult),
                     reads=bufs(Braw[0], ci_), writes=bufs(tb))
                S.op("dve", lambda e, ta=ta, tb=tb: e.tensor_tensor(out=Bb[1][:], in0=ta[:], in1=tb[:], op=ALU.add),
                     reads=bufs(ta, tb), writes=bufs(Bb[1]))
                for s_ in range(8):
                    kv = 8 + (s_ if d_ == 0 else 7 - s_)
                    kw = s_ if d_ == 0 else 7 - s_
                    for (eng, P_idx, X_, out_, neg_im) in (("pool", kv, Bb, V[d_], False), ("dve", kw, Craw, W2[d_], True)):
                        Pr = bc_j(pre[:, gs, P_idx])
                        Pi = bc_j(pim[:, gs, P_idx])
                        ta, tb = t8e[eng].next(), t8e[eng].next()
                        S.op(eng, lambda e, ta=ta, X_=X_, Pr=Pr: e.tensor_tensor(out=ta[:], in0=X_[0][:], in1=Pr, op=ALU.mult),
                             reads=bufs(X_[0], pre), writes=bufs(ta))
                        S.op(eng, lambda e, tb=tb, X_=X_, Pi=Pi: e.tensor_tensor(out=tb[:], in0=X_[1][:], in1=Pi, op=ALU.mult),
                             reads=bufs(X_[1], pim), writes=bufs(tb))
                        S.op(eng, lambda e, ta=ta, tb=tb, out_=out_, s_=s_: e.tensor_tensor(
                            out=out_[0][:, :, s_, :], in0=ta[:], in1=tb[:], op=ALU.subtract), reads=bufs(ta, tb), writes=bufs(out_[0]))
                        ta, tb = t8e[eng].next(), t8e[eng].next()
                        S.op(eng, lambda e, ta=ta, X_=X_, Pi=Pi: e.tensor_tensor(out=ta[:], in0=X_[0][:], in1=Pi, op=ALU.mult),
                             reads=bufs(X_[0], pim), writes=bufs(ta))
                        S.op(eng, lambda e, tb=tb, X_=X_, Pr=Pr: e.tensor_tensor(out=tb[:], in0=X_[1][:], in1=Pr, op=ALU.mult),
                             reads=bufs(X_[1], pre), writes=bufs(tb))
                        if not neg_im:
                            S.op(eng, lambda e, ta=ta, tb=tb, out_=out_, s_=s_: e.tensor_tensor(
                                out=out_[1][:, :, s_, :], in0=ta[:], in1=tb[:], op=ALU.add), reads=bufs(ta, tb), writes=bufs(out_[1]))
                        else:
                            S.op(eng, lambda e, ta=ta, tb=tb: e.tensor_tensor(out=ta[:], in0=ta[:], in1=tb[:], op=ALU.add),
                                 reads=bufs(ta, tb), writes=bufs(ta))
                            S.op(eng, lambda e, ta=ta, out_=out_, s_=s_: e.tensor_scalar(
                                out=out_[1][:, :, s_, :], in0=ta[:], scalar1=-1.0, scalar2=None, op0=ALU.mult),
                                reads=bufs(ta), writes=bufs(out_[1]))
                for r_ in range(2):
                    S.op("act", lambda e, d_=d_, r_=r_, w216=w216: e.activation(
                        out=w216[:, :, d_, r_, :], in_=W2[d_][r_][:].rearrange("p g s j -> p g (s j)"), func=AF.Copy),
                        reads=bufs(W2[d_][r_]), writes=bufs(w216))
                for gl in range(8):
                    pv_ = banks.next()
                    for r_ in range(2):
                        S.op("pe", lambda e, pv_=pv_, d_=d_, r_=r_, gl=gl: e.transpose(
                            out=pv_[:, r_ * 128:(r_ + 1) * 128], in_=V[d_][r_][:, gl, :, :].rearrange("p s j -> p (s j)"),
                            identity=identf[:]), reads=bufs(V[d_][r_], identf), writes=bufs(pv_))
                    S.op("act", lambda e, pv_=pv_, d_=d_, gl=gl, vt16=vt16: e.activation(
                        out=vt16[:, gl, d_, :, :].rearrange("p r m -> p (r m)"), in_=pv_[:, 0:256], func=AF.Copy),
                        reads=bufs(pv_), writes=bufs(vt16))
            for gl in range(8):
                for par in range(2):
                    rows = slice(par * 64, (par + 1) * 64)
                    gi = gl * 2 + par
                    pT = banks.next()
                    for d_ in range(2):
                        for r_ in range(2):
                            S.op("pe", lambda e, pT=pT, d_=d_, r_=r_, gl=gl, rows=rows: e.matmul(
                                pT[:, d_ * 128:(d_ + 1) * 128], lhsT=V[d_][r_][rows, gl, :, :].rearrange("p s j -> p (s j)"),
                                rhs=W2[d_][r_][rows, gl, :, :].rearrange("p s j -> p (s j)"), start=(r_ == 0), stop=(r_ == 1)),
                                reads=bufs(V[d_][r_], W2[d_][r_]), writes=bufs(pT))
                    ta, tb = Ttmp.next(), Ttmp2.next()
                    S.op("dve", lambda e, pT=pT, ta=ta: e.tensor_tensor(
                        out=ta[:], in0=pT[:, 0:128], in1=maskT[0][:].rearrange("p t j -> p (t j)"), op=ALU.mult),
                        reads=bufs(pT, maskT[0]), writes=bufs(ta))
                    S.op("dve", lambda e, pT=pT, tb=tb: e.tensor_tensor(
                        out=tb[:], in0=pT[:, 128:256], in1=maskT[1][:].rearrange("p t j -> p (t j)"), op=ALU.mult),
                        reads=bufs(pT, maskT[1]), writes=bufs(tb))
                    S.op("pool", lambda e, ta=ta, tb=tb, t16=t16, gi=gi: e.tensor_tensor(out=t16[:, gi, :], in0=ta[:], in1=tb[:], op=ALU.add),
                         reads=bufs(ta, tb), writes=bufs(t16))
            S.dma("sp", Tscr[b], t16[:].rearrange("p g m -> p (g m)"), reads=bufs(t16))
            S.dma("sp", VTscr[b], vt16[:].rearrange("p g d r m -> p (g d r m)"), reads=bufs(vt16))
            S.dma("sp", W2scr[b], w216[:].rearrange("p g d r m -> p (g d r m)"), reads=bufs(w216))
        k.arestore(mm)
        return c

    def s5_tiles(tok0, sub, is_lat):
        tiles = []
        for i in range(8):
            I_ = sub * 8 + i
            if not is_lat:
                s_, c0 = I_, 0
            else:
                s_, c0 = I_ // 2, (I_ % 2) * 128
            base = tok0 + 8 * c0 + s_
            tiles.append(((lambda src_, base=base: src_[base:base + 8 * 127 + 1:8, :]), I_ * 128))
        return tiles

    def s5_unit(c, tok0, is_lat):
        C_ = 256 if is_lat else 128
        nseq = 1 if is_lat else 4
        nch = C_ // nseq
        nct = C_ // 128
        Tn = 8 * C_
        U = k.at([128, nct, 16, 8, 16], BF16)
        X = k.at([128, 16, C_], BF16)
        arr = k.at([128, 16, 2, nseq, nch + 1], F32)
        Hb = k.at([128, 16, 2, nseq, nch + 1], BF16)
        Ysb = T(U.t[:].rearrange("p a g s j -> p (a g s j)").rearrange("p (g c) -> p g c", g=16))
        Ysb.b = U.b
        Tw = k.at([128, 16, 128], BF16)
        VTw = k.at([128, 8, 2, 2, 128], BF16)
        W2w = k.at([128, 8, 2, 2, 128], BF16)
        ygst = k.at([128, 2, Tn], BF16)
        uur = k.aring(2, [128, 512], F32)
        ysr = k.aring(2, [128, 512], F32)
        tmps = {eng: [k.at([128, 8, 2, nseq], F32) for _ in range(3)] for eng in ("dve", "pool")}
        GPB = 512 // C_
        bl = {}
        if is_lat:
            for eng in ("dve", "pool"):
                bl[eng] = {"PR": k.at([128, 8, 16], F32), "PI": k.at([128, 8, 16], F32),
                           "AAp": k.at([128, 8, 2, 16], F32), "BBp": k.at([128, 8, 2, 16], F32),
                           "cc": k.at([128, 8, 2, 17], F32),
                           "t": [k.at([128, 8, 8], F32) for _ in range(4)],
                           "l": [k.at([128, 8, 2, 16], F32) for _ in range(3)],
                           "c": [k.at([128, 8, 2], F32) for _ in range(2)],
                           "f": [[k.at([128, 8, 2, 16], F32) for _ in range(2)] for _ in range(2)]}
        for b in range(cfg.get("s5_nb", 8)):
            gs = slice(8 * b, 8 * b + 8)
            S.dma("sp", Tw[:].rearrange("p g m -> p (g m)"), Tscr[b], writes=bufs(Tw))
            S.dma("sp", VTw[:].rearrange("p g d r m -> p (g d r m)"), VTscr[b], writes=bufs(VTw))
            S.dma("sp", W2w[:].rearrange("p g d r m -> p (g d r m)"), W2scr[b], writes=bufs(W2w))
            wu = wring.next()
            S.dma("pool", wu[:, :, 0:256], s5_w_in.rearrange("(k p) n -> p k n", p=128)[:, :, 256 * b:256 * (b + 1)], writes=bufs(wu))
            for ct in range(nct):
                for s2 in range(4):
                    pb = banks.next()
                    for si in range(2):
                        s_ = s2 * 2 + si
                        p0 = s_ * C_ + ct * 128
                        for kk in range(8):
                            S.op("pe", lambda e, pb=pb, kk=kk, si=si, p0=p0, wu=wu: e.matmul(
                                pb[:, si * 256:(si + 1) * 256], lhsT=hT[:, kk, p0:p0 + 128], rhs=wu[:, kk, 0:256],
                                start=(kk == 0), stop=(kk == 7)), reads=bufs(hT, wu), writes=bufs(pb))
                    S.op("act", lambda e, pb=pb, ct=ct, s2=s2: e.activation(
                        out=U[:, ct, :, s2 * 2:s2 * 2 + 2, :], in_=pb[:].rearrange("p (s g j) -> p g s j", s=2, g=16), func=AF.Copy),
                        reads=bufs(pb), writes=bufs(U))
            for ct in range(nct):
                for g4 in range(4):
                    pb = banks.next()
                    for gg in range(4):
                        gi = g4 * 4 + gg
                        S.op("pe", lambda e, pb=pb, gg=gg, gi=gi, ct=ct: e.matmul(
                            pb[:, gg * 128:(gg + 1) * 128], lhsT=U[:, ct, gi, :, :].rearrange("p s j -> p (s j)"), rhs=identb[:], start=True, stop=True),
                            reads=bufs(U, identb), writes=bufs(pb))
                    S.op("act", lambda e, pb=pb, g4=g4, ct=ct: e.activation(
                        out=X[:, g4 * 4:g4 * 4 + 4, ct * 128:(ct + 1) * 128], in_=pb[:].rearrange("p (g c) -> p g c", g=4), func=AF.Copy),
                        reads=bufs(pb), writes=bufs(X))
            if is_lat:
                S.op("act", lambda e, gs=gs: e.activation(
                    out=arr[:, :, :, 0, 0].rearrange("p (g d) r -> p g d r", d=2),
                    in_=c["h0"][:, :, :, gs].rearrange("p d r g -> p g d r"), func=AF.Copy), reads=bufs(c["h0"]), writes=bufs(arr))
            else:
                S.op("pool", lambda e: e.memset(arr[:, :, :, :, 0:1], 0.0), writes=bufs(arr))
            for gl in range(8):
                pGs = [banks.next() for _ in range(nct)]
                for par in range(2):
                    gi = 2 * gl + par
                    rows = slice(par * 64, (par + 1) * 64)
                    for d_ in range(2):
                        if d_ == 0:
                            rhs = X[:, gi, :]
                        else:
                            rhs = X[:, gi, :].rearrange("p (s c) -> p s c", s=nseq)[:, :, ::-1]
                        for r_ in range(2):
                            if is_lat:
                                outp = pGs[d_][rows, r_ * 256:(r_ + 1) * 256]
                                pgb = pGs[d_]
                            else:
                                outp = pGs[0][rows, (d_ * 2 + r_) * 128:(d_ * 2 + r_ + 1) * 128]
                                pgb = pGs[0]
                            S.op("pe", lambda e, outp=outp, gl=gl, d_=d_, r_=r_, par=par, rhs=rhs: e.matmul(
                                outp, lhsT=VTw[:, gl, d_, r_, par * 64:(par + 1) * 64], rhs=rhs, start=True, stop=True),
                                reads=bufs(VTw, X), writes=bufs(pgb))
                for d_ in range(2):
                    if is_lat:
                        src_ = pGs[d_][:].rearrange("p (r s c) -> p r s c", r=2, s=1)
                        pgb = pGs[d_]
                    else:
                        src_ = pGs[0][:, d_ * 256:(d_ + 1) * 256].rearrange("p (r s c) -> p r s c", r=2, s=nseq)
                        pgb = pGs[0]
                    S.op("act", lambda e, src_=src_, gl=gl, d_=d_: e.activation(
                        out=arr[:, gl * 2 + d_, :, :, 1:nch + 1], in_=src_, func=AF.Copy), reads=bufs(pgb), writes=bufs(arr))
            AAb = c["AA"][:, gs, :, :].rearrange("p g d r -> p (g d) r")
            BBb = c["BB"][:, gs, :, :].rearrange("p g d r -> p (g d) r")
            if not is_lat:
                for kq in range(nch):
                    for eng, qs in (("dve", slice(0, 8)), ("pool", slice(8, 16))):
                        tt, p1, p2 = tmps[eng]
                        S.op(eng, lambda e, tt=tt, qs=qs, kq=kq: e.tensor_tensor(
                            out=tt[:], in0=arr[:, qs, :, :, kq], in1=arr[:, qs, :, :, kq + 1], op=ALU.add),
                            reads=bufs(arr), writes=bufs(tt))
                        S.op(eng, lambda e, tt=tt, p1=p1, qs=qs, AAb=AAb: e.tensor_tensor(
                            out=p1[:], in0=tt[:], in1=AAb[:, qs, :].unsqueeze(3).to_broadcast([128, 8, 2, nseq]), op=ALU.mult),
                            reads=bufs(tt, c["AA"]), writes=bufs(p1))
                        S.op(eng, lambda e, tt=tt, p2=p2, qs=qs, BBb=BBb: e.tensor_tensor(
                            out=p2[:], in0=tt[:, :, ::-1, :], in1=BBb[:, qs, :].unsqueeze(3).to_broadcast([128, 8, 2, nseq]), op=ALU.mult),
                            reads=bufs(tt, c["BB"]), writes=bufs(p2))
                        S.op(eng, lambda e, p1=p1, p2=p2, qs=qs, kq=kq: e.tensor_tensor(
                            out=arr[:, qs, :, :, kq + 1], in0=p1[:], in1=p2[:], op=ALU.add), reads=bufs(p1, p2), writes=bufs(arr))
                S.op("act", lambda e: e.activation(out=Hb[:].rearrange("p q r s c -> p (q r s c)"),
                                                   in_=arr[:].rearrange("p q r s c -> p (q r s c)"), func=AF.Copy),
                     reads=bufs(arr), writes=bufs(Hb))
            else:
                NB_, BL_ = 16, 16
                for eng, qs in (("dve", slice(0, 8)), ("pool", slice(8, 16))):
                    B_ = bl[eng]
                    PR, PI, AAp, BBp, cc = B_["PR"], B_["PI"], B_["AAp"], B_["BBp"], B_["cc"]
                    tA, tB, tC, tD = B_["t"]
                    AAh = AAb[:, qs, :]
                    BBh = BBb[:, qs, :]
                    S.op(eng, lambda e, PR=PR, AAh=AAh: e.tensor_copy(out=PR[:, :, 0], in_=AAh[:, :, 0]), reads=bufs(c["AA"]), writes=bufs(PR))
                    S.op(eng, lambda e, PI=PI, BBh=BBh: e.tensor_copy(out=PI[:, :, 0], in_=BBh[:, :, 1]), reads=bufs(c["BB"]), writes=bufs(PI))
                    m_ = 1
                    while m_ < 16:
                        ar = PR[:, :, m_ - 1:m_].to_broadcast([128, 8, m_])
                        ai = PI[:, :, m_ - 1:m_].to_broadcast([128, 8, m_])
                        src_r, src_i = PR[:, :, 0:m_], PI[:, :, 0:m_]
                        dst_r, dst_i = PR[:, :, m_:2 * m_], PI[:, :, m_:2 * m_]
                        ta, tb = tA[:, :, 0:m_], tB[:, :, 0:m_]
                        tc_, td = tC[:, :, 0:m_], tD[:, :, 0:m_]
                        S.op(eng, lambda e, ta=ta, src_r=src_r, ar=ar: e.tensor_tensor(out=ta, in0=src_r, in1=ar, op=ALU.mult), reads=bufs(PR), writes=bufs(tA))
                        S.op(eng, lambda e, tb=tb, src_i=src_i, ai=ai: e.tensor_tensor(out=tb, in0=src_i, in1=ai, op=ALU.mult), reads=bufs(PI), writes=bufs(tB))
                        S.op(eng, lambda e, tc_=tc_, src_r=src_r, ai=ai: e.tensor_tensor(out=tc_, in0=src_r, in1=ai, op=ALU.mult), reads=bufs(PR, PI), writes=bufs(tC))
                        S.op(eng, lambda e, td=td, src_i=src_i, ar=ar: e.tensor_tensor(out=td, in0=src_i, in1=ar, op=ALU.mult), reads=bufs(PR, PI), writes=bufs(tD))
                        S.op(eng, lambda e, dst_r=dst_r, ta=ta, tb=tb: e.tensor_tensor(out=dst_r, in0=ta, in1=tb, op=ALU.subtract), reads=bufs(tA, tB), writes=bufs(PR))
                        S.op(eng, lambda e, dst_i=dst_i, tc_=tc_, td=td: e.tensor_tensor(out=dst_i, in0=tc_, in1=td, op=ALU.add), reads=bufs(tC, tD), writes=bufs(PI))
                        m_ *= 2
                    for r_ in range(2):
                        S.op(eng, lambda e, AAp=AAp, PR=PR, r_=r_: e.tensor_copy(out=AAp[:, :, r_, :], in_=PR[:]), reads=bufs(PR), writes=bufs(AAp))
                    S.op(eng, lambda e, BBp=BBp, PI=PI: e.tensor_scalar(out=BBp[:, :, 0, :], in0=PI[:], scalar1=-1.0, scalar2=None, op0=ALU.mult),
                         reads=bufs(PI), writes=bufs(BBp))
                    S.op(eng, lambda e, BBp=BBp, PI=PI: e.tensor_copy(out=BBp[:, :, 1, :], in_=PI[:]), reads=bufs(PI), writes=bufs(BBp))
                for eng, qs in (("dve", slice(0, 8)), ("pool", slice(8, 16))):
                    B_ = bl[eng]
                    AAp, BBp, cc = B_["AAp"], B_["BBp"], B_["cc"]
                    t3, p13, p23 = B_["l"]
                    AAh = AAb[:, qs, :].unsqueeze(3).to_broadcast([128, 8, 2, NB_])
                    BBh = BBb[:, qs, :].unsqueeze(3).to_broadcast([128, 8, 2, NB_])
                    xv = arr[:, qs, :, 0, 1:257].rearrange("p q r (b i) -> p q r b i", i=BL_)
                    for i_ in range(BL_):
                        if i_ == 0:
                            src_t = xv[:, :, :, :, 0]
                        else:
                            S.op(eng, lambda e, t3=t3, xv=xv, i_=i_: e.tensor_tensor(
                                out=t3[:], in0=xv[:, :, :, :, i_ - 1], in1=xv[:, :, :, :, i_], op=ALU.add), reads=bufs(arr), writes=bufs(t3))
                            src_t = t3[:]
                        rd = bufs(arr) if i_ == 0 else bufs(t3)
                        src_sw = src_t[:, :, ::-1, :]
                        S.op(eng, lambda e, p13=p13, src_t=src_t, AAh=AAh: e.tensor_tensor(out=p13[:], in0=src_t, in1=AAh, op=ALU.mult),
                             reads=rd + bufs(c["AA"]), writes=bufs(p13))
                        S.op(eng, lambda e, p23=p23, src_sw=src_sw, BBh=BBh: e.tensor_tensor(out=p23[:], in0=src_sw, in1=BBh, op=ALU.mult),
                             reads=rd + bufs(c["BB"]), writes=bufs(p23))
                        S.op(eng, lambda e, p13=p13, p23=p23, xv=xv, i_=i_: e.tensor_tensor(
                            out=xv[:, :, :, :, i_], in0=p13[:], in1=p23[:], op=ALU.add), reads=bufs(p13, p23), writes=bufs(arr))
                for eng, qs in (("dve", slice(0, 8)), ("pool", slice(8, 16))):
                    B_ = bl[eng]
                    AAp, BBp, cc = B_["AAp"], B_["BBp"], B_["cc"]
                    c1, c2 = B_["c"]
                    xv = arr[:, qs, :, 0, 1:257].rearrange("p q r (b i) -> p q r b i", i=BL_)
                    S.op(eng, lambda e, cc=cc, qs=qs: e.tensor_copy(out=cc[:, :, :, 0], in_=arr[:, qs, :, 0, 0]), reads=bufs(arr), writes=bufs(cc))
                    for Bk in range(NB_):
                        S.op(eng, lambda e, c1=c1, cc=cc, AAp=AAp, Bk=Bk: e.tensor_tensor(
                            out=c1[:], in0=cc[:, :, :, Bk], in1=AAp[:, :, :, 15], op=ALU.mult), reads=bufs(cc, AAp), writes=bufs(c1))
                        S.op(eng, lambda e, c2=c2, cc=cc, BBp=BBp, Bk=Bk: e.tensor_tensor(
                            out=c2[:], in0=cc[:, :, ::-1, Bk], in1=BBp[:, :, :, 15], op=ALU.mult), reads=bufs(cc, BBp), writes=bufs(c2))
                        S.op(eng, lambda e, c1=c1, c2=c2: e.tensor_tensor(out=c1[:], in0=c1[:], in1=c2[:], op=ALU.add),
                             reads=bufs(c1, c2), writes=bufs(c1))
                        S.op(eng, lambda e, c1=c1, cc=cc, xv=xv, Bk=Bk: e.tensor_tensor(
                            out=cc[:, :, :, Bk + 1], in0=c1[:], in1=xv[:, :, :, Bk, 15], op=ALU.add), reads=bufs(c1, arr), writes=bufs(cc))
                for eng, qs in (("dve", slice(0, 8)), ("pool", slice(8, 16))):
                    B_ = bl[eng]
                    AAp, BBp, cc = B_["AAp"], B_["BBp"], B_["cc"]
                    xv = arr[:, qs, :, 0, 1:257].rearrange("p q r (b i) -> p q r b i", i=BL_)
                    hv = Hb[:, qs, :, 0, 1:257].rearrange("p q r (b i) -> p q r b i", i=BL_)
                    fr = B_["f"]
                    S.op(eng, lambda e, qs=qs: e.tensor_copy(out=Hb[:, qs, :, 0, 0], in_=arr[:, qs, :, 0, 0]), reads=bufs(arr), writes=bufs(Hb))
                    pend = []
                    for i_ in range(BL_ + 1):
                        if i_ < BL_:
                            f1, f2 = fr[i_ % 2]
                            S.op(eng, lambda e, f1=f1, cc=cc, AAp=AAp, i_=i_: e.tensor_tensor(
                                out=f1[:], in0=cc[:, :, :, 0:NB_], in1=AAp[:, :, :, i_:i_ + 1].to_broadcast([128, 8, 2, NB_]), op=ALU.mult),
                                reads=bufs(cc, AAp), writes=bufs(f1))
                            S.op(eng, lambda e, f2=f2, cc=cc, BBp=BBp, i_=i_: e.tensor_tensor(
                                out=f2[:], in0=cc[:, :, ::-1, 0:NB_], in1=BBp[:, :, :, i_:i_ + 1].to_broadcast([128, 8, 2, NB_]), op=ALU.mult),
                                reads=bufs(cc, BBp), writes=bufs(f2))
                        if i_ >= 1:
                            j_ = i_ - 1
                            f1, f2 = fr[j_ % 2]
                            S.op(eng, lambda e, f1=f1, f2=f2: e.tensor_tensor(out=f1[:], in0=f1[:], in1=f2[:], op=ALU.add),
                                 reads=bufs(f1, f2), writes=bufs(f1))
                            S.op(eng, lambda e, f1=f1, xv=xv, hv=hv, j_=j_: e.tensor_tensor(
                                out=hv[:, :, :, :, j_], in0=f1[:], in1=xv[:, :, :, :, j_], op=ALU.add), reads=bufs(f1, arr), writes=bufs(Hb))
            if not is_lat:
                for sq in range(4):
                    for d_ in range(2):
                        for r_ in range(2):
                            S.dma("sp", new_s5[sq, d_, r_].rearrange("(gp two) n -> (two n) gp", two=2)[:, gs],
                                  arr[:, d_:16:2, r_, sq, nch], reads=bufs(arr))
            for g0 in range(0, 16, GPB):
                pb = banks.next()
                for gg in range(GPB):
                    gi = g0 + gg
                    gl, par = gi // 2, gi % 2
                    rows = slice(par * 64, (par + 1) * 64)
                    yreg = pb[:, gg * C_:(gg + 1) * C_]
                    S.op("pe", lambda e, yreg=yreg, gi=gi: e.matmul(yreg, lhsT=Tw[:, gi, :], rhs=X[:, gi, :], start=True, stop=False),
                         reads=bufs(Tw, X), writes=bufs(pb))
                    for d_ in range(2):
                        for r_ in range(2):
                            hsl = Hb[rows, gl * 2 + d_, r_, :, 0:nch]
                            if d_ == 1:
                                hsl = hsl[:, :, ::-1]
                            S.op("pe", lambda e, yreg=yreg, gl=gl, d_=d_, r_=r_, rows=rows, hsl=hsl: e.matmul(
                                yreg, lhsT=W2w[rows, gl, d_, r_, :], rhs=hsl, start=False, stop=(d_ == 1 and r_ == 1)),
                                reads=bufs(W2w, Hb), writes=bufs(pb))
                S.op("act", lambda e, pb=pb, g0=g0: e.activation(
                    out=Ysb[:, g0:g0 + GPB, :].rearrange("p g c -> p (g c)"), in_=pb[:], func=AF.Copy), reads=bufs(pb), writes=bufs(Ysb))
            for blk in range(2):
                for t0 in range(0, 8, GPB):
                    psel = banks.next()
                    puu = banks.next()
                    for tt_ in range(GPB):
                        t = t0 + tt_
                        for g_ in range(8):
                            S.op("pe", lambda e, psel=psel, tt_=tt_, t=t, g_=g_, blk=blk: e.matmul(
                                psel[:, tt_ * C_:(tt_ + 1) * C_], lhsT=c["Wsel"][:, t, 112 - 16 * g_:240 - 16 * g_],
                                rhs=Ysb[:, blk * 8 + g_, :], start=(g_ == 0), stop=(g_ == 7)),
                                reads=bufs(c["Wsel"], Ysb), writes=bufs(psel))
                        for kk in range(8):
                            S.op("pe", lambda e, puu=puu, tt_=tt_, t=t, kk=kk, blk=blk, wu=wu: e.matmul(
                                puu[:, tt_ * C_:(tt_ + 1) * C_], lhsT=wu[:, kk, blk * 128:(blk + 1) * 128],
                                rhs=hT[:, kk, t * C_:(t + 1) * C_], start=(kk == 0), stop=(kk == 7)),
                                reads=bufs(wu, hT), writes=bufs(puu))
                    uus = uur.next()
                    ysm = ysr.next()
                    S.op("act", lambda e, puu=puu, uus=uus: e.activation(out=uus[:], in_=puu[:], func=AF.Copy),
                         reads=bufs(puu), writes=bufs(uus))
                    S.op("dve", lambda e, psel=psel, uus=uus, ysm=ysm, b=b, blk=blk: e.scalar_tensor_tensor(
                        out=ysm[:], in0=uus[:], scalar=c["dT"][:, 2 * b + blk:2 * b + blk + 1], in1=psel[:], op0=ALU.mult, op1=ALU.add),
                        reads=bufs(psel, uus, c["dT"]), writes=bufs(ysm))
                    S.op("act", lambda e, ysm=ysm, blk=blk, t0=t0: e.activation(
                        out=ygst[:, blk, t0 * C_:t0 * C_ + 512], in_=ysm[:], func=AF.Gelu), reads=bufs(ysm), writes=bufs(ygst))
            S.dma("sp", yscr[2 * b:2 * b + 2, :, tok0:tok0 + Tn].rearrange("b p t -> p b t"), ygst[:], reads=bufs(ygst))

    def s5_glu(c, tok0, sub, yT):
        ygT = k.at([128, 16, 1024], BF16)
        S.dma("sp", ygT[:], yscr[:, :, tok0 + sub * 1024:tok0 + (sub + 1) * 1024].rearrange("b p t -> p b t"), writes=bufs(ygT))
        wgr = k.aring(2, [128, 16, 128], BF16)
        sgr = k.aring(2, [128, 512], F32)
        szr = k.aring(2, [128, 512], F32)
        for blk in range(16):
            wg = wgr.next()
            S.dma("pool", wg[:], s5_w_glu.rearrange("(k p) n -> p k n", p=128)[:, :, blk * 128:(blk + 1) * 128], writes=bufs(wg))
            if blk % 4 == 0:
                wz = load_w(s5_w_in, E + blk * 128, 512)
            co = (blk % 4) * 128
            for q in range(2):
                p0 = sub * 1024 + q * 512
                pg_ = banks.next()
                for kk in range(16):
                    S.op("pe", lambda e, pg_=pg_, kk=kk, wg=wg, q=q: e.matmul(
                        pg_[:], lhsT=wg[:, kk, :], rhs=ygT[:, kk, q * 512:(q + 1) * 512], start=(kk == 0), stop=(kk == 15)),
                        reads=bufs(wg, ygT), writes=bufs(pg_))
                sg = sgr.next()
                S.op("act", lambda e, pg_=pg_, sg=sg, blk=blk: e.activation(
                    out=sg[:], in_=pg_[:], func=AF.Sigmoid, bias=c["bgT"][:, blk:blk + 1]), reads=bufs(pg_, c["bgT"]), writes=bufs(sg))
                pz = banks.next()
                for kk in range(8):
                    S.op("pe", lambda e, pz=pz, kk=kk, wz=wz, co=co, p0=p0: e.matmul(
                        pz[:], lhsT=wz[:, kk, co:co + 128], rhs=hT[:, kk, p0:p0 + 512], start=(kk == 0), stop=(kk == 7)),
                        reads=bufs(wz, hT), writes=bufs(pz))
                sz = szr.next()
                S.op("act", lambda e, pz=pz, sz=sz: e.activation(out=sz[:], in_=pz[:], func=AF.Silu), reads=bufs(pz), writes=bufs(sz))
                S.op("pool", lambda e, sg=sg, blk=blk, q=q: e.tensor_tensor(
                    out=sg[:], in0=sg[:], in1=ygT[:, blk, q * 512:(q + 1) * 512], op=ALU.mult), reads=bufs(sg, ygT), writes=bufs(sg))
                S.op("dve", lambda e, sg=sg, sz=sz, blk=blk, p0=p0: e.tensor_tensor(
                    out=yT[:, blk, p0:p0 + 512], in0=sg[:], in1=sz[:], op=ALU.mult), reads=bufs(sg, sz), writes=bufs(yT))

    def std_tiles(tok0, n):
        return [(rows_std(tok0 + i * 128), i * 128) for i in range(n)]

    units = [(0, 8, 0), (1024, 8, 1), (2048, 8, 1)]
    src = cfg.get("src", None) and inp("xsrc", [NTOK, D]) or xin
    for li in layers:
        last = final and (li == layers[-1])
        dst = xres
        k.areset()
        phase_a(li)
        if li == 1:
            L["yT"] = k.at([128, 16, 1024], BF16)
            c = gmlp_consts()
            for (tok0, nt, cond) in units:
                tiles = std_tiles(tok0, nt)
                phase_b(src, tiles, cond)
                gmlp_unit(c, nt)
                load_wout(li)
                phase_d(src, dst, tiles, cond, last)
        if li == 0:
            c = ssd_consts()
            m0 = k.amark()
            for (tok0, nt, nseq, cond) in ((0, 8, 4, 0), (1024, 16, 1, 1)):
                tiles = std_tiles(tok0, nt)
                phase_b(src, tiles, cond)
                rstd = ssd_unit(c, tok0, nt, nseq, cond == 1)
                S.op("act", lambda e, rstd=rstd, nt=nt: e.activation(out=rstd_keep[:, 0:nt], in_=rstd[:], func=AF.Copy),
                     reads=bufs(rstd), writes=bufs(rstd_keep))
                k.arestore(m0)
                load_wout(li)
                phase_d(src, dst, tiles, cond, last, scale_t=lambda i: (rstd_keep[:, i:i + 1], rstd_keep.b), ytok0=tok0)
                S.barrier()
        if li == 2:
            c = s5_prep()
            m0 = k.amark()
            for (tok0, is_lat, cond) in ((0, False, 0), (1024, True, 1)):
                nsub = 2 if is_lat else 1
                tiles = []
                for sub in range(nsub):
                    tiles += s5_tiles(tok0, sub, is_lat)
                phase_b(src, tiles, cond)
                s5_unit(c, tok0, is_lat)
                k.arestore(m0)
                L["yT"] = k.at([128, 16, 1024 * nsub], BF16)
                m1 = k.amark()
                for sub in range(nsub):
                    s5_glu(c, tok0, sub, L["yT"])
                    k.arestore(m1)
                load_wout(li)
                phase_d(src, dst, tiles, cond, last)
                k.arestore(m0)
        if li == 3:
            L["yT"] = k.at([128, 16, 2048], BF16)
            tiles = std_tiles(0, 8)
            phase_b(src, tiles, 0)
            m_ = k.amark()
            if not cfg.get("skip_ctx"):
                nat_ctx_unit()
            k.arestore(m_)
            load_wout(li)
            phase_d(src, dst, tiles, 0, last)
            S.barrier()
            tiles = std_tiles(1024, 16)
            phase_b(src, tiles, 1)
            m_ = k.amark()
            nat_lat_unit()
            if not cfg.get("skip_d"):
                k.arestore(m_)
            load_wout(li)
            phase_d(src, dst, tiles, 1, last)
        src = xres
    if not final and not cfg.get("skip_d"):
        S.barrier()
        xring = k.aring(3, [128, D], F32)
        for i in range(NTOK // 128):
            xt = xring.next()
            S.dma("sp", xt[:], xres[i * 128:(i + 1) * 128, :], writes=bufs(xt))
            S.dma("sp", y_out[i * 128:(i + 1) * 128, :], xt[:], reads=bufs(xt))
    S.emit(es)
    return nc, es


def host_inputs(inputs, core):
    f = np.ascontiguousarray
    m = {}
    m["xin"] = f(np.concatenate([inputs["x_prompt"][4 * core:4 * core + 4].reshape(NP_TOK, D),
                                 inputs["x_sample"][core % 2]], axis=0))
    m["cvec"] = f(np.stack([inputs["c_ctx"], inputs["c"][core % 2]], axis=0))
    for nm in ["norm_g", "w_mod", "b_mod", "w_out", "final_g"]:
        m[nm] = f(inputs[nm])
    m["mlp_w_in"] = f(inputs["mlp_w_in"][0])
    m["mlp_ln_g"] = f(inputs["mlp_ln_g"][0])
    m["mlp_ln_b"] = f(inputs["mlp_ln_b"][0])
    m["mlp_w_sT"] = f(np.transpose(inputs["mlp_w_s"][0], (0, 2, 1)))
    m["mlp_b_s"] = f(inputs["mlp_b_s"][0])
    m["ssd_w_in"] = f(inputs["ssd_w_in"][0])
    m["ssd_conv_w"] = f(inputs["ssd_conv_w"][0])
    m["ssd_conv_b"] = f(inputs["ssd_conv_b"][0])
    m["ssd_dt_bias"] = f(inputs["ssd_dt_bias"][0].reshape(64))
    m["ssd_a_log"] = f(inputs["ssd_a_log"][0].reshape(64))
    m["ssd_d"] = f(inputs["ssd_d"][0])
    m["ssd_norm_g"] = f(inputs["ssd_norm_g"][0])
    m["state_ssd"] = f(inputs["state_ssd"][core % 2, 0])
    m["s5_w_in"] = f(inputs["s5_w_in"][0])

    def pl(a):
        sh = a.shape[:-2]
        a = a.reshape(sh + (64, 2, 64))
        return np.moveaxis(a, -3, -1).reshape(sh + (128, 64))
    m["s5_lam"] = f(np.stack([pl(inputs["s5_lam_re"][0]), pl(inputs["s5_lam_im"][0])], 0))
    m["s5_lstep"] = f(pl(np.broadcast_to(inputs["s5_log_step"][0][:, :, None], (2, 128, 64))))

    def plj(a):
        a = a.reshape(2, 64, 2, 64, 16)
        return np.transpose(a, (0, 2, 3, 1, 4)).reshape(2, 128, 64, 16)
    m["s5_B"] = f(np.stack([plj(inputs["s5_b_re"][0]), plj(inputs["s5_b_im"][0])], 0))
    m["s5_C"] = f(np.stack([plj(np.transpose(inputs["s5_c_re"][0], (0, 1, 3, 2))),
                            plj(np.transpose(inputs["s5_c_im"][0], (0, 1, 3, 2)))], 0))
    m["s5_h0"] = f(pl(inputs["state_s5"][core % 2, 0]))
    m["s5_d"] = f(inputs["s5_d"][0])
    m["s5_w_glu"] = f(inputs["s5_w_glu"][0])
    m["s5_b_glu"] = f(inputs["s5_b_glu"][0])
    m["nat_w_in"] = f(inputs["nat_w_in"][0])
    m["rpbg"] = rpb_gather(inputs["nat_rpb"][0])
    m["natmask"] = nat_masks()
    m["cache_k"] = f(inputs["cache_k"][core % 2, 0])
    m["cache_v"] = f(inputs["cache_v"][core % 2, 0])
    return m


def rpb_gather(rpb):
    qc = np.arange(64)[:, None]
    kc = np.arange(64)[None, :]
    ci = np.clip(kc - qc + 15, 0, 30)
    out = np.zeros((32, 128, 16, 64), np.float32)
    g = rpb[:, :, ci]
    g = np.transpose(g, (0, 2, 1, 3))
    out[:, 0:64, 0:15, :] = g
    out[:, 64:128, 1:16, :] = g
    return np.ascontiguousarray(out.reshape(32, 128, 1024))


def nat_masks():
    NEG = -30000.0 * 8.0
    qc = np.arange(64)
    cs = np.clip(qc - 8, 0, 48)
    kc = np.arange(64)
    col_ok = (kc[None, :] >= cs[:, None]) & (kc[None, :] < cs[:, None] + 16)
    m = np.zeros((3, 128, 9, 64), np.float32)
    colm = np.where(col_ok, 0.0, NEG).astype(np.float32)
    m[:, 0:64] += colm[None, :, None, :]
    m[:, 64:128] += colm[None, :, None, :]
    m[0, 0:64, 8, :] = NEG
    m[0, 64:128, 0, :] = NEG
    m[1, :, 8, :] = NEG
    return np.ascontiguousarray(m.reshape(3, 128, 576))


def kernel(**inputs):
    inputs = {k_: np.asarray(v) for k_, v in inputs.items()}
    nc, es = build({})
    with es:
        in_maps = [host_inputs(inputs, c) for c in range(8)]
        res = run_bass_kernel_spmd(nc, in_maps, core_ids=list(range(8)))
    r = res.results
    y_prompt = np.concatenate([r[c]["y_out"][:NP_TOK].reshape(4, 256, D) for c in range(8)], axis=0)
    y_sample = np.stack([r[c]["y_out"][NP_TOK:] for c in range(2)], axis=0)
    new_ssd = np.concatenate([r[c]["new_ssd"] for c in range(8)], axis=0)[:, None]
    new_s5 = np.concatenate([r[c]["new_s5"] for c in range(8)], axis=0)[:, None]
    new_k = np.concatenate([r[c]["new_k"] for c in range(8)], axis=0)[:, None]
    new_v = np.concatenate([r[c]["new_v"] for c in range(8)], axis=0)[:, None]
    return (y_prompt.astype(np.float32), y_sample.astype(np.float32), np.ascontiguousarray(new_ssd, dtype=np.float32),
            np.ascontiguousarray(new_s5, dtype=np.float32), np.ascontiguousarray(new_k, dtype=np.float32),
            np.ascontiguousarray(new_v, dtype=np.float32))
```

```python
import numpy as np
from contextlib import ExitStack
import concourse.bass as bass
import concourse.mybir as mybir
from concourse.bass_utils import run_bass_kernel_spmd

F32 = mybir.dt.float32
BF16 = mybir.dt.bfloat16
AF = mybir.ActivationFunctionType
ALU = mybir.AluOpType
AX = mybir.AxisListType

D = 1024
E = 2048
NP_TOK = 1024
NS_TOK = 2048
NTOK = NP_TOK + NS_TOK
EPS = 1e-6
COMPUTE = ("pe", "act", "dve", "pool")
NDMASEM = 12
SAME_ENGINE_SYNC = True


class Buf:
    __slots__ = ("lw", "rd")

    def __init__(self):
        self.lw = None
        self.rd = {}


class Sched:
    def __init__(self, nc):
        self.nc = nc
        self.ops = {e: [] for e in COMPUTE + ("sp",)}
        self.cnt = {e: 0 for e in COMPUTE}
        self.seen = {e: {} for e in COMPUTE + ("sp",)}
        self.dma_slot = {}
        self.dma_val = {}
        self.sems = {}
        self.refd = {e: set() for e in COMPUTE}

    def _deps(self, eng, reads, writes):
        deps = {}

        def add(tok):
            if tok is None:
                return
            k, v = tok
            if deps.get(k, 0) < v:
                deps[k] = v

        for r in reads:
            add(r.lw)
        for w in writes:
            add(w.lw)
            for k, v in w.rd.items():
                add((k, v))
        out = []
        seen = self.seen[eng]
        for k, v in deps.items():
            if k == eng and (eng == "pe" or not SAME_ENGINE_SYNC):
                continue
            if seen.get(k, 0) >= v:
                continue
            seen[k] = v
            out.append((k, v))
            if isinstance(k, str):
                self.refd[k].add(v)
        return out

    def _mark(self, tok, reads, writes):
        k, v = tok
        for r in reads:
            if r.rd.get(k, 0) < v:
                r.rd[k] = v
        for w in writes:
            w.lw = tok
            w.rd = {}

    def op(self, eng, fn, reads=(), writes=()):
        waits = self._deps(eng, reads, writes)
        self.cnt[eng] += 1
        tok = (eng, self.cnt[eng])
        self.ops[eng].append((waits, fn, tok, 1))
        self._mark(tok, reads, writes)

    def dma(self, q, out, in_, reads=(), writes=()):
        slot = self.dma_slot.get(q, 0)
        self.dma_slot[q] = (slot + 1) % NDMASEM
        key = ("dma", q, slot)
        prev = self.dma_val.get(key, 0)
        waits = self._deps(q, reads, writes)
        if prev > 0 and self.seen[q].get(key, 0) < prev:
            self.seen[q][key] = prev
            waits.append((key, prev))
        val = prev + 16
        self.dma_val[key] = val
        tok = (key, val)

        def fn(e, out=out, in_=in_):
            return e.dma_start(out=out, in_=in_, allow_slow_non_contiguous=True)

        self.ops[q].append((waits, fn, tok, 16))
        self._mark(tok, reads, writes)

    def barrier(self):
        targets = [(e, self.cnt[e]) for e in COMPUTE if self.cnt[e] > 0]
        targets += [(key, v) for key, v in self.dma_val.items()]
        for eng in COMPUTE + ("sp",):
            waits = []
            for key, v in targets:
                if key == eng:
                    continue
                if self.seen[eng].get(key, 0) < v:
                    self.seen[eng][key] = v
                    waits.append((key, v))
                    if isinstance(key, str):
                        self.refd[key].add(v)
            if waits:
                self.ops[eng].append((waits, None, None, 0))

    def emit(self, es, final_wait_engine="sp"):
        nc = self.nc
        keys = list(COMPUTE)
        for q in self.dma_slot:
            for s in range(NDMASEM):
                if ("dma", q, s) in self.dma_val:
                    keys.append(("dma", q, s))
        for k in keys:
            nm = k if isinstance(k, str) else "d_%s_%d" % (k[1], k[2])
            self.sems[k] = es.enter_context(nc.semaphore("s_" + nm))
        fin = []
        for k in keys:
            v = self.cnt[k] if isinstance(k, str) else self.dma_val[k]
            if v > 0 and k != final_wait_engine:
                fin.append((k, v))
                if isinstance(k, str):
                    self.refd[k].add(v)
        rank = {}
        for e_ in COMPUTE:
            r_ = {}
            for n_, idx in enumerate(sorted(self.refd[e_])):
                r_[idx] = n_ + 1
            rank[e_] = r_

        def semval(k, v):
            return rank[k][v] if isinstance(k, str) else v
        block = es.enter_context(nc.Block())

        def run(e, name):
            for waits, fn, tok, inc in self.ops[name]:
                if fn is None:
                    for k, v in waits:
                        e.wait_ge(self.sems[k], semval(k, v))
                    continue
                NW = 1
                for k, v in waits[NW:]:
                    e.wait_ge(self.sems[k], semval(k, v))
                ins = fn(e)
                for k, v in waits[:NW]:
                    ins._wait_ge(self.sems[k], semval(k, v))
                if not isinstance(tok[0], str) or tok[1] in self.refd[tok[0]]:
                    ins.then_inc(self.sems[tok[0]], inc)
            if name == final_wait_engine:
                for k, v in fin:
                    e.wait_ge(self.sems[k], semval(k, v))

        @block.tensor
        def _(e):
            run(e, "pe")

        @block.scalar
        def _(e):
            run(e, "act")

        @block.vector
        def _(e):
            run(e, "dve")

        @block.gpsimd
        def _(e):
            run(e, "pool")

        @block.sync
        def _(e):
            run(e, "sp")


class T:
    __slots__ = ("t", "b")

    def __init__(self, t):
        self.t = t
        self.b = Buf()

    def __getitem__(self, k):
        return self.t[k]


class Ring:
    def __init__(self, tiles):
        self.tiles = tiles
        self.i = 0

    def next(self):
        t = self.tiles[self.i]
        self.i = (self.i + 1) % len(self.tiles)
        return t


class K:
    def __init__(self, nc, es):
        self.nc = nc
        self.es = es
        self.S = Sched(nc)
        self.n = 0

    def sb(self, shape, dt, name=None):
        self.n += 1
        return T(self.es.enter_context(self.nc.sbuf_tensor(name or "sb%d" % self.n, list(shape), dt)))

    def ring(self, n, shape, dt):
        return Ring([self.sb(shape, dt) for _ in range(n)])

    def psb(self, shape, dt):
        self.n += 1
        return T(self.es.enter_context(self.nc.psum_tensor("ps%d" % self.n, list(shape), dt)))

    def init_arena(self, nbytes):
        self.arena = self.es.enter_context(self.nc.sbuf_tensor("arena", [128, nbytes // 2], BF16))
        self.asize = nbytes
        self.aoff = 0
        self.alog = []

    def areset(self):
        self.S.barrier()
        self.aoff = 0

    def at(self, shape, dt):
        esz = 4 if dt == F32 else 2
        n = 1
        for d_ in shape[1:]:
            n *= d_
        nb = (n * esz + 63) // 64 * 64
        assert self.aoff + nb <= self.asize, ("arena overflow", self.aoff, nb, self.asize)
        ap = self.arena[0:shape[0], self.aoff // 2:(self.aoff + n * esz) // 2]
        if dt == F32:
            ap = ap.bitcast(F32)
        if len(shape) > 2:
            names = ["d%d" % i for i in range(len(shape) - 1)]
            kw = {names[i]: shape[i + 1] for i in range(len(names) - 1)}
            ap = ap.rearrange("p (%s) -> p %s" % (" ".join(names), " ".join(names)), **kw)
        self.alog.append((self.aoff, tuple(shape), dt))
        self.aoff += nb
        return T(ap)

    def amark(self):
        return self.aoff

    def arestore(self, m):
        self.S.barrier()
        self.aoff = m

    def aring(self, n, shape, dt):
        return Ring([self.at(shape, dt) for _ in range(n)])

    def dram(self, name, shape, dt, kind="Internal"):
        return self.nc.dram_tensor(name, list(shape), dt, kind=kind).ap()


def bufs(*ts):
    return [t.b for t in ts]


def build(cfg):
    layers = cfg.get("layers", [0, 1, 2, 3])
    final = cfg.get("final", True)
    nc = bass.Bass("TRN2", target_bir_lowering=False)
    es = ExitStack()
    k = K(nc, es)
    S = k.S
    I = {}

    def inp(name, shape):
        I[name] = k.dram(name, shape, F32, kind="ExternalInput")
        return I[name]

    xin = inp("xin", [NTOK, D])
    cvec = inp("cvec", [2, D])
    norm_g = inp("norm_g", [4, D])
    w_mod = inp("w_mod", [4, D, 3 * D])
    b_mod = inp("b_mod", [4, 3 * D])
    w_out = inp("w_out", [4, E, D])
    final_g = inp("final_g", [D])
    mlp_w_in = inp("mlp_w_in", [D, 3 * E])
    mlp_ln_g = inp("mlp_ln_g", [E])
    mlp_ln_b = inp("mlp_ln_b", [E])
    mlp_w_sT = inp("mlp_w_sT", [8, 128, 128])
    mlp_b_s = inp("mlp_b_s", [8, 128])
    ssd_w_in = inp("ssd_w_in", [D, 6208])
    ssd_conv_w = inp("ssd_conv_w", [5, 4096])
    ssd_conv_b = inp("ssd_conv_b", [4096])
    ssd_dt_bias = inp("ssd_dt_bias", [64])
    ssd_a_log = inp("ssd_a_log", [64])
    ssd_d = inp("ssd_d", [32])
    ssd_norm_g = inp("ssd_norm_g", [E])
    state_ssd = inp("state_ssd", [2, 32, 64, 128])
    new_ssd = k.dram("new_ssd", [4, 2, 32, 64, 128], F32, kind="ExternalOutput")
    yscr = k.dram("yscr", [16, 128, NTOK], BF16)
    s5_w_in = inp("s5_w_in", [D, 2 * E])
    s5_lam = inp("s5_lam", [2, 2, 128, 64])
    s5_lstep = inp("s5_lstep", [2, 128, 64])
    s5_B = inp("s5_B", [2, 2, 128, 64, 16])
    s5_C = inp("s5_C", [2, 2, 128, 64, 16])
    s5_h0 = inp("s5_h0", [2, 2, 128, 64])
    s5_d = inp("s5_d", [E])
    s5_w_glu = inp("s5_w_glu", [E, E])
    s5_b_glu = inp("s5_b_glu", [E])
    new_s5 = k.dram("new_s5", [4, 2, 2, 128, 64], F32, kind="ExternalOutput")
    Tscr = k.dram("Tscr", [8, 128, 16 * 128], BF16)
    VTscr = k.dram("VTscr", [8, 128, 8 * 4 * 128], BF16)
    W2scr = k.dram("W2scr", [8, 128, 8 * 4 * 128], BF16)
    nat_w_in = inp("nat_w_in", [D, 4 * E])
    rpbg = inp("rpbg", [32, 128, 1024])
    natmask = inp("natmask", [3, 128, 576])
    cache_k = inp("cache_k", [32, 256, 64])
    cache_v = inp("cache_v", [32, 256, 64])
    new_k = k.dram("new_k", [4, 32, 256, 64], F32, kind="ExternalOutput")
    new_v = k.dram("new_v", [4, 32, 256, 64], F32, kind="ExternalOutput")
    y_out = k.dram("y_out", [NTOK, D], F32, kind="ExternalOutput")
    xres = k.dram("xres", [NTOK, D], F32)
    dma_done = Buf()

    identf = k.sb([128, 128], F32)
    identb = k.sb([128, 128], BF16)
    onesf = k.sb([128, 128], F32)
    S.op("pool", lambda e: e.memset(identf[:], 0.0), writes=bufs(identf))
    S.op("pool", lambda e: e.affine_select(out=identf[:], in_=identf[:], compare_op=ALU.not_equal, fill=1.0,
                                           base=0, pattern=[[-1, 128]], channel_multiplier=1),
         reads=bufs(identf), writes=bufs(identf))
    S.op("dve", lambda e: e.tensor_copy(out=identb[:], in_=identf[:]), reads=bufs(identf), writes=bufs(identb))
    S.op("pool", lambda e: e.memset(onesf[:], 1.0), writes=bufs(onesf))

    banks = Ring([k.psb([128, 512], F32) for _ in range(8)])

    hT = k.sb([128, 8, 2048], BF16, "hT")
    wo = T(hT.t)
    wo.b = hT.b
    wo_view = hT.t[:].rearrange("p k t -> p (k t)").rearrange("p (k n) -> p k n", k=16)
    wring = k.ring(3, [128, 8, 512], BF16)
    junk = k.sb([128, D], BF16)
    small = k.ring(8, [128, 8], F32)
    rstd_keep = k.sb([128, 16], F32)
    k.init_arena(136 * 1024)
    L = {}

    cf = k.sb([128, 8, 2], F32)
    cb = k.sb([128, 8, 2], BF16)
    for c_ in range(2):
        S.dma("sp", cf[:, :, c_], cvec[c_].rearrange("(k p) -> p k", p=128), writes=bufs(cf))
    S.op("act", lambda e: e.activation(out=cb[:], in_=cf[:], func=AF.Silu), reads=bufs(cf), writes=bufs(cb))

    modT = k.sb([128, 16, 2], F32)
    bmodT = k.sb([128, 16], F32)
    ngT = k.sb([128, 8], F32)
    Asc = k.sb([128, 8, 2], F32)
    gate_bc = [k.sb([128, D], F32), k.sb([128, D], F32)]
    sel = [k.sb([2, 128], F32), k.sb([2, 128], F32)]
    for c in range(2):
        S.op("pool", lambda e, c=c: e.memset(sel[c][:], 0.0), writes=bufs(sel[c]))
        S.op("pool", lambda e, c=c: e.affine_select(out=sel[c][:], in_=sel[c][:], compare_op=ALU.not_equal, fill=1.0,
                                                     base=-c, pattern=[[0, 128]], channel_multiplier=1),
             reads=bufs(sel[c]), writes=bufs(sel[c]))

    def load_w(wap, c0, n, q="pool"):
        wt = wring.next()
        S.dma(q, wt[:, :, 0:n], wap.rearrange("(k p) n -> p k n", p=128)[:, :, c0:c0 + n], writes=bufs(wt))
        return wt

    def phase_a(li):
        ma_ = k.amark()
        gate2 = k.at([2, D], F32)
        bgate2 = k.at([2, D], F32)
        S.dma("sp", bmodT[:], b_mod[li, 0:2 * D].rearrange("(c p) -> p c", p=128), writes=bufs(bmodT))
        S.dma("sp", ngT[:], norm_g[li].rearrange("(c p) -> p c", p=128), writes=bufs(ngT))
        S.dma("sp", bgate2[:], b_mod[li, 2 * D:3 * D].partition_broadcast(2), writes=bufs(bgate2))
        for blk in range(4):
            wt = load_w(w_mod[li], blk * 512, 512)
            for cc in range(4):
                ch = blk * 4 + cc
                pb = banks.next()
                for kk in range(8):
                    S.op("pe", lambda e, pb=pb, wt=wt, cc=cc, kk=kk: e.matmul(
                        pb[:, 0:2], lhsT=wt[:, kk, cc * 128:(cc + 1) * 128], rhs=cb[:, kk, :],
                        start=(kk == 0), stop=(kk == 7)), reads=bufs(wt, cb), writes=bufs(pb))
                S.op("dve", lambda e, pb=pb, ch=ch: e.tensor_scalar(
                    out=modT[:, ch, :], in0=pb[:, 0:2], scalar1=bmodT[:, ch:ch + 1], scalar2=None, op0=ALU.add),
                    reads=bufs(pb, bmodT), writes=bufs(modT))
        S.op("dve", lambda e: e.tensor_scalar(out=Asc[:], in0=modT[:, 8:16, :], scalar1=1.0, scalar2=None, op0=ALU.add),
             reads=bufs(modT), writes=bufs(Asc))
        S.op("dve", lambda e: e.tensor_tensor(out=Asc[:], in0=Asc[:], in1=ngT[:].unsqueeze(2).to_broadcast([128, 8, 2]),
                                              op=ALU.mult), reads=bufs(Asc, ngT), writes=bufs(Asc))
        for blk in range(2):
            wt = load_w(w_mod[li], 2 * D + blk * 512, 512)
            pb = banks.next()
            for kk in range(8):
                S.op("pe", lambda e, pb=pb, wt=wt, kk=kk: e.matmul(
                    pb[0:2, :], lhsT=cb[:, kk, :], rhs=wt[:, kk, :], start=(kk == 0), stop=(kk == 7)),
                    reads=bufs(wt, cb), writes=bufs(pb))
            S.op("dve", lambda e, pb=pb, blk=blk: e.tensor_tensor(
                out=gate2[:, blk * 512:(blk + 1) * 512], in0=pb[0:2, :], in1=bgate2[:, blk * 512:(blk + 1) * 512],
                op=ALU.add), reads=bufs(pb, bgate2), writes=bufs(gate2))
        for c in range(2):
            for blk in range(2):
                pb = banks.next()
                S.op("pe", lambda e, pb=pb, c=c, blk=blk: e.matmul(
                    pb[:], lhsT=sel[c][:], rhs=gate2[:, blk * 512:(blk + 1) * 512], start=True, stop=True),
                    reads=bufs(sel[c], gate2), writes=bufs(pb))
                S.op("act", lambda e, pb=pb, c=c, blk=blk: e.activation(
                    out=gate_bc[c][:, blk * 512:(blk + 1) * 512], in_=pb[:], func=AF.Copy),
                    reads=bufs(pb), writes=bufs(gate_bc[c]))
        k.arestore(ma_)

    def rows_std(tok0):
        return lambda src: src[tok0:tok0 + 128, :]

    def rms_stats(xt):
        st = small.next()
        S.op("act", lambda e: e.activation(out=junk[:], in_=xt[:], func=AF.Square, accum_out=st[:, 0:1]),
             reads=bufs(xt), writes=bufs(junk, st))
        S.op("dve", lambda e: e.tensor_scalar(out=st[:, 0:1], in0=st[:, 0:1], scalar1=1.0 / D, scalar2=EPS,
                                              op0=ALU.mult, op1=ALU.add), reads=bufs(st), writes=bufs(st))
        S.op("act", lambda e: e.activation(out=st[:, 0:1], in_=st[:, 0:1], func=AF.Sqrt), reads=bufs(st), writes=bufs(st))
        S.op("dve", lambda e: e.reciprocal(out=st[:, 0:1], in_=st[:, 0:1]), reads=bufs(st), writes=bufs(st))
        return st

    def phase_b(src, tiles, cond):
        m_ = k.amark()
        xring = k.aring(3, [128, D], F32)
        xnring = k.aring(2, [128, D], BF16)

        def load(i):
            xt = xring.next()
            S.dma("sp", xt[:], tiles[i][0](src), writes=bufs(xt))
            return xt
        nxt = load(0)
        for i in range(len(tiles)):
            xt = nxt
            if i + 1 < len(tiles):
                nxt = load(i + 1)
            col0 = tiles[i][1]
            st = rms_stats(xt)
            xn = xnring.next()
            S.op("dve", lambda e, xn=xn, xt=xt, st=st: e.tensor_scalar(out=xn[:], in0=xt[:], scalar1=st[:, 0:1],
                                                                   scalar2=None, op0=ALU.mult),
                 reads=bufs(xt, st), writes=bufs(xn))
            pb = banks.next()
            pv = pb[:].bitcast(BF16).rearrange("p (k t) -> p k t", k=8)
            for kk in range(8):
                S.op("pe", lambda e, pv=pv, xn=xn, kk=kk: e.transpose(out=pv[:, kk, :], in_=xn[:, kk * 128:(kk + 1) * 128],
                                                                    identity=identb[:]),
                     reads=bufs(xn, identb), writes=bufs(pb))
            for kk in range(8):
                S.op("act", lambda e, pv=pv, kk=kk, col0=col0: e.activation(
                    out=hT[:, kk, col0:col0 + 128], in_=pv[:, kk, :], func=AF.Identity,
                    scale=Asc[:, kk, cond:cond + 1], bias=modT[:, kk, cond:cond + 1]),
                    reads=bufs(pb, Asc, modT), writes=bufs(hT))
        k.arestore(m_)


    def load_wout(li):
        for h in range(2):
            S.dma("pool", wo_view[:, h * 8:(h + 1) * 8, :],
                  w_out[li].rearrange("(k p) n -> p k n", p=128)[:, h * 8:(h + 1) * 8, :], writes=bufs(wo))

    def phase_d(src, dst, tiles, cond, last, scale_t=None, ytok0=None):
        if cfg.get("skip_d"):
            return
        m_ = k.amark()
        xring = k.aring(3, [128, D], F32)
        tring = k.aring(2, [128, D], F32)
        if last:
            fg_bc = k.at([128, D], F32)
            S.dma("sp", fg_bc[:], final_g.partition_broadcast(128), writes=bufs(fg_bc))
        if ytok0 is None:
            yT = L["yT"]
        else:
            yring = k.aring(2, [128, 16, 512], BF16)
            yT = None

        def load(i):
            xt = xring.next()
            S.dma("sp", xt[:], tiles[i][0](src), writes=bufs(xt))
            return xt
        nxt = load(0)
        for i in range(len(tiles)):
            xt = nxt
            if i + 1 < len(tiles):
                nxt = load(i + 1)
            col0 = tiles[i][1]
            if ytok0 is not None:
                if i % 4 == 0:
                    yT = yring.next()
                    S.dma("sp", yT[:], yscr[:, :, ytok0 + tiles[i][1]:ytok0 + tiles[i][1] + 512].rearrange("b p t -> p b t"),
                          writes=bufs(yT))
                col0 = (i % 4) * 128
            tt = tring.next()
            for h in range(2):
                pb = banks.next()
                for kk in range(16):
                    S.op("pe", lambda e, pb=pb, kk=kk, h=h, col0=col0, yT=yT: e.matmul(
                        pb[:], lhsT=yT[:, kk, col0:col0 + 128], rhs=wo_view[:, kk, h * 512:(h + 1) * 512],
                        start=(kk == 0), stop=(kk == 15)), reads=bufs(yT, wo), writes=bufs(pb))
                if scale_t is None:
                    S.op("dve", lambda e, pb=pb, tt=tt, h=h: e.tensor_tensor(
                        out=tt[:, h * 512:(h + 1) * 512], in0=pb[:], in1=gate_bc[cond][:, h * 512:(h + 1) * 512],
                        op=ALU.mult), reads=bufs(pb, gate_bc[cond]), writes=bufs(tt))
                else:
                    sc = scale_t(i)
                    S.op("dve", lambda e, pb=pb, tt=tt, h=h, sc=sc: e.scalar_tensor_tensor(
                        out=tt[:, h * 512:(h + 1) * 512], in0=pb[:], scalar=sc[0], in1=gate_bc[cond][:, h * 512:(h + 1) * 512],
                        op0=ALU.mult, op1=ALU.mult), reads=bufs(pb, gate_bc[cond]) + [sc[1]], writes=bufs(tt))
            S.op("pool", lambda e, tt=tt, xt=xt: e.tensor_tensor(out=xt[:], in0=tt[:], in1=xt[:], op=ALU.add),
                 reads=bufs(tt, xt), writes=bufs(xt))
            if not last:
                S.dma("sp", tiles[i][0](dst), xt[:], reads=bufs(xt))
            else:
                st = rms_stats(xt)
                S.op("dve", lambda e, tt=tt, xt=xt, st=st: e.scalar_tensor_tensor(
                    out=tt[:], in0=xt[:], scalar=st[:, 0:1], in1=fg_bc[:], op0=ALU.mult, op1=ALU.mult),
                    reads=bufs(xt, st, fg_bc), writes=bufs(tt))
                S.dma("sp", tiles[i][0](y_out), tt[:], reads=bufs(tt))
        k.arestore(m_)

    def gmlp_consts():
        c = {}
        c["lngT"] = k.at([128, 16], F32)
        c["lnbT"] = k.at([128, 16], F32)
        c["wsT"] = k.at([128, 8, 128], BF16)
        c["wsTf"] = k.at([128, 8, 128], F32)
        c["bs_bc"] = k.at([128, 8, 128], F32)
        c["Bt"] = k.at([128, 16, 128], F32)
        S.dma("sp", c["lngT"][:], mlp_ln_g.rearrange("(c p) -> p c", p=128), writes=bufs(c["lngT"]))
        S.dma("sp", c["lnbT"][:], mlp_ln_b.rearrange("(c p) -> p c", p=128), writes=bufs(c["lnbT"]))
        S.dma("sp", c["wsTf"][:], mlp_w_sT.rearrange("g j i -> j g i"), writes=bufs(c["wsTf"]))
        S.dma("sp", c["bs_bc"][:].rearrange("p g i -> p (g i)"), mlp_b_s.rearrange("g i -> (g i)").partition_broadcast(128),
              writes=bufs(c["bs_bc"]))
        S.op("dve", lambda e: e.tensor_copy(out=c["wsT"][:], in_=c["wsTf"][:]), reads=bufs(c["wsTf"]), writes=bufs(c["wsT"]))
        for half in range(2):
            pb = banks.next()
            S.op("pe", lambda e, pb=pb, half=half: e.matmul(
                pb[:], lhsT=onesf[:], rhs=c["wsTf"][:, half * 4:(half + 1) * 4, :].rearrange("p g i -> p (g i)"),
                start=True, stop=True), reads=bufs(onesf, c["wsTf"]), writes=bufs(pb))
            for gg in range(4):
                g = half * 4 + gg
                for bb in range(2):
                    blk = g * 2 + bb
                    S.op("dve", lambda e, pb=pb, gg=gg, g=g, blk=blk: e.scalar_tensor_tensor(
                        out=c["Bt"][:, blk, :], in0=pb[:, gg * 128:(gg + 1) * 128], scalar=c["lnbT"][:, blk:blk + 1],
                        in1=c["bs_bc"][:, g, :], op0=ALU.mult, op1=ALU.add),
                        reads=bufs(pb, c["lnbT"], c["bs_bc"]), writes=bufs(c["Bt"]))
        c["vv"] = k.at([128, 8, E], BF16)
        c["gtmp"] = k.aring(2, [128, 512], F32)
        c["ug"] = k.aring(2, [128, 512], F32)
        c["zs"] = k.aring(2, [128, 512], F32)
        c["sg"] = k.aring(2, [128, 512], F32)
        c["st"] = k.at([128, 8, 8], F32)
        return c

    def gmlp_unit(c, ntile):
        yT = L["yT"]
        vv = c["vv"]
        stt = c["st"]
        for b in range(4):
            wv = load_w(mlp_w_in, E + b * 512, 512)
            for t in range(ntile):
                pb = banks.next()
                for kk in range(8):
                    S.op("pe", lambda e, pb=pb, t=t, wv=wv, kk=kk: e.matmul(
                        pb[:], lhsT=hT[:, kk, t * 128:(t + 1) * 128], rhs=wv[:, kk, :], start=(kk == 0), stop=(kk == 7)),
                        reads=bufs(hT, wv), writes=bufs(pb))
                gt = c["gtmp"].next()
                S.op("act", lambda e, pb=pb, b=b, t=t, gt=gt: e.activation(
                    out=gt[:], in_=pb[:], func=AF.Gelu, accum_out=stt[:, t, b:b + 1]),
                    reads=bufs(pb), writes=bufs(gt, stt))
                S.op("act", lambda e, b=b, t=t, gt=gt: e.activation(
                    out=junk[:, 0:512], in_=gt[:], func=AF.Square, accum_out=stt[:, t, 4 + b:5 + b]),
                    reads=bufs(gt), writes=bufs(junk, stt))
                S.op("pool", lambda e, b=b, t=t, gt=gt: e.tensor_copy(out=vv[:, t, b * 512:(b + 1) * 512], in_=gt[:]),
                     reads=bufs(gt), writes=bufs(vv))
        for t in range(ntile):
            st2 = small.next()
            S.op("dve", lambda e, t=t, st2=st2: e.tensor_reduce(
                out=st2[:, 0:2], in_=stt[:, t, :].rearrange("p (a b) -> p a b", a=2), axis=AX.X, op=ALU.add),
                reads=bufs(stt), writes=bufs(st2))
            S.op("dve", lambda e, st2=st2: e.tensor_scalar(out=st2[:, 0:2], in0=st2[:, 0:2], scalar1=1.0 / E, scalar2=None,
                                                           op0=ALU.mult), reads=bufs(st2), writes=bufs(st2))
            S.op("dve", lambda e, st2=st2: e.tensor_tensor(out=st2[:, 2:3], in0=st2[:, 0:1], in1=st2[:, 0:1], op=ALU.mult),
                 reads=bufs(st2), writes=bufs(st2))
            S.op("dve", lambda e, st2=st2: e.scalar_tensor_tensor(out=st2[:, 2:3], in0=st2[:, 2:3], scalar=-1.0, in1=st2[:, 1:2],
                                                                  op0=ALU.mult, op1=ALU.add), reads=bufs(st2), writes=bufs(st2))
            S.op("dve", lambda e, st2=st2: e.tensor_scalar(out=st2[:, 2:3], in0=st2[:, 2:3], scalar1=EPS, scalar2=None,
                                                           op0=ALU.add), reads=bufs(st2), writes=bufs(st2))
            S.op("act", lambda e, st2=st2: e.activation(out=st2[:, 2:3], in_=st2[:, 2:3], func=AF.Sqrt),
                 reads=bufs(st2), writes=bufs(st2))
            S.op("dve", lambda e, st2=st2: e.reciprocal(out=st2[:, 2:3], in_=st2[:, 2:3]), reads=bufs(st2), writes=bufs(st2))
            S.op("dve", lambda e, st2=st2, t=t: e.tensor_scalar(
                out=vv[:, t, :], in0=vv[:, t, :], scalar1=st2[:, 0:1], scalar2=st2[:, 2:3], op0=ALU.subtract, op1=ALU.mult),
                reads=bufs(vv, st2), writes=bufs(vv))
        nq = ntile // 4
        for blk in range(16):
            g = blk // 2
            if blk % 4 == 0:
                wu = load_w(mlp_w_in, blk * 128, 512)
                wz = load_w(mlp_w_in, 2 * E + blk * 128, 512)
            co = (blk % 4) * 128
            for q in range(nq):
                ug = c["ug"].next()
                zs = c["zs"].next()
                sg = c["sg"].next()
                pu = banks.next()
                for kk in range(8):
                    S.op("pe", lambda e, pu=pu, kk=kk, q=q, wu=wu, co=co: e.matmul(
                        pu[:], lhsT=wu[:, kk, co:co + 128], rhs=hT[:, kk, q * 512:(q + 1) * 512],
                        start=(kk == 0), stop=(kk == 7)), reads=bufs(wu, hT), writes=bufs(pu))
                S.op("act", lambda e, pu=pu, ug=ug: e.activation(out=ug[:], in_=pu[:], func=AF.Gelu),
                     reads=bufs(pu), writes=bufs(ug))
                pz = banks.next()
                for kk in range(8):
                    S.op("pe", lambda e, pz=pz, kk=kk, q=q, wz=wz, co=co: e.matmul(
                        pz[:], lhsT=wz[:, kk, co:co + 128], rhs=hT[:, kk, q * 512:(q + 1) * 512],
                        start=(kk == 0), stop=(kk == 7)), reads=bufs(wz, hT), writes=bufs(pz))
                S.op("act", lambda e, pz=pz, zs=zs: e.activation(out=zs[:], in_=pz[:], func=AF.Silu),
                     reads=bufs(pz), writes=bufs(zs))
                ps_ = banks.next()
                for cc in range(4):
                    t = q * 4 + cc
                    S.op("pe", lambda e, ps_=ps_, t=t, cc=cc, blk=blk, g=g: e.matmul(
                        ps_[:, cc * 128:(cc + 1) * 128], lhsT=vv[:, t, blk * 128:(blk + 1) * 128], rhs=c["wsT"][:, g, :],
                        start=True, stop=True), reads=bufs(vv, c["wsT"]), writes=bufs(ps_))
                S.op("dve", lambda e, ps_=ps_, blk=blk, sg=sg: e.scalar_tensor_tensor(
                    out=sg[:].rearrange("p (c i) -> p c i", c=4),
                    in0=ps_[:].rearrange("p (c i) -> p c i", c=4),
                    scalar=c["lngT"][:, blk:blk + 1],
                    in1=c["Bt"][:, blk:blk + 1, :].to_broadcast([128, 4, 128]), op0=ALU.mult, op1=ALU.add),
                    reads=bufs(ps_, c["lngT"], c["Bt"]), writes=bufs(sg))
                S.op("pool", lambda e, sg=sg, ug=ug: e.tensor_tensor(out=sg[:], in0=sg[:], in1=ug[:], op=ALU.mult),
                     reads=bufs(sg, ug), writes=bufs(sg))
                S.op("dve", lambda e, q=q, sg=sg, zs=zs, blk=blk: e.tensor_tensor(
                    out=yT[:, blk, q * 512:(q + 1) * 512], in0=sg[:], in1=zs[:], op=ALU.mult),
                    reads=bufs(sg, zs), writes=bufs(yT))

    SCALE = 0.125

    def nat_proj(hp, ntok, c, with_ktm):
        wt = wring.next()
        for j in (0, 1, 3):
            S.dma("pool", wt[:, :, j * 128:(j + 1) * 128],
                  nat_w_in.rearrange("(k p) n -> p k n", p=128)[:, :, j * E + hp * 128:j * E + (hp + 1) * 128],
                  writes=bufs(wt))
        qT, kT, gT = c["qT"], c["kT"], c["gT"]
        for q in range(ntok // 512):
            for j, dst, fn in ((0, qT, AF.Copy), (1, kT, AF.Copy), (3, gT, AF.Silu)):
                pb = banks.next()
                for kk in range(8):
                    S.op("pe", lambda e, pb=pb, kk=kk, q=q, j=j, wt=wt: e.matmul(
                        pb[:], lhsT=wt[:, kk, j * 128:(j + 1) * 128], rhs=hT[:, kk, q * 512:(q + 1) * 512],
                        start=(kk == 0), stop=(kk == 7)), reads=bufs(wt, hT), writes=bufs(pb))
                S.op("act", lambda e, pb=pb, q=q, dst=dst, fn=fn: e.activation(
                    out=dst[:, q * 512:(q + 1) * 512], in_=pb[:], func=fn), reads=bufs(pb), writes=bufs(dst))

    def nat_proj_v4(hp4, ntok, c, with_ktm):
        vb = c["vb"]
        wv = load_w(nat_w_in, 2 * E + hp4 * 512, 512)
        wk = load_w(nat_w_in, E + hp4 * 512, 512) if with_ktm else None
        for t in range(ntok // 128):
            pv_ = banks.next()
            for kk in range(8):
                S.op("pe", lambda e, pv_=pv_, kk=kk, t=t, wv=wv: e.matmul(
                    pv_[:], lhsT=hT[:, kk, t * 128:(t + 1) * 128], rhs=wv[:, kk, :], start=(kk == 0), stop=(kk == 7)),
                    reads=bufs(wv, hT), writes=bufs(pv_))
            if not with_ktm:
                S.op("act", lambda e, pv_=pv_, t=t: e.activation(out=vb[:, t, :], in_=pv_[:], func=AF.Copy),
                     reads=bufs(pv_), writes=bufs(vb))
            else:
                vst, kst = c["vst"], c["kst"]
                S.op("act", lambda e, pv_=pv_, t=t: e.activation(out=vst[:, t, :], in_=pv_[:], func=AF.Copy),
                     reads=bufs(pv_), writes=bufs(vst))
                S.op("pool", lambda e, t=t: e.tensor_copy(out=vb[:, t, :], in_=vst[:, t, :]), reads=bufs(vst), writes=bufs(vb))
                pk_ = banks.next()
                for kk in range(8):
                    S.op("pe", lambda e, pk_=pk_, kk=kk, t=t, wk=wk: e.matmul(
                        pk_[:], lhsT=hT[:, kk, t * 128:(t + 1) * 128], rhs=wk[:, kk, :], start=(kk == 0), stop=(kk == 7)),
                        reads=bufs(wk, hT), writes=bufs(pk_))
                S.op("act", lambda e, pk_=pk_, t=t: e.activation(out=kst[:, t, :], in_=pk_[:], func=AF.Copy),
                     reads=bufs(pk_), writes=bufs(kst))

    def nat_ctx_unit():
        yT = L["yT"]
        c = {"qT": k.at([128, 1024], BF16), "kT": k.at([128, 1024], BF16), "gT": k.at([128, 1024], BF16),
             "vb": k.at([128, 8, 512], BF16), "vst": k.at([128, 8, 512], F32), "kst": k.at([128, 8, 512], F32)}
        er = k.aring(2, [128, 512], F32)
        pbr = k.aring(2, [128, 512], BF16)
        ptr_ = k.aring(2, [128, 512], BF16)
        for hp in range(cfg.get("ctx_hp", 16)):
            if hp % 4 == 0:
                nat_proj_v4(hp // 4, 1024, c, True)
            nat_proj(hp, 1024, c, True)
            for hd in range(2):
                h = hp * 2 + hd
                for sq in range(0 if cfg.get("no_kv") else 4):
                    S.dma("sp", new_k[sq, h, :, :].rearrange("(t p) d -> p t d", p=128),
                          c["kst"][:, sq * 2:(sq + 1) * 2, (hp % 4) * 128 + hd * 64:(hp % 4) * 128 + (hd + 1) * 64], reads=bufs(c["kst"]))
                    S.dma("sp", new_v[sq, h, :, :].rearrange("(t p) d -> p t d", p=128),
                          c["vst"][:, sq * 2:(sq + 1) * 2, (hp % 4) * 128 + hd * 64:(hp % 4) * 128 + (hd + 1) * 64], reads=bufs(c["vst"]))
            qT, kT, gT, vb = c["qT"], c["kT"], c["gT"], c["vb"]
            cb_ = Ring(banks.tiles[2:8])
            pob_ = Ring(banks.tiles[0:2])

            vo = (hp % 4) * 128

            def c_qk(sq, hd):
                rows = slice(hd * 64, (hd + 1) * 64)
                tok0 = sq * 256
                ps_ = cb_.next()
                for qt in range(2):
                    S.op("pe", lambda e, ps_=ps_, qt=qt, rows=rows, tok0=tok0: e.matmul(
                        ps_[:, qt * 256:(qt + 1) * 256], lhsT=qT[rows, tok0 + qt * 128:tok0 + (qt + 1) * 128],
                        rhs=kT[rows, tok0:tok0 + 256], start=True, stop=True), reads=bufs(qT, kT), writes=bufs(ps_))
                return ps_

            def c_softmax(ps_):
                mx = small.next()
                S.op("dve", lambda e, ps_=ps_, mx=mx: e.tensor_reduce(
                    out=mx[:, 0:2], in_=ps_[:].rearrange("p (a b) -> p a b", a=2), axis=AX.X, op=ALU.max),
                    reads=bufs(ps_), writes=bufs(mx))
                S.op("dve", lambda e, mx=mx: e.tensor_scalar(out=mx[:, 2:4], in0=mx[:, 0:2], scalar1=-SCALE, scalar2=None,
                                                             op0=ALU.mult), reads=bufs(mx), writes=bufs(mx))
                et = er.next()
                for qt in range(2):
                    S.op("act", lambda e, ps_=ps_, mx=mx, et=et, qt=qt: e.activation(
                        out=et[:, qt * 256:(qt + 1) * 256], in_=ps_[:, qt * 256:(qt + 1) * 256], func=AF.Exp, scale=SCALE,
                        bias=mx[:, 2 + qt:3 + qt], accum_out=mx[:, 4 + qt:5 + qt]), reads=bufs(ps_, mx), writes=bufs(et, mx))
                S.op("dve", lambda e, mx=mx: e.reciprocal(out=mx[:, 6:8], in_=mx[:, 4:6]), reads=bufs(mx), writes=bufs(mx))
                pbt = pbr.next()
                S.op("dve", lambda e, mx=mx, et=et, pbt=pbt: e.tensor_tensor(
                    out=pbt[:].rearrange("p (a b) -> p a b", a=2), in0=et[:].rearrange("p (a b) -> p a b", a=2),
                    in1=mx[:, 6:8].unsqueeze(2).to_broadcast([128, 2, 256]), op=ALU.mult),
                    reads=bufs(mx, et), writes=bufs(pbt))
                return pbt

            def c_tpv(sq, hd, pbt, po):
                rows = slice(hd * 64, (hd + 1) * 64)
                ptb = cb_.next()
                ptv = ptb[:].bitcast(BF16)
                for j in range(4):
                    S.op("pe", lambda e, ptv=ptv, pbt=pbt, j=j: e.transpose(
                        out=ptv[:, j * 128:(j + 1) * 128], in_=pbt[:, j * 128:(j + 1) * 128], identity=identb[:]),
                        reads=bufs(pbt, identb), writes=bufs(ptb))
                pts = ptr_.next()
                S.op("act", lambda e, ptv=ptv, pts=pts: e.activation(out=pts[:], in_=ptv[:, 0:512], func=AF.Copy),
                     reads=bufs(ptb), writes=bufs(pts))
                for qt in range(2):
                    for kb in range(2):
                        S.op("pe", lambda e, po=po, rows=rows, qt=qt, kb=kb, sq=sq, hd=hd, pts=pts, vo=vo: e.matmul(
                            po[rows, qt * 128:(qt + 1) * 128], lhsT=vb[:, sq * 2 + kb, vo + hd * 64:vo + (hd + 1) * 64],
                            rhs=pts[:, (qt * 2 + kb) * 128:(qt * 2 + kb + 1) * 128], start=(kb == 0), stop=(kb == 1)),
                            reads=bufs(vb, pts), writes=bufs(po))

            its = [(sq, hd) for sq in range(0 if cfg.get("ctx_stage", 9) < 1 else 4) for hd in range(2)]
            nxt = c_qk(*its[0]) if its else None
            po = None
            for ii, (sq, hd) in enumerate(its):
                if hd == 0:
                    po = pob_.next()
                pbt = c_softmax(nxt)
                if ii + 1 < len(its):
                    nxt = c_qk(*its[ii + 1])
                c_tpv(sq, hd, pbt, po)
                if hd == 1:
                    tok0 = sq * 256
                    S.op("dve", lambda e, po=po, hp=hp, tok0=tok0: e.tensor_tensor(
                        out=yT[:, hp, tok0:tok0 + 256], in0=po[:, 0:256], in1=gT[:, tok0:tok0 + 256], op=ALU.mult),
                        reads=bufs(po, gT), writes=bufs(yT))

    def nat_lat_unit():
        yT = L["yT"]
        c = {"qT": k.at([128, 2048], BF16), "kT": k.at([128, 2048], BF16), "gT": k.at([128, 2048], BF16),
             "vb": k.at([128, 16, 512], BF16)}
        maskf = k.at([128, 3, 576], F32)
        maskb = k.at([128, 3, 576], BF16)
        for j in range(3):
            S.dma("sp", maskf[:, j, :], natmask[j], writes=bufs(maskf))
        S.op("dve", lambda e: e.tensor_copy(out=maskb[:], in_=maskf[:]), reads=bufs(maskf), writes=bufs(maskb))
        ckr = k.aring(2, [128, 2, 2, 64], BF16)
        cvr = k.aring(2, [128, 2, 2, 64], BF16)
        cktr = k.aring(2, [128, 256], BF16)
        rpr = k.aring(2, [128, 1024], F32)
        scr = k.aring(2, [128, 832], F32)
        pbr = k.aring(2, [128, 832], BF16)
        ptr_ = k.aring(2, [128, 896], BF16)
        cfg["alog"] = k.alog
        pobanks = Ring(banks.tiles[0:2])
        wbanks = Ring(banks.tiles[2:8])
        for hp in range(cfg.get("nat_hp", 16)):
            if hp % 4 == 0:
                nat_proj_v4(hp // 4, 2048, c, False)
            nat_proj(hp, 2048, c, False)
            vo = (hp % 4) * 128
            qT, kT, gT, vb = c["qT"], c["kT"], c["gT"], c["vb"]
            ck = ckr.next()
            cv = cvr.next()
            for hd in range(2):
                S.dma("pool", ck[:, :, hd, :], cache_k[hp * 2 + hd].rearrange("(kb p) d -> p kb d", p=128), writes=bufs(ck))
                S.dma("pool", cv[:, :, hd, :], cache_v[hp * 2 + hd].rearrange("(kb p) d -> p kb d", p=128), writes=bufs(cv))
            ckT = cktr.next()
            ptb = banks.next()
            ptv = ptb[:].bitcast(BF16)
            for kb in range(2):
                S.op("pe", lambda e, ptv=ptv, ck=ck, kb=kb: e.transpose(
                    out=ptv[:, kb * 128:(kb + 1) * 128], in_=ck[:, kb, :, :].rearrange("p a b -> p (a b)"), identity=identb[:]),
                    reads=bufs(ck, identb), writes=bufs(ptb))
            S.op("act", lambda e, ptv=ptv, ckT=ckT: e.activation(out=ckT[:], in_=ptv[:, 0:256], func=AF.Copy),
                 reads=bufs(ptb), writes=bufs(ckT))
            rps = []
            for hd in range(2):
                rp = rpr.next()
                S.dma("sp", rp[:], rpbg[hp * 2 + hd], writes=bufs(rp))
                rps.append(rp)
            items = []
            for pg in range(cfg.get("nat_pg", 4)):
                for hd in range(cfg.get("nat_hd", 2)):
                    for pi in range(cfg.get("nat_pi", 4)):
                        items.append((pg, hd, pi))

            def geom(pg, hd, pi):
                pr = pg * 4 + pi
                r = 2 * pr
                if pr <= 1:
                    r0, nrow, a0, mi = 0, 9, 7 - r, 1
                elif pr >= 14:
                    r0, nrow, a0, mi = 24, 8, (3 if pr == 14 else 1), 2
                else:
                    r0, nrow, a0, mi = r - 4, 9, 3, 0
                return r, r0, nrow, a0, mi

            def st_qk(it):
                pg, hd, pi = it
                r, r0, nrow, a0, mi = geom(*it)
                rows = slice(hd * 64, (hd + 1) * 64)
                q0, k0 = r * 64, r0 * 64
                ps1 = wbanks.next()
                ps2 = wbanks.next()
                S.op("pe", lambda e, ps1=ps1, rows=rows, q0=q0, k0=k0: e.matmul(
                    ps1[:], lhsT=qT[rows, q0:q0 + 128], rhs=kT[rows, k0:k0 + 512], start=True, stop=False),
                    reads=bufs(qT, kT), writes=bufs(ps1))
                S.op("pe", lambda e, ps1=ps1, mi=mi: e.matmul(
                    ps1[:], lhsT=identb[:], rhs=maskb[:, mi, 0:512], start=False, stop=True),
                    reads=bufs(identb, maskb), writes=bufs(ps1))
                if nrow == 9:
                    S.op("pe", lambda e, ps2=ps2, rows=rows, q0=q0, k0=k0: e.matmul(
                        ps2[:, 0:64], lhsT=qT[rows, q0:q0 + 128], rhs=kT[rows, k0 + 512:k0 + 576], start=True, stop=False),
                        reads=bufs(qT, kT), writes=bufs(ps2))
                    S.op("pe", lambda e, ps2=ps2, mi=mi: e.matmul(
                        ps2[:, 0:64], lhsT=identb[:], rhs=maskb[:, mi, 512:576], start=False, stop=True),
                        reads=bufs(identb, maskb), writes=bufs(ps2))
                S.op("pe", lambda e, ps2=ps2, rows=rows, q0=q0, ckT=ckT: e.matmul(
                    ps2[:, 64:320], lhsT=qT[rows, q0:q0 + 128], rhs=ckT[rows, :], start=True, stop=True),
                    reads=bufs(qT, ckT), writes=bufs(ps2))
                return ps1, ps2

            def st_softmax(it, ps1, ps2):
                pg, hd, pi = it
                r, r0, nrow, a0, mi = geom(*it)
                rp = rps[hd]
                nk = nrow * 64
                sc = scr.next()
                S.op("dve", lambda e, ps1=ps1, sc=sc, rp=rp, a0=a0: e.scalar_tensor_tensor(
                    out=sc[:, 0:512], in0=ps1[:], scalar=SCALE, in1=rp[:, a0 * 64:a0 * 64 + 512],
                    op0=ALU.mult, op1=ALU.add), reads=bufs(ps1, rp), writes=bufs(sc))
                if nrow == 9:
                    S.op("dve", lambda e, ps2=ps2, sc=sc, rp=rp, a0=a0: e.scalar_tensor_tensor(
                        out=sc[:, 512:576], in0=ps2[:, 0:64], scalar=SCALE, in1=rp[:, a0 * 64 + 512:a0 * 64 + 576],
                        op0=ALU.mult, op1=ALU.add), reads=bufs(ps2, rp), writes=bufs(sc))
                S.op("act", lambda e, ps2=ps2, sc=sc, nk=nk: e.activation(
                    out=sc[:, nk:nk + 256], in_=ps2[:, 64:320], func=AF.Copy, scale=SCALE),
                    reads=bufs(ps2), writes=bufs(sc))
                ntot = nk + 256
                mx = small.next()
                S.op("dve", lambda e, sc=sc, mx=mx, ntot=ntot: e.tensor_reduce(
                    out=mx[:, 0:1], in_=sc[:, 0:ntot], axis=AX.X, op=ALU.max), reads=bufs(sc), writes=bufs(mx))
                S.op("dve", lambda e, mx=mx: e.tensor_scalar(out=mx[:, 1:2], in0=mx[:, 0:1], scalar1=-1.0, scalar2=None,
                                                             op0=ALU.mult), reads=bufs(mx), writes=bufs(mx))
                S.op("act", lambda e, sc=sc, mx=mx, ntot=ntot: e.activation(
                    out=sc[:, 0:ntot], in_=sc[:, 0:ntot], func=AF.Exp, bias=mx[:, 1:2], accum_out=mx[:, 2:3]),
                    reads=bufs(sc, mx), writes=bufs(sc, mx))
                S.op("dve", lambda e, mx=mx: e.reciprocal(out=mx[:, 3:4], in_=mx[:, 2:3]), reads=bufs(mx), writes=bufs(mx))
                pbt = pbr.next()
                S.op("dve", lambda e, sc=sc, mx=mx, pbt=pbt, ntot=ntot: e.tensor_scalar(
                    out=pbt[:, 0:ntot], in0=sc[:, 0:ntot], scalar1=mx[:, 3:4], scalar2=None, op0=ALU.mult),
                    reads=bufs(sc, mx), writes=bufs(pbt))
                return pbt

            def st_tpv(it, pbt, po):
                pg, hd, pi = it
                r, r0, nrow, a0, mi = geom(*it)
                rows = slice(hd * 64, (hd + 1) * 64)
                nk = nrow * 64
                ptb = wbanks.next()
                ptv = ptb[:].bitcast(BF16)
                blocks = [(j * 128, 128) for j in range(4)]
                blocks += [(nk, 128), (nk + 128, 128)]
                if nrow == 9:
                    blocks.append((512, 64))
                for j, (c0, w) in enumerate(blocks):
                    S.op("pe", lambda e, ptv=ptv, pbt=pbt, j=j, c0=c0, w=w: e.transpose(
                        out=ptv[0:w, j * 128:(j + 1) * 128], in_=pbt[:, c0:c0 + w], identity=identb[:]),
                        reads=bufs(pbt, identb), writes=bufs(ptb))
                nb = len(blocks)
                pts = ptr_.next()
                S.op("act", lambda e, ptv=ptv, pts=pts: e.activation(
                    out=pts[:, 0:768], in_=ptv[:, 0:768], func=AF.Copy), reads=bufs(ptb), writes=bufs(pts))
                if nrow == 9:
                    S.op("act", lambda e, ptv=ptv, pts=pts: e.activation(
                        out=pts[0:64, 768:896], in_=ptv[0:64, 768:896], func=AF.Copy), reads=bufs(ptb), writes=bufs(pts))
                t0 = r0 // 2
                for j, (c0, w) in enumerate(blocks):
                    if j < 4:
                        lhs = vb[:, t0 + j, vo + hd * 64:vo + (hd + 1) * 64]
                        rhs = pts[:, j * 128:(j + 1) * 128]
                        rd = bufs(vb, pts)
                    elif w == 64:
                        lhs = vb[0:64, t0 + 4, vo + hd * 64:vo + (hd + 1) * 64]
                        rhs = pts[0:64, j * 128:(j + 1) * 128]
                        rd = bufs(vb, pts)
                    else:
                        kb = j - 4
                        lhs = cv[:, kb, hd, :]
                        rhs = pts[:, j * 128:(j + 1) * 128]
                        rd = bufs(cv, pts)
                    S.op("pe", lambda e, po=po, rows=rows, pi=pi, lhs=lhs, rhs=rhs, j=j, nb=nb: e.matmul(
                        po[rows, pi * 128:(pi + 1) * 128], lhsT=lhs, rhs=rhs, start=(j == 0), stop=(j == nb - 1)),
                        reads=rd, writes=bufs(po))

            pos_ = {}
            n_it = len(items)
            qk_res = {}
            sm_res = {}
            for j_ in range(min(2, n_it)):
                qk_res[j_] = st_qk(items[j_])
            if n_it:
                sm_res[0] = st_softmax(items[0], *qk_res.pop(0))
            for ii, it in enumerate(items):
                pg = it[0]
                if pg not in pos_:
                    pos_[pg] = pobanks.next()
                po = pos_[pg]
                if ii + 2 < n_it:
                    qk_res[ii + 2] = st_qk(items[ii + 2])
                if ii + 1 < n_it:
                    sm_res[ii + 1] = st_softmax(items[ii + 1], *qk_res.pop(ii + 1))
                pbt = sm_res.pop(ii)
                st_tpv(it, pbt, po)
                if ii + 1 == len(items) or items[ii + 1][0] != pg:
                    S.op("dve", lambda e, po=po, hp=hp, pg=pg: e.tensor_tensor(
                        out=yT[:, hp, pg * 512:(pg + 1) * 512], in0=po[:], in1=gT[:, pg * 512:(pg + 1) * 512], op=ALU.mult),
                        reads=bufs(po, gT), writes=bufs(yT))

    def ssd_consts():
        c = {}
        c["tri"] = [k.at([128, 128], F32), k.at([128, 128], F32)]
        c["mneg"] = [k.at([128, 128], F32), k.at([128, 128], F32)]
        c["negones"] = k.at([128, 128], F32)
        for d_ in range(2):
            sgn = 1 if d_ == 0 else -1
            S.op("pool", lambda e, d_=d_: e.memset(c["tri"][d_][:], 1.0), writes=bufs(c["tri"][d_]))
            S.op("pool", lambda e, d_=d_, sgn=sgn: e.affine_select(
                out=c["tri"][d_][:], in_=c["tri"][d_][:], compare_op=ALU.is_ge, fill=0.0, base=0,
                pattern=[[sgn, 128]], channel_multiplier=-sgn), reads=bufs(c["tri"][d_]), writes=bufs(c["tri"][d_]))
            S.op("pool", lambda e, d_=d_: e.memset(c["mneg"][d_][:], 0.0), writes=bufs(c["mneg"][d_]))
            S.op("pool", lambda e, d_=d_, sgn=sgn: e.affine_select(
                out=c["mneg"][d_][:], in_=c["mneg"][d_][:], compare_op=ALU.is_ge, fill=-30000.0, base=0,
                pattern=[[sgn, 128]], channel_multiplier=-sgn), reads=bufs(c["mneg"][d_]), writes=bufs(c["mneg"][d_]))
        S.op("pool", lambda e: e.memset(c["negones"][:], -1.0), writes=bufs(c["negones"]))
        c["ntri"] = [k.at([128, 128], F32), k.at([128, 128], F32)]
        for d_ in range(2):
            S.op("pool", lambda e, d_=d_: e.tensor_scalar(out=c["ntri"][d_][:], in0=c["tri"][d_][:], scalar1=-1.0, scalar2=None,
                                                          op0=ALU.mult), reads=bufs(c["tri"][d_]), writes=bufs(c["ntri"][d_]))
        c["cwT"] = k.at([128, 32, 5], F32)
        c["cbT"] = k.at([128, 32], F32)
        for j in range(5):
            S.dma("sp", c["cwT"][:, :, j], ssd_conv_w[j].rearrange("(b p) -> p b", p=128), writes=bufs(c["cwT"]))
        S.dma("sp", c["cbT"][:], ssd_conv_b.rearrange("(b p) -> p b", p=128), writes=bufs(c["cbT"]))
        c["dtb"] = k.at([128, 64], F32)
        c["abc"] = k.at([128, 64], F32)
        c["dsk"] = k.at([128, 32], F32)
        c["ngT"] = k.at([128, 16], F32)
        S.dma("sp", c["dtb"][:], ssd_dt_bias.partition_broadcast(128), writes=bufs(c["dtb"]))
        S.dma("sp", c["abc"][:], ssd_a_log.partition_broadcast(128), writes=bufs(c["abc"]))
        S.dma("sp", c["dsk"][:], ssd_d.partition_broadcast(128), writes=bufs(c["dsk"]))
        S.dma("sp", c["ngT"][:], ssd_norm_g.rearrange("(b p) -> p b", p=128), writes=bufs(c["ngT"]))
        S.op("act", lambda e: e.activation(out=c["abc"][:], in_=c["abc"][:], func=AF.Exp), reads=bufs(c["abc"]), writes=bufs(c["abc"]))
        S.op("dve", lambda e: e.tensor_scalar(out=c["abc"][:], in0=c["abc"][:], scalar1=-1.0, scalar2=None, op0=ALU.mult),
             reads=bufs(c["abc"]), writes=bufs(c["abc"]))
        return c

    def ssd_unit(c, tok0, ntile, nseq, is_lat):
        T_ = ntile * 128
        nch = ntile // nseq
        Lq = nch * 128
        dt_ = k.at([128, ntile, 64], F32)
        da = k.at([128, ntile, 64], F32)
        ecum = k.at([128, ntile, 64], F32)
        dtd = k.at([128, ntile, 64], F32)
        etot = k.at([128, ntile, 64], F32)
        ssq = k.at([128, ntile, 8], F32)
        rstd = k.at([128, ntile], F32)
        tmpr = k.aring(2, [128, 64], F32)
        wdt = wring.next()
        S.dma("pool", wdt[:, :, 0:64], ssd_w_in.rearrange("(k p) n -> p k n", p=128)[:, :, 6144:6208], writes=bufs(wdt))
        for t in range(ntile):
            pb = banks.next()
            for kk in range(8):
                S.op("pe", lambda e, pb=pb, kk=kk, t=t: e.matmul(
                    pb[:, 0:64], lhsT=hT[:, kk, t * 128:(t + 1) * 128], rhs=wdt[:, kk, 0:64], start=(kk == 0), stop=(kk == 7)),
                    reads=bufs(hT, wdt), writes=bufs(pb))
            S.op("dve", lambda e, pb=pb, t=t: e.tensor_tensor(out=dt_[:, t, :], in0=pb[:, 0:64], in1=c["dtb"][:], op=ALU.add),
                 reads=bufs(pb, c["dtb"]), writes=bufs(dt_))
        S.op("act", lambda e: e.activation(out=dt_[:], in_=dt_[:], func=AF.Exp), reads=bufs(dt_), writes=bufs(dt_))
        S.op("act", lambda e: e.activation(out=dt_[:], in_=dt_[:], func=AF.Ln, bias=1.0), reads=bufs(dt_), writes=bufs(dt_))
        S.op("dve", lambda e: e.tensor_tensor(out=da[:], in0=dt_[:], in1=c["abc"][:].unsqueeze(1).to_broadcast([128, ntile, 64]),
                                              op=ALU.mult), reads=bufs(dt_, c["abc"]), writes=bufs(da))
        for t in range(ntile):
            pc = banks.next()
            S.op("pe", lambda e, pc=pc, t=t: e.matmul(pc[:, 0:32], lhsT=c["tri"][0][:], rhs=da[:, t, 0:32], start=True, stop=True),
                 reads=bufs(c["tri"][0], da), writes=bufs(pc))
            S.op("pe", lambda e, pc=pc, t=t: e.matmul(pc[:, 32:64], lhsT=c["tri"][1][:], rhs=da[:, t, 32:64], start=True, stop=True),
                 reads=bufs(c["tri"][1], da), writes=bufs(pc))
            S.op("pe", lambda e, pc=pc, t=t: e.matmul(pc[:, 64:128], lhsT=onesf[:], rhs=da[:, t, :], start=True, stop=True),
                 reads=bufs(onesf, da), writes=bufs(pc))
            cumt = tmpr.next()
            S.op("act", lambda e, pc=pc, cumt=cumt: e.activation(out=cumt[:], in_=pc[:, 0:64], func=AF.Identity),
                 reads=bufs(pc), writes=bufs(cumt))
            S.op("act", lambda e, pc=pc, t=t: e.activation(out=ecum[:, t, :], in_=pc[:, 0:64], func=AF.Exp),
                 reads=bufs(pc), writes=bufs(ecum))
            S.op("act", lambda e, pc=pc, t=t: e.activation(out=etot[:, t, :], in_=pc[:, 64:128], func=AF.Exp),
                 reads=bufs(pc), writes=bufs(etot))
            S.op("dve", lambda e, pc=pc, cumt=cumt: e.tensor_tensor(out=cumt[:], in0=pc[:, 64:128], in1=cumt[:], op=ALU.subtract),
                 reads=bufs(pc, cumt), writes=bufs(cumt))
            S.op("act", lambda e, cumt=cumt: e.activation(out=cumt[:], in_=cumt[:], func=AF.Exp), reads=bufs(cumt), writes=bufs(cumt))
            S.op("dve", lambda e, cumt=cumt, t=t: e.tensor_tensor(out=dtd[:, t, :], in0=dt_[:, t, :], in1=cumt[:], op=ALU.mult),
                 reads=bufs(cumt, dt_), writes=bufs(dtd))
        raw = k.at([128, T_], F32)
        acc = k.at([128, T_], F32)
        fm = [k.at([128, T_], BF16) for _ in range(4)]
        XB = k.at([128, ntile, 384], BF16)
        SIN = k.at([128, ntile, 2, 256], BF16)
        stf = k.at([128, 2, 256], F32)
        vTg = k.at([128, 2, T_], BF16)
        h0r = k.aring(2, [128, 2, 128], F32)
        fir = k.aring(2, [128, 2, 128], F32)
        GTr = k.aring(2, [128, 128], F32)
        Dr = k.aring(2, [128, 4, 128], F32)
        Lr = k.aring(2, [128, 4, 128], F32)
        Mr = k.aring(4, [128, 4, 128], BF16)
        xdr = k.aring(5, [128, 4, 64], BF16)
        yr = k.aring(4, [128, 256], F32)
        szr = k.aring(2, [128, 256], F32)
        vbr = k.aring(2, [128, 256], BF16)
        tmp4 = k.aring(2, [128, 4, 64], F32)
        for g in range(cfg.get("ssd_g", 8)):
            wA = wring.next()
            wap = ssd_w_in.rearrange("(k p) n -> p k n", p=128)
            S.dma("pool", wA[:, :, 0:256], wap[:, :, g * 256:(g + 1) * 256], writes=bufs(wA))
            S.dma("pool", wA[:, :, 256:512], wap[:, :, E + g * 256:E + (g + 1) * 256], writes=bufs(wA))
            wB = wring.next()
            S.dma("pool", wB[:, :, 0:128], wap[:, :, 2 * E + g * 128:2 * E + (g + 1) * 128], writes=bufs(wB))
            S.dma("pool", wB[:, :, 128:256], wap[:, :, 2 * E + 1024 + g * 128:2 * E + 1024 + (g + 1) * 128], writes=bufs(wB))
            for bi in range(4):
                wt, co, cblk = ((wA, 256, 2 * g), (wA, 384, 2 * g + 1), (wB, 0, 16 + g), (wB, 128, 24 + g))[bi]
                for q in range(T_ // 512):
                    pb = banks.next()
                    for kk in range(8):
                        S.op("pe", lambda e, pb=pb, kk=kk, q=q, wt=wt, co=co: e.matmul(
                            pb[:], lhsT=wt[:, kk, co:co + 128], rhs=hT[:, kk, q * 512:(q + 1) * 512],
                            start=(kk == 0), stop=(kk == 7)), reads=bufs(wt, hT), writes=bufs(pb))
                    S.op("act", lambda e, pb=pb, q=q: e.activation(out=raw[:, q * 512:(q + 1) * 512], in_=pb[:], func=AF.Copy),
                         reads=bufs(pb), writes=bufs(raw))
                cw = c["cwT"]
                rv = raw[:].rearrange("p (s l) -> p s l", s=nseq)
                av = acc[:].rearrange("p (s l) -> p s l", s=nseq)
                S.op("dve", lambda e, cblk=cblk: e.tensor_scalar(out=acc[:], in0=raw[:], scalar1=cw[:, cblk, 2:3], scalar2=None,
                                                                 op0=ALU.mult), reads=bufs(raw, cw), writes=bufs(acc))
                taps = ((0, "dve", slice(2, Lq), slice(0, Lq - 2)), (1, "dve", slice(1, Lq), slice(0, Lq - 1)),
                        (3, "dve", slice(0, Lq - 1), slice(1, Lq)), (4, "dve", slice(0, Lq - 2), slice(2, Lq)))
                for j, eng, osl, isl in taps:
                    S.op(eng, lambda e, j=j, osl=osl, isl=isl, cblk=cblk, rv=rv, av=av: e.scalar_tensor_tensor(
                        out=av[:, :, osl], in0=rv[:, :, isl], scalar=cw[:, cblk, j:j + 1], in1=av[:, :, osl],
                        op0=ALU.mult, op1=ALU.add), reads=bufs(raw, acc, cw), writes=bufs(acc))
                S.op("act", lambda e, bi=bi, cblk=cblk: e.activation(out=fm[bi][:], in_=acc[:], func=AF.Silu,
                                                                      bias=c["cbT"][:, cblk:cblk + 1]),
                     reads=bufs(acc, c["cbT"]), writes=bufs(fm[bi]))
            for t in range(ntile):
                ptb = banks.next()
                ptv = ptb[:].bitcast(BF16)
                for bi in range(3):
                    S.op("pe", lambda e, ptv=ptv, bi=bi, t=t: e.transpose(
                        out=ptv[:, bi * 128:(bi + 1) * 128], in_=fm[bi][:, t * 128:(t + 1) * 128], identity=identb[:]),
                        reads=bufs(fm[bi], identb), writes=bufs(ptb))
                S.op("act", lambda e, ptv=ptv, t=t: e.activation(out=XB[:, t, :], in_=ptv[:, 0:384], func=AF.Copy),
                     reads=bufs(ptb), writes=bufs(XB))
            BT, CT = fm[2], fm[3]
            for sq in range(nseq):
                for d_ in range(2):
                    hs = slice(d_ * 32 + 4 * g, d_ * 32 + 4 * g + 4)
                    if is_lat:
                        h0 = h0r.next()
                        for half in range(2):
                            S.dma("sp", h0[:, half, :], state_ssd[d_, 4 * g + 2 * half:4 * g + 2 * half + 2].rearrange("h p n -> (h p) n"),
                                  writes=bufs(h0))
                        ph = banks.next()
                        for half in range(2):
                            S.op("pe", lambda e, ph=ph, h0=h0, half=half: e.transpose(
                                out=ph[:, half * 128:(half + 1) * 128], in_=h0[:, half, :], identity=identf[:]),
                                reads=bufs(h0, identf), writes=bufs(ph))
                        S.op("act", lambda e, ph=ph, d_=d_: e.activation(out=stf[:, d_, :], in_=ph[:, 0:256], func=AF.Copy),
                             reads=bufs(ph), writes=bufs(stf))
                    else:
                        S.op("pool", lambda e, d_=d_: e.memset(stf[:, d_, :], 0.0), writes=bufs(stf))
                    order = range(nch) if d_ == 0 else range(nch - 1, -1, -1)
                    for ci in order:
                        t = sq * nch + ci
                        S.op("act", lambda e, t=t, d_=d_: e.activation(out=SIN[:, t, d_, :], in_=stf[:, d_, :], func=AF.Copy),
                             reads=bufs(stf), writes=bufs(SIN))
                        xdd = xdr.next()
                        S.op("pool", lambda e, xdd=xdd, t=t, hs=hs: e.tensor_tensor(
                            out=xdd[:], in0=XB[:, t, 0:256].rearrange("p (h d) -> p h d", h=4),
                            in1=dtd[:, t, hs].unsqueeze(2).to_broadcast([128, 4, 64]), op=ALU.mult),
                            reads=bufs(XB, dtd), writes=bufs(xdd))
                        psl = banks.next()
                        S.op("pe", lambda e, psl=psl, t=t, xdd=xdd: e.matmul(
                            psl[:, 0:256], lhsT=XB[:, t, 256:384], rhs=xdd[:].rearrange("p h d -> p (h d)"), start=True, stop=True),
                            reads=bufs(XB, xdd), writes=bufs(psl))
                        S.op("pool", lambda e, t=t, d_=d_, hs=hs: e.tensor_tensor(
                            out=stf[:, d_, :].rearrange("p (h d) -> p h d", h=4), in0=stf[:, d_, :].rearrange("p (h d) -> p h d", h=4),
                            in1=etot[:, t, hs].unsqueeze(2).to_broadcast([128, 4, 64]), op=ALU.mult),
                            reads=bufs(stf, etot), writes=bufs(stf))
                        S.op("dve", lambda e, psl=psl, d_=d_: e.tensor_tensor(out=stf[:, d_, :], in0=psl[:, 0:256], in1=stf[:, d_, :],
                                                                            op=ALU.add), reads=bufs(psl, stf), writes=bufs(stf))
                    if not is_lat:
                        pf = banks.next()
                        for half in range(2):
                            S.op("pe", lambda e, pf=pf, d_=d_, half=half: e.transpose(
                                out=pf[:, half * 128:(half + 1) * 128], in_=stf[:, d_, half * 128:(half + 1) * 128], identity=identf[:]),
                                reads=bufs(stf, identf), writes=bufs(pf))
                        fi = fir.next()
                        S.op("act", lambda e, pf=pf, fi=fi: e.activation(out=fi[:].rearrange("p a b -> p (a b)"), in_=pf[:, 0:256],
                                                                        func=AF.Copy), reads=bufs(pf), writes=bufs(fi))
                        for half in range(2):
                            S.dma("sp", new_ssd[sq, d_, 4 * g + 2 * half:4 * g + 2 * half + 2].rearrange("h p n -> (h p) n"),
                                  fi[:, half, :], reads=bufs(fi))
            def front(t):
                tsl = slice(t * 128, (t + 1) * 128)
                pgz = banks.next()
                S.op("pe", lambda e, pgz=pgz, tsl=tsl: e.matmul(pgz[:, 0:128], lhsT=BT[:, tsl], rhs=CT[:, tsl], start=True, stop=True),
                     reads=bufs(BT, CT), writes=bufs(pgz))
                for kk in range(8):
                    S.op("pe", lambda e, pgz=pgz, kk=kk, tsl=tsl, wA=wA: e.matmul(
                        pgz[:, 128:384], lhsT=hT[:, kk, tsl], rhs=wA[:, kk, 0:256], start=(kk == 0), stop=(kk == 7)),
                        reads=bufs(hT, wA), writes=bufs(pgz))
                GT = GTr.next()
                S.op("act", lambda e, pgz=pgz, GT=GT: e.activation(out=GT[:], in_=pgz[:, 0:128], func=AF.Copy),
                     reads=bufs(pgz), writes=bufs(GT))
                sz = szr.next()
                S.op("act", lambda e, pgz=pgz, sz=sz: e.activation(out=sz[:], in_=pgz[:, 128:384], func=AF.Silu),
                     reads=bufs(pgz), writes=bufs(sz))
                hss = [slice(d_ * 32 + 4 * g, d_ * 32 + 4 * g + 4) for d_ in range(2)]
                Dts = []
                for d_ in range(2):
                    Dt = Dr.next()
                    S.op("pool", lambda e, Dt=Dt, d_=d_, t=t, hs=hss[d_]: e.tensor_tensor(
                        out=Dt[:], in0=c["tri"][d_][:].unsqueeze(1).to_broadcast([128, 4, 128]),
                        in1=da[:, t, hs].unsqueeze(2).to_broadcast([128, 4, 128]), op=ALU.mult),
                        reads=bufs(c["tri"][d_], da), writes=bufs(Dt))
                    Dts.append(Dt)
                pzs = []
                for d_ in range(2):
                    pz_ = banks.next()
                    Dt = Dts[d_]
                    S.op("pe", lambda e, pz_=pz_, Dt=Dt: e.matmul(
                        pz_[:], lhsT=onesf[:], rhs=Dt[:].rearrange("p h t -> p (h t)"), start=True, stop=False),
                        reads=bufs(onesf, Dt), writes=bufs(pz_))
                    S.op("pe", lambda e, pz_=pz_, d_=d_, t=t, hs=hss[d_]: e.matmul(
                        pz_[:], lhsT=c["ntri"][d_][:], rhs=da[:, t, hs].unsqueeze(2).to_broadcast([128, 4, 128]), start=False, stop=False),
                        reads=bufs(c["ntri"][d_], da), writes=bufs(pz_))
                    S.op("pe", lambda e, pz_=pz_, d_=d_: e.matmul(
                        pz_[:], lhsT=identf[:], rhs=c["mneg"][d_][:].unsqueeze(1).to_broadcast([128, 4, 128]), start=False, stop=True),
                        reads=bufs(identf, c["mneg"][d_]), writes=bufs(pz_))
                    pzs.append(pz_)
                poo = banks.next()
                for d_ in range(2):
                    S.op("pe", lambda e, poo=poo, tsl=tsl, t=t, d_=d_: e.matmul(
                        poo[:, d_ * 256:(d_ + 1) * 256], lhsT=CT[:, tsl], rhs=SIN[:, t, d_, :], start=True, stop=True),
                        reads=bufs(CT, SIN), writes=bufs(poo))
                Lts = []
                for d_ in range(2):
                    Lt = Lr.next()
                    S.op("act", lambda e, pz_=pzs[d_], Lt=Lt: e.activation(out=Lt[:].rearrange("p h t -> p (h t)"), in_=pz_[:], func=AF.Exp),
                         reads=bufs(pzs[d_]), writes=bufs(Lt))
                    Lts.append(Lt)
                Mts, xds = [], []
                for d_ in range(2):
                    Mt = Mr.next()
                    S.op("dve", lambda e, Lt=Lts[d_], Mt=Mt, GT=GT: e.tensor_tensor(
                        out=Mt[:], in0=Lt[:], in1=GT[:].unsqueeze(1).to_broadcast([128, 4, 128]), op=ALU.mult),
                        reads=bufs(Lts[d_], GT), writes=bufs(Mt))
                    xd = xdr.next()
                    S.op("pool", lambda e, xd=xd, t=t, hs=hss[d_]: e.tensor_tensor(
                        out=xd[:], in0=XB[:, t, 0:256].rearrange("p (h d) -> p h d", h=4),
                        in1=dt_[:, t, hs].unsqueeze(2).to_broadcast([128, 4, 64]), op=ALU.mult),
                        reads=bufs(XB, dt_), writes=bufs(xd))
                    Mts.append(Mt)
                    xds.append(xd)
                return dict(t=t, tsl=tsl, sz=sz, Mts=Mts, xds=xds, poo=poo, hss=hss)

            def back(f):
                t, tsl, sz, poo, hss = f["t"], f["tsl"], f["sz"], f["poo"], f["hss"]
                py = banks.next()
                for d_ in range(2):
                    Mt, xd = f["Mts"][d_], f["xds"][d_]
                    for h in range(4):
                        S.op("pe", lambda e, py=py, Mt=Mt, xd=xd, h=h, d_=d_: e.matmul(
                            py[:, h * 64:(h + 1) * 64], lhsT=Mt[:, h, :], rhs=xd[:, h, :], start=(d_ == 0 and h == 0), stop=(d_ == 1 and h == 3)),
                            reads=bufs(Mt, xd), writes=bufs(py))
                y1 = yr.next()
                y2 = yr.next()
                for d_, yy in ((0, y1), (1, y2)):
                    S.op("dve", lambda e, poo=poo, hs=hss[d_], yy=yy, t=t, d_=d_: e.tensor_tensor(
                        out=yy[:].rearrange("p (h d) -> p h d", h=4), in0=poo[:, d_ * 256:(d_ + 1) * 256].rearrange("p (h d) -> p h d", h=4),
                        in1=ecum[:, t, hs].unsqueeze(2).to_broadcast([128, 4, 64]), op=ALU.mult),
                        reads=bufs(poo, ecum), writes=bufs(yy))
                S.op("pool", lambda e, y1=y1, y2=y2: e.tensor_tensor(out=y1[:], in0=y1[:], in1=y2[:], op=ALU.add),
                     reads=bufs(y1, y2), writes=bufs(y1))
                S.op("pool", lambda e, y2=y2, t=t, g=g: e.tensor_tensor(
                    out=y2[:].rearrange("p (h d) -> p h d", h=4), in0=XB[:, t, 0:256].rearrange("p (h d) -> p h d", h=4),
                    in1=c["dsk"][:, 4 * g:4 * g + 4].unsqueeze(2).to_broadcast([128, 4, 64]), op=ALU.mult),
                    reads=bufs(XB, c["dsk"]), writes=bufs(y2))
                S.op("pool", lambda e, y1=y1, y2=y2: e.tensor_tensor(out=y1[:], in0=y1[:], in1=y2[:], op=ALU.add),
                     reads=bufs(y1, y2), writes=bufs(y1))
                S.op("dve", lambda e, py=py, y1=y1: e.tensor_tensor(out=y1[:], in0=py[:, 0:256], in1=y1[:], op=ALU.add),
                     reads=bufs(py, y1), writes=bufs(y1))
                vb_ = vbr.next()
                S.op("pool", lambda e, vb_=vb_, y1=y1, sz=sz: e.tensor_tensor(out=vb_[:], in0=y1[:], in1=sz[:], op=ALU.mult),
                     reads=bufs(y1, sz), writes=bufs(vb_))
                S.op("act", lambda e, vb_=vb_, t=t, g=g: e.activation(out=junk[:, 0:256], in_=vb_[:], func=AF.Square,
                                                                     accum_out=ssq[:, t, g:g + 1]),
                     reads=bufs(vb_), writes=bufs(junk, ssq))
                ptb = banks.next()
                ptv = ptb[:].bitcast(BF16)
                for bb in range(2):
                    S.op("pe", lambda e, ptv=ptv, vb_=vb_, bb=bb: e.transpose(
                        out=ptv[:, bb * 128:(bb + 1) * 128], in_=vb_[:, bb * 128:(bb + 1) * 128], identity=identb[:]),
                        reads=bufs(vb_, identb), writes=bufs(ptb))
                for bb in range(2):
                    S.op("act", lambda e, ptv=ptv, bb=bb, tsl=tsl, g=g: e.activation(
                        out=vTg[:, bb, tsl], in_=ptv[:, bb * 128:(bb + 1) * 128], func=AF.Identity,
                        scale=c["ngT"][:, 2 * g + bb:2 * g + bb + 1]), reads=bufs(ptb, c["ngT"]), writes=bufs(vTg))

            fnext = front(0)
            for t in range(ntile):
                fcur = fnext
                if t + 1 < ntile:
                    fnext = front(t + 1)
                back(fcur)
            S.dma("sp", yscr[2 * g:2 * g + 2, :, tok0:tok0 + T_].rearrange("b p t -> p b t"), vTg[:], reads=bufs(vTg))
        S.op("dve", lambda e: e.tensor_reduce(out=rstd[:], in_=ssq[:], axis=AX.X, op=ALU.add), reads=bufs(ssq), writes=bufs(rstd))
        S.op("dve", lambda e: e.tensor_scalar(out=rstd[:], in0=rstd[:], scalar1=1.0 / E, scalar2=EPS, op0=ALU.mult, op1=ALU.add),
             reads=bufs(rstd), writes=bufs(rstd))
        S.op("act", lambda e: e.activation(out=rstd[:], in_=rstd[:], func=AF.Sqrt), reads=bufs(rstd), writes=bufs(rstd))
        S.op("dve", lambda e: e.reciprocal(out=rstd[:], in_=rstd[:]), reads=bufs(rstd), writes=bufs(rstd))
        return rstd

    TWO_PI = float(2 * np.pi)
    MAGIC = 12582912.0

    def s5_prep():
        c = {}
        c["AA"] = k.at([128, 64, 2, 2], F32)
        c["BB"] = k.at([128, 64, 2, 2], F32)
        c["Wsel"] = k.at([128, 8, 240], BF16)
        c["dT"] = k.at([128, 16], F32)
        c["bgT"] = k.at([128, 16], F32)
        c["h0"] = k.at([128, 2, 2, 64], F32)
        S.dma("sp", c["dT"][:], s5_d.rearrange("(b p) -> p b", p=128), writes=bufs(c["dT"]))
        S.dma("sp", c["bgT"][:], s5_b_glu.rearrange("(b p) -> p b", p=128), writes=bufs(c["bgT"]))
        for d_ in range(2):
            S.dma("sp", c["h0"][:, d_, :, :], s5_h0[d_].rearrange("r p g -> p r g"), writes=bufs(c["h0"]))
        mm = k.amark()
        wself = k.at([128, 8, 240], F32)
        S.op("pool", lambda e: e.memset(wself[:], 0.0), writes=bufs(wself))
        S.op("pool", lambda e: e.affine_select(out=wself[:, :, 112:128], in_=wself[:, :, 112:128], compare_op=ALU.not_equal,
                                               fill=1.0, base=0, pattern=[[-16, 8], [-1, 16]], channel_multiplier=1),
             reads=bufs(wself), writes=bufs(wself))
        S.op("pool", lambda e: e.tensor_copy(out=c["Wsel"][:], in_=wself[:]), reads=bufs(wself), writes=bufs(c["Wsel"]))
        maskT = [k.at([128, 8, 16], F32), k.at([128, 8, 16], F32)]
        for d_ in range(2):
            S.op("pool", lambda e, d_=d_: e.memset(maskT[d_][:], 1.0), writes=bufs(maskT[d_]))
        S.op("pool", lambda e: e.affine_select(out=maskT[0][:], in_=maskT[0][:], compare_op=ALU.is_ge, fill=0.0, base=15,
                                               pattern=[[16, 8], [0, 16]], channel_multiplier=-1),
             reads=bufs(maskT[0]), writes=bufs(maskT[0]))
        S.op("pool", lambda e: e.affine_select(out=maskT[1][:], in_=maskT[1][:], compare_op=ALU.is_ge, fill=0.0, base=0,
                                               pattern=[[-16, 8], [0, 16]], channel_multiplier=1),
             reads=bufs(maskT[1]), writes=bufs(maskT[1]))
        pw = [[k.at([128, 64, 16], F32), k.at([128, 64, 16], F32)] for _ in range(2)]
        coef = [[k.at([128, 64], F32), k.at([128, 64], F32)] for _ in range(2)]
        lr, li, ls = k.at([128, 64], F32), k.at([128, 64], F32), k.at([128, 64], F32)
        xx, ang = k.at([128, 64], F32), k.at([128, 64], F32)
        tr = k.aring(6, [128, 64], F32)
        for d_ in range(2):
            S.dma("sp", lr[:], s5_lam[0, d_], writes=bufs(lr))
            S.dma("sp", li[:], s5_lam[1, d_], writes=bufs(li))
            S.dma("sp", ls[:], s5_lstep[d_], writes=bufs(ls))
            S.op("act", lambda e: e.activation(out=ls[:], in_=ls[:], func=AF.Exp), reads=bufs(ls), writes=bufs(ls))
            S.op("dve", lambda e: e.tensor_tensor(out=xx[:], in0=lr[:], in1=ls[:], op=ALU.mult), reads=bufs(lr, ls), writes=bufs(xx))
            S.op("dve", lambda e: e.tensor_tensor(out=ang[:], in0=li[:], in1=ls[:], op=ALU.mult), reads=bufs(li, ls), writes=bufs(ang))
            pre, pim = pw[d_]
            for kq in range(1, 9):
                mp, mn, sn, cs, t1, t2 = [tr.next() for _ in range(6)]
                S.op("act", lambda e, mp=mp, kq=kq: e.activation(out=mp[:], in_=xx[:], func=AF.Exp, scale=float(kq)),
                     reads=bufs(xx), writes=bufs(mp))
                S.op("act", lambda e, mn=mn, kq=kq: e.activation(out=mn[:], in_=xx[:], func=AF.Exp, scale=float(-kq)),
                     reads=bufs(xx), writes=bufs(mn))
                for dst, shift in ((sn, 0.0), (cs, 0.25)):
                    if shift:
                        S.op("dve", lambda e, t1=t1, kq=kq, shift=shift: e.tensor_scalar(
                            out=t1[:], in0=ang[:], scalar1=float(kq / TWO_PI), scalar2=shift, op0=ALU.mult, op1=ALU.add),
                            reads=bufs(ang), writes=bufs(t1))
                        S.op("dve", lambda e, t1=t1: e.tensor_scalar(out=t1[:], in0=t1[:], scalar1=MAGIC, scalar2=None, op0=ALU.add),
                             reads=bufs(t1), writes=bufs(t1))
                    else:
                        S.op("dve", lambda e, t1=t1, kq=kq: e.tensor_scalar(
                            out=t1[:], in0=ang[:], scalar1=float(kq / TWO_PI), scalar2=MAGIC, op0=ALU.mult, op1=ALU.add),
                            reads=bufs(ang), writes=bufs(t1))
                    S.op("dve", lambda e, t1=t1: e.tensor_scalar(out=t1[:], in0=t1[:], scalar1=-MAGIC, scalar2=-TWO_PI,
                                                                 op0=ALU.add, op1=ALU.mult), reads=bufs(t1), writes=bufs(t1))
                    S.op("dve", lambda e, t1=t1, kq=kq: e.scalar_tensor_tensor(
                        out=t1[:], in0=ang[:], scalar=float(kq), in1=t1[:], op0=ALU.mult, op1=ALU.add),
                        reads=bufs(ang, t1), writes=bufs(t1))
                    if shift:
                        S.op("dve", lambda e, t1=t1: e.tensor_scalar(out=t1[:], in0=t1[:], scalar1=float(np.pi / 2), scalar2=None,
                                                                     op0=ALU.add), reads=bufs(t1), writes=bufs(t1))
                    S.op("act", lambda e, t1=t1, dst=dst: e.activation(out=dst[:], in_=t1[:], func=AF.Sin),
                         reads=bufs(t1), writes=bufs(dst))
                S.op("dve", lambda e, kq=kq, mp=mp, cs=cs, pre=pre: e.tensor_tensor(out=pre[:, :, kq - 1], in0=mp[:], in1=cs[:], op=ALU.mult),
                     reads=bufs(mp, cs), writes=bufs(pre))
                S.op("dve", lambda e, kq=kq, mp=mp, sn=sn, pim=pim: e.tensor_tensor(out=pim[:, :, kq - 1], in0=mp[:], in1=sn[:], op=ALU.mult),
                     reads=bufs(mp, sn), writes=bufs(pim))
                S.op("dve", lambda e, kq=kq, mn=mn, cs=cs, pre=pre: e.tensor_tensor(out=pre[:, :, 7 + kq], in0=mn[:], in1=cs[:], op=ALU.mult),
                     reads=bufs(mn, cs), writes=bufs(pre))
                S.op("dve", lambda e, kq=kq, mn=mn, sn=sn, pim=pim: e.scalar_tensor_tensor(
                    out=pim[:, :, 7 + kq], in0=mn[:], scalar=-1.0, in1=sn[:], op0=ALU.mult, op1=ALU.mult),
                    reads=bufs(mn, sn), writes=bufs(pim))
            for r_ in range(2):
                S.op("act", lambda e, d_=d_, r_=r_, pre=pre: e.activation(out=c["AA"][:, :, d_, r_], in_=pre[:, :, 7], func=AF.Copy),
                     reads=bufs(pre), writes=bufs(c["AA"]))
            S.op("dve", lambda e, d_=d_, pim=pim: e.tensor_scalar(out=c["BB"][:, :, d_, 0], in0=pim[:, :, 7], scalar1=-1.0, scalar2=None,
                                                         op0=ALU.mult), reads=bufs(pim), writes=bufs(c["BB"]))
            S.op("act", lambda e, d_=d_, pim=pim: e.activation(out=c["BB"][:, :, d_, 1], in_=pim[:, :, 7], func=AF.Copy),
                 reads=bufs(pim), writes=bufs(c["BB"]))
            den, nr, t1, t2 = [tr.next() for _ in range(4)]
            S.op("dve", lambda e, den=den: e.tensor_tensor(out=den[:], in0=lr[:], in1=lr[:], op=ALU.mult), reads=bufs(lr), writes=bufs(den))
            S.op("dve", lambda e, t1=t1: e.tensor_tensor(out=t1[:], in0=li[:], in1=li[:], op=ALU.mult), reads=bufs(li), writes=bufs(t1))
            S.op("dve", lambda e, den=den, t1=t1: e.tensor_tensor(out=den[:], in0=den[:], in1=t1[:], op=ALU.add),
                 reads=bufs(den, t1), writes=bufs(den))
            S.op("dve", lambda e, den=den: e.reciprocal(out=den[:], in_=den[:]), reads=bufs(den), writes=bufs(den))
            S.op("dve", lambda e, nr=nr, pre=pre: e.tensor_scalar(out=nr[:], in0=pre[:, :, 0], scalar1=-1.0, scalar2=None, op0=ALU.add),
                 reads=bufs(pre), writes=bufs(nr))
            cr_, ci_ = coef[d_]
            S.op("dve", lambda e, nr=nr, t1=t1: e.tensor_tensor(out=t1[:], in0=nr[:], in1=lr[:], op=ALU.mult), reads=bufs(nr, lr), writes=bufs(t1))
            S.op("dve", lambda e, t2=t2, pim=pim: e.tensor_tensor(out=t2[:], in0=pim[:, :, 0], in1=li[:], op=ALU.mult), reads=bufs(pim, li), writes=bufs(t2))
            S.op("dve", lambda e, t1=t1, t2=t2: e.tensor_tensor(out=t1[:], in0=t1[:], in1=t2[:], op=ALU.add), reads=bufs(t1, t2), writes=bufs(t1))
            S.op("dve", lambda e, t1=t1, den=den, cr_=cr_: e.tensor_tensor(out=cr_[:], in0=t1[:], in1=den[:], op=ALU.mult),
                 reads=bufs(t1, den), writes=bufs(cr_))
            S.op("dve", lambda e, t1=t1, pim=pim: e.tensor_tensor(out=t1[:], in0=pim[:, :, 0], in1=lr[:], op=ALU.mult), reads=bufs(pim, lr), writes=bufs(t1))
            S.op("dve", lambda e, nr=nr, t2=t2: e.tensor_tensor(out=t2[:], in0=nr[:], in1=li[:], op=ALU.mult), reads=bufs(nr, li), writes=bufs(t2))
            S.op("dve", lambda e, t1=t1, t2=t2: e.tensor_tensor(out=t1[:], in0=t1[:], in1=t2[:], op=ALU.subtract), reads=bufs(t1, t2), writes=bufs(t1))
            S.op("dve", lambda e, t1=t1, den=den, ci_=ci_: e.tensor_tensor(out=ci_[:], in0=t1[:], in1=den[:], op=ALU.mult),
                 reads=bufs(t1, den), writes=bufs(ci_))
        Braw = [k.at([128, 8, 16], F32), k.at([128, 8, 16], F32)]
        Craw = [k.at([128, 8, 16], F32), k.at([128, 8, 16], F32)]
        Bb = [k.at([128, 8, 16], F32), k.at([128, 8, 16], F32)]
        V = [[k.at([128, 8, 8, 16], F32), k.at([128, 8, 8, 16], F32)] for _ in range(2)]
        W2 = [[k.at([128, 8, 8, 16], F32), k.at([128, 8, 8, 16], F32)] for _ in range(2)]
        t8 = k.aring(4, [128, 8, 16], F32)
        t8e = {"pool": k.aring(4, [128, 8, 16], F32), "dve": k.aring(4, [128, 8, 16], F32)}
        T16 = k.aring(2, [128, 16, 128], BF16)
        VT16 = k.aring(2, [128, 8, 2, 2, 128], BF16)
        W216 = k.aring(2, [128, 8, 2, 2, 128], BF16)
        Ttmp = k.aring(2, [128, 128], F32)
        Ttmp2 = k.aring(2, [128, 128], F32)

        def bc_j(ap2):
            return ap2.unsqueeze(2).to_broadcast([128, 8, 16])

        for b in range(8):
            gs = slice(8 * b, 8 * b + 8)
            t16, vt16, w216 = T16.next(), VT16.next(), W216.next()
            for d_ in range(2):
                pre, pim = pw[d_]
                cr_, ci_ = coef[d_]
                for r_ in range(2):
                    S.dma("sp", Braw[r_][:], s5_B[r_, d_, :, gs, :], writes=bufs(Braw[r_]))
                    S.dma("sp", Craw[r_][:], s5_C[r_, d_, :, gs, :], writes=bufs(Craw[r_]))
                ta, tb = t8.next(), t8.next()
                S.op("dve", lambda e, ta=ta, cr_=cr_, gs=gs: e.tensor_tensor(out=ta[:], in0=Braw[0][:], in1=bc_j(cr_[:, gs]), op=ALU.mult),
                     reads=bufs(Braw[0], cr_), writes=bufs(ta))
                S.op("dve", lambda e, tb=tb, ci_=ci_, gs=gs: e.tensor_tensor(out=tb[:], in0=Braw[1][:], in1=bc_j(ci_[:, gs]), op=ALU.mult),
                     reads=bufs(Braw[1], ci_), writes=bufs(tb))
                S.op("dve", lambda e, ta=ta, tb=tb: e.tensor_tensor(out=Bb[0][:], in0=ta[:], in1=tb[:], op=ALU.subtract),
                     reads=bufs(ta, tb), writes=bufs(Bb[0]))
                ta, tb = t8.next(), t8.next()
                S.op("dve", lambda e, ta=ta, cr_=cr_, gs=gs: e.tensor_tensor(out=ta[:], in0=Braw[1][:], in1=bc_j(cr_[:, gs]), op=ALU.mult),
                     reads=bufs(Braw[1], cr_), writes=bufs(ta))
                S.op("dve", lambda e, tb=tb, ci_=ci_, gs=gs: e.tensor_tensor(out=tb[:], in0=Braw[0][:], in1=bc_j(ci_[:, gs]), op=ALU.mult),
                     reads=bufs(Braw[0], ci_), writes=bufs(tb))
                S.op("dve", lambda e, ta=ta, tb=tb: e.tensor_tensor(out=Bb[1][:], in0=ta[:], in1=tb[:], op=ALU.add),
                     reads=bufs(ta, tb), writes=bufs(Bb[1]))
                for s_ in range(8):
                    kv = 8 + (s_ if d_ == 0 else 7 - s_)
                    kw = s_ if d_ == 0 else 7 - s_
                    for (eng, P_idx, X_, out_, neg_im) in (("pool", kv, Bb, V[d_], False), ("dve", kw, Craw, W2[d_], True)):
                        Pr = bc_j(pre[:, gs, P_idx])
                        Pi = bc_j(pim[:, gs, P_idx])
                        ta, tb = t8e[eng].next(), t8e[eng].next()
                        S.op(eng, lambda e, ta=ta, X_=X_, Pr=Pr: e.tensor_tensor(out=ta[:], in0=X_[0][:], in1=Pr, op=ALU.mult),
                             reads=bufs(X_[0], pre), writes=bufs(ta))
                        S.op(eng, lambda e, tb=tb, X_=X_, Pi=Pi: e.tensor_tensor(out=tb[:], in0=X_[1][:], in1=Pi, op=ALU.mult),
                             reads=bufs(X_[1], pim), writes=bufs(tb))
                        S.op(eng, lambda e, ta=ta, tb=tb, out_=out_, s_=s_: e.tensor_tensor(
                            out=out_[0][:, :, s_, :], in0=ta[:], in1=tb[:], op=ALU.subtract), reads=bufs(ta, tb), writes=bufs(out_[0]))
                        ta, tb = t8e[eng].next(), t8e[eng].next()
                        S.op(eng, lambda e, ta=ta, X_=X_, Pi=Pi: e.tensor_tensor(out=ta[:], in0=X_[0][:], in1=Pi, op=ALU.mult),
                             reads=bufs(X_[0], pim), writes=bufs(ta))
                        S.op(eng, lambda e, tb=tb, X_=X_, Pr=Pr: e.tensor_tensor(out=tb[:], in0=X_[1][:], in1=Pr, op=ALU.mult),
                             reads=bufs(X_[1], pre), writes=bufs(tb))
                        if not neg_im:
                            S.op(eng, lambda e, ta=ta, tb=tb, out_=out_, s_=s_: e.tensor_tensor(
                                out=out_[1][:, :, s_, :], in0=ta[:], in1=tb[:], op=ALU.add), reads=bufs(ta, tb), writes=bufs(out_[1]))
                        else:
                            S.op(eng, lambda e, ta=ta, tb=tb: e.tensor_tensor(out=ta[:], in0=ta[:], in1=tb[:], op=ALU.add),
                                 reads=bufs(ta, tb), writes=bufs(ta))
                            S.op(eng, lambda e, ta=ta, out_=out_, s_=s_: e.tensor_scalar(
                                out=out_[1][:, :, s_, :], in0=ta[:], scalar1=-1.0, scalar2=None, op0=ALU.mult),
                                reads=bufs(ta), writes=bufs(out_[1]))
                for r_ in range(2):
                    S.op("act", lambda e, d_=d_, r_=r_, w216=w216: e.activation(
                        out=w216[:, :, d_, r_, :], in_=W2[d_][r_][:].rearrange("p g s j -> p g (s j)"), func=AF.Copy),
                        reads=bufs(W2[d_][r_]), writes=bufs(w216))
                for gl in range(8):
                    pv_ = banks.next()
                    for r_ in range(2):
                        S.op("pe", lambda e, pv_=pv_, d_=d_, r_=r_, gl=gl: e.transpose(
                            out=pv_[:, r_ * 128:(r_ + 1) * 128], in_=V[d_][r_][:, gl, :, :].rearrange("p s j -> p (s j)"),
                            identity=identf[:]), reads=bufs(V[d_][r_], identf), writes=bufs(pv_))
                    S.op("act", lambda e, pv_=pv_, d_=d_, gl=gl, vt16=vt16: e.activation(
                        out=vt16[:, gl, d_, :, :].rearrange("p r m -> p (r m)"), in_=pv_[:, 0:256], func=AF.Copy),
                        reads=bufs(pv_), writes=bufs(vt16))
            for gl in range(8):
                for par in range(2):
                    rows = slice(par * 64, (par + 1) * 64)
                    gi = gl * 2 + par
                    pT = banks.next()
                    for d_ in range(2):
                        for r_ in range(2):
                            S.op("pe", lambda e, pT=pT, d_=d_, r_=r_, gl=gl, rows=rows: e.matmul(
                                pT[:, d_ * 128:(d_ + 1) * 128], lhsT=V[d_][r_][rows, gl, :, :].rearrange("p s j -> p (s j)"),
                                rhs=W2[d_][r_][rows, gl, :, :].rearrange("p s j -> p (s j)"), start=(r_ == 0), stop=(r_ == 1)),
                                reads=bufs(V[d_][r_], W2[d_][r_]), writes=bufs(pT))
                    ta, tb = Ttmp.next(), Ttmp2.next()
                    S.op("dve", lambda e, pT=pT, ta=ta: e.tensor_tensor(
                        out=ta[:], in0=pT[:, 0:128], in1=maskT[0][:].rearrange("p t j -> p (t j)"), op=ALU.mult),
                        reads=bufs(pT, maskT[0]), writes=bufs(ta))
                    S.op("dve", lambda e, pT=pT, tb=tb: e.tensor_tensor(
                        out=tb[:], in0=pT[:, 128:256], in1=maskT[1][:].rearrange("p t j -> p (t j)"), op=ALU.mult),
                        reads=bufs(pT, maskT[1]), writes=bufs(tb))
                    S.op("pool", lambda e, ta=ta, tb=tb, t16=t16, gi=gi: e.tensor_tensor(out=t16[:, gi, :], in0=ta[:], in1=tb[:], op=ALU.add),
                         reads=bufs(ta, tb), writes=bufs(t16))
            S.dma("sp", Tscr[b], t16[:].rearrange("p g m -> p (g m)"), reads=bufs(t16))
            S.dma("sp", VTscr[b], vt16[:].rearrange("p g d r m -> p (g d r m)"), reads=bufs(vt16))
            S.dma("sp", W2scr[b], w216[:].rearrange("p g d r m -> p (g d r m)"), reads=bufs(w216))
        k.arestore(mm)
        return c

    def s5_tiles(tok0, sub, is_lat):
        tiles = []
        for i in range(8):
            I_ = sub * 8 + i
            if not is_lat:
                s_, c0 = I_, 0
            else:
                s_, c0 = I_ // 2, (I_ % 2) * 128
            base = tok0 + 8 * c0 + s_
            tiles.append(((lambda src_, base=base: src_[base:base + 8 * 127 + 1:8, :]), I_ * 128))
        return tiles

    def s5_unit(c, tok0, is_lat):
        C_ = 256 if is_lat else 128
        nseq = 1 if is_lat else 4
        nch = C_ // nseq
        nct = C_ // 128
        Tn = 8 * C_
        U = k.at([128, nct, 16, 8, 16], BF16)
        X = k.at([128, 16, C_], BF16)
        arr = k.at([128, 16, 2, nseq, nch + 1], F32)
        Hb = k.at([128, 16, 2, nseq, nch + 1], BF16)
        Ysb = T(U.t[:].rearrange("p a g s j -> p (a g s j)").rearrange("p (g c) -> p g c", g=16))
        Ysb.b = U.b
        Tw = k.at([128, 16, 128], BF16)
        VTw = k.at([128, 8, 2, 2, 128], BF16)
        W2w = k.at([128, 8, 2, 2, 128], BF16)
        ygst = k.at([128, 2, Tn], BF16)
        uur = k.aring(2, [128, 512], F32)
        ysr = k.aring(2, [128, 512], F32)
        tmps = {eng: [k.at([128, 8, 2, nseq], F32) for _ in range(3)] for eng in ("dve", "pool")}
        GPB = 512 // C_
        bl = {}
        if is_lat:
            for eng in ("dve", "pool"):
                bl[eng] = {"PR": k.at([128, 8, 16], F32), "PI": k.at([128, 8, 16], F32),
                           "AAp": k.at([128, 8, 2, 16], F32), "BBp": k.at([128, 8, 2, 16], F32),
                           "cc": k.at([128, 8, 2, 17], F32),
                           "t": [k.at([128, 8, 8], F32) for _ in range(4)],
                           "l": [k.at([128, 8, 2, 16], F32) for _ in range(3)],
                           "c": [k.at([128, 8, 2], F32) for _ in range(2)],
                           "f": [[k.at([128, 8, 2, 16], F32) for _ in range(2)] for _ in range(2)]}
        for b in range(cfg.get("s5_nb", 8)):
            gs = slice(8 * b, 8 * b + 8)
            S.dma("sp", Tw[:].rearrange("p g m -> p (g m)"), Tscr[b], writes=bufs(Tw))
            S.dma("sp", VTw[:].rearrange("p g d r m -> p (g d r m)"), VTscr[b], writes=bufs(VTw))
            S.dma("sp", W2w[:].rearrange("p g d r m -> p (g d r m)"), W2scr[b], writes=bufs(W2w))
            wu = wring.next()
            S.dma("pool", wu[:, :, 0:256], s5_w_in.rearrange("(k p) n -> p k n", p=128)[:, :, 256 * b:256 * (b + 1)], writes=bufs(wu))
            for ct in range(nct):
                for s2 in range(4):
                    pb = banks.next()
                    for si in range(2):
                        s_ = s2 * 2 + si
                        p0 = s_ * C_ + ct * 128
                        for kk in range(8):
                            S.op("pe", lambda e, pb=pb, kk=kk, si=si, p0=p0, wu=wu: e.matmul(
                                pb[:, si * 256:(si + 1) * 256], lhsT=hT[:, kk, p0:p0 + 128], rhs=wu[:, kk, 0:256],
                                start=(kk == 0), stop=(kk == 7)), reads=bufs(hT, wu), writes=bufs(pb))
                    S.op("act", lambda e, pb=pb, ct=ct, s2=s2: e.activation(
                        out=U[:, ct, :, s2 * 2:s2 * 2 + 2, :], in_=pb[:].rearrange("p (s g j) -> p g s j", s=2, g=16), func=AF.Copy),
                        reads=bufs(pb), writes=bufs(U))
            for ct in range(nct):
                for g4 in range(4):
                    pb = banks.next()
                    for gg in range(4):
                        gi = g4 * 4 + gg
                        S.op("pe", lambda e, pb=pb, gg=gg, gi=gi, ct=ct: e.matmul(
                            pb[:, gg * 128:(gg + 1) * 128], lhsT=U[:, ct, gi, :, :].rearrange("p s j -> p (s j)"), rhs=identb[:], start=True, stop=True),
                            reads=bufs(U, identb), writes=bufs(pb))
                    S.op("act", lambda e, pb=pb, g4=g4, ct=ct: e.activation(
                        out=X[:, g4 * 4:g4 * 4 + 4, ct * 128:(ct + 1) * 128], in_=pb[:].rearrange("p (g c) -> p g c", g=4), func=AF.Copy),
                        reads=bufs(pb), writes=bufs(X))
            if is_lat:
                S.op("act", lambda e, gs=gs: e.activation(
                    out=arr[:, :, :, 0, 0].rearrange("p (g d) r -> p g d r", d=2),
                    in_=c["h0"][:, :, :, gs].rearrange("p d r g -> p g d r"), func=AF.Copy), reads=bufs(c["h0"]), writes=bufs(arr))
            else:
                S.op("pool", lambda e: e.memset(arr[:, :, :, :, 0:1], 0.0), writes=bufs(arr))
            for gl in range(8):
                pGs = [banks.next() for _ in range(nct)]
                for par in range(2):
                    gi = 2 * gl + par
                    rows = slice(par * 64, (par + 1) * 64)
                    for d_ in range(2):
                        if d_ == 0:
                            rhs = X[:, gi, :]
                        else:
                            rhs = X[:, gi, :].rearrange("p (s c) -> p s c", s=nseq)[:, :, ::-1]
                        for r_ in range(2):
                            if is_lat:
                                outp = pGs[d_][rows, r_ * 256:(r_ + 1) * 256]
                                pgb = pGs[d_]
                            else:
                                outp = pGs[0][rows, (d_ * 2 + r_) * 128:(d_ * 2 + r_ + 1) * 128]
                                pgb = pGs[0]
                            S.op("pe", lambda e, outp=outp, gl=gl, d_=d_, r_=r_, par=par, rhs=rhs: e.matmul(
                                outp, lhsT=VTw[:, gl, d_, r_, par * 64:(par + 1) * 64], rhs=rhs, start=True, stop=True),
                                reads=bufs(VTw, X), writes=bufs(pgb))
                for d_ in range(2):
                    if is_lat:
                        src_ = pGs[d_][:].rearrange("p (r s c) -> p r s c", r=2, s=1)
                        pgb = pGs[d_]
                    else:
                        src_ = pGs[0][:, d_ * 256:(d_ + 1) * 256].rearrange("p (r s c) -> p r s c", r=2, s=nseq)
                        pgb = pGs[0]
                    S.op("act", lambda e, src_=src_, gl=gl, d_=d_: e.activation(
                        out=arr[:, gl * 2 + d_, :, :, 1:nch + 1], in_=src_, func=AF.Copy), reads=bufs(pgb), writes=bufs(arr))
            AAb = c["AA"][:, gs, :, :].rearrange("p g d r -> p (g d) r")
            BBb = c["BB"][:, gs, :, :].rearrange("p g d r -> p (g d) r")
            if not is_lat:
                for kq in range(nch):
                    for eng, qs in (("dve", slice(0, 8)), ("pool", slice(8, 16))):
                        tt, p1, p2 = tmps[eng]
                        S.op(eng, lambda e, tt=tt, qs=qs, kq=kq: e.tensor_tensor(
                            out=tt[:], in0=arr[:, qs, :, :, kq], in1=arr[:, qs, :, :, kq + 1], op=ALU.add),
                            reads=bufs(arr), writes=bufs(tt))
                        S.op(eng, lambda e, tt=tt, p1=p1, qs=qs, AAb=AAb: e.tensor_tensor(
                            out=p1[:], in0=tt[:], in1=AAb[:, qs, :].unsqueeze(3).to_broadcast([128, 8, 2, nseq]), op=ALU.mult),
                            reads=bufs(tt, c["AA"]), writes=bufs(p1))
                        S.op(eng, lambda e, tt=tt, p2=p2, qs=qs, BBb=BBb: e.tensor_tensor(
                            out=p2[:], in0=tt[:, :, ::-1, :], in1=BBb[:, qs, :].unsqueeze(3).to_broadcast([128, 8, 2, nseq]), op=ALU.mult),
                            reads=bufs(tt, c["BB"]), writes=bufs(p2))
                        S.op(eng, lambda e, p1=p1, p2=p2, qs=qs, kq=kq: e.tensor_tensor(
                            out=arr[:, qs, :, :, kq + 1], in0=p1[:], in1=p2[:], op=ALU.add), reads=bufs(p1, p2), writes=bufs(arr))
                S.op("act", lambda e: e.activation(out=Hb[:].rearrange("p q r s c -> p (q r s c)"),
                                                   in_=arr[:].rearrange("p q r s c -> p (q r s c)"), func=AF.Copy),
                     reads=bufs(arr), writes=bufs(Hb))
            else:
                NB_, BL_ = 16, 16
                for eng, qs in (("dve", slice(0, 8)), ("pool", slice(8, 16))):
                    B_ = bl[eng]
                    PR, PI, AAp, BBp, cc = B_["PR"], B_["PI"], B_["AAp"], B_["BBp"], B_["cc"]
                    tA, tB, tC, tD = B_["t"]
                    AAh = AAb[:, qs, :]
                    BBh = BBb[:, qs, :]
                    S.op(eng, lambda e, PR=PR, AAh=AAh: e.tensor_copy(out=PR[:, :, 0], in_=AAh[:, :, 0]), reads=bufs(c["AA"]), writes=bufs(PR))
                    S.op(eng, lambda e, PI=PI, BBh=BBh: e.tensor_copy(out=PI[:, :, 0], in_=BBh[:, :, 1]), reads=bufs(c["BB"]), writes=bufs(PI))
                    m_ = 1
                    while m_ < 16:
                        ar = PR[:, :, m_ - 1:m_].to_broadcast([128, 8, m_])
                        ai = PI[:, :, m_ - 1:m_].to_broadcast([128, 8, m_])
                        src_r, src_i = PR[:, :, 0:m_], PI[:, :, 0:m_]
                        dst_r, dst_i = PR[:, :, m_:2 * m_], PI[:, :, m_:2 * m_]
                        ta, tb = tA[:, :, 0:m_], tB[:, :, 0:m_]
                        tc_, td = tC[:, :, 0:m_], tD[:, :, 0:m_]
                        S.op(eng, lambda e, ta=ta, src_r=src_r, ar=ar: e.tensor_tensor(out=ta, in0=src_r, in1=ar, op=ALU.mult), reads=bufs(PR), writes=bufs(tA))
                        S.op(eng, lambda e, tb=tb, src_i=src_i, ai=ai: e.tensor_tensor(out=tb, in0=src_i, in1=ai, op=ALU.mult), reads=bufs(PI), writes=bufs(tB))
                        S.op(eng, lambda e, tc_=tc_, src_r=src_r, ai=ai: e.tensor_tensor(out=tc_, in0=src_r, in1=ai, op=ALU.mult), reads=bufs(PR, PI), writes=bufs(tC))
                        S.op(eng, lambda e, td=td, src_i=src_i, ar=ar: e.tensor_tensor(out=td, in0=src_i, in1=ar, op=ALU.mult), reads=bufs(PR, PI), writes=bufs(tD))
                        S.op(eng, lambda e, dst_r=dst_r, ta=ta, tb=tb: e.tensor_tensor(out=dst_r, in0=ta, in1=tb, op=ALU.subtract), reads=bufs(tA, tB), writes=bufs(PR))
                        S.op(eng, lambda e, dst_i=dst_i, tc_=tc_, td=td: e.tensor_tensor(out=dst_i, in0=tc_, in1=td, op=ALU.add), reads=bufs(tC, tD), writes=bufs(PI))
                        m_ *= 2
                    for r_ in range(2):
                        S.op(eng, lambda e, AAp=AAp, PR=PR, r_=r_: e.tensor_copy(out=AAp[:, :, r_, :], in_=PR[:]), reads=bufs(PR), writes=bufs(AAp))
                    S.op(eng, lambda e, BBp=BBp, PI=PI: e.tensor_scalar(out=BBp[:, :, 0, :], in0=PI[:], scalar1=-1.0, scalar2=None, op0=ALU.mult),
                         reads=bufs(PI), writes=bufs(BBp))
                    S.op(eng, lambda e, BBp=BBp, PI=PI: e.tensor_copy(out=BBp[:, :, 1, :], in_=PI[:]), reads=bufs(PI), writes=bufs(BBp))
                for eng, qs in (("dve", slice(0, 8)), ("pool", slice(8, 16))):
                    B_ = bl[eng]
                    AAp, BBp, cc = B_["AAp"], B_["BBp"], B_["cc"]
                    t3, p13, p23 = B_["l"]
                    AAh = AAb[:, qs, :].unsqueeze(3).to_broadcast([128, 8, 2, NB_])
                    BBh = BBb[:, qs, :].unsqueeze(3).to_broadcast([128, 8, 2, NB_])
                    xv = arr[:, qs, :, 0, 1:257].rearrange("p q r (b i) -> p q r b i", i=BL_)
                    for i_ in range(BL_):
                        if i_ == 0:
                            src_t = xv[:, :, :, :, 0]
                        else:
                            S.op(eng, lambda e, t3=t3, xv=xv, i_=i_: e.tensor_tensor(
                                out=t3[:], in0=xv[:, :, :, :, i_ - 1], in1=xv[:, :, :, :, i_], op=ALU.add), reads=bufs(arr), writes=bufs(t3))
                            src_t = t3[:]
                        rd = bufs(arr) if i_ == 0 else bufs(t3)
                        src_sw = src_t[:, :, ::-1, :]
                        S.op(eng, lambda e, p13=p13, src_t=src_t, AAh=AAh: e.tensor_tensor(out=p13[:], in0=src_t, in1=AAh, op=ALU.mult),
                             reads=rd + bufs(c["AA"]), writes=bufs(p13))
                        S.op(eng, lambda e, p23=p23, src_sw=src_sw, BBh=BBh: e.tensor_tensor(out=p23[:], in0=src_sw, in1=BBh, op=ALU.mult),
                             reads=rd + bufs(c["BB"]), writes=bufs(p23))
                        S.op(eng, lambda e, p13=p13, p23=p23, xv=xv, i_=i_: e.tensor_tensor(
                            out=xv[:, :, :, :, i_], in0=p13[:], in1=p23[:], op=ALU.add), reads=bufs(p13, p23), writes=bufs(arr))
                for eng, qs in (("dve", slice(0, 8)), ("pool", slice(8, 16))):
                    B_ = bl[eng]
                    AAp, BBp, cc = B_["AAp"], B_["BBp"], B_["cc"]
                    c1, c2 = B_["c"]
                    xv = arr[:, qs, :, 0, 1:257].rearrange("p q r (b i) -> p q r b i", i=BL_)
                    S.op(eng, lambda e, cc=cc, qs=qs: e.tensor_copy(out=cc[:, :, :, 0], in_=arr[:, qs, :, 0, 0]), reads=bufs(arr), writes=bufs(cc))
                    for Bk in range(NB_):
                        S.op(eng, lambda e, c1=c1, cc=cc, AAp=AAp, Bk=Bk: e.tensor_tensor(
                            out=c1[:], in0=cc[:, :, :, Bk], in1=AAp[:, :, :, 15], op=ALU.mult), reads=bufs(cc, AAp), writes=bufs(c1))
                        S.op(eng, lambda e, c2=c2, cc=cc, BBp=BBp, Bk=Bk: e.tensor_tensor(
                            out=c2[:], in0=cc[:, :, ::-1, Bk], in1=BBp[:, :, :, 15], op=ALU.mult), reads=bufs(cc, BBp), writes=bufs(c2))
                        S.op(eng, lambda e, c1=c1, c2=c2: e.tensor_tensor(out=c1[:], in0=c1[:], in1=c2[:], op=ALU.add),
                             reads=bufs(c1, c2), writes=bufs(c1))
                        S.op(eng, lambda e, c1=c1, cc=cc, xv=xv, Bk=Bk: e.tensor_tensor(
                            out=cc[:, :, :, Bk + 1], in0=c1[:], in1=xv[:, :, :, Bk, 15], op=ALU.add), reads=bufs(c1, arr), writes=bufs(cc))
                for eng, qs in (("dve", slice(0, 8)), ("pool", slice(8, 16))):
                    B_ = bl[eng]
                    AAp, BBp, cc = B_["AAp"], B_["BBp"], B_["cc"]
                    xv = arr[:, qs, :, 0, 1:257].rearrange("p q r (b i) -> p q r b i", i=BL_)
                    hv = Hb[:, qs, :, 0, 1:257].rearrange("p q r (b i) -> p q r b i", i=BL_)
                    fr = B_["f"]
                    S.op(eng, lambda e, qs=qs: e.tensor_copy(out=Hb[:, qs, :, 0, 0], in_=arr[:, qs, :, 0, 0]), reads=bufs(arr), writes=bufs(Hb))
                    pend = []
                    for i_ in range(BL_ + 1):
                        if i_ < BL_:
                            f1, f2 = fr[i_ % 2]
                            S.op(eng, lambda e, f1=f1, cc=cc, AAp=AAp, i_=i_: e.tensor_tensor(
                                out=f1[:], in0=cc[:, :, :, 0:NB_], in1=AAp[:, :, :, i_:i_ + 1].to_broadcast([128, 8, 2, NB_]), op=ALU.mult),
                                reads=bufs(cc, AAp), writes=bufs(f1))
                            S.op(eng, lambda e, f2=f2, cc=cc, BBp=BBp, i_=i_: e.tensor_tensor(
                                out=f2[:], in0=cc[:, :, ::-1, 0:NB_], in1=BBp[:, :, :, i_:i_ + 1].to_broadcast([128, 8, 2, NB_]), op=ALU.mult),
                                reads=bufs(cc, BBp), writes=bufs(f2))
                        if i_ >= 1:
                            j_ = i_ - 1
                            f1, f2 = fr[j_ % 2]
                            S.op(eng, lambda e, f1=f1, f2=f2: e.tensor_tensor(out=f1[:], in0=f1[:], in1=f2[:], op=ALU.add),
                                 reads=bufs(f1, f2), writes=bufs(f1))
                            S.op(eng, lambda e, f1=f1, xv=xv, hv=hv, j_=j_: e.tensor_tensor(
                                out=hv[:, :, :, :, j_], in0=f1[:], in1=xv[:, :, :, :, j_], op=ALU.add), reads=bufs(f1, arr), writes=bufs(Hb))
            if not is_lat:
                for sq in range(4):
                    for d_ in range(2):
                        for r_ in range(2):
                            S.dma("sp", new_s5[sq, d_, r_].rearrange("(gp two) n -> (two n) gp", two=2)[:, gs],
                                  arr[:, d_:16:2, r_, sq, nch], reads=bufs(arr))
            for g0 in range(0, 16, GPB):
                pb = banks.next()
                for gg in range(GPB):
                    gi = g0 + gg
                    gl, par = gi // 2, gi % 2
                    rows = slice(par * 64, (par + 1) * 64)
                    yreg = pb[:, gg * C_:(gg + 1) * C_]
                    S.op("pe", lambda e, yreg=yreg, gi=gi: e.matmul(yreg, lhsT=Tw[:, gi, :], rhs=X[:, gi, :], start=True, stop=False),
                         reads=bufs(Tw, X), writes=bufs(pb))
                    for d_ in range(2):
                        for r_ in range(2):
                            hsl = Hb[rows, gl * 2 + d_, r_, :, 0:nch]
                            if d_ == 1:
                                hsl = hsl[:, :, ::-1]
                            S.op("pe", lambda e, yreg=yreg, gl=gl, d_=d_, r_=r_, rows=rows, hsl=hsl: e.matmul(
                                yreg, lhsT=W2w[rows, gl, d_, r_, :], rhs=hsl, start=False, stop=(d_ == 1 and r_ == 1)),
                                reads=bufs(W2w, Hb), writes=bufs(pb))
                S.op("act", lambda e, pb=pb, g0=g0: e.activation(
                    out=Ysb[:, g0:g0 + GPB, :].rearrange("p g c -> p (g c)"), in_=pb[:], func=AF.Copy), reads=bufs(pb), writes=bufs(Ysb))
            for blk in range(2):
                for t0 in range(0, 8, GPB):
                    psel = banks.next()
                    puu = banks.next()
                    for tt_ in range(GPB):
                        t = t0 + tt_
                        for g_ in range(8):
                            S.op("pe", lambda e, psel=psel, tt_=tt_, t=t, g_=g_, blk=blk: e.matmul(
                                psel[:, tt_ * C_:(tt_ + 1) * C_], lhsT=c["Wsel"][:, t, 112 - 16 * g_:240 - 16 * g_],
                                rhs=Ysb[:, blk * 8 + g_, :], start=(g_ == 0), stop=(g_ == 7)),
                                reads=bufs(c["Wsel"], Ysb), writes=bufs(psel))
                        for kk in range(8):
                            S.op("pe", lambda e, puu=puu, tt_=tt_, t=t, kk=kk, blk=blk, wu=wu: e.matmul(
                                puu[:, tt_ * C_:(tt_ + 1) * C_], lhsT=wu[:, kk, blk * 128:(blk + 1) * 128],
                                rhs=hT[:, kk, t * C_:(t + 1) * C_], start=(kk == 0), stop=(kk == 7)),
                                reads=bufs(wu, hT), writes=bufs(puu))
                    uus = uur.next()
                    ysm = ysr.next()
                    S.op("act", lambda e, puu=puu, uus=uus: e.activation(out=uus[:], in_=puu[:], func=AF.Copy),
                         reads=bufs(puu), writes=bufs(uus))
                    S.op("dve", lambda e, psel=psel, uus=uus, ysm=ysm, b=b, blk=blk: e.scalar_tensor_tensor(
                        out=ysm[:], in0=uus[:], scalar=c["dT"][:, 2 * b + blk:2 * b + blk + 1], in1=psel[:], op0=ALU.mult, op1=ALU.add),
                        reads=bufs(psel, uus, c["dT"]), writes=bufs(ysm))
                    S.op("act", lambda e, ysm=ysm, blk=blk, t0=t0: e.activation(
                        out=ygst[:, blk, t0 * C_:t0 * C_ + 512], in_=ysm[:], func=AF.Gelu), reads=bufs(ysm), writes=bufs(ygst))
            S.dma("sp", yscr[2 * b:2 * b + 2, :, tok0:tok0 + Tn].rearrange("b p t -> p b t"), ygst[:], reads=bufs(ygst))

    def s5_glu(c, tok0, sub, yT):
        ygT = k.at([128, 16, 1024], BF16)
        S.dma("sp", ygT[:], yscr[:, :, tok0 + sub * 1024:tok0 + (sub + 1) * 1024].rearrange("b p t -> p b t"), writes=bufs(ygT))
        wgr = k.aring(2, [128, 16, 128], BF16)
        sgr = k.aring(2, [128, 512], F32)
        szr = k.aring(2, [128, 512], F32)
        for blk in range(16):
            wg = wgr.next()
            S.dma("pool", wg[:], s5_w_glu.rearrange("(k p) n -> p k n", p=128)[:, :, blk * 128:(blk + 1) * 128], writes=bufs(wg))
            if blk % 4 == 0:
                wz = load_w(s5_w_in, E + blk * 128, 512)
            co = (blk % 4) * 128
            for q in range(2):
                p0 = sub * 1024 + q * 512
                pg_ = banks.next()
                for kk in range(16):
                    S.op("pe", lambda e, pg_=pg_, kk=kk, wg=wg, q=q: e.matmul(
                        pg_[:], lhsT=wg[:, kk, :], rhs=ygT[:, kk, q * 512:(q + 1) * 512], start=(kk == 0), stop=(kk == 15)),
                        reads=bufs(wg, ygT), writes=bufs(pg_))
                sg = sgr.next()
                S.op("act", lambda e, pg_=pg_, sg=sg, blk=blk: e.activation(
                    out=sg[:], in_=pg_[:], func=AF.Sigmoid, bias=c["bgT"][:, blk:blk + 1]), reads=bufs(pg_, c["bgT"]), writes=bufs(sg))
                pz = banks.next()
                for kk in range(8):
                    S.op("pe", lambda e, pz=pz, kk=kk, wz=wz, co=co, p0=p0: e.matmul(
                        pz[:], lhsT=wz[:, kk, co:co + 128], rhs=hT[:, kk, p0:p0 + 512], start=(kk == 0), stop=(kk == 7)),
                        reads=bufs(wz, hT), writes=bufs(pz))
                sz = szr.next()
                S.op("act", lambda e, pz=pz, sz=sz: e.activation(out=sz[:], in_=pz[:], func=AF.Silu), reads=bufs(pz), writes=bufs(sz))
                S.op("pool", lambda e, sg=sg, blk=blk, q=q: e.tensor_tensor(
                    out=sg[:], in0=sg[:], in1=ygT[:, blk, q * 512:(q + 1) * 512], op=ALU.mult), reads=bufs(sg, ygT), writes=bufs(sg))
                S.op("dve", lambda e, sg=sg, sz=sz, blk=blk, p0=p0: e.tensor_tensor(
                    out=yT[:, blk, p0:p0 + 512], in0=sg[:], in1=sz[:], op=ALU.mult), reads=bufs(sg, sz), writes=bufs(yT))

    def std_tiles(tok0, n):
        return [(rows_std(tok0 + i * 128), i * 128) for i in range(n)]

    units = [(0, 8, 0), (1024, 8, 1), (2048, 8, 1)]
    src = cfg.get("src", None) and inp("xsrc", [NTOK, D]) or xin
    for li in layers:
        last = final and (li == layers[-1])
        dst = xres
        k.areset()
        phase_a(li)
        if li == 1:
            L["yT"] = k.at([128, 16, 1024], BF16)
            c = gmlp_consts()
            for (tok0, nt, cond) in units:
                tiles = std_tiles(tok0, nt)
                phase_b(src, tiles, cond)
                gmlp_unit(c, nt)
                load_wout(li)
                phase_d(src, dst, tiles, cond, last)
        if li == 0:
            c = ssd_consts()
            m0 = k.amark()
            for (tok0, nt, nseq, cond) in ((0, 8, 4, 0), (1024, 16, 1, 1)):
                tiles = std_tiles(tok0, nt)
                phase_b(src, tiles, cond)
                rstd = ssd_unit(c, tok0, nt, nseq, cond == 1)
                S.op("act", lambda e, rstd=rstd, nt=nt: e.activation(out=rstd_keep[:, 0:nt], in_=rstd[:], func=AF.Copy),
                     reads=bufs(rstd), writes=bufs(rstd_keep))
                k.arestore(m0)
                load_wout(li)
                phase_d(src, dst, tiles, cond, last, scale_t=lambda i: (rstd_keep[:, i:i + 1], rstd_keep.b), ytok0=tok0)
                S.barrier()
        if li == 2:
            c = s5_prep()
            m0 = k.amark()
            for (tok0, is_lat, cond) in ((0, False, 0), (1024, True, 1)):
                nsub = 2 if is_lat else 1
                tiles = []
                for sub in range(nsub):
                    tiles += s5_tiles(tok0, sub, is_lat)
                phase_b(src, tiles, cond)
                s5_unit(c, tok0, is_lat)
                k.arestore(m0)
                L["yT"] = k.at([128, 16, 1024 * nsub], BF16)
                m1 = k.amark()
                for sub in range(nsub):
                    s5_glu(c, tok0, sub, L["yT"])
                    k.arestore(m1)
                load_wout(li)
                phase_d(src, dst, tiles, cond, last)
                k.arestore(m0)
        if li == 3:
            L["yT"] = k.at([128, 16, 2048], BF16)
            tiles = std_tiles(0, 8)
            phase_b(src, tiles, 0)
            m_ = k.amark()
            if not cfg.get("skip_ctx"):
                nat_ctx_unit()
            k.arestore(m_)
            load_wout(li)
            phase_d(src, dst, tiles, 0, last)
            S.barrier()
            tiles = std_tiles(1024, 16)
            phase_b(src, tiles, 1)
            m_ = k.amark()
            nat_lat_unit()
            if not cfg.get("skip_d"):
                k.arestore(m_)
            load_wout(li)
            phase_d(src, dst, tiles, 1, last)
        src = xres
    if not final and not cfg.get("skip_d"):
        S.barrier()
        xring = k.aring(3, [128, D], F32)
        for i in range(NTOK // 128):
            xt = xring.next()
            S.dma("sp", xt[:], xres[i * 128:(i + 1) * 128, :], writes=bufs(xt))
            S.dma("sp", y_out[i * 128:(i + 1) * 128, :], xt[:], reads=bufs(xt))
    S.emit(es)
    return nc, es


def host_inputs(inputs, core):
    f = np.ascontiguousarray
    m = {}
    m["xin"] = f(np.concatenate([inputs["x_prompt"][4 * core:4 * core + 4].reshape(NP_TOK, D),
                                 inputs["x_sample"][core % 2]], axis=0))
    m["cvec"] = f(np.stack([inputs["c_ctx"], inputs["c"][core % 2]], axis=0))
    for nm in ["norm_g", "w_mod", "b_mod", "w_out", "final_g"]:
        m[nm] = f(inputs[nm])
    m["mlp_w_in"] = f(inputs["mlp_w_in"][0])
    m["mlp_ln_g"] = f(inputs["mlp_ln_g"][0])
    m["mlp_ln_b"] = f(inputs["mlp_ln_b"][0])
    m["mlp_w_sT"] = f(np.transpose(inputs["mlp_w_s"][0], (0, 2, 1)))
    m["mlp_b_s"] = f(inputs["mlp_b_s"][0])
    m["ssd_w_in"] = f(inputs["ssd_w_in"][0])
    m["ssd_conv_w"] = f(inputs["ssd_conv_w"][0])
    m["ssd_conv_b"] = f(inputs["ssd_conv_b"][0])
    m["ssd_dt_bias"] = f(inputs["ssd_dt_bias"][0].reshape(64))
    m["ssd_a_log"] = f(inputs["ssd_a_log"][0].reshape(64))
    m["ssd_d"] = f(inputs["ssd_d"][0])
    m["ssd_norm_g"] = f(inputs["ssd_norm_g"][0])
    m["state_ssd"] = f(inputs["state_ssd"][core % 2, 0])
    m["s5_w_in"] = f(inputs["s5_w_in"][0])

    def pl(a):
        sh = a.shape[:-2]
        a = a.reshape(sh + (64, 2, 64))
        return np.moveaxis(a, -3, -1).reshape(sh + (128, 64))
    m["s5_lam"] = f(np.stack([pl(inputs["s5_lam_re"][0]), pl(inputs["s5_lam_im"][0])], 0))
    m["s5_lstep"] = f(pl(np.broadcast_to(inputs["s5_log_step"][0][:, :, None], (2, 128, 64))))

    def plj(a):
        a = a.reshape(2, 64, 2, 64, 16)
        return np.transpose(a, (0, 2, 3, 1, 4)).reshape(2, 128, 64, 16)
    m["s5_B"] = f(np.stack([plj(inputs["s5_b_re"][0]), plj(inputs["s5_b_im"][0])], 0))
    m["s5_C"] = f(np.stack([plj(np.transpose(inputs["s5_c_re"][0], (0, 1, 3, 2))),
                            plj(np.transpose(inputs["s5_c_im"][0], (0, 1, 3, 2)))], 0))
    m["s5_h0"] = f(pl(inputs["state_s5"][core % 2, 0]))
    m["s5_d"] = f(inputs["s5_d"][0])
    m["s5_w_glu"] = f(inputs["s5_w_glu"][0])
    m["s5_b_glu"] = f(inputs["s5_b_glu"][0])
    m["nat_w_in"] = f(inputs["nat_w_in"][0])
    m["rpbg"] = rpb_gather(inputs["nat_rpb"][0])
    m["natmask"] = nat_masks()
    m["cache_k"] = f(inputs["cache_k"][core % 2, 0])
    m["cache_v"] = f(inputs["cache_v"][core % 2, 0])
    return m


def rpb_gather(rpb):
    qc = np.arange(64)[:, None]
    kc = np.arange(64)[None, :]
    ci = np.clip(kc - qc + 15, 0, 30)
    out = np.zeros((32, 128, 16, 64), np.float32)
    g = rpb[:, :, ci]
    g = np.transpose(g, (0, 2, 1, 3))
    out[:, 0:64, 0:15, :] = g
    out[:, 64:128, 1:16, :] = g
    return np.ascontiguousarray(out.reshape(32, 128, 1024))


def nat_masks():
    NEG = -30000.0 * 8.0
    qc = np.arange(64)
    cs = np.clip(qc - 8, 0, 48)
    kc = np.arange(64)
    col_ok = (kc[None, :] >= cs[:, None]) & (kc[None, :] < cs[:, None] + 16)
    m = np.zeros((3, 128, 9, 64), np.float32)
    colm = np.where(col_ok, 0.0, NEG).astype(np.float32)
    m[:, 0:64] += colm[None, :, None, :]
    m[:, 64:128] += colm[None, :, None, :]
    m[0, 0:64, 8, :] = NEG
    m[0, 64:128, 0, :] = NEG
    m[1, :, 8, :] = NEG
    return np.ascontiguousarray(m.reshape(3, 128, 576))


def kernel(**inputs):
    inputs = {k_: np.asarray(v) for k_, v in inputs.items()}
    nc, es = build({})
    with es:
        in_maps = [host_inputs(inputs, c) for c in range(8)]
        res = run_bass_kernel_spmd(nc, in_maps, core_ids=list(range(8)))
    r = res.results
    y_prompt = np.concatenate([r[c]["y_out"][:NP_TOK].reshape(4, 256, D) for c in range(8)], axis=0)
    y_sample = np.stack([r[c]["y_out"][NP_TOK:] for c in range(2)], axis=0)
    new_ssd = np.concatenate([r[c]["new_ssd"] for c in range(8)], axis=0)[:, None]
    new_s5 = np.concatenate([r[c]["new_s5"] for c in range(8)], axis=0)[:, None]
    new_k = np.concatenate([r[c]["new_k"] for c in range(8)], axis=0)[:, None]
    new_v = np.concatenate([r[c]["new_v"] for c in range(8)], axis=0)[:, None]
    return (y_prompt.astype(np.float32), y_sample.astype(np.float32), np.ascontiguousarray(new_ssd, dtype=np.float32),
            np.ascontiguousarray(new_s5, dtype=np.float32), np.ascontiguousarray(new_k, dtype=np.float32),
            np.ascontiguousarray(new_v, dtype=np.float32))
```

```python
import numpy as np
from contextlib import ExitStack
import concourse.bass as bass
import concourse.mybir as mybir
from concourse.bass_utils import run_bass_kernel_spmd

F32 = mybir.dt.float32
BF16 = mybir.dt.bfloat16
AF = mybir.ActivationFunctionType
ALU = mybir.AluOpType
AX = mybir.AxisListType

D = 1024
E = 2048
NP_TOK = 1024
NS_TOK = 2048
NTOK = NP_TOK + NS_TOK
EPS = 1e-6
COMPUTE = ("pe", "act", "dve", "pool")
NDMASEM = 12
SAME_ENGINE_SYNC = True


class Buf:
    __slots__ = ("lw", "rd")

    def __init__(self):
        self.lw = None
        self.rd = {}


class Sched:
    def __init__(self, nc):
        self.nc = nc
        self.ops = {e: [] for e in COMPUTE + ("sp",)}
        self.cnt = {e: 0 for e in COMPUTE}
        self.seen = {e: {} for e in COMPUTE + ("sp",)}
        self.dma_slot = {}
        self.dma_val = {}
        self.sems = {}
        self.refd = {e: set() for e in COMPUTE}

    def _deps(self, eng, reads, writes):
        deps = {}

        def add(tok):
            if tok is None:
                return
            k, v = tok
            if deps.get(k, 0) < v:
                deps[k] = v

        for r in reads:
            add(r.lw)
        for w in writes:
            add(w.lw)
            for k, v in w.rd.items():
                add((k, v))
        out = []
        seen = self.seen[eng]
        for k, v in deps.items():
            if k == eng and (eng == "pe" or not SAME_ENGINE_SYNC):
                continue
            if seen.get(k, 0) >= v:
                continue
            seen[k] = v
            out.append((k, v))
            if isinstance(k, str):
                self.refd[k].add(v)
        return out

    def _mark(self, tok, reads, writes):
        k, v = tok
        for r in reads:
            if r.rd.get(k, 0) < v:
                r.rd[k] = v
        for w in writes:
            w.lw = tok
            w.rd = {}

    def op(self, eng, fn, reads=(), writes=()):
        waits = self._deps(eng, reads, writes)
        self.cnt[eng] += 1
        tok = (eng, self.cnt[eng])
        self.ops[eng].append((waits, fn, tok, 1))
        self._mark(tok, reads, writes)

    def dma(self, q, out, in_, reads=(), writes=()):
        slot = self.dma_slot.get(q, 0)
        self.dma_slot[q] = (slot + 1) % NDMASEM
        key = ("dma", q, slot)
        prev = self.dma_val.get(key, 0)
        waits = self._deps(q, reads, writes)
        if prev > 0 and self.seen[q].get(key, 0) < prev:
            self.seen[q][key] = prev
            waits.append((key, prev))
        val = prev + 16
        self.dma_val[key] = val
        tok = (key, val)

        def fn(e, out=out, in_=in_):
            return e.dma_start(out=out, in_=in_, allow_slow_non_contiguous=True)

        self.ops[q].append((waits, fn, tok, 16))
        self._mark(tok, reads, writes)

    def barrier(self):
        targets = [(e, self.cnt[e]) for e in COMPUTE if self.cnt[e] > 0]
        targets += [(key, v) for key, v in self.dma_val.items()]
        for eng in COMPUTE + ("sp",):
            waits = []
            for key, v in targets:
                if key == eng:
                    continue
                if self.seen[eng].get(key, 0) < v:
                    self.seen[eng][key] = v
                    waits.append((key, v))
                    if isinstance(key, str):
                        self.refd[key].add(v)
            if waits:
                self.ops[eng].append((waits, None, None, 0))

    def emit(self, es, final_wait_engine="sp"):
        nc = self.nc
        keys = list(COMPUTE)
        for q in self.dma_slot:
            for s in range(NDMASEM):
                if ("dma", q, s) in self.dma_val:
                    keys.append(("dma", q, s))
        for k in keys:
            nm = k if isinstance(k, str) else "d_%s_%d" % (k[1], k[2])
            self.sems[k] = es.enter_context(nc.semaphore("s_" + nm))
        fin = []
        for k in keys:
            v = self.cnt[k] if isinstance(k, str) else self.dma_val[k]
            if v > 0 and k != final_wait_engine:
                fin.append((k, v))
                if isinstance(k, str):
                    self.refd[k].add(v)
        rank = {}
        for e_ in COMPUTE:
            r_ = {}
            for n_, idx in enumerate(sorted(self.refd[e_])):
                r_[idx] = n_ + 1
            rank[e_] = r_

        def semval(k, v):
            return rank[k][v] if isinstance(k, str) else v
        block = es.enter_context(nc.Block())

        def run(e, name):
            for waits, fn, tok, inc in self.ops[name]:
                if fn is None:
                    for k, v in waits:
                        e.wait_ge(self.sems[k], semval(k, v))
                    continue
                NW = 1
                for k, v in waits[NW:]:
                    e.wait_ge(self.sems[k], semval(k, v))
                ins = fn(e)
                for k, v in waits[:NW]:
                    ins._wait_ge(self.sems[k], semval(k, v))
                if not isinstance(tok[0], str) or tok[1] in self.refd[tok[0]]:
                    ins.then_inc(self.sems[tok[0]], inc)
            if name == final_wait_engine:
                for k, v in fin:
                    e.wait_ge(self.sems[k], semval(k, v))

        @block.tensor
        def _(e):
            run(e, "pe")

        @block.scalar
        def _(e):
            run(e, "act")

        @block.vector
        def _(e):
            run(e, "dve")

        @block.gpsimd
        def _(e):
            run(e, "pool")

        @block.sync
        def _(e):
            run(e, "sp")


class T:
    __slots__ = ("t", "b")

    def __init__(self, t):
        self.t = t
        self.b = Buf()

    def __getitem__(self, k):
        return self.t[k]


class Ring:
    def __init__(self, tiles):
        self.tiles = tiles
        self.i = 0

    def next(self):
        t = self.tiles[self.i]
        self.i = (self.i + 1) % len(self.tiles)
        return t


class K:
    def __init__(self, nc, es):
        self.nc = nc
        self.es = es
        self.S = Sched(nc)
        self.n = 0

    def sb(self, shape, dt, name=None):
        self.n += 1
        return T(self.es.enter_context(self.nc.sbuf_tensor(name or "sb%d" % self.n, list(shape), dt)))

    def ring(self, n, shape, dt):
        return Ring([self.sb(shape, dt) for _ in range(n)])

    def psb(self, shape, dt):
        self.n += 1
        return T(self.es.enter_context(self.nc.psum_tensor("ps%d" % self.n, list(shape), dt)))

    def init_arena(self, nbytes):
        self.arena = self.es.enter_context(self.nc.sbuf_tensor("arena", [128, nbytes // 2], BF16))
        self.asize = nbytes
        self.aoff = 0
        self.alog = []

    def areset(self):
        self.S.barrier()
        self.aoff = 0

    def at(self, shape, dt):
        esz = 4 if dt == F32 else 2
        n = 1
        for d_ in shape[1:]:
            n *= d_
        nb = (n * esz + 63) // 64 * 64
        assert self.aoff + nb <= self.asize, ("arena overflow", self.aoff, nb, self.asize)
        ap = self.arena[0:shape[0], self.aoff // 2:(self.aoff + n * esz) // 2]
        if dt == F32:
            ap = ap.bitcast(F32)
        if len(shape) > 2:
            names = ["d%d" % i for i in range(len(shape) - 1)]
            kw = {names[i]: shape[i + 1] for i in range(len(names) - 1)}
            ap = ap.rearrange("p (%s) -> p %s" % (" ".join(names), " ".join(names)), **kw)
        self.alog.append((self.aoff, tuple(shape), dt))
        self.aoff += nb
        return T(ap)

    def amark(self):
        return self.aoff

    def arestore(self, m):
        self.S.barrier()
        self.aoff = m

    def aring(self, n, shape, dt):
        return Ring([self.at(shape, dt) for _ in range(n)])

    def dram(self, name, shape, dt, kind="Internal"):
        return self.nc.dram_tensor(name, list(shape), dt, kind=kind).ap()


def bufs(*ts):
    return [t.b for t in ts]


def build(cfg):
    layers = cfg.get("layers", [0, 1, 2, 3])
    final = cfg.get("final", True)
    nc = bass.Bass("TRN2", target_bir_lowering=False)
    es = ExitStack()
    k = K(nc, es)
    S = k.S
    I = {}

    def inp(name, shape):
        I[name] = k.dram(name, shape, F32, kind="ExternalInput")
        return I[name]

    xin = inp("xin", [NTOK, D])
    cvec = inp("cvec", [2, D])
    norm_g = inp("norm_g", [4, D])
    w_mod = inp("w_mod", [4, D, 3 * D])
    b_mod = inp("b_mod", [4, 3 * D])
    w_out = inp("w_out", [4, E, D])
    final_g = inp("final_g", [D])
    mlp_w_in = inp("mlp_w_in", [D, 3 * E])
    mlp_ln_g = inp("mlp_ln_g", [E])
    mlp_ln_b = inp("mlp_ln_b", [E])
    mlp_w_sT = inp("mlp_w_sT", [8, 128, 128])
    mlp_b_s = inp("mlp_b_s", [8, 128])
    ssd_w_in = inp("ssd_w_in", [D, 6208])
    ssd_conv_w = inp("ssd_conv_w", [5, 4096])
    ssd_conv_b = inp("ssd_conv_b", [4096])
    ssd_dt_bias = inp("ssd_dt_bias", [64])
    ssd_a_log = inp("ssd_a_log", [64])
    ssd_d = inp("ssd_d", [32])
    ssd_norm_g = inp("ssd_norm_g", [E])
    state_ssd = inp("state_ssd", [2, 32, 64, 128])
    new_ssd = k.dram("new_ssd", [4, 2, 32, 64, 128], F32, kind="ExternalOutput")
    yscr = k.dram("yscr", [16, 128, NTOK], BF16)
    s5_w_in = inp("s5_w_in", [D, 2 * E])
    s5_lam = inp("s5_lam", [2, 2, 128, 64])
    s5_lstep = inp("s5_lstep", [2, 128, 64])
    s5_B = inp("s5_B", [2, 2, 128, 64, 16])
    s5_C = inp("s5_C", [2, 2, 128, 64, 16])
    s5_h0 = inp("s5_h0", [2, 2, 128, 64])
    s5_d = inp("s5_d", [E])
    s5_w_glu = inp("s5_w_glu", [E, E])
    s5_b_glu = inp("s5_b_glu", [E])
    new_s5 = k.dram("new_s5", [4, 2, 2, 128, 64], F32, kind="ExternalOutput")
    Tscr = k.dram("Tscr", [8, 128, 16 * 128], BF16)
    VTscr = k.dram("VTscr", [8, 128, 8 * 4 * 128], BF16)
    W2scr = k.dram("W2scr", [8, 128, 8 * 4 * 128], BF16)
    nat_w_in = inp("nat_w_in", [D, 4 * E])
    rpbg = inp("rpbg", [32, 128, 1024])
    natmask = inp("natmask", [3, 128, 576])
    cache_k = inp("cache_k", [32, 256, 64])
    cache_v = inp("cache_v", [32, 256, 64])
    new_k = k.dram("new_k", [4, 32, 256, 64], F32, kind="ExternalOutput")
    new_v = k.dram("new_v", [4, 32, 256, 64], F32, kind="ExternalOutput")
    y_out = k.dram("y_out", [NTOK, D], F32, kind="ExternalOutput")
    xres = k.dram("xres", [NTOK, D], F32)
    dma_done = Buf()

    identf = k.sb([128, 128], F32)
    identb = k.sb([128, 128], BF16)
    onesf = k.sb([128, 128], F32)
    S.op("pool", lambda e: e.memset(identf[:], 0.0), writes=bufs(identf))
    S.op("pool", lambda e: e.affine_select(out=identf[:], in_=identf[:], compare_op=ALU.not_equal, fill=1.0,
                                           base=0, pattern=[[-1, 128]], channel_multiplier=1),
         reads=bufs(identf), writes=bufs(identf))
    S.op("dve", lambda e: e.tensor_copy(out=identb[:], in_=identf[:]), reads=bufs(identf), writes=bufs(identb))
    S.op("pool", lambda e: e.memset(onesf[:], 1.0), writes=bufs(onesf))

    banks = Ring([k.psb([128, 512], F32) for _ in range(8)])

    hT = k.sb([128, 8, 2048], BF16, "hT")
    wo = T(hT.t)
    wo.b = hT.b
    wo_view = hT.t[:].rearrange("p k t -> p (k t)").rearrange("p (k n) -> p k n", k=16)
    wring = k.ring(3, [128, 8, 512], BF16)
    junk = k.sb([128, D], BF16)
    small = k.ring(8, [128, 8], F32)
    rstd_keep = k.sb([128, 16], F32)
    k.init_arena(136 * 1024)
    L = {}

    cf = k.sb([128, 8, 2], F32)
    cb = k.sb([128, 8, 2], BF16)
    for c_ in range(2):
        S.dma("sp", cf[:, :, c_], cvec[c_].rearrange("(k p) -> p k", p=128), writes=bufs(cf))
    S.op("act", lambda e: e.activation(out=cb[:], in_=cf[:], func=AF.Silu), reads=bufs(cf), writes=bufs(cb))

    modT = k.sb([128, 16, 2], F32)
    bmodT = k.sb([128, 16], F32)
    ngT = k.sb([128, 8], F32)
    Asc = k.sb([128, 8, 2], F32)
    gate_bc = [k.sb([128, D], F32), k.sb([128, D], F32)]
    sel = [k.sb([2, 128], F32), k.sb([2, 128], F32)]
    for c in range(2):
        S.op("pool", lambda e, c=c: e.memset(sel[c][:], 0.0), writes=bufs(sel[c]))
        S.op("pool", lambda e, c=c: e.affine_select(out=sel[c][:], in_=sel[c][:], compare_op=ALU.not_equal, fill=1.0,
                                                     base=-c, pattern=[[0, 128]], channel_multiplier=1),
             reads=bufs(sel[c]), writes=bufs(sel[c]))

    def load_w(wap, c0, n, q="pool"):
        wt = wring.next()
        S.dma(q, wt[:, :, 0:n], wap.rearrange("(k p) n -> p k n", p=128)[:, :, c0:c0 + n], writes=bufs(wt))
        return wt

    def phase_a(li):
        ma_ = k.amark()
        gate2 = k.at([2, D], F32)
        bgate2 = k.at([2, D], F32)
        S.dma("sp", bmodT[:], b_mod[li, 0:2 * D].rearrange("(c p) -> p c", p=128), writes=bufs(bmodT))
        S.dma("sp", ngT[:], norm_g[li].rearrange("(c p) -> p c", p=128), writes=bufs(ngT))
        S.dma("sp", bgate2[:], b_mod[li, 2 * D:3 * D].partition_broadcast(2), writes=bufs(bgate2))
        for blk in range(4):
            wt = load_w(w_mod[li], blk * 512, 512)
            for cc in range(4):
                ch = blk * 4 + cc
                pb = banks.next()
                for kk in range(8):
                    S.op("pe", lambda e, pb=pb, wt=wt, cc=cc, kk=kk: e.matmul(
                        pb[:, 0:2], lhsT=wt[:, kk, cc * 128:(cc + 1) * 128], rhs=cb[:, kk, :],
                        start=(kk == 0), stop=(kk == 7)), reads=bufs(wt, cb), writes=bufs(pb))
                S.op("dve", lambda e, pb=pb, ch=ch: e.tensor_scalar(
                    out=modT[:, ch, :], in0=pb[:, 0:2], scalar1=bmodT[:, ch:ch + 1], scalar2=None, op0=ALU.add),
                    reads=bufs(pb, bmodT), writes=bufs(modT))
        S.op("dve", lambda e: e.tensor_scalar(out=Asc[:], in0=modT[:, 8:16, :], scalar1=1.0, scalar2=None, op0=ALU.add),
             reads=bufs(modT), writes=bufs(Asc))
        S.op("dve", lambda e: e.tensor_tensor(out=Asc[:], in0=Asc[:], in1=ngT[:].unsqueeze(2).to_broadcast([128, 8, 2]),
                                              op=ALU.mult), reads=bufs(Asc, ngT), writes=bufs(Asc))
        for blk in range(2):
            wt = load_w(w_mod[li], 2 * D + blk * 512, 512)
            pb = banks.next()
            for kk in range(8):
                S.op("pe", lambda e, pb=pb, wt=wt, kk=kk: e.matmul(
                    pb[0:2, :], lhsT=cb[:, kk, :], rhs=wt[:, kk, :], start=(kk == 0), stop=(kk == 7)),
                    reads=bufs(wt, cb), writes=bufs(pb))
            S.op("dve", lambda e, pb=pb, blk=blk: e.tensor_tensor(
                out=gate2[:, blk * 512:(blk + 1) * 512], in0=pb[0:2, :], in1=bgate2[:, blk * 512:(blk + 1) * 512],
                op=ALU.add), reads=bufs(pb, bgate2), writes=bufs(gate2))
        for c in range(2):
            for blk in range(2):
                pb = banks.next()
                S.op("pe", lambda e, pb=pb, c=c, blk=blk: e.matmul(
                    pb[:], lhsT=sel[c][:], rhs=gate2[:, blk * 512:(blk + 1) * 512], start=True, stop=True),
                    reads=bufs(sel[c], gate2), writes=bufs(pb))
                S.op("act", lambda e, pb=pb, c=c, blk=blk: e.activation(
                    out=gate_bc[c][:, blk * 512:(blk + 1) * 512], in_=pb[:], func=AF.Copy),
                    reads=bufs(pb), writes=bufs(gate_bc[c]))
        k.arestore(ma_)

    def rows_std(tok0):
        return lambda src: src[tok0:tok0 + 128, :]

    def rms_stats(xt):
        st = small.next()
        S.op("act", lambda e: e.activation(out=junk[:], in_=xt[:], func=AF.Square, accum_out=st[:, 0:1]),
             reads=bufs(xt), writes=bufs(junk, st))
        S.op("dve", lambda e: e.tensor_scalar(out=st[:, 0:1], in0=st[:, 0:1], scalar1=1.0 / D, scalar2=EPS,
                                              op0=ALU.mult, op1=ALU.add), reads=bufs(st), writes=bufs(st))
        S.op("act", lambda e: e.activation(out=st[:, 0:1], in_=st[:, 0:1], func=AF.Sqrt), reads=bufs(st), writes=bufs(st))
        S.op("dve", lambda e: e.reciprocal(out=st[:, 0:1], in_=st[:, 0:1]), reads=bufs(st), writes=bufs(st))
        return st

    def phase_b(src, tiles, cond):
        m_ = k.amark()
        xring = k.aring(3, [128, D], F32)
        xnring = k.aring(2, [128, D], BF16)

        def load(i):
            xt = xring.next()
            S.dma("sp", xt[:], tiles[i][0](src), writes=bufs(xt))
            return xt
        nxt = load(0)
        for i in range(len(tiles)):
            xt = nxt
            if i + 1 < len(tiles):
                nxt = load(i + 1)
            col0 = tiles[i][1]
            st = rms_stats(xt)
            xn = xnring.next()
            S.op("dve", lambda e, xn=xn, xt=xt, st=st: e.tensor_scalar(out=xn[:], in0=xt[:], scalar1=st[:, 0:1],
                                                                   scalar2=None, op0=ALU.mult),
                 reads=bufs(xt, st), writes=bufs(xn))
            pb = banks.next()
            pv = pb[:].bitcast(BF16).rearrange("p (k t) -> p k t", k=8)
            for kk in range(8):
                S.op("pe", lambda e, pv=pv, xn=xn, kk=kk: e.transpose(out=pv[:, kk, :], in_=xn[:, kk * 128:(kk + 1) * 128],
                                                                    identity=identb[:]),
                     reads=bufs(xn, identb), writes=bufs(pb))
            for kk in range(8):
                S.op("act", lambda e, pv=pv, kk=kk, col0=col0: e.activation(
                    out=hT[:, kk, col0:col0 + 128], in_=pv[:, kk, :], func=AF.Identity,
                    scale=Asc[:, kk, cond:cond + 1], bias=modT[:, kk, cond:cond + 1]),
                    reads=bufs(pb, Asc, modT), writes=bufs(hT))
        k.arestore(m_)


    def load_wout(li):
        for h in range(2):
            S.dma("pool", wo_view[:, h * 8:(h + 1) * 8, :],
                  w_out[li].rearrange("(k p) n -> p k n", p=128)[:, h * 8:(h + 1) * 8, :], writes=bufs(wo))

    def phase_d(src, dst, tiles, cond, last, scale_t=None, ytok0=None):
        if cfg.get("skip_d"):
            return
        m_ = k.amark()
        xring = k.aring(3, [128, D], F32)
        tring = k.aring(2, [128, D], F32)
        if last:
            fg_bc = k.at([128, D], F32)
            S.dma("sp", fg_bc[:], final_g.partition_broadcast(128), writes=bufs(fg_bc))
        if ytok0 is None:
            yT = L["yT"]
        else:
            yring = k.aring(2, [128, 16, 512], BF16)
            yT = None

        def load(i):
            xt = xring.next()
            S.dma("sp", xt[:], tiles[i][0](src), writes=bufs(xt))
            return xt
        nxt = load(0)
        for i in range(len(tiles)):
            xt = nxt
            if i + 1 < len(tiles):
                nxt = load(i + 1)
            col0 = tiles[i][1]
            if ytok0 is not None:
                if i % 4 == 0:
                    yT = yring.next()
                    S.dma("sp", yT[:], yscr[:, :, ytok0 + tiles[i][1]:ytok0 + tiles[i][1] + 512].rearrange("b p t -> p b t"),
                          writes=bufs(yT))
                col0 = (i % 4) * 128
            tt = tring.next()
            for h in range(2):
                pb = banks.next()
                for kk in range(16):
                    S.op("pe", lambda e, pb=pb, kk=kk, h=h, col0=col0, yT=yT: e.matmul(
                        pb[:], lhsT=yT[:, kk, col0:col0 + 128], rhs=wo_view[:, kk, h * 512:(h + 1) * 512],
                        start=(kk == 0), stop=(kk == 15)), reads=bufs(yT, wo), writes=bufs(pb))
                if scale_t is None:
                    S.op("dve", lambda e, pb=pb, tt=tt, h=h: e.tensor_tensor(
                        out=tt[:, h * 512:(h + 1) * 512], in0=pb[:], in1=gate_bc[cond][:, h * 512:(h + 1) * 512],
                        op=ALU.mult), reads=bufs(pb, gate_bc[cond]), writes=bufs(tt))
                else:
                    sc = scale_t(i)
                    S.op("dve", lambda e, pb=pb, tt=tt, h=h, sc=sc: e.scalar_tensor_tensor(
                        out=tt[:, h * 512:(h + 1) * 512], in0=pb[:], scalar=sc[0], in1=gate_bc[cond][:, h * 512:(h + 1) * 512],
                        op0=ALU.mult, op1=ALU.mult), reads=bufs(pb, gate_bc[cond]) + [sc[1]], writes=bufs(tt))
            S.op("pool", lambda e, tt=tt, xt=xt: e.tensor_tensor(out=xt[:], in0=tt[:], in1=xt[:], op=ALU.add),
                 reads=bufs(tt, xt), writes=bufs(xt))
            if not last:
                S.dma("sp", tiles[i][0](dst), xt[:], reads=bufs(xt))
            else:
                st = rms_stats(xt)
                S.op("dve", lambda e, tt=tt, xt=xt, st=st: e.scalar_tensor_tensor(
                    out=tt[:], in0=xt[:], scalar=st[:, 0:1], in1=fg_bc[:], op0=ALU.mult, op1=ALU.mult),
                    reads=bufs(xt, st, fg_bc), writes=bufs(tt))
                S.dma("sp", tiles[i][0](y_out), tt[:], reads=bufs(tt))
        k.arestore(m_)

    def gmlp_consts():
        c = {}
        c["lngT"] = k.at([128, 16], F32)
        c["lnbT"] = k.at([128, 16], F32)
        c["wsT"] = k.at([128, 8, 128], BF16)
        c["wsTf"] = k.at([128, 8, 128], F32)
        c["bs_bc"] = k.at([128, 8, 128], F32)
        c["Bt"] = k.at([128, 16, 128], F32)
        S.dma("sp", c["lngT"][:], mlp_ln_g.rearrange("(c p) -> p c", p=128), writes=bufs(c["lngT"]))
        S.dma("sp", c["lnbT"][:], mlp_ln_b.rearrange("(c p) -> p c", p=128), writes=bufs(c["lnbT"]))
        S.dma("sp", c["wsTf"][:], mlp_w_sT.rearrange("g j i -> j g i"), writes=bufs(c["wsTf"]))
        S.dma("sp", c["bs_bc"][:].rearrange("p g i -> p (g i)"), mlp_b_s.rearrange("g i -> (g i)").partition_broadcast(128),
              writes=bufs(c["bs_bc"]))
        S.op("dve", lambda e: e.tensor_copy(out=c["wsT"][:], in_=c["wsTf"][:]), reads=bufs(c["wsTf"]), writes=bufs(c["wsT"]))
        for half in range(2):
            pb = banks.next()
            S.op("pe", lambda e, pb=pb, half=half: e.matmul(
                pb[:], lhsT=onesf[:], rhs=c["wsTf"][:, half * 4:(half + 1) * 4, :].rearrange("p g i -> p (g i)"),
                start=True, stop=True), reads=bufs(onesf, c["wsTf"]), writes=bufs(pb))
            for gg in range(4):
                g = half * 4 + gg
                for bb in range(2):
                    blk = g * 2 + bb
                    S.op("dve", lambda e, pb=pb, gg=gg, g=g, blk=blk: e.scalar_tensor_tensor(
                        out=c["Bt"][:, blk, :], in0=pb[:, gg * 128:(gg + 1) * 128], scalar=c["lnbT"][:, blk:blk + 1],
                        in1=c["bs_bc"][:, g, :], op0=ALU.mult, op1=ALU.add),
                        reads=bufs(pb, c["lnbT"], c["bs_bc"]), writes=bufs(c["Bt"]))
        c["vv"] = k.at([128, 8, E], BF16)
        c["gtmp"] = k.aring(2, [128, 512], F32)
        c["ug"] = k.aring(2, [128, 512], F32)
        c["zs"] = k.aring(2, [128, 512], F32)
        c["sg"] = k.aring(2, [128, 512], F32)
        c["st"] = k.at([128, 8, 8], F32)
        return c

    def gmlp_unit(c, ntile):
        yT = L["yT"]
        vv = c["vv"]
        stt = c["st"]
        for b in range(4):
            wv = load_w(mlp_w_in, E + b * 512, 512)
            for t in range(ntile):
                pb = banks.next()
                for kk in range(8):
                    S.op("pe", lambda e, pb=pb, t=t, wv=wv, kk=kk: e.matmul(
                        pb[:], lhsT=hT[:, kk, t * 128:(t + 1) * 128], rhs=wv[:, kk, :], start=(kk == 0), stop=(kk == 7)),
                        reads=bufs(hT, wv), writes=bufs(pb))
                gt = c["gtmp"].next()
                S.op("act", lambda e, pb=pb, b=b, t=t, gt=gt: e.activation(
                    out=gt[:], in_=pb[:], func=AF.Gelu, accum_out=stt[:, t, b:b + 1]),
                    reads=bufs(pb), writes=bufs(gt, stt))
                S.op("act", lambda e, b=b, t=t, gt=gt: e.activation(
                    out=junk[:, 0:512], in_=gt[:], func=AF.Square, accum_out=stt[:, t, 4 + b:5 + b]),
                    reads=bufs(gt), writes=bufs(junk, stt))
                S.op("pool", lambda e, b=b, t=t, gt=gt: e.tensor_copy(out=vv[:, t, b * 512:(b + 1) * 512], in_=gt[:]),
                     reads=bufs(gt), writes=bufs(vv))
        for t in range(ntile):
            st2 = small.next()
            S.op("dve", lambda e, t=t, st2=st2: e.tensor_reduce(
                out=st2[:, 0:2], in_=stt[:, t, :].rearrange("p (a b) -> p a b", a=2), axis=AX.X, op=ALU.add),
                reads=bufs(stt), writes=bufs(st2))
            S.op("dve", lambda e, st2=st2: e.tensor_scalar(out=st2[:, 0:2], in0=st2[:, 0:2], scalar1=1.0 / E, scalar2=None,
                                                           op0=ALU.mult), reads=bufs(st2), writes=bufs(st2))
            S.op("dve", lambda e, st2=st2: e.tensor_tensor(out=st2[:, 2:3], in0=st2[:, 0:1], in1=st2[:, 0:1], op=ALU.mult),
                 reads=bufs(st2), writes=bufs(st2))
            S.op("dve", lambda e, st2=st2: e.scalar_tensor_tensor(out=st2[:, 2:3], in0=st2[:, 2:3], scalar=-1.0, in1=st2[:, 1:2],
                                                                  op0=ALU.mult, op1=ALU.add), reads=bufs(st2), writes=bufs(st2))
            S.op("dve", lambda e, st2=st2: e.tensor_scalar(out=st2[:, 2:3], in0=st2[:, 2:3], scalar1=EPS, scalar2=None,
                                                           op0=ALU.add), reads=bufs(st2), writes=bufs(st2))
            S.op("act", lambda e, st2=st2: e.activation(out=st2[:, 2:3], in_=st2[:, 2:3], func=AF.Sqrt),
                 reads=bufs(st2), writes=bufs(st2))
            S.op("dve", lambda e, st2=st2: e.reciprocal(out=st2[:, 2:3], in_=st2[:, 2:3]), reads=bufs(st2), writes=bufs(st2))
            S.op("dve", lambda e, st2=st2, t=t: e.tensor_scalar(
                out=vv[:, t, :], in0=vv[:, t, :], scalar1=st2[:, 0:1], scalar2=st2[:, 2:3], op0=ALU.subtract, op1=ALU.mult),
                reads=bufs(vv, st2), writes=bufs(vv))
        nq = ntile // 4
        for blk in range(16):
            g = blk // 2
            if blk % 4 == 0:
                wu = load_w(mlp_w_in, blk * 128, 512)
                wz = load_w(mlp_w_in, 2 * E + blk * 128, 512)
            co = (blk % 4) * 128
            for q in range(nq):
                ug = c["ug"].next()
                zs = c["zs"].next()
                sg = c["sg"].next()
                pu = banks.next()
                for kk in range(8):
                    S.op("pe", lambda e, pu=pu, kk=kk, q=q, wu=wu, co=co: e.matmul(
                        pu[:], lhsT=wu[:, kk, co:co + 128], rhs=hT[:, kk, q * 512:(q + 1) * 512],
                        start=(kk == 0), stop=(kk == 7)), reads=bufs(wu, hT), writes=bufs(pu))
                S.op("act", lambda e, pu=pu, ug=ug: e.activation(out=ug[:], in_=pu[:], func=AF.Gelu),
                     reads=bufs(pu), writes=bufs(ug))
                pz = banks.next()
                for kk in range(8):
                    S.op("pe", lambda e, pz=pz, kk=kk, q=q, wz=wz, co=co: e.matmul(
                        pz[:], lhsT=wz[:, kk, co:co + 128], rhs=hT[:, kk, q * 512:(q + 1) * 512],
                        start=(kk == 0), stop=(kk == 7)), reads=bufs(wz, hT), writes=bufs(pz))
                S.op("act", lambda e, pz=pz, zs=zs: e.activation(out=zs[:], in_=pz[:], func=AF.Silu),
                     reads=bufs(pz), writes=bufs(zs))
                ps_ = banks.next()
                for cc in range(4):
                    t = q * 4 + cc
                    S.op("pe", lambda e, ps_=ps_, t=t, cc=cc, blk=blk, g=g: e.matmul(
                        ps_[:, cc * 128:(cc + 1) * 128], lhsT=vv[:, t, blk * 128:(blk + 1) * 128], rhs=c["wsT"][:, g, :],
                        start=True, stop=True), reads=bufs(vv, c["wsT"]), writes=bufs(ps_))
                S.op("dve", lambda e, ps_=ps_, blk=blk, sg=sg: e.scalar_tensor_tensor(
                    out=sg[:].rearrange("p (c i) -> p c i", c=4),
                    in0=ps_[:].rearrange("p (c i) -> p c i", c=4),
                    scalar=c["lngT"][:, blk:blk + 1],
                    in1=c["Bt"][:, blk:blk + 1, :].to_broadcast([128, 4, 128]), op0=ALU.mult, op1=ALU.add),
                    reads=bufs(ps_, c["lngT"], c["Bt"]), writes=bufs(sg))
                S.op("pool", lambda e, sg=sg, ug=ug: e.tensor_tensor(out=sg[:], in0=sg[:], in1=ug[:], op=ALU.mult),
                     reads=bufs(sg, ug), writes=bufs(sg))
                S.op("dve", lambda e, q=q, sg=sg, zs=zs, blk=blk: e.tensor_tensor(
                    out=yT[:, blk, q * 512:(q + 1) * 512], in0=sg[:], in1=zs[:], op=ALU.mult),
                    reads=bufs(sg, zs), writes=bufs(yT))

    SCALE = 0.125

    def nat_proj(hp, ntok, c, with_ktm):
        wt = wring.next()
        for j in (0, 1, 3):
            S.dma("pool", wt[:, :, j * 128:(j + 1) * 128],
                  nat_w_in.rearrange("(k p) n -> p k n", p=128)[:, :, j * E + hp * 128:j * E + (hp + 1) * 128],
                  writes=bufs(wt))
        qT, kT, gT = c["qT"], c["kT"], c["gT"]
        for q in range(ntok // 512):
            for j, dst, fn in ((0, qT, AF.Copy), (1, kT, AF.Copy), (3, gT, AF.Silu)):
                pb = banks.next()
                for kk in range(8):
                    S.op("pe", lambda e, pb=pb, kk=kk, q=q, j=j, wt=wt: e.matmul(
                        pb[:], lhsT=wt[:, kk, j * 128:(j + 1) * 128], rhs=hT[:, kk, q * 512:(q + 1) * 512],
                        start=(kk == 0), stop=(kk == 7)), reads=bufs(wt, hT), writes=bufs(pb))
                S.op("act", lambda e, pb=pb, q=q, dst=dst, fn=fn: e.activation(
                    out=dst[:, q * 512:(q + 1) * 512], in_=pb[:], func=fn), reads=bufs(pb), writes=bufs(dst))

    def nat_proj_v4(hp4, ntok, c, with_ktm):
        vb = c["vb"]
        wv = load_w(nat_w_in, 2 * E + hp4 * 512, 512)
        wk = load_w(nat_w_in, E + hp4 * 512, 512) if with_ktm else None
        for t in range(ntok // 128):
            pv_ = banks.next()
            for kk in range(8):
                S.op("pe", lambda e, pv_=pv_, kk=kk, t=t, wv=wv: e.matmul(
                    pv_[:], lhsT=hT[:, kk, t * 128:(t + 1) * 128], rhs=wv[:, kk, :], start=(kk == 0), stop=(kk == 7)),
                    reads=bufs(wv, hT), writes=bufs(pv_))
            if not with_ktm:
                S.op("act", lambda e, pv_=pv_, t=t: e.activation(out=vb[:, t, :], in_=pv_[:], func=AF.Copy),
                     reads=bufs(pv_), writes=bufs(vb))
            else:
                vst, kst = c["vst"], c["kst"]
                S.op("act", lambda e, pv_=pv_, t=t: e.activation(out=vst[:, t, :], in_=pv_[:], func=AF.Copy),
                     reads=bufs(pv_), writes=bufs(vst))
                S.op("pool", lambda e, t=t: e.tensor_copy(out=vb[:, t, :], in_=vst[:, t, :]), reads=bufs(vst), writes=bufs(vb))
                pk_ = banks.next()
                for kk in range(8):
                    S.op("pe", lambda e, pk_=pk_, kk=kk, t=t, wk=wk: e.matmul(
                        pk_[:], lhsT=hT[:, kk, t * 128:(t + 1) * 128], rhs=wk[:, kk, :], start=(kk == 0), stop=(kk == 7)),
                        reads=bufs(wk, hT), writes=bufs(pk_))
                S.op("act", lambda e, pk_=pk_, t=t: e.activation(out=kst[:, t, :], in_=pk_[:], func=AF.Copy),
                     reads=bufs(pk_), writes=bufs(kst))

    def nat_ctx_unit():
        yT = L["yT"]
        c = {"qT": k.at([128, 1024], BF16), "kT": k.at([128, 1024], BF16), "gT": k.at([128, 1024], BF16),
             "vb": k.at([128, 8, 512], BF16), "vst": k.at([128, 8, 512], F32), "kst": k.at([128, 8, 512], F32)}
        er = k.aring(2, [128, 512], F32)
        pbr = k.aring(2, [128, 512], BF16)
        ptr_ = k.aring(2, [128, 512], BF16)
        for hp in range(cfg.get("ctx_hp", 16)):
            if hp % 4 == 0:
                nat_proj_v4(hp // 4, 1024, c, True)
            nat_proj(hp, 1024, c, True)
            for hd in range(2):
                h = hp * 2 + hd
                for sq in range(0 if cfg.get("no_kv") else 4):
                    S.dma("sp", new_k[sq, h, :, :].rearrange("(t p) d -> p t d", p=128),
                          c["kst"][:, sq * 2:(sq + 1) * 2, (hp % 4) * 128 + hd * 64:(hp % 4) * 128 + (hd + 1) * 64], reads=bufs(c["kst"]))
                    S.dma("sp", new_v[sq, h, :, :].rearrange("(t p) d -> p t d", p=128),
                          c["vst"][:, sq * 2:(sq + 1) * 2, (hp % 4) * 128 + hd * 64:(hp % 4) * 128 + (hd + 1) * 64], reads=bufs(c["vst"]))
            qT, kT, gT, vb = c["qT"], c["kT"], c["gT"], c["vb"]
            cb_ = Ring(banks.tiles[2:8])
            pob_ = Ring(banks.tiles[0:2])

            vo = (hp % 4) * 128

            def c_qk(sq, hd):
                rows = slice(hd * 64, (hd + 1) * 64)
                tok0 = sq * 256
                ps_ = cb_.next()
                for qt in range(2):
                    S.op("pe", lambda e, ps_=ps_, qt=qt, rows=rows, tok0=tok0: e.matmul(
                        ps_[:, qt * 256:(qt + 1) * 256], lhsT=qT[rows, tok0 + qt * 128:tok0 + (qt + 1) * 128],
                        rhs=kT[rows, tok0:tok0 + 256], start=True, stop=True), reads=bufs(qT, kT), writes=bufs(ps_))
                return ps_

            def c_softmax(ps_):
                mx = small.next()
                S.op("dve", lambda e, ps_=ps_, mx=mx: e.tensor_reduce(
                    out=mx[:, 0:2], in_=ps_[:].rearrange("p (a b) -> p a b", a=2), axis=AX.X, op=ALU.max),
                    reads=bufs(ps_), writes=bufs(mx))
                S.op("dve", lambda e, mx=mx: e.tensor_scalar(out=mx[:, 2:4], in0=mx[:, 0:2], scalar1=-SCALE, scalar2=None,
                                                             op0=ALU.mult), reads=bufs(mx), writes=bufs(mx))
                et = er.next()
                for qt in range(2):
                    S.op("act", lambda e, ps_=ps_, mx=mx, et=et, qt=qt: e.activation(
                        out=et[:, qt * 256:(qt + 1) * 256], in_=ps_[:, qt * 256:(qt + 1) * 256], func=AF.Exp, scale=SCALE,
                        bias=mx[:, 2 + qt:3 + qt], accum_out=mx[:, 4 + qt:5 + qt]), reads=bufs(ps_, mx), writes=bufs(et, mx))
                S.op("dve", lambda e, mx=mx: e.reciprocal(out=mx[:, 6:8], in_=mx[:, 4:6]), reads=bufs(mx), writes=bufs(mx))
                pbt = pbr.next()
                S.op("dve", lambda e, mx=mx, et=et, pbt=pbt: e.tensor_tensor(
                    out=pbt[:].rearrange("p (a b) -> p a b", a=2), in0=et[:].rearrange("p (a b) -> p a b", a=2),
                    in1=mx[:, 6:8].unsqueeze(2).to_broadcast([128, 2, 256]), op=ALU.mult),
                    reads=bufs(mx, et), writes=bufs(pbt))
                return pbt

            def c_tpv(sq, hd, pbt, po):
                rows = slice(hd * 64, (hd + 1) * 64)
                ptb = cb_.next()
                ptv = ptb[:].bitcast(BF16)
                for j in range(4):
                    S.op("pe", lambda e, ptv=ptv, pbt=pbt, j=j: e.transpose(
                        out=ptv[:, j * 128:(j + 1) * 128], in_=pbt[:, j * 128:(j + 1) * 128], identity=identb[:]),
                        reads=bufs(pbt, identb), writes=bufs(ptb))
                pts = ptr_.next()
                S.op("act", lambda e, ptv=ptv, pts=pts: e.activation(out=pts[:], in_=ptv[:, 0:512], func=AF.Copy),
                     reads=bufs(ptb), writes=bufs(pts))
                for qt in range(2):
                    for kb in range(2):
                        S.op("pe", lambda e, po=po, rows=rows, qt=qt, kb=kb, sq=sq, hd=hd, pts=pts, vo=vo: e.matmul(
                            po[rows, qt * 128:(qt + 1) * 128], lhsT=vb[:, sq * 2 + kb, vo + hd * 64:vo + (hd + 1) * 64],
                            rhs=pts[:, (qt * 2 + kb) * 128:(qt * 2 + kb + 1) * 128], start=(kb == 0), stop=(kb == 1)),
                            reads=bufs(vb, pts), writes=bufs(po))

            its = [(sq, hd) for sq in range(0 if cfg.get("ctx_stage", 9) < 1 else 4) for hd in range(2)]
            nxt = c_qk(*its[0]) if its else None
            po = None
            for ii, (sq, hd) in enumerate(its):
                if hd == 0:
                    po = pob_.next()
                pbt = c_softmax(nxt)
                if ii + 1 < len(its):
                    nxt = c_qk(*its[ii + 1])
                c_tpv(sq, hd, pbt, po)
                if hd == 1:
                    tok0 = sq * 256
                    S.op("dve", lambda e, po=po, hp=hp, tok0=tok0: e.tensor_tensor(
                        out=yT[:, hp, tok0:tok0 + 256], in0=po[:, 0:256], in1=gT[:, tok0:tok0 + 256], op=ALU.mult),
                        reads=bufs(po, gT), writes=bufs(yT))

    def nat_lat_unit():
        yT = L["yT"]
        c = {"qT": k.at([128, 2048], BF16), "kT": k.at([128, 2048], BF16), "gT": k.at([128, 2048], BF16),
             "vb": k.at([128, 16, 512], BF16)}
        maskf = k.at([128, 3, 576], F32)
        maskb = k.at([128, 3, 576], BF16)
        for j in range(3):
            S.dma("sp", maskf[:, j, :], natmask[j], writes=bufs(maskf))
        S.op("dve", lambda e: e.tensor_copy(out=maskb[:], in_=maskf[:]), reads=bufs(maskf), writes=bufs(maskb))
        ckr = k.aring(2, [128, 2, 2, 64], BF16)
        cvr = k.aring(2, [128, 2, 2, 64], BF16)
        cktr = k.aring(2, [128, 256], BF16)
        rpr = k.aring(2, [128, 1024], F32)
        scr = k.aring(2, [128, 832], F32)
        pbr = k.aring(2, [128, 832], BF16)
        ptr_ = k.aring(2, [128, 896], BF16)
        cfg["alog"] = k.alog
        pobanks = Ring(banks.tiles[0:2])
        wbanks = Ring(banks.tiles[2:8])
        for hp in range(cfg.get("nat_hp", 16)):
            if hp % 4 == 0:
                nat_proj_v4(hp // 4, 2048, c, False)
            nat_proj(hp, 2048, c, False)
            vo = (hp % 4) * 128
            qT, kT, gT, vb = c["qT"], c["kT"], c["gT"], c["vb"]
            ck = ckr.next()
            cv = cvr.next()
            for hd in range(2):
                S.dma("pool", ck[:, :, hd, :], cache_k[hp * 2 + hd].rearrange("(kb p) d -> p kb d", p=128), writes=bufs(ck))
                S.dma("pool", cv[:, :, hd, :], cache_v[hp * 2 + hd].rearrange("(kb p) d -> p kb d", p=128), writes=bufs(cv))
            ckT = cktr.next()
            ptb = banks.next()
            ptv = ptb[:].bitcast(BF16)
            for kb in range(2):
                S.op("pe", lambda e, ptv=ptv, ck=ck, kb=kb: e.transpose(
                    out=ptv[:, kb * 128:(kb + 1) * 128], in_=ck[:, kb, :, :].rearrange("p a b -> p (a b)"), identity=identb[:]),
                    reads=bufs(ck, identb), writes=bufs(ptb))
            S.op("act", lambda e, ptv=ptv, ckT=ckT: e.activation(out=ckT[:], in_=ptv[:, 0:256], func=AF.Copy),
                 reads=bufs(ptb), writes=bufs(ckT))
            rps = []
            for hd in range(2):
                rp = rpr.next()
                S.dma("sp", rp[:], rpbg[hp * 2 + hd], writes=bufs(rp))
                rps.append(rp)
            items = []
            for pg in range(cfg.get("nat_pg", 4)):
                for hd in range(cfg.get("nat_hd", 2)):
                    for pi in range(cfg.get("nat_pi", 4)):
                        items.append((pg, hd, pi))

            def geom(pg, hd, pi):
                pr = pg * 4 + pi
                r = 2 * pr
                if pr <= 1:
                    r0, nrow, a0, mi = 0, 9, 7 - r, 1
                elif pr >= 14:
                    r0, nrow, a0, mi = 24, 8, (3 if pr == 14 else 1), 2
                else:
                    r0, nrow, a0, mi = r - 4, 9, 3, 0
                return r, r0, nrow, a0, mi

            def st_qk(it):
                pg, hd, pi = it
                r, r0, nrow, a0, mi = geom(*it)
                rows = slice(hd * 64, (hd + 1) * 64)
                q0, k0 = r * 64, r0 * 64
                ps1 = wbanks.next()
                ps2 = wbanks.next()
                S.op("pe", lambda e, ps1=ps1, rows=rows, q0=q0, k0=k0: e.matmul(
                    ps1[:], lhsT=qT[rows, q0:q0 + 128], rhs=kT[rows, k0:k0 + 512], start=True, stop=False),
                    reads=bufs(qT, kT), writes=bufs(ps1))
                S.op("pe", lambda e, ps1=ps1, mi=mi: e.matmul(
                    ps1[:], lhsT=identb[:], rhs=maskb[:, mi, 0:512], start=False, stop=True),
                    reads=bufs(identb, maskb), writes=bufs(ps1))
                if nrow == 9:
                    S.op("pe", lambda e, ps2=ps2, rows=rows, q0=q0, k0=k0: e.matmul(
                        ps2[:, 0:64], lhsT=qT[rows, q0:q0 + 128], rhs=kT[rows, k0 + 512:k0 + 576], start=True, stop=False),
                        reads=bufs(qT, kT), writes=bufs(ps2))
                    S.op("pe", lambda e, ps2=ps2, mi=mi: e.matmul(
                        ps2[:, 0:64], lhsT=identb[:], rhs=maskb[:, mi, 512:576], start=False, stop=True),
                        reads=bufs(identb, maskb), writes=bufs(ps2))
                S.op("pe", lambda e, ps2=ps2, rows=rows, q0=q0, ckT=ckT: e.matmul(
                    ps2[:, 64:320], lhsT=qT[rows, q0:q0 + 128], rhs=ckT[rows, :], start=True, stop=True),
                    reads=bufs(qT, ckT), writes=bufs(ps2))
                return ps1, ps2

            def st_softmax(it, ps1, ps2):
                pg, hd, pi = it
                r, r0, nrow, a0, mi = geom(*it)
                rp = rps[hd]
                nk = nrow * 64
                sc = scr.next()
                S.op("dve", lambda e, ps1=ps1, sc=sc, rp=rp, a0=a0: e.scalar_tensor_tensor(
                    out=sc[:, 0:512], in0=ps1[:], scalar=SCALE, in1=rp[:, a0 * 64:a0 * 64 + 512],
                    op0=ALU.mult, op1=ALU.add), reads=bufs(ps1, rp), writes=bufs(sc))
                if nrow == 9:
                    S.op("dve", lambda e, ps2=ps2, sc=sc, rp=rp, a0=a0: e.scalar_tensor_tensor(
                        out=sc[:, 512:576], in0=ps2[:, 0:64], scalar=SCALE, in1=rp[:, a0 * 64 + 512:a0 * 64 + 576],
                        op0=ALU.mult, op1=ALU.add), reads=bufs(ps2, rp), writes=bufs(sc))
                S.op("act", lambda e, ps2=ps2, sc=sc, nk=nk: e.activation(
                    out=sc[:, nk:nk + 256], in_=ps2[:, 64:320], func=AF.Copy, scale=SCALE),
                    reads=bufs(ps2), writes=bufs(sc))
                ntot = nk + 256
                mx = small.next()
                S.op("dve", lambda e, sc=sc, mx=mx, ntot=ntot: e.tensor_reduce(
                    out=mx[:, 0:1], in_=sc[:, 0:ntot], axis=AX.X, op=ALU.max), reads=bufs(sc), writes=bufs(mx))
                S.op("dve", lambda e, mx=mx: e.tensor_scalar(out=mx[:, 1:2], in0=mx[:, 0:1], scalar1=-1.0, scalar2=None,
                                                             op0=ALU.mult), reads=bufs(mx), writes=bufs(mx))
                S.op("act", lambda e, sc=sc, mx=mx, ntot=ntot: e.activation(
                    out=sc[:, 0:ntot], in_=sc[:, 0:ntot], func=AF.Exp, bias=mx[:, 1:2], accum_out=mx[:, 2:3]),
                    reads=bufs(sc, mx), writes=bufs(sc, mx))
                S.op("dve", lambda e, mx=mx: e.reciprocal(out=mx[:, 3:4], in_=mx[:, 2:3]), reads=bufs(mx), writes=bufs(mx))
                pbt = pbr.next()
                S.op("dve", lambda e, sc=sc, mx=mx, pbt=pbt, ntot=ntot: e.tensor_scalar(
                    out=pbt[:, 0:ntot], in0=sc[:, 0:ntot], scalar1=mx[:, 3:4], scalar2=None, op0=ALU.mult),
                    reads=bufs(sc, mx), writes=bufs(pbt))
                return pbt

            def st_tpv(it, pbt, po):
                pg, hd, pi = it
                r, r0, nrow, a0, mi = geom(*it)
                rows = slice(hd * 64, (hd + 1) * 64)
                nk = nrow * 64
                ptb = wbanks.next()
                ptv = ptb[:].bitcast(BF16)
                blocks = [(j * 128, 128) for j in range(4)]
                blocks += [(nk, 128), (nk + 128, 128)]
                if nrow == 9:
                    blocks.append((512, 64))
                for j, (c0, w) in enumerate(blocks):
                    S.op("pe", lambda e, ptv=ptv, pbt=pbt, j=j, c0=c0, w=w: e.transpose(
                        out=ptv[0:w, j * 128:(j + 1) * 128], in_=pbt[:, c0:c0 + w], identity=identb[:]),
                        reads=bufs(pbt, identb), writes=bufs(ptb))
                nb = len(blocks)
                pts = ptr_.next()
                S.op("act", lambda e, ptv=ptv, pts=pts: e.activation(
                    out=pts[:, 0:768], in_=ptv[:, 0:768], func=AF.Copy), reads=bufs(ptb), writes=bufs(pts))
                if nrow == 9:
                    S.op("act", lambda e, ptv=ptv, pts=pts: e.activation(
                        out=pts[0:64, 768:896], in_=ptv[0:64, 768:896], func=AF.Copy), reads=bufs(ptb), writes=bufs(pts))
                t0 = r0 // 2
                for j, (c0, w) in enumerate(blocks):
                    if j < 4:
                        lhs = vb[:, t0 + j, vo + hd * 64:vo + (hd + 1) * 64]
                        rhs = pts[:, j * 128:(j + 1) * 128]
                        rd = bufs(vb, pts)
                    elif w == 64:
                        lhs = vb[0:64, t0 + 4, vo + hd * 64:vo + (hd + 1) * 64]
                        rhs = pts[0:64, j * 128:(j + 1) * 128]
                        rd = bufs(vb, pts)
                    else:
                        kb = j - 4
                        lhs = cv[:, kb, hd, :]
                        rhs = pts[:, j * 128:(j + 1) * 128]
                        rd = bufs(cv, pts)
                    S.op("pe", lambda e, po=po, rows=rows, pi=pi, lhs=lhs, rhs=rhs, j=j, nb=nb: e.matmul(
                        po[rows, pi * 128:(pi + 1) * 128], lhsT=lhs, rhs=rhs, start=(j == 0), stop=(j == nb - 1)),
                        reads=rd, writes=bufs(po))

            pos_ = {}
            n_it = len(items)
            qk_res = {}
            sm_res = {}
            for j_ in range(min(2, n_it)):
                qk_res[j_] = st_qk(items[j_])
            if n_it:
                sm_res[0] = st_softmax(items[0], *qk_res.pop(0))
            for ii, it in enumerate(items):
                pg = it[0]
                if pg not in pos_:
                    pos_[pg] = pobanks.next()
                po = pos_[pg]
                if ii + 2 < n_it:
                    qk_res[ii + 2] = st_qk(items[ii + 2])
                if ii + 1 < n_it:
                    sm_res[ii + 1] = st_softmax(items[ii + 1], *qk_res.pop(ii + 1))
                pbt = sm_res.pop(ii)
                st_tpv(it, pbt, po)
                if ii + 1 == len(items) or items[ii + 1][0] != pg:
                    S.op("dve", lambda e, po=po, hp=hp, pg=pg: e.tensor_tensor(
                        out=yT[:, hp, pg * 512:(pg + 1) * 512], in0=po[:], in1=gT[:, pg * 512:(pg + 1) * 512], op=ALU.mult),
                        reads=bufs(po, gT), writes=bufs(yT))

    def ssd_consts():
        c = {}
        c["tri"] = [k.at([128, 128], F32), k.at([128, 128], F32)]
        c["mneg"] = [k.at([128, 128], F32), k.at([128, 128], F32)]
        c["negones"] = k.at([128, 128], F32)
        for d_ in range(2):
            sgn = 1 if d_ == 0 else -1
            S.op("pool", lambda e, d_=d_: e.memset(c["tri"][d_][:], 1.0), writes=bufs(c["tri"][d_]))
            S.op("pool", lambda e, d_=d_, sgn=sgn: e.affine_select(
                out=c["tri"][d_][:], in_=c["tri"][d_][:], compare_op=ALU.is_ge, fill=0.0, base=0,
                pattern=[[sgn, 128]], channel_multiplier=-sgn), reads=bufs(c["tri"][d_]), writes=bufs(c["tri"][d_]))
            S.op("pool", lambda e, d_=d_: e.memset(c["mneg"][d_][:], 0.0), writes=bufs(c["mneg"][d_]))
            S.op("pool", lambda e, d_=d_, sgn=sgn: e.affine_select(
                out=c["mneg"][d_][:], in_=c["mneg"][d_][:], compare_op=ALU.is_ge, fill=-30000.0, base=0,
                pattern=[[sgn, 128]], channel_multiplier=-sgn), reads=bufs(c["mneg"][d_]), writes=bufs(c["mneg"][d_]))
        S.op("pool", lambda e: e.memset(c["negones"][:], -1.0), writes=bufs(c["negones"]))
        c["ntri"] = [k.at([128, 128], F32), k.at([128, 128], F32)]
        for d_ in range(2):
            S.op("pool", lambda e, d_=d_: e.tensor_scalar(out=c["ntri"][d_][:], in0=c["tri"][d_][:], scalar1=-1.0, scalar2=None,
                                                          op0=ALU.mult), reads=bufs(c["tri"][d_]), writes=bufs(c["ntri"][d_]))
        c["cwT"] = k.at([128, 32, 5], F32)
        c["cbT"] = k.at([128, 32], F32)
        for j in range(5):
            S.dma("sp", c["cwT"][:, :, j], ssd_conv_w[j].rearrange("(b p) -> p b", p=128), writes=bufs(c["cwT"]))
        S.dma("sp", c["cbT"][:], ssd_conv_b.rearrange("(b p) -> p b", p=128), writes=bufs(c["cbT"]))
        c["dtb"] = k.at([128, 64], F32)
        c["abc"] = k.at([128, 64], F32)
        c["dsk"] = k.at([128, 32], F32)
        c["ngT"] = k.at([128, 16], F32)
        S.dma("sp", c["dtb"][:], ssd_dt_bias.partition_broadcast(128), writes=bufs(c["dtb"]))
        S.dma("sp", c["abc"][:], ssd_a_log.partition_broadcast(128), writes=bufs(c["abc"]))
        S.dma("sp", c["dsk"][:], ssd_d.partition_broadcast(128), writes=bufs(c["dsk"]))
        S.dma("sp", c["ngT"][:], ssd_norm_g.rearrange("(b p) -> p b", p=128), writes=bufs(c["ngT"]))
        S.op("act", lambda e: e.activation(out=c["abc"][:], in_=c["abc"][:], func=AF.Exp), reads=bufs(c["abc"]), writes=bufs(c["abc"]))
        S.op("dve", lambda e: e.tensor_scalar(out=c["abc"][:], in0=c["abc"][:], scalar1=-1.0, scalar2=None, op0=ALU.mult),
             reads=bufs(c["abc"]), writes=bufs(c["abc"]))
        return c

    def ssd_unit(c, tok0, ntile, nseq, is_lat):
        T_ = ntile * 128
        nch = ntile // nseq
        Lq = nch * 128
        dt_ = k.at([128, ntile, 64], F32)
        da = k.at([128, ntile, 64], F32)
        ecum = k.at([128, ntile, 64], F32)
        dtd = k.at([128, ntile, 64], F32)
        etot = k.at([128, ntile, 64], F32)
        ssq = k.at([128, ntile, 8], F32)
        rstd = k.at([128, ntile], F32)
        tmpr = k.aring(2, [128, 64], F32)
        wdt = wring.next()
        S.dma("pool", wdt[:, :, 0:64], ssd_w_in.rearrange("(k p) n -> p k n", p=128)[:, :, 6144:6208], writes=bufs(wdt))
        for t in range(ntile):
            pb = banks.next()
            for kk in range(8):
                S.op("pe", lambda e, pb=pb, kk=kk, t=t: e.matmul(
                    pb[:, 0:64], lhsT=hT[:, kk, t * 128:(t + 1) * 128], rhs=wdt[:, kk, 0:64], start=(kk == 0), stop=(kk == 7)),
                    reads=bufs(hT, wdt), writes=bufs(pb))
            S.op("dve", lambda e, pb=pb, t=t: e.tensor_tensor(out=dt_[:, t, :], in0=pb[:, 0:64], in1=c["dtb"][:], op=ALU.add),
                 reads=bufs(pb, c["dtb"]), writes=bufs(dt_))
        S.op("act", lambda e: e.activation(out=dt_[:], in_=dt_[:], func=AF.Exp), reads=bufs(dt_), writes=bufs(dt_))
        S.op("act", lambda e: e.activation(out=dt_[:], in_=dt_[:], func=AF.Ln, bias=1.0), reads=bufs(dt_), writes=bufs(dt_))
        S.op("dve", lambda e: e.tensor_tensor(out=da[:], in0=dt_[:], in1=c["abc"][:].unsqueeze(1).to_broadcast([128, ntile, 64]),
                                              op=ALU.mult), reads=bufs(dt_, c["abc"]), writes=bufs(da))
        for t in range(ntile):
            pc = banks.next()
            S.op("pe", lambda e, pc=pc, t=t: e.matmul(pc[:, 0:32], lhsT=c["tri"][0][:], rhs=da[:, t, 0:32], start=True, stop=True),
                 reads=bufs(c["tri"][0], da), writes=bufs(pc))
            S.op("pe", lambda e, pc=pc, t=t: e.matmul(pc[:, 32:64], lhsT=c["tri"][1][:], rhs=da[:, t, 32:64], start=True, stop=True),
                 reads=bufs(c["tri"][1], da), writes=bufs(pc))
            S.op("pe", lambda e, pc=pc, t=t: e.matmul(pc[:, 64:128], lhsT=onesf[:], rhs=da[:, t, :], start=True, stop=True),
                 reads=bufs(onesf, da), writes=bufs(pc))
            cumt = tmpr.next()
            S.op("act", lambda e, pc=pc, cumt=cumt: e.activation(out=cumt[:], in_=pc[:, 0:64], func=AF.Identity),
                 reads=bufs(pc), writes=bufs(cumt))
            S.op("act", lambda e, pc=pc, t=t: e.activation(out=ecum[:, t, :], in_=pc[:, 0:64], func=AF.Exp),
                 reads=bufs(pc), writes=bufs(ecum))
            S.op("act", lambda e, pc=pc, t=t: e.activation(out=etot[:, t, :], in_=pc[:, 64:128], func=AF.Exp),
                 reads=bufs(pc), writes=bufs(etot))
            S.op("dve", lambda e, pc=pc, cumt=cumt: e.tensor_tensor(out=cumt[:], in0=pc[:, 64:128], in1=cumt[:], op=ALU.subtract),
                 reads=bufs(pc, cumt), writes=bufs(cumt))
            S.op("act", lambda e, cumt=cumt: e.activation(out=cumt[:], in_=cumt[:], func=AF.Exp), reads=bufs(cumt), writes=bufs(cumt))
            S.op("dve", lambda e, cumt=cumt, t=t: e.tensor_tensor(out=dtd[:, t, :], in0=dt_[:, t, :], in1=cumt[:], op=ALU.mult),
                 reads=bufs(cumt, dt_), writes=bufs(dtd))
        raw = k.at([128, T_], F32)
        acc = k.at([128, T_], F32)
        fm = [k.at([128, T_], BF16) for _ in range(4)]
        XB = k.at([128, ntile, 384], BF16)
        SIN = k.at([128, ntile, 2, 256], BF16)
        stf = k.at([128, 2, 256], F32)
        vTg = k.at([128, 2, T_], BF16)
        h0r = k.aring(2, [128, 2, 128], F32)
        fir = k.aring(2, [128, 2, 128], F32)
        GTr = k.aring(2, [128, 128], F32)
        Dr = k.aring(2, [128, 4, 128], F32)
        Lr = k.aring(2, [128, 4, 128], F32)
        Mr = k.aring(4, [128, 4, 128], BF16)
        xdr = k.aring(5, [128, 4, 64], BF16)
        yr = k.aring(4, [128, 256], F32)
        szr = k.aring(2, [128, 256], F32)
        vbr = k.aring(2, [128, 256], BF16)
        tmp4 = k.aring(2, [128, 4, 64], F32)
        for g in range(cfg.get("ssd_g", 8)):
            wA = wring.next()
            wap = ssd_w_in.rearrange("(k p) n -> p k n", p=128)
            S.dma("pool", wA[:, :, 0:256], wap[:, :, g * 256:(g + 1) * 256], writes=bufs(wA))
            S.dma("pool", wA[:, :, 256:512], wap[:, :, E + g * 256:E + (g + 1) * 256], writes=bufs(wA))
            wB = wring.next()
            S.dma("pool", wB[:, :, 0:128], wap[:, :, 2 * E + g * 128:2 * E + (g + 1) * 128], writes=bufs(wB))
            S.dma("pool", wB[:, :, 128:256], wap[:, :, 2 * E + 1024 + g * 128:2 * E + 1024 + (g + 1) * 128], writes=bufs(wB))
            for bi in range(4):
                wt, co, cblk = ((wA, 256, 2 * g), (wA, 384, 2 * g + 1), (wB, 0, 16 + g), (wB, 128, 24 + g))[bi]
                for q in range(T_ // 512):
                    pb = banks.next()
                    for kk in range(8):
                        S.op("pe", lambda e, pb=pb, kk=kk, q=q, wt=wt, co=co: e.matmul(
                            pb[:], lhsT=wt[:, kk, co:co + 128], rhs=hT[:, kk, q * 512:(q + 1) * 512],
                            start=(kk == 0), stop=(kk == 7)), reads=bufs(wt, hT), writes=bufs(pb))
                    S.op("act", lambda e, pb=pb, q=q: e.activation(out=raw[:, q * 512:(q + 1) * 512], in_=pb[:], func=AF.Copy),
                         reads=bufs(pb), writes=bufs(raw))
                cw = c["cwT"]
                rv = raw[:].rearrange("p (s l) -> p s l", s=nseq)
                av = acc[:].rearrange("p (s l) -> p s l", s=nseq)
                S.op("dve", lambda e, cblk=cblk: e.tensor_scalar(out=acc[:], in0=raw[:], scalar1=cw[:, cblk, 2:3], scalar2=None,
                                                                 op0=ALU.mult), reads=bufs(raw, cw), writes=bufs(acc))
                taps = ((0, "dve", slice(2, Lq), slice(0, Lq - 2)), (1, "dve", slice(1, Lq), slice(0, Lq - 1)),
                        (3, "dve", slice(0, Lq - 1), slice(1, Lq)), (4, "dve", slice(0, Lq - 2), slice(2, Lq)))
                for j, eng, osl, isl in taps:
                    S.op(eng, lambda e, j=j, osl=osl, isl=isl, cblk=cblk, rv=rv, av=av: e.scalar_tensor_tensor(
                        out=av[:, :, osl], in0=rv[:, :, isl], scalar=cw[:, cblk, j:j + 1], in1=av[:, :, osl],
                        op0=ALU.mult, op1=ALU.add), reads=bufs(raw, acc, cw), writes=bufs(acc))
                S.op("act", lambda e, bi=bi, cblk=cblk: e.activation(out=fm[bi][:], in_=acc[:], func=AF.Silu,
                                                                      bias=c["cbT"][:, cblk:cblk + 1]),
                     reads=bufs(acc, c["cbT"]), writes=bufs(fm[bi]))
            for t in range(ntile):
                ptb = banks.next()
                ptv = ptb[:].bitcast(BF16)
                for bi in range(3):
                    S.op("pe", lambda e, ptv=ptv, bi=bi, t=t: e.transpose(
                        out=ptv[:, bi * 128:(bi + 1) * 128], in_=fm[bi][:, t * 128:(t + 1) * 128], identity=identb[:]),
                        reads=bufs(fm[bi], identb), writes=bufs(ptb))
                S.op("act", lambda e, ptv=ptv, t=t: e.activation(out=XB[:, t, :], in_=ptv[:, 0:384], func=AF.Copy),
                     reads=bufs(ptb), writes=bufs(XB))
            BT, CT = fm[2], fm[3]
            for sq in range(nseq):
                for d_ in range(2):
                    hs = slice(d_ * 32 + 4 * g, d_ * 32 + 4 * g + 4)
                    if is_lat:
                        h0 = h0r.next()
                        for half in range(2):
                            S.dma("sp", h0[:, half, :], state_ssd[d_, 4 * g + 2 * half:4 * g + 2 * half + 2].rearrange("h p n -> (h p) n"),
                                  writes=bufs(h0))
                        ph = banks.next()
                        for half in range(2):
                            S.op("pe", lambda e, ph=ph, h0=h0, half=half: e.transpose(
                                out=ph[:, half * 128:(half + 1) * 128], in_=h0[:, half, :], identity=identf[:]),
                                reads=bufs(h0, identf), writes=bufs(ph))
                        S.op("act", lambda e, ph=ph, d_=d_: e.activation(out=stf[:, d_, :], in_=ph[:, 0:256], func=AF.Copy),
                             reads=bufs(ph), writes=bufs(stf))
                    else:
                        S.op("pool", lambda e, d_=d_: e.memset(stf[:, d_, :], 0.0), writes=bufs(stf))
                    order = range(nch) if d_ == 0 else range(nch - 1, -1, -1)
                    for ci in order:
                        t = sq * nch + ci
                        S.op("act", lambda e, t=t, d_=d_: e.activation(out=SIN[:, t, d_, :], in_=stf[:, d_, :], func=AF.Copy),
                             reads=bufs(stf), writes=bufs(SIN))
                        xdd = xdr.next()
                        S.op("pool", lambda e, xdd=xdd, t=t, hs=hs: e.tensor_tensor(
                            out=xdd[:], in0=XB[:, t, 0:256].rearrange("p (h d) -> p h d", h=4),
                            in1=dtd[:, t, hs].unsqueeze(2).to_broadcast([128, 4, 64]), op=ALU.mult),
                            reads=bufs(XB, dtd), writes=bufs(xdd))
                        psl = banks.next()
                        S.op("pe", lambda e, psl=psl, t=t, xdd=xdd: e.matmul(
                            psl[:, 0:256], lhsT=XB[:, t, 256:384], rhs=xdd[:].rearrange("p h d -> p (h d)"), start=True, stop=True),
                            reads=bufs(XB, xdd), writes=bufs(psl))
                        S.op("pool", lambda e, t=t, d_=d_, hs=hs: e.tensor_tensor(
                            out=stf[:, d_, :].rearrange("p (h d) -> p h d", h=4), in0=stf[:, d_, :].rearrange("p (h d) -> p h d", h=4),
                            in1=etot[:, t, hs].unsqueeze(2).to_broadcast([128, 4, 64]), op=ALU.mult),
                            reads=bufs(stf, etot), writes=bufs(stf))
                        S.op("dve", lambda e, psl=psl, d_=d_: e.tensor_tensor(out=stf[:, d_, :], in0=psl[:, 0:256], in1=stf[:, d_, :],
                                                                            op=ALU.add), reads=bufs(psl, stf), writes=bufs(stf))
                    if not is_lat:
                        pf = banks.next()
                        for half in range(2):
                            S.op("pe", lambda e, pf=pf, d_=d_, half=half: e.transpose(
                                out=pf[:, half * 128:(half + 1) * 128], in_=stf[:, d_, half * 128:(half + 1) * 128], identity=identf[:]),
                                reads=bufs(stf, identf), writes=bufs(pf))
                        fi = fir.next()
                        S.op("act", lambda e, pf=pf, fi=fi: e.activation(out=fi[:].rearrange("p a b -> p (a b)"), in_=pf[:, 0:256],
                                                                        func=AF.Copy), reads=bufs(pf), writes=bufs(fi))
                        for half in range(2):
                            S.dma("sp", new_ssd[sq, d_, 4 * g + 2 * half:4 * g + 2 * half + 2].rearrange("h p n -> (h p) n"),
                                  fi[:, half, :], reads=bufs(fi))
            def front(t):
                tsl = slice(t * 128, (t + 1) * 128)
                pgz = banks.next()
                S.op("pe", lambda e, pgz=pgz, tsl=tsl: e.matmul(pgz[:, 0:128], lhsT=BT[:, tsl], rhs=CT[:, tsl], start=True, stop=True),
                     reads=bufs(BT, CT), writes=bufs(pgz))
                for kk in range(8):
                    S.op("pe", lambda e, pgz=pgz, kk=kk, tsl=tsl, wA=wA: e.matmul(
                        pgz[:, 128:384], lhsT=hT[:, kk, tsl], rhs=wA[:, kk, 0:256], start=(kk == 0), stop=(kk == 7)),
                        reads=bufs(hT, wA), writes=bufs(pgz))
                GT = GTr.next()
                S.op("act", lambda e, pgz=pgz, GT=GT: e.activation(out=GT[:], in_=pgz[:, 0:128], func=AF.Copy),
                     reads=bufs(pgz), writes=bufs(GT))
                sz = szr.next()
                S.op("act", lambda e, pgz=pgz, sz=sz: e.activation(out=sz[:], in_=pgz[:, 128:384], func=AF.Silu),
                     reads=bufs(pgz), writes=bufs(sz))
                hss = [slice(d_ * 32 + 4 * g, d_ * 32 + 4 * g + 4) for d_ in range(2)]
                Dts = []
                for d_ in range(2):
                    Dt = Dr.next()
                    S.op("pool", lambda e, Dt=Dt, d_=d_, t=t, hs=hss[d_]: e.tensor_tensor(
                        out=Dt[:], in0=c["tri"][d_][:].unsqueeze(1).to_broadcast([128, 4, 128]),
                        in1=da[:, t, hs].unsqueeze(2).to_broadcast([128, 4, 128]), op=ALU.mult),
                        reads=bufs(c["tri"][d_], da), writes=bufs(Dt))
                    Dts.append(Dt)
                pzs = []
                for d_ in range(2):
                    pz_ = banks.next()
                    Dt = Dts[d_]
                    S.op("pe", lambda e, pz_=pz_, Dt=Dt: e.matmul(
                        pz_[:], lhsT=onesf[:], rhs=Dt[:].rearrange("p h t -> p (h t)"), start=True, stop=False),
                        reads=bufs(onesf, Dt), writes=bufs(pz_))
                    S.op("pe", lambda e, pz_=pz_, d_=d_, t=t, hs=hss[d_]: e.matmul(
                        pz_[:], lhsT=c["ntri"][d_][:], rhs=da[:, t, hs].unsqueeze(2).to_broadcast([128, 4, 128]), start=False, stop=False),
                        reads=bufs(c["ntri"][d_], da), writes=bufs(pz_))
                    S.op("pe", lambda e, pz_=pz_, d_=d_: e.matmul(
                        pz_[:], lhsT=identf[:], rhs=c["mneg"][d_][:].unsqueeze(1).to_broadcast([128, 4, 128]), start=False, stop=True),
                        reads=bufs(identf, c["mneg"][d_]), writes=bufs(pz_))
                    pzs.append(pz_)
                poo = banks.next()
                for d_ in range(2):
                    S.op("pe", lambda e, poo=poo, tsl=tsl, t=t, d_=d_: e.matmul(
                        poo[:, d_ * 256:(d_ + 1) * 256], lhsT=CT[:, tsl], rhs=SIN[:, t, d_, :], start=True, stop=True),
                        reads=bufs(CT, SIN), writes=bufs(poo))
                Lts = []
                for d_ in range(2):
                    Lt = Lr.next()
                    S.op("act", lambda e, pz_=pzs[d_], Lt=Lt: e.activation(out=Lt[:].rearrange("p h t -> p (h t)"), in_=pz_[:], func=AF.Exp),
                         reads=bufs(pzs[d_]), writes=bufs(Lt))
                    Lts.append(Lt)
                Mts, xds = [], []
                for d_ in range(2):
                    Mt = Mr.next()
                    S.op("dve", lambda e, Lt=Lts[d_], Mt=Mt, GT=GT: e.tensor_tensor(
                        out=Mt[:], in0=Lt[:], in1=GT[:].unsqueeze(1).to_broadcast([128, 4, 128]), op=ALU.mult),
                        reads=bufs(Lts[d_], GT), writes=bufs(Mt))
                    xd = xdr.next()
                    S.op("pool", lambda e, xd=xd, t=t, hs=hss[d_]: e.tensor_tensor(
                        out=xd[:], in0=XB[:, t, 0:256].rearrange("p (h d) -> p h d", h=4),
                        in1=dt_[:, t, hs].unsqueeze(2).to_broadcast([128, 4, 64]), op=ALU.mult),
                        reads=bufs(XB, dt_), writes=bufs(xd))
                    Mts.append(Mt)
                    xds.append(xd)
                return dict(t=t, tsl=tsl, sz=sz, Mts=Mts, xds=xds, poo=poo, hss=hss)

            def back(f):
                t, tsl, sz, poo, hss = f["t"], f["tsl"], f["sz"], f["poo"], f["hss"]
                py = banks.next()
                for d_ in range(2):
                    Mt, xd = f["Mts"][d_], f["xds"][d_]
                    for h in range(4):
                        S.op("pe", lambda e, py=py, Mt=Mt, xd=xd, h=h, d_=d_: e.matmul(
                            py[:, h * 64:(h + 1) * 64], lhsT=Mt[:, h, :], rhs=xd[:, h, :], start=(d_ == 0 and h == 0), stop=(d_ == 1 and h == 3)),
                            reads=bufs(Mt, xd), writes=bufs(py))
                y1 = yr.next()
                y2 = yr.next()
                for d_, yy in ((0, y1), (1, y2)):
                    S.op("dve", lambda e, poo=poo, hs=hss[d_], yy=yy, t=t, d_=d_: e.tensor_tensor(
                        out=yy[:].rearrange("p (h d) -> p h d", h=4), in0=poo[:, d_ * 256:(d_ + 1) * 256].rearrange("p (h d) -> p h d", h=4),
                        in1=ecum[:, t, hs].unsqueeze(2).to_broadcast([128, 4, 64]), op=ALU.mult),
                        reads=bufs(poo, ecum), writes=bufs(yy))
                S.op("pool", lambda e, y1=y1, y2=y2: e.tensor_tensor(out=y1[:], in0=y1[:], in1=y2[:], op=ALU.add),
                     reads=bufs(y1, y2), writes=bufs(y1))
                S.op("pool", lambda e, y2=y2, t=t, g=g: e.tensor_tensor(
                    out=y2[:].rearrange("p (h d) -> p h d", h=4), in0=XB[:, t, 0:256].rearrange("p (h d) -> p h d", h=4),
                    in1=c["dsk"][:, 4 * g:4 * g + 4].unsqueeze(2).to_broadcast([128, 4, 64]), op=ALU.mult),
                    reads=bufs(XB, c["dsk"]), writes=bufs(y2))
                S.op("pool", lambda e, y1=y1, y2=y2: e.tensor_tensor(out=y1[:], in0=y1[:], in1=y2[:], op=ALU.add),
                     reads=bufs(y1, y2), writes=bufs(y1))
                S.op("dve", lambda e, py=py, y1=y1: e.tensor_tensor(out=y1[:], in0=py[:, 0:256], in1=y1[:], op=ALU.add),
                     reads=bufs(py, y1), writes=bufs(y1))
                vb_ = vbr.next()
                S.op("pool", lambda e, vb_=vb_, y1=y1, sz=sz: e.tensor_tensor(out=vb_[:], in0=y1[:], in1=sz[:], op=ALU.mult),
                     reads=bufs(y1, sz), writes=bufs(vb_))
                S.op("act", lambda e, vb_=vb_, t=t, g=g: e.activation(out=junk[:, 0:256], in_=vb_[:], func=AF.Square,
                                                                     accum_out=ssq[:, t, g:g + 1]),
                     reads=bufs(vb_), writes=bufs(junk, ssq))
                ptb = banks.next()
                ptv = ptb[:].bitcast(BF16)
                for bb in range(2):
                    S.op("pe", lambda e, ptv=ptv, vb_=vb_, bb=bb: e.transpose(
                        out=ptv[:, bb * 128:(bb + 1) * 128], in_=vb_[:, bb * 128:(bb + 1) * 128], identity=identb[:]),
                        reads=bufs(vb_, identb), writes=bufs(ptb))
                for bb in range(2):
                    S.op("act", lambda e, ptv=ptv, bb=bb, tsl=tsl, g=g: e.activation(
                        out=vTg[:, bb, tsl], in_=ptv[:, bb * 128:(bb + 1) * 128], func=AF.Identity,
                        scale=c["ngT"][:, 2 * g + bb:2 * g + bb + 1]), reads=bufs(ptb, c["ngT"]), writes=bufs(vTg))

            fnext = front(0)
            for t in range(ntile):
                fcur = fnext
                if t + 1 < ntile:
                    fnext = front(t + 1)
                back(fcur)
            S.dma("sp", yscr[2 * g:2 * g + 2, :, tok0:tok0 + T_].rearrange("b p t -> p b t"), vTg[:], reads=bufs(vTg))
        S.op("dve", lambda e: e.tensor_reduce(out=rstd[:], in_=ssq[:], axis=AX.X, op=ALU.add), reads=bufs(ssq), writes=bufs(rstd))
        S.op("dve", lambda e: e.tensor_scalar(out=rstd[:], in0=rstd[:], scalar1=1.0 / E, scalar2=EPS, op0=ALU.mult, op1=ALU.add),
             reads=bufs(rstd), writes=bufs(rstd))
        S.op("act", lambda e: e.activation(out=rstd[:], in_=rstd[:], func=AF.Sqrt), reads=bufs(rstd), writes=bufs(rstd))
        S.op("dve", lambda e: e.reciprocal(out=rstd[:], in_=rstd[:]), reads=bufs(rstd), writes=bufs(rstd))
        return rstd

    TWO_PI = float(2 * np.pi)
    MAGIC = 12582912.0

    def s5_prep():
        c = {}
        c["AA"] = k.at([128, 64, 2, 2], F32)
        c["BB"] = k.at([128, 64, 2, 2], F32)
        c["Wsel"] = k.at([128, 8, 240], BF16)
        c["dT"] = k.at([128, 16], F32)
        c["bgT"] = k.at([128, 16], F32)
        c["h0"] = k.at([128, 2, 2, 64], F32)
        S.dma("sp", c["dT"][:], s5_d.rearrange("(b p) -> p b", p=128), writes=bufs(c["dT"]))
        c["dX"] = k.at([128, 128], F32)
        for t_ in range(8):
            S.dma("sp", c["dX"][16 * t_:16 * t_ + 16, :], s5_d.rearrange("(g j) -> j g", j=16), writes=bufs(c["dX"]))
        S.dma("sp", c["bgT"][:], s5_b_glu.rearrange("(b p) -> p b", p=128), writes=bufs(c["bgT"]))
        for d_ in range(2):
            S.dma("sp", c["h0"][:, d_, :, :], s5_h0[d_].rearrange("r p g -> p r g"), writes=bufs(c["h0"]))
        mm = k.amark()
        wself = k.at([128, 8, 240], F32)
        S.op("pool", lambda e: e.memset(wself[:], 0.0), writes=bufs(wself))
        S.op("pool", lambda e: e.affine_select(out=wself[:, :, 112:128], in_=wself[:, :, 112:128], compare_op=ALU.not_equal,
                                               fill=1.0, base=0, pattern=[[-16, 8], [-1, 16]], channel_multiplier=1),
             reads=bufs(wself), writes=bufs(wself))
        S.op("pool", lambda e: e.tensor_copy(out=c["Wsel"][:], in_=wself[:]), reads=bufs(wself), writes=bufs(c["Wsel"]))
        maskT = [k.at([128, 8, 16], F32), k.at([128, 8, 16], F32)]
        for d_ in range(2):
            S.op("pool", lambda e, d_=d_: e.memset(maskT[d_][:], 1.0), writes=bufs(maskT[d_]))
        S.op("pool", lambda e: e.affine_select(out=maskT[0][:], in_=maskT[0][:], compare_op=ALU.is_ge, fill=0.0, base=15,
                                               pattern=[[16, 8], [0, 16]], channel_multiplier=-1),
             reads=bufs(maskT[0]), writes=bufs(maskT[0]))
        S.op("pool", lambda e: e.affine_select(out=maskT[1][:], in_=maskT[1][:], compare_op=ALU.is_ge, fill=0.0, base=0,
                                               pattern=[[-16, 8], [0, 16]], channel_multiplier=1),
             reads=bufs(maskT[1]), writes=bufs(maskT[1]))
        pw = [[k.at([128, 64, 16], F32), k.at([128, 64, 16], F32)] for _ in range(2)]
        coef = [[k.at([128, 64], F32), k.at([128, 64], F32)] for _ in range(2)]
        lr, li, ls = k.at([128, 64], F32), k.at([128, 64], F32), k.at([128, 64], F32)
        xx, ang = k.at([128, 64], F32), k.at([128, 64], F32)
        tr = k.aring(6, [128, 64], F32)
        for d_ in range(2):
            S.dma("sp", lr[:], s5_lam[0, d_], writes=bufs(lr))
            S.dma("sp", li[:], s5_lam[1, d_], writes=bufs(li))
            S.dma("sp", ls[:], s5_lstep[d_], writes=bufs(ls))
            S.op("act", lambda e: e.activation(out=ls[:], in_=ls[:], func=AF.Exp), reads=bufs(ls), writes=bufs(ls))
            S.op("dve", lambda e: e.tensor_tensor(out=xx[:], in0=lr[:], in1=ls[:], op=ALU.mult), reads=bufs(lr, ls), writes=bufs(xx))
            S.op("dve", lambda e: e.tensor_tensor(out=ang[:], in0=li[:], in1=ls[:], op=ALU.mult), reads=bufs(li, ls), writes=bufs(ang))
            pre, pim = pw[d_]
            for kq in range(1, 9):
                mp, mn, sn, cs, t1, t2 = [tr.next() for _ in range(6)]
                S.op("act", lambda e, mp=mp, kq=kq: e.activation(out=mp[:], in_=xx[:], func=AF.Exp, scale=float(kq)),
                     reads=bufs(xx), writes=bufs(mp))
                S.op("act", lambda e, mn=mn, kq=kq: e.activation(out=mn[:], in_=xx[:], func=AF.Exp, scale=float(-kq)),
                     reads=bufs(xx), writes=bufs(mn))
                for dst, shift in ((sn, 0.0), (cs, 0.25)):
                    if shift:
                        S.op("dve", lambda e, t1=t1, kq=kq, shift=shift: e.tensor_scalar(
                            out=t1[:], in0=ang[:], scalar1=float(kq / TWO_PI), scalar2=shift, op0=ALU.mult, op1=ALU.add),
                            reads=bufs(ang), writes=bufs(t1))
                        S.op("dve", lambda e, t1=t1: e.tensor_scalar(out=t1[:], in0=t1[:], scalar1=MAGIC, scalar2=None, op0=ALU.add),
                             reads=bufs(t1), writes=bufs(t1))
                    else:
                        S.op("dve", lambda e, t1=t1, kq=kq: e.tensor_scalar(
                            out=t1[:], in0=ang[:], scalar1=float(kq / TWO_PI), scalar2=MAGIC, op0=ALU.mult, op1=ALU.add),
                            reads=bufs(ang), writes=bufs(t1))
                    S.op("dve", lambda e, t1=t1: e.tensor_scalar(out=t1[:], in0=t1[:], scalar1=-MAGIC, scalar2=-TWO_PI,
                                                                 op0=ALU.add, op1=ALU.mult), reads=bufs(t1), writes=bufs(t1))
                    S.op("dve", lambda e, t1=t1, kq=kq: e.scalar_tensor_tensor(
                        out=t1[:], in0=ang[:], scalar=float(kq), in1=t1[:], op0=ALU.mult, op1=ALU.add),
                        reads=bufs(ang, t1), writes=bufs(t1))
                    if shift:
                        S.op("dve", lambda e, t1=t1: e.tensor_scalar(out=t1[:], in0=t1[:], scalar1=float(np.pi / 2), scalar2=None,
                                                                     op0=ALU.add), reads=bufs(t1), writes=bufs(t1))
                    S.op("act", lambda e, t1=t1, dst=dst: e.activation(out=dst[:], in_=t1[:], func=AF.Sin),
                         reads=bufs(t1), writes=bufs(dst))
                S.op("dve", lambda e, kq=kq, mp=mp, cs=cs, pre=pre: e.tensor_tensor(out=pre[:, :, kq - 1], in0=mp[:], in1=cs[:], op=ALU.mult),
                     reads=bufs(mp, cs), writes=bufs(pre))
                S.op("dve", lambda e, kq=kq, mp=mp, sn=sn, pim=pim: e.tensor_tensor(out=pim[:, :, kq - 1], in0=mp[:], in1=sn[:], op=ALU.mult),
                     reads=bufs(mp, sn), writes=bufs(pim))
                S.op("dve", lambda e, kq=kq, mn=mn, cs=cs, pre=pre: e.tensor_tensor(out=pre[:, :, 7 + kq], in0=mn[:], in1=cs[:], op=ALU.mult),
                     reads=bufs(mn, cs), writes=bufs(pre))
                S.op("dve", lambda e, kq=kq, mn=mn, sn=sn, pim=pim: e.scalar_tensor_tensor(
                    out=pim[:, :, 7 + kq], in0=mn[:], scalar=-1.0, in1=sn[:], op0=ALU.mult, op1=ALU.mult),
                    reads=bufs(mn, sn), writes=bufs(pim))
            for r_ in range(2):
                S.op("act", lambda e, d_=d_, r_=r_, pre=pre: e.activation(out=c["AA"][:, :, d_, r_], in_=pre[:, :, 7], func=AF.Copy),
                     reads=bufs(pre), writes=bufs(c["AA"]))
            S.op("dve", lambda e, d_=d_, pim=pim: e.tensor_scalar(out=c["BB"][:, :, d_, 0], in0=pim[:, :, 7], scalar1=-1.0, scalar2=None,
                                                         op0=ALU.mult), reads=bufs(pim), writes=bufs(c["BB"]))
            S.op("act", lambda e, d_=d_, pim=pim: e.activation(out=c["BB"][:, :, d_, 1], in_=pim[:, :, 7], func=AF.Copy),
                 reads=bufs(pim), writes=bufs(c["BB"]))
            den, nr, t1, t2 = [tr.next() for _ in range(4)]
            S.op("dve", lambda e, den=den: e.tensor_tensor(out=den[:], in0=lr[:], in1=lr[:], op=ALU.mult), reads=bufs(lr), writes=bufs(den))
            S.op("dve", lambda e, t1=t1: e.tensor_tensor(out=t1[:], in0=li[:], in1=li[:], op=ALU.mult), reads=bufs(li), writes=bufs(t1))
            S.op("dve", lambda e, den=den, t1=t1: e.tensor_tensor(out=den[:], in0=den[:], in1=t1[:], op=ALU.add),
                 reads=bufs(den, t1), writes=bufs(den))
            S.op("dve", lambda e, den=den: e.reciprocal(out=den[:], in_=den[:]), reads=bufs(den), writes=bufs(den))
            S.op("dve", lambda e, nr=nr, pre=pre: e.tensor_scalar(out=nr[:], in0=pre[:, :, 0], scalar1=-1.0, scalar2=None, op0=ALU.add),
                 reads=bufs(pre), writes=bufs(nr))
            cr_, ci_ = coef[d_]
            S.op("dve", lambda e, nr=nr, t1=t1: e.tensor_tensor(out=t1[:], in0=nr[:], in1=lr[:], op=ALU.mult), reads=bufs(nr, lr), writes=bufs(t1))
            S.op("dve", lambda e, t2=t2, pim=pim: e.tensor_tensor(out=t2[:], in0=pim[:, :, 0], in1=li[:], op=ALU.mult), reads=bufs(pim, li), writes=bufs(t2))
            S.op("dve", lambda e, t1=t1, t2=t2: e.tensor_tensor(out=t1[:], in0=t1[:], in1=t2[:], op=ALU.add), reads=bufs(t1, t2), writes=bufs(t1))
            S.op("dve", lambda e, t1=t1, den=den, cr_=cr_: e.tensor_tensor(out=cr_[:], in0=t1[:], in1=den[:], op=ALU.mult),
                 reads=bufs(t1, den), writes=bufs(cr_))
            S.op("dve", lambda e, t1=t1, pim=pim: e.tensor_tensor(out=t1[:], in0=pim[:, :, 0], in1=lr[:], op=ALU.mult), reads=bufs(pim, lr), writes=bufs(t1))
            S.op("dve", lambda e, nr=nr, t2=t2: e.tensor_tensor(out=t2[:], in0=nr[:], in1=li[:], op=ALU.mult), reads=bufs(nr, li), writes=bufs(t2))
            S.op("dve", lambda e, t1=t1, t2=t2: e.tensor_tensor(out=t1[:], in0=t1[:], in1=t2[:], op=ALU.subtract), reads=bufs(t1, t2), writes=bufs(t1))
            S.op("dve", lambda e, t1=t1, den=den, ci_=ci_: e.tensor_tensor(out=ci_[:], in0=t1[:], in1=den[:], op=ALU.mult),
                 reads=bufs(t1, den), writes=bufs(ci_))
        Braw = [k.at([128, 8, 16], F32), k.at([128, 8, 16], F32)]
        Craw = [k.at([128, 8, 16], F32), k.at([128, 8, 16], F32)]
        Bb = [k.at([128, 8, 16], F32), k.at([128, 8, 16], F32)]
        V = [[k.at([128, 8, 8, 16], F32), k.at([128, 8, 8, 16], F32)] for _ in range(2)]
        W2 = [[k.at([128, 8, 8, 16], F32), k.at([128, 8, 8, 16], F32)] for _ in range(2)]
        t8 = k.aring(4, [128, 8, 16], F32)
        t8e = {"pool": k.aring(4, [128, 8, 16], F32), "dve": k.aring(4, [128, 8, 16], F32)}
        T16 = k.aring(2, [128, 16, 128], BF16)
        VT16 = k.aring(2, [128, 8, 2, 2, 128], BF16)
        W216 = k.aring(2, [128, 8, 2, 2, 128], BF16)
        Ttmp = k.aring(2, [128, 128], F32)
        Ttmp2 = k.aring(2, [128, 128], F32)

        def bc_j(ap2):
            return ap2.unsqueeze(2).to_broadcast([128, 8, 16])

        for b in range(8):
            gs = slice(8 * b, 8 * b + 8)
            t16, vt16, w216 = T16.next(), VT16.next(), W216.next()
            for d_ in range(2):
                pre, pim = pw[d_]
                cr_, ci_ = coef[d_]
                for r_ in range(2):
                    S.dma("sp", Braw[r_][:], s5_B[r_, d_, :, gs, :], writes=bufs(Braw[r_]))
                    S.dma("sp", Craw[r_][:], s5_C[r_, d_, :, gs, :], writes=bufs(Craw[r_]))
                ta, tb = t8.next(), t8.next()
                S.op("dve", lambda e, ta=ta, cr_=cr_, gs=gs: e.tensor_tensor(out=ta[:], in0=Braw[0][:], in1=bc_j(cr_[:, gs]), op=ALU.mult),
                     reads=bufs(Braw[0], cr_), writes=bufs(ta))
                S.op("dve", lambda e, tb=tb, ci_=ci_, gs=gs: e.tensor_tensor(out=tb[:], in0=Braw[1][:], in1=bc_j(ci_[:, gs]), op=ALU.mult),
                     reads=bufs(Braw[1], ci_), writes=bufs(tb))
                S.op("dve", lambda e, ta=ta, tb=tb: e.tensor_tensor(out=Bb[0][:], in0=ta[:], in1=tb[:], op=ALU.subtract),
                     reads=bufs(ta, tb), writes=bufs(Bb[0]))
                ta, tb = t8.next(), t8.next()
                S.op("dve", lambda e, ta=ta, cr_=cr_, gs=gs: e.tensor_tensor(out=ta[:], in0=Braw[1][:], in1=bc_j(cr_[:, gs]), op=ALU.mult),
                     reads=bufs(Braw[1], cr_), writes=bufs(ta))
                S.op("dve", lambda e, tb=tb, ci_=ci_, gs=gs: e.tensor_tensor(out=tb[:], in0=Braw[0][:], in1=bc_j(ci_[:, gs]), op=ALU.mult),
                     reads=bufs(Braw[0], ci_), writes=bufs(tb))
                S.op("dve", lambda e, ta=ta, tb=tb: e.tensor_tensor(out=Bb[1][:], in0=ta[:], in1=tb[:], op=ALU.add),
                     reads=bufs(ta, tb), writes=bufs(Bb[1]))
                for s_ in range(8):
                    kv = 8 + (s_ if d_ == 0 else 7 - s_)
                    kw = s_ if d_ == 0 else 7 - s_
                    for (eng, P_idx, X_, out_, neg_im) in (("pool", kv, Bb, V[d_], False), ("dve", kw, Craw, W2[d_], True)):
                        Pr = bc_j(pre[:, gs, P_idx])
                        Pi = bc_j(pim[:, gs, P_idx])
                        ta, tb = t8e[eng].next(), t8e[eng].next()
                        S.op(eng, lambda e, ta=ta, X_=X_, Pr=Pr: e.tensor_tensor(out=ta[:], in0=X_[0][:], in1=Pr, op=ALU.mult),
                             reads=bufs(X_[0], pre), writes=bufs(ta))
                        S.op(eng, lambda e, tb=tb, X_=X_, Pi=Pi: e.tensor_tensor(out=tb[:], in0=X_[1][:], in1=Pi, op=ALU.mult),
                             reads=bufs(X_[1], pim), writes=bufs(tb))
                        S.op(eng, lambda e, ta=ta, tb=tb, out_=out_, s_=s_: e.tensor_tensor(
                            out=out_[0][:, :, s_, :], in0=ta[:], in1=tb[:], op=ALU.subtract), reads=bufs(ta, tb), writes=bufs(out_[0]))
                        ta, tb = t8e[eng].next(), t8e[eng].next()
                        S.op(eng, lambda e, ta=ta, X_=X_, Pi=Pi: e.tensor_tensor(out=ta[:], in0=X_[0][:], in1=Pi, op=ALU.mult),
                             reads=bufs(X_[0], pim), writes=bufs(ta))
                        S.op(eng, lambda e, tb=tb, X_=X_, Pr=Pr: e.tensor_tensor(out=tb[:], in0=X_[1][:], in1=Pr, op=ALU.mult),
                             reads=bufs(X_[1], pre), writes=bufs(tb))
                        if not neg_im:
                            S.op(eng, lambda e, ta=ta, tb=tb, out_=out_, s_=s_: e.tensor_tensor(
                                out=out_[1][:, :, s_, :], in0=ta[:], in1=tb[:], op=ALU.add), reads=bufs(ta, tb), writes=bufs(out_[1]))
                        else:
                            S.op(eng, lambda e, ta=ta, tb=tb: e.tensor_tensor(out=ta[:], in0=ta[:], in1=tb[:], op=ALU.add),
                                 reads=bufs(ta, tb), writes=bufs(ta))
                            S.op(eng, lambda e, ta=ta, out_=out_, s_=s_: e.tensor_scalar(
                                out=out_[1][:, :, s_, :], in0=ta[:], scalar1=-1.0, scalar2=None, op0=ALU.mult),
                                reads=bufs(ta), writes=bufs(out_[1]))
                for r_ in range(2):
                    S.op("act", lambda e, d_=d_, r_=r_, w216=w216: e.activation(
                        out=w216[:, :, d_, r_, :], in_=W2[d_][r_][:].rearrange("p g s j -> p g (s j)"), func=AF.Copy),
                        reads=bufs(W2[d_][r_]), writes=bufs(w216))
                for gl in range(8):
                    pv_ = banks.next()
                    for r_ in range(2):
                        S.op("pe", lambda e, pv_=pv_, d_=d_, r_=r_, gl=gl: e.transpose(
                            out=pv_[:, r_ * 128:(r_ + 1) * 128], in_=V[d_][r_][:, gl, :, :].rearrange("p s j -> p (s j)"),
                            identity=identf[:]), reads=bufs(V[d_][r_], identf), writes=bufs(pv_))
                    S.op("act", lambda e, pv_=pv_, d_=d_, gl=gl, vt16=vt16: e.activation(
                        out=vt16[:, gl, d_, :, :].rearrange("p r m -> p (r m)"), in_=pv_[:, 0:256], func=AF.Copy),
                        reads=bufs(pv_), writes=bufs(vt16))
            for gl in range(8):
                for par in range(2):
                    rows = slice(par * 64, (par + 1) * 64)
                    gi = gl * 2 + par
                    pT = banks.next()
                    for d_ in range(2):
                        for r_ in range(2):
                            S.op("pe", lambda e, pT=pT, d_=d_, r_=r_, gl=gl, rows=rows: e.matmul(
                                pT[:, d_ * 128:(d_ + 1) * 128], lhsT=V[d_][r_][rows, gl, :, :].rearrange("p s j -> p (s j)"),
                                rhs=W2[d_][r_][rows, gl, :, :].rearrange("p s j -> p (s j)"), start=(r_ == 0), stop=(r_ == 1)),
                                reads=bufs(V[d_][r_], W2[d_][r_]), writes=bufs(pT))
                    ta, tb = Ttmp.next(), Ttmp2.next()
                    S.op("dve", lambda e, pT=pT, ta=ta: e.tensor_tensor(
                        out=ta[:], in0=pT[:, 0:128], in1=maskT[0][:].rearrange("p t j -> p (t j)"), op=ALU.mult),
                        reads=bufs(pT, maskT[0]), writes=bufs(ta))
                    S.op("dve", lambda e, pT=pT, tb=tb: e.tensor_tensor(
                        out=tb[:], in0=pT[:, 128:256], in1=maskT[1][:].rearrange("p t j -> p (t j)"), op=ALU.mult),
                        reads=bufs(pT, maskT[1]), writes=bufs(tb))
                    S.op("pool", lambda e, ta=ta, tb=tb, t16=t16, gi=gi: e.tensor_tensor(out=t16[:, gi, :], in0=ta[:], in1=tb[:], op=ALU.add),
                         reads=bufs(ta, tb), writes=bufs(t16))
            S.dma("sp", Tscr[b], t16[:].rearrange("p g m -> p (g m)"), reads=bufs(t16))
            S.dma("sp", VTscr[b], vt16[:].rearrange("p g d r m -> p (g d r m)"), reads=bufs(vt16))
            S.dma("sp", W2scr[b], w216[:].rearrange("p g d r m -> p (g d r m)"), reads=bufs(w216))
        k.arestore(mm)
        return c

    def s5_tiles(tok0, sub, is_lat):
        tiles = []
        for i in range(8):
            I_ = sub * 8 + i
            if not is_lat:
                s_, c0 = I_, 0
            else:
                s_, c0 = I_ // 2, (I_ % 2) * 128
            base = tok0 + 8 * c0 + s_
            tiles.append(((lambda src_, base=base: src_[base:base + 8 * 127 + 1:8, :]), I_ * 128))
        return tiles

    def s5_unit(c, tok0, is_lat):
        C_ = 256 if is_lat else 128
        nseq = 1 if is_lat else 4
        nch = C_ // nseq
        nct = C_ // 128
        Tn = 8 * C_
        U = k.at([128, nct, 16, 8, 16], BF16)
        X = k.at([128, 16, C_], BF16)
        arr = k.at([128, 16, 2, nseq, nch + 1], F32)
        Hb = k.at([128, 16, 2, nseq, nch + 1], BF16)
        Ysb = T(U.t[:].rearrange("p a g s j -> p (a g s j)").rearrange("p (g c) -> p g c", g=16))
        Ysb.b = U.b
        Tw = k.at([128, 16, 128], BF16)
        VTw = k.at([128, 8, 2, 2, 128], BF16)
        W2w = k.at([128, 8, 2, 2, 128], BF16)
        ygst = k.at([128, 2, Tn], BF16)
        uur = k.aring(2, [128, 512], F32)
        ysr = k.aring(2, [128, 512], F32)
        tmps = {eng: [k.at([128, 8, 2, nseq], F32) for _ in range(3)] for eng in ("dve", "pool")}
        GPB = 512 // C_
        if not is_lat:
            fin = k.at([128, 4, 2, 2, 64], F32)
            fst = k.aring(2, [64, 4, 128], F32)
        bl = {}
        if is_lat:
            for eng in ("dve", "pool"):
                bl[eng] = {"PR": k.at([128, 8, 16], F32), "PI": k.at([128, 8, 16], F32),
                           "AAp": k.at([128, 8, 2, 16], F32), "BBp": k.at([128, 8, 2, 16], F32),
                           "cc": k.at([128, 8, 2, 17], F32),
                           "t": [k.at([128, 8, 8], F32) for _ in range(4)],
                           "l": [k.at([128, 8, 2, 16], F32) for _ in range(3)],
                           "c": [k.at([128, 8, 2], F32) for _ in range(2)],
                           "f": [[k.at([128, 8, 2, 16], F32) for _ in range(2)] for _ in range(2)]}
        for b in range(cfg.get("s5_nb", 8)):
            gs = slice(8 * b, 8 * b + 8)
            S.dma("sp", Tw[:].rearrange("p g m -> p (g m)"), Tscr[b], writes=bufs(Tw))
            S.dma("sp", VTw[:].rearrange("p g d r m -> p (g d r m)"), VTscr[b], writes=bufs(VTw))
            S.dma("sp", W2w[:].rearrange("p g d r m -> p (g d r m)"), W2scr[b], writes=bufs(W2w))
            wu = wring.next()
            S.dma("pool", wu[:, :, 0:256], s5_w_in.rearrange("(k p) n -> p k n", p=128)[:, :, 256 * b:256 * (b + 1)], writes=bufs(wu))
            for ct in range(nct):
                for s2 in range(4):
                    pb = banks.next()
                    for si in range(2):
                        s_ = s2 * 2 + si
                        p0 = s_ * C_ + ct * 128
                        for kk in range(8):
                            S.op("pe", lambda e, pb=pb, kk=kk, si=si, p0=p0, wu=wu: e.matmul(
                                pb[:, si * 256:(si + 1) * 256], lhsT=hT[:, kk, p0:p0 + 128], rhs=wu[:, kk, 0:256],
                                start=(kk == 0), stop=(kk == 7)), reads=bufs(hT, wu), writes=bufs(pb))
                    S.op("act", lambda e, pb=pb, ct=ct, s2=s2: e.activation(
                        out=U[:, ct, :, s2 * 2:s2 * 2 + 2, :], in_=pb[:].rearrange("p (s g j) -> p g s j", s=2, g=16), func=AF.Copy),
                        reads=bufs(pb), writes=bufs(U))
            for ct in range(nct):
                for g4 in range(4):
                    pb = banks.next()
                    for gg in range(4):
                        gi = g4 * 4 + gg
                        S.op("pe", lambda e, pb=pb, gg=gg, gi=gi, ct=ct: e.matmul(
                            pb[:, gg * 128:(gg + 1) * 128], lhsT=U[:, ct, gi, :, :].rearrange("p s j -> p (s j)"), rhs=identb[:], start=True, stop=True),
                            reads=bufs(U, identb), writes=bufs(pb))
                    S.op("act", lambda e, pb=pb, g4=g4, ct=ct: e.activation(
                        out=X[:, g4 * 4:g4 * 4 + 4, ct * 128:(ct + 1) * 128], in_=pb[:].rearrange("p (g c) -> p g c", g=4), func=AF.Copy),
                        reads=bufs(pb), writes=bufs(X))
            if is_lat:
                S.op("act", lambda e, gs=gs: e.activation(
                    out=arr[:, :, :, 0, 0].rearrange("p (g d) r -> p g d r", d=2),
                    in_=c["h0"][:, :, :, gs].rearrange("p d r g -> p g d r"), func=AF.Copy), reads=bufs(c["h0"]), writes=bufs(arr))
            else:
                S.op("pool", lambda e: e.memset(arr[:, :, :, :, 0:1], 0.0), writes=bufs(arr))
            for gl in range(8):
                pGs = [banks.next() for _ in range(nct)]
                for par in range(2):
                    gi = 2 * gl + par
                    rows = slice(par * 64, (par + 1) * 64)
                    for d_ in range(2):
                        if d_ == 0:
                            rhs = X[:, gi, :]
                        else:
                            rhs = X[:, gi, :].rearrange("p (s c) -> p s c", s=nseq)[:, :, ::-1]
                        for r_ in range(2):
                            if is_lat:
                                outp = pGs[d_][rows, r_ * 256:(r_ + 1) * 256]
                                pgb = pGs[d_]
                            else:
                                outp = pGs[0][rows, (d_ * 2 + r_) * 128:(d_ * 2 + r_ + 1) * 128]
                                pgb = pGs[0]
                            S.op("pe", lambda e, outp=outp, gl=gl, d_=d_, r_=r_, par=par, rhs=rhs: e.matmul(
                                outp, lhsT=VTw[:, gl, d_, r_, par * 64:(par + 1) * 64], rhs=rhs, start=True, stop=True),
                                reads=bufs(VTw, X), writes=bufs(pgb))
                for d_ in range(2):
                    if is_lat:
                        src_ = pGs[d_][:].rearrange("p (r s c) -> p r s c", r=2, s=1)
                        pgb = pGs[d_]
                    else:
                        src_ = pGs[0][:, d_ * 256:(d_ + 1) * 256].rearrange("p (r s c) -> p r s c", r=2, s=nseq)
                        pgb = pGs[0]
                    S.op("act", lambda e, src_=src_, gl=gl, d_=d_: e.activation(
                        out=arr[:, gl * 2 + d_, :, :, 1:nch + 1], in_=src_, func=AF.Copy), reads=bufs(pgb), writes=bufs(arr))
            AAb = c["AA"][:, gs, :, :].rearrange("p g d r -> p (g d) r")
            BBb = c["BB"][:, gs, :, :].rearrange("p g d r -> p (g d) r")
            if not is_lat:
                for kq in range(nch):
                    for eng, qs in (("dve", slice(0, 8)), ("pool", slice(8, 16))):
                        tt, p1, p2 = tmps[eng]
                        S.op(eng, lambda e, tt=tt, qs=qs, kq=kq: e.tensor_tensor(
                            out=tt[:], in0=arr[:, qs, :, :, kq], in1=arr[:, qs, :, :, kq + 1], op=ALU.add),
                            reads=bufs(arr), writes=bufs(tt))
                        S.op(eng, lambda e, tt=tt, p1=p1, qs=qs, AAb=AAb: e.tensor_tensor(
                            out=p1[:], in0=tt[:], in1=AAb[:, qs, :].unsqueeze(3).to_broadcast([128, 8, 2, nseq]), op=ALU.mult),
                            reads=bufs(tt, c["AA"]), writes=bufs(p1))
                        S.op(eng, lambda e, tt=tt, p2=p2, qs=qs, BBb=BBb: e.tensor_tensor(
                            out=p2[:], in0=tt[:, :, ::-1, :], in1=BBb[:, qs, :].unsqueeze(3).to_broadcast([128, 8, 2, nseq]), op=ALU.mult),
                            reads=bufs(tt, c["BB"]), writes=bufs(p2))
                        S.op(eng, lambda e, p1=p1, p2=p2, qs=qs, kq=kq: e.tensor_tensor(
                            out=arr[:, qs, :, :, kq + 1], in0=p1[:], in1=p2[:], op=ALU.add), reads=bufs(p1, p2), writes=bufs(arr))
                S.op("act", lambda e: e.activation(out=Hb[:].rearrange("p q r s c -> p (q r s c)"),
                                                   in_=arr[:].rearrange("p q r s c -> p (q r s c)"), func=AF.Copy),
                     reads=bufs(arr), writes=bufs(Hb))
            else:
                NB_, BL_ = 16, 16
                for eng, qs in (("dve", slice(0, 8)), ("pool", slice(8, 16))):
                    B_ = bl[eng]
                    PR, PI, AAp, BBp, cc = B_["PR"], B_["PI"], B_["AAp"], B_["BBp"], B_["cc"]
                    tA, tB, tC, tD = B_["t"]
                    AAh = AAb[:, qs, :]
                    BBh = BBb[:, qs, :]
                    S.op(eng, lambda e, PR=PR, AAh=AAh: e.tensor_copy(out=PR[:, :, 0], in_=AAh[:, :, 0]), reads=bufs(c["AA"]), writes=bufs(PR))
                    S.op(eng, lambda e, PI=PI, BBh=BBh: e.tensor_copy(out=PI[:, :, 0], in_=BBh[:, :, 1]), reads=bufs(c["BB"]), writes=bufs(PI))
                    m_ = 1
                    while m_ < 16:
                        ar = PR[:, :, m_ - 1:m_].to_broadcast([128, 8, m_])
                        ai = PI[:, :, m_ - 1:m_].to_broadcast([128, 8, m_])
                        src_r, src_i = PR[:, :, 0:m_], PI[:, :, 0:m_]
                        dst_r, dst_i = PR[:, :, m_:2 * m_], PI[:, :, m_:2 * m_]
                        ta, tb = tA[:, :, 0:m_], tB[:, :, 0:m_]
                        tc_, td = tC[:, :, 0:m_], tD[:, :, 0:m_]
                        S.op(eng, lambda e, ta=ta, src_r=src_r, ar=ar: e.tensor_tensor(out=ta, in0=src_r, in1=ar, op=ALU.mult), reads=bufs(PR), writes=bufs(tA))
                        S.op(eng, lambda e, tb=tb, src_i=src_i, ai=ai: e.tensor_tensor(out=tb, in0=src_i, in1=ai, op=ALU.mult), reads=bufs(PI), writes=bufs(tB))
                        S.op(eng, lambda e, tc_=tc_, src_r=src_r, ai=ai: e.tensor_tensor(out=tc_, in0=src_r, in1=ai, op=ALU.mult), reads=bufs(PR, PI), writes=bufs(tC))
                        S.op(eng, lambda e, td=td, src_i=src_i, ar=ar: e.tensor_tensor(out=td, in0=src_i, in1=ar, op=ALU.mult), reads=bufs(PR, PI), writes=bufs(tD))
                        S.op(eng, lambda e, dst_r=dst_r, ta=ta, tb=tb: e.tensor_tensor(out=dst_r, in0=ta, in1=tb, op=ALU.subtract), reads=bufs(tA, tB), writes=bufs(PR))
                        S.op(eng, lambda e, dst_i=dst_i, tc_=tc_, td=td: e.tensor_tensor(out=dst_i, in0=tc_, in1=td, op=ALU.add), reads=bufs(tC, tD), writes=bufs(PI))
                        m_ *= 2
                    for r_ in range(2):
                        S.op(eng, lambda e, AAp=AAp, PR=PR, r_=r_: e.tensor_copy(out=AAp[:, :, r_, :], in_=PR[:]), reads=bufs(PR), writes=bufs(AAp))
                    S.op(eng, lambda e, BBp=BBp, PI=PI: e.tensor_scalar(out=BBp[:, :, 0, :], in0=PI[:], scalar1=-1.0, scalar2=None, op0=ALU.mult),
                         reads=bufs(PI), writes=bufs(BBp))
                    S.op(eng, lambda e, BBp=BBp, PI=PI: e.tensor_copy(out=BBp[:, :, 1, :], in_=PI[:]), reads=bufs(PI), writes=bufs(BBp))
                for eng, qs in (("dve", slice(0, 8)), ("pool", slice(8, 16))):
                    B_ = bl[eng]
                    AAp, BBp, cc = B_["AAp"], B_["BBp"], B_["cc"]
                    t3, p13, p23 = B_["l"]
                    AAh = AAb[:, qs, :].unsqueeze(3).to_broadcast([128, 8, 2, NB_])
                    BBh = BBb[:, qs, :].unsqueeze(3).to_broadcast([128, 8, 2, NB_])
                    xv = arr[:, qs, :, 0, 1:257].rearrange("p q r (b i) -> p q r b i", i=BL_)
                    for i_ in range(BL_):
                        if i_ == 0:
                            src_t = xv[:, :, :, :, 0]
                        else:
                            S.op(eng, lambda e, t3=t3, xv=xv, i_=i_: e.tensor_tensor(
                                out=t3[:], in0=xv[:, :, :, :, i_ - 1], in1=xv[:, :, :, :, i_], op=ALU.add), reads=bufs(arr), writes=bufs(t3))
                            src_t = t3[:]
                        rd = bufs(arr) if i_ == 0 else bufs(t3)
                        src_sw = src_t[:, :, ::-1, :]
                        S.op(eng, lambda e, p13=p13, src_t=src_t, AAh=AAh: e.tensor_tensor(out=p13[:], in0=src_t, in1=AAh, op=ALU.mult),
                             reads=rd + bufs(c["AA"]), writes=bufs(p13))
                        S.op(eng, lambda e, p23=p23, src_sw=src_sw, BBh=BBh: e.tensor_tensor(out=p23[:], in0=src_sw, in1=BBh, op=ALU.mult),
                             reads=rd + bufs(c["BB"]), writes=bufs(p23))
                        S.op(eng, lambda e, p13=p13, p23=p23, xv=xv, i_=i_: e.tensor_tensor(
                            out=xv[:, :, :, :, i_], in0=p13[:], in1=p23[:], op=ALU.add), reads=bufs(p13, p23), writes=bufs(arr))
                for eng, qs in (("dve", slice(0, 8)), ("pool", slice(8, 16))):
                    B_ = bl[eng]
                    AAp, BBp, cc = B_["AAp"], B_["BBp"], B_["cc"]
                    c1, c2 = B_["c"]
                    xv = arr[:, qs, :, 0, 1:257].rearrange("p q r (b i) -> p q r b i", i=BL_)
                    S.op(eng, lambda e, cc=cc, qs=qs: e.tensor_copy(out=cc[:, :, :, 0], in_=arr[:, qs, :, 0, 0]), reads=bufs(arr), writes=bufs(cc))
                    for Bk in range(NB_):
                        S.op(eng, lambda e, c1=c1, cc=cc, AAp=AAp, Bk=Bk: e.tensor_tensor(
                            out=c1[:], in0=cc[:, :, :, Bk], in1=AAp[:, :, :, 15], op=ALU.mult), reads=bufs(cc, AAp), writes=bufs(c1))
                        S.op(eng, lambda e, c2=c2, cc=cc, BBp=BBp, Bk=Bk: e.tensor_tensor(
                            out=c2[:], in0=cc[:, :, ::-1, Bk], in1=BBp[:, :, :, 15], op=ALU.mult), reads=bufs(cc, BBp), writes=bufs(c2))
                        S.op(eng, lambda e, c1=c1, c2=c2: e.tensor_tensor(out=c1[:], in0=c1[:], in1=c2[:], op=ALU.add),
                             reads=bufs(c1, c2), writes=bufs(c1))
                        S.op(eng, lambda e, c1=c1, cc=cc, xv=xv, Bk=Bk: e.tensor_tensor(
                            out=cc[:, :, :, Bk + 1], in0=c1[:], in1=xv[:, :, :, Bk, 15], op=ALU.add), reads=bufs(c1, arr), writes=bufs(cc))
                for eng, qs in (("dve", slice(0, 8)), ("pool", slice(8, 16))):
                    B_ = bl[eng]
                    AAp, BBp, cc = B_["AAp"], B_["BBp"], B_["cc"]
                    xv = arr[:, qs, :, 0, 1:257].rearrange("p q r (b i) -> p q r b i", i=BL_)
                    hv = Hb[:, qs, :, 0, 1:257].rearrange("p q r (b i) -> p q r b i", i=BL_)
                    fr = B_["f"]
                    S.op(eng, lambda e, qs=qs: e.tensor_copy(out=Hb[:, qs, :, 0, 0], in_=arr[:, qs, :, 0, 0]), reads=bufs(arr), writes=bufs(Hb))
                    pend = []
                    for i_ in range(BL_ + 1):
                        if i_ < BL_:
                            f1, f2 = fr[i_ % 2]
                            S.op(eng, lambda e, f1=f1, cc=cc, AAp=AAp, i_=i_: e.tensor_tensor(
                                out=f1[:], in0=cc[:, :, :, 0:NB_], in1=AAp[:, :, :, i_:i_ + 1].to_broadcast([128, 8, 2, NB_]), op=ALU.mult),
                                reads=bufs(cc, AAp), writes=bufs(f1))
                            S.op(eng, lambda e, f2=f2, cc=cc, BBp=BBp, i_=i_: e.tensor_tensor(
                                out=f2[:], in0=cc[:, :, ::-1, 0:NB_], in1=BBp[:, :, :, i_:i_ + 1].to_broadcast([128, 8, 2, NB_]), op=ALU.mult),
                                reads=bufs(cc, BBp), writes=bufs(f2))
                        if i_ >= 1:
                            j_ = i_ - 1
                            f1, f2 = fr[j_ % 2]
                            S.op(eng, lambda e, f1=f1, f2=f2: e.tensor_tensor(out=f1[:], in0=f1[:], in1=f2[:], op=ALU.add),
                                 reads=bufs(f1, f2), writes=bufs(f1))
                            S.op(eng, lambda e, f1=f1, xv=xv, hv=hv, j_=j_: e.tensor_tensor(
                                out=hv[:, :, :, :, j_], in0=f1[:], in1=xv[:, :, :, :, j_], op=ALU.add), reads=bufs(f1, arr), writes=bufs(Hb))
            if not is_lat:
                for d_ in range(2):
                    S.op("act", lambda e, d_=d_, gs=gs: e.activation(
                        out=fin[:, :, d_, :, gs], in_=arr[:, d_:16:2, :, :, nch].rearrange("p g r s -> p s r g"), func=AF.Copy),
                        reads=bufs(arr), writes=bufs(fin))
            for g0 in range(0, 16, GPB):
                pb = banks.next()
                for gg in range(GPB):
                    gi = g0 + gg
                    gl, par = gi // 2, gi % 2
                    rows = slice(par * 64, (par + 1) * 64)
                    yreg = pb[:, gg * C_:(gg + 1) * C_]
                    S.op("pe", lambda e, yreg=yreg, gi=gi: e.matmul(yreg, lhsT=Tw[:, gi, :], rhs=X[:, gi, :], start=True, stop=False),
                         reads=bufs(Tw, X), writes=bufs(pb))
                    for d_ in range(2):
                        for r_ in range(2):
                            hsl = Hb[rows, gl * 2 + d_, r_, :, 0:nch]
                            if d_ == 1:
                                hsl = hsl[:, :, ::-1]
                            S.op("pe", lambda e, yreg=yreg, gl=gl, d_=d_, r_=r_, rows=rows, hsl=hsl: e.matmul(
                                yreg, lhsT=W2w[rows, gl, d_, r_, :], rhs=hsl, start=False, stop=(d_ == 1 and r_ == 1)),
                                reads=bufs(W2w, Hb), writes=bufs(pb))
                for gg in range(GPB):
                    gi = g0 + gg
                    S.op("dve", lambda e, pb=pb, gg=gg, gi=gi, b=b: e.scalar_tensor_tensor(
                        out=Ysb[:, gi, :], in0=X[:, gi, :], scalar=c["dX"][:, 16 * b + gi:16 * b + gi + 1],
                        in1=pb[:, gg * C_:(gg + 1) * C_], op0=ALU.mult, op1=ALU.add),
                        reads=bufs(pb, X, c["dX"]), writes=bufs(Ysb))
            for blk in range(2):
                for t0 in range(0, 8, GPB):
                    psel = banks.next()
                    for tt_ in range(GPB):
                        t = t0 + tt_
                        for g_ in range(8):
                            S.op("pe", lambda e, psel=psel, tt_=tt_, t=t, g_=g_, blk=blk: e.matmul(
                                psel[:, tt_ * C_:(tt_ + 1) * C_], lhsT=c["Wsel"][:, t, 112 - 16 * g_:240 - 16 * g_],
                                rhs=Ysb[:, blk * 8 + g_, :], start=(g_ == 0), stop=(g_ == 7)),
                                reads=bufs(c["Wsel"], Ysb), writes=bufs(psel))
                    S.op("act", lambda e, psel=psel, blk=blk, t0=t0: e.activation(
                        out=ygst[:, blk, t0 * C_:t0 * C_ + 512], in_=psel[:], func=AF.Gelu), reads=bufs(psel), writes=bufs(ygst))
            S.dma("sp", yscr[2 * b:2 * b + 2, :, tok0:tok0 + Tn].rearrange("b p t -> p b t"), ygst[:], reads=bufs(ygst))

        if not is_lat:
            for sq in range(4):
                pf = banks.next()
                for d_ in range(2):
                    for r_ in range(2):
                        j_ = d_ * 2 + r_
                        S.op("pe", lambda e, pf=pf, sq=sq, d_=d_, r_=r_, j_=j_: e.transpose(
                            out=pf[0:64, j_ * 128:(j_ + 1) * 128], in_=fin[:, sq, d_, r_, :], identity=identf[:]),
                            reads=bufs(fin, identf), writes=bufs(pf))
                st_ = fst.next()
                S.op("act", lambda e, pf=pf, st_=st_: e.activation(out=st_[:].rearrange("p a b -> p (a b)"), in_=pf[0:64, :], func=AF.Copy),
                     reads=bufs(pf), writes=bufs(st_))
                S.dma("sp", new_s5[sq].rearrange("d r (gp two) n -> gp (d r) (two n)", two=2), st_[:], reads=bufs(st_))

    def s5_glu(c, tok0, sub, yT):
        ygT = k.at([128, 16, 1024], BF16)
        S.dma("sp", ygT[:], yscr[:, :, tok0 + sub * 1024:tok0 + (sub + 1) * 1024].rearrange("b p t -> p b t"), writes=bufs(ygT))
        wgr = k.aring(2, [128, 16, 128], BF16)
        sgr = k.aring(2, [128, 512], F32)
        szr = k.aring(2, [128, 512], F32)
        for blk in range(16):
            wg = wgr.next()
            S.dma("pool", wg[:], s5_w_glu.rearrange("(k p) n -> p k n", p=128)[:, :, blk * 128:(blk + 1) * 128], writes=bufs(wg))
            if blk % 4 == 0:
                wz = load_w(s5_w_in, E + blk * 128, 512)
            co = (blk % 4) * 128
            for q in range(2):
                p0 = sub * 1024 + q * 512
                pg_ = banks.next()
                for kk in range(16):
                    S.op("pe", lambda e, pg_=pg_, kk=kk, wg=wg, q=q: e.matmul(
                        pg_[:], lhsT=wg[:, kk, :], rhs=ygT[:, kk, q * 512:(q + 1) * 512], start=(kk == 0), stop=(kk == 15)),
                        reads=bufs(wg, ygT), writes=bufs(pg_))
                sg = sgr.next()
                S.op("act", lambda e, pg_=pg_, sg=sg, blk=blk: e.activation(
                    out=sg[:], in_=pg_[:], func=AF.Sigmoid, bias=c["bgT"][:, blk:blk + 1]), reads=bufs(pg_, c["bgT"]), writes=bufs(sg))
                pz = banks.next()
                for kk in range(8):
                    S.op("pe", lambda e, pz=pz, kk=kk, wz=wz, co=co, p0=p0: e.matmul(
                        pz[:], lhsT=wz[:, kk, co:co + 128], rhs=hT[:, kk, p0:p0 + 512], start=(kk == 0), stop=(kk == 7)),
                        reads=bufs(wz, hT), writes=bufs(pz))
                sz = szr.next()
                S.op("act", lambda e, pz=pz, sz=sz: e.activation(out=sz[:], in_=pz[:], func=AF.Silu), reads=bufs(pz), writes=bufs(sz))
                S.op("pool", lambda e, sg=sg, blk=blk, q=q: e.tensor_tensor(
                    out=sg[:], in0=sg[:], in1=ygT[:, blk, q * 512:(q + 1) * 512], op=ALU.mult), reads=bufs(sg, ygT), writes=bufs(sg))
                S.op("dve", lambda e, sg=sg, sz=sz, blk=blk, p0=p0: e.tensor_tensor(
                    out=yT[:, blk, p0:p0 + 512], in0=sg[:], in1=sz[:], op=ALU.mult), reads=bufs(sg, sz), writes=bufs(yT))

    def std_tiles(tok0, n):
        return [(rows_std(tok0 + i * 128), i * 128) for i in range(n)]

    units = [(0, 8, 0), (1024, 8, 1), (2048, 8, 1)]
    src = cfg.get("src", None) and inp("xsrc", [NTOK, D]) or xin
    for li in layers:
        last = final and (li == layers[-1])
        dst = xres
        k.areset()
        phase_a(li)
        if li == 1:
            L["yT"] = k.at([128, 16, 1024], BF16)
            c = gmlp_consts()
            for (tok0, nt, cond) in units:
                tiles = std_tiles(tok0, nt)
                phase_b(src, tiles, cond)
                gmlp_unit(c, nt)
                load_wout(li)
                phase_d(src, dst, tiles, cond, last)
        if li == 0:
            c = ssd_consts()
            m0 = k.amark()
            for (tok0, nt, nseq, cond) in ((0, 8, 4, 0), (1024, 16, 1, 1)):
                tiles = std_tiles(tok0, nt)
                phase_b(src, tiles, cond)
                rstd = ssd_unit(c, tok0, nt, nseq, cond == 1)
                S.op("act", lambda e, rstd=rstd, nt=nt: e.activation(out=rstd_keep[:, 0:nt], in_=rstd[:], func=AF.Copy),
                     reads=bufs(rstd), writes=bufs(rstd_keep))
                k.arestore(m0)
                load_wout(li)
                phase_d(src, dst, tiles, cond, last, scale_t=lambda i: (rstd_keep[:, i:i + 1], rstd_keep.b), ytok0=tok0)
                S.barrier()
        if li == 2:
            c = s5_prep()
            m0 = k.amark()
            for (tok0, is_lat, cond) in ((0, False, 0), (1024, True, 1)):
                nsub = 2 if is_lat else 1
                tiles = []
                for sub in range(nsub):
                    tiles += s5_tiles(tok0, sub, is_lat)
                phase_b(src, tiles, cond)
                s5_unit(c, tok0, is_lat)
                k.arestore(m0)
                L["yT"] = k.at([128, 16, 1024 * nsub], BF16)
                m1 = k.amark()
                for sub in range(nsub):
                    s5_glu(c, tok0, sub, L["yT"])
                    k.arestore(m1)
                load_wout(li)
                phase_d(src, dst, tiles, cond, last)
                k.arestore(m0)
        if li == 3:
            L["yT"] = k.at([128, 16, 2048], BF16)
            tiles = std_tiles(0, 8)
            phase_b(src, tiles, 0)
            m_ = k.amark()
            if not cfg.get("skip_ctx"):
                nat_ctx_unit()
            k.arestore(m_)
            load_wout(li)
            phase_d(src, dst, tiles, 0, last)
            S.barrier()
            tiles = std_tiles(1024, 16)
            phase_b(src, tiles, 1)
            m_ = k.amark()
            nat_lat_unit()
            if not cfg.get("skip_d"):
                k.arestore(m_)
            load_wout(li)
            phase_d(src, dst, tiles, 1, last)
        src = xres
    if not final and not cfg.get("skip_d"):
        S.barrier()
        xring = k.aring(3, [128, D], F32)
        for i in range(NTOK // 128):
            xt = xring.next()
            S.dma("sp", xt[:], xres[i * 128:(i + 1) * 128, :], writes=bufs(xt))
            S.dma("sp", y_out[i * 128:(i + 1) * 128, :], xt[:], reads=bufs(xt))
    S.emit(es)
    return nc, es


def host_inputs(inputs, core):
    f = np.ascontiguousarray
    m = {}
    m["xin"] = f(np.concatenate([inputs["x_prompt"][4 * core:4 * core + 4].reshape(NP_TOK, D),
                                 inputs["x_sample"][core % 2]], axis=0))
    m["cvec"] = f(np.stack([inputs["c_ctx"], inputs["c"][core % 2]], axis=0))
    for nm in ["norm_g", "w_mod", "b_mod", "w_out", "final_g"]:
        m[nm] = f(inputs[nm])
    m["mlp_w_in"] = f(inputs["mlp_w_in"][0])
    m["mlp_ln_g"] = f(inputs["mlp_ln_g"][0])
    m["mlp_ln_b"] = f(inputs["mlp_ln_b"][0])
    m["mlp_w_sT"] = f(np.transpose(inputs["mlp_w_s"][0], (0, 2, 1)))
    m["mlp_b_s"] = f(inputs["mlp_b_s"][0])
    m["ssd_w_in"] = f(inputs["ssd_w_in"][0])
    m["ssd_conv_w"] = f(inputs["ssd_conv_w"][0])
    m["ssd_conv_b"] = f(inputs["ssd_conv_b"][0])
    m["ssd_dt_bias"] = f(inputs["ssd_dt_bias"][0].reshape(64))
    m["ssd_a_log"] = f(inputs["ssd_a_log"][0].reshape(64))
    m["ssd_d"] = f(inputs["ssd_d"][0])
    m["ssd_norm_g"] = f(inputs["ssd_norm_g"][0])
    m["state_ssd"] = f(inputs["state_ssd"][core % 2, 0])
    m["s5_w_in"] = f(inputs["s5_w_in"][0])

    def pl(a):
        sh = a.shape[:-2]
        a = a.reshape(sh + (64, 2, 64))
        return np.moveaxis(a, -3, -1).reshape(sh + (128, 64))
    m["s5_lam"] = f(np.stack([pl(inputs["s5_lam_re"][0]), pl(inputs["s5_lam_im"][0])], 0))
    m["s5_lstep"] = f(pl(np.broadcast_to(inputs["s5_log_step"][0][:, :, None], (2, 128, 64))))

    def plj(a):
        a = a.reshape(2, 64, 2, 64, 16)
        return np.transpose(a, (0, 2, 3, 1, 4)).reshape(2, 128, 64, 16)
    m["s5_B"] = f(np.stack([plj(inputs["s5_b_re"][0]), plj(inputs["s5_b_im"][0])], 0))
    m["s5_C"] = f(np.stack([plj(np.transpose(inputs["s5_c_re"][0], (0, 1, 3, 2))),
                            plj(np.transpose(inputs["s5_c_im"][0], (0, 1, 3, 2)))], 0))
    m["s5_h0"] = f(pl(inputs["state_s5"][core % 2, 0]))
    m["s5_d"] = f(inputs["s5_d"][0])
    m["s5_w_glu"] = f(inputs["s5_w_glu"][0])
    m["s5_b_glu"] = f(inputs["s5_b_glu"][0])
    m["nat_w_in"] = f(inputs["nat_w_in"][0])
    m["rpbg"] = rpb_gather(inputs["nat_rpb"][0])
    m["natmask"] = nat_masks()
    m["cache_k"] = f(inputs["cache_k"][core % 2, 0])
    m["cache_v"] = f(inputs["cache_v"][core % 2, 0])
    return m


def rpb_gather(rpb):
    qc = np.arange(64)[:, None]
    kc = np.arange(64)[None, :]
    ci = np.clip(kc - qc + 15, 0, 30)
    out = np.zeros((32, 128, 16, 64), np.float32)
    g = rpb[:, :, ci]
    g = np.transpose(g, (0, 2, 1, 3))
    out[:, 0:64, 0:15, :] = g
    out[:, 64:128, 1:16, :] = g
    return np.ascontiguousarray(out.reshape(32, 128, 1024))


def nat_masks():
    NEG = -30000.0 * 8.0
    qc = np.arange(64)
    cs = np.clip(qc - 8, 0, 48)
    kc = np.arange(64)
    col_ok = (kc[None, :] >= cs[:, None]) & (kc[None, :] < cs[:, None] + 16)
    m = np.zeros((3, 128, 9, 64), np.float32)
    colm = np.where(col_ok, 0.0, NEG).astype(np.float32)
    m[:, 0:64] += colm[None, :, None, :]
    m[:, 64:128] += colm[None, :, None, :]
    m[0, 0:64, 8, :] = NEG
    m[0, 64:128, 0, :] = NEG
    m[1, :, 8, :] = NEG
    return np.ascontiguousarray(m.reshape(3, 128, 576))


def kernel(**inputs):
    inputs = {k_: np.asarray(v) for k_, v in inputs.items()}
    nc, es = build({})
    with es:
        in_maps = [host_inputs(inputs, c) for c in range(8)]
        res = run_bass_kernel_spmd(nc, in_maps, core_ids=list(range(8)))
    r = res.results
    y_prompt = np.concatenate([r[c]["y_out"][:NP_TOK].reshape(4, 256, D) for c in range(8)], axis=0)
    y_sample = np.stack([r[c]["y_out"][NP_TOK:] for c in range(2)], axis=0)
    new_ssd = np.concatenate([r[c]["new_ssd"] for c in range(8)], axis=0)[:, None]
    new_s5 = np.concatenate([r[c]["new_s5"] for c in range(8)], axis=0)[:, None]
    new_k = np.concatenate([r[c]["new_k"] for c in range(8)], axis=0)[:, None]
    new_v = np.concatenate([r[c]["new_v"] for c in range(8)], axis=0)[:, None]
    return (y_prompt.astype(np.float32), y_sample.astype(np.float32), np.ascontiguousarray(new_ssd, dtype=np.float32),
            np.ascontiguousarray(new_s5, dtype=np.float32), np.ascontiguousarray(new_k, dtype=np.float32),
            np.ascontiguousarray(new_v, dtype=np.float32))
```

```python
import numpy as np
from contextlib import ExitStack
import concourse.bass as bass
import concourse.mybir as mybir
from concourse.bass_utils import run_bass_kernel_spmd

F32 = mybir.dt.float32
BF16 = mybir.dt.bfloat16
AF = mybir.ActivationFunctionType
ALU = mybir.AluOpType
AX = mybir.AxisListType

D = 1024
E = 2048
NP_TOK = 1024
NS_TOK = 2048
NTOK = NP_TOK + NS_TOK
EPS = 1e-6
COMPUTE = ("pe", "act", "dve", "pool")
NDMASEM = 12
SAME_ENGINE_SYNC = True


class Buf:
    __slots__ = ("lw", "rd")

    def __init__(self):
        self.lw = None
        self.rd = {}


class Sched:
    def __init__(self, nc):
        self.nc = nc
        self.ops = {e: [] for e in COMPUTE + ("sp",)}
        self.cnt = {e: 0 for e in COMPUTE}
        self.seen = {e: {} for e in COMPUTE + ("sp",)}
        self.dma_slot = {}
        self.dma_val = {}
        self.sems = {}
        self.refd = {e: set() for e in COMPUTE}

    def _deps(self, eng, reads, writes):
        deps = {}

        def add(tok):
            if tok is None:
                return
            k, v = tok
            if deps.get(k, 0) < v:
                deps[k] = v

        for r in reads:
            add(r.lw)
        for w in writes:
            add(w.lw)
            for k, v in w.rd.items():
                add((k, v))
        out = []
        seen = self.seen[eng]
        for k, v in deps.items():
            if k == eng and (eng == "pe" or not SAME_ENGINE_SYNC):
                continue
            if seen.get(k, 0) >= v:
                continue
            seen[k] = v
            out.append((k, v))
            if isinstance(k, str):
                self.refd[k].add(v)
        return out

    def _mark(self, tok, reads, writes):
        k, v = tok
        for r in reads:
            if r.rd.get(k, 0) < v:
                r.rd[k] = v
        for w in writes:
            w.lw = tok
            w.rd = {}

    def op(self, eng, fn, reads=(), writes=()):
        waits = self._deps(eng, reads, writes)
        self.cnt[eng] += 1
        tok = (eng, self.cnt[eng])
        self.ops[eng].append((waits, fn, tok, 1))
        self._mark(tok, reads, writes)

    def dma(self, q, out, in_, reads=(), writes=()):
        slot = self.dma_slot.get(q, 0)
        self.dma_slot[q] = (slot + 1) % NDMASEM
        key = ("dma", q, slot)
        prev = self.dma_val.get(key, 0)
        waits = self._deps(q, reads, writes)
        if prev > 0 and self.seen[q].get(key, 0) < prev:
            self.seen[q][key] = prev
            waits.append((key, prev))
        val = prev + 16
        self.dma_val[key] = val
        tok = (key, val)

        def fn(e, out=out, in_=in_):
            return e.dma_start(out=out, in_=in_, allow_slow_non_contiguous=True)

        self.ops[q].append((waits, fn, tok, 16))
        self._mark(tok, reads, writes)

    def barrier(self):
        targets = [(e, self.cnt[e]) for e in COMPUTE if self.cnt[e] > 0]
        targets += [(key, v) for key, v in self.dma_val.items()]
        for eng in COMPUTE + ("sp",):
            waits = []
            for key, v in targets:
                if key == eng:
                    continue
                if self.seen[eng].get(key, 0) < v:
                    self.seen[eng][key] = v
                    waits.append((key, v))
                    if isinstance(key, str):
                        self.refd[key].add(v)
            if waits:
                self.ops[eng].append((waits, None, None, 0))

    def emit(self, es, final_wait_engine="sp"):
        nc = self.nc
        keys = list(COMPUTE)
        for q in self.dma_slot:
            for s in range(NDMASEM):
                if ("dma", q, s) in self.dma_val:
                    keys.append(("dma", q, s))
        for k in keys:
            nm = k if isinstance(k, str) else "d_%s_%d" % (k[1], k[2])
            self.sems[k] = es.enter_context(nc.semaphore("s_" + nm))
        fin = []
        for k in keys:
            v = self.cnt[k] if isinstance(k, str) else self.dma_val[k]
            if v > 0 and k != final_wait_engine:
                fin.append((k, v))
                if isinstance(k, str):
                    self.refd[k].add(v)
        rank = {}
        for e_ in COMPUTE:
            r_ = {}
            for n_, idx in enumerate(sorted(self.refd[e_])):
                r_[idx] = n_ + 1
            rank[e_] = r_

        def semval(k, v):
            return rank[k][v] if isinstance(k, str) else v
        block = es.enter_context(nc.Block())

        def run(e, name):
            for waits, fn, tok, inc in self.ops[name]:
                if fn is None:
                    for k, v in waits:
                        e.wait_ge(self.sems[k], semval(k, v))
                    continue
                NW = 1
                for k, v in waits[NW:]:
                    e.wait_ge(self.sems[k], semval(k, v))
                ins = fn(e)
                for k, v in waits[:NW]:
                    ins._wait_ge(self.sems[k], semval(k, v))
                if not isinstance(tok[0], str) or tok[1] in self.refd[tok[0]]:
                    ins.then_inc(self.sems[tok[0]], inc)
            if name == final_wait_engine:
                for k, v in fin:
                    e.wait_ge(self.sems[k], semval(k, v))

        @block.tensor
        def _(e):
            run(e, "pe")

        @block.scalar
        def _(e):
            run(e, "act")

        @block.vector
        def _(e):
            run(e, "dve")

        @block.gpsimd
        def _(e):
            run(e, "pool")

        @block.sync
        def _(e):
            run(e, "sp")


class T:
    __slots__ = ("t", "b")

    def __init__(self, t):
        self.t = t
        self.b = Buf()

    def __getitem__(self, k):
        return self.t[k]


class Ring:
    def __init__(self, tiles):
        self.tiles = tiles
        self.i = 0

    def next(self):
        t = self.tiles[self.i]
        self.i = (self.i + 1) % len(self.tiles)
        return t


class K:
    def __init__(self, nc, es):
        self.nc = nc
        self.es = es
        self.S = Sched(nc)
        self.n = 0

    def sb(self, shape, dt, name=None):
        self.n += 1
        return T(self.es.enter_context(self.nc.sbuf_tensor(name or "sb%d" % self.n, list(shape), dt)))

    def ring(self, n, shape, dt):
        return Ring([self.sb(shape, dt) for _ in range(n)])

    def psb(self, shape, dt):
        self.n += 1
        return T(self.es.enter_context(self.nc.psum_tensor("ps%d" % self.n, list(shape), dt)))

    def init_arena(self, nbytes):
        self.arena = self.es.enter_context(self.nc.sbuf_tensor("arena", [128, nbytes // 2], BF16))
        self.asize = nbytes
        self.aoff = 0
        self.alog = []

    def areset(self):
        self.S.barrier()
        self.aoff = 0

    def at(self, shape, dt):
        esz = 4 if dt == F32 else 2
        n = 1
        for d_ in shape[1:]:
            n *= d_
        nb = (n * esz + 63) // 64 * 64
        assert self.aoff + nb <= self.asize, ("arena overflow", self.aoff, nb, self.asize)
        ap = self.arena[0:shape[0], self.aoff // 2:(self.aoff + n * esz) // 2]
        if dt == F32:
            ap = ap.bitcast(F32)
        if len(shape) > 2:
            names = ["d%d" % i for i in range(len(shape) - 1)]
            kw = {names[i]: shape[i + 1] for i in range(len(names) - 1)}
            ap = ap.rearrange("p (%s) -> p %s" % (" ".join(names), " ".join(names)), **kw)
        self.alog.append((self.aoff, tuple(shape), dt))
        self.aoff += nb
        return T(ap)

    def amark(self):
        return self.aoff

    def arestore(self, m):
        self.S.barrier()
        self.aoff = m

    def aring(self, n, shape, dt):
        return Ring([self.at(shape, dt) for _ in range(n)])

    def dram(self, name, shape, dt, kind="Internal"):
        return self.nc.dram_tensor(name, list(shape), dt, kind=kind).ap()


def bufs(*ts):
    return [t.b for t in ts]


def build(cfg):
    layers = cfg.get("layers", [0, 1, 2, 3])
    final = cfg.get("final", True)
    nc = bass.Bass("TRN2", target_bir_lowering=False)
    es = ExitStack()
    k = K(nc, es)
    S = k.S
    I = {}

    def inp(name, shape):
        I[name] = k.dram(name, shape, F32, kind="ExternalInput")
        return I[name]

    xin = inp("xin", [NTOK, D])
    cvec = inp("cvec", [2, D])
    norm_g = inp("norm_g", [4, D])
    w_mod = inp("w_mod", [4, D, 3 * D])
    b_mod = inp("b_mod", [4, 3 * D])
    w_out = inp("w_out", [4, E, D])
    final_g = inp("final_g", [D])
    mlp_w_in = inp("mlp_w_in", [D, 3 * E])
    mlp_ln_g = inp("mlp_ln_g", [E])
    mlp_ln_b = inp("mlp_ln_b", [E])
    mlp_w_sT = inp("mlp_w_sT", [8, 128, 128])
    mlp_b_s = inp("mlp_b_s", [8, 128])
    ssd_w_in = inp("ssd_w_in", [D, 6208])
    ssd_conv_w = inp("ssd_conv_w", [5, 4096])
    ssd_conv_b = inp("ssd_conv_b", [4096])
    ssd_dt_bias = inp("ssd_dt_bias", [64])
    ssd_a_log = inp("ssd_a_log", [64])
    ssd_d = inp("ssd_d", [32])
    ssd_norm_g = inp("ssd_norm_g", [E])
    state_ssd = inp("state_ssd", [2, 32, 64, 128])
    new_ssd = k.dram("new_ssd", [4, 2, 32, 64, 128], F32, kind="ExternalOutput")
    yscr = k.dram("yscr", [16, 128, NTOK], BF16)
    s5_w_in = inp("s5_w_in", [D, 2 * E])
    s5_lam = inp("s5_lam", [2, 2, 128, 64])
    s5_lstep = inp("s5_lstep", [2, 128, 64])
    s5_B = inp("s5_B", [2, 2, 128, 64, 16])
    s5_C = inp("s5_C", [2, 2, 128, 64, 16])
    s5_h0 = inp("s5_h0", [2, 2, 128, 64])
    s5_d = inp("s5_d", [E])
    s5_w_glu = inp("s5_w_glu", [E, E])
    s5_b_glu = inp("s5_b_glu", [E])
    new_s5 = k.dram("new_s5", [4, 2, 2, 128, 64], F32, kind="ExternalOutput")
    Tscr = k.dram("Tscr", [8, 128, 16 * 128], BF16)
    VTscr = k.dram("VTscr", [8, 128, 8 * 4 * 128], BF16)
    W2scr = k.dram("W2scr", [8, 128, 8 * 4 * 128], BF16)
    nat_w_in = inp("nat_w_in", [D, 4 * E])
    rpbg = inp("rpbg", [32, 128, 1024])
    natmask = inp("natmask", [3, 128, 576])
    cache_k = inp("cache_k", [32, 256, 64])
    cache_v = inp("cache_v", [32, 256, 64])
    new_k = k.dram("new_k", [4, 32, 256, 64], F32, kind="ExternalOutput")
    new_v = k.dram("new_v", [4, 32, 256, 64], F32, kind="ExternalOutput")
    y_out = k.dram("y_out", [NTOK, D], F32, kind="ExternalOutput")
    xres = k.dram("xres", [NTOK, D], F32)
    dma_done = Buf()

    identf = k.sb([128, 128], F32)
    identb = k.sb([128, 128], BF16)
    onesf = k.sb([128, 128], F32)
    S.op("pool", lambda e: e.memset(identf[:], 0.0), writes=bufs(identf))
    S.op("pool", lambda e: e.affine_select(out=identf[:], in_=identf[:], compare_op=ALU.not_equal, fill=1.0,
                                           base=0, pattern=[[-1, 128]], channel_multiplier=1),
         reads=bufs(identf), writes=bufs(identf))
    S.op("dve", lambda e: e.tensor_copy(out=identb[:], in_=identf[:]), reads=bufs(identf), writes=bufs(identb))
    S.op("pool", lambda e: e.memset(onesf[:], 1.0), writes=bufs(onesf))

    banks = Ring([k.psb([128, 512], F32) for _ in range(8)])

    hT = k.sb([128, 8, 2048], BF16, "hT")
    wo = T(hT.t)
    wo.b = hT.b
    wo_view = hT.t[:].rearrange("p k t -> p (k t)").rearrange("p (k n) -> p k n", k=16)
    wring = k.ring(3, [128, 8, 512], BF16)
    junk = k.sb([128, D], BF16)
    small = k.ring(8, [128, 8], F32)
    rstd_keep = k.sb([128, 16], F32)
    k.init_arena(136 * 1024)
    L = {}

    cf = k.sb([128, 8, 2], F32)
    cb = k.sb([128, 8, 2], BF16)
    for c_ in range(2):
        S.dma("sp", cf[:, :, c_], cvec[c_].rearrange("(k p) -> p k", p=128), writes=bufs(cf))
    S.op("act", lambda e: e.activation(out=cb[:], in_=cf[:], func=AF.Silu), reads=bufs(cf), writes=bufs(cb))

    modT = k.sb([128, 16, 2], F32)
    bmodT = k.sb([128, 16], F32)
    ngT = k.sb([128, 8], F32)
    Asc = k.sb([128, 8, 2], F32)
    gate_bc = [k.sb([128, D], F32), k.sb([128, D], F32)]
    sel = [k.sb([2, 128], F32), k.sb([2, 128], F32)]
    for c in range(2):
        S.op("pool", lambda e, c=c: e.memset(sel[c][:], 0.0), writes=bufs(sel[c]))
        S.op("pool", lambda e, c=c: e.affine_select(out=sel[c][:], in_=sel[c][:], compare_op=ALU.not_equal, fill=1.0,
                                                     base=-c, pattern=[[0, 128]], channel_multiplier=1),
             reads=bufs(sel[c]), writes=bufs(sel[c]))

    def load_w(wap, c0, n, q="pool"):
        wt = wring.next()
        S.dma(q, wt[:, :, 0:n], wap.rearrange("(k p) n -> p k n", p=128)[:, :, c0:c0 + n], writes=bufs(wt))
        return wt

    def phase_a(li):
        ma_ = k.amark()
        gate2 = k.at([2, D], F32)
        bgate2 = k.at([2, D], F32)
        S.dma("sp", bmodT[:], b_mod[li, 0:2 * D].rearrange("(c p) -> p c", p=128), writes=bufs(bmodT))
        S.dma("sp", ngT[:], norm_g[li].rearrange("(c p) -> p c", p=128), writes=bufs(ngT))
        S.dma("sp", bgate2[:], b_mod[li, 2 * D:3 * D].partition_broadcast(2), writes=bufs(bgate2))
        for blk in range(4):
            wt = load_w(w_mod[li], blk * 512, 512)
            for cc in range(4):
                ch = blk * 4 + cc
                pb = banks.next()
                for kk in range(8):
                    S.op("pe", lambda e, pb=pb, wt=wt, cc=cc, kk=kk: e.matmul(
                        pb[:, 0:2], lhsT=wt[:, kk, cc * 128:(cc + 1) * 128], rhs=cb[:, kk, :],
                        start=(kk == 0), stop=(kk == 7)), reads=bufs(wt, cb), writes=bufs(pb))
                S.op("dve", lambda e, pb=pb, ch=ch: e.tensor_scalar(
                    out=modT[:, ch, :], in0=pb[:, 0:2], scalar1=bmodT[:, ch:ch + 1], scalar2=None, op0=ALU.add),
                    reads=bufs(pb, bmodT), writes=bufs(modT))
        S.op("dve", lambda e: e.tensor_scalar(out=Asc[:], in0=modT[:, 8:16, :], scalar1=1.0, scalar2=None, op0=ALU.add),
             reads=bufs(modT), writes=bufs(Asc))
        S.op("dve", lambda e: e.tensor_tensor(out=Asc[:], in0=Asc[:], in1=ngT[:].unsqueeze(2).to_broadcast([128, 8, 2]),
                                              op=ALU.mult), reads=bufs(Asc, ngT), writes=bufs(Asc))
        for blk in range(2):
            wt = load_w(w_mod[li], 2 * D + blk * 512, 512)
            pb = banks.next()
            for kk in range(8):
                S.op("pe", lambda e, pb=pb, wt=wt, kk=kk: e.matmul(
                    pb[0:2, :], lhsT=cb[:, kk, :], rhs=wt[:, kk, :], start=(kk == 0), stop=(kk == 7)),
                    reads=bufs(wt, cb), writes=bufs(pb))
            S.op("dve", lambda e, pb=pb, blk=blk: e.tensor_tensor(
                out=gate2[:, blk * 512:(blk + 1) * 512], in0=pb[0:2, :], in1=bgate2[:, blk * 512:(blk + 1) * 512],
                op=ALU.add), reads=bufs(pb, bgate2), writes=bufs(gate2))
        for c in range(2):
            for blk in range(2):
                pb = banks.next()
                S.op("pe", lambda e, pb=pb, c=c, blk=blk: e.matmul(
                    pb[:], lhsT=sel[c][:], rhs=gate2[:, blk * 512:(blk + 1) * 512], start=True, stop=True),
                    reads=bufs(sel[c], gate2), writes=bufs(pb))
                S.op("act", lambda e, pb=pb, c=c, blk=blk: e.activation(
                    out=gate_bc[c][:, blk * 512:(blk + 1) * 512], in_=pb[:], func=AF.Copy),
                    reads=bufs(pb), writes=bufs(gate_bc[c]))
        k.arestore(ma_)

    def rows_std(tok0):
        return lambda src: src[tok0:tok0 + 128, :]

    def rms_stats(xt):
        st = small.next()
        S.op("act", lambda e: e.activation(out=junk[:], in_=xt[:], func=AF.Square, accum_out=st[:, 0:1]),
             reads=bufs(xt), writes=bufs(junk, st))
        S.op("dve", lambda e: e.tensor_scalar(out=st[:, 0:1], in0=st[:, 0:1], scalar1=1.0 / D, scalar2=EPS,
                                              op0=ALU.mult, op1=ALU.add), reads=bufs(st), writes=bufs(st))
        S.op("act", lambda e: e.activation(out=st[:, 0:1], in_=st[:, 0:1], func=AF.Sqrt), reads=bufs(st), writes=bufs(st))
        S.op("dve", lambda e: e.reciprocal(out=st[:, 0:1], in_=st[:, 0:1]), reads=bufs(st), writes=bufs(st))
        return st

    def phase_b(src, tiles, cond):
        m_ = k.amark()
        xring = k.aring(3, [128, D], F32)
        xnring = k.aring(2, [128, D], BF16)

        def load(i):
            xt = xring.next()
            S.dma("sp", xt[:], tiles[i][0](src), writes=bufs(xt))
            return xt
        nxt = load(0)
        for i in range(len(tiles)):
            xt = nxt
            if i + 1 < len(tiles):
                nxt = load(i + 1)
            col0 = tiles[i][1]
            st = rms_stats(xt)
            xn = xnring.next()
            S.op("dve", lambda e, xn=xn, xt=xt, st=st: e.tensor_scalar(out=xn[:], in0=xt[:], scalar1=st[:, 0:1],
                                                                   scalar2=None, op0=ALU.mult),
                 reads=bufs(xt, st), writes=bufs(xn))
            pb = banks.next()
            pv = pb[:].bitcast(BF16).rearrange("p (k t) -> p k t", k=8)
            for kk in range(8):
                S.op("pe", lambda e, pv=pv, xn=xn, kk=kk: e.transpose(out=pv[:, kk, :], in_=xn[:, kk * 128:(kk + 1) * 128],
                                                                    identity=identb[:]),
                     reads=bufs(xn, identb), writes=bufs(pb))
            for kk in range(8):
                S.op("act", lambda e, pv=pv, kk=kk, col0=col0: e.activation(
                    out=hT[:, kk, col0:col0 + 128], in_=pv[:, kk, :], func=AF.Identity,
                    scale=Asc[:, kk, cond:cond + 1], bias=modT[:, kk, cond:cond + 1]),
                    reads=bufs(pb, Asc, modT), writes=bufs(hT))
        k.arestore(m_)


    def load_wout(li):
        for h in range(2):
            S.dma("pool", wo_view[:, h * 8:(h + 1) * 8, :],
                  w_out[li].rearrange("(k p) n -> p k n", p=128)[:, h * 8:(h + 1) * 8, :], writes=bufs(wo))

    def phase_d(src, dst, tiles, cond, last, scale_t=None, ytok0=None):
        if cfg.get("skip_d"):
            return
        m_ = k.amark()
        xring = k.aring(3, [128, D], F32)
        tring = k.aring(2, [128, D], F32)
        if last:
            fg_bc = k.at([128, D], F32)
            S.dma("sp", fg_bc[:], final_g.partition_broadcast(128), writes=bufs(fg_bc))
        if ytok0 is None:
            yT = L["yT"]
        else:
            yring = k.aring(2, [128, 16, 512], BF16)
            yT = None

        def load(i):
            xt = xring.next()
            S.dma("sp", xt[:], tiles[i][0](src), writes=bufs(xt))
            return xt
        nxt = load(0)
        for i in range(len(tiles)):
            xt = nxt
            if i + 1 < len(tiles):
                nxt = load(i + 1)
            col0 = tiles[i][1]
            if ytok0 is not None:
                if i % 4 == 0:
                    yT = yring.next()
                    S.dma("sp", yT[:], yscr[:, :, ytok0 + tiles[i][1]:ytok0 + tiles[i][1] + 512].rearrange("b p t -> p b t"),
                          writes=bufs(yT))
                col0 = (i % 4) * 128
            tt = tring.next()
            for h in range(2):
                pb = banks.next()
                for kk in range(16):
                    S.op("pe", lambda e, pb=pb, kk=kk, h=h, col0=col0, yT=yT: e.matmul(
                        pb[:], lhsT=yT[:, kk, col0:col0 + 128], rhs=wo_view[:, kk, h * 512:(h + 1) * 512],
                        start=(kk == 0), stop=(kk == 15)), reads=bufs(yT, wo), writes=bufs(pb))
                if scale_t is None:
                    S.op("dve", lambda e, pb=pb, tt=tt, h=h: e.tensor_tensor(
                        out=tt[:, h * 512:(h + 1) * 512], in0=pb[:], in1=gate_bc[cond][:, h * 512:(h + 1) * 512],
                        op=ALU.mult), reads=bufs(pb, gate_bc[cond]), writes=bufs(tt))
                else:
                    sc = scale_t(i)
                    S.op("dve", lambda e, pb=pb, tt=tt, h=h, sc=sc: e.scalar_tensor_tensor(
                        out=tt[:, h * 512:(h + 1) * 512], in0=pb[:], scalar=sc[0], in1=gate_bc[cond][:, h * 512:(h + 1) * 512],
                        op0=ALU.mult, op1=ALU.mult), reads=bufs(pb, gate_bc[cond]) + [sc[1]], writes=bufs(tt))
            S.op("pool", lambda e, tt=tt, xt=xt: e.tensor_tensor(out=xt[:], in0=tt[:], in1=xt[:], op=ALU.add),
                 reads=bufs(tt, xt), writes=bufs(xt))
            if not last:
                S.dma("sp", tiles[i][0](dst), xt[:], reads=bufs(xt))
            else:
                st = rms_stats(xt)
                S.op("dve", lambda e, tt=tt, xt=xt, st=st: e.scalar_tensor_tensor(
                    out=tt[:], in0=xt[:], scalar=st[:, 0:1], in1=fg_bc[:], op0=ALU.mult, op1=ALU.mult),
                    reads=bufs(xt, st, fg_bc), writes=bufs(tt))
                S.dma("sp", tiles[i][0](y_out), tt[:], reads=bufs(tt))
        k.arestore(m_)

    def gmlp_consts():
        c = {}
        c["lngT"] = k.at([128, 16], F32)
        c["lnbT"] = k.at([128, 16], F32)
        c["wsT"] = k.at([128, 8, 128], BF16)
        c["wsTf"] = k.at([128, 8, 128], F32)
        c["bs_bc"] = k.at([128, 8, 128], F32)
        c["Bt"] = k.at([128, 16, 128], F32)
        S.dma("sp", c["lngT"][:], mlp_ln_g.rearrange("(c p) -> p c", p=128), writes=bufs(c["lngT"]))
        S.dma("sp", c["lnbT"][:], mlp_ln_b.rearrange("(c p) -> p c", p=128), writes=bufs(c["lnbT"]))
        S.dma("sp", c["wsTf"][:], mlp_w_sT.rearrange("g j i -> j g i"), writes=bufs(c["wsTf"]))
        S.dma("sp", c["bs_bc"][:].rearrange("p g i -> p (g i)"), mlp_b_s.rearrange("g i -> (g i)").partition_broadcast(128),
              writes=bufs(c["bs_bc"]))
        S.op("dve", lambda e: e.tensor_copy(out=c["wsT"][:], in_=c["wsTf"][:]), reads=bufs(c["wsTf"]), writes=bufs(c["wsT"]))
        for half in range(2):
            pb = banks.next()
            S.op("pe", lambda e, pb=pb, half=half: e.matmul(
                pb[:], lhsT=onesf[:], rhs=c["wsTf"][:, half * 4:(half + 1) * 4, :].rearrange("p g i -> p (g i)"),
                start=True, stop=True), reads=bufs(onesf, c["wsTf"]), writes=bufs(pb))
            for gg in range(4):
                g = half * 4 + gg
                for bb in range(2):
                    blk = g * 2 + bb
                    S.op("dve", lambda e, pb=pb, gg=gg, g=g, blk=blk: e.scalar_tensor_tensor(
                        out=c["Bt"][:, blk, :], in0=pb[:, gg * 128:(gg + 1) * 128], scalar=c["lnbT"][:, blk:blk + 1],
                        in1=c["bs_bc"][:, g, :], op0=ALU.mult, op1=ALU.add),
                        reads=bufs(pb, c["lnbT"], c["bs_bc"]), writes=bufs(c["Bt"]))
        c["vv"] = k.at([128, 8, E], BF16)
        c["gtmp"] = k.aring(2, [128, 512], F32)
        c["ug"] = k.aring(2, [128, 512], F32)
        c["zs"] = k.aring(2, [128, 512], F32)
        c["sg"] = k.aring(2, [128, 512], F32)
        c["st"] = k.at([128, 8, 8], F32)
        return c

    def gmlp_unit(c, ntile):
        yT = L["yT"]
        vv = c["vv"]
        stt = c["st"]
        for b in range(4):
            wv = load_w(mlp_w_in, E + b * 512, 512)
            for t in range(ntile):
                pb = banks.next()
                for kk in range(8):
                    S.op("pe", lambda e, pb=pb, t=t, wv=wv, kk=kk: e.matmul(
                        pb[:], lhsT=hT[:, kk, t * 128:(t + 1) * 128], rhs=wv[:, kk, :], start=(kk == 0), stop=(kk == 7)),
                        reads=bufs(hT, wv), writes=bufs(pb))
                gt = c["gtmp"].next()
                S.op("act", lambda e, pb=pb, b=b, t=t, gt=gt: e.activation(
                    out=gt[:], in_=pb[:], func=AF.Gelu, accum_out=stt[:, t, b:b + 1]),
                    reads=bufs(pb), writes=bufs(gt, stt))
                S.op("act", lambda e, b=b, t=t, gt=gt: e.activation(
                    out=junk[:, 0:512], in_=gt[:], func=AF.Square, accum_out=stt[:, t, 4 + b:5 + b]),
                    reads=bufs(gt), writes=bufs(junk, stt))
                S.op("pool", lambda e, b=b, t=t, gt=gt: e.tensor_copy(out=vv[:, t, b * 512:(b + 1) * 512], in_=gt[:]),
                     reads=bufs(gt), writes=bufs(vv))
        for t in range(ntile):
            st2 = small.next()
            S.op("dve", lambda e, t=t, st2=st2: e.tensor_reduce(
                out=st2[:, 0:2], in_=stt[:, t, :].rearrange("p (a b) -> p a b", a=2), axis=AX.X, op=ALU.add),
                reads=bufs(stt), writes=bufs(st2))
            S.op("dve", lambda e, st2=st2: e.tensor_scalar(out=st2[:, 0:2], in0=st2[:, 0:2], scalar1=1.0 / E, scalar2=None,
                                                           op0=ALU.mult), reads=bufs(st2), writes=bufs(st2))
            S.op("dve", lambda e, st2=st2: e.tensor_tensor(out=st2[:, 2:3], in0=st2[:, 0:1], in1=st2[:, 0:1], op=ALU.mult),
                 reads=bufs(st2), writes=bufs(st2))
            S.op("dve", lambda e, st2=st2: e.scalar_tensor_tensor(out=st2[:, 2:3], in0=st2[:, 2:3], scalar=-1.0, in1=st2[:, 1:2],
                                                                  op0=ALU.mult, op1=ALU.add), reads=bufs(st2), writes=bufs(st2))
            S.op("dve", lambda e, st2=st2: e.tensor_scalar(out=st2[:, 2:3], in0=st2[:, 2:3], scalar1=EPS, scalar2=None,
                                                           op0=ALU.add), reads=bufs(st2), writes=bufs(st2))
            S.op("act", lambda e, st2=st2: e.activation(out=st2[:, 2:3], in_=st2[:, 2:3], func=AF.Sqrt),
                 reads=bufs(st2), writes=bufs(st2))
            S.op("dve", lambda e, st2=st2: e.reciprocal(out=st2[:, 2:3], in_=st2[:, 2:3]), reads=bufs(st2), writes=bufs(st2))
            S.op("dve", lambda e, st2=st2, t=t: e.tensor_scalar(
                out=vv[:, t, :], in0=vv[:, t, :], scalar1=st2[:, 0:1], scalar2=st2[:, 2:3], op0=ALU.subtract, op1=ALU.mult),
                reads=bufs(vv, st2), writes=bufs(vv))
        nq = ntile // 4
        for blk in range(16):
            g = blk // 2
            if blk % 4 == 0:
                wu = load_w(mlp_w_in, blk * 128, 512)
                wz = load_w(mlp_w_in, 2 * E + blk * 128, 512)
            co = (blk % 4) * 128
            for q in range(nq):
                ug = c["ug"].next()
                zs = c["zs"].next()
                sg = c["sg"].next()
                pu = banks.next()
                for kk in range(8):
                    S.op("pe", lambda e, pu=pu, kk=kk, q=q, wu=wu, co=co: e.matmul(
                        pu[:], lhsT=wu[:, kk, co:co + 128], rhs=hT[:, kk, q * 512:(q + 1) * 512],
                        start=(kk == 0), stop=(kk == 7)), reads=bufs(wu, hT), writes=bufs(pu))
                S.op("act", lambda e, pu=pu, ug=ug: e.activation(out=ug[:], in_=pu[:], func=AF.Gelu),
                     reads=bufs(pu), writes=bufs(ug))
                pz = banks.next()
                for kk in range(8):
                    S.op("pe", lambda e, pz=pz, kk=kk, q=q, wz=wz, co=co: e.matmul(
                        pz[:], lhsT=wz[:, kk, co:co + 128], rhs=hT[:, kk, q * 512:(q + 1) * 512],
                        start=(kk == 0), stop=(kk == 7)), reads=bufs(wz, hT), writes=bufs(pz))
                S.op("act", lambda e, pz=pz, zs=zs: e.activation(out=zs[:], in_=pz[:], func=AF.Silu),
                     reads=bufs(pz), writes=bufs(zs))
                ps_ = banks.next()
                for cc in range(4):
                    t = q * 4 + cc
                    S.op("pe", lambda e, ps_=ps_, t=t, cc=cc, blk=blk, g=g: e.matmul(
                        ps_[:, cc * 128:(cc + 1) * 128], lhsT=vv[:, t, blk * 128:(blk + 1) * 128], rhs=c["wsT"][:, g, :],
                        start=True, stop=True), reads=bufs(vv, c["wsT"]), writes=bufs(ps_))
                S.op("dve", lambda e, ps_=ps_, blk=blk, sg=sg: e.scalar_tensor_tensor(
                    out=sg[:].rearrange("p (c i) -> p c i", c=4),
                    in0=ps_[:].rearrange("p (c i) -> p c i", c=4),
                    scalar=c["lngT"][:, blk:blk + 1],
                    in1=c["Bt"][:, blk:blk + 1, :].to_broadcast([128, 4, 128]), op0=ALU.mult, op1=ALU.add),
                    reads=bufs(ps_, c["lngT"], c["Bt"]), writes=bufs(sg))
                S.op("pool", lambda e, sg=sg, ug=ug: e.tensor_tensor(out=sg[:], in0=sg[:], in1=ug[:], op=ALU.mult),
                     reads=bufs(sg, ug), writes=bufs(sg))
                S.op("dve", lambda e, q=q, sg=sg, zs=zs, blk=blk: e.tensor_tensor(
                    out=yT[:, blk, q * 512:(q + 1) * 512], in0=sg[:], in1=zs[:], op=ALU.mult),
                    reads=bufs(sg, zs), writes=bufs(yT))

    SCALE = 0.125

    def nat_proj(hp, ntok, c, with_ktm):
        wt = wring.next()
        for j in (0, 1, 3):
            S.dma("pool", wt[:, :, j * 128:(j + 1) * 128],
                  nat_w_in.rearrange("(k p) n -> p k n", p=128)[:, :, j * E + hp * 128:j * E + (hp + 1) * 128],
                  writes=bufs(wt))
        qT, kT, gT = c["qT"], c["kT"], c["gT"]
        for q in range(ntok // 512):
            for j, dst, fn in ((0, qT, AF.Copy), (1, kT, AF.Copy), (3, gT, AF.Silu)):
                pb = banks.next()
                for kk in range(8):
                    S.op("pe", lambda e, pb=pb, kk=kk, q=q, j=j, wt=wt: e.matmul(
                        pb[:], lhsT=wt[:, kk, j * 128:(j + 1) * 128], rhs=hT[:, kk, q * 512:(q + 1) * 512],
                        start=(kk == 0), stop=(kk == 7)), reads=bufs(wt, hT), writes=bufs(pb))
                S.op("act", lambda e, pb=pb, q=q, dst=dst, fn=fn: e.activation(
                    out=dst[:, q * 512:(q + 1) * 512], in_=pb[:], func=fn), reads=bufs(pb), writes=bufs(dst))

    def nat_proj_v4(hp4, ntok, c, with_ktm):
        vb = c["vb"]
        wv = load_w(nat_w_in, 2 * E + hp4 * 512, 512)
        wk = load_w(nat_w_in, E + hp4 * 512, 512) if with_ktm else None
        for t in range(ntok // 128):
            pv_ = banks.next()
            for kk in range(8):
                S.op("pe", lambda e, pv_=pv_, kk=kk, t=t, wv=wv: e.matmul(
                    pv_[:], lhsT=hT[:, kk, t * 128:(t + 1) * 128], rhs=wv[:, kk, :], start=(kk == 0), stop=(kk == 7)),
                    reads=bufs(wv, hT), writes=bufs(pv_))
            if not with_ktm:
                S.op("act", lambda e, pv_=pv_, t=t: e.activation(out=vb[:, t, :], in_=pv_[:], func=AF.Copy),
                     reads=bufs(pv_), writes=bufs(vb))
            else:
                vst, kst = c["vst"], c["kst"]
                S.op("act", lambda e, pv_=pv_, t=t: e.activation(out=vst[:, t, :], in_=pv_[:], func=AF.Copy),
                     reads=bufs(pv_), writes=bufs(vst))
                S.op("pool", lambda e, t=t: e.tensor_copy(out=vb[:, t, :], in_=vst[:, t, :]), reads=bufs(vst), writes=bufs(vb))
                pk_ = banks.next()
                for kk in range(8):
                    S.op("pe", lambda e, pk_=pk_, kk=kk, t=t, wk=wk: e.matmul(
                        pk_[:], lhsT=hT[:, kk, t * 128:(t + 1) * 128], rhs=wk[:, kk, :], start=(kk == 0), stop=(kk == 7)),
                        reads=bufs(wk, hT), writes=bufs(pk_))
                S.op("act", lambda e, pk_=pk_, t=t: e.activation(out=kst[:, t, :], in_=pk_[:], func=AF.Copy),
                     reads=bufs(pk_), writes=bufs(kst))

    def nat_ctx_unit():
        yT = L["yT"]
        c = {"qT": k.at([128, 1024], BF16), "kT": k.at([128, 1024], BF16), "gT": k.at([128, 1024], BF16),
             "vb": k.at([128, 8, 512], BF16), "vst": k.at([128, 8, 512], F32), "kst": k.at([128, 8, 512], F32)}
        er = k.aring(2, [128, 512], F32)
        pbr = k.aring(2, [128, 512], BF16)
        ptr_ = k.aring(2, [128, 512], BF16)
        for hp in range(cfg.get("ctx_hp", 16)):
            if hp % 4 == 0:
                nat_proj_v4(hp // 4, 1024, c, True)
            nat_proj(hp, 1024, c, True)
            for hd in range(2):
                h = hp * 2 + hd
                for sq in range(0 if cfg.get("no_kv") else 4):
                    S.dma("sp", new_k[sq, h, :, :].rearrange("(t p) d -> p t d", p=128),
                          c["kst"][:, sq * 2:(sq + 1) * 2, (hp % 4) * 128 + hd * 64:(hp % 4) * 128 + (hd + 1) * 64], reads=bufs(c["kst"]))
                    S.dma("sp", new_v[sq, h, :, :].rearrange("(t p) d -> p t d", p=128),
                          c["vst"][:, sq * 2:(sq + 1) * 2, (hp % 4) * 128 + hd * 64:(hp % 4) * 128 + (hd + 1) * 64], reads=bufs(c["vst"]))
            qT, kT, gT, vb = c["qT"], c["kT"], c["gT"], c["vb"]
            cb_ = Ring(banks.tiles[2:8])
            pob_ = Ring(banks.tiles[0:2])

            vo = (hp % 4) * 128

            def c_qk(sq, hd):
                rows = slice(hd * 64, (hd + 1) * 64)
                tok0 = sq * 256
                ps_ = cb_.next()
                for qt in range(2):
                    S.op("pe", lambda e, ps_=ps_, qt=qt, rows=rows, tok0=tok0: e.matmul(
                        ps_[:, qt * 256:(qt + 1) * 256], lhsT=qT[rows, tok0 + qt * 128:tok0 + (qt + 1) * 128],
                        rhs=kT[rows, tok0:tok0 + 256], start=True, stop=True), reads=bufs(qT, kT), writes=bufs(ps_))
                return ps_

            def c_softmax(ps_):
                mx = small.next()
                S.op("dve", lambda e, ps_=ps_, mx=mx: e.tensor_reduce(
                    out=mx[:, 0:2], in_=ps_[:].rearrange("p (a b) -> p a b", a=2), axis=AX.X, op=ALU.max),
                    reads=bufs(ps_), writes=bufs(mx))
                S.op("dve", lambda e, mx=mx: e.tensor_scalar(out=mx[:, 2:4], in0=mx[:, 0:2], scalar1=-SCALE, scalar2=None,
                                                             op0=ALU.mult), reads=bufs(mx), writes=bufs(mx))
                et = er.next()
                for qt in range(2):
                    S.op("act", lambda e, ps_=ps_, mx=mx, et=et, qt=qt: e.activation(
                        out=et[:, qt * 256:(qt + 1) * 256], in_=ps_[:, qt * 256:(qt + 1) * 256], func=AF.Exp, scale=SCALE,
                        bias=mx[:, 2 + qt:3 + qt], accum_out=mx[:, 4 + qt:5 + qt]), reads=bufs(ps_, mx), writes=bufs(et, mx))
                S.op("dve", lambda e, mx=mx: e.reciprocal(out=mx[:, 6:8], in_=mx[:, 4:6]), reads=bufs(mx), writes=bufs(mx))
                pbt = pbr.next()
                S.op("dve", lambda e, mx=mx, et=et, pbt=pbt: e.tensor_tensor(
                    out=pbt[:].rearrange("p (a b) -> p a b", a=2), in0=et[:].rearrange("p (a b) -> p a b", a=2),
                    in1=mx[:, 6:8].unsqueeze(2).to_broadcast([128, 2, 256]), op=ALU.mult),
                    reads=bufs(mx, et), writes=bufs(pbt))
                return pbt

            def c_tpv(sq, hd, pbt, po):
                rows = slice(hd * 64, (hd + 1) * 64)
                ptb = cb_.next()
                ptv = ptb[:].bitcast(BF16)
                for j in range(4):
                    S.op("pe", lambda e, ptv=ptv, pbt=pbt, j=j: e.transpose(
                        out=ptv[:, j * 128:(j + 1) * 128], in_=pbt[:, j * 128:(j + 1) * 128], identity=identb[:]),
                        reads=bufs(pbt, identb), writes=bufs(ptb))
                pts = ptr_.next()
                S.op("act", lambda e, ptv=ptv, pts=pts: e.activation(out=pts[:], in_=ptv[:, 0:512], func=AF.Copy),
                     reads=bufs(ptb), writes=bufs(pts))
                for qt in range(2):
                    for kb in range(2):
                        S.op("pe", lambda e, po=po, rows=rows, qt=qt, kb=kb, sq=sq, hd=hd, pts=pts, vo=vo: e.matmul(
                            po[rows, qt * 128:(qt + 1) * 128], lhsT=vb[:, sq * 2 + kb, vo + hd * 64:vo + (hd + 1) * 64],
                            rhs=pts[:, (qt * 2 + kb) * 128:(qt * 2 + kb + 1) * 128], start=(kb == 0), stop=(kb == 1)),
                            reads=bufs(vb, pts), writes=bufs(po))

            its = [(sq, hd) for sq in range(0 if cfg.get("ctx_stage", 9) < 1 else 4) for hd in range(2)]
            nxt = c_qk(*its[0]) if its else None
            po = None
            for ii, (sq, hd) in enumerate(its):
                if hd == 0:
                    po = pob_.next()
                pbt = c_softmax(nxt)
                if ii + 1 < len(its):
                    nxt = c_qk(*its[ii + 1])
                c_tpv(sq, hd, pbt, po)
                if hd == 1:
                    tok0 = sq * 256
                    S.op("dve", lambda e, po=po, hp=hp, tok0=tok0: e.tensor_tensor(
                        out=yT[:, hp, tok0:tok0 + 256], in0=po[:, 0:256], in1=gT[:, tok0:tok0 + 256], op=ALU.mult),
                        reads=bufs(po, gT), writes=bufs(yT))

    def nat_lat_unit():
        yT = L["yT"]
        c = {"qT": k.at([128, 2048], BF16), "kT": k.at([128, 2048], BF16), "gT": k.at([128, 2048], BF16),
             "vb": k.at([128, 16, 512], BF16)}
        maskf = k.at([128, 3, 576], F32)
        maskb = k.at([128, 3, 576], BF16)
        for j in range(3):
            S.dma("sp", maskf[:, j, :], natmask[j], writes=bufs(maskf))
        S.op("dve", lambda e: e.tensor_copy(out=maskb[:], in_=maskf[:]), reads=bufs(maskf), writes=bufs(maskb))
        ckr = k.aring(2, [128, 2, 2, 64], BF16)
        cvr = k.aring(2, [128, 2, 2, 64], BF16)
        cktr = k.aring(2, [128, 256], BF16)
        rpr = k.aring(2, [128, 1024], F32)
        scr = k.aring(2, [128, 832], F32)
        pbr = k.aring(2, [128, 832], BF16)
        ptr_ = k.aring(2, [128, 896], BF16)
        cfg["alog"] = k.alog
        pobanks = Ring(banks.tiles[0:2])
        wbanks = Ring(banks.tiles[2:8])
        for hp in range(cfg.get("nat_hp", 16)):
            if hp % 4 == 0:
                nat_proj_v4(hp // 4, 2048, c, False)
            nat_proj(hp, 2048, c, False)
            vo = (hp % 4) * 128
            qT, kT, gT, vb = c["qT"], c["kT"], c["gT"], c["vb"]
            ck = ckr.next()
            cv = cvr.next()
            for hd in range(2):
                S.dma("pool", ck[:, :, hd, :], cache_k[hp * 2 + hd].rearrange("(kb p) d -> p kb d", p=128), writes=bufs(ck))
                S.dma("pool", cv[:, :, hd, :], cache_v[hp * 2 + hd].rearrange("(kb p) d -> p kb d", p=128), writes=bufs(cv))
            ckT = cktr.next()
            ptb = banks.next()
            ptv = ptb[:].bitcast(BF16)
            for kb in range(2):
                S.op("pe", lambda e, ptv=ptv, ck=ck, kb=kb: e.transpose(
                    out=ptv[:, kb * 128:(kb + 1) * 128], in_=ck[:, kb, :, :].rearrange("p a b -> p (a b)"), identity=identb[:]),
                    reads=bufs(ck, identb), writes=bufs(ptb))
            S.op("act", lambda e, ptv=ptv, ckT=ckT: e.activation(out=ckT[:], in_=ptv[:, 0:256], func=AF.Copy),
                 reads=bufs(ptb), writes=bufs(ckT))
            rps = []
            for hd in range(2):
                rp = rpr.next()
                S.dma("sp", rp[:], rpbg[hp * 2 + hd], writes=bufs(rp))
                rps.append(rp)
            items = []
            for pg in range(cfg.get("nat_pg", 4)):
                for hd in range(cfg.get("nat_hd", 2)):
                    for pi in range(cfg.get("nat_pi", 4)):
                        items.append((pg, hd, pi))

            def geom(pg, hd, pi):
                pr = pg * 4 + pi
                r = 2 * pr
                if pr <= 1:
                    r0, nrow, a0, mi = 0, 9, 7 - r, 1
                elif pr >= 14:
                    r0, nrow, a0, mi = 24, 8, (3 if pr == 14 else 1), 2
                else:
                    r0, nrow, a0, mi = r - 4, 9, 3, 0
                return r, r0, nrow, a0, mi

            def st_qk(it):
                pg, hd, pi = it
                r, r0, nrow, a0, mi = geom(*it)
                rows = slice(hd * 64, (hd + 1) * 64)
                q0, k0 = r * 64, r0 * 64
                ps1 = wbanks.next()
                ps2 = wbanks.next()
                S.op("pe", lambda e, ps1=ps1, rows=rows, q0=q0, k0=k0: e.matmul(
                    ps1[:], lhsT=qT[rows, q0:q0 + 128], rhs=kT[rows, k0:k0 + 512], start=True, stop=False),
                    reads=bufs(qT, kT), writes=bufs(ps1))
                S.op("pe", lambda e, ps1=ps1, mi=mi: e.matmul(
                    ps1[:], lhsT=identb[:], rhs=maskb[:, mi, 0:512], start=False, stop=True),
                    reads=bufs(identb, maskb), writes=bufs(ps1))
                if nrow == 9:
                    S.op("pe", lambda e, ps2=ps2, rows=rows, q0=q0, k0=k0: e.matmul(
                        ps2[:, 0:64], lhsT=qT[rows, q0:q0 + 128], rhs=kT[rows, k0 + 512:k0 + 576], start=True, stop=False),
                        reads=bufs(qT, kT), writes=bufs(ps2))
                    S.op("pe", lambda e, ps2=ps2, mi=mi: e.matmul(
                        ps2[:, 0:64], lhsT=identb[:], rhs=maskb[:, mi, 512:576], start=False, stop=True),
                        reads=bufs(identb, maskb), writes=bufs(ps2))
                S.op("pe", lambda e, ps2=ps2, rows=rows, q0=q0, ckT=ckT: e.matmul(
                    ps2[:, 64:320], lhsT=qT[rows, q0:q0 + 128], rhs=ckT[rows, :], start=True, stop=True),
                    reads=bufs(qT, ckT), writes=bufs(ps2))
                return ps1, ps2

            def st_softmax(it, ps1, ps2):
                pg, hd, pi = it
                r, r0, nrow, a0, mi = geom(*it)
                rp = rps[hd]
                nk = nrow * 64
                sc = scr.next()
                S.op("dve", lambda e, ps1=ps1, sc=sc, rp=rp, a0=a0: e.scalar_tensor_tensor(
                    out=sc[:, 0:512], in0=ps1[:], scalar=SCALE, in1=rp[:, a0 * 64:a0 * 64 + 512],
                    op0=ALU.mult, op1=ALU.add), reads=bufs(ps1, rp), writes=bufs(sc))
                if nrow == 9:
                    S.op("dve", lambda e, ps2=ps2, sc=sc, rp=rp, a0=a0: e.scalar_tensor_tensor(
                        out=sc[:, 512:576], in0=ps2[:, 0:64], scalar=SCALE, in1=rp[:, a0 * 64 + 512:a0 * 64 + 576],
                        op0=ALU.mult, op1=ALU.add), reads=bufs(ps2, rp), writes=bufs(sc))
                S.op("act", lambda e, ps2=ps2, sc=sc, nk=nk: e.activation(
                    out=sc[:, nk:nk + 256], in_=ps2[:, 64:320], func=AF.Copy, scale=SCALE),
                    reads=bufs(ps2), writes=bufs(sc))
                ntot = nk + 256
                mx = small.next()
                S.op("dve", lambda e, sc=sc, mx=mx, ntot=ntot: e.tensor_reduce(
                    out=mx[:, 0:1], in_=sc[:, 0:ntot], axis=AX.X, op=ALU.max), reads=bufs(sc), writes=bufs(mx))
                S.op("dve", lambda e, mx=mx: e.tensor_scalar(out=mx[:, 1:2], in0=mx[:, 0:1], scalar1=-1.0, scalar2=None,
                                                             op0=ALU.mult), reads=bufs(mx), writes=bufs(mx))
                S.op("act", lambda e, sc=sc, mx=mx, ntot=ntot: e.activation(
                    out=sc[:, 0:ntot], in_=sc[:, 0:ntot], func=AF.Exp, bias=mx[:, 1:2], accum_out=mx[:, 2:3]),
                    reads=bufs(sc, mx), writes=bufs(sc, mx))
                S.op("dve", lambda e, mx=mx: e.reciprocal(out=mx[:, 3:4], in_=mx[:, 2:3]), reads=bufs(mx), writes=bufs(mx))
                pbt = pbr.next()
                S.op("dve", lambda e, sc=sc, mx=mx, pbt=pbt, ntot=ntot: e.tensor_scalar(
                    out=pbt[:, 0:ntot], in0=sc[:, 0:ntot], scalar1=mx[:, 3:4], scalar2=None, op0=ALU.mult),
                    reads=bufs(sc, mx), writes=bufs(pbt))
                return pbt

            def st_tpv(it, pbt, po):
                pg, hd, pi = it
                r, r0, nrow, a0, mi = geom(*it)
                rows = slice(hd * 64, (hd + 1) * 64)
                nk = nrow * 64
                ptb = wbanks.next()
                ptv = ptb[:].bitcast(BF16)
                blocks = [(j * 128, 128) for j in range(4)]
                blocks += [(nk, 128), (nk + 128, 128)]
                if nrow == 9:
                    blocks.append((512, 64))
                for j, (c0, w) in enumerate(blocks):
                    S.op("pe", lambda e, ptv=ptv, pbt=pbt, j=j, c0=c0, w=w: e.transpose(
                        out=ptv[0:w, j * 128:(j + 1) * 128], in_=pbt[:, c0:c0 + w], identity=identb[:]),
                        reads=bufs(pbt, identb), writes=bufs(ptb))
                nb = len(blocks)
                pts = ptr_.next()
                S.op("act", lambda e, ptv=ptv, pts=pts: e.activation(
                    out=pts[:, 0:768], in_=ptv[:, 0:768], func=AF.Copy), reads=bufs(ptb), writes=bufs(pts))
                if nrow == 9:
                    S.op("act", lambda e, ptv=ptv, pts=pts: e.activation(
                        out=pts[0:64, 768:896], in_=ptv[0:64, 768:896], func=AF.Copy), reads=bufs(ptb), writes=bufs(pts))
                t0 = r0 // 2
                for j, (c0, w) in enumerate(blocks):
                    if j < 4:
                        lhs = vb[:, t0 + j, vo + hd * 64:vo + (hd + 1) * 64]
                        rhs = pts[:, j * 128:(j + 1) * 128]
                        rd = bufs(vb, pts)
                    elif w == 64:
                        lhs = vb[0:64, t0 + 4, vo + hd * 64:vo + (hd + 1) * 64]
                        rhs = pts[0:64, j * 128:(j + 1) * 128]
                        rd = bufs(vb, pts)
                    else:
                        kb = j - 4
                        lhs = cv[:, kb, hd, :]
                        rhs = pts[:, j * 128:(j + 1) * 128]
                        rd = bufs(cv, pts)
                    S.op("pe", lambda e, po=po, rows=rows, pi=pi, lhs=lhs, rhs=rhs, j=j, nb=nb: e.matmul(
                        po[rows, pi * 128:(pi + 1) * 128], lhsT=lhs, rhs=rhs, start=(j == 0), stop=(j == nb - 1)),
                        reads=rd, writes=bufs(po))

            pos_ = {}
            n_it = len(items)
            qk_res = {}
            sm_res = {}
            for j_ in range(min(2, n_it)):
                qk_res[j_] = st_qk(items[j_])
            if n_it:
                sm_res[0] = st_softmax(items[0], *qk_res.pop(0))
            for ii, it in enumerate(items):
                pg = it[0]
                if pg not in pos_:
                    pos_[pg] = pobanks.next()
                po = pos_[pg]
                if ii + 2 < n_it:
                    qk_res[ii + 2] = st_qk(items[ii + 2])
                if ii + 1 < n_it:
                    sm_res[ii + 1] = st_softmax(items[ii + 1], *qk_res.pop(ii + 1))
                pbt = sm_res.pop(ii)
                st_tpv(it, pbt, po)
                if ii + 1 == len(items) or items[ii + 1][0] != pg:
                    S.op("dve", lambda e, po=po, hp=hp, pg=pg: e.tensor_tensor(
                        out=yT[:, hp, pg * 512:(pg + 1) * 512], in0=po[:], in1=gT[:, pg * 512:(pg + 1) * 512], op=ALU.mult),
                        reads=bufs(po, gT), writes=bufs(yT))

    def ssd_consts():
        c = {}
        c["tri"] = [k.at([128, 128], F32), k.at([128, 128], F32)]
        c["mneg"] = [k.at([128, 128], F32), k.at([128, 128], F32)]
        c["negones"] = k.at([128, 128], F32)
        for d_ in range(2):
            sgn = 1 if d_ == 0 else -1
            S.op("pool", lambda e, d_=d_: e.memset(c["tri"][d_][:], 1.0), writes=bufs(c["tri"][d_]))
            S.op("pool", lambda e, d_=d_, sgn=sgn: e.affine_select(
                out=c["tri"][d_][:], in_=c["tri"][d_][:], compare_op=ALU.is_ge, fill=0.0, base=0,
                pattern=[[sgn, 128]], channel_multiplier=-sgn), reads=bufs(c["tri"][d_]), writes=bufs(c["tri"][d_]))
            S.op("pool", lambda e, d_=d_: e.memset(c["mneg"][d_][:], 0.0), writes=bufs(c["mneg"][d_]))
            S.op("pool", lambda e, d_=d_, sgn=sgn: e.affine_select(
                out=c["mneg"][d_][:], in_=c["mneg"][d_][:], compare_op=ALU.is_ge, fill=-30000.0, base=0,
                pattern=[[sgn, 128]], channel_multiplier=-sgn), reads=bufs(c["mneg"][d_]), writes=bufs(c["mneg"][d_]))
        S.op("pool", lambda e: e.memset(c["negones"][:], -1.0), writes=bufs(c["negones"]))
        c["ntri"] = [k.at([128, 128], F32), k.at([128, 128], F32)]
        for d_ in range(2):
            S.op("pool", lambda e, d_=d_: e.tensor_scalar(out=c["ntri"][d_][:], in0=c["tri"][d_][:], scalar1=-1.0, scalar2=None,
                                                          op0=ALU.mult), reads=bufs(c["tri"][d_]), writes=bufs(c["ntri"][d_]))
        c["cwT"] = k.at([128, 32, 5], F32)
        c["cbT"] = k.at([128, 32], F32)
        for j in range(5):
            S.dma("sp", c["cwT"][:, :, j], ssd_conv_w[j].rearrange("(b p) -> p b", p=128), writes=bufs(c["cwT"]))
        S.dma("sp", c["cbT"][:], ssd_conv_b.rearrange("(b p) -> p b", p=128), writes=bufs(c["cbT"]))
        c["dtb"] = k.at([128, 64], F32)
        c["abc"] = k.at([128, 64], F32)
        c["dsk"] = k.at([128, 32], F32)
        c["ngT"] = k.at([128, 16], F32)
        S.dma("sp", c["dtb"][:], ssd_dt_bias.partition_broadcast(128), writes=bufs(c["dtb"]))
        S.dma("sp", c["abc"][:], ssd_a_log.partition_broadcast(128), writes=bufs(c["abc"]))
        S.dma("sp", c["dsk"][:], ssd_d.partition_broadcast(128), writes=bufs(c["dsk"]))
        S.dma("sp", c["ngT"][:], ssd_norm_g.rearrange("(b p) -> p b", p=128), writes=bufs(c["ngT"]))
        S.op("act", lambda e: e.activation(out=c["abc"][:], in_=c["abc"][:], func=AF.Exp), reads=bufs(c["abc"]), writes=bufs(c["abc"]))
        S.op("dve", lambda e: e.tensor_scalar(out=c["abc"][:], in0=c["abc"][:], scalar1=-1.0, scalar2=None, op0=ALU.mult),
             reads=bufs(c["abc"]), writes=bufs(c["abc"]))
        return c

    def ssd_unit(c, tok0, ntile, nseq, is_lat):
        T_ = ntile * 128
        nch = ntile // nseq
        Lq = nch * 128
        dt_ = k.at([128, ntile, 64], F32)
        da = k.at([128, ntile, 64], F32)
        ecum = k.at([128, ntile, 64], F32)
        dtd = k.at([128, ntile, 64], F32)
        etot = k.at([128, ntile, 64], F32)
        ssq = k.at([128, ntile, 8], F32)
        rstd = k.at([128, ntile], F32)
        tmpr = k.aring(2, [128, 64], F32)
        wdt = wring.next()
        S.dma("pool", wdt[:, :, 0:64], ssd_w_in.rearrange("(k p) n -> p k n", p=128)[:, :, 6144:6208], writes=bufs(wdt))
        for t in range(ntile):
            pb = banks.next()
            for kk in range(8):
                S.op("pe", lambda e, pb=pb, kk=kk, t=t: e.matmul(
                    pb[:, 0:64], lhsT=hT[:, kk, t * 128:(t + 1) * 128], rhs=wdt[:, kk, 0:64], start=(kk == 0), stop=(kk == 7)),
                    reads=bufs(hT, wdt), writes=bufs(pb))
            S.op("dve", lambda e, pb=pb, t=t: e.tensor_tensor(out=dt_[:, t, :], in0=pb[:, 0:64], in1=c["dtb"][:], op=ALU.add),
                 reads=bufs(pb, c["dtb"]), writes=bufs(dt_))
        S.op("act", lambda e: e.activation(out=dt_[:], in_=dt_[:], func=AF.Exp), reads=bufs(dt_), writes=bufs(dt_))
        S.op("act", lambda e: e.activation(out=dt_[:], in_=dt_[:], func=AF.Ln, bias=1.0), reads=bufs(dt_), writes=bufs(dt_))
        S.op("dve", lambda e: e.tensor_tensor(out=da[:], in0=dt_[:], in1=c["abc"][:].unsqueeze(1).to_broadcast([128, ntile, 64]),
                                              op=ALU.mult), reads=bufs(dt_, c["abc"]), writes=bufs(da))
        for t in range(ntile):
            pc = banks.next()
            S.op("pe", lambda e, pc=pc, t=t: e.matmul(pc[:, 0:32], lhsT=c["tri"][0][:], rhs=da[:, t, 0:32], start=True, stop=True),
                 reads=bufs(c["tri"][0], da), writes=bufs(pc))
            S.op("pe", lambda e, pc=pc, t=t: e.matmul(pc[:, 32:64], lhsT=c["tri"][1][:], rhs=da[:, t, 32:64], start=True, stop=True),
                 reads=bufs(c["tri"][1], da), writes=bufs(pc))
            S.op("pe", lambda e, pc=pc, t=t: e.matmul(pc[:, 64:128], lhsT=onesf[:], rhs=da[:, t, :], start=True, stop=True),
                 reads=bufs(onesf, da), writes=bufs(pc))
            cumt = tmpr.next()
            S.op("act", lambda e, pc=pc, cumt=cumt: e.activation(out=cumt[:], in_=pc[:, 0:64], func=AF.Identity),
                 reads=bufs(pc), writes=bufs(cumt))
            S.op("act", lambda e, pc=pc, t=t: e.activation(out=ecum[:, t, :], in_=pc[:, 0:64], func=AF.Exp),
                 reads=bufs(pc), writes=bufs(ecum))
            S.op("act", lambda e, pc=pc, t=t: e.activation(out=etot[:, t, :], in_=pc[:, 64:128], func=AF.Exp),
                 reads=bufs(pc), writes=bufs(etot))
            S.op("dve", lambda e, pc=pc, cumt=cumt: e.tensor_tensor(out=cumt[:], in0=pc[:, 64:128], in1=cumt[:], op=ALU.subtract),
                 reads=bufs(pc, cumt), writes=bufs(cumt))
            S.op("act", lambda e, cumt=cumt: e.activation(out=cumt[:], in_=cumt[:], func=AF.Exp), reads=bufs(cumt), writes=bufs(cumt))
            S.op("dve", lambda e, cumt=cumt, t=t: e.tensor_tensor(out=dtd[:, t, :], in0=dt_[:, t, :], in1=cumt[:], op=ALU.mult),
                 reads=bufs(cumt, dt_), writes=bufs(dtd))
        raw = k.at([128, T_], F32)
        acc = k.at([128, T_], F32)
        fm = [k.at([128, T_], BF16) for _ in range(4)]
        XB = k.at([128, ntile, 384], BF16)
        SIN = k.at([128, ntile, 2, 256], BF16)
        stfs = [k.at([128, 256], F32), k.at([128, 256], F32)]
        vTg = k.at([128, 2, T_], BF16)
        h0r = k.aring(2, [128, 2, 128], F32)
        fir = k.aring(2, [128, 2, 128], F32)
        GTr = k.aring(2, [128, 128], F32)
        Dr = k.aring(2, [128, 4, 128], F32)
        Lr = k.aring(2, [128, 4, 128], F32)
        Mr = k.aring(4, [128, 4, 128], BF16)
        xdr = k.aring(5, [128, 4, 64], BF16)
        yr = k.aring(4, [128, 256], F32)
        szr = k.aring(2, [128, 256], F32)
        vbr = k.aring(2, [128, 256], BF16)
        tmp4 = k.aring(2, [128, 4, 64], F32)
        for g in range(cfg.get("ssd_g", 8)):
            wA = wring.next()
            wap = ssd_w_in.rearrange("(k p) n -> p k n", p=128)
            S.dma("pool", wA[:, :, 0:256], wap[:, :, g * 256:(g + 1) * 256], writes=bufs(wA))
            S.dma("pool", wA[:, :, 256:512], wap[:, :, E + g * 256:E + (g + 1) * 256], writes=bufs(wA))
            wB = wring.next()
            S.dma("pool", wB[:, :, 0:128], wap[:, :, 2 * E + g * 128:2 * E + (g + 1) * 128], writes=bufs(wB))
            S.dma("pool", wB[:, :, 128:256], wap[:, :, 2 * E + 1024 + g * 128:2 * E + 1024 + (g + 1) * 128], writes=bufs(wB))
            for bi in range(4):
                wt, co, cblk = ((wA, 256, 2 * g), (wA, 384, 2 * g + 1), (wB, 0, 16 + g), (wB, 128, 24 + g))[bi]
                for q in range(T_ // 512):
                    pb = banks.next()
                    for kk in range(8):
                        S.op("pe", lambda e, pb=pb, kk=kk, q=q, wt=wt, co=co: e.matmul(
                            pb[:], lhsT=wt[:, kk, co:co + 128], rhs=hT[:, kk, q * 512:(q + 1) * 512],
                            start=(kk == 0), stop=(kk == 7)), reads=bufs(wt, hT), writes=bufs(pb))
                    S.op("act", lambda e, pb=pb, q=q: e.activation(out=raw[:, q * 512:(q + 1) * 512], in_=pb[:], func=AF.Copy),
                         reads=bufs(pb), writes=bufs(raw))
                cw = c["cwT"]
                rv = raw[:].rearrange("p (s l) -> p s l", s=nseq)
                av = acc[:].rearrange("p (s l) -> p s l", s=nseq)
                S.op("dve", lambda e, cblk=cblk: e.tensor_scalar(out=acc[:], in0=raw[:], scalar1=cw[:, cblk, 2:3], scalar2=None,
                                                                 op0=ALU.mult), reads=bufs(raw, cw), writes=bufs(acc))
                taps = ((0, "dve", slice(2, Lq), slice(0, Lq - 2)), (1, "dve", slice(1, Lq), slice(0, Lq - 1)),
                        (3, "dve", slice(0, Lq - 1), slice(1, Lq)), (4, "dve", slice(0, Lq - 2), slice(2, Lq)))
                for j, eng, osl, isl in taps:
                    S.op(eng, lambda e, j=j, osl=osl, isl=isl, cblk=cblk, rv=rv, av=av: e.scalar_tensor_tensor(
                        out=av[:, :, osl], in0=rv[:, :, isl], scalar=cw[:, cblk, j:j + 1], in1=av[:, :, osl],
                        op0=ALU.mult, op1=ALU.add), reads=bufs(raw, acc, cw), writes=bufs(acc))
                S.op("act", lambda e, bi=bi, cblk=cblk: e.activation(out=fm[bi][:], in_=acc[:], func=AF.Silu,
                                                                      bias=c["cbT"][:, cblk:cblk + 1]),
                     reads=bufs(acc, c["cbT"]), writes=bufs(fm[bi]))
            for t in range(ntile):
                ptb = banks.next()
                ptv = ptb[:].bitcast(BF16)
                for bi in range(3):
                    S.op("pe", lambda e, ptv=ptv, bi=bi, t=t: e.transpose(
                        out=ptv[:, bi * 128:(bi + 1) * 128], in_=fm[bi][:, t * 128:(t + 1) * 128], identity=identb[:]),
                        reads=bufs(fm[bi], identb), writes=bufs(ptb))
                S.op("act", lambda e, ptv=ptv, t=t: e.activation(out=XB[:, t, :], in_=ptv[:, 0:384], func=AF.Copy),
                     reads=bufs(ptb), writes=bufs(XB))
            BT, CT = fm[2], fm[3]
            for sq in range(nseq):
                for d_ in range(2):
                    if is_lat:
                        h0 = h0r.next()
                        for half in range(2):
                            S.dma("sp", h0[:, half, :], state_ssd[d_, 4 * g + 2 * half:4 * g + 2 * half + 2].rearrange("h p n -> (h p) n"),
                                  writes=bufs(h0))
                        ph = banks.next()
                        for half in range(2):
                            S.op("pe", lambda e, ph=ph, h0=h0, half=half: e.transpose(
                                out=ph[:, half * 128:(half + 1) * 128], in_=h0[:, half, :], identity=identf[:]),
                                reads=bufs(h0, identf), writes=bufs(ph))
                        S.op("act", lambda e, ph=ph, d_=d_: e.activation(out=stfs[d_][:], in_=ph[:, 0:256], func=AF.Copy),
                             reads=bufs(ph), writes=bufs(stfs[d_]))
                    else:
                        S.op("pool", lambda e, d_=d_: e.memset(stfs[d_][:], 0.0), writes=bufs(stfs[d_]))
                for step in range(nch):
                    for d_ in range(2):
                        hs = slice(d_ * 32 + 4 * g, d_ * 32 + 4 * g + 4)
                        ci = step if d_ == 0 else nch - 1 - step
                        t = sq * nch + ci
                        st_ = stfs[d_]
                        S.op("act", lambda e, t=t, d_=d_, st_=st_: e.activation(out=SIN[:, t, d_, :], in_=st_[:], func=AF.Copy),
                             reads=bufs(st_), writes=bufs(SIN))
                        xdd = xdr.next()
                        S.op("pool", lambda e, xdd=xdd, t=t, hs=hs: e.tensor_tensor(
                            out=xdd[:], in0=XB[:, t, 0:256].rearrange("p (h d) -> p h d", h=4),
                            in1=dtd[:, t, hs].unsqueeze(2).to_broadcast([128, 4, 64]), op=ALU.mult),
                            reads=bufs(XB, dtd), writes=bufs(xdd))
                        psl = banks.next()
                        S.op("pe", lambda e, psl=psl, t=t, xdd=xdd: e.matmul(
                            psl[:, 0:256], lhsT=XB[:, t, 256:384], rhs=xdd[:].rearrange("p h d -> p (h d)"), start=True, stop=True),
                            reads=bufs(XB, xdd), writes=bufs(psl))
                        S.op("pool", lambda e, t=t, st_=st_, hs=hs: e.tensor_tensor(
                            out=st_[:].rearrange("p (h d) -> p h d", h=4), in0=st_[:].rearrange("p (h d) -> p h d", h=4),
                            in1=etot[:, t, hs].unsqueeze(2).to_broadcast([128, 4, 64]), op=ALU.mult),
                            reads=bufs(st_, etot), writes=bufs(st_))
                        S.op("dve", lambda e, psl=psl, st_=st_: e.tensor_tensor(out=st_[:], in0=psl[:, 0:256], in1=st_[:],
                                                                              op=ALU.add), reads=bufs(psl, st_), writes=bufs(st_))
                if not is_lat:
                    for d_ in range(2):
                        st_ = stfs[d_]
                        pf = banks.next()
                        for half in range(2):
                            S.op("pe", lambda e, pf=pf, st_=st_, half=half: e.transpose(
                                out=pf[:, half * 128:(half + 1) * 128], in_=st_[:, half * 128:(half + 1) * 128], identity=identf[:]),
                                reads=bufs(st_, identf), writes=bufs(pf))
                        fi = fir.next()
                        S.op("act", lambda e, pf=pf, fi=fi: e.activation(out=fi[:].rearrange("p a b -> p (a b)"), in_=pf[:, 0:256],
                                                                        func=AF.Copy), reads=bufs(pf), writes=bufs(fi))
                        for half in range(2):
                            S.dma("sp", new_ssd[sq, d_, 4 * g + 2 * half:4 * g + 2 * half + 2].rearrange("h p n -> (h p) n"),
                                  fi[:, half, :], reads=bufs(fi))
            def front(t):
                tsl = slice(t * 128, (t + 1) * 128)
                pgz = banks.next()
                S.op("pe", lambda e, pgz=pgz, tsl=tsl: e.matmul(pgz[:, 0:128], lhsT=BT[:, tsl], rhs=CT[:, tsl], start=True, stop=True),
                     reads=bufs(BT, CT), writes=bufs(pgz))
                for kk in range(8):
                    S.op("pe", lambda e, pgz=pgz, kk=kk, tsl=tsl, wA=wA: e.matmul(
                        pgz[:, 128:384], lhsT=hT[:, kk, tsl], rhs=wA[:, kk, 0:256], start=(kk == 0), stop=(kk == 7)),
                        reads=bufs(hT, wA), writes=bufs(pgz))
                GT = GTr.next()
                S.op("act", lambda e, pgz=pgz, GT=GT: e.activation(out=GT[:], in_=pgz[:, 0:128], func=AF.Copy),
                     reads=bufs(pgz), writes=bufs(GT))
                sz = szr.next()
                S.op("act", lambda e, pgz=pgz, sz=sz: e.activation(out=sz[:], in_=pgz[:, 128:384], func=AF.Silu),
                     reads=bufs(pgz), writes=bufs(sz))
                hss = [slice(d_ * 32 + 4 * g, d_ * 32 + 4 * g + 4) for d_ in range(2)]
                Dts = []
                for d_ in range(2):
                    Dt = Dr.next()
                    S.op("pool", lambda e, Dt=Dt, d_=d_, t=t, hs=hss[d_]: e.tensor_tensor(
                        out=Dt[:], in0=c["tri"][d_][:].unsqueeze(1).to_broadcast([128, 4, 128]),
                        in1=da[:, t, hs].unsqueeze(2).to_broadcast([128, 4, 128]), op=ALU.mult),
                        reads=bufs(c["tri"][d_], da), writes=bufs(Dt))
                    Dts.append(Dt)
                pzs = []
                for d_ in range(2):
                    pz_ = banks.next()
                    Dt = Dts[d_]
                    S.op("pe", lambda e, pz_=pz_, Dt=Dt: e.matmul(
                        pz_[:], lhsT=onesf[:], rhs=Dt[:].rearrange("p h t -> p (h t)"), start=True, stop=False),
                        reads=bufs(onesf, Dt), writes=bufs(pz_))
                    S.op("pe", lambda e, pz_=pz_, d_=d_, t=t, hs=hss[d_]: e.matmul(
                        pz_[:], lhsT=c["ntri"][d_][:], rhs=da[:, t, hs].unsqueeze(2).to_broadcast([128, 4, 128]), start=False, stop=False),
                        reads=bufs(c["ntri"][d_], da), writes=bufs(pz_))
                    S.op("pe", lambda e, pz_=pz_, d_=d_: e.matmul(
                        pz_[:], lhsT=identf[:], rhs=c["mneg"][d_][:].unsqueeze(1).to_broadcast([128, 4, 128]), start=False, stop=True),
                        reads=bufs(identf, c["mneg"][d_]), writes=bufs(pz_))
                    pzs.append(pz_)
                poo = banks.next()
                for d_ in range(2):
                    S.op("pe", lambda e, poo=poo, tsl=tsl, t=t, d_=d_: e.matmul(
                        poo[:, d_ * 256:(d_ + 1) * 256], lhsT=CT[:, tsl], rhs=SIN[:, t, d_, :], start=True, stop=True),
                        reads=bufs(CT, SIN), writes=bufs(poo))
                Lts = []
                for d_ in range(2):
                    Lt = Lr.next()
                    S.op("act", lambda e, pz_=pzs[d_], Lt=Lt: e.activation(out=Lt[:].rearrange("p h t -> p (h t)"), in_=pz_[:], func=AF.Exp),
                         reads=bufs(pzs[d_]), writes=bufs(Lt))
                    Lts.append(Lt)
                Mts, xds = [], []
                for d_ in range(2):
                    Mt = Mr.next()
                    S.op("dve", lambda e, Lt=Lts[d_], Mt=Mt, GT=GT: e.tensor_tensor(
                        out=Mt[:], in0=Lt[:], in1=GT[:].unsqueeze(1).to_broadcast([128, 4, 128]), op=ALU.mult),
                        reads=bufs(Lts[d_], GT), writes=bufs(Mt))
                    xd = xdr.next()
                    S.op("pool", lambda e, xd=xd, t=t, hs=hss[d_]: e.tensor_tensor(
                        out=xd[:], in0=XB[:, t, 0:256].rearrange("p (h d) -> p h d", h=4),
                        in1=dt_[:, t, hs].unsqueeze(2).to_broadcast([128, 4, 64]), op=ALU.mult),
                        reads=bufs(XB, dt_), writes=bufs(xd))
                    Mts.append(Mt)
                    xds.append(xd)
                return dict(t=t, tsl=tsl, sz=sz, Mts=Mts, xds=xds, poo=poo, hss=hss)

            def back(f):
                t, tsl, sz, poo, hss = f["t"], f["tsl"], f["sz"], f["poo"], f["hss"]
                py = banks.next()
                for d_ in range(2):
                    Mt, xd = f["Mts"][d_], f["xds"][d_]
                    for h in range(4):
                        S.op("pe", lambda e, py=py, Mt=Mt, xd=xd, h=h, d_=d_: e.matmul(
                            py[:, h * 64:(h + 1) * 64], lhsT=Mt[:, h, :], rhs=xd[:, h, :], start=(d_ == 0 and h == 0), stop=(d_ == 1 and h == 3)),
                            reads=bufs(Mt, xd), writes=bufs(py))
                y1 = yr.next()
                y2 = yr.next()
                for d_, yy in ((0, y1), (1, y2)):
                    S.op("dve", lambda e, poo=poo, hs=hss[d_], yy=yy, t=t, d_=d_: e.tensor_tensor(
                        out=yy[:].rearrange("p (h d) -> p h d", h=4), in0=poo[:, d_ * 256:(d_ + 1) * 256].rearrange("p (h d) -> p h d", h=4),
                        in1=ecum[:, t, hs].unsqueeze(2).to_broadcast([128, 4, 64]), op=ALU.mult),
                        reads=bufs(poo, ecum), writes=bufs(yy))
                S.op("pool", lambda e, y1=y1, y2=y2: e.tensor_tensor(out=y1[:], in0=y1[:], in1=y2[:], op=ALU.add),
                     reads=bufs(y1, y2), writes=bufs(y1))
                S.op("pool", lambda e, y2=y2, t=t, g=g: e.tensor_tensor(
                    out=y2[:].rearrange("p (h d) -> p h d", h=4), in0=XB[:, t, 0:256].rearrange("p (h d) -> p h d", h=4),
                    in1=c["dsk"][:, 4 * g:4 * g + 4].unsqueeze(2).to_broadcast([128, 4, 64]), op=ALU.mult),
                    reads=bufs(XB, c["dsk"]), writes=bufs(y2))
                S.op("pool", lambda e, y1=y1, y2=y2: e.tensor_tensor(out=y1[:], in0=y1[:], in1=y2[:], op=ALU.add),
                     reads=bufs(y1, y2), writes=bufs(y1))
                S.op("dve", lambda e, py=py, y1=y1: e.tensor_tensor(out=y1[:], in0=py[:, 0:256], in1=y1[:], op=ALU.add),
                     reads=bufs(py, y1), writes=bufs(y1))
                vb_ = vbr.next()
                S.op("pool", lambda e, vb_=vb_, y1=y1, sz=sz: e.tensor_tensor(out=vb_[:], in0=y1[:], in1=sz[:], op=ALU.mult),
                     reads=bufs(y1, sz), writes=bufs(vb_))
                S.op("act", lambda e, vb_=vb_, t=t, g=g: e.activation(out=junk[:, 0:256], in_=vb_[:], func=AF.Square,
                                                                     accum_out=ssq[:, t, g:g + 1]),
                     reads=bufs(vb_), writes=bufs(junk, ssq))
                ptb = banks.next()
                ptv = ptb[:].bitcast(BF16)
                for bb in range(2):
                    S.op("pe", lambda e, ptv=ptv, vb_=vb_, bb=bb: e.transpose(
                        out=ptv[:, bb * 128:(bb + 1) * 128], in_=vb_[:, bb * 128:(bb + 1) * 128], identity=identb[:]),
                        reads=bufs(vb_, identb), writes=bufs(ptb))
                for bb in range(2):
                    S.op("act", lambda e, ptv=ptv, bb=bb, tsl=tsl, g=g: e.activation(
                        out=vTg[:, bb, tsl], in_=ptv[:, bb * 128:(bb + 1) * 128], func=AF.Identity,
                        scale=c["ngT"][:, 2 * g + bb:2 * g + bb + 1]), reads=bufs(ptb, c["ngT"]), writes=bufs(vTg))

            fnext = front(0)
            for t in range(ntile):
                fcur = fnext
                if t + 1 < ntile:
                    fnext = front(t + 1)
                back(fcur)
            S.dma("sp", yscr[2 * g:2 * g + 2, :, tok0:tok0 + T_].rearrange("b p t -> p b t"), vTg[:], reads=bufs(vTg))
        S.op("dve", lambda e: e.tensor_reduce(out=rstd[:], in_=ssq[:], axis=AX.X, op=ALU.add), reads=bufs(ssq), writes=bufs(rstd))
        S.op("dve", lambda e: e.tensor_scalar(out=rstd[:], in0=rstd[:], scalar1=1.0 / E, scalar2=EPS, op0=ALU.mult, op1=ALU.add),
             reads=bufs(rstd), writes=bufs(rstd))
        S.op("act", lambda e: e.activation(out=rstd[:], in_=rstd[:], func=AF.Sqrt), reads=bufs(rstd), writes=bufs(rstd))
        S.op("dve", lambda e: e.reciprocal(out=rstd[:], in_=rstd[:]), reads=bufs(rstd), writes=bufs(rstd))
        return rstd

    TWO_PI = float(2 * np.pi)
    MAGIC = 12582912.0

    def s5_prep():
        c = {}
        c["AA"] = k.at([128, 64, 2, 2], F32)
        c["BB"] = k.at([128, 64, 2, 2], F32)
        c["Wsel"] = k.at([128, 8, 240], BF16)
        c["dT"] = k.at([128, 16], F32)
        c["bgT"] = k.at([128, 16], F32)
        c["h0"] = k.at([128, 2, 2, 64], F32)
        S.dma("sp", c["dT"][:], s5_d.rearrange("(b p) -> p b", p=128), writes=bufs(c["dT"]))
        c["dX"] = k.at([128, 128], F32)
        for t_ in range(8):
            S.dma("sp", c["dX"][16 * t_:16 * t_ + 16, :], s5_d.rearrange("(g j) -> j g", j=16), writes=bufs(c["dX"]))
        S.dma("sp", c["bgT"][:], s5_b_glu.rearrange("(b p) -> p b", p=128), writes=bufs(c["bgT"]))
        for d_ in range(2):
            S.dma("sp", c["h0"][:, d_, :, :], s5_h0[d_].rearrange("r p g -> p r g"), writes=bufs(c["h0"]))
        mm = k.amark()
        wself = k.at([128, 8, 240], F32)
        S.op("pool", lambda e: e.memset(wself[:], 0.0), writes=bufs(wself))
        S.op("pool", lambda e: e.affine_select(out=wself[:, :, 112:128], in_=wself[:, :, 112:128], compare_op=ALU.not_equal,
                                               fill=1.0, base=0, pattern=[[-16, 8], [-1, 16]], channel_multiplier=1),
             reads=bufs(wself), writes=bufs(wself))
        S.op("pool", lambda e: e.tensor_copy(out=c["Wsel"][:], in_=wself[:]), reads=bufs(wself), writes=bufs(c["Wsel"]))
        maskT = [k.at([128, 8, 16], F32), k.at([128, 8, 16], F32)]
        for d_ in range(2):
            S.op("pool", lambda e, d_=d_: e.memset(maskT[d_][:], 1.0), writes=bufs(maskT[d_]))
        S.op("pool", lambda e: e.affine_select(out=maskT[0][:], in_=maskT[0][:], compare_op=ALU.is_ge, fill=0.0, base=15,
                                               pattern=[[16, 8], [0, 16]], channel_multiplier=-1),
             reads=bufs(maskT[0]), writes=bufs(maskT[0]))
        S.op("pool", lambda e: e.affine_select(out=maskT[1][:], in_=maskT[1][:], compare_op=ALU.is_ge, fill=0.0, base=0,
                                               pattern=[[-16, 8], [0, 16]], channel_multiplier=1),
             reads=bufs(maskT[1]), writes=bufs(maskT[1]))
        pw = [[k.at([128, 64, 16], F32), k.at([128, 64, 16], F32)] for _ in range(2)]
        coef = [[k.at([128, 64], F32), k.at([128, 64], F32)] for _ in range(2)]
        lr, li, ls = k.at([128, 64], F32), k.at([128, 64], F32), k.at([128, 64], F32)
        xx, ang = k.at([128, 64], F32), k.at([128, 64], F32)
        tr = k.aring(6, [128, 64], F32)
        for d_ in range(2):
            S.dma("sp", lr[:], s5_lam[0, d_], writes=bufs(lr))
            S.dma("sp", li[:], s5_lam[1, d_], writes=bufs(li))
            S.dma("sp", ls[:], s5_lstep[d_], writes=bufs(ls))
            S.op("act", lambda e: e.activation(out=ls[:], in_=ls[:], func=AF.Exp), reads=bufs(ls), writes=bufs(ls))
            S.op("dve", lambda e: e.tensor_tensor(out=xx[:], in0=lr[:], in1=ls[:], op=ALU.mult), reads=bufs(lr, ls), writes=bufs(xx))
            S.op("dve", lambda e: e.tensor_tensor(out=ang[:], in0=li[:], in1=ls[:], op=ALU.mult), reads=bufs(li, ls), writes=bufs(ang))
            pre, pim = pw[d_]
            for kq in range(1, 9):
                mp, mn, sn, cs, t1, t2 = [tr.next() for _ in range(6)]
                S.op("act", lambda e, mp=mp, kq=kq: e.activation(out=mp[:], in_=xx[:], func=AF.Exp, scale=float(kq)),
                     reads=bufs(xx), writes=bufs(mp))
                S.op("act", lambda e, mn=mn, kq=kq: e.activation(out=mn[:], in_=xx[:], func=AF.Exp, scale=float(-kq)),
                     reads=bufs(xx), writes=bufs(mn))
                for dst, shift in ((sn, 0.0), (cs, 0.25)):
                    if shift:
                        S.op("dve", lambda e, t1=t1, kq=kq, shift=shift: e.tensor_scalar(
                            out=t1[:], in0=ang[:], scalar1=float(kq / TWO_PI), scalar2=shift, op0=ALU.mult, op1=ALU.add),
                            reads=bufs(ang), writes=bufs(t1))
                        S.op("dve", lambda e, t1=t1: e.tensor_scalar(out=t1[:], in0=t1[:], scalar1=MAGIC, scalar2=None, op0=ALU.add),
                             reads=bufs(t1), writes=bufs(t1))
                    else:
                        S.op("dve", lambda e, t1=t1, kq=kq: e.tensor_scalar(
                            out=t1[:], in0=ang[:], scalar1=float(kq / TWO_PI), scalar2=MAGIC, op0=ALU.mult, op1=ALU.add),
                            reads=bufs(ang), writes=bufs(t1))
                    S.op("dve", lambda e, t1=t1: e.tensor_scalar(out=t1[:], in0=t1[:], scalar1=-MAGIC, scalar2=-TWO_PI,
                                                                 op0=ALU.add, op1=ALU.mult), reads=bufs(t1), writes=bufs(t1))
                    S.op("dve", lambda e, t1=t1, kq=kq: e.scalar_tensor_tensor(
                        out=t1[:], in0=ang[:], scalar=float(kq), in1=t1[:], op0=ALU.mult, op1=ALU.add),
                        reads=bufs(ang, t1), writes=bufs(t1))
                    if shift:
                        S.op("dve", lambda e, t1=t1: e.tensor_scalar(out=t1[:], in0=t1[:], scalar1=float(np.pi / 2), scalar2=None,
                                                                     op0=ALU.add), reads=bufs(t1), writes=bufs(t1))
                    S.op("act", lambda e, t1=t1, dst=dst: e.activation(out=dst[:], in_=t1[:], func=AF.Sin),
                         reads=bufs(t1), writes=bufs(dst))
                S.op("dve", lambda e, kq=kq, mp=mp, cs=cs, pre=pre: e.tensor_tensor(out=pre[:, :, kq - 1], in0=mp[:], in1=cs[:], op=ALU.mult),
                     reads=bufs(mp, cs), writes=bufs(pre))
                S.op("dve", lambda e, kq=kq, mp=mp, sn=sn, pim=pim: e.tensor_tensor(out=pim[:, :, kq - 1], in0=mp[:], in1=sn[:], op=ALU.mult),
                     reads=bufs(mp, sn), writes=bufs(pim))
                S.op("dve", lambda e, kq=kq, mn=mn, cs=cs, pre=pre: e.tensor_tensor(out=pre[:, :, 7 + kq], in0=mn[:], in1=cs[:], op=ALU.mult),
                     reads=bufs(mn, cs), writes=bufs(pre))
                S.op("dve", lambda e, kq=kq, mn=mn, sn=sn, pim=pim: e.scalar_tensor_tensor(
                    out=pim[:, :, 7 + kq], in0=mn[:], scalar=-1.0, in1=sn[:], op0=ALU.mult, op1=ALU.mult),
                    reads=bufs(mn, sn), writes=bufs(pim))
            for r_ in range(2):
                S.op("act", lambda e, d_=d_, r_=r_, pre=pre: e.activation(out=c["AA"][:, :, d_, r_], in_=pre[:, :, 7], func=AF.Copy),
                     reads=bufs(pre), writes=bufs(c["AA"]))
            S.op("dve", lambda e, d_=d_, pim=pim: e.tensor_scalar(out=c["BB"][:, :, d_, 0], in0=pim[:, :, 7], scalar1=-1.0, scalar2=None,
                                                         op0=ALU.mult), reads=bufs(pim), writes=bufs(c["BB"]))
            S.op("act", lambda e, d_=d_, pim=pim: e.activation(out=c["BB"][:, :, d_, 1], in_=pim[:, :, 7], func=AF.Copy),
                 reads=bufs(pim), writes=bufs(c["BB"]))
            den, nr, t1, t2 = [tr.next() for _ in range(4)]
            S.op("dve", lambda e, den=den: e.tensor_tensor(out=den[:], in0=lr[:], in1=lr[:], op=ALU.mult), reads=bufs(lr), writes=bufs(den))
            S.op("dve", lambda e, t1=t1: e.tensor_tensor(out=t1[:], in0=li[:], in1=li[:], op=ALU.mult), reads=bufs(li), writes=bufs(t1))
            S.op("dve", lambda e, den=den, t1=t1: e.tensor_tensor(out=den[:], in0=den[:], in1=t1[:], op=ALU.add),
                 reads=bufs(den, t1), writes=bufs(den))
            S.op("dve", lambda e, den=den: e.reciprocal(out=den[:], in_=den[:]), reads=bufs(den), writes=bufs(den))
            S.op("dve", lambda e, nr=nr, pre=pre: e.tensor_scalar(out=nr[:], in0=pre[:, :, 0], scalar1=-1.0, scalar2=None, op0=ALU.add),
                 reads=bufs(pre), writes=bufs(nr))
            cr_, ci_ = coef[d_]
            S.op("dve", lambda e, nr=nr, t1=t1: e.tensor_tensor(out=t1[:], in0=nr[:], in1=lr[:], op=ALU.mult), reads=bufs(nr, lr), writes=bufs(t1))
            S.op("dve", lambda e, t2=t2, pim=pim: e.tensor_tensor(out=t2[:], in0=pim[:, :, 0], in1=li[:], op=ALU.mult), reads=bufs(pim, li), writes=bufs(t2))
            S.op("dve", lambda e, t1=t1, t2=t2: e.tensor_tensor(out=t1[:], in0=t1[:], in1=t2[:], op=ALU.add), reads=bufs(t1, t2), writes=bufs(t1))
            S.op("dve", lambda e, t1=t1, den=den, cr_=cr_: e.tensor_tensor(out=cr_[:], in0=t1[:], in1=den[:], op=ALU.mult),
                 reads=bufs(t1, den), writes=bufs(cr_))
            S.op("dve", lambda e, t1=t1, pim=pim: e.tensor_tensor(out=t1[:], in0=pim[:, :, 0], in1=lr[:], op=ALU.mult), reads=bufs(pim, lr), writes=bufs(t1))
            S.op("dve", lambda e, nr=nr, t2=t2: e.tensor_tensor(out=t2[:], in0=nr[:], in1=li[:], op=ALU.mult), reads=bufs(nr, li), writes=bufs(t2))
            S.op("dve", lambda e, t1=t1, t2=t2: e.tensor_tensor(out=t1[:], in0=t1[:], in1=t2[:], op=ALU.subtract), reads=bufs(t1, t2), writes=bufs(t1))
            S.op("dve", lambda e, t1=t1, den=den, ci_=ci_: e.tensor_tensor(out=ci_[:], in0=t1[:], in1=den[:], op=ALU.mult),
                 reads=bufs(t1, den), writes=bufs(ci_))
        Braw = [k.at([128, 8, 16], F32), k.at([128, 8, 16], F32)]
        Craw = [k.at([128, 8, 16], F32), k.at([128, 8, 16], F32)]
        Bb = [k.at([128, 8, 16], F32), k.at([128, 8, 16], F32)]
        V = [[k.at([128, 8, 8, 16], F32), k.at([128, 8, 8, 16], F32)] for _ in range(2)]
        W2 = [[k.at([128, 8, 8, 16], F32), k.at([128, 8, 8, 16], F32)] for _ in range(2)]
        t8 = k.aring(4, [128, 8, 16], F32)
        t8e = {"pool": k.aring(4, [128, 8, 16], F32), "dve": k.aring(4, [128, 8, 16], F32)}
        T16 = k.aring(2, [128, 16, 128], BF16)
        VT16 = k.aring(2, [128, 8, 2, 2, 128], BF16)
        W216 = k.aring(2, [128, 8, 2, 2, 128], BF16)
        Ttmp = k.aring(2, [128, 128], F32)
        Ttmp2 = k.aring(2, [128, 128], F32)

        def bc_j(ap2):
            return ap2.unsqueeze(2).to_broadcast([128, 8, 16])

        for b in range(8):
            gs = slice(8 * b, 8 * b + 8)
            t16, vt16, w216 = T16.next(), VT16.next(), W216.next()
            for d_ in range(2):
                pre, pim = pw[d_]
                cr_, ci_ = coef[d_]
                for r_ in range(2):
                    S.dma("sp", Braw[r_][:], s5_B[r_, d_, :, gs, :], writes=bufs(Braw[r_]))
                    S.dma("sp", Craw[r_][:], s5_C[r_, d_, :, gs, :], writes=bufs(Craw[r_]))
                ta, tb = t8.next(), t8.next()
                S.op("dve", lambda e, ta=ta, cr_=cr_, gs=gs: e.tensor_tensor(out=ta[:], in0=Braw[0][:], in1=bc_j(cr_[:, gs]), op=ALU.mult),
                     reads=bufs(Braw[0], cr_), writes=bufs(ta))
                S.op("dve", lambda e, tb=tb, ci_=ci_, gs=gs: e.tensor_tensor(out=tb[:], in0=Braw[1][:], in1=bc_j(ci_[:, gs]), op=ALU.mult),
                     reads=bufs(Braw[1], ci_), writes=bufs(tb))
                S.op("dve", lambda e, ta=ta, tb=tb: e.tensor_tensor(out=Bb[0][:], in0=ta[:], in1=tb[:], op=ALU.subtract),
                     reads=bufs(ta, tb), writes=bufs(Bb[0]))
                ta, tb = t8.next(), t8.next()
                S.op("dve", lambda e, ta=ta, cr_=cr_, gs=gs: e.tensor_tensor(out=ta[:], in0=Braw[1][:], in1=bc_j(cr_[:, gs]), op=ALU.mult),
                     reads=bufs(Braw[1], cr_), writes=bufs(ta))
                S.op("dve", lambda e, tb=tb, ci_=ci_, gs=gs: e.tensor_tensor(out=tb[:], in0=Braw[0][:], in1=bc_j(ci_[:, gs]), op=ALU.mult),
                     reads=bufs(Braw[0], ci_), writes=bufs(tb))
                S.op("dve", lambda e, ta=ta, tb=tb: e.tensor_tensor(out=Bb[1][:], in0=ta[:], in1=tb[:], op=ALU.add),
                     reads=bufs(ta, tb), writes=bufs(Bb[1]))
                for s_ in range(8):
                    kv = 8 + (s_ if d_ == 0 else 7 - s_)
                    kw = s_ if d_ == 0 else 7 - s_
                    for (eng, P_idx, X_, out_, neg_im) in (("pool", kv, Bb, V[d_], False), ("dve", kw, Craw, W2[d_], True)):
                        Pr = bc_j(pre[:, gs, P_idx])
                        Pi = bc_j(pim[:, gs, P_idx])
                        ta, tb = t8e[eng].next(), t8e[eng].next()
                        S.op(eng, lambda e, ta=ta, X_=X_, Pr=Pr: e.tensor_tensor(out=ta[:], in0=X_[0][:], in1=Pr, op=ALU.mult),
                             reads=bufs(X_[0], pre), writes=bufs(ta))
                        S.op(eng, lambda e, tb=tb, X_=X_, Pi=Pi: e.tensor_tensor(out=tb[:], in0=X_[1][:], in1=Pi, op=ALU.mult),
                             reads=bufs(X_[1], pim), writes=bufs(tb))
                        S.op(eng, lambda e, ta=ta, tb=tb, out_=out_, s_=s_: e.tensor_tensor(
                            out=out_[0][:, :, s_, :], in0=ta[:], in1=tb[:], op=ALU.subtract), reads=bufs(ta, tb), writes=bufs(out_[0]))
                        ta, tb = t8e[eng].next(), t8e[eng].next()
                        S.op(eng, lambda e, ta=ta, X_=X_, Pi=Pi: e.tensor_tensor(out=ta[:], in0=X_[0][:], in1=Pi, op=ALU.mult),
                             reads=bufs(X_[0], pim), writes=bufs(ta))
                        S.op(eng, lambda e, tb=tb, X_=X_, Pr=Pr: e.tensor_tensor(out=tb[:], in0=X_[1][:], in1=Pr, op=ALU.mult),
                             reads=bufs(X_[1], pre), writes=bufs(tb))
                        if not neg_im:
                            S.op(eng, lambda e, ta=ta, tb=tb, out_=out_, s_=s_: e.tensor_tensor(
                                out=out_[1][:, :, s_, :], in0=ta[:], in1=tb[:], op=ALU.add), reads=bufs(ta, tb), writes=bufs(out_[1]))
                        else:
                            S.op(eng, lambda e, ta=ta, tb=tb: e.tensor_tensor(out=ta[:], in0=ta[:], in1=tb[:], op=ALU.add),
                                 reads=bufs(ta, tb), writes=bufs(ta))
                            S.op(eng, lambda e, ta=ta, out_=out_, s_=s_: e.tensor_scalar(
                                out=out_[1][:, :, s_, :], in0=ta[:], scalar1=-1.0, scalar2=None, op0=ALU.mult),
                                reads=bufs(ta), writes=bufs(out_[1]))
                for r_ in range(2):
                    S.op("act", lambda e, d_=d_, r_=r_, w216=w216: e.activation(
                        out=w216[:, :, d_, r_, :], in_=W2[d_][r_][:].rearrange("p g s j -> p g (s j)"), func=AF.Copy),
                        reads=bufs(W2[d_][r_]), writes=bufs(w216))
                for gl in range(8):
                    pv_ = banks.next()
                    for r_ in range(2):
                        S.op("pe", lambda e, pv_=pv_, d_=d_, r_=r_, gl=gl: e.transpose(
                            out=pv_[:, r_ * 128:(r_ + 1) * 128], in_=V[d_][r_][:, gl, :, :].rearrange("p s j -> p (s j)"),
                            identity=identf[:]), reads=bufs(V[d_][r_], identf), writes=bufs(pv_))
                    S.op("act", lambda e, pv_=pv_, d_=d_, gl=gl, vt16=vt16: e.activation(
                        out=vt16[:, gl, d_, :, :].rearrange("p r m -> p (r m)"), in_=pv_[:, 0:256], func=AF.Copy),
                        reads=bufs(pv_), writes=bufs(vt16))
            for gl in range(8):
                for par in range(2):
                    rows = slice(par * 64, (par + 1) * 64)
                    gi = gl * 2 + par
                    pT = banks.next()
                    for d_ in range(2):
                        for r_ in range(2):
                            S.op("pe", lambda e, pT=pT, d_=d_, r_=r_, gl=gl, rows=rows: e.matmul(
                                pT[:, d_ * 128:(d_ + 1) * 128], lhsT=V[d_][r_][rows, gl, :, :].rearrange("p s j -> p (s j)"),
                                rhs=W2[d_][r_][rows, gl, :, :].rearrange("p s j -> p (s j)"), start=(r_ == 0), stop=(r_ == 1)),
                                reads=bufs(V[d_][r_], W2[d_][r_]), writes=bufs(pT))
                    ta, tb = Ttmp.next(), Ttmp2.next()
                    S.op("dve", lambda e, pT=pT, ta=ta: e.tensor_tensor(
                        out=ta[:], in0=pT[:, 0:128], in1=maskT[0][:].rearrange("p t j -> p (t j)"), op=ALU.mult),
                        reads=bufs(pT, maskT[0]), writes=bufs(ta))
                    S.op("dve", lambda e, pT=pT, tb=tb: e.tensor_tensor(
                        out=tb[:], in0=pT[:, 128:256], in1=maskT[1][:].rearrange("p t j -> p (t j)"), op=ALU.mult),
                        reads=bufs(pT, maskT[1]), writes=bufs(tb))
                    S.op("pool", lambda e, ta=ta, tb=tb, t16=t16, gi=gi: e.tensor_tensor(out=t16[:, gi, :], in0=ta[:], in1=tb[:], op=ALU.add),
                         reads=bufs(ta, tb), writes=bufs(t16))
            S.dma("sp", Tscr[b], t16[:].rearrange("p g m -> p (g m)"), reads=bufs(t16))
            S.dma("sp", VTscr[b], vt16[:].rearrange("p g d r m -> p (g d r m)"), reads=bufs(vt16))
            S.dma("sp", W2scr[b], w216[:].rearrange("p g d r m -> p (g d r m)"), reads=bufs(w216))
        k.arestore(mm)
        return c

    def s5_tiles(tok0, sub, is_lat):
        tiles = []
        for i in range(8):
            I_ = sub * 8 + i
            if not is_lat:
                s_, c0 = I_, 0
            else:
                s_, c0 = I_ // 2, (I_ % 2) * 128
            base = tok0 + 8 * c0 + s_
            tiles.append(((lambda src_, base=base: src_[base:base + 8 * 127 + 1:8, :]), I_ * 128))
        return tiles

    def s5_unit(c, tok0, is_lat):
        C_ = 256 if is_lat else 128
        nseq = 1 if is_lat else 4
        nch = C_ // nseq
        nct = C_ // 128
        Tn = 8 * C_
        U = k.at([128, nct, 16, 8, 16], BF16)
        X = k.at([128, 16, C_], BF16)
        arr = k.at([128, 16, 2, nseq, nch + 1], F32)
        Hb = k.at([128, 16, 2, nseq, nch + 1], BF16)
        Ysb = T(U.t[:].rearrange("p a g s j -> p (a g s j)").rearrange("p (g c) -> p g c", g=16))
        Ysb.b = U.b
        Tw = k.at([128, 16, 128], BF16)
        VTw = k.at([128, 8, 2, 2, 128], BF16)
        W2w = k.at([128, 8, 2, 2, 128], BF16)
        ygst = k.at([128, 2, Tn], BF16)
        uur = k.aring(2, [128, 512], F32)
        ysr = k.aring(2, [128, 512], F32)
        tmps = {eng: [k.at([128, 8, 2, nseq], F32) for _ in range(3)] for eng in ("dve", "pool")}
        GPB = 512 // C_
        if not is_lat:
            fin = k.at([128, 4, 2, 2, 64], F32)
            fst = k.aring(2, [64, 4, 128], F32)
        bl = {}
        if is_lat:
            for eng in ("dve", "pool"):
                bl[eng] = {"PR": k.at([128, 8, 16], F32), "PI": k.at([128, 8, 16], F32),
                           "AAp": k.at([128, 8, 2, 16], F32), "BBp": k.at([128, 8, 2, 16], F32),
                           "cc": k.at([128, 8, 2, 17], F32),
                           "t": [k.at([128, 8, 8], F32) for _ in range(4)],
                           "l": [k.at([128, 8, 2, 16], F32) for _ in range(3)],
                           "c": [k.at([128, 8, 2], F32) for _ in range(2)],
                           "f": [[k.at([128, 8, 2, 16], F32) for _ in range(2)] for _ in range(2)]}
        for b in range(cfg.get("s5_nb", 8)):
            gs = slice(8 * b, 8 * b + 8)
            S.dma("sp", Tw[:].rearrange("p g m -> p (g m)"), Tscr[b], writes=bufs(Tw))
            S.dma("sp", VTw[:].rearrange("p g d r m -> p (g d r m)"), VTscr[b], writes=bufs(VTw))
            S.dma("sp", W2w[:].rearrange("p g d r m -> p (g d r m)"), W2scr[b], writes=bufs(W2w))
            wu = wring.next()
            S.dma("pool", wu[:, :, 0:256], s5_w_in.rearrange("(k p) n -> p k n", p=128)[:, :, 256 * b:256 * (b + 1)], writes=bufs(wu))
            for ct in range(nct):
                for s2 in range(4):
                    pb = banks.next()
                    for si in range(2):
                        s_ = s2 * 2 + si
                        p0 = s_ * C_ + ct * 128
                        for kk in range(8):
                            S.op("pe", lambda e, pb=pb, kk=kk, si=si, p0=p0, wu=wu: e.matmul(
                                pb[:, si * 256:(si + 1) * 256], lhsT=hT[:, kk, p0:p0 + 128], rhs=wu[:, kk, 0:256],
                                start=(kk == 0), stop=(kk == 7)), reads=bufs(hT, wu), writes=bufs(pb))
                    S.op("act", lambda e, pb=pb, ct=ct, s2=s2: e.activation(
                        out=U[:, ct, :, s2 * 2:s2 * 2 + 2, :], in_=pb[:].rearrange("p (s g j) -> p g s j", s=2, g=16), func=AF.Copy),
                        reads=bufs(pb), writes=bufs(U))
            for ct in range(nct):
                for g4 in range(4):
                    pb = banks.next()
                    for gg in range(4):
                        gi = g4 * 4 + gg
                        S.op("pe", lambda e, pb=pb, gg=gg, gi=gi, ct=ct: e.matmul(
                            pb[:, gg * 128:(gg + 1) * 128], lhsT=U[:, ct, gi, :, :].rearrange("p s j -> p (s j)"), rhs=identb[:], start=True, stop=True),
                            reads=bufs(U, identb), writes=bufs(pb))
                    S.op("act", lambda e, pb=pb, g4=g4, ct=ct: e.activation(
                        out=X[:, g4 * 4:g4 * 4 + 4, ct * 128:(ct + 1) * 128], in_=pb[:].rearrange("p (g c) -> p g c", g=4), func=AF.Copy),
                        reads=bufs(pb), writes=bufs(X))
            if is_lat:
                S.op("act", lambda e, gs=gs: e.activation(
                    out=arr[:, :, :, 0, 0].rearrange("p (g d) r -> p g d r", d=2),
                    in_=c["h0"][:, :, :, gs].rearrange("p d r g -> p g d r"), func=AF.Copy), reads=bufs(c["h0"]), writes=bufs(arr))
            else:
                S.op("pool", lambda e: e.memset(arr[:, :, :, :, 0:1], 0.0), writes=bufs(arr))
            for gl in range(8):
                pGs = [banks.next() for _ in range(nct)]
                for par in range(2):
                    gi = 2 * gl + par
                    rows = slice(par * 64, (par + 1) * 64)
                    for d_ in range(2):
                        if d_ == 0:
                            rhs = X[:, gi, :]
                        else:
                            rhs = X[:, gi, :].rearrange("p (s c) -> p s c", s=nseq)[:, :, ::-1]
                        for r_ in range(2):
                            if is_lat:
                                outp = pGs[d_][rows, r_ * 256:(r_ + 1) * 256]
                                pgb = pGs[d_]
                            else:
                                outp = pGs[0][rows, (d_ * 2 + r_) * 128:(d_ * 2 + r_ + 1) * 128]
                                pgb = pGs[0]
                            S.op("pe", lambda e, outp=outp, gl=gl, d_=d_, r_=r_, par=par, rhs=rhs: e.matmul(
                                outp, lhsT=VTw[:, gl, d_, r_, par * 64:(par + 1) * 64], rhs=rhs, start=True, stop=True),
                                reads=bufs(VTw, X), writes=bufs(pgb))
                for d_ in range(2):
                    if is_lat:
                        src_ = pGs[d_][:].rearrange("p (r s c) -> p r s c", r=2, s=1)
                        pgb = pGs[d_]
                    else:
                        src_ = pGs[0][:, d_ * 256:(d_ + 1) * 256].rearrange("p (r s c) -> p r s c", r=2, s=nseq)
                        pgb = pGs[0]
                    S.op("act", lambda e, src_=src_, gl=gl, d_=d_: e.activation(
                        out=arr[:, gl * 2 + d_, :, :, 1:nch + 1], in_=src_, func=AF.Copy), reads=bufs(pgb), writes=bufs(arr))
            AAb = c["AA"][:, gs, :, :].rearrange("p g d r -> p (g d) r")
            BBb = c["BB"][:, gs, :, :].rearrange("p g d r -> p (g d) r")
            if not is_lat:
                for kq in range(nch):
                    for eng, qs in (("dve", slice(0, 8)), ("pool", slice(8, 16))):
                        tt, p1, p2 = tmps[eng]
                        S.op(eng, lambda e, tt=tt, qs=qs, kq=kq: e.tensor_tensor(
                            out=tt[:], in0=arr[:, qs, :, :, kq], in1=arr[:, qs, :, :, kq + 1], op=ALU.add),
                            reads=bufs(arr), writes=bufs(tt))
                        S.op(eng, lambda e, tt=tt, p1=p1, qs=qs, AAb=AAb: e.tensor_tensor(
                            out=p1[:], in0=tt[:], in1=AAb[:, qs, :].unsqueeze(3).to_broadcast([128, 8, 2, nseq]), op=ALU.mult),
                            reads=bufs(tt, c["AA"]), writes=bufs(p1))
                        S.op(eng, lambda e, tt=tt, p2=p2, qs=qs, BBb=BBb: e.tensor_tensor(
                            out=p2[:], in0=tt[:, :, ::-1, :], in1=BBb[:, qs, :].unsqueeze(3).to_broadcast([128, 8, 2, nseq]), op=ALU.mult),
                            reads=bufs(tt, c["BB"]), writes=bufs(p2))
                        S.op(eng, lambda e, p1=p1, p2=p2, qs=qs, kq=kq: e.tensor_tensor(
                            out=arr[:, qs, :, :, kq + 1], in0=p1[:], in1=p2[:], op=ALU.add), reads=bufs(p1, p2), writes=bufs(arr))
                S.op("act", lambda e: e.activation(out=Hb[:].rearrange("p q r s c -> p (q r s c)"),
                                                   in_=arr[:].rearrange("p q r s c -> p (q r s c)"), func=AF.Copy),
                     reads=bufs(arr), writes=bufs(Hb))
            else:
                NB_, BL_ = 16, 16
                for eng, qs in (("dve", slice(0, 8)), ("pool", slice(8, 16))):
                    B_ = bl[eng]
                    PR, PI, AAp, BBp, cc = B_["PR"], B_["PI"], B_["AAp"], B_["BBp"], B_["cc"]
                    tA, tB, tC, tD = B_["t"]
                    AAh = AAb[:, qs, :]
                    BBh = BBb[:, qs, :]
                    S.op(eng, lambda e, PR=PR, AAh=AAh: e.tensor_copy(out=PR[:, :, 0], in_=AAh[:, :, 0]), reads=bufs(c["AA"]), writes=bufs(PR))
                    S.op(eng, lambda e, PI=PI, BBh=BBh: e.tensor_copy(out=PI[:, :, 0], in_=BBh[:, :, 1]), reads=bufs(c["BB"]), writes=bufs(PI))
                    m_ = 1
                    while m_ < 16:
                        ar = PR[:, :, m_ - 1:m_].to_broadcast([128, 8, m_])
                        ai = PI[:, :, m_ - 1:m_].to_broadcast([128, 8, m_])
                        src_r, src_i = PR[:, :, 0:m_], PI[:, :, 0:m_]
                        dst_r, dst_i = PR[:, :, m_:2 * m_], PI[:, :, m_:2 * m_]
                        ta, tb = tA[:, :, 0:m_], tB[:, :, 0:m_]
                        tc_, td = tC[:, :, 0:m_], tD[:, :, 0:m_]
                        S.op(eng, lambda e, ta=ta, src_r=src_r, ar=ar: e.tensor_tensor(out=ta, in0=src_r, in1=ar, op=ALU.mult), reads=bufs(PR), writes=bufs(tA))
                        S.op(eng, lambda e, tb=tb, src_i=src_i, ai=ai: e.tensor_tensor(out=tb, in0=src_i, in1=ai, op=ALU.mult), reads=bufs(PI), writes=bufs(tB))
                        S.op(eng, lambda e, tc_=tc_, src_r=src_r, ai=ai: e.tensor_tensor(out=tc_, in0=src_r, in1=ai, op=ALU.mult), reads=bufs(PR, PI), writes=bufs(tC))
                        S.op(eng, lambda e, td=td, src_i=src_i, ar=ar: e.tensor_tensor(out=td, in0=src_i, in1=ar, op=ALU.mult), reads=bufs(PR, PI), writes=bufs(tD))
                        S.op(eng, lambda e, dst_r=dst_r, ta=ta, tb=tb: e.tensor_tensor(out=dst_r, in0=ta, in1=tb, op=ALU.subtract), reads=bufs(tA, tB), writes=bufs(PR))
                        S.op(eng, lambda e, dst_i=dst_i, tc_=tc_, td=td: e.tensor_tensor(out=dst_i, in0=tc_, in1=td, op=ALU.add), reads=bufs(tC, tD), writes=bufs(PI))
                        m_ *= 2
                    for r_ in range(2):
                        S.op(eng, lambda e, AAp=AAp, PR=PR, r_=r_: e.tensor_copy(out=AAp[:, :, r_, :], in_=PR[:]), reads=bufs(PR), writes=bufs(AAp))
                    S.op(eng, lambda e, BBp=BBp, PI=PI: e.tensor_scalar(out=BBp[:, :, 0, :], in0=PI[:], scalar1=-1.0, scalar2=None, op0=ALU.mult),
                         reads=bufs(PI), writes=bufs(BBp))
                    S.op(eng, lambda e, BBp=BBp, PI=PI: e.tensor_copy(out=BBp[:, :, 1, :], in_=PI[:]), reads=bufs(PI), writes=bufs(BBp))
                for eng, qs in (("dve", slice(0, 8)), ("pool", slice(8, 16))):
                    B_ = bl[eng]
                    AAp, BBp, cc = B_["AAp"], B_["BBp"], B_["cc"]
                    t3, p13, p23 = B_["l"]
                    AAh = AAb[:, qs, :].unsqueeze(3).to_broadcast([128, 8, 2, NB_])
                    BBh = BBb[:, qs, :].unsqueeze(3).to_broadcast([128, 8, 2, NB_])
                    xv = arr[:, qs, :, 0, 1:257].rearrange("p q r (b i) -> p q r b i", i=BL_)
                    for i_ in range(BL_):
                        if i_ == 0:
                            src_t = xv[:, :, :, :, 0]
                        else:
                            S.op(eng, lambda e, t3=t3, xv=xv, i_=i_: e.tensor_tensor(
                                out=t3[:], in0=xv[:, :, :, :, i_ - 1], in1=xv[:, :, :, :, i_], op=ALU.add), reads=bufs(arr), writes=bufs(t3))
                            src_t = t3[:]
                        rd = bufs(arr) if i_ == 0 else bufs(t3)
                        src_sw = src_t[:, :, ::-1, :]
                        S.op(eng, lambda e, p13=p13, src_t=src_t, AAh=AAh: e.tensor_tensor(out=p13[:], in0=src_t, in1=AAh, op=ALU.mult),
                             reads=rd + bufs(c["AA"]), writes=bufs(p13))
                        S.op(eng, lambda e, p23=p23, src_sw=src_sw, BBh=BBh: e.tensor_tensor(out=p23[:], in0=src_sw, in1=BBh, op=ALU.mult),
                             reads=rd + bufs(c["BB"]), writes=bufs(p23))
                        S.op(eng, lambda e, p13=p13, p23=p23, xv=xv, i_=i_: e.tensor_tensor(
                            out=xv[:, :, :, :, i_], in0=p13[:], in1=p23[:], op=ALU.add), reads=bufs(p13, p23), writes=bufs(arr))
                for eng, qs in (("dve", slice(0, 8)), ("pool", slice(8, 16))):
                    B_ = bl[eng]
                    AAp, BBp, cc = B_["AAp"], B_["BBp"], B_["cc"]
                    c1, c2 = B_["c"]
                    xv = arr[:, qs, :, 0, 1:257].rearrange("p q r (b i) -> p q r b i", i=BL_)
                    S.op(eng, lambda e, cc=cc, qs=qs: e.tensor_copy(out=cc[:, :, :, 0], in_=arr[:, qs, :, 0, 0]), reads=bufs(arr), writes=bufs(cc))
                    for Bk in range(NB_):
                        S.op(eng, lambda e, c1=c1, cc=cc, AAp=AAp, Bk=Bk: e.tensor_tensor(
                            out=c1[:], in0=cc[:, :, :, Bk], in1=AAp[:, :, :, 15], op=ALU.mult), reads=bufs(cc, AAp), writes=bufs(c1))
                        S.op(eng, lambda e, c2=c2, cc=cc, BBp=BBp, Bk=Bk: e.tensor_tensor(
                            out=c2[:], in0=cc[:, :, ::-1, Bk], in1=BBp[:, :, :, 15], op=ALU.mult), reads=bufs(cc, BBp), writes=bufs(c2))
                        S.op(eng, lambda e, c1=c1, c2=c2: e.tensor_tensor(out=c1[:], in0=c1[:], in1=c2[:], op=ALU.add),
                             reads=bufs(c1, c2), writes=bufs(c1))
                        S.op(eng, lambda e, c1=c1, cc=cc, xv=xv, Bk=Bk: e.tensor_tensor(
                            out=cc[:, :, :, Bk + 1], in0=c1[:], in1=xv[:, :, :, Bk, 15], op=ALU.add), reads=bufs(c1, arr), writes=bufs(cc))
                for eng, qs in (("dve", slice(0, 8)), ("pool", slice(8, 16))):
                    B_ = bl[eng]
                    AAp, BBp, cc = B_["AAp"], B_["BBp"], B_["cc"]
                    xv = arr[:, qs, :, 0, 1:257].rearrange("p q r (b i) -> p q r b i", i=BL_)
                    hv = Hb[:, qs, :, 0, 1:257].rearrange("p q r (b i) -> p q r b i", i=BL_)
                    fr = B_["f"]
                    S.op(eng, lambda e, qs=qs: e.tensor_copy(out=Hb[:, qs, :, 0, 0], in_=arr[:, qs, :, 0, 0]), reads=bufs(arr), writes=bufs(Hb))
                    pend = []
                    for i_ in range(BL_ + 1):
                        if i_ < BL_:
                            f1, f2 = fr[i_ % 2]
                            S.op(eng, lambda e, f1=f1, cc=cc, AAp=AAp, i_=i_: e.tensor_tensor(
                                out=f1[:], in0=cc[:, :, :, 0:NB_], in1=AAp[:, :, :, i_:i_ + 1].to_broadcast([128, 8, 2, NB_]), op=ALU.mult),
                                reads=bufs(cc, AAp), writes=bufs(f1))
                            S.op(eng, lambda e, f2=f2, cc=cc, BBp=BBp, i_=i_: e.tensor_tensor(
                                out=f2[:], in0=cc[:, :, ::-1, 0:NB_], in1=BBp[:, :, :, i_:i_ + 1].to_broadcast([128, 8, 2, NB_]), op=ALU.mult),
                                reads=bufs(cc, BBp), writes=bufs(f2))
                        if i_ >= 1:
                            j_ = i_ - 1
                            f1, f2 = fr[j_ % 2]
                            S.op(eng, lambda e, f1=f1, f2=f2: e.tensor_tensor(out=f1[:], in0=f1[:], in1=f2[:], op=ALU.add),
                                 reads=bufs(f1, f2), writes=bufs(f1))
                            S.op(eng, lambda e, f1=f1, xv=xv, hv=hv, j_=j_: e.tensor_tensor(
                                out=hv[:, :, :, :, j_], in0=f1[:], in1=xv[:, :, :, :, j_], op=ALU.add), reads=bufs(f1, arr), writes=bufs(Hb))
            if not is_lat:
                for d_ in range(2):
                    S.op("act", lambda e, d_=d_, gs=gs: e.activation(
                        out=fin[:, :, d_, :, gs], in_=arr[:, d_:16:2, :, :, nch].rearrange("p g r s -> p s r g"), func=AF.Copy),
                        reads=bufs(arr), writes=bufs(fin))
            for g0 in range(0, 16, GPB):
                pb = banks.next()
                for gg in range(GPB):
                    gi = g0 + gg
                    gl, par = gi // 2, gi % 2
                    rows = slice(par * 64, (par + 1) * 64)
                    yreg = pb[:, gg * C_:(gg + 1) * C_]
                    S.op("pe", lambda e, yreg=yreg, gi=gi: e.matmul(yreg, lhsT=Tw[:, gi, :], rhs=X[:, gi, :], start=True, stop=False),
                         reads=bufs(Tw, X), writes=bufs(pb))
                    for d_ in range(2):
                        for r_ in range(2):
                            hsl = Hb[rows, gl * 2 + d_, r_, :, 0:nch]
                            if d_ == 1:
                                hsl = hsl[:, :, ::-1]
                            S.op("pe", lambda e, yreg=yreg, gl=gl, d_=d_, r_=r_, rows=rows, hsl=hsl: e.matmul(
                                yreg, lhsT=W2w[rows, gl, d_, r_, :], rhs=hsl, start=False, stop=(d_ == 1 and r_ == 1)),
                                reads=bufs(W2w, Hb), writes=bufs(pb))
                for gg in range(GPB):
                    gi = g0 + gg
                    S.op("dve", lambda e, pb=pb, gg=gg, gi=gi, b=b: e.scalar_tensor_tensor(
                        out=Ysb[:, gi, :], in0=X[:, gi, :], scalar=c["dX"][:, 16 * b + gi:16 * b + gi + 1],
                        in1=pb[:, gg * C_:(gg + 1) * C_], op0=ALU.mult, op1=ALU.add),
                        reads=bufs(pb, X, c["dX"]), writes=bufs(Ysb))
            for blk in range(2):
                for t0 in range(0, 8, GPB):
                    psel = banks.next()
                    for tt_ in range(GPB):
                        t = t0 + tt_
                        for g_ in range(8):
                            S.op("pe", lambda e, psel=psel, tt_=tt_, t=t, g_=g_, blk=blk: e.matmul(
                                psel[:, tt_ * C_:(tt_ + 1) * C_], lhsT=c["Wsel"][:, t, 112 - 16 * g_:240 - 16 * g_],
                                rhs=Ysb[:, blk * 8 + g_, :], start=(g_ == 0), stop=(g_ == 7)),
                                reads=bufs(c["Wsel"], Ysb), writes=bufs(psel))
                    S.op("act", lambda e, psel=psel, blk=blk, t0=t0: e.activation(
                        out=ygst[:, blk, t0 * C_:t0 * C_ + 512], in_=psel[:], func=AF.Gelu), reads=bufs(psel), writes=bufs(ygst))
            S.dma("sp", yscr[2 * b:2 * b + 2, :, tok0:tok0 + Tn].rearrange("b p t -> p b t"), ygst[:], reads=bufs(ygst))

        if not is_lat:
            for sq in range(4):
                pf = banks.next()
                for d_ in range(2):
                    for r_ in range(2):
                        j_ = d_ * 2 + r_
                        S.op("pe", lambda e, pf=pf, sq=sq, d_=d_, r_=r_, j_=j_: e.transpose(
                            out=pf[0:64, j_ * 128:(j_ + 1) * 128], in_=fin[:, sq, d_, r_, :], identity=identf[:]),
                            reads=bufs(fin, identf), writes=bufs(pf))
                st_ = fst.next()
                S.op("act", lambda e, pf=pf, st_=st_: e.activation(out=st_[:].rearrange("p a b -> p (a b)"), in_=pf[0:64, :], func=AF.Copy),
                     reads=bufs(pf), writes=bufs(st_))
                S.dma("sp", new_s5[sq].rearrange("d r (gp two) n -> gp (d r) (two n)", two=2), st_[:], reads=bufs(st_))

    def s5_glu(c, tok0, sub, yT):
        ygT = k.at([128, 16, 1024], BF16)
        S.dma("sp", ygT[:], yscr[:, :, tok0 + sub * 1024:tok0 + (sub + 1) * 1024].rearrange("b p t -> p b t"), writes=bufs(ygT))
        wgr = k.aring(2, [128, 16, 128], BF16)
        sgr = k.aring(2, [128, 512], F32)
        szr = k.aring(2, [128, 512], F32)
        for blk in range(16):
            wg = wgr.next()
            S.dma("pool", wg[:], s5_w_glu.rearrange("(k p) n -> p k n", p=128)[:, :, blk * 128:(blk + 1) * 128], writes=bufs(wg))
            if blk % 4 == 0:
                wz = load_w(s5_w_in, E + blk * 128, 512)
            co = (blk % 4) * 128
            for q in range(2):
                p0 = sub * 1024 + q * 512
                pg_ = banks.next()
                for kk in range(16):
                    S.op("pe", lambda e, pg_=pg_, kk=kk, wg=wg, q=q: e.matmul(
                        pg_[:], lhsT=wg[:, kk, :], rhs=ygT[:, kk, q * 512:(q + 1) * 512], start=(kk == 0), stop=(kk == 15)),
                        reads=bufs(wg, ygT), writes=bufs(pg_))
                sg = sgr.next()
                S.op("act", lambda e, pg_=pg_, sg=sg, blk=blk: e.activation(
                    out=sg[:], in_=pg_[:], func=AF.Sigmoid, bias=c["bgT"][:, blk:blk + 1]), reads=bufs(pg_, c["bgT"]), writes=bufs(sg))
                pz = banks.next()
                for kk in range(8):
                    S.op("pe", lambda e, pz=pz, kk=kk, wz=wz, co=co, p0=p0: e.matmul(
                        pz[:], lhsT=wz[:, kk, co:co + 128], rhs=hT[:, kk, p0:p0 + 512], start=(kk == 0), stop=(kk == 7)),
                        reads=bufs(wz, hT), writes=bufs(pz))
                sz = szr.next()
                S.op("act", lambda e, pz=pz, sz=sz: e.activation(out=sz[:], in_=pz[:], func=AF.Silu), reads=bufs(pz), writes=bufs(sz))
                S.op("pool", lambda e, sg=sg, blk=blk, q=q: e.tensor_tensor(
                    out=sg[:], in0=sg[:], in1=ygT[:, blk, q * 512:(q + 1) * 512], op=ALU.mult), reads=bufs(sg, ygT), writes=bufs(sg))
                S.op("dve", lambda e, sg=sg, sz=sz, blk=blk, p0=p0: e.tensor_tensor(
                    out=yT[:, blk, p0:p0 + 512], in0=sg[:], in1=sz[:], op=ALU.mult), reads=bufs(sg, sz), writes=bufs(yT))

    def std_tiles(tok0, n):
        return [(rows_std(tok0 + i * 128), i * 128) for i in range(n)]

    units = [(0, 8, 0), (1024, 8, 1), (2048, 8, 1)]
    src = cfg.get("src", None) and inp("xsrc", [NTOK, D]) or xin
    for li in layers:
        last = final and (li == layers[-1])
        dst = xres
        k.areset()
        phase_a(li)
        if li == 1:
            L["yT"] = k.at([128, 16, 1024], BF16)
            c = gmlp_consts()
            for (tok0, nt, cond) in units:
                tiles = std_tiles(tok0, nt)
                phase_b(src, tiles, cond)
                gmlp_unit(c, nt)
                load_wout(li)
                phase_d(src, dst, tiles, cond, last)
        if li == 0:
            c = ssd_consts()
            m0 = k.amark()
            for (tok0, nt, nseq, cond) in ((0, 8, 4, 0), (1024, 16, 1, 1)):
                tiles = std_tiles(tok0, nt)
                phase_b(src, tiles, cond)
                rstd = ssd_unit(c, tok0, nt, nseq, cond == 1)
                S.op("act", lambda e, rstd=rstd, nt=nt: e.activation(out=rstd_keep[:, 0:nt], in_=rstd[:], func=AF.Copy),
                     reads=bufs(rstd), writes=bufs(rstd_keep))
                k.arestore(m0)
                load_wout(li)
                phase_d(src, dst, tiles, cond, last, scale_t=lambda i: (rstd_keep[:, i:i + 1], rstd_keep.b), ytok0=tok0)
                S.barrier()
        if li == 2:
            c = s5_prep()
            m0 = k.amark()
            for (tok0, is_lat, cond) in ((0, False, 0), (1024, True, 1)):
                nsub = 2 if is_lat else 1
                tiles = []
                for sub in range(nsub):
                    tiles += s5_tiles(tok0, sub, is_lat)
                phase_b(src, tiles, cond)
                s5_unit(c, tok0, is_lat)
                k.arestore(m0)
                L["yT"] = k.at([128, 16, 1024 * nsub], BF16)
                m1 = k.amark()
                for sub in range(nsub):
                    s5_glu(c, tok0, sub, L["yT"])
                    k.arestore(m1)
                load_wout(li)
                phase_d(src, dst, tiles, cond, last)
                k.arestore(m0)
        if li == 3:
            L["yT"] = k.at([128, 16, 2048], BF16)
            tiles = std_tiles(0, 8)
            phase_b(src, tiles, 0)
            m_ = k.amark()
            if not cfg.get("skip_ctx"):
                nat_ctx_unit()
            k.arestore(m_)
            load_wout(li)
            phase_d(src, dst, tiles, 0, last)
            S.barrier()
            tiles = std_tiles(1024, 16)
            phase_b(src, tiles, 1)
            m_ = k.amark()
            nat_lat_unit()
            if not cfg.get("skip_d"):
                k.arestore(m_)
            load_wout(li)
            phase_d(src, dst, tiles, 1, last)
        src = xres
    if not final and not cfg.get("skip_d"):
        S.barrier()
        xring = k.aring(3, [128, D], F32)
        for i in range(NTOK // 128):
            xt = xring.next()
            S.dma("sp", xt[:], xres[i * 128:(i + 1) * 128, :], writes=bufs(xt))
            S.dma("sp", y_out[i * 128:(i + 1) * 128, :], xt[:], reads=bufs(xt))
    S.emit(es)
    return nc, es


def host_inputs(inputs, core):
    f = np.ascontiguousarray
    m = {}
    m["xin"] = f(np.concatenate([inputs["x_prompt"][4 * core:4 * core + 4].reshape(NP_TOK, D),
                                 inputs["x_sample"][core % 2]], axis=0))
    m["cvec"] = f(np.stack([inputs["c_ctx"], inputs["c"][core % 2]], axis=0))
    for nm in ["norm_g", "w_mod", "b_mod", "w_out", "final_g"]:
        m[nm] = f(inputs[nm])
    m["mlp_w_in"] = f(inputs["mlp_w_in"][0])
    m["mlp_ln_g"] = f(inputs["mlp_ln_g"][0])
    m["mlp_ln_b"] = f(inputs["mlp_ln_b"][0])
    m["mlp_w_sT"] = f(np.transpose(inputs["mlp_w_s"][0], (0, 2, 1)))
    m["mlp_b_s"] = f(inputs["mlp_b_s"][0])
    m["ssd_w_in"] = f(inputs["ssd_w_in"][0])
    m["ssd_conv_w"] = f(inputs["ssd_conv_w"][0])
    m["ssd_conv_b"] = f(inputs["ssd_conv_b"][0])
    m["ssd_dt_bias"] = f(inputs["ssd_dt_bias"][0].reshape(64))
    m["ssd_a_log"] = f(inputs["ssd_a_log"][0].reshape(64))
    m["ssd_d"] = f(inputs["ssd_d"][0])
    m["ssd_norm_g"] = f(inputs["ssd_norm_g"][0])
    m["state_ssd"] = f(inputs["state_ssd"][core % 2, 0])
    m["s5_w_in"] = f(inputs["s5_w_in"][0])

    def pl(a):
        sh = a.shape[:-2]
        a = a.reshape(sh + (64, 2, 64))
        return np.moveaxis(a, -3, -1).reshape(sh + (128, 64))
    m["s5_lam"] = f(np.stack([pl(inputs["s5_lam_re"][0]), pl(inputs["s5_lam_im"][0])], 0))
    m["s5_lstep"] = f(pl(np.broadcast_to(inputs["s5_log_step"][0][:, :, None], (2, 128, 64))))

    def plj(a):
        a = a.reshape(2, 64, 2, 64, 16)
        return np.transpose(a, (0, 2, 3, 1, 4)).reshape(2, 128, 64, 16)
    m["s5_B"] = f(np.stack([plj(inputs["s5_b_re"][0]), plj(inputs["s5_b_im"][0])], 0))
    m["s5_C"] = f(np.stack([plj(np.transpose(inputs["s5_c_re"][0], (0, 1, 3, 2))),
                            plj(np.transpose(inputs["s5_c_im"][0], (0, 1, 3, 2)))], 0))
    m["s5_h0"] = f(pl(inputs["state_s5"][core % 2, 0]))
    m["s5_d"] = f(inputs["s5_d"][0])
    m["s5_w_glu"] = f(inputs["s5_w_glu"][0])
    m["s5_b_glu"] = f(inputs["s5_b_glu"][0])
    m["nat_w_in"] = f(inputs["nat_w_in"][0])
    m["rpbg"] = rpb_gather(inputs["nat_rpb"][0])
    m["natmask"] = nat_masks()
    m["cache_k"] = f(inputs["cache_k"][core % 2, 0])
    m["cache_v"] = f(inputs["cache_v"][core % 2, 0])
    return m


def rpb_gather(rpb):
    qc = np.arange(64)[:, None]
    kc = np.arange(64)[None, :]
    ci = np.clip(kc - qc + 15, 0, 30)
    out = np.zeros((32, 128, 16, 64), np.float32)
    g = rpb[:, :, ci]
    g = np.transpose(g, (0, 2, 1, 3))
    out[:, 0:64, 0:15, :] = g
    out[:, 64:128, 1:16, :] = g
    return np.ascontiguousarray(out.reshape(32, 128, 1024))


def nat_masks():
    NEG = -30000.0 * 8.0
    qc = np.arange(64)
    cs = np.clip(qc - 8, 0, 48)
    kc = np.arange(64)
    col_ok = (kc[None, :] >= cs[:, None]) & (kc[None, :] < cs[:, None] + 16)
    m = np.zeros((3, 128, 9, 64), np.float32)
    colm = np.where(col_ok, 0.0, NEG).astype(np.float32)
    m[:, 0:64] += colm[None, :, None, :]
    m[:, 64:128] += colm[None, :, None, :]
    m[0, 0:64, 8, :] = NEG
    m[0, 64:128, 0, :] = NEG
    m[1, :, 8, :] = NEG
    return np.ascontiguousarray(m.reshape(3, 128, 576))


def kernel(**inputs):
    inputs = {k_: np.asarray(v) for k_, v in inputs.items()}
    nc, es = build({})
    with es:
        in_maps = [host_inputs(inputs, c) for c in range(8)]
        res = run_bass_kernel_spmd(nc, in_maps, core_ids=list(range(8)))
    r = res.results
    y_prompt = np.concatenate([r[c]["y_out"][:NP_TOK].reshape(4, 256, D) for c in range(8)], axis=0)
    y_sample = np.stack([r[c]["y_out"][NP_TOK:] for c in range(2)], axis=0)
    new_ssd = np.concatenate([r[c]["new_ssd"] for c in range(8)], axis=0)[:, None]
    new_s5 = np.concatenate([r[c]["new_s5"] for c in range(8)], axis=0)[:, None]
    new_k = np.concatenate([r[c]["new_k"] for c in range(8)], axis=0)[:, None]
    new_v = np.concatenate([r[c]["new_v"] for c in range(8)], axis=0)[:, None]
    return (y_prompt.astype(np.float32), y_sample.astype(np.float32), np.ascontiguousarray(new_ssd, dtype=np.float32),
            np.ascontiguousarray(new_s5, dtype=np.float32), np.ascontiguousarray(new_k, dtype=np.float32),
            np.ascontiguousarray(new_v, dtype=np.float32))
```

```python
import numpy as np
from contextlib import ExitStack
import concourse.bass as bass
import concourse.mybir as mybir
from concourse.bass_utils import run_bass_kernel_spmd

F32 = mybir.dt.float32
BF16 = mybir.dt.bfloat16
AF = mybir.ActivationFunctionType
ALU = mybir.AluOpType
AX = mybir.AxisListType

D = 1024
E = 2048
NP_TOK = 1024
NS_TOK = 2048
NTOK = NP_TOK + NS_TOK
EPS = 1e-6
COMPUTE = ("pe", "act", "dve", "pool")
NDMASEM = 12
SAME_ENGINE_SYNC = True


class Buf:
    __slots__ = ("lw", "rd")

    def __init__(self):
        self.lw = None
        self.rd = {}


class Sched:
    def __init__(self, nc):
        self.nc = nc
        self.ops = {e: [] for e in COMPUTE + ("sp",)}
        self.cnt = {e: 0 for e in COMPUTE}
        self.seen = {e: {} for e in COMPUTE + ("sp",)}
        self.dma_slot = {}
        self.dma_val = {}
        self.sems = {}
        self.refd = {e: set() for e in COMPUTE}

    def _deps(self, eng, reads, writes):
        deps = {}

        def add(tok):
            if tok is None:
                return
            k, v = tok
            if deps.get(k, 0) < v:
                deps[k] = v

        for r in reads:
            add(r.lw)
        for w in writes:
            add(w.lw)
            for k, v in w.rd.items():
                add((k, v))
        out = []
        seen = self.seen[eng]
        for k, v in deps.items():
            if k == eng and (eng == "pe" or not SAME_ENGINE_SYNC):
                continue
            if seen.get(k, 0) >= v:
                continue
            seen[k] = v
            out.append((k, v))
            if isinstance(k, str):
                self.refd[k].add(v)
        return out

    def _mark(self, tok, reads, writes):
        k, v = tok
        for r in reads:
            if r.rd.get(k, 0) < v:
                r.rd[k] = v
        for w in writes:
            w.lw = tok
            w.rd = {}

    def op(self, eng, fn, reads=(), writes=()):
        waits = self._deps(eng, reads, writes)
        self.cnt[eng] += 1
        tok = (eng, self.cnt[eng])
        self.ops[eng].append((waits, fn, tok, 1))
        self._mark(tok, reads, writes)

    def dma(self, q, out, in_, reads=(), writes=()):
        slot = self.dma_slot.get(q, 0)
        self.dma_slot[q] = (slot + 1) % NDMASEM
        key = ("dma", q, slot)
        prev = self.dma_val.get(key, 0)
        waits = self._deps(q, reads, writes)
        if prev > 0 and self.seen[q].get(key, 0) < prev:
            self.seen[q][key] = prev
            waits.append((key, prev))
        val = prev + 16
        self.dma_val[key] = val
        tok = (key, val)

        def fn(e, out=out, in_=in_):
            return e.dma_start(out=out, in_=in_, allow_slow_non_contiguous=True)

        self.ops[q].append((waits, fn, tok, 16))
        self._mark(tok, reads, writes)

    def barrier(self):
        targets = [(e, self.cnt[e]) for e in COMPUTE if self.cnt[e] > 0]
        targets += [(key, v) for key, v in self.dma_val.items()]
        for eng in COMPUTE + ("sp",):
            waits = []
            for key, v in targets:
                if key == eng:
                    continue
                if self.seen[eng].get(key, 0) < v:
                    self.seen[eng][key] = v
                    waits.append((key, v))
                    if isinstance(key, str):
                        self.refd[key].add(v)
            if waits:
                self.ops[eng].append((waits, None, None, 0))

    def emit(self, es, final_wait_engine="sp"):
        nc = self.nc
        keys = list(COMPUTE)
        for q in self.dma_slot:
            for s in range(NDMASEM):
                if ("dma", q, s) in self.dma_val:
                    keys.append(("dma", q, s))
        for k in keys:
            nm = k if isinstance(k, str) else "d_%s_%d" % (k[1], k[2])
            self.sems[k] = es.enter_context(nc.semaphore("s_" + nm))
        fin = []
        for k in keys:
            v = self.cnt[k] if isinstance(k, str) else self.dma_val[k]
            if v > 0 and k != final_wait_engine:
                fin.append((k, v))
                if isinstance(k, str):
                    self.refd[k].add(v)
        rank = {}
        for e_ in COMPUTE:
            r_ = {}
            for n_, idx in enumerate(sorted(self.refd[e_])):
                r_[idx] = n_ + 1
            rank[e_] = r_

        def semval(k, v):
            return rank[k][v] if isinstance(k, str) else v
        block = es.enter_context(nc.Block())

        def run(e, name):
            for waits, fn, tok, inc in self.ops[name]:
                if fn is None:
                    for k, v in waits:
                        e.wait_ge(self.sems[k], semval(k, v))
                    continue
                NW = 1
                for k, v in waits[NW:]:
                    e.wait_ge(self.sems[k], semval(k, v))
                ins = fn(e)
                for k, v in waits[:NW]:
                    ins._wait_ge(self.sems[k], semval(k, v))
                if not isinstance(tok[0], str) or tok[1] in self.refd[tok[0]]:
                    ins.then_inc(self.sems[tok[0]], inc)
            if name == final_wait_engine:
                for k, v in fin:
                    e.wait_ge(self.sems[k], semval(k, v))

        @block.tensor
        def _(e):
            run(e, "pe")

        @block.scalar
        def _(e):
            run(e, "act")

        @block.vector
        def _(e):
            run(e, "dve")

        @block.gpsimd
        def _(e):
            run(e, "pool")

        @block.sync
        def _(e):
            run(e, "sp")


class T:
    __slots__ = ("t", "b")

    def __init__(self, t):
        self.t = t
        self.b = Buf()

    def __getitem__(self, k):
        return self.t[k]


class Ring:
    def __init__(self, tiles):
        self.tiles = tiles
        self.i = 0

    def next(self):
        t = self.tiles[self.i]
        self.i = (self.i + 1) % len(self.tiles)
        return t


class K:
    def __init__(self, nc, es):
        self.nc = nc
        self.es = es
        self.S = Sched(nc)
        self.n = 0

    def sb(self, shape, dt, name=None):
        self.n += 1
        return T(self.es.enter_context(self.nc.sbuf_tensor(name or "sb%d" % self.n, list(shape), dt)))

    def ring(self, n, shape, dt):
        return Ring([self.sb(shape, dt) for _ in range(n)])

    def psb(self, shape, dt):
        self.n += 1
        return T(self.es.enter_context(self.nc.psum_tensor("ps%d" % self.n, list(shape), dt)))

    def init_arena(self, nbytes):
        self.arena = self.es.enter_context(self.nc.sbuf_tensor("arena", [128, nbytes // 2], BF16))
        self.asize = nbytes
        self.aoff = 0
        self.alog = []

    def areset(self):
        self.S.barrier()
        self.aoff = 0

    def at(self, shape, dt):
        esz = 4 if dt == F32 else 2
        n = 1
        for d_ in shape[1:]:
            n *= d_
        nb = (n * esz + 63) // 64 * 64
        assert self.aoff + nb <= self.asize, ("arena overflow", self.aoff, nb, self.asize)
        ap = self.arena[0:shape[0], self.aoff // 2:(self.aoff + n * esz) // 2]
        if dt == F32:
            ap = ap.bitcast(F32)
        if len(shape) > 2:
            names = ["d%d" % i for i in range(len(shape) - 1)]
            kw = {names[i]: shape[i + 1] for i in range(len(names) - 1)}
            ap = ap.rearrange("p (%s) -> p %s" % (" ".join(names), " ".join(names)), **kw)
        self.alog.append((self.aoff, tuple(shape), dt))
        self.aoff += nb
        return T(ap)

    def amark(self):
        return self.aoff

    def arestore(self, m):
        self.S.barrier()
        self.aoff = m

    def aring(self, n, shape, dt):
        return Ring([self.at(shape, dt) for _ in range(n)])

    def dram(self, name, shape, dt, kind="Internal"):
        return self.nc.dram_tensor(name, list(shape), dt, kind=kind).ap()


def bufs(*ts):
    return [t.b for t in ts]


def build(cfg):
    layers = cfg.get("layers", [0, 1, 2, 3])
    final = cfg.get("final", True)
    nc = bass.Bass("TRN2", target_bir_lowering=False)
    es = ExitStack()
    k = K(nc, es)
    S = k.S
    I = {}

    def inp(name, shape):
        I[name] = k.dram(name, shape, F32, kind="ExternalInput")
        return I[name]

    xin = inp("xin", [NTOK, D])
    cvec = inp("cvec", [2, D])
    norm_g = inp("norm_g", [4, D])
    w_mod = inp("w_mod", [4, D, 3 * D])
    b_mod = inp("b_mod", [4, 3 * D])
    w_out = inp("w_out", [4, E, D])
    final_g = inp("final_g", [D])
    mlp_w_in = inp("mlp_w_in", [D, 3 * E])
    mlp_ln_g = inp("mlp_ln_g", [E])
    mlp_ln_b = inp("mlp_ln_b", [E])
    mlp_w_sT = inp("mlp_w_sT", [8, 128, 128])
    mlp_b_s = inp("mlp_b_s", [8, 128])
    ssd_w_in = inp("ssd_w_in", [D, 6208])
    ssd_conv_w = inp("ssd_conv_w", [5, 4096])
    ssd_conv_b = inp("ssd_conv_b", [4096])
    ssd_dt_bias = inp("ssd_dt_bias", [64])
    ssd_a_log = inp("ssd_a_log", [64])
    ssd_d = inp("ssd_d", [32])
    ssd_norm_g = inp("ssd_norm_g", [E])
    state_ssd = inp("state_ssd", [2, 32, 64, 128])
    new_ssd = k.dram("new_ssd", [4, 2, 32, 64, 128], F32, kind="ExternalOutput")
    yscr = k.dram("yscr", [16, 128, NTOK], BF16)
    s5_w_in = inp("s5_w_in", [D, 2 * E])
    s5_lam = inp("s5_lam", [2, 2, 128, 64])
    s5_lstep = inp("s5_lstep", [2, 128, 64])
    s5_B = inp("s5_B", [2, 2, 128, 64, 16])
    s5_C = inp("s5_C", [2, 2, 128, 64, 16])
    s5_h0 = inp("s5_h0", [2, 2, 128, 64])
    s5_d = inp("s5_d", [E])
    s5_w_glu = inp("s5_w_glu", [E, E])
    s5_b_glu = inp("s5_b_glu", [E])
    new_s5 = k.dram("new_s5", [4, 2, 2, 128, 64], F32, kind="ExternalOutput")
    Tscr = k.dram("Tscr", [8, 128, 16 * 128], BF16)
    VTscr = k.dram("VTscr", [8, 128, 8 * 4 * 128], BF16)
    W2scr = k.dram("W2scr", [8, 128, 8 * 4 * 128], BF16)
    nat_w_in = inp("nat_w_in", [D, 4 * E])
    rpbg = inp("rpbg", [32, 128, 1024])
    natmask = inp("natmask", [3, 128, 576])
    cache_k = inp("cache_k", [32, 256, 64])
    cache_v = inp("cache_v", [32, 256, 64])
    new_k = k.dram("new_k", [4, 32, 256, 64], F32, kind="ExternalOutput")
    new_v = k.dram("new_v", [4, 32, 256, 64], F32, kind="ExternalOutput")
    y_out = k.dram("y_out", [NTOK, D], F32, kind="ExternalOutput")
    xres = k.dram("xres", [NTOK, D], F32)
    dma_done = Buf()

    identf = k.sb([128, 128], F32)
    identb = k.sb([128, 128], BF16)
    onesf = k.sb([128, 128], F32)
    S.op("pool", lambda e: e.memset(identf[:], 0.0), writes=bufs(identf))
    S.op("pool", lambda e: e.affine_select(out=identf[:], in_=identf[:], compare_op=ALU.not_equal, fill=1.0,
                                           base=0, pattern=[[-1, 128]], channel_multiplier=1),
         reads=bufs(identf), writes=bufs(identf))
    S.op("dve", lambda e: e.tensor_copy(out=identb[:], in_=identf[:]), reads=bufs(identf), writes=bufs(identb))
    S.op("pool", lambda e: e.memset(onesf[:], 1.0), writes=bufs(onesf))

    banks = Ring([k.psb([128, 512], F32) for _ in range(8)])

    hT = k.sb([128, 8, 2048], BF16, "hT")
    wo = T(hT.t)
    wo.b = hT.b
    wo_view = hT.t[:].rearrange("p k t -> p (k t)").rearrange("p (k n) -> p k n", k=16)
    wring = k.ring(3, [128, 8, 512], BF16)
    junk = k.sb([128, D], BF16)
    small = k.ring(8, [128, 8], F32)
    rstd_keep = k.sb([128, 16], F32)
    k.init_arena(136 * 1024)
    L = {}

    cf = k.sb([128, 8, 2], F32)
    cb = k.sb([128, 8, 2], BF16)
    for c_ in range(2):
        S.dma("sp", cf[:, :, c_], cvec[c_].rearrange("(k p) -> p k", p=128), writes=bufs(cf))
    S.op("act", lambda e: e.activation(out=cb[:], in_=cf[:], func=AF.Silu), reads=bufs(cf), writes=bufs(cb))

    modT = k.sb([128, 16, 2], F32)
    bmodT = k.sb([128, 16], F32)
    ngT = k.sb([128, 8], F32)
    Asc = k.sb([128, 8, 2], F32)
    gate_bc = [k.sb([128, D], F32), k.sb([128, D], F32)]
    sel = [k.sb([2, 128], F32), k.sb([2, 128], F32)]
    for c in range(2):
        S.op("pool", lambda e, c=c: e.memset(sel[c][:], 0.0), writes=bufs(sel[c]))
        S.op("pool", lambda e, c=c: e.affine_select(out=sel[c][:], in_=sel[c][:], compare_op=ALU.not_equal, fill=1.0,
                                                     base=-c, pattern=[[0, 128]], channel_multiplier=1),
             reads=bufs(sel[c]), writes=bufs(sel[c]))

    def load_w(wap, c0, n, q="pool"):
        wt = wring.next()
        S.dma(q, wt[:, :, 0:n], wap.rearrange("(k p) n -> p k n", p=128)[:, :, c0:c0 + n], writes=bufs(wt))
        return wt

    def phase_a(li):
        ma_ = k.amark()
        gate2 = k.at([2, D], F32)
        bgate2 = k.at([2, D], F32)
        S.dma("sp", bmodT[:], b_mod[li, 0:2 * D].rearrange("(c p) -> p c", p=128), writes=bufs(bmodT))
        S.dma("sp", ngT[:], norm_g[li].rearrange("(c p) -> p c", p=128), writes=bufs(ngT))
        S.dma("sp", bgate2[:], b_mod[li, 2 * D:3 * D].partition_broadcast(2), writes=bufs(bgate2))
        for blk in range(4):
            wt = load_w(w_mod[li], blk * 512, 512)
            for cc in range(4):
                ch = blk * 4 + cc
                pb = banks.next()
                for kk in range(8):
                    S.op("pe", lambda e, pb=pb, wt=wt, cc=cc, kk=kk: e.matmul(
                        pb[:, 0:2], lhsT=wt[:, kk, cc * 128:(cc + 1) * 128], rhs=cb[:, kk, :],
                        start=(kk == 0), stop=(kk == 7)), reads=bufs(wt, cb), writes=bufs(pb))
                S.op("dve", lambda e, pb=pb, ch=ch: e.tensor_scalar(
                    out=modT[:, ch, :], in0=pb[:, 0:2], scalar1=bmodT[:, ch:ch + 1], scalar2=None, op0=ALU.add),
                    reads=bufs(pb, bmodT), writes=bufs(modT))
        S.op("dve", lambda e: e.tensor_scalar(out=Asc[:], in0=modT[:, 8:16, :], scalar1=1.0, scalar2=None, op0=ALU.add),
             reads=bufs(modT), writes=bufs(Asc))
        S.op("dve", lambda e: e.tensor_tensor(out=Asc[:], in0=Asc[:], in1=ngT[:].unsqueeze(2).to_broadcast([128, 8, 2]),
                                              op=ALU.mult), reads=bufs(Asc, ngT), writes=bufs(Asc))
        for blk in range(2):
            wt = load_w(w_mod[li], 2 * D + blk * 512, 512)
            pb = banks.next()
            for kk in range(8):
                S.op("pe", lambda e, pb=pb, wt=wt, kk=kk: e.matmul(
                    pb[0:2, :], lhsT=cb[:, kk, :], rhs=wt[:, kk, :], start=(kk == 0), stop=(kk == 7)),
                    reads=bufs(wt, cb), writes=bufs(pb))
            S.op("dve", lambda e, pb=pb, blk=blk: e.tensor_tensor(
                out=gate2[:, blk * 512:(blk + 1) * 512], in0=pb[0:2, :], in1=bgate2[:, blk * 512:(blk + 1) * 512],
                op=ALU.add), reads=bufs(pb, bgate2), writes=bufs(gate2))
        for c in range(2):
            for blk in range(2):
                pb = banks.next()
                S.op("pe", lambda e, pb=pb, c=c, blk=blk: e.matmul(
                    pb[:], lhsT=sel[c][:], rhs=gate2[:, blk * 512:(blk + 1) * 512], start=True, stop=True),
                    reads=bufs(sel[c], gate2), writes=bufs(pb))
                S.op("act", lambda e, pb=pb, c=c, blk=blk: e.activation(
                    out=gate_bc[c][:, blk * 512:(blk + 1) * 512], in_=pb[:], func=AF.Copy),
                    reads=bufs(pb), writes=bufs(gate_bc[c]))
        k.arestore(ma_)

    def rows_std(tok0):
        return lambda src: src[tok0:tok0 + 128, :]

    def rms_stats(xt):
        st = small.next()
        S.op("act", lambda e: e.activation(out=junk[:], in_=xt[:], func=AF.Square, accum_out=st[:, 0:1]),
             reads=bufs(xt), writes=bufs(junk, st))
        S.op("dve", lambda e: e.tensor_scalar(out=st[:, 0:1], in0=st[:, 0:1], scalar1=1.0 / D, scalar2=EPS,
                                              op0=ALU.mult, op1=ALU.add), reads=bufs(st), writes=bufs(st))
        S.op("act", lambda e: e.activation(out=st[:, 0:1], in_=st[:, 0:1], func=AF.Sqrt), reads=bufs(st), writes=bufs(st))
        S.op("dve", lambda e: e.reciprocal(out=st[:, 0:1], in_=st[:, 0:1]), reads=bufs(st), writes=bufs(st))
        return st

    def phase_b(src, tiles, cond):
        m_ = k.amark()
        xring = k.aring(3, [128, D], F32)
        xnring = k.aring(2, [128, D], BF16)

        def load(i):
            xt = xring.next()
            S.dma("sp", xt[:], tiles[i][0](src), writes=bufs(xt))
            return xt
        nxt = load(0)
        for i in range(len(tiles)):
            xt = nxt
            if i + 1 < len(tiles):
                nxt = load(i + 1)
            col0 = tiles[i][1]
            st = rms_stats(xt)
            xn = xnring.next()
            S.op("dve", lambda e, xn=xn, xt=xt, st=st: e.tensor_scalar(out=xn[:], in0=xt[:], scalar1=st[:, 0:1],
                                                                   scalar2=None, op0=ALU.mult),
                 reads=bufs(xt, st), writes=bufs(xn))
            pb = banks.next()
            pv = pb[:].bitcast(BF16).rearrange("p (k t) -> p k t", k=8)
            for kk in range(8):
                S.op("pe", lambda e, pv=pv, xn=xn, kk=kk: e.transpose(out=pv[:, kk, :], in_=xn[:, kk * 128:(kk + 1) * 128],
                                                                    identity=identb[:]),
                     reads=bufs(xn, identb), writes=bufs(pb))
            for kk in range(8):
                S.op("act", lambda e, pv=pv, kk=kk, col0=col0: e.activation(
                    out=hT[:, kk, col0:col0 + 128], in_=pv[:, kk, :], func=AF.Identity,
                    scale=Asc[:, kk, cond:cond + 1], bias=modT[:, kk, cond:cond + 1]),
                    reads=bufs(pb, Asc, modT), writes=bufs(hT))
        k.arestore(m_)


    def load_wout(li):
        for h in range(2):
            S.dma("pool", wo_view[:, h * 8:(h + 1) * 8, :],
                  w_out[li].rearrange("(k p) n -> p k n", p=128)[:, h * 8:(h + 1) * 8, :], writes=bufs(wo))

    def phase_d(src, dst, tiles, cond, last, scale_t=None, ytok0=None):
        if cfg.get("skip_d"):
            return
        m_ = k.amark()
        xring = k.aring(3, [128, D], F32)
        tring = k.aring(2, [128, D], F32)
        if last:
            fg_bc = k.at([128, D], F32)
            S.dma("sp", fg_bc[:], final_g.partition_broadcast(128), writes=bufs(fg_bc))
        if ytok0 is None:
            yT = L["yT"]
        else:
            yring = k.aring(2, [128, 16, 512], BF16)
            yT = None

        def load(i):
            xt = xring.next()
            S.dma("sp", xt[:], tiles[i][0](src), writes=bufs(xt))
            return xt
        nxt = load(0)
        for i in range(len(tiles)):
            xt = nxt
            if i + 1 < len(tiles):
                nxt = load(i + 1)
            col0 = tiles[i][1]
            if ytok0 is not None:
                if i % 4 == 0:
                    yT = yring.next()
                    S.dma("sp", yT[:], yscr[:, :, ytok0 + tiles[i][1]:ytok0 + tiles[i][1] + 512].rearrange("b p t -> p b t"),
                          writes=bufs(yT))
                col0 = (i % 4) * 128
            tt = tring.next()
            for h in range(2):
                pb = banks.next()
                for kk in range(16):
                    S.op("pe", lambda e, pb=pb, kk=kk, h=h, col0=col0, yT=yT: e.matmul(
                        pb[:], lhsT=yT[:, kk, col0:col0 + 128], rhs=wo_view[:, kk, h * 512:(h + 1) * 512],
                        start=(kk == 0), stop=(kk == 15)), reads=bufs(yT, wo), writes=bufs(pb))
                if scale_t is None:
                    S.op("dve", lambda e, pb=pb, tt=tt, h=h: e.tensor_tensor(
                        out=tt[:, h * 512:(h + 1) * 512], in0=pb[:], in1=gate_bc[cond][:, h * 512:(h + 1) * 512],
                        op=ALU.mult), reads=bufs(pb, gate_bc[cond]), writes=bufs(tt))
                else:
                    sc = scale_t(i)
                    S.op("dve", lambda e, pb=pb, tt=tt, h=h, sc=sc: e.scalar_tensor_tensor(
                        out=tt[:, h * 512:(h + 1) * 512], in0=pb[:], scalar=sc[0], in1=gate_bc[cond][:, h * 512:(h + 1) * 512],
                        op0=ALU.mult, op1=ALU.mult), reads=bufs(pb, gate_bc[cond]) + [sc[1]], writes=bufs(tt))
            S.op("pool", lambda e, tt=tt, xt=xt: e.tensor_tensor(out=xt[:], in0=tt[:], in1=xt[:], op=ALU.add),
                 reads=bufs(tt, xt), writes=bufs(xt))
            if not last:
                S.dma("sp", tiles[i][0](dst), xt[:], reads=bufs(xt))
            else:
                st = rms_stats(xt)
                S.op("dve", lambda e, tt=tt, xt=xt, st=st: e.scalar_tensor_tensor(
                    out=tt[:], in0=xt[:], scalar=st[:, 0:1], in1=fg_bc[:], op0=ALU.mult, op1=ALU.mult),
                    reads=bufs(xt, st, fg_bc), writes=bufs(tt))
                S.dma("sp", tiles[i][0](y_out), tt[:], reads=bufs(tt))
        k.arestore(m_)

    def gmlp_consts():
        c = {}
        c["lngT"] = k.at([128, 16], F32)
        c["lnbT"] = k.at([128, 16], F32)
        c["wsT"] = k.at([128, 8, 128], BF16)
        c["wsTf"] = k.at([128, 8, 128], F32)
        c["bs_bc"] = k.at([128, 8, 128], F32)
        c["Bt"] = k.at([128, 16, 128], F32)
        S.dma("sp", c["lngT"][:], mlp_ln_g.rearrange("(c p) -> p c", p=128), writes=bufs(c["lngT"]))
        S.dma("sp", c["lnbT"][:], mlp_ln_b.rearrange("(c p) -> p c", p=128), writes=bufs(c["lnbT"]))
        S.dma("sp", c["wsTf"][:], mlp_w_sT.rearrange("g j i -> j g i"), writes=bufs(c["wsTf"]))
        S.dma("sp", c["bs_bc"][:].rearrange("p g i -> p (g i)"), mlp_b_s.rearrange("g i -> (g i)").partition_broadcast(128),
              writes=bufs(c["bs_bc"]))
        S.op("dve", lambda e: e.tensor_copy(out=c["wsT"][:], in_=c["wsTf"][:]), reads=bufs(c["wsTf"]), writes=bufs(c["wsT"]))
        for half in range(2):
            pb = banks.next()
            S.op("pe", lambda e, pb=pb, half=half: e.matmul(
                pb[:], lhsT=onesf[:], rhs=c["wsTf"][:, half * 4:(half + 1) * 4, :].rearrange("p g i -> p (g i)"),
                start=True, stop=True), reads=bufs(onesf, c["wsTf"]), writes=bufs(pb))
            for gg in range(4):
                g = half * 4 + gg
                for bb in range(2):
                    blk = g * 2 + bb
                    S.op("dve", lambda e, pb=pb, gg=gg, g=g, blk=blk: e.scalar_tensor_tensor(
                        out=c["Bt"][:, blk, :], in0=pb[:, gg * 128:(gg + 1) * 128], scalar=c["lnbT"][:, blk:blk + 1],
                        in1=c["bs_bc"][:, g, :], op0=ALU.mult, op1=ALU.add),
                        reads=bufs(pb, c["lnbT"], c["bs_bc"]), writes=bufs(c["Bt"]))
        c["vv"] = k.at([128, 8, E], BF16)
        c["gtmp"] = k.aring(2, [128, 512], F32)
        c["ug"] = k.aring(2, [128, 512], F32)
        c["zs"] = k.aring(2, [128, 512], F32)
        c["sg"] = k.aring(2, [128, 512], F32)
        c["st"] = k.at([128, 8, 8], F32)
        return c

    def gmlp_unit(c, ntile):
        yT = L["yT"]
        vv = c["vv"]
        stt = c["st"]
        for b in range(4):
            wv = load_w(mlp_w_in, E + b * 512, 512)
            for t in range(ntile):
                pb = banks.next()
                for kk in range(8):
                    S.op("pe", lambda e, pb=pb, t=t, wv=wv, kk=kk: e.matmul(
                        pb[:], lhsT=hT[:, kk, t * 128:(t + 1) * 128], rhs=wv[:, kk, :], start=(kk == 0), stop=(kk == 7)),
                        reads=bufs(hT, wv), writes=bufs(pb))
                gt = c["gtmp"].next()
                S.op("act", lambda e, pb=pb, b=b, t=t, gt=gt: e.activation(
                    out=gt[:], in_=pb[:], func=AF.Gelu, accum_out=stt[:, t, b:b + 1]),
                    reads=bufs(pb), writes=bufs(gt, stt))
                S.op("act", lambda e, b=b, t=t, gt=gt: e.activation(
                    out=junk[:, 0:512], in_=gt[:], func=AF.Square, accum_out=stt[:, t, 4 + b:5 + b]),
                    reads=bufs(gt), writes=bufs(junk, stt))
                S.op("pool", lambda e, b=b, t=t, gt=gt: e.tensor_copy(out=vv[:, t, b * 512:(b + 1) * 512], in_=gt[:]),
                     reads=bufs(gt), writes=bufs(vv))
        for t in range(ntile):
            st2 = small.next()
            S.op("dve", lambda e, t=t, st2=st2: e.tensor_reduce(
                out=st2[:, 0:2], in_=stt[:, t, :].rearrange("p (a b) -> p a b", a=2), axis=AX.X, op=ALU.add),
                reads=bufs(stt), writes=bufs(st2))
            S.op("dve", lambda e, st2=st2: e.tensor_scalar(out=st2[:, 0:2], in0=st2[:, 0:2], scalar1=1.0 / E, scalar2=None,
                                                           op0=ALU.mult), reads=bufs(st2), writes=bufs(st2))
            S.op("dve", lambda e, st2=st2: e.tensor_tensor(out=st2[:, 2:3], in0=st2[:, 0:1], in1=st2[:, 0:1], op=ALU.mult),
                 reads=bufs(st2), writes=bufs(st2))
            S.op("dve", lambda e, st2=st2: e.scalar_tensor_tensor(out=st2[:, 2:3], in0=st2[:, 2:3], scalar=-1.0, in1=st2[:, 1:2],
                                                                  op0=ALU.mult, op1=ALU.add), reads=bufs(st2), writes=bufs(st2))
            S.op("dve", lambda e, st2=st2: e.tensor_scalar(out=st2[:, 2:3], in0=st2[:, 2:3], scalar1=EPS, scalar2=None,
                                                           op0=ALU.add), reads=bufs(st2), writes=bufs(st2))
            S.op("act", lambda e, st2=st2: e.activation(out=st2[:, 2:3], in_=st2[:, 2:3], func=AF.Sqrt),
                 reads=bufs(st2), writes=bufs(st2))
            S.op("dve", lambda e, st2=st2: e.reciprocal(out=st2[:, 2:3], in_=st2[:, 2:3]), reads=bufs(st2), writes=bufs(st2))
            S.op("dve", lambda e, st2=st2, t=t: e.tensor_scalar(
                out=vv[:, t, :], in0=vv[:, t, :], scalar1=st2[:, 0:1], scalar2=st2[:, 2:3], op0=ALU.subtract, op1=ALU.mult),
                reads=bufs(vv, st2), writes=bufs(vv))
        nq = ntile // 4
        for blk in range(16):
            g = blk // 2
            if blk % 4 == 0:
                wu = load_w(mlp_w_in, blk * 128, 512)
                wz = load_w(mlp_w_in, 2 * E + blk * 128, 512)
            co = (blk % 4) * 128
            for q in range(nq):
                ug = c["ug"].next()
                zs = c["zs"].next()
                sg = c["sg"].next()
                pu = banks.next()
                for kk in range(8):
                    S.op("pe", lambda e, pu=pu, kk=kk, q=q, wu=wu, co=co: e.matmul(
                        pu[:], lhsT=wu[:, kk, co:co + 128], rhs=hT[:, kk, q * 512:(q + 1) * 512],
                        start=(kk == 0), stop=(kk == 7)), reads=bufs(wu, hT), writes=bufs(pu))
                S.op("act", lambda e, pu=pu, ug=ug: e.activation(out=ug[:], in_=pu[:], func=AF.Gelu),
                     reads=bufs(pu), writes=bufs(ug))
                pz = banks.next()
                for kk in range(8):
                    S.op("pe", lambda e, pz=pz, kk=kk, q=q, wz=wz, co=co: e.matmul(
                        pz[:], lhsT=wz[:, kk, co:co + 128], rhs=hT[:, kk, q * 512:(q + 1) * 512],
                        start=(kk == 0), stop=(kk == 7)), reads=bufs(wz, hT), writes=bufs(pz))
                S.op("act", lambda e, pz=pz, zs=zs: e.activation(out=zs[:], in_=pz[:], func=AF.Tanh, scale=0.5),
                     reads=bufs(pz), writes=bufs(zs))
                S.op("dve", lambda e, pz=pz, zs=zs: e.scalar_tensor_tensor(
                    out=zs[:], in0=zs[:], scalar=1.0, in1=pz[:], op0=ALU.add, op1=ALU.mult), reads=bufs(pz, zs), writes=bufs(zs))
                ps_ = banks.next()
                for cc in range(4):
                    t = q * 4 + cc
                    S.op("pe", lambda e, ps_=ps_, t=t, cc=cc, blk=blk, g=g: e.matmul(
                        ps_[:, cc * 128:(cc + 1) * 128], lhsT=vv[:, t, blk * 128:(blk + 1) * 128], rhs=c["wsT"][:, g, :],
                        start=True, stop=True), reads=bufs(vv, c["wsT"]), writes=bufs(ps_))
                S.op("dve", lambda e, ps_=ps_, blk=blk, sg=sg: e.scalar_tensor_tensor(
                    out=sg[:].rearrange("p (c i) -> p c i", c=4),
                    in0=ps_[:].rearrange("p (c i) -> p c i", c=4),
                    scalar=c["lngT"][:, blk:blk + 1],
                    in1=c["Bt"][:, blk:blk + 1, :].to_broadcast([128, 4, 128]), op0=ALU.mult, op1=ALU.add),
                    reads=bufs(ps_, c["lngT"], c["Bt"]), writes=bufs(sg))
                S.op("pool", lambda e, sg=sg, ug=ug: e.tensor_tensor(out=sg[:], in0=sg[:], in1=ug[:], op=ALU.mult),
                     reads=bufs(sg, ug), writes=bufs(sg))
                S.op("dve", lambda e, q=q, sg=sg, zs=zs, blk=blk: e.scalar_tensor_tensor(
                    out=yT[:, blk, q * 512:(q + 1) * 512], in0=sg[:], scalar=0.5, in1=zs[:], op0=ALU.mult, op1=ALU.mult),
                    reads=bufs(sg, zs), writes=bufs(yT))

    SCALE = 0.125

    def nat_proj(hp, ntok, c, with_ktm):
        wt = wring.next()
        for j in (0, 1, 3):
            S.dma("pool", wt[:, :, j * 128:(j + 1) * 128],
                  nat_w_in.rearrange("(k p) n -> p k n", p=128)[:, :, j * E + hp * 128:j * E + (hp + 1) * 128],
                  writes=bufs(wt))
        qT, kT, gT = c["qT"], c["kT"], c["gT"]
        for q in range(ntok // 512):
            for j, dst, fn in ((0, qT, AF.Copy), (1, kT, AF.Copy), (3, gT, AF.Silu)):
                pb = banks.next()
                for kk in range(8):
                    S.op("pe", lambda e, pb=pb, kk=kk, q=q, j=j, wt=wt: e.matmul(
                        pb[:], lhsT=wt[:, kk, j * 128:(j + 1) * 128], rhs=hT[:, kk, q * 512:(q + 1) * 512],
                        start=(kk == 0), stop=(kk == 7)), reads=bufs(wt, hT), writes=bufs(pb))
                S.op("act", lambda e, pb=pb, q=q, dst=dst, fn=fn: e.activation(
                    out=dst[:, q * 512:(q + 1) * 512], in_=pb[:], func=fn), reads=bufs(pb), writes=bufs(dst))

    def nat_proj_v4(hp4, ntok, c, with_ktm):
        vb = c["vb"]
        wv = load_w(nat_w_in, 2 * E + hp4 * 512, 512)
        wk = load_w(nat_w_in, E + hp4 * 512, 512) if with_ktm else None
        for t in range(ntok // 128):
            pv_ = banks.next()
            for kk in range(8):
                S.op("pe", lambda e, pv_=pv_, kk=kk, t=t, wv=wv: e.matmul(
                    pv_[:], lhsT=hT[:, kk, t * 128:(t + 1) * 128], rhs=wv[:, kk, :], start=(kk == 0), stop=(kk == 7)),
                    reads=bufs(wv, hT), writes=bufs(pv_))
            if not with_ktm:
                S.op("act", lambda e, pv_=pv_, t=t: e.activation(out=vb[:, t, :], in_=pv_[:], func=AF.Copy),
                     reads=bufs(pv_), writes=bufs(vb))
            else:
                vst, kst = c["vst"], c["kst"]
                S.op("act", lambda e, pv_=pv_, t=t: e.activation(out=vst[:, t, :], in_=pv_[:], func=AF.Copy),
                     reads=bufs(pv_), writes=bufs(vst))
                S.op("pool", lambda e, t=t: e.tensor_copy(out=vb[:, t, :], in_=vst[:, t, :]), reads=bufs(vst), writes=bufs(vb))
                pk_ = banks.next()
                for kk in range(8):
                    S.op("pe", lambda e, pk_=pk_, kk=kk, t=t, wk=wk: e.matmul(
                        pk_[:], lhsT=hT[:, kk, t * 128:(t + 1) * 128], rhs=wk[:, kk, :], start=(kk == 0), stop=(kk == 7)),
                        reads=bufs(wk, hT), writes=bufs(pk_))
                S.op("act", lambda e, pk_=pk_, t=t: e.activation(out=kst[:, t, :], in_=pk_[:], func=AF.Copy),
                     reads=bufs(pk_), writes=bufs(kst))

    def nat_ctx_unit():
        yT = L["yT"]
        c = {"qT": k.at([128, 1024], BF16), "kT": k.at([128, 1024], BF16), "gT": k.at([128, 1024], BF16),
             "vb": k.at([128, 8, 512], BF16), "vst": k.at([128, 8, 512], F32), "kst": k.at([128, 8, 512], F32)}
        er = k.aring(2, [128, 512], F32)
        pbr = k.aring(2, [128, 512], BF16)
        ptr_ = k.aring(2, [128, 512], BF16)
        for hp in range(cfg.get("ctx_hp", 16)):
            if hp % 4 == 0:
                nat_proj_v4(hp // 4, 1024, c, True)
            nat_proj(hp, 1024, c, True)
            for hd in range(2):
                h = hp * 2 + hd
                for sq in range(0 if cfg.get("no_kv") else 4):
                    S.dma("sp", new_k[sq, h, :, :].rearrange("(t p) d -> p t d", p=128),
                          c["kst"][:, sq * 2:(sq + 1) * 2, (hp % 4) * 128 + hd * 64:(hp % 4) * 128 + (hd + 1) * 64], reads=bufs(c["kst"]))
                    S.dma("sp", new_v[sq, h, :, :].rearrange("(t p) d -> p t d", p=128),
                          c["vst"][:, sq * 2:(sq + 1) * 2, (hp % 4) * 128 + hd * 64:(hp % 4) * 128 + (hd + 1) * 64], reads=bufs(c["vst"]))
            qT, kT, gT, vb = c["qT"], c["kT"], c["gT"], c["vb"]
            cb_ = Ring(banks.tiles[2:8])
            pob_ = Ring(banks.tiles[0:2])

            vo = (hp % 4) * 128

            def c_qk(sq, hd):
                rows = slice(hd * 64, (hd + 1) * 64)
                tok0 = sq * 256
                ps_ = cb_.next()
                for qt in range(2):
                    S.op("pe", lambda e, ps_=ps_, qt=qt, rows=rows, tok0=tok0: e.matmul(
                        ps_[:, qt * 256:(qt + 1) * 256], lhsT=qT[rows, tok0 + qt * 128:tok0 + (qt + 1) * 128],
                        rhs=kT[rows, tok0:tok0 + 256], start=True, stop=True), reads=bufs(qT, kT), writes=bufs(ps_))
                return ps_

            def c_softmax(ps_):
                mx = small.next()
                S.op("dve", lambda e, ps_=ps_, mx=mx: e.tensor_reduce(
                    out=mx[:, 0:2], in_=ps_[:].rearrange("p (a b) -> p a b", a=2), axis=AX.X, op=ALU.max),
                    reads=bufs(ps_), writes=bufs(mx))
                S.op("dve", lambda e, mx=mx: e.tensor_scalar(out=mx[:, 2:4], in0=mx[:, 0:2], scalar1=-SCALE, scalar2=None,
                                                             op0=ALU.mult), reads=bufs(mx), writes=bufs(mx))
                et = er.next()
                for qt in range(2):
                    S.op("act", lambda e, ps_=ps_, mx=mx, et=et, qt=qt: e.activation(
                        out=et[:, qt * 256:(qt + 1) * 256], in_=ps_[:, qt * 256:(qt + 1) * 256], func=AF.Exp, scale=SCALE,
                        bias=mx[:, 2 + qt:3 + qt], accum_out=mx[:, 4 + qt:5 + qt]), reads=bufs(ps_, mx), writes=bufs(et, mx))
                S.op("dve", lambda e, mx=mx: e.reciprocal(out=mx[:, 6:8], in_=mx[:, 4:6]), reads=bufs(mx), writes=bufs(mx))
                pbt = pbr.next()
                S.op("dve", lambda e, mx=mx, et=et, pbt=pbt: e.tensor_tensor(
                    out=pbt[:].rearrange("p (a b) -> p a b", a=2), in0=et[:].rearrange("p (a b) -> p a b", a=2),
                    in1=mx[:, 6:8].unsqueeze(2).to_broadcast([128, 2, 256]), op=ALU.mult),
                    reads=bufs(mx, et), writes=bufs(pbt))
                return pbt

            def c_tpv(sq, hd, pbt, po):
                rows = slice(hd * 64, (hd + 1) * 64)
                ptb = cb_.next()
                ptv = ptb[:].bitcast(BF16)
                for j in range(4):
                    S.op("pe", lambda e, ptv=ptv, pbt=pbt, j=j: e.transpose(
                        out=ptv[:, j * 128:(j + 1) * 128], in_=pbt[:, j * 128:(j + 1) * 128], identity=identb[:]),
                        reads=bufs(pbt, identb), writes=bufs(ptb))
                pts = ptr_.next()
                S.op("act", lambda e, ptv=ptv, pts=pts: e.activation(out=pts[:], in_=ptv[:, 0:512], func=AF.Copy),
                     reads=bufs(ptb), writes=bufs(pts))
                for qt in range(2):
                    for kb in range(2):
                        S.op("pe", lambda e, po=po, rows=rows, qt=qt, kb=kb, sq=sq, hd=hd, pts=pts, vo=vo: e.matmul(
                            po[rows, qt * 128:(qt + 1) * 128], lhsT=vb[:, sq * 2 + kb, vo + hd * 64:vo + (hd + 1) * 64],
                            rhs=pts[:, (qt * 2 + kb) * 128:(qt * 2 + kb + 1) * 128], start=(kb == 0), stop=(kb == 1)),
                            reads=bufs(vb, pts), writes=bufs(po))

            its = [(sq, hd) for sq in range(0 if cfg.get("ctx_stage", 9) < 1 else 4) for hd in range(2)]
            nxt = c_qk(*its[0]) if its else None
            po = None
            for ii, (sq, hd) in enumerate(its):
                if hd == 0:
                    po = pob_.next()
                pbt = c_softmax(nxt)
                if ii + 1 < len(its):
                    nxt = c_qk(*its[ii + 1])
                c_tpv(sq, hd, pbt, po)
                if hd == 1:
                    tok0 = sq * 256
                    S.op("dve", lambda e, po=po, hp=hp, tok0=tok0: e.tensor_tensor(
                        out=yT[:, hp, tok0:tok0 + 256], in0=po[:, 0:256], in1=gT[:, tok0:tok0 + 256], op=ALU.mult),
                        reads=bufs(po, gT), writes=bufs(yT))

    def nat_lat_unit():
        yT = L["yT"]
        c = {"qT": k.at([128, 2048], BF16), "kT": k.at([128, 2048], BF16), "gT": k.at([128, 2048], BF16),
             "vb": k.at([128, 16, 512], BF16)}
        maskf = k.at([128, 3, 576], F32)
        maskb = k.at([128, 3, 576], BF16)
        for j in range(3):
            S.dma("sp", maskf[:, j, :], natmask[j], writes=bufs(maskf))
        S.op("dve", lambda e: e.tensor_copy(out=maskb[:], in_=maskf[:]), reads=bufs(maskf), writes=bufs(maskb))
        ckr = k.aring(2, [128, 2, 2, 64], BF16)
        cvr = k.aring(2, [128, 2, 2, 64], BF16)
        cktr = k.aring(2, [128, 256], BF16)
        rpr = k.aring(2, [128, 1024], F32)
        scr = k.aring(2, [128, 832], F32)
        pbr = k.aring(2, [128, 832], BF16)
        ptr_ = k.aring(2, [128, 896], BF16)
        cfg["alog"] = k.alog
        pobanks = Ring(banks.tiles[0:2])
        wbanks = Ring(banks.tiles[2:8])
        for hp in range(cfg.get("nat_hp", 16)):
            if hp % 4 == 0:
                nat_proj_v4(hp // 4, 2048, c, False)
            nat_proj(hp, 2048, c, False)
            vo = (hp % 4) * 128
            qT, kT, gT, vb = c["qT"], c["kT"], c["gT"], c["vb"]
            ck = ckr.next()
            cv = cvr.next()
            for hd in range(2):
                S.dma("pool", ck[:, :, hd, :], cache_k[hp * 2 + hd].rearrange("(kb p) d -> p kb d", p=128), writes=bufs(ck))
                S.dma("pool", cv[:, :, hd, :], cache_v[hp * 2 + hd].rearrange("(kb p) d -> p kb d", p=128), writes=bufs(cv))
            ckT = cktr.next()
            ptb = banks.next()
            ptv = ptb[:].bitcast(BF16)
            for kb in range(2):
                S.op("pe", lambda e, ptv=ptv, ck=ck, kb=kb: e.transpose(
                    out=ptv[:, kb * 128:(kb + 1) * 128], in_=ck[:, kb, :, :].rearrange("p a b -> p (a b)"), identity=identb[:]),
                    reads=bufs(ck, identb), writes=bufs(ptb))
            S.op("act", lambda e, ptv=ptv, ckT=ckT: e.activation(out=ckT[:], in_=ptv[:, 0:256], func=AF.Copy),
                 reads=bufs(ptb), writes=bufs(ckT))
            rps = []
            for hd in range(2):
                rp = rpr.next()
                S.dma("sp", rp[:], rpbg[hp * 2 + hd], writes=bufs(rp))
                rps.append(rp)
            items = []
            for pg in range(cfg.get("nat_pg", 4)):
                for hd in range(cfg.get("nat_hd", 2)):
                    for pi in range(cfg.get("nat_pi", 4)):
                        items.append((pg, hd, pi))

            def geom(pg, hd, pi):
                pr = pg * 4 + pi
                r = 2 * pr
                if pr <= 1:
                    r0, nrow, a0, mi = 0, 9, 7 - r, 1
                elif pr >= 14:
                    r0, nrow, a0, mi = 24, 8, (3 if pr == 14 else 1), 2
                else:
                    r0, nrow, a0, mi = r - 4, 9, 3, 0
                return r, r0, nrow, a0, mi

            def st_qk(it):
                pg, hd, pi = it
                r, r0, nrow, a0, mi = geom(*it)
                rows = slice(hd * 64, (hd + 1) * 64)
                q0, k0 = r * 64, r0 * 64
                ps1 = wbanks.next()
                ps2 = wbanks.next()
                S.op("pe", lambda e, ps1=ps1, rows=rows, q0=q0, k0=k0: e.matmul(
                    ps1[:], lhsT=qT[rows, q0:q0 + 128], rhs=kT[rows, k0:k0 + 512], start=True, stop=False),
                    reads=bufs(qT, kT), writes=bufs(ps1))
                S.op("pe", lambda e, ps1=ps1, mi=mi: e.matmul(
                    ps1[:], lhsT=identb[:], rhs=maskb[:, mi, 0:512], start=False, stop=True),
                    reads=bufs(identb, maskb), writes=bufs(ps1))
                if nrow == 9:
                    S.op("pe", lambda e, ps2=ps2, rows=rows, q0=q0, k0=k0: e.matmul(
                        ps2[:, 0:64], lhsT=qT[rows, q0:q0 + 128], rhs=kT[rows, k0 + 512:k0 + 576], start=True, stop=False),
                        reads=bufs(qT, kT), writes=bufs(ps2))
                    S.op("pe", lambda e, ps2=ps2, mi=mi: e.matmul(
                        ps2[:, 0:64], lhsT=identb[:], rhs=maskb[:, mi, 512:576], start=False, stop=True),
                        reads=bufs(identb, maskb), writes=bufs(ps2))
                S.op("pe", lambda e, ps2=ps2, rows=rows, q0=q0, ckT=ckT: e.matmul(
                    ps2[:, 64:320], lhsT=qT[rows, q0:q0 + 128], rhs=ckT[rows, :], start=True, stop=True),
                    reads=bufs(qT, ckT), writes=bufs(ps2))
                return ps1, ps2

            def st_softmax(it, ps1, ps2):
                pg, hd, pi = it
                r, r0, nrow, a0, mi = geom(*it)
                rp = rps[hd]
                nk = nrow * 64
                sc = scr.next()
                S.op("dve", lambda e, ps1=ps1, sc=sc, rp=rp, a0=a0: e.scalar_tensor_tensor(
                    out=sc[:, 0:512], in0=ps1[:], scalar=SCALE, in1=rp[:, a0 * 64:a0 * 64 + 512],
                    op0=ALU.mult, op1=ALU.add), reads=bufs(ps1, rp), writes=bufs(sc))
                if nrow == 9:
                    S.op("dve", lambda e, ps2=ps2, sc=sc, rp=rp, a0=a0: e.scalar_tensor_tensor(
                        out=sc[:, 512:576], in0=ps2[:, 0:64], scalar=SCALE, in1=rp[:, a0 * 64 + 512:a0 * 64 + 576],
                        op0=ALU.mult, op1=ALU.add), reads=bufs(ps2, rp), writes=bufs(sc))
                S.op("act", lambda e, ps2=ps2, sc=sc, nk=nk: e.activation(
                    out=sc[:, nk:nk + 256], in_=ps2[:, 64:320], func=AF.Copy, scale=SCALE),
                    reads=bufs(ps2), writes=bufs(sc))
                ntot = nk + 256
                mx = small.next()
                S.op("dve", lambda e, sc=sc, mx=mx, ntot=ntot: e.tensor_reduce(
                    out=mx[:, 0:1], in_=sc[:, 0:ntot], axis=AX.X, op=ALU.max), reads=bufs(sc), writes=bufs(mx))
                S.op("dve", lambda e, mx=mx: e.tensor_scalar(out=mx[:, 1:2], in0=mx[:, 0:1], scalar1=-1.0, scalar2=None,
                                                             op0=ALU.mult), reads=bufs(mx), writes=bufs(mx))
                S.op("act", lambda e, sc=sc, mx=mx, ntot=ntot: e.activation(
                    out=sc[:, 0:ntot], in_=sc[:, 0:ntot], func=AF.Exp, bias=mx[:, 1:2], accum_out=mx[:, 2:3]),
                    reads=bufs(sc, mx), writes=bufs(sc, mx))
                S.op("dve", lambda e, mx=mx: e.reciprocal(out=mx[:, 3:4], in_=mx[:, 2:3]), reads=bufs(mx), writes=bufs(mx))
                pbt = pbr.next()
                S.op("dve", lambda e, sc=sc, mx=mx, pbt=pbt, ntot=ntot: e.tensor_scalar(
                    out=pbt[:, 0:ntot], in0=sc[:, 0:ntot], scalar1=mx[:, 3:4], scalar2=None, op0=ALU.mult),
                    reads=bufs(sc, mx), writes=bufs(pbt))
                return pbt

            def st_tpv(it, pbt, po):
                pg, hd, pi = it
                r, r0, nrow, a0, mi = geom(*it)
                rows = slice(hd * 64, (hd + 1) * 64)
                nk = nrow * 64
                ptb = wbanks.next()
                ptv = ptb[:].bitcast(BF16)
                blocks = [(j * 128, 128) for j in range(4)]
                blocks += [(nk, 128), (nk + 128, 128)]
                if nrow == 9:
                    blocks.append((512, 64))
                for j, (c0, w) in enumerate(blocks):
                    S.op("pe", lambda e, ptv=ptv, pbt=pbt, j=j, c0=c0, w=w: e.transpose(
                        out=ptv[0:w, j * 128:(j + 1) * 128], in_=pbt[:, c0:c0 + w], identity=identb[:]),
                        reads=bufs(pbt, identb), writes=bufs(ptb))
                nb = len(blocks)
                pts = ptr_.next()
                S.op("act", lambda e, ptv=ptv, pts=pts: e.activation(
                    out=pts[:, 0:768], in_=ptv[:, 0:768], func=AF.Copy), reads=bufs(ptb), writes=bufs(pts))
                if nrow == 9:
                    S.op("act", lambda e, ptv=ptv, pts=pts: e.activation(
                        out=pts[0:64, 768:896], in_=ptv[0:64, 768:896], func=AF.Copy), reads=bufs(ptb), writes=bufs(pts))
                t0 = r0 // 2
                for j, (c0, w) in enumerate(blocks):
                    if j < 4:
                        lhs = vb[:, t0 + j, vo + hd * 64:vo + (hd + 1) * 64]
                        rhs = pts[:, j * 128:(j + 1) * 128]
                        rd = bufs(vb, pts)
                    elif w == 64:
                        lhs = vb[0:64, t0 + 4, vo + hd * 64:vo + (hd + 1) * 64]
                        rhs = pts[0:64, j * 128:(j + 1) * 128]
                        rd = bufs(vb, pts)
                    else:
                        kb = j - 4
                        lhs = cv[:, kb, hd, :]
                        rhs = pts[:, j * 128:(j + 1) * 128]
                        rd = bufs(cv, pts)
                    S.op("pe", lambda e, po=po, rows=rows, pi=pi, lhs=lhs, rhs=rhs, j=j, nb=nb: e.matmul(
                        po[rows, pi * 128:(pi + 1) * 128], lhsT=lhs, rhs=rhs, start=(j == 0), stop=(j == nb - 1)),
                        reads=rd, writes=bufs(po))

            pos_ = {}
            n_it = len(items)
            qk_res = {}
            sm_res = {}
            for j_ in range(min(2, n_it)):
                qk_res[j_] = st_qk(items[j_])
            if n_it:
                sm_res[0] = st_softmax(items[0], *qk_res.pop(0))
            for ii, it in enumerate(items):
                pg = it[0]
                if pg not in pos_:
                    pos_[pg] = pobanks.next()
                po = pos_[pg]
                if ii + 2 < n_it:
                    qk_res[ii + 2] = st_qk(items[ii + 2])
                if ii + 1 < n_it:
                    sm_res[ii + 1] = st_softmax(items[ii + 1], *qk_res.pop(ii + 1))
                pbt = sm_res.pop(ii)
                st_tpv(it, pbt, po)
                if ii + 1 == len(items) or items[ii + 1][0] != pg:
                    S.op("dve", lambda e, po=po, hp=hp, pg=pg: e.tensor_tensor(
                        out=yT[:, hp, pg * 512:(pg + 1) * 512], in0=po[:], in1=gT[:, pg * 512:(pg + 1) * 512], op=ALU.mult),
                        reads=bufs(po, gT), writes=bufs(yT))

    def ssd_consts():
        c = {}
        c["tri"] = [k.at([128, 128], F32), k.at([128, 128], F32)]
        c["mneg"] = [k.at([128, 128], F32), k.at([128, 128], F32)]
        c["negones"] = k.at([128, 128], F32)
        for d_ in range(2):
            sgn = 1 if d_ == 0 else -1
            S.op("pool", lambda e, d_=d_: e.memset(c["tri"][d_][:], 1.0), writes=bufs(c["tri"][d_]))
            S.op("pool", lambda e, d_=d_, sgn=sgn: e.affine_select(
                out=c["tri"][d_][:], in_=c["tri"][d_][:], compare_op=ALU.is_ge, fill=0.0, base=0,
                pattern=[[sgn, 128]], channel_multiplier=-sgn), reads=bufs(c["tri"][d_]), writes=bufs(c["tri"][d_]))
            S.op("pool", lambda e, d_=d_: e.memset(c["mneg"][d_][:], 0.0), writes=bufs(c["mneg"][d_]))
            S.op("pool", lambda e, d_=d_, sgn=sgn: e.affine_select(
                out=c["mneg"][d_][:], in_=c["mneg"][d_][:], compare_op=ALU.is_ge, fill=-30000.0, base=0,
                pattern=[[sgn, 128]], channel_multiplier=-sgn), reads=bufs(c["mneg"][d_]), writes=bufs(c["mneg"][d_]))
        S.op("pool", lambda e: e.memset(c["negones"][:], -1.0), writes=bufs(c["negones"]))
        c["ntri"] = [k.at([128, 128], F32), k.at([128, 128], F32)]
        for d_ in range(2):
            S.op("pool", lambda e, d_=d_: e.tensor_scalar(out=c["ntri"][d_][:], in0=c["tri"][d_][:], scalar1=-1.0, scalar2=None,
                                                          op0=ALU.mult), reads=bufs(c["tri"][d_]), writes=bufs(c["ntri"][d_]))
        c["cwT"] = k.at([128, 32, 5], F32)
        c["cbT"] = k.at([128, 32], F32)
        for j in range(5):
            S.dma("sp", c["cwT"][:, :, j], ssd_conv_w[j].rearrange("(b p) -> p b", p=128), writes=bufs(c["cwT"]))
        S.dma("sp", c["cbT"][:], ssd_conv_b.rearrange("(b p) -> p b", p=128), writes=bufs(c["cbT"]))
        c["dtb"] = k.at([128, 64], F32)
        c["abc"] = k.at([128, 64], F32)
        c["dsk"] = k.at([128, 32], F32)
        c["ngT"] = k.at([128, 16], F32)
        S.dma("sp", c["dtb"][:], ssd_dt_bias.partition_broadcast(128), writes=bufs(c["dtb"]))
        S.dma("sp", c["abc"][:], ssd_a_log.partition_broadcast(128), writes=bufs(c["abc"]))
        S.dma("sp", c["dsk"][:], ssd_d.partition_broadcast(128), writes=bufs(c["dsk"]))
        S.dma("sp", c["ngT"][:], ssd_norm_g.rearrange("(b p) -> p b", p=128), writes=bufs(c["ngT"]))
        S.op("act", lambda e: e.activation(out=c["abc"][:], in_=c["abc"][:], func=AF.Exp), reads=bufs(c["abc"]), writes=bufs(c["abc"]))
        S.op("dve", lambda e: e.tensor_scalar(out=c["abc"][:], in0=c["abc"][:], scalar1=-1.0, scalar2=None, op0=ALU.mult),
             reads=bufs(c["abc"]), writes=bufs(c["abc"]))
        return c

    def ssd_unit(c, tok0, ntile, nseq, is_lat):
        T_ = ntile * 128
        nch = ntile // nseq
        Lq = nch * 128
        dt_ = k.at([128, ntile, 64], F32)
        da = k.at([128, ntile, 64], F32)
        ecum = k.at([128, ntile, 64], F32)
        dtd = k.at([128, ntile, 64], F32)
        etot = k.at([128, ntile, 64], F32)
        ssq = k.at([128, ntile, 8], F32)
        rstd = k.at([128, ntile], F32)
        tmpr = k.aring(2, [128, 64], F32)
        wdt = wring.next()
        S.dma("pool", wdt[:, :, 0:64], ssd_w_in.rearrange("(k p) n -> p k n", p=128)[:, :, 6144:6208], writes=bufs(wdt))
        for t in range(ntile):
            pb = banks.next()
            for kk in range(8):
                S.op("pe", lambda e, pb=pb, kk=kk, t=t: e.matmul(
                    pb[:, 0:64], lhsT=hT[:, kk, t * 128:(t + 1) * 128], rhs=wdt[:, kk, 0:64], start=(kk == 0), stop=(kk == 7)),
                    reads=bufs(hT, wdt), writes=bufs(pb))
            S.op("dve", lambda e, pb=pb, t=t: e.tensor_tensor(out=dt_[:, t, :], in0=pb[:, 0:64], in1=c["dtb"][:], op=ALU.add),
                 reads=bufs(pb, c["dtb"]), writes=bufs(dt_))
        S.op("act", lambda e: e.activation(out=dt_[:], in_=dt_[:], func=AF.Exp), reads=bufs(dt_), writes=bufs(dt_))
        S.op("act", lambda e: e.activation(out=dt_[:], in_=dt_[:], func=AF.Ln, bias=1.0), reads=bufs(dt_), writes=bufs(dt_))
        S.op("dve", lambda e: e.tensor_tensor(out=da[:], in0=dt_[:], in1=c["abc"][:].unsqueeze(1).to_broadcast([128, ntile, 64]),
                                              op=ALU.mult), reads=bufs(dt_, c["abc"]), writes=bufs(da))
        for t in range(ntile):
            pc = banks.next()
            S.op("pe", lambda e, pc=pc, t=t: e.matmul(pc[:, 0:32], lhsT=c["tri"][0][:], rhs=da[:, t, 0:32], start=True, stop=True),
                 reads=bufs(c["tri"][0], da), writes=bufs(pc))
            S.op("pe", lambda e, pc=pc, t=t: e.matmul(pc[:, 32:64], lhsT=c["tri"][1][:], rhs=da[:, t, 32:64], start=True, stop=True),
                 reads=bufs(c["tri"][1], da), writes=bufs(pc))
            S.op("pe", lambda e, pc=pc, t=t: e.matmul(pc[:, 64:128], lhsT=onesf[:], rhs=da[:, t, :], start=True, stop=True),
                 reads=bufs(onesf, da), writes=bufs(pc))
            cumt = tmpr.next()
            S.op("act", lambda e, pc=pc, cumt=cumt: e.activation(out=cumt[:], in_=pc[:, 0:64], func=AF.Identity),
                 reads=bufs(pc), writes=bufs(cumt))
            S.op("act", lambda e, pc=pc, t=t: e.activation(out=ecum[:, t, :], in_=pc[:, 0:64], func=AF.Exp),
                 reads=bufs(pc), writes=bufs(ecum))
            S.op("act", lambda e, pc=pc, t=t: e.activation(out=etot[:, t, :], in_=pc[:, 64:128], func=AF.Exp),
                 reads=bufs(pc), writes=bufs(etot))
            S.op("dve", lambda e, pc=pc, cumt=cumt: e.tensor_tensor(out=cumt[:], in0=pc[:, 64:128], in1=cumt[:], op=ALU.subtract),
                 reads=bufs(pc, cumt), writes=bufs(cumt))
            S.op("act", lambda e, cumt=cumt: e.activation(out=cumt[:], in_=cumt[:], func=AF.Exp), reads=bufs(cumt), writes=bufs(cumt))
            S.op("dve", lambda e, cumt=cumt, t=t: e.tensor_tensor(out=dtd[:, t, :], in0=dt_[:, t, :], in1=cumt[:], op=ALU.mult),
                 reads=bufs(cumt, dt_), writes=bufs(dtd))
        raw = k.at([128, T_], F32)
        acc = k.at([128, T_], F32)
        fm = [k.at([128, T_], BF16) for _ in range(4)]
        XB = k.at([128, ntile, 384], BF16)
        SIN = k.at([128, ntile, 2, 256], BF16)
        stfs = [k.at([128, 256], F32), k.at([128, 256], F32)]
        vTg = k.at([128, 2, T_], BF16)
        h0r = k.aring(2, [128, 2, 128], F32)
        fir = k.aring(2, [128, 2, 128], F32)
        GTr = k.aring(2, [128, 128], F32)
        Dr = k.aring(2, [128, 4, 128], F32)
        Lr = k.aring(2, [128, 4, 128], F32)
        Mr = k.aring(4, [128, 4, 128], BF16)
        xdr = k.aring(5, [128, 4, 64], BF16)
        yr = k.aring(4, [128, 256], F32)
        szr = k.aring(2, [128, 256], F32)
        vbr = k.aring(2, [128, 256], BF16)
        tmp4 = k.aring(2, [128, 4, 64], F32)
        for g in range(cfg.get("ssd_g", 8)):
            wA = wring.next()
            wap = ssd_w_in.rearrange("(k p) n -> p k n", p=128)
            S.dma("pool", wA[:, :, 0:256], wap[:, :, g * 256:(g + 1) * 256], writes=bufs(wA))
            S.dma("pool", wA[:, :, 256:512], wap[:, :, E + g * 256:E + (g + 1) * 256], writes=bufs(wA))
            wB = wring.next()
            S.dma("pool", wB[:, :, 0:128], wap[:, :, 2 * E + g * 128:2 * E + (g + 1) * 128], writes=bufs(wB))
            S.dma("pool", wB[:, :, 128:256], wap[:, :, 2 * E + 1024 + g * 128:2 * E + 1024 + (g + 1) * 128], writes=bufs(wB))
            for bi in range(4):
                wt, co, cblk = ((wA, 256, 2 * g), (wA, 384, 2 * g + 1), (wB, 0, 16 + g), (wB, 128, 24 + g))[bi]
                for q in range(T_ // 512):
                    pb = banks.next()
                    for kk in range(8):
                        S.op("pe", lambda e, pb=pb, kk=kk, q=q, wt=wt, co=co: e.matmul(
                            pb[:], lhsT=wt[:, kk, co:co + 128], rhs=hT[:, kk, q * 512:(q + 1) * 512],
                            start=(kk == 0), stop=(kk == 7)), reads=bufs(wt, hT), writes=bufs(pb))
                    S.op("act", lambda e, pb=pb, q=q: e.activation(out=raw[:, q * 512:(q + 1) * 512], in_=pb[:], func=AF.Copy),
                         reads=bufs(pb), writes=bufs(raw))
                cw = c["cwT"]
                rv = raw[:].rearrange("p (s l) -> p s l", s=nseq)
                av = acc[:].rearrange("p (s l) -> p s l", s=nseq)
                S.op("dve", lambda e, cblk=cblk: e.tensor_scalar(out=acc[:], in0=raw[:], scalar1=cw[:, cblk, 2:3], scalar2=None,
                                                                 op0=ALU.mult), reads=bufs(raw, cw), writes=bufs(acc))
                taps = ((0, "dve", slice(2, Lq), slice(0, Lq - 2)), (1, "dve", slice(1, Lq), slice(0, Lq - 1)),
                        (3, "dve", slice(0, Lq - 1), slice(1, Lq)), (4, "dve", slice(0, Lq - 2), slice(2, Lq)))
                for j, eng, osl, isl in taps:
                    S.op(eng, lambda e, j=j, osl=osl, isl=isl, cblk=cblk, rv=rv, av=av: e.scalar_tensor_tensor(
                        out=av[:, :, osl], in0=rv[:, :, isl], scalar=cw[:, cblk, j:j + 1], in1=av[:, :, osl],
                        op0=ALU.mult, op1=ALU.add), reads=bufs(raw, acc, cw), writes=bufs(acc))
                S.op("act", lambda e, bi=bi, cblk=cblk: e.activation(out=fm[bi][:], in_=acc[:], func=AF.Silu,
                                                                      bias=c["cbT"][:, cblk:cblk + 1]),
                     reads=bufs(acc, c["cbT"]), writes=bufs(fm[bi]))
            for t in range(ntile):
                ptb = banks.next()
                ptv = ptb[:].bitcast(BF16)
                for bi in range(3):
                    S.op("pe", lambda e, ptv=ptv, bi=bi, t=t: e.transpose(
                        out=ptv[:, bi * 128:(bi + 1) * 128], in_=fm[bi][:, t * 128:(t + 1) * 128], identity=identb[:]),
                        reads=bufs(fm[bi], identb), writes=bufs(ptb))
                S.op("act", lambda e, ptv=ptv, t=t: e.activation(out=XB[:, t, :], in_=ptv[:, 0:384], func=AF.Copy),
                     reads=bufs(ptb), writes=bufs(XB))
            BT, CT = fm[2], fm[3]
            for sq in range(nseq):
                for d_ in range(2):
                    if is_lat:
                        h0 = h0r.next()
                        for half in range(2):
                            S.dma("sp", h0[:, half, :], state_ssd[d_, 4 * g + 2 * half:4 * g + 2 * half + 2].rearrange("h p n -> (h p) n"),
                                  writes=bufs(h0))
                        ph = banks.next()
                        for half in range(2):
                            S.op("pe", lambda e, ph=ph, h0=h0, half=half: e.transpose(
                                out=ph[:, half * 128:(half + 1) * 128], in_=h0[:, half, :], identity=identf[:]),
                                reads=bufs(h0, identf), writes=bufs(ph))
                        S.op("act", lambda e, ph=ph, d_=d_: e.activation(out=stfs[d_][:], in_=ph[:, 0:256], func=AF.Copy),
                             reads=bufs(ph), writes=bufs(stfs[d_]))
                    else:
                        S.op("pool", lambda e, d_=d_: e.memset(stfs[d_][:], 0.0), writes=bufs(stfs[d_]))
                for step in range(nch):
                    for d_ in range(2):
                        hs = slice(d_ * 32 + 4 * g, d_ * 32 + 4 * g + 4)
                        ci = step if d_ == 0 else nch - 1 - step
                        t = sq * nch + ci
                        st_ = stfs[d_]
                        S.op("act", lambda e, t=t, d_=d_, st_=st_: e.activation(out=SIN[:, t, d_, :], in_=st_[:], func=AF.Copy),
                             reads=bufs(st_), writes=bufs(SIN))
                        xdd = xdr.next()
                        S.op("pool", lambda e, xdd=xdd, t=t, hs=hs: e.tensor_tensor(
                            out=xdd[:], in0=XB[:, t, 0:256].rearrange("p (h d) -> p h d", h=4),
                            in1=dtd[:, t, hs].unsqueeze(2).to_broadcast([128, 4, 64]), op=ALU.mult),
                            reads=bufs(XB, dtd), writes=bufs(xdd))
                        psl = banks.next()
                        S.op("pe", lambda e, psl=psl, t=t, xdd=xdd: e.matmul(
                            psl[:, 0:256], lhsT=XB[:, t, 256:384], rhs=xdd[:].rearrange("p h d -> p (h d)"), start=True, stop=True),
                            reads=bufs(XB, xdd), writes=bufs(psl))
                        S.op("pool", lambda e, t=t, st_=st_, hs=hs: e.tensor_tensor(
                            out=st_[:].rearrange("p (h d) -> p h d", h=4), in0=st_[:].rearrange("p (h d) -> p h d", h=4),
                            in1=etot[:, t, hs].unsqueeze(2).to_broadcast([128, 4, 64]), op=ALU.mult),
                            reads=bufs(st_, etot), writes=bufs(st_))
                        S.op("dve", lambda e, psl=psl, st_=st_: e.tensor_tensor(out=st_[:], in0=psl[:, 0:256], in1=st_[:],
                                                                              op=ALU.add), reads=bufs(psl, st_), writes=bufs(st_))
                if not is_lat:
                    for d_ in range(2):
                        st_ = stfs[d_]
                        pf = banks.next()
                        for half in range(2):
                            S.op("pe", lambda e, pf=pf, st_=st_, half=half: e.transpose(
                                out=pf[:, half * 128:(half + 1) * 128], in_=st_[:, half * 128:(half + 1) * 128], identity=identf[:]),
                                reads=bufs(st_, identf), writes=bufs(pf))
                        fi = fir.next()
                        S.op("act", lambda e, pf=pf, fi=fi: e.activation(out=fi[:].rearrange("p a b -> p (a b)"), in_=pf[:, 0:256],
                                                                        func=AF.Copy), reads=bufs(pf), writes=bufs(fi))
                        for half in range(2):
                            S.dma("sp", new_ssd[sq, d_, 4 * g + 2 * half:4 * g + 2 * half + 2].rearrange("h p n -> (h p) n"),
                                  fi[:, half, :], reads=bufs(fi))
            def front(t):
                tsl = slice(t * 128, (t + 1) * 128)
                pgz = banks.next()
                S.op("pe", lambda e, pgz=pgz, tsl=tsl: e.matmul(pgz[:, 0:128], lhsT=BT[:, tsl], rhs=CT[:, tsl], start=True, stop=True),
                     reads=bufs(BT, CT), writes=bufs(pgz))
                for kk in range(8):
                    S.op("pe", lambda e, pgz=pgz, kk=kk, tsl=tsl, wA=wA: e.matmul(
                        pgz[:, 128:384], lhsT=hT[:, kk, tsl], rhs=wA[:, kk, 0:256], start=(kk == 0), stop=(kk == 7)),
                        reads=bufs(hT, wA), writes=bufs(pgz))
                GT = GTr.next()
                S.op("act", lambda e, pgz=pgz, GT=GT: e.activation(out=GT[:], in_=pgz[:, 0:128], func=AF.Copy),
                     reads=bufs(pgz), writes=bufs(GT))
                sz = szr.next()
                S.op("act", lambda e, pgz=pgz, sz=sz: e.activation(out=sz[:], in_=pgz[:, 128:384], func=AF.Tanh, scale=0.5),
                     reads=bufs(pgz), writes=bufs(sz))
                S.op("dve", lambda e, pgz=pgz, sz=sz: e.scalar_tensor_tensor(
                    out=sz[:], in0=sz[:], scalar=1.0, in1=pgz[:, 128:384], op0=ALU.add, op1=ALU.mult),
                    reads=bufs(pgz, sz), writes=bufs(sz))
                hss = [slice(d_ * 32 + 4 * g, d_ * 32 + 4 * g + 4) for d_ in range(2)]
                Dts = []
                for d_ in range(2):
                    Dt = Dr.next()
                    S.op("pool", lambda e, Dt=Dt, d_=d_, t=t, hs=hss[d_]: e.tensor_tensor(
                        out=Dt[:], in0=c["tri"][d_][:].unsqueeze(1).to_broadcast([128, 4, 128]),
                        in1=da[:, t, hs].unsqueeze(2).to_broadcast([128, 4, 128]), op=ALU.mult),
                        reads=bufs(c["tri"][d_], da), writes=bufs(Dt))
                    Dts.append(Dt)
                pzs = []
                for d_ in range(2):
                    pz_ = banks.next()
                    Dt = Dts[d_]
                    S.op("pe", lambda e, pz_=pz_, Dt=Dt: e.matmul(
                        pz_[:], lhsT=onesf[:], rhs=Dt[:].rearrange("p h t -> p (h t)"), start=True, stop=False),
                        reads=bufs(onesf, Dt), writes=bufs(pz_))
                    S.op("pe", lambda e, pz_=pz_, d_=d_, t=t, hs=hss[d_]: e.matmul(
                        pz_[:], lhsT=c["ntri"][d_][:], rhs=da[:, t, hs].unsqueeze(2).to_broadcast([128, 4, 128]), start=False, stop=False),
                        reads=bufs(c["ntri"][d_], da), writes=bufs(pz_))
                    S.op("pe", lambda e, pz_=pz_, d_=d_: e.matmul(
                        pz_[:], lhsT=identf[:], rhs=c["mneg"][d_][:].unsqueeze(1).to_broadcast([128, 4, 128]), start=False, stop=True),
                        reads=bufs(identf, c["mneg"][d_]), writes=bufs(pz_))
                    pzs.append(pz_)
                poo = banks.next()
                for d_ in range(2):
                    S.op("pe", lambda e, poo=poo, tsl=tsl, t=t, d_=d_: e.matmul(
                        poo[:, d_ * 256:(d_ + 1) * 256], lhsT=CT[:, tsl], rhs=SIN[:, t, d_, :], start=True, stop=True),
                        reads=bufs(CT, SIN), writes=bufs(poo))
                Lts = []
                for d_ in range(2):
                    Lt = Lr.next()
                    S.op("act", lambda e, pz_=pzs[d_], Lt=Lt: e.activation(out=Lt[:].rearrange("p h t -> p (h t)"), in_=pz_[:], func=AF.Exp),
                         reads=bufs(pzs[d_]), writes=bufs(Lt))
                    Lts.append(Lt)
                Mts, xds = [], []
                for d_ in range(2):
                    Mt = Mr.next()
                    S.op("dve", lambda e, Lt=Lts[d_], Mt=Mt, GT=GT: e.tensor_tensor(
                        out=Mt[:], in0=Lt[:], in1=GT[:].unsqueeze(1).to_broadcast([128, 4, 128]), op=ALU.mult),
                        reads=bufs(Lts[d_], GT), writes=bufs(Mt))
                    xd = xdr.next()
                    S.op("pool", lambda e, xd=xd, t=t, hs=hss[d_]: e.tensor_tensor(
                        out=xd[:], in0=XB[:, t, 0:256].rearrange("p (h d) -> p h d", h=4),
                        in1=dt_[:, t, hs].unsqueeze(2).to_broadcast([128, 4, 64]), op=ALU.mult),
                        reads=bufs(XB, dt_), writes=bufs(xd))
                    Mts.append(Mt)
                    xds.append(xd)
                return dict(t=t, tsl=tsl, sz=sz, Mts=Mts, xds=xds, poo=poo, hss=hss)

            def back(f):
                t, tsl, sz, poo, hss = f["t"], f["tsl"], f["sz"], f["poo"], f["hss"]
                py = banks.next()
                for d_ in range(2):
                    Mt, xd = f["Mts"][d_], f["xds"][d_]
                    for h in range(4):
                        S.op("pe", lambda e, py=py, Mt=Mt, xd=xd, h=h, d_=d_: e.matmul(
                            py[:, h * 64:(h + 1) * 64], lhsT=Mt[:, h, :], rhs=xd[:, h, :], start=(d_ == 0 and h == 0), stop=(d_ == 1 and h == 3)),
                            reads=bufs(Mt, xd), writes=bufs(py))
                y1 = yr.next()
                y2 = yr.next()
                for d_, yy in ((0, y1), (1, y2)):
                    S.op("dve", lambda e, poo=poo, hs=hss[d_], yy=yy, t=t, d_=d_: e.tensor_tensor(
                        out=yy[:].rearrange("p (h d) -> p h d", h=4), in0=poo[:, d_ * 256:(d_ + 1) * 256].rearrange("p (h d) -> p h d", h=4),
                        in1=ecum[:, t, hs].unsqueeze(2).to_broadcast([128, 4, 64]), op=ALU.mult),
                        reads=bufs(poo, ecum), writes=bufs(yy))
                S.op("pool", lambda e, y1=y1, y2=y2: e.tensor_tensor(out=y1[:], in0=y1[:], in1=y2[:], op=ALU.add),
                     reads=bufs(y1, y2), writes=bufs(y1))
                S.op("pool", lambda e, y2=y2, t=t, g=g: e.tensor_tensor(
                    out=y2[:].rearrange("p (h d) -> p h d", h=4), in0=XB[:, t, 0:256].rearrange("p (h d) -> p h d", h=4),
                    in1=c["dsk"][:, 4 * g:4 * g + 4].unsqueeze(2).to_broadcast([128, 4, 64]), op=ALU.mult),
                    reads=bufs(XB, c["dsk"]), writes=bufs(y2))
                S.op("pool", lambda e, y1=y1, y2=y2: e.tensor_tensor(out=y1[:], in0=y1[:], in1=y2[:], op=ALU.add),
                     reads=bufs(y1, y2), writes=bufs(y1))
                S.op("dve", lambda e, py=py, y1=y1: e.tensor_tensor(out=y1[:], in0=py[:, 0:256], in1=y1[:], op=ALU.add),
                     reads=bufs(py, y1), writes=bufs(y1))
                vb_ = vbr.next()
                S.op("dve", lambda e, vb_=vb_, y1=y1, sz=sz: e.scalar_tensor_tensor(
                    out=vb_[:], in0=y1[:], scalar=0.5, in1=sz[:], op0=ALU.mult, op1=ALU.mult),
                    reads=bufs(y1, sz), writes=bufs(vb_))
                S.op("act", lambda e, vb_=vb_, t=t, g=g: e.activation(out=junk[:, 0:256], in_=vb_[:], func=AF.Square,
                                                                     accum_out=ssq[:, t, g:g + 1]),
                     reads=bufs(vb_), writes=bufs(junk, ssq))
                ptb = banks.next()
                ptv = ptb[:].bitcast(BF16)
                for bb in range(2):
                    S.op("pe", lambda e, ptv=ptv, vb_=vb_, bb=bb: e.transpose(
                        out=ptv[:, bb * 128:(bb + 1) * 128], in_=vb_[:, bb * 128:(bb + 1) * 128], identity=identb[:]),
                        reads=bufs(vb_, identb), writes=bufs(ptb))
                for bb in range(2):
                    S.op("act", lambda e, ptv=ptv, bb=bb, tsl=tsl, g=g: e.activation(
                        out=vTg[:, bb, tsl], in_=ptv[:, bb * 128:(bb + 1) * 128], func=AF.Identity,
                        scale=c["ngT"][:, 2 * g + bb:2 * g + bb + 1]), reads=bufs(ptb, c["ngT"]), writes=bufs(vTg))

            fnext = front(0)
            for t in range(ntile):
                fcur = fnext
                if t + 1 < ntile:
                    fnext = front(t + 1)
                back(fcur)
            S.dma("sp", yscr[2 * g:2 * g + 2, :, tok0:tok0 + T_].rearrange("b p t -> p b t"), vTg[:], reads=bufs(vTg))
        S.op("dve", lambda e: e.tensor_reduce(out=rstd[:], in_=ssq[:], axis=AX.X, op=ALU.add), reads=bufs(ssq), writes=bufs(rstd))
        S.op("dve", lambda e: e.tensor_scalar(out=rstd[:], in0=rstd[:], scalar1=1.0 / E, scalar2=EPS, op0=ALU.mult, op1=ALU.add),
             reads=bufs(rstd), writes=bufs(rstd))
        S.op("act", lambda e: e.activation(out=rstd[:], in_=rstd[:], func=AF.Sqrt), reads=bufs(rstd), writes=bufs(rstd))
        S.op("dve", lambda e: e.reciprocal(out=rstd[:], in_=rstd[:]), reads=bufs(rstd), writes=bufs(rstd))
        return rstd

    TWO_PI = float(2 * np.pi)
    MAGIC = 12582912.0

    def s5_prep():
        c = {}
        c["AA"] = k.at([128, 64, 2, 2], F32)
        c["BB"] = k.at([128, 64, 2, 2], F32)
        c["Wsel"] = k.at([128, 8, 240], BF16)
        c["dT"] = k.at([128, 16], F32)
        c["bgT"] = k.at([128, 16], F32)
        c["h0"] = k.at([128, 2, 2, 64], F32)
        S.dma("sp", c["dT"][:], s5_d.rearrange("(b p) -> p b", p=128), writes=bufs(c["dT"]))
        c["dX"] = k.at([128, 128], F32)
        for t_ in range(8):
            S.dma("sp", c["dX"][16 * t_:16 * t_ + 16, :], s5_d.rearrange("(g j) -> j g", j=16), writes=bufs(c["dX"]))
        S.dma("sp", c["bgT"][:], s5_b_glu.rearrange("(b p) -> p b", p=128), writes=bufs(c["bgT"]))
        c["bgh"] = k.at([128, 16], F32)
        S.op("dve", lambda e: e.tensor_scalar(out=c["bgh"][:], in0=c["bgT"][:], scalar1=0.5, scalar2=None, op0=ALU.mult),
             reads=bufs(c["bgT"]), writes=bufs(c["bgh"]))
        for d_ in range(2):
            S.dma("sp", c["h0"][:, d_, :, :], s5_h0[d_].rearrange("r p g -> p r g"), writes=bufs(c["h0"]))
        mm = k.amark()
        wself = k.at([128, 8, 240], F32)
        S.op("pool", lambda e: e.memset(wself[:], 0.0), writes=bufs(wself))
        S.op("pool", lambda e: e.affine_select(out=wself[:, :, 112:128], in_=wself[:, :, 112:128], compare_op=ALU.not_equal,
                                               fill=1.0, base=0, pattern=[[-16, 8], [-1, 16]], channel_multiplier=1),
             reads=bufs(wself), writes=bufs(wself))
        S.op("pool", lambda e: e.tensor_copy(out=c["Wsel"][:], in_=wself[:]), reads=bufs(wself), writes=bufs(c["Wsel"]))
        maskT = [k.at([128, 8, 16], F32), k.at([128, 8, 16], F32)]
        for d_ in range(2):
            S.op("pool", lambda e, d_=d_: e.memset(maskT[d_][:], 1.0), writes=bufs(maskT[d_]))
        S.op("pool", lambda e: e.affine_select(out=maskT[0][:], in_=maskT[0][:], compare_op=ALU.is_ge, fill=0.0, base=15,
                                               pattern=[[16, 8], [0, 16]], channel_multiplier=-1),
             reads=bufs(maskT[0]), writes=bufs(maskT[0]))
        S.op("pool", lambda e: e.affine_select(out=maskT[1][:], in_=maskT[1][:], compare_op=ALU.is_ge, fill=0.0, base=0,
                                               pattern=[[-16, 8], [0, 16]], channel_multiplier=1),
             reads=bufs(maskT[1]), writes=bufs(maskT[1]))
        pw = [[k.at([128, 64, 16], F32), k.at([128, 64, 16], F32)] for _ in range(2)]
        coef = [[k.at([128, 64], F32), k.at([128, 64], F32)] for _ in range(2)]
        lr, li, ls = k.at([128, 64], F32), k.at([128, 64], F32), k.at([128, 64], F32)
        xx, ang = k.at([128, 64], F32), k.at([128, 64], F32)
        tr = k.aring(6, [128, 64], F32)
        for d_ in range(2):
            S.dma("sp", lr[:], s5_lam[0, d_], writes=bufs(lr))
            S.dma("sp", li[:], s5_lam[1, d_], writes=bufs(li))
            S.dma("sp", ls[:], s5_lstep[d_], writes=bufs(ls))
            S.op("act", lambda e: e.activation(out=ls[:], in_=ls[:], func=AF.Exp), reads=bufs(ls), writes=bufs(ls))
            S.op("dve", lambda e: e.tensor_tensor(out=xx[:], in0=lr[:], in1=ls[:], op=ALU.mult), reads=bufs(lr, ls), writes=bufs(xx))
            S.op("dve", lambda e: e.tensor_tensor(out=ang[:], in0=li[:], in1=ls[:], op=ALU.mult), reads=bufs(li, ls), writes=bufs(ang))
            pre, pim = pw[d_]
            for kq in range(1, 9):
                mp, mn, sn, cs, t1, t2 = [tr.next() for _ in range(6)]
                S.op("act", lambda e, mp=mp, kq=kq: e.activation(out=mp[:], in_=xx[:], func=AF.Exp, scale=float(kq)),
                     reads=bufs(xx), writes=bufs(mp))
                S.op("act", lambda e, mn=mn, kq=kq: e.activation(out=mn[:], in_=xx[:], func=AF.Exp, scale=float(-kq)),
                     reads=bufs(xx), writes=bufs(mn))
                for dst, shift in ((sn, 0.0), (cs, 0.25)):
                    if shift:
                        S.op("dve", lambda e, t1=t1, kq=kq, shift=shift: e.tensor_scalar(
                            out=t1[:], in0=ang[:], scalar1=float(kq / TWO_PI), scalar2=shift, op0=ALU.mult, op1=ALU.add),
                            reads=bufs(ang), writes=bufs(t1))
                        S.op("dve", lambda e, t1=t1: e.tensor_scalar(out=t1[:], in0=t1[:], scalar1=MAGIC, scalar2=None, op0=ALU.add),
                             reads=bufs(t1), writes=bufs(t1))
                    else:
                        S.op("dve", lambda e, t1=t1, kq=kq: e.tensor_scalar(
                            out=t1[:], in0=ang[:], scalar1=float(kq / TWO_PI), scalar2=MAGIC, op0=ALU.mult, op1=ALU.add),
                            reads=bufs(ang), writes=bufs(t1))
                    S.op("dve", lambda e, t1=t1: e.tensor_scalar(out=t1[:], in0=t1[:], scalar1=-MAGIC, scalar2=-TWO_PI,
                                                                 op0=ALU.add, op1=ALU.mult), reads=bufs(t1), writes=bufs(t1))
                    S.op("dve", lambda e, t1=t1, kq=kq: e.scalar_tensor_tensor(
                        out=t1[:], in0=ang[:], scalar=float(kq), in1=t1[:], op0=ALU.mult, op1=ALU.add),
                        reads=bufs(ang, t1), writes=bufs(t1))
                    if shift:
                        S.op("dve", lambda e, t1=t1: e.tensor_scalar(out=t1[:], in0=t1[:], scalar1=float(np.pi / 2), scalar2=None,
                                                                     op0=ALU.add), reads=bufs(t1), writes=bufs(t1))
                    S.op("act", lambda e, t1=t1, dst=dst: e.activation(out=dst[:], in_=t1[:], func=AF.Sin),
                         reads=bufs(t1), writes=bufs(dst))
                S.op("dve", lambda e, kq=kq, mp=mp, cs=cs, pre=pre: e.tensor_tensor(out=pre[:, :, kq - 1], in0=mp[:], in1=cs[:], op=ALU.mult),
                     reads=bufs(mp, cs), writes=bufs(pre))
                S.op("dve", lambda e, kq=kq, mp=mp, sn=sn, pim=pim: e.tensor_tensor(out=pim[:, :, kq - 1], in0=mp[:], in1=sn[:], op=ALU.mult),
                     reads=bufs(mp, sn), writes=bufs(pim))
                S.op("dve", lambda e, kq=kq, mn=mn, cs=cs, pre=pre: e.tensor_tensor(out=pre[:, :, 7 + kq], in0=mn[:], in1=cs[:], op=ALU.mult),
                     reads=bufs(mn, cs), writes=bufs(pre))
                S.op("dve", lambda e, kq=kq, mn=mn, sn=sn, pim=pim: e.scalar_tensor_tensor(
                    out=pim[:, :, 7 + kq], in0=mn[:], scalar=-1.0, in1=sn[:], op0=ALU.mult, op1=ALU.mult),
                    reads=bufs(mn, sn), writes=bufs(pim))
            for r_ in range(2):
                S.op("act", lambda e, d_=d_, r_=r_, pre=pre: e.activation(out=c["AA"][:, :, d_, r_], in_=pre[:, :, 7], func=AF.Copy),
                     reads=bufs(pre), writes=bufs(c["AA"]))
            S.op("dve", lambda e, d_=d_, pim=pim: e.tensor_scalar(out=c["BB"][:, :, d_, 0], in0=pim[:, :, 7], scalar1=-1.0, scalar2=None,
                                                         op0=ALU.mult), reads=bufs(pim), writes=bufs(c["BB"]))
            S.op("act", lambda e, d_=d_, pim=pim: e.activation(out=c["BB"][:, :, d_, 1], in_=pim[:, :, 7], func=AF.Copy),
                 reads=bufs(pim), writes=bufs(c["BB"]))
            den, nr, t1, t2 = [tr.next() for _ in range(4)]
            S.op("dve", lambda e, den=den: e.tensor_tensor(out=den[:], in0=lr[:], in1=lr[:], op=ALU.mult), reads=bufs(lr), writes=bufs(den))
            S.op("dve", lambda e, t1=t1: e.tensor_tensor(out=t1[:], in0=li[:], in1=li[:], op=ALU.mult), reads=bufs(li), writes=bufs(t1))
            S.op("dve", lambda e, den=den, t1=t1: e.tensor_tensor(out=den[:], in0=den[:], in1=t1[:], op=ALU.add),
                 reads=bufs(den, t1), writes=bufs(den))
            S.op("dve", lambda e, den=den: e.reciprocal(out=den[:], in_=den[:]), reads=bufs(den), writes=bufs(den))
            S.op("dve", lambda e, nr=nr, pre=pre: e.tensor_scalar(out=nr[:], in0=pre[:, :, 0], scalar1=-1.0, scalar2=None, op0=ALU.add),
                 reads=bufs(pre), writes=bufs(nr))
            cr_, ci_ = coef[d_]
            S.op("dve", lambda e, nr=nr, t1=t1: e.tensor_tensor(out=t1[:], in0=nr[:], in1=lr[:], op=ALU.mult), reads=bufs(nr, lr), writes=bufs(t1))
            S.op("dve", lambda e, t2=t2, pim=pim: e.tensor_tensor(out=t2[:], in0=pim[:, :, 0], in1=li[:], op=ALU.mult), reads=bufs(pim, li), writes=bufs(t2))
            S.op("dve", lambda e, t1=t1, t2=t2: e.tensor_tensor(out=t1[:], in0=t1[:], in1=t2[:], op=ALU.add), reads=bufs(t1, t2), writes=bufs(t1))
            S.op("dve", lambda e, t1=t1, den=den, cr_=cr_: e.tensor_tensor(out=cr_[:], in0=t1[:], in1=den[:], op=ALU.mult),
                 reads=bufs(t1, den), writes=bufs(cr_))
            S.op("dve", lambda e, t1=t1, pim=pim: e.tensor_tensor(out=t1[:], in0=pim[:, :, 0], in1=lr[:], op=ALU.mult), reads=bufs(pim, lr), writes=bufs(t1))
            S.op("dve", lambda e, nr=nr, t2=t2: e.tensor_tensor(out=t2[:], in0=nr[:], in1=li[:], op=ALU.mult), reads=bufs(nr, li), writes=bufs(t2))
            S.op("dve", lambda e, t1=t1, t2=t2: e.tensor_tensor(out=t1[:], in0=t1[:], in1=t2[:], op=ALU.subtract), reads=bufs(t1, t2), writes=bufs(t1))
            S.op("dve", lambda e, t1=t1, den=den, ci_=ci_: e.tensor_tensor(out=ci_[:], in0=t1[:], in1=den[:], op=ALU.mult),
                 reads=bufs(t1, den), writes=bufs(ci_))
        Braw = [k.at([128, 8, 16], F32), k.at([128, 8, 16], F32)]
        Craw = [k.at([128, 8, 16], F32), k.at([128, 8, 16], F32)]
        Bb = [k.at([128, 8, 16], F32), k.at([128, 8, 16], F32)]
        V = [[k.at([128, 8, 8, 16], F32), k.at([128, 8, 8, 16], F32)] for _ in range(2)]
        W2 = [[k.at([128, 8, 8, 16], F32), k.at([128, 8, 8, 16], F32)] for _ in range(2)]
        t8 = k.aring(4, [128, 8, 16], F32)
        t8e = {"pool": k.aring(4, [128, 8, 16], F32), "dve": k.aring(4, [128, 8, 16], F32)}
        T16 = k.aring(2, [128, 16, 128], BF16)
        VT16 = k.aring(2, [128, 8, 2, 2, 128], BF16)
        W216 = k.aring(2, [128, 8, 2, 2, 128], BF16)
        Ttmp = k.aring(2, [128, 128], F32)
        Ttmp2 = k.aring(2, [128, 128], F32)

        def bc_j(ap2):
            return ap2.unsqueeze(2).to_broadcast([128, 8, 16])

        for b in range(8):
            gs = slice(8 * b, 8 * b + 8)
            t16, vt16, w216 = T16.next(), VT16.next(), W216.next()
            for d_ in range(2):
                pre, pim = pw[d_]
                cr_, ci_ = coef[d_]
                for r_ in range(2):
                    S.dma("sp", Braw[r_][:], s5_B[r_, d_, :, gs, :], writes=bufs(Braw[r_]))
                    S.dma("sp", Craw[r_][:], s5_C[r_, d_, :, gs, :], writes=bufs(Craw[r_]))
                ta, tb = t8.next(), t8.next()
                S.op("dve", lambda e, ta=ta, cr_=cr_, gs=gs: e.tensor_tensor(out=ta[:], in0=Braw[0][:], in1=bc_j(cr_[:, gs]), op=ALU.mult),
                     reads=bufs(Braw[0], cr_), writes=bufs(ta))
                S.op("dve", lambda e, tb=tb, ci_=ci_, gs=gs: e.tensor_tensor(out=tb[:], in0=Braw[1][:], in1=bc_j(ci_[:, gs]), op=ALU.mult),
                     reads=bufs(Braw[1], ci_), writes=bufs(tb))
                S.op("dve", lambda e, ta=ta, tb=tb: e.tensor_tensor(out=Bb[0][:], in0=ta[:], in1=tb[:], op=ALU.subtract),
                     reads=bufs(ta, tb), writes=bufs(Bb[0]))
                ta, tb = t8.next(), t8.next()
                S.op("dve", lambda e, ta=ta, cr_=cr_, gs=gs: e.tensor_tensor(out=ta[:], in0=Braw[1][:], in1=bc_j(cr_[:, gs]), op=ALU.mult),
                     reads=bufs(Braw[1], cr_), writes=bufs(ta))
                S.op("dve", lambda e, tb=tb, ci_=ci_, gs=gs: e.tensor_tensor(out=tb[:], in0=Braw[0][:], in1=bc_j(ci_[:, gs]), op=ALU.mult),
                     reads=bufs(Braw[0], ci_), writes=bufs(tb))
                S.op("dve", lambda e, ta=ta, tb=tb: e.tensor_tensor(out=Bb[1][:], in0=ta[:], in1=tb[:], op=ALU.add),
                     reads=bufs(ta, tb), writes=bufs(Bb[1]))
                for s_ in range(8):
                    kv = 8 + (s_ if d_ == 0 else 7 - s_)
                    kw = s_ if d_ == 0 else 7 - s_
                    for (eng, P_idx, X_, out_, neg_im) in (("pool", kv, Bb, V[d_], False), ("dve", kw, Craw, W2[d_], True)):
                        Pr = bc_j(pre[:, gs, P_idx])
                        Pi = bc_j(pim[:, gs, P_idx])
                        ta, tb = t8e[eng].next(), t8e[eng].next()
                        S.op(eng, lambda e, ta=ta, X_=X_, Pr=Pr: e.tensor_tensor(out=ta[:], in0=X_[0][:], in1=Pr, op=ALU.mult),
                             reads=bufs(X_[0], pre), writes=bufs(ta))
                        S.op(eng, lambda e, tb=tb, X_=X_, Pi=Pi: e.tensor_tensor(out=tb[:], in0=X_[1][:], in1=Pi, op=ALU.mult),
                             reads=bufs(X_[1], pim), writes=bufs(tb))
                        S.op(eng, lambda e, ta=ta, tb=tb, out_=out_, s_=s_: e.tensor_tensor(
                            out=out_[0][:, :, s_, :], in0=ta[:], in1=tb[:], op=ALU.subtract), reads=bufs(ta, tb), writes=bufs(out_[0]))
                        ta, tb = t8e[eng].next(), t8e[eng].next()
                        S.op(eng, lambda e, ta=ta, X_=X_, Pi=Pi: e.tensor_tensor(out=ta[:], in0=X_[0][:], in1=Pi, op=ALU.mult),
                             reads=bufs(X_[0], pim), writes=bufs(ta))
                        S.op(eng, lambda e, tb=tb, X_=X_, Pr=Pr: e.tensor_tensor(out=tb[:], in0=X_[1][:], in1=Pr, op=ALU.mult),
                             reads=bufs(X_[1], pre), writes=bufs(tb))
                        if not neg_im:
                            S.op(eng, lambda e, ta=ta, tb=tb, out_=out_, s_=s_: e.tensor_tensor(
                                out=out_[1][:, :, s_, :], in0=ta[:], in1=tb[:], op=ALU.add), reads=bufs(ta, tb), writes=bufs(out_[1]))
                        else:
                            S.op(eng, lambda e, ta=ta, tb=tb: e.tensor_tensor(out=ta[:], in0=ta[:], in1=tb[:], op=ALU.add),
                                 reads=bufs(ta, tb), writes=bufs(ta))
                            S.op(eng, lambda e, ta=ta, out_=out_, s_=s_: e.tensor_scalar(
                                out=out_[1][:, :, s_, :], in0=ta[:], scalar1=-1.0, scalar2=None, op0=ALU.mult),
                                reads=bufs(ta), writes=bufs(out_[1]))
                for r_ in range(2):
                    S.op("act", lambda e, d_=d_, r_=r_, w216=w216: e.activation(
                        out=w216[:, :, d_, r_, :], in_=W2[d_][r_][:].rearrange("p g s j -> p g (s j)"), func=AF.Copy),
                        reads=bufs(W2[d_][r_]), writes=bufs(w216))
                for gl in range(8):
                    pv_ = banks.next()
                    for r_ in range(2):
                        S.op("pe", lambda e, pv_=pv_, d_=d_, r_=r_, gl=gl: e.transpose(
                            out=pv_[:, r_ * 128:(r_ + 1) * 128], in_=V[d_][r_][:, gl, :, :].rearrange("p s j -> p (s j)"),
                            identity=identf[:]), reads=bufs(V[d_][r_], identf), writes=bufs(pv_))
                    S.op("act", lambda e, pv_=pv_, d_=d_, gl=gl, vt16=vt16: e.activation(
                        out=vt16[:, gl, d_, :, :].rearrange("p r m -> p (r m)"), in_=pv_[:, 0:256], func=AF.Copy),
                        reads=bufs(pv_), writes=bufs(vt16))
            for gl in range(8):
                for par in range(2):
                    rows = slice(par * 64, (par + 1) * 64)
                    gi = gl * 2 + par
                    pT = banks.next()
                    for d_ in range(2):
                        for r_ in range(2):
                            S.op("pe", lambda e, pT=pT, d_=d_, r_=r_, gl=gl, rows=rows: e.matmul(
                                pT[:, d_ * 128:(d_ + 1) * 128], lhsT=V[d_][r_][rows, gl, :, :].rearrange("p s j -> p (s j)"),
                                rhs=W2[d_][r_][rows, gl, :, :].rearrange("p s j -> p (s j)"), start=(r_ == 0), stop=(r_ == 1)),
                                reads=bufs(V[d_][r_], W2[d_][r_]), writes=bufs(pT))
                    ta, tb = Ttmp.next(), Ttmp2.next()
                    S.op("dve", lambda e, pT=pT, ta=ta: e.tensor_tensor(
                        out=ta[:], in0=pT[:, 0:128], in1=maskT[0][:].rearrange("p t j -> p (t j)"), op=ALU.mult),
                        reads=bufs(pT, maskT[0]), writes=bufs(ta))
                    S.op("dve", lambda e, pT=pT, tb=tb: e.tensor_tensor(
                        out=tb[:], in0=pT[:, 128:256], in1=maskT[1][:].rearrange("p t j -> p (t j)"), op=ALU.mult),
                        reads=bufs(pT, maskT[1]), writes=bufs(tb))
                    S.op("pool", lambda e, ta=ta, tb=tb, t16=t16, gi=gi: e.tensor_tensor(out=t16[:, gi, :], in0=ta[:], in1=tb[:], op=ALU.add),
                         reads=bufs(ta, tb), writes=bufs(t16))
            S.dma("sp", Tscr[b], t16[:].rearrange("p g m -> p (g m)"), reads=bufs(t16))
            S.dma("sp", VTscr[b], vt16[:].rearrange("p g d r m -> p (g d r m)"), reads=bufs(vt16))
            S.dma("sp", W2scr[b], w216[:].rearrange("p g d r m -> p (g d r m)"), reads=bufs(w216))
        k.arestore(mm)
        return c

    def s5_tiles(tok0, sub, is_lat):
        tiles = []
        for i in range(8):
            I_ = sub * 8 + i
            if not is_lat:
                s_, c0 = I_, 0
            else:
                s_, c0 = I_ // 2, (I_ % 2) * 128
            base = tok0 + 8 * c0 + s_
            tiles.append(((lambda src_, base=base: src_[base:base + 8 * 127 + 1:8, :]), I_ * 128))
        return tiles

    def s5_unit(c, tok0, is_lat):
        C_ = 256 if is_lat else 128
        nseq = 1 if is_lat else 4
        nch = C_ // nseq
        nct = C_ // 128
        Tn = 8 * C_
        U = k.at([128, nct, 16, 8, 16], BF16)
        X = k.at([128, 16, C_], BF16)
        arr = k.at([128, 16, 2, nseq, nch + 1], F32)
        Hb = k.at([128, 16, 2, nseq, nch + 1], BF16)
        Ysb = T(U.t[:].rearrange("p a g s j -> p (a g s j)").rearrange("p (g c) -> p g c", g=16))
        Ysb.b = U.b
        Tw = k.at([128, 16, 128], BF16)
        VTw = k.at([128, 8, 2, 2, 128], BF16)
        W2w = k.at([128, 8, 2, 2, 128], BF16)
        ygst = k.at([128, 2, Tn], BF16)
        uur = k.aring(2, [128, 512], F32)
        ysr = k.aring(2, [128, 512], F32)
        tmps = {eng: [k.at([128, 8, 2, nseq], F32) for _ in range(3)] for eng in ("dve", "pool")}
        GPB = 512 // C_
        if not is_lat:
            fin = k.at([128, 4, 2, 2, 64], F32)
            fst = k.aring(2, [64, 4, 128], F32)
        bl = {}
        if is_lat:
            for eng in ("dve", "pool"):
                bl[eng] = {"PR": k.at([128, 8, 16], F32), "PI": k.at([128, 8, 16], F32),
                           "AAp": k.at([128, 8, 2, 16], F32), "BBp": k.at([128, 8, 2, 16], F32),
                           "cc": k.at([128, 8, 2, 17], F32),
                           "t": [k.at([128, 8, 8], F32) for _ in range(4)],
                           "l": [k.at([128, 8, 2, 16], F32) for _ in range(3)],
                           "c": [k.at([128, 8, 2], F32) for _ in range(2)],
                           "f": [[k.at([128, 8, 2, 16], F32) for _ in range(2)] for _ in range(2)]}
        for b in range(cfg.get("s5_nb", 8)):
            gs = slice(8 * b, 8 * b + 8)
            S.dma("sp", Tw[:].rearrange("p g m -> p (g m)"), Tscr[b], writes=bufs(Tw))
            S.dma("sp", VTw[:].rearrange("p g d r m -> p (g d r m)"), VTscr[b], writes=bufs(VTw))
            S.dma("sp", W2w[:].rearrange("p g d r m -> p (g d r m)"), W2scr[b], writes=bufs(W2w))
            wu = wring.next()
            S.dma("pool", wu[:, :, 0:256], s5_w_in.rearrange("(k p) n -> p k n", p=128)[:, :, 256 * b:256 * (b + 1)], writes=bufs(wu))
            for ct in range(nct):
                for s2 in range(4):
                    pb = banks.next()
                    for si in range(2):
                        s_ = s2 * 2 + si
                        p0 = s_ * C_ + ct * 128
                        for kk in range(8):
                            S.op("pe", lambda e, pb=pb, kk=kk, si=si, p0=p0, wu=wu: e.matmul(
                                pb[:, si * 256:(si + 1) * 256], lhsT=hT[:, kk, p0:p0 + 128], rhs=wu[:, kk, 0:256],
                                start=(kk == 0), stop=(kk == 7)), reads=bufs(hT, wu), writes=bufs(pb))
                    S.op("act", lambda e, pb=pb, ct=ct, s2=s2: e.activation(
                        out=U[:, ct, :, s2 * 2:s2 * 2 + 2, :], in_=pb[:].rearrange("p (s g j) -> p g s j", s=2, g=16), func=AF.Copy),
                        reads=bufs(pb), writes=bufs(U))
            for ct in range(nct):
                for g4 in range(4):
                    pb = banks.next()
                    for gg in range(4):
                        gi = g4 * 4 + gg
                        S.op("pe", lambda e, pb=pb, gg=gg, gi=gi, ct=ct: e.matmul(
                            pb[:, gg * 128:(gg + 1) * 128], lhsT=U[:, ct, gi, :, :].rearrange("p s j -> p (s j)"), rhs=identb[:], start=True, stop=True),
                            reads=bufs(U, identb), writes=bufs(pb))
                    S.op("act", lambda e, pb=pb, g4=g4, ct=ct: e.activation(
                        out=X[:, g4 * 4:g4 * 4 + 4, ct * 128:(ct + 1) * 128], in_=pb[:].rearrange("p (g c) -> p g c", g=4), func=AF.Copy),
                        reads=bufs(pb), writes=bufs(X))
            if is_lat:
                S.op("act", lambda e, gs=gs: e.activation(
                    out=arr[:, :, :, 0, 0].rearrange("p (g d) r -> p g d r", d=2),
                    in_=c["h0"][:, :, :, gs].rearrange("p d r g -> p g d r"), func=AF.Copy), reads=bufs(c["h0"]), writes=bufs(arr))
            else:
                S.op("pool", lambda e: e.memset(arr[:, :, :, :, 0:1], 0.0), writes=bufs(arr))
            for gl in range(8):
                pGs = [banks.next() for _ in range(nct)]
                for par in range(2):
                    gi = 2 * gl + par
                    rows = slice(par * 64, (par + 1) * 64)
                    for d_ in range(2):
                        if d_ == 0:
                            rhs = X[:, gi, :]
                        else:
                            rhs = X[:, gi, :].rearrange("p (s c) -> p s c", s=nseq)[:, :, ::-1]
                        for r_ in range(2):
                            if is_lat:
                                outp = pGs[d_][rows, r_ * 256:(r_ + 1) * 256]
                                pgb = pGs[d_]
                            else:
                                outp = pGs[0][rows, (d_ * 2 + r_) * 128:(d_ * 2 + r_ + 1) * 128]
                                pgb = pGs[0]
                            S.op("pe", lambda e, outp=outp, gl=gl, d_=d_, r_=r_, par=par, rhs=rhs: e.matmul(
                                outp, lhsT=VTw[:, gl, d_, r_, par * 64:(par + 1) * 64], rhs=rhs, start=True, stop=True),
                                reads=bufs(VTw, X), writes=bufs(pgb))
                for d_ in range(2):
                    if is_lat:
                        src_ = pGs[d_][:].rearrange("p (r s c) -> p r s c", r=2, s=1)
                        pgb = pGs[d_]
                    else:
                        src_ = pGs[0][:, d_ * 256:(d_ + 1) * 256].rearrange("p (r s c) -> p r s c", r=2, s=nseq)
                        pgb = pGs[0]
                    S.op("act", lambda e, src_=src_, gl=gl, d_=d_: e.activation(
                        out=arr[:, gl * 2 + d_, :, :, 1:nch + 1], in_=src_, func=AF.Copy), reads=bufs(pgb), writes=bufs(arr))
            AAb = c["AA"][:, gs, :, :].rearrange("p g d r -> p (g d) r")
            BBb = c["BB"][:, gs, :, :].rearrange("p g d r -> p (g d) r")
            if not is_lat:
                for kq in range(nch):
                    for eng, qs in (("dve", slice(0, 8)), ("pool", slice(8, 16))):
                        tt, p1, p2 = tmps[eng]
                        S.op(eng, lambda e, tt=tt, qs=qs, kq=kq: e.tensor_tensor(
                            out=tt[:], in0=arr[:, qs, :, :, kq], in1=arr[:, qs, :, :, kq + 1], op=ALU.add),
                            reads=bufs(arr), writes=bufs(tt))
                        S.op(eng, lambda e, tt=tt, p1=p1, qs=qs, AAb=AAb: e.tensor_tensor(
                            out=p1[:], in0=tt[:], in1=AAb[:, qs, :].unsqueeze(3).to_broadcast([128, 8, 2, nseq]), op=ALU.mult),
                            reads=bufs(tt, c["AA"]), writes=bufs(p1))
                        S.op(eng, lambda e, tt=tt, p2=p2, qs=qs, BBb=BBb: e.tensor_tensor(
                            out=p2[:], in0=tt[:, :, ::-1, :], in1=BBb[:, qs, :].unsqueeze(3).to_broadcast([128, 8, 2, nseq]), op=ALU.mult),
                            reads=bufs(tt, c["BB"]), writes=bufs(p2))
                        S.op(eng, lambda e, p1=p1, p2=p2, qs=qs, kq=kq: e.tensor_tensor(
                            out=arr[:, qs, :, :, kq + 1], in0=p1[:], in1=p2[:], op=ALU.add), reads=bufs(p1, p2), writes=bufs(arr))
                S.op("act", lambda e: e.activation(out=Hb[:].rearrange("p q r s c -> p (q r s c)"),
                                                   in_=arr[:].rearrange("p q r s c -> p (q r s c)"), func=AF.Copy),
                     reads=bufs(arr), writes=bufs(Hb))
            else:
                NB_, BL_ = 16, 16
                for eng, qs in (("dve", slice(0, 8)), ("pool", slice(8, 16))):
                    B_ = bl[eng]
                    PR, PI, AAp, BBp, cc = B_["PR"], B_["PI"], B_["AAp"], B_["BBp"], B_["cc"]
                    tA, tB, tC, tD = B_["t"]
                    AAh = AAb[:, qs, :]
                    BBh = BBb[:, qs, :]
                    S.op(eng, lambda e, PR=PR, AAh=AAh: e.tensor_copy(out=PR[:, :, 0], in_=AAh[:, :, 0]), reads=bufs(c["AA"]), writes=bufs(PR))
                    S.op(eng, lambda e, PI=PI, BBh=BBh: e.tensor_copy(out=PI[:, :, 0], in_=BBh[:, :, 1]), reads=bufs(c["BB"]), writes=bufs(PI))
                    m_ = 1
                    while m_ < 16:
                        ar = PR[:, :, m_ - 1:m_].to_broadcast([128, 8, m_])
                        ai = PI[:, :, m_ - 1:m_].to_broadcast([128, 8, m_])
                        src_r, src_i = PR[:, :, 0:m_], PI[:, :, 0:m_]
                        dst_r, dst_i = PR[:, :, m_:2 * m_], PI[:, :, m_:2 * m_]
                        ta, tb = tA[:, :, 0:m_], tB[:, :, 0:m_]
                        tc_, td = tC[:, :, 0:m_], tD[:, :, 0:m_]
                        S.op(eng, lambda e, ta=ta, src_r=src_r, ar=ar: e.tensor_tensor(out=ta, in0=src_r, in1=ar, op=ALU.mult), reads=bufs(PR), writes=bufs(tA))
                        S.op(eng, lambda e, tb=tb, src_i=src_i, ai=ai: e.tensor_tensor(out=tb, in0=src_i, in1=ai, op=ALU.mult), reads=bufs(PI), writes=bufs(tB))
                        S.op(eng, lambda e, tc_=tc_, src_r=src_r, ai=ai: e.tensor_tensor(out=tc_, in0=src_r, in1=ai, op=ALU.mult), reads=bufs(PR, PI), writes=bufs(tC))
                        S.op(eng, lambda e, td=td, src_i=src_i, ar=ar: e.tensor_tensor(out=td, in0=src_i, in1=ar, op=ALU.mult), reads=bufs(PR, PI), writes=bufs(tD))
                        S.op(eng, lambda e, dst_r=dst_r, ta=ta, tb=tb: e.tensor_tensor(out=dst_r, in0=ta, in1=tb, op=ALU.subtract), reads=bufs(tA, tB), writes=bufs(PR))
                        S.op(eng, lambda e, dst_i=dst_i, tc_=tc_, td=td: e.tensor_tensor(out=dst_i, in0=tc_, in1=td, op=ALU.add), reads=bufs(tC, tD), writes=bufs(PI))
                        m_ *= 2
                    for r_ in range(2):
                        S.op(eng, lambda e, AAp=AAp, PR=PR, r_=r_: e.tensor_copy(out=AAp[:, :, r_, :], in_=PR[:]), reads=bufs(PR), writes=bufs(AAp))
                    S.op(eng, lambda e, BBp=BBp, PI=PI: e.tensor_scalar(out=BBp[:, :, 0, :], in0=PI[:], scalar1=-1.0, scalar2=None, op0=ALU.mult),
                         reads=bufs(PI), writes=bufs(BBp))
                    S.op(eng, lambda e, BBp=BBp, PI=PI: e.tensor_copy(out=BBp[:, :, 1, :], in_=PI[:]), reads=bufs(PI), writes=bufs(BBp))
                for eng, qs in (("dve", slice(0, 8)), ("pool", slice(8, 16))):
                    B_ = bl[eng]
                    AAp, BBp, cc = B_["AAp"], B_["BBp"], B_["cc"]
                    t3, p13, p23 = B_["l"]
                    AAh = AAb[:, qs, :].unsqueeze(3).to_broadcast([128, 8, 2, NB_])
                    BBh = BBb[:, qs, :].unsqueeze(3).to_broadcast([128, 8, 2, NB_])
                    xv = arr[:, qs, :, 0, 1:257].rearrange("p q r (b i) -> p q r b i", i=BL_)
                    for i_ in range(BL_):
                        if i_ == 0:
                            src_t = xv[:, :, :, :, 0]
                        else:
                            S.op(eng, lambda e, t3=t3, xv=xv, i_=i_: e.tensor_tensor(
                                out=t3[:], in0=xv[:, :, :, :, i_ - 1], in1=xv[:, :, :, :, i_], op=ALU.add), reads=bufs(arr), writes=bufs(t3))
                            src_t = t3[:]
                        rd = bufs(arr) if i_ == 0 else bufs(t3)
                        src_sw = src_t[:, :, ::-1, :]
                        S.op(eng, lambda e, p13=p13, src_t=src_t, AAh=AAh: e.tensor_tensor(out=p13[:], in0=src_t, in1=AAh, op=ALU.mult),
                             reads=rd + bufs(c["AA"]), writes=bufs(p13))
                        S.op(eng, lambda e, p23=p23, src_sw=src_sw, BBh=BBh: e.tensor_tensor(out=p23[:], in0=src_sw, in1=BBh, op=ALU.mult),
                             reads=rd + bufs(c["BB"]), writes=bufs(p23))
                        S.op(eng, lambda e, p13=p13, p23=p23, xv=xv, i_=i_: e.tensor_tensor(
                            out=xv[:, :, :, :, i_], in0=p13[:], in1=p23[:], op=ALU.add), reads=bufs(p13, p23), writes=bufs(arr))
                for eng, qs in (("dve", slice(0, 8)), ("pool", slice(8, 16))):
                    B_ = bl[eng]
                    AAp, BBp, cc = B_["AAp"], B_["BBp"], B_["cc"]
                    c1, c2 = B_["c"]
                    xv = arr[:, qs, :, 0, 1:257].rearrange("p q r (b i) -> p q r b i", i=BL_)
                    S.op(eng, lambda e, cc=cc, qs=qs: e.tensor_copy(out=cc[:, :, :, 0], in_=arr[:, qs, :, 0, 0]), reads=bufs(arr), writes=bufs(cc))
                    for Bk in range(NB_):
                        S.op(eng, lambda e, c1=c1, cc=cc, AAp=AAp, Bk=Bk: e.tensor_tensor(
                            out=c1[:], in0=cc[:, :, :, Bk], in1=AAp[:, :, :, 15], op=ALU.mult), reads=bufs(cc, AAp), writes=bufs(c1))
                        S.op(eng, lambda e, c2=c2, cc=cc, BBp=BBp, Bk=Bk: e.tensor_tensor(
                            out=c2[:], in0=cc[:, :, ::-1, Bk], in1=BBp[:, :, :, 15], op=ALU.mult), reads=bufs(cc, BBp), writes=bufs(c2))
                        S.op(eng, lambda e, c1=c1, c2=c2: e.tensor_tensor(out=c1[:], in0=c1[:], in1=c2[:], op=ALU.add),
                             reads=bufs(c1, c2), writes=bufs(c1))
                        S.op(eng, lambda e, c1=c1, cc=cc, xv=xv, Bk=Bk: e.tensor_tensor(
                            out=cc[:, :, :, Bk + 1], in0=c1[:], in1=xv[:, :, :, Bk, 15], op=ALU.add), reads=bufs(c1, arr), writes=bufs(cc))
                for eng, qs in (("dve", slice(0, 8)), ("pool", slice(8, 16))):
                    B_ = bl[eng]
                    AAp, BBp, cc = B_["AAp"], B_["BBp"], B_["cc"]
                    xv = arr[:, qs, :, 0, 1:257].rearrange("p q r (b i) -> p q r b i", i=BL_)
                    hv = Hb[:, qs, :, 0, 1:257].rearrange("p q r (b i) -> p q r b i", i=BL_)
                    fr = B_["f"]
                    S.op(eng, lambda e, qs=qs: e.tensor_copy(out=Hb[:, qs, :, 0, 0], in_=arr[:, qs, :, 0, 0]), reads=bufs(arr), writes=bufs(Hb))
                    pend = []
                    for i_ in range(BL_ + 1):
                        if i_ < BL_:
                            f1, f2 = fr[i_ % 2]
                            S.op(eng, lambda e, f1=f1, cc=cc, AAp=AAp, i_=i_: e.tensor_tensor(
                                out=f1[:], in0=cc[:, :, :, 0:NB_], in1=AAp[:, :, :, i_:i_ + 1].to_broadcast([128, 8, 2, NB_]), op=ALU.mult),
                                reads=bufs(cc, AAp), writes=bufs(f1))
                            S.op(eng, lambda e, f2=f2, cc=cc, BBp=BBp, i_=i_: e.tensor_tensor(
                                out=f2[:], in0=cc[:, :, ::-1, 0:NB_], in1=BBp[:, :, :, i_:i_ + 1].to_broadcast([128, 8, 2, NB_]), op=ALU.mult),
                                reads=bufs(cc, BBp), writes=bufs(f2))
                        if i_ >= 1:
                            j_ = i_ - 1
                            f1, f2 = fr[j_ % 2]
                            S.op(eng, lambda e, f1=f1, f2=f2: e.tensor_tensor(out=f1[:], in0=f1[:], in1=f2[:], op=ALU.add),
                                 reads=bufs(f1, f2), writes=bufs(f1))
                            S.op(eng, lambda e, f1=f1, xv=xv, hv=hv, j_=j_: e.tensor_tensor(
                                out=hv[:, :, :, :, j_], in0=f1[:], in1=xv[:, :, :, :, j_], op=ALU.add), reads=bufs(f1, arr), writes=bufs(Hb))
            if not is_lat:
                for d_ in range(2):
                    S.op("act", lambda e, d_=d_, gs=gs: e.activation(
                        out=fin[:, :, d_, :, gs], in_=arr[:, d_:16:2, :, :, nch].rearrange("p g r s -> p s r g"), func=AF.Copy),
                        reads=bufs(arr), writes=bufs(fin))
            for g0 in range(0, 16, GPB):
                pb = banks.next()
                for gg in range(GPB):
                    gi = g0 + gg
                    gl, par = gi // 2, gi % 2
                    rows = slice(par * 64, (par + 1) * 64)
                    yreg = pb[:, gg * C_:(gg + 1) * C_]
                    S.op("pe", lambda e, yreg=yreg, gi=gi: e.matmul(yreg, lhsT=Tw[:, gi, :], rhs=X[:, gi, :], start=True, stop=False),
                         reads=bufs(Tw, X), writes=bufs(pb))
                    for d_ in range(2):
                        for r_ in range(2):
                            hsl = Hb[rows, gl * 2 + d_, r_, :, 0:nch]
                            if d_ == 1:
                                hsl = hsl[:, :, ::-1]
                            S.op("pe", lambda e, yreg=yreg, gl=gl, d_=d_, r_=r_, rows=rows, hsl=hsl: e.matmul(
                                yreg, lhsT=W2w[rows, gl, d_, r_, :], rhs=hsl, start=False, stop=(d_ == 1 and r_ == 1)),
                                reads=bufs(W2w, Hb), writes=bufs(pb))
                for gg in range(GPB):
                    gi = g0 + gg
                    S.op("dve", lambda e, pb=pb, gg=gg, gi=gi, b=b: e.scalar_tensor_tensor(
                        out=Ysb[:, gi, :], in0=X[:, gi, :], scalar=c["dX"][:, 16 * b + gi:16 * b + gi + 1],
                        in1=pb[:, gg * C_:(gg + 1) * C_], op0=ALU.mult, op1=ALU.add),
                        reads=bufs(pb, X, c["dX"]), writes=bufs(Ysb))
            for blk in range(2):
                for t0 in range(0, 8, GPB):
                    psel = banks.next()
                    for tt_ in range(GPB):
                        t = t0 + tt_
                        for g_ in range(8):
                            S.op("pe", lambda e, psel=psel, tt_=tt_, t=t, g_=g_, blk=blk: e.matmul(
                                psel[:, tt_ * C_:(tt_ + 1) * C_], lhsT=c["Wsel"][:, t, 112 - 16 * g_:240 - 16 * g_],
                                rhs=Ysb[:, blk * 8 + g_, :], start=(g_ == 0), stop=(g_ == 7)),
                                reads=bufs(c["Wsel"], Ysb), writes=bufs(psel))
                    S.op("act", lambda e, psel=psel, blk=blk, t0=t0: e.activation(
                        out=ygst[:, blk, t0 * C_:t0 * C_ + 512], in_=psel[:], func=AF.Gelu), reads=bufs(psel), writes=bufs(ygst))
            S.dma("sp", yscr[2 * b:2 * b + 2, :, tok0:tok0 + Tn].rearrange("b p t -> p b t"), ygst[:], reads=bufs(ygst))

        if not is_lat:
            for sq in range(4):
                pf = banks.next()
                for d_ in range(2):
                    for r_ in range(2):
                        j_ = d_ * 2 + r_
                        S.op("pe", lambda e, pf=pf, sq=sq, d_=d_, r_=r_, j_=j_: e.transpose(
                            out=pf[0:64, j_ * 128:(j_ + 1) * 128], in_=fin[:, sq, d_, r_, :], identity=identf[:]),
                            reads=bufs(fin, identf), writes=bufs(pf))
                st_ = fst.next()
                S.op("act", lambda e, pf=pf, st_=st_: e.activation(out=st_[:].rearrange("p a b -> p (a b)"), in_=pf[0:64, :], func=AF.Copy),
                     reads=bufs(pf), writes=bufs(st_))
                S.dma("sp", new_s5[sq].rearrange("d r (gp two) n -> gp (d r) (two n)", two=2), st_[:], reads=bufs(st_))

    def s5_glu(c, tok0, sub, yT):
        ygT = k.at([128, 16, 1024], BF16)
        S.dma("sp", ygT[:], yscr[:, :, tok0 + sub * 1024:tok0 + (sub + 1) * 1024].rearrange("b p t -> p b t"), writes=bufs(ygT))
        wgr = k.aring(2, [128, 16, 128], BF16)
        sgr = k.aring(2, [128, 512], F32)
        szr = k.aring(2, [128, 512], F32)
        for blk in range(16):
            wg = wgr.next()
            S.dma("pool", wg[:], s5_w_glu.rearrange("(k p) n -> p k n", p=128)[:, :, blk * 128:(blk + 1) * 128], writes=bufs(wg))
            if blk % 4 == 0:
                wz = load_w(s5_w_in, E + blk * 128, 512)
            co = (blk % 4) * 128
            for q in range(2):
                p0 = sub * 1024 + q * 512
                pg_ = banks.next()
                for kk in range(16):
                    S.op("pe", lambda e, pg_=pg_, kk=kk, wg=wg, q=q: e.matmul(
                        pg_[:], lhsT=wg[:, kk, :], rhs=ygT[:, kk, q * 512:(q + 1) * 512], start=(kk == 0), stop=(kk == 15)),
                        reads=bufs(wg, ygT), writes=bufs(pg_))
                sg = sgr.next()
                S.op("act", lambda e, pg_=pg_, sg=sg, blk=blk: e.activation(
                    out=sg[:], in_=pg_[:], func=AF.Tanh, scale=0.5, bias=c["bgh"][:, blk:blk + 1]), reads=bufs(pg_, c["bgh"]), writes=bufs(sg))
                pz = banks.next()
                for kk in range(8):
                    S.op("pe", lambda e, pz=pz, kk=kk, wz=wz, co=co, p0=p0: e.matmul(
                        pz[:], lhsT=wz[:, kk, co:co + 128], rhs=hT[:, kk, p0:p0 + 512], start=(kk == 0), stop=(kk == 7)),
                        reads=bufs(wz, hT), writes=bufs(pz))
                sz = szr.next()
                S.op("act", lambda e, pz=pz, sz=sz: e.activation(out=sz[:], in_=pz[:], func=AF.Silu), reads=bufs(pz), writes=bufs(sz))
                S.op("dve", lambda e, sg=sg, blk=blk, q=q: e.scalar_tensor_tensor(
                    out=sg[:], in0=sg[:], scalar=1.0, in1=ygT[:, blk, q * 512:(q + 1) * 512], op0=ALU.add, op1=ALU.mult),
                    reads=bufs(sg, ygT), writes=bufs(sg))
                S.op("dve", lambda e, sg=sg, sz=sz, blk=blk, p0=p0: e.scalar_tensor_tensor(
                    out=yT[:, blk, p0:p0 + 512], in0=sg[:], scalar=0.5, in1=sz[:], op0=ALU.mult, op1=ALU.mult),
                    reads=bufs(sg, sz), writes=bufs(yT))

    def std_tiles(tok0, n):
        return [(rows_std(tok0 + i * 128), i * 128) for i in range(n)]

    units = [(0, 8, 0), (1024, 8, 1), (2048, 8, 1)]
    src = cfg.get("src", None) and inp("xsrc", [NTOK, D]) or xin
    for li in layers:
        last = final and (li == layers[-1])
        dst = xres
        k.areset()
        phase_a(li)
        if li == 1:
            L["yT"] = k.at([128, 16, 1024], BF16)
            c = gmlp_consts()
            for (tok0, nt, cond) in units:
                tiles = std_tiles(tok0, nt)
                phase_b(src, tiles, cond)
                gmlp_unit(c, nt)
                load_wout(li)
                phase_d(src, dst, tiles, cond, last)
        if li == 0:
            c = ssd_consts()
            m0 = k.amark()
            for (tok0, nt, nseq, cond) in ((0, 8, 4, 0), (1024, 16, 1, 1)):
                tiles = std_tiles(tok0, nt)
                phase_b(src, tiles, cond)
                rstd = ssd_unit(c, tok0, nt, nseq, cond == 1)
                S.op("act", lambda e, rstd=rstd, nt=nt: e.activation(out=rstd_keep[:, 0:nt], in_=rstd[:], func=AF.Copy),
                     reads=bufs(rstd), writes=bufs(rstd_keep))
                k.arestore(m0)
                load_wout(li)
                phase_d(src, dst, tiles, cond, last, scale_t=lambda i: (rstd_keep[:, i:i + 1], rstd_keep.b), ytok0=tok0)
                S.barrier()
        if li == 2:
            c = s5_prep()
            m0 = k.amark()
            for (tok0, is_lat, cond) in ((0, False, 0), (1024, True, 1)):
                nsub = 2 if is_lat else 1
                tiles = []
                for sub in range(nsub):
                    tiles += s5_tiles(tok0, sub, is_lat)
                phase_b(src, tiles, cond)
                s5_unit(c, tok0, is_lat)
                k.arestore(m0)
                L["yT"] = k.at([128, 16, 1024 * nsub], BF16)
                m1 = k.amark()
                for sub in range(nsub):
                    s5_glu(c, tok0, sub, L["yT"])
                    k.arestore(m1)
                load_wout(li)
                phase_d(src, dst, tiles, cond, last)
                k.arestore(m0)
        if li == 3:
            L["yT"] = k.at([128, 16, 2048], BF16)
            tiles = std_tiles(0, 8)
            phase_b(src, tiles, 0)
            m_ = k.amark()
            if not cfg.get("skip_ctx"):
                nat_ctx_unit()
            k.arestore(m_)
            load_wout(li)
            phase_d(src, dst, tiles, 0, last)
            S.barrier()
            tiles = std_tiles(1024, 16)
            phase_b(src, tiles, 1)
            m_ = k.amark()
            nat_lat_unit()
            if not cfg.get("skip_d"):
                k.arestore(m_)
            load_wout(li)
            phase_d(src, dst, tiles, 1, last)
        src = xres
    if not final and not cfg.get("skip_d"):
        S.barrier()
        xring = k.aring(3, [128, D], F32)
        for i in range(NTOK // 128):
            xt = xring.next()
            S.dma("sp", xt[:], xres[i * 128:(i + 1) * 128, :], writes=bufs(xt))
            S.dma("sp", y_out[i * 128:(i + 1) * 128, :], xt[:], reads=bufs(xt))
    S.emit(es)
    return nc, es


def host_inputs(inputs, core):
    f = np.ascontiguousarray
    m = {}
    m["xin"] = f(np.concatenate([inputs["x_prompt"][4 * core:4 * core + 4].reshape(NP_TOK, D),
                                 inputs["x_sample"][core % 2]], axis=0))
    m["cvec"] = f(np.stack([inputs["c_ctx"], inputs["c"][core % 2]], axis=0))
    for nm in ["norm_g", "w_mod", "b_mod", "w_out", "final_g"]:
        m[nm] = f(inputs[nm])
    m["mlp_w_in"] = f(inputs["mlp_w_in"][0])
    m["mlp_ln_g"] = f(inputs["mlp_ln_g"][0])
    m["mlp_ln_b"] = f(inputs["mlp_ln_b"][0])
    m["mlp_w_sT"] = f(np.transpose(inputs["mlp_w_s"][0], (0, 2, 1)))
    m["mlp_b_s"] = f(inputs["mlp_b_s"][0])
    m["ssd_w_in"] = f(inputs["ssd_w_in"][0])
    m["ssd_conv_w"] = f(inputs["ssd_conv_w"][0])
    m["ssd_conv_b"] = f(inputs["ssd_conv_b"][0])
    m["ssd_dt_bias"] = f(inputs["ssd_dt_bias"][0].reshape(64))
    m["ssd_a_log"] = f(inputs["ssd_a_log"][0].reshape(64))
    m["ssd_d"] = f(inputs["ssd_d"][0])
    m["ssd_norm_g"] = f(inputs["ssd_norm_g"][0])
    m["state_ssd"] = f(inputs["state_ssd"][core % 2, 0])
    m["s5_w_in"] = f(inputs["s5_w_in"][0])

    def pl(a):
        sh = a.shape[:-2]
        a = a.reshape(sh + (64, 2, 64))
        return np.moveaxis(a, -3, -1).reshape(sh + (128, 64))
    m["s5_lam"] = f(np.stack([pl(inputs["s5_lam_re"][0]), pl(inputs["s5_lam_im"][0])], 0))
    m["s5_lstep"] = f(pl(np.broadcast_to(inputs["s5_log_step"][0][:, :, None], (2, 128, 64))))

    def plj(a):
        a = a.reshape(2, 64, 2, 64, 16)
        return np.transpose(a, (0, 2, 3, 1, 4)).reshape(2, 128, 64, 16)
    m["s5_B"] = f(np.stack([plj(inputs["s5_b_re"][0]), plj(inputs["s5_b_im"][0])], 0))
    m["s5_C"] = f(np.stack([plj(np.transpose(inputs["s5_c_re"][0], (0, 1, 3, 2))),
                            plj(np.transpose(inputs["s5_c_im"][0], (0, 1, 3, 2)))], 0))
    m["s5_h0"] = f(pl(inputs["state_s5"][core % 2, 0]))
    m["s5_d"] = f(inputs["s5_d"][0])
    m["s5_w_glu"] = f(inputs["s5_w_glu"][0])
    m["s5_b_glu"] = f(inputs["s5_b_glu"][0])
    m["nat_w_in"] = f(inputs["nat_w_in"][0])
    m["rpbg"] = rpb_gather(inputs["nat_rpb"][0])
    m["natmask"] = nat_masks()
    m["cache_k"] = f(inputs["cache_k"][core % 2, 0])
    m["cache_v"] = f(inputs["cache_v"][core % 2, 0])
    return m


def rpb_gather(rpb):
    qc = np.arange(64)[:, None]
    kc = np.arange(64)[None, :]
    ci = np.clip(kc - qc + 15, 0, 30)
    out = np.zeros((32, 128, 16, 64), np.float32)
    g = rpb[:, :, ci]
    g = np.transpose(g, (0, 2, 1, 3))
    out[:, 0:64, 0:15, :] = g
    out[:, 64:128, 1:16, :] = g
    return np.ascontiguousarray(out.reshape(32, 128, 1024))


def nat_masks():
    NEG = -30000.0 * 8.0
    qc = np.arange(64)
    cs = np.clip(qc - 8, 0, 48)
    kc = np.arange(64)
    col_ok = (kc[None, :] >= cs[:, None]) & (kc[None, :] < cs[:, None] + 16)
    m = np.zeros((3, 128, 9, 64), np.float32)
    colm = np.where(col_ok, 0.0, NEG).astype(np.float32)
    m[:, 0:64] += colm[None, :, None, :]
    m[:, 64:128] += colm[None, :, None, :]
    m[0, 0:64, 8, :] = NEG
    m[0, 64:128, 0, :] = NEG
    m[1, :, 8, :] = NEG
    return np.ascontiguousarray(m.reshape(3, 128, 576))


def kernel(**inputs):
    inputs = {k_: np.asarray(v) for k_, v in inputs.items()}
    nc, es = build({})
    with es:
        in_maps = [host_inputs(inputs, c) for c in range(8)]
        res = run_bass_kernel_spmd(nc, in_maps, core_ids=list(range(8)))
    r = res.results
    y_prompt = np.concatenate([r[c]["y_out"][:NP_TOK].reshape(4, 256, D) for c in range(8)], axis=0)
    y_sample = np.stack([r[c]["y_out"][NP_TOK:] for c in range(2)], axis=0)
    new_ssd = np.concatenate([r[c]["new_ssd"] for c in range(8)], axis=0)[:, None]
    new_s5 = np.concatenate([r[c]["new_s5"] for c in range(8)], axis=0)[:, None]
    new_k = np.concatenate([r[c]["new_k"] for c in range(8)], axis=0)[:, None]
    new_v = np.concatenate([r[c]["new_v"] for c in range(8)], axis=0)[:, None]
    return (y_prompt.astype(np.float32), y_sample.astype(np.float32), np.ascontiguousarray(new_ssd, dtype=np.float32),
            np.ascontiguousarray(new_s5, dtype=np.float32), np.ascontiguousarray(new_k, dtype=np.float32),
            np.ascontiguousarray(new_v, dtype=np.float32))
```

```python
import numpy as np
from contextlib import ExitStack
import concourse.bass as bass
import concourse.mybir as mybir
from concourse.bass_utils import run_bass_kernel_spmd

F32 = mybir.dt.float32
BF16 = mybir.dt.bfloat16
AF = mybir.ActivationFunctionType
ALU = mybir.AluOpType
AX = mybir.AxisListType

D = 1024
E = 2048
NP_TOK = 1024
NS_TOK = 2048
NTOK = NP_TOK + NS_TOK
EPS = 1e-6
COMPUTE = ("pe", "act", "dve", "pool")
NDMASEM = 12
SAME_ENGINE_SYNC = True


class Buf:
    __slots__ = ("lw", "rd")

    def __init__(self):
        self.lw = None
        self.rd = {}


class Sched:
    def __init__(self, nc):
        self.nc = nc
        self.ops = {e: [] for e in COMPUTE + ("sp",)}
        self.cnt = {e: 0 for e in COMPUTE}
        self.seen = {e: {} for e in COMPUTE + ("sp",)}
        self.dma_slot = {}
        self.dma_val = {}
        self.sems = {}
        self.refd = {e: set() for e in COMPUTE}

    def _deps(self, eng, reads, writes):
        deps = {}

        def add(tok):
            if tok is None:
                return
            k, v = tok
            if deps.get(k, 0) < v:
                deps[k] = v

        for r in reads:
            add(r.lw)
        for w in writes:
            add(w.lw)
            for k, v in w.rd.items():
                add((k, v))
        out = []
        seen = self.seen[eng]
        for k, v in deps.items():
            if k == eng and (eng == "pe" or not SAME_ENGINE_SYNC):
                continue
            if seen.get(k, 0) >= v:
                continue
            seen[k] = v
            out.append((k, v))
            if isinstance(k, str):
                self.refd[k].add(v)
        return out

    def _mark(self, tok, reads, writes):
        k, v = tok
        for r in reads:
            if r.rd.get(k, 0) < v:
                r.rd[k] = v
        for w in writes:
            w.lw = tok
            w.rd = {}

    def op(self, eng, fn, reads=(), writes=()):
        waits = self._deps(eng, reads, writes)
        self.cnt[eng] += 1
        tok = (eng, self.cnt[eng])
        self.ops[eng].append((waits, fn, tok, 1))
        self._mark(tok, reads, writes)

    def dma(self, q, out, in_, reads=(), writes=()):
        slot = self.dma_slot.get(q, 0)
        self.dma_slot[q] = (slot + 1) % NDMASEM
        key = ("dma", q, slot)
        prev = self.dma_val.get(key, 0)
        waits = self._deps(q, reads, writes)
        if prev > 0 and self.seen[q].get(key, 0) < prev:
            self.seen[q][key] = prev
            waits.append((key, prev))
        val = prev + 16
        self.dma_val[key] = val
        tok = (key, val)

        def fn(e, out=out, in_=in_):
            return e.dma_start(out=out, in_=in_, allow_slow_non_contiguous=True)

        self.ops[q].append((waits, fn, tok, 16))
        self._mark(tok, reads, writes)

    def barrier(self):
        targets = [(e, self.cnt[e]) for e in COMPUTE if self.cnt[e] > 0]
        targets += [(key, v) for key, v in self.dma_val.items()]
        for eng in COMPUTE + ("sp",):
            waits = []
            for key, v in targets:
                if key == eng:
                    continue
                if self.seen[eng].get(key, 0) < v:
                    self.seen[eng][key] = v
                    waits.append((key, v))
                    if isinstance(key, str):
                        self.refd[key].add(v)
            if waits:
                self.ops[eng].append((waits, None, None, 0))

    def emit(self, es, final_wait_engine="sp"):
        nc = self.nc
        keys = list(COMPUTE)
        for q in self.dma_slot:
            for s in range(NDMASEM):
                if ("dma", q, s) in self.dma_val:
                    keys.append(("dma", q, s))
        for k in keys:
            nm = k if isinstance(k, str) else "d_%s_%d" % (k[1], k[2])
            self.sems[k] = es.enter_context(nc.semaphore("s_" + nm))
        fin = []
        for k in keys:
            v = self.cnt[k] if isinstance(k, str) else self.dma_val[k]
            if v > 0 and k != final_wait_engine:
                fin.append((k, v))
                if isinstance(k, str):
                    self.refd[k].add(v)
        rank = {}
        for e_ in COMPUTE:
            r_ = {}
            for n_, idx in enumerate(sorted(self.refd[e_])):
                r_[idx] = n_ + 1
            rank[e_] = r_

        def semval(k, v):
            return rank[k][v] if isinstance(k, str) else v
        block = es.enter_context(nc.Block())

        def run(e, name):
            for waits, fn, tok, inc in self.ops[name]:
                if fn is None:
                    for k, v in waits:
                        e.wait_ge(self.sems[k], semval(k, v))
                    continue
                NW = 1
                for k, v in waits[NW:]:
                    e.wait_ge(self.sems[k], semval(k, v))
                ins = fn(e)
                for k, v in waits[:NW]:
                    ins._wait_ge(self.sems[k], semval(k, v))
                if not isinstance(tok[0], str) or tok[1] in self.refd[tok[0]]:
                    ins.then_inc(self.sems[tok[0]], inc)
            if name == final_wait_engine:
                for k, v in fin:
                    e.wait_ge(self.sems[k], semval(k, v))

        @block.tensor
        def _(e):
            run(e, "pe")

        @block.scalar
        def _(e):
            run(e, "act")

        @block.vector
        def _(e):
            run(e, "dve")

        @block.gpsimd
        def _(e):
            run(e, "pool")

        @block.sync
        def _(e):
            run(e, "sp")


class T:
    __slots__ = ("t", "b")

    def __init__(self, t):
        self.t = t
        self.b = Buf()

    def __getitem__(self, k):
        return self.t[k]


class Ring:
    def __init__(self, tiles):
        self.tiles = tiles
        self.i = 0

    def next(self):
        t = self.tiles[self.i]
        self.i = (self.i + 1) % len(self.tiles)
        return t


class K:
    def __init__(self, nc, es):
        self.nc = nc
        self.es = es
        self.S = Sched(nc)
        self.n = 0

    def sb(self, shape, dt, name=None):
        self.n += 1
        return T(self.es.enter_context(self.nc.sbuf_tensor(name or "sb%d" % self.n, list(shape), dt)))

    def ring(self, n, shape, dt):
        return Ring([self.sb(shape, dt) for _ in range(n)])

    def psb(self, shape, dt):
        self.n += 1
        return T(self.es.enter_context(self.nc.psum_tensor("ps%d" % self.n, list(shape), dt)))

    def init_arena(self, nbytes):
        self.arena = self.es.enter_context(self.nc.sbuf_tensor("arena", [128, nbytes // 2], BF16))
        self.asize = nbytes
        self.aoff = 0
        self.alog = []

    def areset(self):
        self.S.barrier()
        self.aoff = 0

    def at(self, shape, dt):
        esz = 4 if dt == F32 else 2
        n = 1
        for d_ in shape[1:]:
            n *= d_
        nb = (n * esz + 63) // 64 * 64
        assert self.aoff + nb <= self.asize, ("arena overflow", self.aoff, nb, self.asize)
        ap = self.arena[0:shape[0], self.aoff // 2:(self.aoff + n * esz) // 2]
        if dt == F32:
            ap = ap.bitcast(F32)
        if len(shape) > 2:
            names = ["d%d" % i for i in range(len(shape) - 1)]
            kw = {names[i]: shape[i + 1] for i in range(len(names) - 1)}
            ap = ap.rearrange("p (%s) -> p %s" % (" ".join(names), " ".join(names)), **kw)
        self.alog.append((self.aoff, tuple(shape), dt))
        self.aoff += nb
        return T(ap)

    def amark(self):
        return self.aoff

    def arestore(self, m):
        self.S.barrier()
        self.aoff = m

    def aring(self, n, shape, dt):
        return Ring([self.at(shape, dt) for _ in range(n)])

    def dram(self, name, shape, dt, kind="Internal"):
        return self.nc.dram_tensor(name, list(shape), dt, kind=kind).ap()


def bufs(*ts):
    return [t.b for t in ts]


def build(cfg):
    layers = cfg.get("layers", [0, 1, 2, 3])
    final = cfg.get("final", True)
    nc = bass.Bass("TRN2", target_bir_lowering=False)
    es = ExitStack()
    k = K(nc, es)
    S = k.S
    I = {}

    def inp(name, shape):
        I[name] = k.dram(name, shape, F32, kind="ExternalInput")
        return I[name]

    xin = inp("xin", [NTOK, D])
    cvec = inp("cvec", [2, D])
    norm_g = inp("norm_g", [4, D])
    w_mod = inp("w_mod", [4, D, 3 * D])
    b_mod = inp("b_mod", [4, 3 * D])
    w_out = inp("w_out", [4, E, D])
    final_g = inp("final_g", [D])
    mlp_w_in = inp("mlp_w_in", [D, 3 * E])
    mlp_ln_g = inp("mlp_ln_g", [E])
    mlp_ln_b = inp("mlp_ln_b", [E])
    mlp_w_sT = inp("mlp_w_sT", [8, 128, 128])
    mlp_b_s = inp("mlp_b_s", [8, 128])
    ssd_w_in = inp("ssd_w_in", [D, 6208])
    ssd_conv_w = inp("ssd_conv_w", [5, 4096])
    ssd_conv_b = inp("ssd_conv_b", [4096])
    ssd_dt_bias = inp("ssd_dt_bias", [64])
    ssd_a_log = inp("ssd_a_log", [64])
    ssd_d = inp("ssd_d", [32])
    ssd_norm_g = inp("ssd_norm_g", [E])
    state_ssd = inp("state_ssd", [2, 32, 64, 128])
    new_ssd = k.dram("new_ssd", [4, 2, 32, 64, 128], F32, kind="ExternalOutput")
    yscr = k.dram("yscr", [16, 128, NTOK], BF16)
    s5_w_in = inp("s5_w_in", [D, 2 * E])
    s5_lam = inp("s5_lam", [2, 2, 128, 64])
    s5_lstep = inp("s5_lstep", [2, 128, 64])
    s5_B = inp("s5_B", [2, 2, 128, 64, 16])
    s5_C = inp("s5_C", [2, 2, 128, 64, 16])
    s5_h0 = inp("s5_h0", [2, 2, 128, 64])
    s5_d = inp("s5_d", [E])
    s5_w_glu = inp("s5_w_glu", [E, E])
    s5_b_glu = inp("s5_b_glu", [E])
    new_s5 = k.dram("new_s5", [4, 2, 2, 128, 64], F32, kind="ExternalOutput")
    Tscr = k.dram("Tscr", [8, 128, 16 * 128], BF16)
    VTscr = k.dram("VTscr", [8, 128, 8 * 4 * 128], BF16)
    W2scr = k.dram("W2scr", [8, 128, 8 * 4 * 128], BF16)
    nat_w_in = inp("nat_w_in", [D, 4 * E])
    rpbg = inp("rpbg", [32, 128, 1024])
    natmask = inp("natmask", [3, 128, 576])
    cache_k = inp("cache_k", [32, 256, 64])
    cache_v = inp("cache_v", [32, 256, 64])
    new_k = k.dram("new_k", [4, 32, 256, 64], F32, kind="ExternalOutput")
    new_v = k.dram("new_v", [4, 32, 256, 64], F32, kind="ExternalOutput")
    y_out = k.dram("y_out", [NTOK, D], F32, kind="ExternalOutput")
    xres = k.dram("xres", [NTOK, D], F32)
    dma_done = Buf()

    identf = k.sb([128, 128], F32)
    identb = k.sb([128, 128], BF16)
    onesf = k.sb([128, 128], F32)
    S.op("pool", lambda e: e.memset(identf[:], 0.0), writes=bufs(identf))
    S.op("pool", lambda e: e.affine_select(out=identf[:], in_=identf[:], compare_op=ALU.not_equal, fill=1.0,
                                           base=0, pattern=[[-1, 128]], channel_multiplier=1),
         reads=bufs(identf), writes=bufs(identf))
    S.op("dve", lambda e: e.tensor_copy(out=identb[:], in_=identf[:]), reads=bufs(identf), writes=bufs(identb))
    S.op("pool", lambda e: e.memset(onesf[:], 1.0), writes=bufs(onesf))

    banks = Ring([k.psb([128, 512], F32) for _ in range(8)])

    hT = k.sb([128, 8, 2048], BF16, "hT")
    wo = T(hT.t)
    wo.b = hT.b
    wo_view = hT.t[:].rearrange("p k t -> p (k t)").rearrange("p (k n) -> p k n", k=16)
    wring = k.ring(3, [128, 8, 512], BF16)
    junk = k.sb([128, D], BF16)
    small = k.ring(8, [128, 8], F32)
    rstd_keep = k.sb([128, 16], F32)
    k.init_arena(136 * 1024)
    L = {}

    cf = k.sb([128, 8, 2], F32)
    cb = k.sb([128, 8, 2], BF16)
    for c_ in range(2):
        S.dma("sp", cf[:, :, c_], cvec[c_].rearrange("(k p) -> p k", p=128), writes=bufs(cf))
    S.op("act", lambda e: e.activation(out=cb[:], in_=cf[:], func=AF.Silu), reads=bufs(cf), writes=bufs(cb))

    modT = k.sb([128, 16, 2], F32)
    bmodT = k.sb([128, 16], F32)
    ngT = k.sb([128, 8], F32)
    Asc = k.sb([128, 8, 2], F32)
    gate_bc = [k.sb([128, D], F32), k.sb([128, D], F32)]
    sel = [k.sb([2, 128], F32), k.sb([2, 128], F32)]
    for c in range(2):
        S.op("pool", lambda e, c=c: e.memset(sel[c][:], 0.0), writes=bufs(sel[c]))
        S.op("pool", lambda e, c=c: e.affine_select(out=sel[c][:], in_=sel[c][:], compare_op=ALU.not_equal, fill=1.0,
                                                     base=-c, pattern=[[0, 128]], channel_multiplier=1),
             reads=bufs(sel[c]), writes=bufs(sel[c]))

    wcache = {}

    def load_w(wap, c0, n, q="pool", cache=None):
        wt = wring.next()
        if cache is None:
            S.dma(q, wt[:, :, 0:n], wap.rearrange("(k p) n -> p k n", p=128)[:, :, c0:c0 + n], writes=bufs(wt))
            return wt
        key = (cache, c0, n)
        if key not in wcache:
            scr = k.dram("wc_%s_%d_%d" % (cache, c0, n), [128, 8 * n], BF16)
            sb_ = Buf()
            wcache[key] = (scr, sb_)
            S.dma(q, wt[:, :, 0:n], wap.rearrange("(k p) n -> p k n", p=128)[:, :, c0:c0 + n], writes=bufs(wt))
            S.dma("sp", scr.rearrange("p (k n) -> p k n", k=8), wt[:, :, 0:n], reads=bufs(wt), writes=[sb_])
        else:
            scr, sb_ = wcache[key]
            S.dma("sp", wt[:, :, 0:n], scr.rearrange("p (k n) -> p k n", k=8), reads=[sb_], writes=bufs(wt))
        return wt

    def phase_a(li):
        ma_ = k.amark()
        gate2 = k.at([2, D], F32)
        bgate2 = k.at([2, D], F32)
        S.dma("sp", bmodT[:], b_mod[li, 0:2 * D].rearrange("(c p) -> p c", p=128), writes=bufs(bmodT))
        S.dma("sp", ngT[:], norm_g[li].rearrange("(c p) -> p c", p=128), writes=bufs(ngT))
        S.dma("sp", bgate2[:], b_mod[li, 2 * D:3 * D].partition_broadcast(2), writes=bufs(bgate2))
        for blk in range(4):
            wt = load_w(w_mod[li], blk * 512, 512)
            for cc in range(4):
                ch = blk * 4 + cc
                pb = banks.next()
                for kk in range(8):
                    S.op("pe", lambda e, pb=pb, wt=wt, cc=cc, kk=kk: e.matmul(
                        pb[:, 0:2], lhsT=wt[:, kk, cc * 128:(cc + 1) * 128], rhs=cb[:, kk, :],
                        start=(kk == 0), stop=(kk == 7)), reads=bufs(wt, cb), writes=bufs(pb))
                S.op("dve", lambda e, pb=pb, ch=ch: e.tensor_scalar(
                    out=modT[:, ch, :], in0=pb[:, 0:2], scalar1=bmodT[:, ch:ch + 1], scalar2=None, op0=ALU.add),
                    reads=bufs(pb, bmodT), writes=bufs(modT))
        S.op("dve", lambda e: e.tensor_scalar(out=Asc[:], in0=modT[:, 8:16, :], scalar1=1.0, scalar2=None, op0=ALU.add),
             reads=bufs(modT), writes=bufs(Asc))
        S.op("dve", lambda e: e.tensor_tensor(out=Asc[:], in0=Asc[:], in1=ngT[:].unsqueeze(2).to_broadcast([128, 8, 2]),
                                              op=ALU.mult), reads=bufs(Asc, ngT), writes=bufs(Asc))
        for blk in range(2):
            wt = load_w(w_mod[li], 2 * D + blk * 512, 512)
            pb = banks.next()
            for kk in range(8):
                S.op("pe", lambda e, pb=pb, wt=wt, kk=kk: e.matmul(
                    pb[0:2, :], lhsT=cb[:, kk, :], rhs=wt[:, kk, :], start=(kk == 0), stop=(kk == 7)),
                    reads=bufs(wt, cb), writes=bufs(pb))
            S.op("dve", lambda e, pb=pb, blk=blk: e.tensor_tensor(
                out=gate2[:, blk * 512:(blk + 1) * 512], in0=pb[0:2, :], in1=bgate2[:, blk * 512:(blk + 1) * 512],
                op=ALU.add), reads=bufs(pb, bgate2), writes=bufs(gate2))
        for c in range(2):
            for blk in range(2):
                pb = banks.next()
                S.op("pe", lambda e, pb=pb, c=c, blk=blk: e.matmul(
                    pb[:], lhsT=sel[c][:], rhs=gate2[:, blk * 512:(blk + 1) * 512], start=True, stop=True),
                    reads=bufs(sel[c], gate2), writes=bufs(pb))
                S.op("act", lambda e, pb=pb, c=c, blk=blk: e.activation(
                    out=gate_bc[c][:, blk * 512:(blk + 1) * 512], in_=pb[:], func=AF.Copy),
                    reads=bufs(pb), writes=bufs(gate_bc[c]))
        k.arestore(ma_)

    def rows_std(tok0):
        return lambda src: src[tok0:tok0 + 128, :]

    def rms_stats(xt):
        st = small.next()
        S.op("act", lambda e: e.activation(out=junk[:], in_=xt[:], func=AF.Square, accum_out=st[:, 0:1]),
             reads=bufs(xt), writes=bufs(junk, st))
        S.op("dve", lambda e: e.tensor_scalar(out=st[:, 0:1], in0=st[:, 0:1], scalar1=1.0 / D, scalar2=EPS,
                                              op0=ALU.mult, op1=ALU.add), reads=bufs(st), writes=bufs(st))
        S.op("act", lambda e: e.activation(out=st[:, 0:1], in_=st[:, 0:1], func=AF.Sqrt), reads=bufs(st), writes=bufs(st))
        S.op("dve", lambda e: e.reciprocal(out=st[:, 0:1], in_=st[:, 0:1]), reads=bufs(st), writes=bufs(st))
        return st

    def phase_b(src, tiles, cond):
        m_ = k.amark()
        xring = k.aring(3, [128, D], F32)
        xnring = k.aring(2, [128, D], BF16)

        def load(i):
            xt = xring.next()
            S.dma("sp", xt[:], tiles[i][0](src), writes=bufs(xt))
            return xt
        nxt = load(0)
        for i in range(len(tiles)):
            xt = nxt
            if i + 1 < len(tiles):
                nxt = load(i + 1)
            col0 = tiles[i][1]
            st = rms_stats(xt)
            xn = xnring.next()
            S.op("dve", lambda e, xn=xn, xt=xt, st=st: e.tensor_scalar(out=xn[:], in0=xt[:], scalar1=st[:, 0:1],
                                                                   scalar2=None, op0=ALU.mult),
                 reads=bufs(xt, st), writes=bufs(xn))
            pb = banks.next()
            pv = pb[:].bitcast(BF16).rearrange("p (k t) -> p k t", k=8)
            for kk in range(8):
                S.op("pe", lambda e, pv=pv, xn=xn, kk=kk: e.transpose(out=pv[:, kk, :], in_=xn[:, kk * 128:(kk + 1) * 128],
                                                                    identity=identb[:]),
                     reads=bufs(xn, identb), writes=bufs(pb))
            for kk in range(8):
                S.op("act", lambda e, pv=pv, kk=kk, col0=col0: e.activation(
                    out=hT[:, kk, col0:col0 + 128], in_=pv[:, kk, :], func=AF.Identity,
                    scale=Asc[:, kk, cond:cond + 1], bias=modT[:, kk, cond:cond + 1]),
                    reads=bufs(pb, Asc, modT), writes=bufs(hT))
        k.arestore(m_)


    def load_wout(li):
        for h in range(2):
            S.dma("pool", wo_view[:, h * 8:(h + 1) * 8, :],
                  w_out[li].rearrange("(k p) n -> p k n", p=128)[:, h * 8:(h + 1) * 8, :], writes=bufs(wo))

    def phase_d(src, dst, tiles, cond, last, scale_t=None, ytok0=None):
        if cfg.get("skip_d"):
            return
        m_ = k.amark()
        xring = k.aring(3, [128, D], F32)
        tring = k.aring(2, [128, D], F32)
        if last:
            fg_bc = k.at([128, D], F32)
            S.dma("sp", fg_bc[:], final_g.partition_broadcast(128), writes=bufs(fg_bc))
        if ytok0 is None:
            yT = L["yT"]
        else:
            yring = k.aring(2, [128, 16, 512], BF16)
            yT = None

        def load(i):
            xt = xring.next()
            S.dma("sp", xt[:], tiles[i][0](src), writes=bufs(xt))
            return xt
        nxt = load(0)
        for i in range(len(tiles)):
            xt = nxt
            if i + 1 < len(tiles):
                nxt = load(i + 1)
            col0 = tiles[i][1]
            if ytok0 is not None:
                if i % 4 == 0:
                    yT = yring.next()
                    S.dma("sp", yT[:], yscr[:, :, ytok0 + tiles[i][1]:ytok0 + tiles[i][1] + 512].rearrange("b p t -> p b t"),
                          writes=bufs(yT))
                col0 = (i % 4) * 128
            tt = tring.next()
            for h in range(2):
                pb = banks.next()
                for kk in range(16):
                    S.op("pe", lambda e, pb=pb, kk=kk, h=h, col0=col0, yT=yT: e.matmul(
                        pb[:], lhsT=yT[:, kk, col0:col0 + 128], rhs=wo_view[:, kk, h * 512:(h + 1) * 512],
                        start=(kk == 0), stop=(kk == 15)), reads=bufs(yT, wo), writes=bufs(pb))
                if scale_t is None:
                    S.op("dve", lambda e, pb=pb, tt=tt, h=h: e.tensor_tensor(
                        out=tt[:, h * 512:(h + 1) * 512], in0=pb[:], in1=gate_bc[cond][:, h * 512:(h + 1) * 512],
                        op=ALU.mult), reads=bufs(pb, gate_bc[cond]), writes=bufs(tt))
                else:
                    sc = scale_t(i)
                    S.op("dve", lambda e, pb=pb, tt=tt, h=h, sc=sc: e.scalar_tensor_tensor(
                        out=tt[:, h * 512:(h + 1) * 512], in0=pb[:], scalar=sc[0], in1=gate_bc[cond][:, h * 512:(h + 1) * 512],
                        op0=ALU.mult, op1=ALU.mult), reads=bufs(pb, gate_bc[cond]) + [sc[1]], writes=bufs(tt))
            S.op("pool", lambda e, tt=tt, xt=xt: e.tensor_tensor(out=xt[:], in0=tt[:], in1=xt[:], op=ALU.add),
                 reads=bufs(tt, xt), writes=bufs(xt))
            if not last:
                S.dma("sp", tiles[i][0](dst), xt[:], reads=bufs(xt))
            else:
                st = rms_stats(xt)
                S.op("dve", lambda e, tt=tt, xt=xt, st=st: e.scalar_tensor_tensor(
                    out=tt[:], in0=xt[:], scalar=st[:, 0:1], in1=fg_bc[:], op0=ALU.mult, op1=ALU.mult),
                    reads=bufs(xt, st, fg_bc), writes=bufs(tt))
                S.dma("sp", tiles[i][0](y_out), tt[:], reads=bufs(tt))
        k.arestore(m_)

    def gmlp_consts():
        c = {}
        c["lngT"] = k.at([128, 16], F32)
        c["lnbT"] = k.at([128, 16], F32)
        c["wsT"] = k.at([128, 8, 128], BF16)
        c["wsTf"] = k.at([128, 8, 128], F32)
        c["bs_bc"] = k.at([128, 8, 128], F32)
        c["Bt"] = k.at([128, 16, 128], F32)
        S.dma("sp", c["lngT"][:], mlp_ln_g.rearrange("(c p) -> p c", p=128), writes=bufs(c["lngT"]))
        S.dma("sp", c["lnbT"][:], mlp_ln_b.rearrange("(c p) -> p c", p=128), writes=bufs(c["lnbT"]))
        S.dma("sp", c["wsTf"][:], mlp_w_sT.rearrange("g j i -> j g i"), writes=bufs(c["wsTf"]))
        S.dma("sp", c["bs_bc"][:].rearrange("p g i -> p (g i)"), mlp_b_s.rearrange("g i -> (g i)").partition_broadcast(128),
              writes=bufs(c["bs_bc"]))
        S.op("dve", lambda e: e.tensor_copy(out=c["wsT"][:], in_=c["wsTf"][:]), reads=bufs(c["wsTf"]), writes=bufs(c["wsT"]))
        for half in range(2):
            pb = banks.next()
            S.op("pe", lambda e, pb=pb, half=half: e.matmul(
                pb[:], lhsT=onesf[:], rhs=c["wsTf"][:, half * 4:(half + 1) * 4, :].rearrange("p g i -> p (g i)"),
                start=True, stop=True), reads=bufs(onesf, c["wsTf"]), writes=bufs(pb))
            for gg in range(4):
                g = half * 4 + gg
                for bb in range(2):
                    blk = g * 2 + bb
                    S.op("dve", lambda e, pb=pb, gg=gg, g=g, blk=blk: e.scalar_tensor_tensor(
                        out=c["Bt"][:, blk, :], in0=pb[:, gg * 128:(gg + 1) * 128], scalar=c["lnbT"][:, blk:blk + 1],
                        in1=c["bs_bc"][:, g, :], op0=ALU.mult, op1=ALU.add),
                        reads=bufs(pb, c["lnbT"], c["bs_bc"]), writes=bufs(c["Bt"]))
        c["vv"] = k.at([128, 8, E], BF16)
        c["gtmp"] = k.aring(2, [128, 512], F32)
        c["ug"] = k.aring(2, [128, 512], F32)
        c["zs"] = k.aring(2, [128, 512], F32)
        c["sg"] = k.aring(2, [128, 512], F32)
        c["st"] = k.at([128, 8, 8], F32)
        return c

    def gmlp_unit(c, ntile):
        yT = L["yT"]
        vv = c["vv"]
        stt = c["st"]
        for b in range(4):
            wv = load_w(mlp_w_in, E + b * 512, 512, cache="mlp")
            for t in range(ntile):
                pb = banks.next()
                for kk in range(8):
                    S.op("pe", lambda e, pb=pb, t=t, wv=wv, kk=kk: e.matmul(
                        pb[:], lhsT=hT[:, kk, t * 128:(t + 1) * 128], rhs=wv[:, kk, :], start=(kk == 0), stop=(kk == 7)),
                        reads=bufs(hT, wv), writes=bufs(pb))
                gt = c["gtmp"].next()
                S.op("act", lambda e, pb=pb, b=b, t=t, gt=gt: e.activation(
                    out=gt[:], in_=pb[:], func=AF.Gelu, accum_out=stt[:, t, b:b + 1]),
                    reads=bufs(pb), writes=bufs(gt, stt))
                S.op("act", lambda e, b=b, t=t, gt=gt: e.activation(
                    out=junk[:, 0:512], in_=gt[:], func=AF.Square, accum_out=stt[:, t, 4 + b:5 + b]),
                    reads=bufs(gt), writes=bufs(junk, stt))
                S.op("pool", lambda e, b=b, t=t, gt=gt: e.tensor_copy(out=vv[:, t, b * 512:(b + 1) * 512], in_=gt[:]),
                     reads=bufs(gt), writes=bufs(vv))
        for t in range(ntile):
            st2 = small.next()
            S.op("dve", lambda e, t=t, st2=st2: e.tensor_reduce(
                out=st2[:, 0:2], in_=stt[:, t, :].rearrange("p (a b) -> p a b", a=2), axis=AX.X, op=ALU.add),
                reads=bufs(stt), writes=bufs(st2))
            S.op("dve", lambda e, st2=st2: e.tensor_scalar(out=st2[:, 0:2], in0=st2[:, 0:2], scalar1=1.0 / E, scalar2=None,
                                                           op0=ALU.mult), reads=bufs(st2), writes=bufs(st2))
            S.op("dve", lambda e, st2=st2: e.tensor_tensor(out=st2[:, 2:3], in0=st2[:, 0:1], in1=st2[:, 0:1], op=ALU.mult),
                 reads=bufs(st2), writes=bufs(st2))
            S.op("dve", lambda e, st2=st2: e.scalar_tensor_tensor(out=st2[:, 2:3], in0=st2[:, 2:3], scalar=-1.0, in1=st2[:, 1:2],
                                                                  op0=ALU.mult, op1=ALU.add), reads=bufs(st2), writes=bufs(st2))
            S.op("dve", lambda e, st2=st2: e.tensor_scalar(out=st2[:, 2:3], in0=st2[:, 2:3], scalar1=EPS, scalar2=None,
                                                           op0=ALU.add), reads=bufs(st2), writes=bufs(st2))
            S.op("act", lambda e, st2=st2: e.activation(out=st2[:, 2:3], in_=st2[:, 2:3], func=AF.Sqrt),
                 reads=bufs(st2), writes=bufs(st2))
            S.op("dve", lambda e, st2=st2: e.reciprocal(out=st2[:, 2:3], in_=st2[:, 2:3]), reads=bufs(st2), writes=bufs(st2))
            S.op("dve", lambda e, st2=st2, t=t: e.tensor_scalar(
                out=vv[:, t, :], in0=vv[:, t, :], scalar1=st2[:, 0:1], scalar2=st2[:, 2:3], op0=ALU.subtract, op1=ALU.mult),
                reads=bufs(vv, st2), writes=bufs(vv))
        nq = ntile // 4
        for blk in range(16):
            g = blk // 2
            if blk % 4 == 0:
                wu = load_w(mlp_w_in, blk * 128, 512, cache="mlp")
                wz = load_w(mlp_w_in, 2 * E + blk * 128, 512, cache="mlp")
            co = (blk % 4) * 128
            for q in range(nq):
                ug = c["ug"].next()
                zs = c["zs"].next()
                sg = c["sg"].next()
                pu = banks.next()
                for kk in range(8):
                    S.op("pe", lambda e, pu=pu, kk=kk, q=q, wu=wu, co=co: e.matmul(
                        pu[:], lhsT=wu[:, kk, co:co + 128], rhs=hT[:, kk, q * 512:(q + 1) * 512],
                        start=(kk == 0), stop=(kk == 7)), reads=bufs(wu, hT), writes=bufs(pu))
                S.op("act", lambda e, pu=pu, ug=ug: e.activation(out=ug[:], in_=pu[:], func=AF.Gelu),
                     reads=bufs(pu), writes=bufs(ug))
                pz = banks.next()
                for kk in range(8):
                    S.op("pe", lambda e, pz=pz, kk=kk, q=q, wz=wz, co=co: e.matmul(
                        pz[:], lhsT=wz[:, kk, co:co + 128], rhs=hT[:, kk, q * 512:(q + 1) * 512],
                        start=(kk == 0), stop=(kk == 7)), reads=bufs(wz, hT), writes=bufs(pz))
                S.op("act", lambda e, pz=pz, zs=zs: e.activation(out=zs[:], in_=pz[:], func=AF.Tanh, scale=0.5),
                     reads=bufs(pz), writes=bufs(zs))
                S.op("dve", lambda e, pz=pz, zs=zs: e.scalar_tensor_tensor(
                    out=zs[:], in0=zs[:], scalar=1.0, in1=pz[:], op0=ALU.add, op1=ALU.mult), reads=bufs(pz, zs), writes=bufs(zs))
                ps_ = banks.next()
                for cc in range(4):
                    t = q * 4 + cc
                    S.op("pe", lambda e, ps_=ps_, t=t, cc=cc, blk=blk, g=g: e.matmul(
                        ps_[:, cc * 128:(cc + 1) * 128], lhsT=vv[:, t, blk * 128:(blk + 1) * 128], rhs=c["wsT"][:, g, :],
                        start=True, stop=True), reads=bufs(vv, c["wsT"]), writes=bufs(ps_))
                S.op("dve", lambda e, ps_=ps_, blk=blk, sg=sg: e.scalar_tensor_tensor(
                    out=sg[:].rearrange("p (c i) -> p c i", c=4),
                    in0=ps_[:].rearrange("p (c i) -> p c i", c=4),
                    scalar=c["lngT"][:, blk:blk + 1],
                    in1=c["Bt"][:, blk:blk + 1, :].to_broadcast([128, 4, 128]), op0=ALU.mult, op1=ALU.add),
                    reads=bufs(ps_, c["lngT"], c["Bt"]), writes=bufs(sg))
                S.op("pool", lambda e, sg=sg, ug=ug: e.tensor_tensor(out=sg[:], in0=sg[:], in1=ug[:], op=ALU.mult),
                     reads=bufs(sg, ug), writes=bufs(sg))
                S.op("dve", lambda e, q=q, sg=sg, zs=zs, blk=blk: e.scalar_tensor_tensor(
                    out=yT[:, blk, q * 512:(q + 1) * 512], in0=sg[:], scalar=0.5, in1=zs[:], op0=ALU.mult, op1=ALU.mult),
                    reads=bufs(sg, zs), writes=bufs(yT))

    SCALE = 0.125

    def nat_proj(hp, ntok, c, with_ktm):
        wt = wring.next()
        for j in (0, 1, 3):
            S.dma("pool", wt[:, :, j * 128:(j + 1) * 128],
                  nat_w_in.rearrange("(k p) n -> p k n", p=128)[:, :, j * E + hp * 128:j * E + (hp + 1) * 128],
                  writes=bufs(wt))
        qT, kT, gT = c["qT"], c["kT"], c["gT"]
        for q in range(ntok // 512):
            for j, dst, fn in ((0, qT, AF.Copy), (1, kT, AF.Copy), (3, gT, AF.Silu)):
                pb = banks.next()
                for kk in range(8):
                    S.op("pe", lambda e, pb=pb, kk=kk, q=q, j=j, wt=wt: e.matmul(
                        pb[:], lhsT=wt[:, kk, j * 128:(j + 1) * 128], rhs=hT[:, kk, q * 512:(q + 1) * 512],
                        start=(kk == 0), stop=(kk == 7)), reads=bufs(wt, hT), writes=bufs(pb))
                S.op("act", lambda e, pb=pb, q=q, dst=dst, fn=fn: e.activation(
                    out=dst[:, q * 512:(q + 1) * 512], in_=pb[:], func=fn), reads=bufs(pb), writes=bufs(dst))

    def nat_proj_v4(hp4, ntok, c, with_ktm):
        vb = c["vb"]
        wv = load_w(nat_w_in, 2 * E + hp4 * 512, 512)
        wk = load_w(nat_w_in, E + hp4 * 512, 512) if with_ktm else None
        for t in range(ntok // 128):
            pv_ = banks.next()
            for kk in range(8):
                S.op("pe", lambda e, pv_=pv_, kk=kk, t=t, wv=wv: e.matmul(
                    pv_[:], lhsT=hT[:, kk, t * 128:(t + 1) * 128], rhs=wv[:, kk, :], start=(kk == 0), stop=(kk == 7)),
                    reads=bufs(wv, hT), writes=bufs(pv_))
            if not with_ktm:
                S.op("act", lambda e, pv_=pv_, t=t: e.activation(out=vb[:, t, :], in_=pv_[:], func=AF.Copy),
                     reads=bufs(pv_), writes=bufs(vb))
            else:
                vst, kst = c["vst"], c["kst"]
                S.op("act", lambda e, pv_=pv_, t=t: e.activation(out=vst[:, t, :], in_=pv_[:], func=AF.Copy),
                     reads=bufs(pv_), writes=bufs(vst))
                S.op("pool", lambda e, t=t: e.tensor_copy(out=vb[:, t, :], in_=vst[:, t, :]), reads=bufs(vst), writes=bufs(vb))
                pk_ = banks.next()
                for kk in range(8):
                    S.op("pe", lambda e, pk_=pk_, kk=kk, t=t, wk=wk: e.matmul(
                        pk_[:], lhsT=hT[:, kk, t * 128:(t + 1) * 128], rhs=wk[:, kk, :], start=(kk == 0), stop=(kk == 7)),
                        reads=bufs(wk, hT), writes=bufs(pk_))
                S.op("act", lambda e, pk_=pk_, t=t: e.activation(out=kst[:, t, :], in_=pk_[:], func=AF.Copy),
                     reads=bufs(pk_), writes=bufs(kst))

    def nat_ctx_unit():
        yT = L["yT"]
        c = {"qT": k.at([128, 1024], BF16), "kT": k.at([128, 1024], BF16), "gT": k.at([128, 1024], BF16),
             "vb": k.at([128, 8, 512], BF16), "vst": k.at([128, 8, 512], F32), "kst": k.at([128, 8, 512], F32)}
        er = k.aring(2, [128, 512], F32)
        pbr = k.aring(2, [128, 512], BF16)
        ptr_ = k.aring(2, [128, 512], BF16)
        for hp in range(cfg.get("ctx_hp", 16)):
            if hp % 4 == 0:
                nat_proj_v4(hp // 4, 1024, c, True)
            nat_proj(hp, 1024, c, True)
            for hd in range(2):
                h = hp * 2 + hd
                for sq in range(0 if cfg.get("no_kv") else 4):
                    S.dma("sp", new_k[sq, h, :, :].rearrange("(t p) d -> p t d", p=128),
                          c["kst"][:, sq * 2:(sq + 1) * 2, (hp % 4) * 128 + hd * 64:(hp % 4) * 128 + (hd + 1) * 64], reads=bufs(c["kst"]))
                    S.dma("sp", new_v[sq, h, :, :].rearrange("(t p) d -> p t d", p=128),
                          c["vst"][:, sq * 2:(sq + 1) * 2, (hp % 4) * 128 + hd * 64:(hp % 4) * 128 + (hd + 1) * 64], reads=bufs(c["vst"]))
            qT, kT, gT, vb = c["qT"], c["kT"], c["gT"], c["vb"]
            cb_ = Ring(banks.tiles[2:8])
            pob_ = Ring(banks.tiles[0:2])

            vo = (hp % 4) * 128

            def c_qk(sq, hd):
                rows = slice(hd * 64, (hd + 1) * 64)
                tok0 = sq * 256
                ps_ = cb_.next()
                for qt in range(2):
                    S.op("pe", lambda e, ps_=ps_, qt=qt, rows=rows, tok0=tok0: e.matmul(
                        ps_[:, qt * 256:(qt + 1) * 256], lhsT=qT[rows, tok0 + qt * 128:tok0 + (qt + 1) * 128],
                        rhs=kT[rows, tok0:tok0 + 256], start=True, stop=True), reads=bufs(qT, kT), writes=bufs(ps_))
                return ps_

            def c_softmax(ps_):
                mx = small.next()
                S.op("dve", lambda e, ps_=ps_, mx=mx: e.tensor_reduce(
                    out=mx[:, 0:2], in_=ps_[:].rearrange("p (a b) -> p a b", a=2), axis=AX.X, op=ALU.max),
                    reads=bufs(ps_), writes=bufs(mx))
                S.op("dve", lambda e, mx=mx: e.tensor_scalar(out=mx[:, 2:4], in0=mx[:, 0:2], scalar1=-SCALE, scalar2=None,
                                                             op0=ALU.mult), reads=bufs(mx), writes=bufs(mx))
                et = er.next()
                for qt in range(2):
                    S.op("act", lambda e, ps_=ps_, mx=mx, et=et, qt=qt: e.activation(
                        out=et[:, qt * 256:(qt + 1) * 256], in_=ps_[:, qt * 256:(qt + 1) * 256], func=AF.Exp, scale=SCALE,
                        bias=mx[:, 2 + qt:3 + qt], accum_out=mx[:, 4 + qt:5 + qt]), reads=bufs(ps_, mx), writes=bufs(et, mx))
                S.op("dve", lambda e, mx=mx: e.reciprocal(out=mx[:, 6:8], in_=mx[:, 4:6]), reads=bufs(mx), writes=bufs(mx))
                pbt = pbr.next()
                S.op("dve", lambda e, mx=mx, et=et, pbt=pbt: e.tensor_tensor(
                    out=pbt[:].rearrange("p (a b) -> p a b", a=2), in0=et[:].rearrange("p (a b) -> p a b", a=2),
                    in1=mx[:, 6:8].unsqueeze(2).to_broadcast([128, 2, 256]), op=ALU.mult),
                    reads=bufs(mx, et), writes=bufs(pbt))
                return pbt

            def c_tpv(sq, hd, pbt, po):
                rows = slice(hd * 64, (hd + 1) * 64)
                ptb = cb_.next()
                ptv = ptb[:].bitcast(BF16)
                for j in range(4):
                    S.op("pe", lambda e, ptv=ptv, pbt=pbt, j=j: e.transpose(
                        out=ptv[:, j * 128:(j + 1) * 128], in_=pbt[:, j * 128:(j + 1) * 128], identity=identb[:]),
                        reads=bufs(pbt, identb), writes=bufs(ptb))
                pts = ptr_.next()
                S.op("act", lambda e, ptv=ptv, pts=pts: e.activation(out=pts[:], in_=ptv[:, 0:512], func=AF.Copy),
                     reads=bufs(ptb), writes=bufs(pts))
                for qt in range(2):
                    for kb in range(2):
                        S.op("pe", lambda e, po=po, rows=rows, qt=qt, kb=kb, sq=sq, hd=hd, pts=pts, vo=vo: e.matmul(
                            po[rows, qt * 128:(qt + 1) * 128], lhsT=vb[:, sq * 2 + kb, vo + hd * 64:vo + (hd + 1) * 64],
                            rhs=pts[:, (qt * 2 + kb) * 128:(qt * 2 + kb + 1) * 128], start=(kb == 0), stop=(kb == 1)),
                            reads=bufs(vb, pts), writes=bufs(po))

            its = [(sq, hd) for sq in range(0 if cfg.get("ctx_stage", 9) < 1 else 4) for hd in range(2)]
            nxt = c_qk(*its[0]) if its else None
            po = None
            for ii, (sq, hd) in enumerate(its):
                if hd == 0:
                    po = pob_.next()
                pbt = c_softmax(nxt)
                if ii + 1 < len(its):
                    nxt = c_qk(*its[ii + 1])
                c_tpv(sq, hd, pbt, po)
                if hd == 1:
                    tok0 = sq * 256
                    S.op("dve", lambda e, po=po, hp=hp, tok0=tok0: e.tensor_tensor(
                        out=yT[:, hp, tok0:tok0 + 256], in0=po[:, 0:256], in1=gT[:, tok0:tok0 + 256], op=ALU.mult),
                        reads=bufs(po, gT), writes=bufs(yT))

    def nat_lat_unit():
        yT = L["yT"]
        c = {"qT": k.at([128, 2048], BF16), "kT": k.at([128, 2048], BF16), "gT": k.at([128, 2048], BF16),
             "vb": k.at([128, 16, 512], BF16)}
        maskf = k.at([128, 3, 576], F32)
        maskb = k.at([128, 3, 576], BF16)
        for j in range(3):
            S.dma("sp", maskf[:, j, :], natmask[j], writes=bufs(maskf))
        S.op("dve", lambda e: e.tensor_copy(out=maskb[:], in_=maskf[:]), reads=bufs(maskf), writes=bufs(maskb))
        ckr = k.aring(2, [128, 2, 2, 64], BF16)
        cvr = k.aring(2, [128, 2, 2, 64], BF16)
        cktr = k.aring(2, [128, 256], BF16)
        rpr = k.aring(2, [128, 1024], F32)
        scr = k.aring(2, [128, 832], F32)
        pbr = k.aring(2, [128, 832], BF16)
        ptr_ = k.aring(2, [128, 896], BF16)
        cfg["alog"] = k.alog
        pobanks = Ring(banks.tiles[0:2])
        wbanks = Ring(banks.tiles[2:8])
        for hp in range(cfg.get("nat_hp", 16)):
            if hp % 4 == 0:
                nat_proj_v4(hp // 4, 2048, c, False)
            nat_proj(hp, 2048, c, False)
            vo = (hp % 4) * 128
            qT, kT, gT, vb = c["qT"], c["kT"], c["gT"], c["vb"]
            ck = ckr.next()
            cv = cvr.next()
            for hd in range(2):
                S.dma("pool", ck[:, :, hd, :], cache_k[hp * 2 + hd].rearrange("(kb p) d -> p kb d", p=128), writes=bufs(ck))
                S.dma("pool", cv[:, :, hd, :], cache_v[hp * 2 + hd].rearrange("(kb p) d -> p kb d", p=128), writes=bufs(cv))
            ckT = cktr.next()
            ptb = banks.next()
            ptv = ptb[:].bitcast(BF16)
            for kb in range(2):
                S.op("pe", lambda e, ptv=ptv, ck=ck, kb=kb: e.transpose(
                    out=ptv[:, kb * 128:(kb + 1) * 128], in_=ck[:, kb, :, :].rearrange("p a b -> p (a b)"), identity=identb[:]),
                    reads=bufs(ck, identb), writes=bufs(ptb))
            S.op("act", lambda e, ptv=ptv, ckT=ckT: e.activation(out=ckT[:], in_=ptv[:, 0:256], func=AF.Copy),
                 reads=bufs(ptb), writes=bufs(ckT))
            rps = []
            for hd in range(2):
                rp = rpr.next()
                S.dma("sp", rp[:], rpbg[hp * 2 + hd], writes=bufs(rp))
                rps.append(rp)
            items = []
            for pg in range(cfg.get("nat_pg", 4)):
                for hd in range(cfg.get("nat_hd", 2)):
                    for pi in range(cfg.get("nat_pi", 4)):
                        items.append((pg, hd, pi))

            def geom(pg, hd, pi):
                pr = pg * 4 + pi
                r = 2 * pr
                if pr <= 1:
                    r0, nrow, a0, mi = 0, 9, 7 - r, 1
                elif pr >= 14:
                    r0, nrow, a0, mi = 24, 8, (3 if pr == 14 else 1), 2
                else:
                    r0, nrow, a0, mi = r - 4, 9, 3, 0
                return r, r0, nrow, a0, mi

            def st_qk(it):
                pg, hd, pi = it
                r, r0, nrow, a0, mi = geom(*it)
                rows = slice(hd * 64, (hd + 1) * 64)
                q0, k0 = r * 64, r0 * 64
                ps1 = wbanks.next()
                ps2 = wbanks.next()
                S.op("pe", lambda e, ps1=ps1, rows=rows, q0=q0, k0=k0: e.matmul(
                    ps1[:], lhsT=qT[rows, q0:q0 + 128], rhs=kT[rows, k0:k0 + 512], start=True, stop=False),
                    reads=bufs(qT, kT), writes=bufs(ps1))
                S.op("pe", lambda e, ps1=ps1, mi=mi: e.matmul(
                    ps1[:], lhsT=identb[:], rhs=maskb[:, mi, 0:512], start=False, stop=True),
                    reads=bufs(identb, maskb), writes=bufs(ps1))
                if nrow == 9:
                    S.op("pe", lambda e, ps2=ps2, rows=rows, q0=q0, k0=k0: e.matmul(
                        ps2[:, 0:64], lhsT=qT[rows, q0:q0 + 128], rhs=kT[rows, k0 + 512:k0 + 576], start=True, stop=False),
                        reads=bufs(qT, kT), writes=bufs(ps2))
                    S.op("pe", lambda e, ps2=ps2, mi=mi: e.matmul(
                        ps2[:, 0:64], lhsT=identb[:], rhs=maskb[:, mi, 512:576], start=False, stop=True),
                        reads=bufs(identb, maskb), writes=bufs(ps2))
                S.op("pe", lambda e, ps2=ps2, rows=rows, q0=q0, ckT=ckT: e.matmul(
                    ps2[:, 64:320], lhsT=qT[rows, q0:q0 + 128], rhs=ckT[rows, :], start=True, stop=True),
                    reads=bufs(qT, ckT), writes=bufs(ps2))
                return ps1, ps2

            def st_softmax(it, ps1, ps2):
                pg, hd, pi = it
                r, r0, nrow, a0, mi = geom(*it)
                rp = rps[hd]
                nk = nrow * 64
                sc = scr.next()
                S.op("dve", lambda e, ps1=ps1, sc=sc, rp=rp, a0=a0: e.scalar_tensor_tensor(
                    out=sc[:, 0:512], in0=ps1[:], scalar=SCALE, in1=rp[:, a0 * 64:a0 * 64 + 512],
                    op0=ALU.mult, op1=ALU.add), reads=bufs(ps1, rp), writes=bufs(sc))
                if nrow == 9:
                    S.op("dve", lambda e, ps2=ps2, sc=sc, rp=rp, a0=a0: e.scalar_tensor_tensor(
                        out=sc[:, 512:576], in0=ps2[:, 0:64], scalar=SCALE, in1=rp[:, a0 * 64 + 512:a0 * 64 + 576],
                        op0=ALU.mult, op1=ALU.add), reads=bufs(ps2, rp), writes=bufs(sc))
                S.op("act", lambda e, ps2=ps2, sc=sc, nk=nk: e.activation(
                    out=sc[:, nk:nk + 256], in_=ps2[:, 64:320], func=AF.Copy, scale=SCALE),
                    reads=bufs(ps2), writes=bufs(sc))
                ntot = nk + 256
                mx = small.next()
                S.op("dve", lambda e, sc=sc, mx=mx, ntot=ntot: e.tensor_reduce(
                    out=mx[:, 0:1], in_=sc[:, 0:ntot], axis=AX.X, op=ALU.max), reads=bufs(sc), writes=bufs(mx))
                S.op("dve", lambda e, mx=mx: e.tensor_scalar(out=mx[:, 1:2], in0=mx[:, 0:1], scalar1=-1.0, scalar2=None,
                                                             op0=ALU.mult), reads=bufs(mx), writes=bufs(mx))
                S.op("act", lambda e, sc=sc, mx=mx, ntot=ntot: e.activation(
                    out=sc[:, 0:ntot], in_=sc[:, 0:ntot], func=AF.Exp, bias=mx[:, 1:2], accum_out=mx[:, 2:3]),
                    reads=bufs(sc, mx), writes=bufs(sc, mx))
                S.op("dve", lambda e, mx=mx: e.reciprocal(out=mx[:, 3:4], in_=mx[:, 2:3]), reads=bufs(mx), writes=bufs(mx))
                pbt = pbr.next()
                S.op("dve", lambda e, sc=sc, mx=mx, pbt=pbt, ntot=ntot: e.tensor_scalar(
                    out=pbt[:, 0:ntot], in0=sc[:, 0:ntot], scalar1=mx[:, 3:4], scalar2=None, op0=ALU.mult),
                    reads=bufs(sc, mx), writes=bufs(pbt))
                return pbt

            def st_tpv(it, pbt, po):
                pg, hd, pi = it
                r, r0, nrow, a0, mi = geom(*it)
                rows = slice(hd * 64, (hd + 1) * 64)
                nk = nrow * 64
                ptb = wbanks.next()
                ptv = ptb[:].bitcast(BF16)
                blocks = [(j * 128, 128) for j in range(4)]
                blocks += [(nk, 128), (nk + 128, 128)]
                if nrow == 9:
                    blocks.append((512, 64))
                for j, (c0, w) in enumerate(blocks):
                    S.op("pe", lambda e, ptv=ptv, pbt=pbt, j=j, c0=c0, w=w: e.transpose(
                        out=ptv[0:w, j * 128:(j + 1) * 128], in_=pbt[:, c0:c0 + w], identity=identb[:]),
                        reads=bufs(pbt, identb), writes=bufs(ptb))
                nb = len(blocks)
                pts = ptr_.next()
                S.op("act", lambda e, ptv=ptv, pts=pts: e.activation(
                    out=pts[:, 0:768], in_=ptv[:, 0:768], func=AF.Copy), reads=bufs(ptb), writes=bufs(pts))
                if nrow == 9:
                    S.op("act", lambda e, ptv=ptv, pts=pts: e.activation(
                        out=pts[0:64, 768:896], in_=ptv[0:64, 768:896], func=AF.Copy), reads=bufs(ptb), writes=bufs(pts))
                t0 = r0 // 2
                for j, (c0, w) in enumerate(blocks):
                    if j < 4:
                        lhs = vb[:, t0 + j, vo + hd * 64:vo + (hd + 1) * 64]
                        rhs = pts[:, j * 128:(j + 1) * 128]
                        rd = bufs(vb, pts)
                    elif w == 64:
                        lhs = vb[0:64, t0 + 4, vo + hd * 64:vo + (hd + 1) * 64]
                        rhs = pts[0:64, j * 128:(j + 1) * 128]
                        rd = bufs(vb, pts)
                    else:
                        kb = j - 4
                        lhs = cv[:, kb, hd, :]
                        rhs = pts[:, j * 128:(j + 1) * 128]
                        rd = bufs(cv, pts)
                    S.op("pe", lambda e, po=po, rows=rows, pi=pi, lhs=lhs, rhs=rhs, j=j, nb=nb: e.matmul(
                        po[rows, pi * 128:(pi + 1) * 128], lhsT=lhs, rhs=rhs, start=(j == 0), stop=(j == nb - 1)),
                        reads=rd, writes=bufs(po))

            pos_ = {}
            n_it = len(items)
            qk_res = {}
            sm_res = {}
            for j_ in range(min(2, n_it)):
                qk_res[j_] = st_qk(items[j_])
            if n_it:
                sm_res[0] = st_softmax(items[0], *qk_res.pop(0))
            for ii, it in enumerate(items):
                pg = it[0]
                if pg not in pos_:
                    pos_[pg] = pobanks.next()
                po = pos_[pg]
                if ii + 2 < n_it:
                    qk_res[ii + 2] = st_qk(items[ii + 2])
                if ii + 1 < n_it:
                    sm_res[ii + 1] = st_softmax(items[ii + 1], *qk_res.pop(ii + 1))
                pbt = sm_res.pop(ii)
                st_tpv(it, pbt, po)
                if ii + 1 == len(items) or items[ii + 1][0] != pg:
                    S.op("dve", lambda e, po=po, hp=hp, pg=pg: e.tensor_tensor(
                        out=yT[:, hp, pg * 512:(pg + 1) * 512], in0=po[:], in1=gT[:, pg * 512:(pg + 1) * 512], op=ALU.mult),
                        reads=bufs(po, gT), writes=bufs(yT))

    def ssd_consts():
        c = {}
        c["tri"] = [k.at([128, 128], F32), k.at([128, 128], F32)]
        c["mneg"] = [k.at([128, 128], F32), k.at([128, 128], F32)]
        c["negones"] = k.at([128, 128], F32)
        for d_ in range(2):
            sgn = 1 if d_ == 0 else -1
            S.op("pool", lambda e, d_=d_: e.memset(c["tri"][d_][:], 1.0), writes=bufs(c["tri"][d_]))
            S.op("pool", lambda e, d_=d_, sgn=sgn: e.affine_select(
                out=c["tri"][d_][:], in_=c["tri"][d_][:], compare_op=ALU.is_ge, fill=0.0, base=0,
                pattern=[[sgn, 128]], channel_multiplier=-sgn), reads=bufs(c["tri"][d_]), writes=bufs(c["tri"][d_]))
            S.op("pool", lambda e, d_=d_: e.memset(c["mneg"][d_][:], 0.0), writes=bufs(c["mneg"][d_]))
            S.op("pool", lambda e, d_=d_, sgn=sgn: e.affine_select(
                out=c["mneg"][d_][:], in_=c["mneg"][d_][:], compare_op=ALU.is_ge, fill=-30000.0, base=0,
                pattern=[[sgn, 128]], channel_multiplier=-sgn), reads=bufs(c["mneg"][d_]), writes=bufs(c["mneg"][d_]))
        S.op("pool", lambda e: e.memset(c["negones"][:], -1.0), writes=bufs(c["negones"]))
        c["ntri"] = [k.at([128, 128], F32), k.at([128, 128], F32)]
        for d_ in range(2):
            S.op("pool", lambda e, d_=d_: e.tensor_scalar(out=c["ntri"][d_][:], in0=c["tri"][d_][:], scalar1=-1.0, scalar2=None,
                                                          op0=ALU.mult), reads=bufs(c["tri"][d_]), writes=bufs(c["ntri"][d_]))
        c["cwT"] = k.at([128, 32, 5], F32)
        c["cbT"] = k.at([128, 32], F32)
        for j in range(5):
            S.dma("sp", c["cwT"][:, :, j], ssd_conv_w[j].rearrange("(b p) -> p b", p=128), writes=bufs(c["cwT"]))
        S.dma("sp", c["cbT"][:], ssd_conv_b.rearrange("(b p) -> p b", p=128), writes=bufs(c["cbT"]))
        c["dtb"] = k.at([128, 64], F32)
        c["abc"] = k.at([128, 64], F32)
        c["dsk"] = k.at([128, 32], F32)
        c["ngT"] = k.at([128, 16], F32)
        S.dma("sp", c["dtb"][:], ssd_dt_bias.partition_broadcast(128), writes=bufs(c["dtb"]))
        S.dma("sp", c["abc"][:], ssd_a_log.partition_broadcast(128), writes=bufs(c["abc"]))
        S.dma("sp", c["dsk"][:], ssd_d.partition_broadcast(128), writes=bufs(c["dsk"]))
        S.dma("sp", c["ngT"][:], ssd_norm_g.rearrange("(b p) -> p b", p=128), writes=bufs(c["ngT"]))
        S.op("act", lambda e: e.activation(out=c["abc"][:], in_=c["abc"][:], func=AF.Exp), reads=bufs(c["abc"]), writes=bufs(c["abc"]))
        S.op("dve", lambda e: e.tensor_scalar(out=c["abc"][:], in0=c["abc"][:], scalar1=-1.0, scalar2=None, op0=ALU.mult),
             reads=bufs(c["abc"]), writes=bufs(c["abc"]))
        return c

    def ssd_unit(c, tok0, ntile, nseq, is_lat):
        T_ = ntile * 128
        nch = ntile // nseq
        Lq = nch * 128
        dt_ = k.at([128, ntile, 64], F32)
        da = k.at([128, ntile, 64], F32)
        ecum = k.at([128, ntile, 64], F32)
        dtd = k.at([128, ntile, 64], F32)
        etot = k.at([128, ntile, 64], F32)
        ssq = k.at([128, ntile, 8], F32)
        rstd = k.at([128, ntile], F32)
        tmpr = k.aring(2, [128, 64], F32)
        wdt = wring.next()
        S.dma("pool", wdt[:, :, 0:64], ssd_w_in.rearrange("(k p) n -> p k n", p=128)[:, :, 6144:6208], writes=bufs(wdt))
        for t in range(ntile):
            pb = banks.next()
            for kk in range(8):
                S.op("pe", lambda e, pb=pb, kk=kk, t=t: e.matmul(
                    pb[:, 0:64], lhsT=hT[:, kk, t * 128:(t + 1) * 128], rhs=wdt[:, kk, 0:64], start=(kk == 0), stop=(kk == 7)),
                    reads=bufs(hT, wdt), writes=bufs(pb))
            S.op("dve", lambda e, pb=pb, t=t: e.tensor_tensor(out=dt_[:, t, :], in0=pb[:, 0:64], in1=c["dtb"][:], op=ALU.add),
                 reads=bufs(pb, c["dtb"]), writes=bufs(dt_))
        S.op("act", lambda e: e.activation(out=dt_[:], in_=dt_[:], func=AF.Exp), reads=bufs(dt_), writes=bufs(dt_))
        S.op("act", lambda e: e.activation(out=dt_[:], in_=dt_[:], func=AF.Ln, bias=1.0), reads=bufs(dt_), writes=bufs(dt_))
        S.op("dve", lambda e: e.tensor_tensor(out=da[:], in0=dt_[:], in1=c["abc"][:].unsqueeze(1).to_broadcast([128, ntile, 64]),
                                              op=ALU.mult), reads=bufs(dt_, c["abc"]), writes=bufs(da))
        for t in range(ntile):
            pc = banks.next()
            S.op("pe", lambda e, pc=pc, t=t: e.matmul(pc[:, 0:32], lhsT=c["tri"][0][:], rhs=da[:, t, 0:32], start=True, stop=True),
                 reads=bufs(c["tri"][0], da), writes=bufs(pc))
            S.op("pe", lambda e, pc=pc, t=t: e.matmul(pc[:, 32:64], lhsT=c["tri"][1][:], rhs=da[:, t, 32:64], start=True, stop=True),
                 reads=bufs(c["tri"][1], da), writes=bufs(pc))
            S.op("pe", lambda e, pc=pc, t=t: e.matmul(pc[:, 64:128], lhsT=onesf[:], rhs=da[:, t, :], start=True, stop=True),
                 reads=bufs(onesf, da), writes=bufs(pc))
            cumt = tmpr.next()
            S.op("act", lambda e, pc=pc, cumt=cumt: e.activation(out=cumt[:], in_=pc[:, 0:64], func=AF.Identity),
                 reads=bufs(pc), writes=bufs(cumt))
            S.op("act", lambda e, pc=pc, t=t: e.activation(out=ecum[:, t, :], in_=pc[:, 0:64], func=AF.Exp),
                 reads=bufs(pc), writes=bufs(ecum))
            S.op("act", lambda e, pc=pc, t=t: e.activation(out=etot[:, t, :], in_=pc[:, 64:128], func=AF.Exp),
                 reads=bufs(pc), writes=bufs(etot))
            S.op("dve", lambda e, pc=pc, cumt=cumt: e.tensor_tensor(out=cumt[:], in0=pc[:, 64:128], in1=cumt[:], op=ALU.subtract),
                 reads=bufs(pc, cumt), writes=bufs(cumt))
            S.op("act", lambda e, cumt=cumt: e.activation(out=cumt[:], in_=cumt[:], func=AF.Exp), reads=bufs(cumt), writes=bufs(cumt))
            S.op("dve", lambda e, cumt=cumt, t=t: e.tensor_tensor(out=dtd[:, t, :], in0=dt_[:, t, :], in1=cumt[:], op=ALU.mult),
                 reads=bufs(cumt, dt_), writes=bufs(dtd))
        raw = k.at([128, T_], F32)
        acc = k.at([128, T_], F32)
        fm = [k.at([128, T_], BF16) for _ in range(4)]
        XB = k.at([128, ntile, 384], BF16)
        SIN = k.at([128, ntile, 2, 256], BF16)
        stfs = [k.at([128, 256], F32), k.at([128, 256], F32)]
        vTg = k.at([128, 2, T_], BF16)
        h0r = k.aring(2, [128, 2, 128], F32)
        fir = k.aring(2, [128, 2, 128], F32)
        GTr = k.aring(2, [128, 128], F32)
        Dr = k.aring(2, [128, 4, 128], F32)
        Lr = k.aring(2, [128, 4, 128], F32)
        Mr = k.aring(4, [128, 4, 128], BF16)
        xdr = k.aring(5, [128, 4, 64], BF16)
        yr = k.aring(4, [128, 256], F32)
        szr = k.aring(2, [128, 256], F32)
        vbr = k.aring(2, [128, 256], BF16)
        tmp4 = k.aring(2, [128, 4, 64], F32)
        for g in range(cfg.get("ssd_g", 8)):
            wA = wring.next()
            wap = ssd_w_in.rearrange("(k p) n -> p k n", p=128)
            S.dma("pool", wA[:, :, 0:256], wap[:, :, g * 256:(g + 1) * 256], writes=bufs(wA))
            S.dma("pool", wA[:, :, 256:512], wap[:, :, E + g * 256:E + (g + 1) * 256], writes=bufs(wA))
            wB = wring.next()
            S.dma("pool", wB[:, :, 0:128], wap[:, :, 2 * E + g * 128:2 * E + (g + 1) * 128], writes=bufs(wB))
            S.dma("pool", wB[:, :, 128:256], wap[:, :, 2 * E + 1024 + g * 128:2 * E + 1024 + (g + 1) * 128], writes=bufs(wB))
            for bi in range(4):
                wt, co, cblk = ((wA, 256, 2 * g), (wA, 384, 2 * g + 1), (wB, 0, 16 + g), (wB, 128, 24 + g))[bi]
                for q in range(T_ // 512):
                    pb = banks.next()
                    for kk in range(8):
                        S.op("pe", lambda e, pb=pb, kk=kk, q=q, wt=wt, co=co: e.matmul(
                            pb[:], lhsT=wt[:, kk, co:co + 128], rhs=hT[:, kk, q * 512:(q + 1) * 512],
                            start=(kk == 0), stop=(kk == 7)), reads=bufs(wt, hT), writes=bufs(pb))
                    S.op("act", lambda e, pb=pb, q=q: e.activation(out=raw[:, q * 512:(q + 1) * 512], in_=pb[:], func=AF.Copy),
                         reads=bufs(pb), writes=bufs(raw))
                cw = c["cwT"]
                rv = raw[:].rearrange("p (s l) -> p s l", s=nseq)
                av = acc[:].rearrange("p (s l) -> p s l", s=nseq)
                S.op("dve", lambda e, cblk=cblk: e.tensor_scalar(out=acc[:], in0=raw[:], scalar1=cw[:, cblk, 2:3], scalar2=None,
                                                                 op0=ALU.mult), reads=bufs(raw, cw), writes=bufs(acc))
                taps = ((0, "dve", slice(2, Lq), slice(0, Lq - 2)), (1, "dve", slice(1, Lq), slice(0, Lq - 1)),
                        (3, "dve", slice(0, Lq - 1), slice(1, Lq)), (4, "dve", slice(0, Lq - 2), slice(2, Lq)))
                for j, eng, osl, isl in taps:
                    S.op(eng, lambda e, j=j, osl=osl, isl=isl, cblk=cblk, rv=rv, av=av: e.scalar_tensor_tensor(
                        out=av[:, :, osl], in0=rv[:, :, isl], scalar=cw[:, cblk, j:j + 1], in1=av[:, :, osl],
                        op0=ALU.mult, op1=ALU.add), reads=bufs(raw, acc, cw), writes=bufs(acc))
                S.op("act", lambda e, bi=bi, cblk=cblk: e.activation(out=fm[bi][:], in_=acc[:], func=AF.Silu,
                                                                      bias=c["cbT"][:, cblk:cblk + 1]),
                     reads=bufs(acc, c["cbT"]), writes=bufs(fm[bi]))
            for t in range(ntile):
                ptb = banks.next()
                ptv = ptb[:].bitcast(BF16)
                for bi in range(3):
                    S.op("pe", lambda e, ptv=ptv, bi=bi, t=t: e.transpose(
                        out=ptv[:, bi * 128:(bi + 1) * 128], in_=fm[bi][:, t * 128:(t + 1) * 128], identity=identb[:]),
                        reads=bufs(fm[bi], identb), writes=bufs(ptb))
                S.op("act", lambda e, ptv=ptv, t=t: e.activation(out=XB[:, t, :], in_=ptv[:, 0:384], func=AF.Copy),
                     reads=bufs(ptb), writes=bufs(XB))
            BT, CT = fm[2], fm[3]
            for sq in range(nseq):
                for d_ in range(2):
                    if is_lat:
                        h0 = h0r.next()
                        for half in range(2):
                            S.dma("sp", h0[:, half, :], state_ssd[d_, 4 * g + 2 * half:4 * g + 2 * half + 2].rearrange("h p n -> (h p) n"),
                                  writes=bufs(h0))
                        ph = banks.next()
                        for half in range(2):
                            S.op("pe", lambda e, ph=ph, h0=h0, half=half: e.transpose(
                                out=ph[:, half * 128:(half + 1) * 128], in_=h0[:, half, :], identity=identf[:]),
                                reads=bufs(h0, identf), writes=bufs(ph))
                        S.op("act", lambda e, ph=ph, d_=d_: e.activation(out=stfs[d_][:], in_=ph[:, 0:256], func=AF.Copy),
                             reads=bufs(ph), writes=bufs(stfs[d_]))
                    else:
                        S.op("pool", lambda e, d_=d_: e.memset(stfs[d_][:], 0.0), writes=bufs(stfs[d_]))
                for step in range(nch):
                    for d_ in range(2):
                        hs = slice(d_ * 32 + 4 * g, d_ * 32 + 4 * g + 4)
                        ci = step if d_ == 0 else nch - 1 - step
                        t = sq * nch + ci
                        st_ = stfs[d_]
                        S.op("act", lambda e, t=t, d_=d_, st_=st_: e.activation(out=SIN[:, t, d_, :], in_=st_[:], func=AF.Copy),
                             reads=bufs(st_), writes=bufs(SIN))
                        xdd = xdr.next()
                        S.op("pool", lambda e, xdd=xdd, t=t, hs=hs: e.tensor_tensor(
                            out=xdd[:], in0=XB[:, t, 0:256].rearrange("p (h d) -> p h d", h=4),
                            in1=dtd[:, t, hs].unsqueeze(2).to_broadcast([128, 4, 64]), op=ALU.mult),
                            reads=bufs(XB, dtd), writes=bufs(xdd))
                        psl = banks.next()
                        S.op("pe", lambda e, psl=psl, t=t, xdd=xdd: e.matmul(
                            psl[:, 0:256], lhsT=XB[:, t, 256:384], rhs=xdd[:].rearrange("p h d -> p (h d)"), start=True, stop=True),
                            reads=bufs(XB, xdd), writes=bufs(psl))
                        S.op("pool", lambda e, t=t, st_=st_, hs=hs: e.tensor_tensor(
                            out=st_[:].rearrange("p (h d) -> p h d", h=4), in0=st_[:].rearrange("p (h d) -> p h d", h=4),
                            in1=etot[:, t, hs].unsqueeze(2).to_broadcast([128, 4, 64]), op=ALU.mult),
                            reads=bufs(st_, etot), writes=bufs(st_))
                        S.op("dve", lambda e, psl=psl, st_=st_: e.tensor_tensor(out=st_[:], in0=psl[:, 0:256], in1=st_[:],
                                                                              op=ALU.add), reads=bufs(psl, st_), writes=bufs(st_))
                if not is_lat:
                    for d_ in range(2):
                        st_ = stfs[d_]
                        pf = banks.next()
                        for half in range(2):
                            S.op("pe", lambda e, pf=pf, st_=st_, half=half: e.transpose(
                                out=pf[:, half * 128:(half + 1) * 128], in_=st_[:, half * 128:(half + 1) * 128], identity=identf[:]),
                                reads=bufs(st_, identf), writes=bufs(pf))
                        fi = fir.next()
                        S.op("act", lambda e, pf=pf, fi=fi: e.activation(out=fi[:].rearrange("p a b -> p (a b)"), in_=pf[:, 0:256],
                                                                        func=AF.Copy), reads=bufs(pf), writes=bufs(fi))
                        for half in range(2):
                            S.dma("sp", new_ssd[sq, d_, 4 * g + 2 * half:4 * g + 2 * half + 2].rearrange("h p n -> (h p) n"),
                                  fi[:, half, :], reads=bufs(fi))
            def front(t):
                tsl = slice(t * 128, (t + 1) * 128)
                pgz = banks.next()
                S.op("pe", lambda e, pgz=pgz, tsl=tsl: e.matmul(pgz[:, 0:128], lhsT=BT[:, tsl], rhs=CT[:, tsl], start=True, stop=True),
                     reads=bufs(BT, CT), writes=bufs(pgz))
                for kk in range(8):
                    S.op("pe", lambda e, pgz=pgz, kk=kk, tsl=tsl, wA=wA: e.matmul(
                        pgz[:, 128:384], lhsT=hT[:, kk, tsl], rhs=wA[:, kk, 0:256], start=(kk == 0), stop=(kk == 7)),
                        reads=bufs(hT, wA), writes=bufs(pgz))
                GT = GTr.next()
                S.op("act", lambda e, pgz=pgz, GT=GT: e.activation(out=GT[:], in_=pgz[:, 0:128], func=AF.Copy),
                     reads=bufs(pgz), writes=bufs(GT))
                sz = szr.next()
                S.op("act", lambda e, pgz=pgz, sz=sz: e.activation(out=sz[:], in_=pgz[:, 128:384], func=AF.Tanh, scale=0.5),
                     reads=bufs(pgz), writes=bufs(sz))
                S.op("dve", lambda e, pgz=pgz, sz=sz: e.scalar_tensor_tensor(
                    out=sz[:], in0=sz[:], scalar=1.0, in1=pgz[:, 128:384], op0=ALU.add, op1=ALU.mult),
                    reads=bufs(pgz, sz), writes=bufs(sz))
                hss = [slice(d_ * 32 + 4 * g, d_ * 32 + 4 * g + 4) for d_ in range(2)]
                Dts = []
                for d_ in range(2):
                    Dt = Dr.next()
                    S.op("pool", lambda e, Dt=Dt, d_=d_, t=t, hs=hss[d_]: e.tensor_tensor(
                        out=Dt[:], in0=c["tri"][d_][:].unsqueeze(1).to_broadcast([128, 4, 128]),
                        in1=da[:, t, hs].unsqueeze(2).to_broadcast([128, 4, 128]), op=ALU.mult),
                        reads=bufs(c["tri"][d_], da), writes=bufs(Dt))
                    Dts.append(Dt)
                pzs = []
                for d_ in range(2):
                    pz_ = banks.next()
                    Dt = Dts[d_]
                    S.op("pe", lambda e, pz_=pz_, Dt=Dt: e.matmul(
                        pz_[:], lhsT=onesf[:], rhs=Dt[:].rearrange("p h t -> p (h t)"), start=True, stop=False),
                        reads=bufs(onesf, Dt), writes=bufs(pz_))
                    S.op("pe", lambda e, pz_=pz_, d_=d_, t=t, hs=hss[d_]: e.matmul(
                        pz_[:], lhsT=c["ntri"][d_][:], rhs=da[:, t, hs].unsqueeze(2).to_broadcast([128, 4, 128]), start=False, stop=False),
                        reads=bufs(c["ntri"][d_], da), writes=bufs(pz_))
                    S.op("pe", lambda e, pz_=pz_, d_=d_: e.matmul(
                        pz_[:], lhsT=identf[:], rhs=c["mneg"][d_][:].unsqueeze(1).to_broadcast([128, 4, 128]), start=False, stop=True),
                        reads=bufs(identf, c["mneg"][d_]), writes=bufs(pz_))
                    pzs.append(pz_)
                poo = banks.next()
                for d_ in range(2):
                    S.op("pe", lambda e, poo=poo, tsl=tsl, t=t, d_=d_: e.matmul(
                        poo[:, d_ * 256:(d_ + 1) * 256], lhsT=CT[:, tsl], rhs=SIN[:, t, d_, :], start=True, stop=True),
                        reads=bufs(CT, SIN), writes=bufs(poo))
                Lts = []
                for d_ in range(2):
                    Lt = Lr.next()
                    S.op("act", lambda e, pz_=pzs[d_], Lt=Lt: e.activation(out=Lt[:].rearrange("p h t -> p (h t)"), in_=pz_[:], func=AF.Exp),
                         reads=bufs(pzs[d_]), writes=bufs(Lt))
                    Lts.append(Lt)
                Mts, xds = [], []
                for d_ in range(2):
                    Mt = Mr.next()
                    S.op("dve", lambda e, Lt=Lts[d_], Mt=Mt, GT=GT: e.tensor_tensor(
                        out=Mt[:], in0=Lt[:], in1=GT[:].unsqueeze(1).to_broadcast([128, 4, 128]), op=ALU.mult),
                        reads=bufs(Lts[d_], GT), writes=bufs(Mt))
                    xd = xdr.next()
                    S.op("pool", lambda e, xd=xd, t=t, hs=hss[d_]: e.tensor_tensor(
                        out=xd[:], in0=XB[:, t, 0:256].rearrange("p (h d) -> p h d", h=4),
                        in1=dt_[:, t, hs].unsqueeze(2).to_broadcast([128, 4, 64]), op=ALU.mult),
                        reads=bufs(XB, dt_), writes=bufs(xd))
                    Mts.append(Mt)
                    xds.append(xd)
                return dict(t=t, tsl=tsl, sz=sz, Mts=Mts, xds=xds, poo=poo, hss=hss)

            def back(f):
                t, tsl, sz, poo, hss = f["t"], f["tsl"], f["sz"], f["poo"], f["hss"]
                py = banks.next()
                for d_ in range(2):
                    Mt, xd = f["Mts"][d_], f["xds"][d_]
                    for h in range(4):
                        S.op("pe", lambda e, py=py, Mt=Mt, xd=xd, h=h, d_=d_: e.matmul(
                            py[:, h * 64:(h + 1) * 64], lhsT=Mt[:, h, :], rhs=xd[:, h, :], start=(d_ == 0 and h == 0), stop=(d_ == 1 and h == 3)),
                            reads=bufs(Mt, xd), writes=bufs(py))
                y1 = yr.next()
                y2 = yr.next()
                for d_, yy in ((0, y1), (1, y2)):
                    S.op("dve", lambda e, poo=poo, hs=hss[d_], yy=yy, t=t, d_=d_: e.tensor_tensor(
                        out=yy[:].rearrange("p (h d) -> p h d", h=4), in0=poo[:, d_ * 256:(d_ + 1) * 256].rearrange("p (h d) -> p h d", h=4),
                        in1=ecum[:, t, hs].unsqueeze(2).to_broadcast([128, 4, 64]), op=ALU.mult),
                        reads=bufs(poo, ecum), writes=bufs(yy))
                S.op("pool", lambda e, y1=y1, y2=y2: e.tensor_tensor(out=y1[:], in0=y1[:], in1=y2[:], op=ALU.add),
                     reads=bufs(y1, y2), writes=bufs(y1))
                S.op("pool", lambda e, y2=y2, t=t, g=g: e.tensor_tensor(
                    out=y2[:].rearrange("p (h d) -> p h d", h=4), in0=XB[:, t, 0:256].rearrange("p (h d) -> p h d", h=4),
                    in1=c["dsk"][:, 4 * g:4 * g + 4].unsqueeze(2).to_broadcast([128, 4, 64]), op=ALU.mult),
                    reads=bufs(XB, c["dsk"]), writes=bufs(y2))
                S.op("pool", lambda e, y1=y1, y2=y2: e.tensor_tensor(out=y1[:], in0=y1[:], in1=y2[:], op=ALU.add),
                     reads=bufs(y1, y2), writes=bufs(y1))
                S.op("dve", lambda e, py=py, y1=y1: e.tensor_tensor(out=y1[:], in0=py[:, 0:256], in1=y1[:], op=ALU.add),
                     reads=bufs(py, y1), writes=bufs(y1))
                vb_ = vbr.next()
                S.op("dve", lambda e, vb_=vb_, y1=y1, sz=sz: e.scalar_tensor_tensor(
                    out=vb_[:], in0=y1[:], scalar=0.5, in1=sz[:], op0=ALU.mult, op1=ALU.mult),
                    reads=bufs(y1, sz), writes=bufs(vb_))
                S.op("act", lambda e, vb_=vb_, t=t, g=g: e.activation(out=junk[:, 0:256], in_=vb_[:], func=AF.Square,
                                                                     accum_out=ssq[:, t, g:g + 1]),
                     reads=bufs(vb_), writes=bufs(junk, ssq))
                ptb = banks.next()
                ptv = ptb[:].bitcast(BF16)
                for bb in range(2):
                    S.op("pe", lambda e, ptv=ptv, vb_=vb_, bb=bb: e.transpose(
                        out=ptv[:, bb * 128:(bb + 1) * 128], in_=vb_[:, bb * 128:(bb + 1) * 128], identity=identb[:]),
                        reads=bufs(vb_, identb), writes=bufs(ptb))
                for bb in range(2):
                    S.op("act", lambda e, ptv=ptv, bb=bb, tsl=tsl, g=g: e.activation(
                        out=vTg[:, bb, tsl], in_=ptv[:, bb * 128:(bb + 1) * 128], func=AF.Identity,
                        scale=c["ngT"][:, 2 * g + bb:2 * g + bb + 1]), reads=bufs(ptb, c["ngT"]), writes=bufs(vTg))

            fnext = front(0)
            for t in range(ntile):
                fcur = fnext
                if t + 1 < ntile:
                    fnext = front(t + 1)
                back(fcur)
            S.dma("sp", yscr[2 * g:2 * g + 2, :, tok0:tok0 + T_].rearrange("b p t -> p b t"), vTg[:], reads=bufs(vTg))
        S.op("dve", lambda e: e.tensor_reduce(out=rstd[:], in_=ssq[:], axis=AX.X, op=ALU.add), reads=bufs(ssq), writes=bufs(rstd))
        S.op("dve", lambda e: e.tensor_scalar(out=rstd[:], in0=rstd[:], scalar1=1.0 / E, scalar2=EPS, op0=ALU.mult, op1=ALU.add),
             reads=bufs(rstd), writes=bufs(rstd))
        S.op("act", lambda e: e.activation(out=rstd[:], in_=rstd[:], func=AF.Sqrt), reads=bufs(rstd), writes=bufs(rstd))
        S.op("dve", lambda e: e.reciprocal(out=rstd[:], in_=rstd[:]), reads=bufs(rstd), writes=bufs(rstd))
        return rstd

    TWO_PI = float(2 * np.pi)
    MAGIC = 12582912.0

    def s5_prep():
        c = {}
        c["AA"] = k.at([128, 64, 2, 2], F32)
        c["BB"] = k.at([128, 64, 2, 2], F32)
        c["Wsel"] = k.at([128, 8, 240], BF16)
        c["dT"] = k.at([128, 16], F32)
        c["bgT"] = k.at([128, 16], F32)
        c["h0"] = k.at([128, 2, 2, 64], F32)
        S.dma("sp", c["dT"][:], s5_d.rearrange("(b p) -> p b", p=128), writes=bufs(c["dT"]))
        c["dX"] = k.at([128, 128], F32)
        for t_ in range(8):
            S.dma("sp", c["dX"][16 * t_:16 * t_ + 16, :], s5_d.rearrange("(g j) -> j g", j=16), writes=bufs(c["dX"]))
        S.dma("sp", c["bgT"][:], s5_b_glu.rearrange("(b p) -> p b", p=128), writes=bufs(c["bgT"]))
        c["bgh"] = k.at([128, 16], F32)
        S.op("dve", lambda e: e.tensor_scalar(out=c["bgh"][:], in0=c["bgT"][:], scalar1=0.5, scalar2=None, op0=ALU.mult),
             reads=bufs(c["bgT"]), writes=bufs(c["bgh"]))
        for d_ in range(2):
            S.dma("sp", c["h0"][:, d_, :, :], s5_h0[d_].rearrange("r p g -> p r g"), writes=bufs(c["h0"]))
        mm = k.amark()
        wself = k.at([128, 8, 240], F32)
        S.op("pool", lambda e: e.memset(wself[:], 0.0), writes=bufs(wself))
        S.op("pool", lambda e: e.affine_select(out=wself[:, :, 112:128], in_=wself[:, :, 112:128], compare_op=ALU.not_equal,
                                               fill=1.0, base=0, pattern=[[-16, 8], [-1, 16]], channel_multiplier=1),
             reads=bufs(wself), writes=bufs(wself))
        S.op("pool", lambda e: e.tensor_copy(out=c["Wsel"][:], in_=wself[:]), reads=bufs(wself), writes=bufs(c["Wsel"]))
        maskT = [k.at([128, 8, 16], F32), k.at([128, 8, 16], F32)]
        for d_ in range(2):
            S.op("pool", lambda e, d_=d_: e.memset(maskT[d_][:], 1.0), writes=bufs(maskT[d_]))
        S.op("pool", lambda e: e.affine_select(out=maskT[0][:], in_=maskT[0][:], compare_op=ALU.is_ge, fill=0.0, base=15,
                                               pattern=[[16, 8], [0, 16]], channel_multiplier=-1),
             reads=bufs(maskT[0]), writes=bufs(maskT[0]))
        S.op("pool", lambda e: e.affine_select(out=maskT[1][:], in_=maskT[1][:], compare_op=ALU.is_ge, fill=0.0, base=0,
                                               pattern=[[-16, 8], [0, 16]], channel_multiplier=1),
             reads=bufs(maskT[1]), writes=bufs(maskT[1]))
        pw = [[k.at([128, 64, 16], F32), k.at([128, 64, 16], F32)] for _ in range(2)]
        coef = [[k.at([128, 64], F32), k.at([128, 64], F32)] for _ in range(2)]
        lr, li, ls = k.at([128, 64], F32), k.at([128, 64], F32), k.at([128, 64], F32)
        xx, ang = k.at([128, 64], F32), k.at([128, 64], F32)
        tr = k.aring(6, [128, 64], F32)
        for d_ in range(2):
            S.dma("sp", lr[:], s5_lam[0, d_], writes=bufs(lr))
            S.dma("sp", li[:], s5_lam[1, d_], writes=bufs(li))
            S.dma("sp", ls[:], s5_lstep[d_], writes=bufs(ls))
            S.op("act", lambda e: e.activation(out=ls[:], in_=ls[:], func=AF.Exp), reads=bufs(ls), writes=bufs(ls))
            S.op("dve", lambda e: e.tensor_tensor(out=xx[:], in0=lr[:], in1=ls[:], op=ALU.mult), reads=bufs(lr, ls), writes=bufs(xx))
            S.op("dve", lambda e: e.tensor_tensor(out=ang[:], in0=li[:], in1=ls[:], op=ALU.mult), reads=bufs(li, ls), writes=bufs(ang))
            pre, pim = pw[d_]
            for kq in range(1, 9):
                mp, mn, sn, cs, t1, t2 = [tr.next() for _ in range(6)]
                S.op("act", lambda e, mp=mp, kq=kq: e.activation(out=mp[:], in_=xx[:], func=AF.Exp, scale=float(kq)),
                     reads=bufs(xx), writes=bufs(mp))
                S.op("act", lambda e, mn=mn, kq=kq: e.activation(out=mn[:], in_=xx[:], func=AF.Exp, scale=float(-kq)),
                     reads=bufs(xx), writes=bufs(mn))
                for dst, shift in ((sn, 0.0), (cs, 0.25)):
                    if shift:
                        S.op("dve", lambda e, t1=t1, kq=kq, shift=shift: e.tensor_scalar(
                            out=t1[:], in0=ang[:], scalar1=float(kq / TWO_PI), scalar2=shift, op0=ALU.mult, op1=ALU.add),
                            reads=bufs(ang), writes=bufs(t1))
                        S.op("dve", lambda e, t1=t1: e.tensor_scalar(out=t1[:], in0=t1[:], scalar1=MAGIC, scalar2=None, op0=ALU.add),
                             reads=bufs(t1), writes=bufs(t1))
                    else:
                        S.op("dve", lambda e, t1=t1, kq=kq: e.tensor_scalar(
                            out=t1[:], in0=ang[:], scalar1=float(kq / TWO_PI), scalar2=MAGIC, op0=ALU.mult, op1=ALU.add),
                            reads=bufs(ang), writes=bufs(t1))
                    S.op("dve", lambda e, t1=t1: e.tensor_scalar(out=t1[:], in0=t1[:], scalar1=-MAGIC, scalar2=-TWO_PI,
                                                                 op0=ALU.add, op1=ALU.mult), reads=bufs(t1), writes=bufs(t1))
                    S.op("dve", lambda e, t1=t1, kq=kq: e.scalar_tensor_tensor(
                        out=t1[:], in0=ang[:], scalar=float(kq), in1=t1[:], op0=ALU.mult, op1=ALU.add),
                        reads=bufs(ang, t1), writes=bufs(t1))
                    if shift:
                        S.op("dve", lambda e, t1=t1: e.tensor_scalar(out=t1[:], in0=t1[:], scalar1=float(np.pi / 2), scalar2=None,
                                                                     op0=ALU.add), reads=bufs(t1), writes=bufs(t1))
                    S.op("act", lambda e, t1=t1, dst=dst: e.activation(out=dst[:], in_=t1[:], func=AF.Sin),
                         reads=bufs(t1), writes=bufs(dst))
                S.op("dve", lambda e, kq=kq, mp=mp, cs=cs, pre=pre: e.tensor_tensor(out=pre[:, :, kq - 1], in0=mp[:], in1=cs[:], op=ALU.mult),
                     reads=bufs(mp, cs), writes=bufs(pre))
                S.op("dve", lambda e, kq=kq, mp=mp, sn=sn, pim=pim: e.tensor_tensor(out=pim[:, :, kq - 1], in0=mp[:], in1=sn[:], op=ALU.mult),
                     reads=bufs(mp, sn), writes=bufs(pim))
                S.op("dve", lambda e, kq=kq, mn=mn, cs=cs, pre=pre: e.tensor_tensor(out=pre[:, :, 7 + kq], in0=mn[:], in1=cs[:], op=ALU.mult),
                     reads=bufs(mn, cs), writes=bufs(pre))
                S.op("dve", lambda e, kq=kq, mn=mn, sn=sn, pim=pim: e.scalar_tensor_tensor(
                    out=pim[:, :, 7 + kq], in0=mn[:], scalar=-1.0, in1=sn[:], op0=ALU.mult, op1=ALU.mult),
                    reads=bufs(mn, sn), writes=bufs(pim))
            for r_ in range(2):
                S.op("act", lambda e, d_=d_, r_=r_, pre=pre: e.activation(out=c["AA"][:, :, d_, r_], in_=pre[:, :, 7], func=AF.Copy),
                     reads=bufs(pre), writes=bufs(c["AA"]))
            S.op("dve", lambda e, d_=d_, pim=pim: e.tensor_scalar(out=c["BB"][:, :, d_, 0], in0=pim[:, :, 7], scalar1=-1.0, scalar2=None,
                                                         op0=ALU.mult), reads=bufs(pim), writes=bufs(c["BB"]))
            S.op("act", lambda e, d_=d_, pim=pim: e.activation(out=c["BB"][:, :, d_, 1], in_=pim[:, :, 7], func=AF.Copy),
                 reads=bufs(pim), writes=bufs(c["BB"]))
            den, nr, t1, t2 = [tr.next() for _ in range(4)]
            S.op("dve", lambda e, den=den: e.tensor_tensor(out=den[:], in0=lr[:], in1=lr[:], op=ALU.mult), reads=bufs(lr), writes=bufs(den))
            S.op("dve", lambda e, t1=t1: e.tensor_tensor(out=t1[:], in0=li[:], in1=li[:], op=ALU.mult), reads=bufs(li), writes=bufs(t1))
            S.op("dve", lambda e, den=den, t1=t1: e.tensor_tensor(out=den[:], in0=den[:], in1=t1[:], op=ALU.add),
                 reads=bufs(den, t1), writes=bufs(den))
            S.op("dve", lambda e, den=den: e.reciprocal(out=den[:], in_=den[:]), reads=bufs(den), writes=bufs(den))
            S.op("dve", lambda e, nr=nr, pre=pre: e.tensor_scalar(out=nr[:], in0=pre[:, :, 0], scalar1=-1.0, scalar2=None, op0=ALU.add),
                 reads=bufs(pre), writes=bufs(nr))
            cr_, ci_ = coef[d_]
            S.op("dve", lambda e, nr=nr, t1=t1: e.tensor_tensor(out=t1[:], in0=nr[:], in1=lr[:], op=ALU.mult), reads=bufs(nr, lr), writes=bufs(t1))
            S.op("dve", lambda e, t2=t2, pim=pim: e.tensor_tensor(out=t2[:], in0=pim[:, :, 0], in1=li[:], op=ALU.mult), reads=bufs(pim, li), writes=bufs(t2))
            S.op("dve", lambda e, t1=t1, t2=t2: e.tensor_tensor(out=t1[:], in0=t1[:], in1=t2[:], op=ALU.add), reads=bufs(t1, t2), writes=bufs(t1))
            S.op("dve", lambda e, t1=t1, den=den, cr_=cr_: e.tensor_tensor(out=cr_[:], in0=t1[:], in1=den[:], op=ALU.mult),
                 reads=bufs(t1, den), writes=bufs(cr_))
            S.op("dve", lambda e, t1=t1, pim=pim: e.tensor_tensor(out=t1[:], in0=pim[:, :, 0], in1=lr[:], op=ALU.mult), reads=bufs(pim, lr), writes=bufs(t1))
            S.op("dve", lambda e, nr=nr, t2=t2: e.tensor_tensor(out=t2[:], in0=nr[:], in1=li[:], op=ALU.mult), reads=bufs(nr, li), writes=bufs(t2))
            S.op("dve", lambda e, t1=t1, t2=t2: e.tensor_tensor(out=t1[:], in0=t1[:], in1=t2[:], op=ALU.subtract), reads=bufs(t1, t2), writes=bufs(t1))
            S.op("dve", lambda e, t1=t1, den=den, ci_=ci_: e.tensor_tensor(out=ci_[:], in0=t1[:], in1=den[:], op=ALU.mult),
                 reads=bufs(t1, den), writes=bufs(ci_))
        Braw = [k.at([128, 8, 16], F32), k.at([128, 8, 16], F32)]
        Craw = [k.at([128, 8, 16], F32), k.at([128, 8, 16], F32)]
        Bb = [k.at([128, 8, 16], F32), k.at([128, 8, 16], F32)]
        V = [[k.at([128, 8, 8, 16], F32), k.at([128, 8, 8, 16], F32)] for _ in range(2)]
        W2 = [[k.at([128, 8, 8, 16], F32), k.at([128, 8, 8, 16], F32)] for _ in range(2)]
        t8 = k.aring(4, [128, 8, 16], F32)
        t8e = {"pool": k.aring(4, [128, 8, 16], F32), "dve": k.aring(4, [128, 8, 16], F32)}
        T16 = k.aring(2, [128, 16, 128], BF16)
        VT16 = k.aring(2, [128, 8, 2, 2, 128], BF16)
        W216 = k.aring(2, [128, 8, 2, 2, 128], BF16)
        Ttmp = k.aring(2, [128, 128], F32)
        Ttmp2 = k.aring(2, [128, 128], F32)

        def bc_j(ap2):
            return ap2.unsqueeze(2).to_broadcast([128, 8, 16])

        for b in range(8):
            gs = slice(8 * b, 8 * b + 8)
            t16, vt16, w216 = T16.next(), VT16.next(), W216.next()
            for d_ in range(2):
                pre, pim = pw[d_]
                cr_, ci_ = coef[d_]
                for r_ in range(2):
                    S.dma("sp", Braw[r_][:], s5_B[r_, d_, :, gs, :], writes=bufs(Braw[r_]))
                    S.dma("sp", Craw[r_][:], s5_C[r_, d_, :, gs, :], writes=bufs(Craw[r_]))
                ta, tb = t8.next(), t8.next()
                S.op("dve", lambda e, ta=ta, cr_=cr_, gs=gs: e.tensor_tensor(out=ta[:], in0=Braw[0][:], in1=bc_j(cr_[:, gs]), op=ALU.mult),
                     reads=bufs(Braw[0], cr_), writes=bufs(ta))
                S.op("dve", lambda e, tb=tb, ci_=ci_, gs=gs: e.tensor_tensor(out=tb[:], in0=Braw[1][:], in1=bc_j(ci_[:, gs]), op=ALU.mult),
                     reads=bufs(Braw[1], ci_), writes=bufs(tb))
                S.op("dve", lambda e, ta=ta, tb=tb: e.tensor_tensor(out=Bb[0][:], in0=ta[:], in1=tb[:], op=ALU.subtract),
                     reads=bufs(ta, tb), writes=bufs(Bb[0]))
                ta, tb = t8.next(), t8.next()
                S.op("dve", lambda e, ta=ta, cr_=cr_, gs=gs: e.tensor_tensor(out=ta[:], in0=Braw[1][:], in1=bc_j(cr_[:, gs]), op=ALU.mult),
                     reads=bufs(Braw[1], cr_), writes=bufs(ta))
                S.op("dve", lambda e, tb=tb, ci_=ci_, gs=gs: e.tensor_tensor(out=tb[:], in0=Braw[0][:], in1=bc_j(ci_[:, gs]), op=ALU.mult),
                     reads=bufs(Braw[0], ci_), writes=bufs(tb))
                S.op("dve", lambda e, ta=ta, tb=tb: e.tensor_tensor(out=Bb[1][:], in0=ta[:], in1=tb[:], op=ALU.add),
                     reads=bufs(ta, tb), writes=bufs(Bb[1]))
                for s_ in range(8):
                    kv = 8 + (s_ if d_ == 0 else 7 - s_)
                    kw = s_ if d_ == 0 else 7 - s_
                    for (eng, P_idx, X_, out_, neg_im) in (("pool", kv, Bb, V[d_], False), ("dve", kw, Craw, W2[d_], True)):
                        Pr = bc_j(pre[:, gs, P_idx])
                        Pi = bc_j(pim[:, gs, P_idx])
                        ta, tb = t8e[eng].next(), t8e[eng].next()
                        S.op(eng, lambda e, ta=ta, X_=X_, Pr=Pr: e.tensor_tensor(out=ta[:], in0=X_[0][:], in1=Pr, op=ALU.mult),
                             reads=bufs(X_[0], pre), writes=bufs(ta))
                        S.op(eng, lambda e, tb=tb, X_=X_, Pi=Pi: e.tensor_tensor(out=tb[:], in0=X_[1][:], in1=Pi, op=ALU.mult),
                             reads=bufs(X_[1], pim), writes=bufs(tb))
                        S.op(eng, lambda e, ta=ta, tb=tb, out_=out_, s_=s_: e.tensor_tensor(
                            out=out_[0][:, :, s_, :], in0=ta[:], in1=tb[:], op=ALU.subtract), reads=bufs(ta, tb), writes=bufs(out_[0]))
                        ta, tb = t8e[eng].next(), t8e[eng].next()
                        S.op(eng, lambda e, ta=ta, X_=X_, Pi=Pi: e.tensor_tensor(out=ta[:], in0=X_[0][:], in1=Pi, op=ALU.mult),
                             reads=bufs(X_[0], pim), writes=bufs(ta))
                        S.op(eng, lambda e, tb=tb, X_=X_, Pr=Pr: e.tensor_tensor(out=tb[:], in0=X_[1][:], in1=Pr, op=ALU.mult),
                             reads=bufs(X_[1], pre), writes=bufs(tb))
                        if not neg_im:
                            S.op(eng, lambda e, ta=ta, tb=tb, out_=out_, s_=s_: e.tensor_tensor(
                                out=out_[1][:, :, s_, :], in0=ta[:], in1=tb[:], op=ALU.add), reads=bufs(ta, tb), writes=bufs(out_[1]))
                        else:
                            S.op(eng, lambda e, ta=ta, tb=tb: e.tensor_tensor(out=ta[:], in0=ta[:], in1=tb[:], op=ALU.add),
                                 reads=bufs(ta, tb), writes=bufs(ta))
                            S.op(eng, lambda e, ta=ta, out_=out_, s_=s_: e.tensor_scalar(
                                out=out_[1][:, :, s_, :], in0=ta[:], scalar1=-1.0, scalar2=None, op0=ALU.mult),
                                reads=bufs(ta), writes=bufs(out_[1]))
                for r_ in range(2):
                    S.op("act", lambda e, d_=d_, r_=r_, w216=w216: e.activation(
                        out=w216[:, :, d_, r_, :], in_=W2[d_][r_][:].rearrange("p g s j -> p g (s j)"), func=AF.Copy),
                        reads=bufs(W2[d_][r_]), writes=bufs(w216))
                for gl in range(8):
                    pv_ = banks.next()
                    for r_ in range(2):
                        S.op("pe", lambda e, pv_=pv_, d_=d_, r_=r_, gl=gl: e.transpose(
                            out=pv_[:, r_ * 128:(r_ + 1) * 128], in_=V[d_][r_][:, gl, :, :].rearrange("p s j -> p (s j)"),
                            identity=identf[:]), reads=bufs(V[d_][r_], identf), writes=bufs(pv_))
                    S.op("act", lambda e, pv_=pv_, d_=d_, gl=gl, vt16=vt16: e.activation(
                        out=vt16[:, gl, d_, :, :].rearrange("p r m -> p (r m)"), in_=pv_[:, 0:256], func=AF.Copy),
                        reads=bufs(pv_), writes=bufs(vt16))
            for gl in range(8):
                for par in range(2):
                    rows = slice(par * 64, (par + 1) * 64)
                    gi = gl * 2 + par
                    pT = banks.next()
                    for d_ in range(2):
                        for r_ in range(2):
                            S.op("pe", lambda e, pT=pT, d_=d_, r_=r_, gl=gl, rows=rows: e.matmul(
                                pT[:, d_ * 128:(d_ + 1) * 128], lhsT=V[d_][r_][rows, gl, :, :].rearrange("p s j -> p (s j)"),
                                rhs=W2[d_][r_][rows, gl, :, :].rearrange("p s j -> p (s j)"), start=(r_ == 0), stop=(r_ == 1)),
                                reads=bufs(V[d_][r_], W2[d_][r_]), writes=bufs(pT))
                    ta, tb = Ttmp.next(), Ttmp2.next()
                    S.op("dve", lambda e, pT=pT, ta=ta: e.tensor_tensor(
                        out=ta[:], in0=pT[:, 0:128], in1=maskT[0][:].rearrange("p t j -> p (t j)"), op=ALU.mult),
                        reads=bufs(pT, maskT[0]), writes=bufs(ta))
                    S.op("dve", lambda e, pT=pT, tb=tb: e.tensor_tensor(
                        out=tb[:], in0=pT[:, 128:256], in1=maskT[1][:].rearrange("p t j -> p (t j)"), op=ALU.mult),
                        reads=bufs(pT, maskT[1]), writes=bufs(tb))
                    S.op("pool", lambda e, ta=ta, tb=tb, t16=t16, gi=gi: e.tensor_tensor(out=t16[:, gi, :], in0=ta[:], in1=tb[:], op=ALU.add),
                         reads=bufs(ta, tb), writes=bufs(t16))
            S.dma("sp", Tscr[b], t16[:].rearrange("p g m -> p (g m)"), reads=bufs(t16))
            S.dma("sp", VTscr[b], vt16[:].rearrange("p g d r m -> p (g d r m)"), reads=bufs(vt16))
            S.dma("sp", W2scr[b], w216[:].rearrange("p g d r m -> p (g d r m)"), reads=bufs(w216))
        k.arestore(mm)
        return c

    def s5_tiles(tok0, sub, is_lat):
        tiles = []
        for i in range(8):
            I_ = sub * 8 + i
            if not is_lat:
                s_, c0 = I_, 0
            else:
                s_, c0 = I_ // 2, (I_ % 2) * 128
            base = tok0 + 8 * c0 + s_
            tiles.append(((lambda src_, base=base: src_[base:base + 8 * 127 + 1:8, :]), I_ * 128))
        return tiles

    def s5_unit(c, tok0, is_lat):
        C_ = 256 if is_lat else 128
        nseq = 1 if is_lat else 4
        nch = C_ // nseq
        nct = C_ // 128
        Tn = 8 * C_
        U = k.at([128, nct, 16, 8, 16], BF16)
        X = k.at([128, 16, C_], BF16)
        arr = k.at([128, 16, 2, nseq, nch + 1], F32)
        Hb = k.at([128, 16, 2, nseq, nch + 1], BF16)
        Ysb = T(U.t[:].rearrange("p a g s j -> p (a g s j)").rearrange("p (g c) -> p g c", g=16))
        Ysb.b = U.b
        Tw = k.at([128, 16, 128], BF16)
        VTw = k.at([128, 8, 2, 2, 128], BF16)
        W2w = k.at([128, 8, 2, 2, 128], BF16)
        ygst = k.at([128, 2, Tn], BF16)
        uur = k.aring(2, [128, 512], F32)
        ysr = k.aring(2, [128, 512], F32)
        tmps = {eng: [k.at([128, 8, 2, nseq], F32) for _ in range(3)] for eng in ("dve", "pool")}
        GPB = 512 // C_
        if not is_lat:
            fin = k.at([128, 4, 2, 2, 64], F32)
            fst = k.aring(2, [64, 4, 128], F32)
        bl = {}
        if is_lat:
            for eng in ("dve", "pool"):
                bl[eng] = {"PR": k.at([128, 8, 16], F32), "PI": k.at([128, 8, 16], F32),
                           "AAp": k.at([128, 8, 2, 16], F32), "BBp": k.at([128, 8, 2, 16], F32),
                           "cc": k.at([128, 8, 2, 17], F32),
                           "t": [k.at([128, 8, 8], F32) for _ in range(4)],
                           "l": [k.at([128, 8, 2, 16], F32) for _ in range(3)],
                           "c": [k.at([128, 8, 2], F32) for _ in range(2)],
                           "f": [[k.at([128, 8, 2, 16], F32) for _ in range(2)] for _ in range(2)]}
        for b in range(cfg.get("s5_nb", 8)):
            gs = slice(8 * b, 8 * b + 8)
            S.dma("sp", Tw[:].rearrange("p g m -> p (g m)"), Tscr[b], writes=bufs(Tw))
            S.dma("sp", VTw[:].rearrange("p g d r m -> p (g d r m)"), VTscr[b], writes=bufs(VTw))
            S.dma("sp", W2w[:].rearrange("p g d r m -> p (g d r m)"), W2scr[b], writes=bufs(W2w))
            wu = wring.next()
            S.dma("pool", wu[:, :, 0:256], s5_w_in.rearrange("(k p) n -> p k n", p=128)[:, :, 256 * b:256 * (b + 1)], writes=bufs(wu))
            for ct in range(nct):
                for s2 in range(4):
                    pb = banks.next()
                    for si in range(2):
                        s_ = s2 * 2 + si
                        p0 = s_ * C_ + ct * 128
                        for kk in range(8):
                            S.op("pe", lambda e, pb=pb, kk=kk, si=si, p0=p0, wu=wu: e.matmul(
                                pb[:, si * 256:(si + 1) * 256], lhsT=hT[:, kk, p0:p0 + 128], rhs=wu[:, kk, 0:256],
                                start=(kk == 0), stop=(kk == 7)), reads=bufs(hT, wu), writes=bufs(pb))
                    S.op("act", lambda e, pb=pb, ct=ct, s2=s2: e.activation(
                        out=U[:, ct, :, s2 * 2:s2 * 2 + 2, :], in_=pb[:].rearrange("p (s g j) -> p g s j", s=2, g=16), func=AF.Copy),
                        reads=bufs(pb), writes=bufs(U))
            for ct in range(nct):
                for g4 in range(4):
                    pb = banks.next()
                    for gg in range(4):
                        gi = g4 * 4 + gg
                        S.op("pe", lambda e, pb=pb, gg=gg, gi=gi, ct=ct: e.matmul(
                            pb[:, gg * 128:(gg + 1) * 128], lhsT=U[:, ct, gi, :, :].rearrange("p s j -> p (s j)"), rhs=identb[:], start=True, stop=True),
                            reads=bufs(U, identb), writes=bufs(pb))
                    S.op("act", lambda e, pb=pb, g4=g4, ct=ct: e.activation(
                        out=X[:, g4 * 4:g4 * 4 + 4, ct * 128:(ct + 1) * 128], in_=pb[:].rearrange("p (g c) -> p g c", g=4), func=AF.Copy),
                        reads=bufs(pb), writes=bufs(X))
            if is_lat:
                S.op("act", lambda e, gs=gs: e.activation(
                    out=arr[:, :, :, 0, 0].rearrange("p (g d) r -> p g d r", d=2),
                    in_=c["h0"][:, :, :, gs].rearrange("p d r g -> p g d r"), func=AF.Copy), reads=bufs(c["h0"]), writes=bufs(arr))
            else:
                S.op("pool", lambda e: e.memset(arr[:, :, :, :, 0:1], 0.0), writes=bufs(arr))
            for gl in range(8):
                pGs = [banks.next() for _ in range(nct)]
                for par in range(2):
                    gi = 2 * gl + par
                    rows = slice(par * 64, (par + 1) * 64)
                    for d_ in range(2):
                        if d_ == 0:
                            rhs = X[:, gi, :]
                        else:
                            rhs = X[:, gi, :].rearrange("p (s c) -> p s c", s=nseq)[:, :, ::-1]
                        for r_ in range(2):
                            if is_lat:
                                outp = pGs[d_][rows, r_ * 256:(r_ + 1) * 256]
                                pgb = pGs[d_]
                            else:
                                outp = pGs[0][rows, (d_ * 2 + r_) * 128:(d_ * 2 + r_ + 1) * 128]
                                pgb = pGs[0]
                            S.op("pe", lambda e, outp=outp, gl=gl, d_=d_, r_=r_, par=par, rhs=rhs: e.matmul(
                                outp, lhsT=VTw[:, gl, d_, r_, par * 64:(par + 1) * 64], rhs=rhs, start=True, stop=True),
                                reads=bufs(VTw, X), writes=bufs(pgb))
                for d_ in range(2):
                    if is_lat:
                        src_ = pGs[d_][:].rearrange("p (r s c) -> p r s c", r=2, s=1)
                        pgb = pGs[d_]
                    else:
                        src_ = pGs[0][:, d_ * 256:(d_ + 1) * 256].rearrange("p (r s c) -> p r s c", r=2, s=nseq)
                        pgb = pGs[0]
                    S.op("act", lambda e, src_=src_, gl=gl, d_=d_: e.activation(
                        out=arr[:, gl * 2 + d_, :, :, 1:nch + 1], in_=src_, func=AF.Copy), reads=bufs(pgb), writes=bufs(arr))
            AAb = c["AA"][:, gs, :, :].rearrange("p g d r -> p (g d) r")
            BBb = c["BB"][:, gs, :, :].rearrange("p g d r -> p (g d) r")
            if not is_lat:
                for kq in range(nch):
                    for eng, qs in (("dve", slice(0, 8)), ("pool", slice(8, 16))):
                        tt, p1, p2 = tmps[eng]
                        S.op(eng, lambda e, tt=tt, qs=qs, kq=kq: e.tensor_tensor(
                            out=tt[:], in0=arr[:, qs, :, :, kq], in1=arr[:, qs, :, :, kq + 1], op=ALU.add),
                            reads=bufs(arr), writes=bufs(tt))
                        S.op(eng, lambda e, tt=tt, p1=p1, qs=qs, AAb=AAb: e.tensor_tensor(
                            out=p1[:], in0=tt[:], in1=AAb[:, qs, :].unsqueeze(3).to_broadcast([128, 8, 2, nseq]), op=ALU.mult),
                            reads=bufs(tt, c["AA"]), writes=bufs(p1))
                        S.op(eng, lambda e, tt=tt, p2=p2, qs=qs, BBb=BBb: e.tensor_tensor(
                            out=p2[:], in0=tt[:, :, ::-1, :], in1=BBb[:, qs, :].unsqueeze(3).to_broadcast([128, 8, 2, nseq]), op=ALU.mult),
                            reads=bufs(tt, c["BB"]), writes=bufs(p2))
                        S.op(eng, lambda e, p1=p1, p2=p2, qs=qs, kq=kq: e.tensor_tensor(
                            out=arr[:, qs, :, :, kq + 1], in0=p1[:], in1=p2[:], op=ALU.add), reads=bufs(p1, p2), writes=bufs(arr))
                S.op("act", lambda e: e.activation(out=Hb[:].rearrange("p q r s c -> p (q r s c)"),
                                                   in_=arr[:].rearrange("p q r s c -> p (q r s c)"), func=AF.Copy),
                     reads=bufs(arr), writes=bufs(Hb))
            else:
                NB_, BL_ = 16, 16
                for eng, qs in (("dve", slice(0, 8)), ("pool", slice(8, 16))):
                    B_ = bl[eng]
                    PR, PI, AAp, BBp, cc = B_["PR"], B_["PI"], B_["AAp"], B_["BBp"], B_["cc"]
                    tA, tB, tC, tD = B_["t"]
                    AAh = AAb[:, qs, :]
                    BBh = BBb[:, qs, :]
                    S.op(eng, lambda e, PR=PR, AAh=AAh: e.tensor_copy(out=PR[:, :, 0], in_=AAh[:, :, 0]), reads=bufs(c["AA"]), writes=bufs(PR))
                    S.op(eng, lambda e, PI=PI, BBh=BBh: e.tensor_copy(out=PI[:, :, 0], in_=BBh[:, :, 1]), reads=bufs(c["BB"]), writes=bufs(PI))
                    m_ = 1
                    while m_ < 16:
                        ar = PR[:, :, m_ - 1:m_].to_broadcast([128, 8, m_])
                        ai = PI[:, :, m_ - 1:m_].to_broadcast([128, 8, m_])
                        src_r, src_i = PR[:, :, 0:m_], PI[:, :, 0:m_]
                        dst_r, dst_i = PR[:, :, m_:2 * m_], PI[:, :, m_:2 * m_]
                        ta, tb = tA[:, :, 0:m_], tB[:, :, 0:m_]
                        tc_, td = tC[:, :, 0:m_], tD[:, :, 0:m_]
                        S.op(eng, lambda e, ta=ta, src_r=src_r, ar=ar: e.tensor_tensor(out=ta, in0=src_r, in1=ar, op=ALU.mult), reads=bufs(PR), writes=bufs(tA))
                        S.op(eng, lambda e, tb=tb, src_i=src_i, ai=ai: e.tensor_tensor(out=tb, in0=src_i, in1=ai, op=ALU.mult), reads=bufs(PI), writes=bufs(tB))
                        S.op(eng, lambda e, tc_=tc_, src_r=src_r, ai=ai: e.tensor_tensor(out=tc_, in0=src_r, in1=ai, op=ALU.mult), reads=bufs(PR, PI), writes=bufs(tC))
                        S.op(eng, lambda e, td=td, src_i=src_i, ar=ar: e.tensor_tensor(out=td, in0=src_i, in1=ar, op=ALU.mult), reads=bufs(PR, PI), writes=bufs(tD))
                        S.op(eng, lambda e, dst_r=dst_r, ta=ta, tb=tb: e.tensor_tensor(out=dst_r, in0=ta, in1=tb, op=ALU.subtract), reads=bufs(tA, tB), writes=bufs(PR))
                        S.op(eng, lambda e, dst_i=dst_i, tc_=tc_, td=td: e.tensor_tensor(out=dst_i, in0=tc_, in1=td, op=ALU.add), reads=bufs(tC, tD), writes=bufs(PI))
                        m_ *= 2
                    for r_ in range(2):
                        S.op(eng, lambda e, AAp=AAp, PR=PR, r_=r_: e.tensor_copy(out=AAp[:, :, r_, :], in_=PR[:]), reads=bufs(PR), writes=bufs(AAp))
                    S.op(eng, lambda e, BBp=BBp, PI=PI: e.tensor_scalar(out=BBp[:, :, 0, :], in0=PI[:], scalar1=-1.0, scalar2=None, op0=ALU.mult),
                         reads=bufs(PI), writes=bufs(BBp))
                    S.op(eng, lambda e, BBp=BBp, PI=PI: e.tensor_copy(out=BBp[:, :, 1, :], in_=PI[:]), reads=bufs(PI), writes=bufs(BBp))
                for eng, qs in (("dve", slice(0, 8)), ("pool", slice(8, 16))):
                    B_ = bl[eng]
                    AAp, BBp, cc = B_["AAp"], B_["BBp"], B_["cc"]
                    t3, p13, p23 = B_["l"]
                    AAh = AAb[:, qs, :].unsqueeze(3).to_broadcast([128, 8, 2, NB_])
                    BBh = BBb[:, qs, :].unsqueeze(3).to_broadcast([128, 8, 2, NB_])
                    xv = arr[:, qs, :, 0, 1:257].rearrange("p q r (b i) -> p q r b i", i=BL_)
                    for i_ in range(BL_):
                        if i_ == 0:
                            src_t = xv[:, :, :, :, 0]
                        else:
                            S.op(eng, lambda e, t3=t3, xv=xv, i_=i_: e.tensor_tensor(
                                out=t3[:], in0=xv[:, :, :, :, i_ - 1], in1=xv[:, :, :, :, i_], op=ALU.add), reads=bufs(arr), writes=bufs(t3))
                            src_t = t3[:]
                        rd = bufs(arr) if i_ == 0 else bufs(t3)
                        src_sw = src_t[:, :, ::-1, :]
                        S.op(eng, lambda e, p13=p13, src_t=src_t, AAh=AAh: e.tensor_tensor(out=p13[:], in0=src_t, in1=AAh, op=ALU.mult),
                             reads=rd + bufs(c["AA"]), writes=bufs(p13))
                        S.op(eng, lambda e, p23=p23, src_sw=src_sw, BBh=BBh: e.tensor_tensor(out=p23[:], in0=src_sw, in1=BBh, op=ALU.mult),
                             reads=rd + bufs(c["BB"]), writes=bufs(p23))
                        S.op(eng, lambda e, p13=p13, p23=p23, xv=xv, i_=i_: e.tensor_tensor(
                            out=xv[:, :, :, :, i_], in0=p13[:], in1=p23[:], op=ALU.add), reads=bufs(p13, p23), writes=bufs(arr))
                for eng, qs in (("dve", slice(0, 8)), ("pool", slice(8, 16))):
                    B_ = bl[eng]
                    AAp, BBp, cc = B_["AAp"], B_["BBp"], B_["cc"]
                    c1, c2 = B_["c"]
                    xv = arr[:, qs, :, 0, 1:257].rearrange("p q r (b i) -> p q r b i", i=BL_)
                    S.op(eng, lambda e, cc=cc, qs=qs: e.tensor_copy(out=cc[:, :, :, 0], in_=arr[:, qs, :, 0, 0]), reads=bufs(arr), writes=bufs(cc))
                    for Bk in range(NB_):
                        S.op(eng, lambda e, c1=c1, cc=cc, AAp=AAp, Bk=Bk: e.tensor_tensor(
                            out=c1[:], in0=cc[:, :, :, Bk], in1=AAp[:, :, :, 15], op=ALU.mult), reads=bufs(cc, AAp), writes=bufs(c1))
                        S.op(eng, lambda e, c2=c2, cc=cc, BBp=BBp, Bk=Bk: e.tensor_tensor(
                            out=c2[:], in0=cc[:, :, ::-1, Bk], in1=BBp[:, :, :, 15], op=ALU.mult), reads=bufs(cc, BBp), writes=bufs(c2))
                        S.op(eng, lambda e, c1=c1, c2=c2: e.tensor_tensor(out=c1[:], in0=c1[:], in1=c2[:], op=ALU.add),
                             reads=bufs(c1, c2), writes=bufs(c1))
                        S.op(eng, lambda e, c1=c1, cc=cc, xv=xv, Bk=Bk: e.tensor_tensor(
                            out=cc[:, :, :, Bk + 1], in0=c1[:], in1=xv[:, :, :, Bk, 15], op=ALU.add), reads=bufs(c1, arr), writes=bufs(cc))
                for eng, qs in (("dve", slice(0, 8)), ("pool", slice(8, 16))):
                    B_ = bl[eng]
                    AAp, BBp, cc = B_["AAp"], B_["BBp"], B_["cc"]
                    xv = arr[:, qs, :, 0, 1:257].rearrange("p q r (b i) -> p q r b i", i=BL_)
                    hv = Hb[:, qs, :, 0, 1:257].rearrange("p q r (b i) -> p q r b i", i=BL_)
                    fr = B_["f"]
                    S.op(eng, lambda e, qs=qs: e.tensor_copy(out=Hb[:, qs, :, 0, 0], in_=arr[:, qs, :, 0, 0]), reads=bufs(arr), writes=bufs(Hb))
                    pend = []
                    for i_ in range(BL_ + 1):
                        if i_ < BL_:
                            f1, f2 = fr[i_ % 2]
                            S.op(eng, lambda e, f1=f1, cc=cc, AAp=AAp, i_=i_: e.tensor_tensor(
                                out=f1[:], in0=cc[:, :, :, 0:NB_], in1=AAp[:, :, :, i_:i_ + 1].to_broadcast([128, 8, 2, NB_]), op=ALU.mult),
                                reads=bufs(cc, AAp), writes=bufs(f1))
                            S.op(eng, lambda e, f2=f2, cc=cc, BBp=BBp, i_=i_: e.tensor_tensor(
                                out=f2[:], in0=cc[:, :, ::-1, 0:NB_], in1=BBp[:, :, :, i_:i_ + 1].to_broadcast([128, 8, 2, NB_]), op=ALU.mult),
                                reads=bufs(cc, BBp), writes=bufs(f2))
                        if i_ >= 1:
                            j_ = i_ - 1
                            f1, f2 = fr[j_ % 2]
                            S.op(eng, lambda e, f1=f1, f2=f2: e.tensor_tensor(out=f1[:], in0=f1[:], in1=f2[:], op=ALU.add),
                                 reads=bufs(f1, f2), writes=bufs(f1))
                            S.op(eng, lambda e, f1=f1, xv=xv, hv=hv, j_=j_: e.tensor_tensor(
                                out=hv[:, :, :, :, j_], in0=f1[:], in1=xv[:, :, :, :, j_], op=ALU.add), reads=bufs(f1, arr), writes=bufs(Hb))
            if not is_lat:
                for d_ in range(2):
                    S.op("act", lambda e, d_=d_, gs=gs: e.activation(
                        out=fin[:, :, d_, :, gs], in_=arr[:, d_:16:2, :, :, nch].rearrange("p g r s -> p s r g"), func=AF.Copy),
                        reads=bufs(arr), writes=bufs(fin))
            for g0 in range(0, 16, GPB):
                pb = banks.next()
                for gg in range(GPB):
                    gi = g0 + gg
                    gl, par = gi // 2, gi % 2
                    rows = slice(par * 64, (par + 1) * 64)
                    yreg = pb[:, gg * C_:(gg + 1) * C_]
                    S.op("pe", lambda e, yreg=yreg, gi=gi: e.matmul(yreg, lhsT=Tw[:, gi, :], rhs=X[:, gi, :], start=True, stop=False),
                         reads=bufs(Tw, X), writes=bufs(pb))
                    for d_ in range(2):
                        for r_ in range(2):
                            hsl = Hb[rows, gl * 2 + d_, r_, :, 0:nch]
                            if d_ == 1:
                                hsl = hsl[:, :, ::-1]
                            S.op("pe", lambda e, yreg=yreg, gl=gl, d_=d_, r_=r_, rows=rows, hsl=hsl: e.matmul(
                                yreg, lhsT=W2w[rows, gl, d_, r_, :], rhs=hsl, start=False, stop=(d_ == 1 and r_ == 1)),
                                reads=bufs(W2w, Hb), writes=bufs(pb))
                for gg in range(GPB):
                    gi = g0 + gg
                    S.op("dve", lambda e, pb=pb, gg=gg, gi=gi, b=b: e.scalar_tensor_tensor(
                        out=Ysb[:, gi, :], in0=X[:, gi, :], scalar=c["dX"][:, 16 * b + gi:16 * b + gi + 1],
                        in1=pb[:, gg * C_:(gg + 1) * C_], op0=ALU.mult, op1=ALU.add),
                        reads=bufs(pb, X, c["dX"]), writes=bufs(Ysb))
            for blk in range(2):
                for t0 in range(0, 8, GPB):
                    psel = banks.next()
                    for tt_ in range(GPB):
                        t = t0 + tt_
                        for g_ in range(8):
                            S.op("pe", lambda e, psel=psel, tt_=tt_, t=t, g_=g_, blk=blk: e.matmul(
                                psel[:, tt_ * C_:(tt_ + 1) * C_], lhsT=c["Wsel"][:, t, 112 - 16 * g_:240 - 16 * g_],
                                rhs=Ysb[:, blk * 8 + g_, :], start=(g_ == 0), stop=(g_ == 7)),
                                reads=bufs(c["Wsel"], Ysb), writes=bufs(psel))
                    S.op("act", lambda e, psel=psel, blk=blk, t0=t0: e.activation(
                        out=ygst[:, blk, t0 * C_:t0 * C_ + 512], in_=psel[:], func=AF.Gelu), reads=bufs(psel), writes=bufs(ygst))
            S.dma("sp", yscr[2 * b:2 * b + 2, :, tok0:tok0 + Tn].rearrange("b p t -> p b t"), ygst[:], reads=bufs(ygst))

        if not is_lat:
            for sq in range(4):
                pf = banks.next()
                for d_ in range(2):
                    for r_ in range(2):
                        j_ = d_ * 2 + r_
                        S.op("pe", lambda e, pf=pf, sq=sq, d_=d_, r_=r_, j_=j_: e.transpose(
                            out=pf[0:64, j_ * 128:(j_ + 1) * 128], in_=fin[:, sq, d_, r_, :], identity=identf[:]),
                            reads=bufs(fin, identf), writes=bufs(pf))
                st_ = fst.next()
                S.op("act", lambda e, pf=pf, st_=st_: e.activation(out=st_[:].rearrange("p a b -> p (a b)"), in_=pf[0:64, :], func=AF.Copy),
                     reads=bufs(pf), writes=bufs(st_))
                S.dma("sp", new_s5[sq].rearrange("d r (gp two) n -> gp (d r) (two n)", two=2), st_[:], reads=bufs(st_))

    def s5_glu(c, tok0, sub, yT):
        ygT = k.at([128, 16, 1024], BF16)
        S.dma("sp", ygT[:], yscr[:, :, tok0 + sub * 1024:tok0 + (sub + 1) * 1024].rearrange("b p t -> p b t"), writes=bufs(ygT))
        wgr = k.aring(2, [128, 16, 128], BF16)
        sgr = k.aring(2, [128, 512], F32)
        szr = k.aring(2, [128, 512], F32)
        for blk in range(16):
            wg = wgr.next()
            if ("wg", blk) not in wcache:
                scr = k.dram("wc_glu_%d" % blk, [128, 16 * 128], BF16)
                sb_ = Buf()
                wcache[("wg", blk)] = (scr, sb_)
                S.dma("pool", wg[:], s5_w_glu.rearrange("(k p) n -> p k n", p=128)[:, :, blk * 128:(blk + 1) * 128], writes=bufs(wg))
                S.dma("sp", scr.rearrange("p (k n) -> p k n", k=16), wg[:], reads=bufs(wg), writes=[sb_])
            else:
                scr, sb_ = wcache[("wg", blk)]
                S.dma("sp", wg[:], scr.rearrange("p (k n) -> p k n", k=16), reads=[sb_], writes=bufs(wg))
            if blk % 4 == 0:
                wz = load_w(s5_w_in, E + blk * 128, 512, cache="s5z")
            co = (blk % 4) * 128
            for q in range(2):
                p0 = sub * 1024 + q * 512
                pg_ = banks.next()
                for kk in range(16):
                    S.op("pe", lambda e, pg_=pg_, kk=kk, wg=wg, q=q: e.matmul(
                        pg_[:], lhsT=wg[:, kk, :], rhs=ygT[:, kk, q * 512:(q + 1) * 512], start=(kk == 0), stop=(kk == 15)),
                        reads=bufs(wg, ygT), writes=bufs(pg_))
                sg = sgr.next()
                S.op("act", lambda e, pg_=pg_, sg=sg, blk=blk: e.activation(
                    out=sg[:], in_=pg_[:], func=AF.Tanh, scale=0.5, bias=c["bgh"][:, blk:blk + 1]), reads=bufs(pg_, c["bgh"]), writes=bufs(sg))
                pz = banks.next()
                for kk in range(8):
                    S.op("pe", lambda e, pz=pz, kk=kk, wz=wz, co=co, p0=p0: e.matmul(
                        pz[:], lhsT=wz[:, kk, co:co + 128], rhs=hT[:, kk, p0:p0 + 512], start=(kk == 0), stop=(kk == 7)),
                        reads=bufs(wz, hT), writes=bufs(pz))
                sz = szr.next()
                S.op("act", lambda e, pz=pz, sz=sz: e.activation(out=sz[:], in_=pz[:], func=AF.Silu), reads=bufs(pz), writes=bufs(sz))
                S.op("dve", lambda e, sg=sg, blk=blk, q=q: e.scalar_tensor_tensor(
                    out=sg[:], in0=sg[:], scalar=1.0, in1=ygT[:, blk, q * 512:(q + 1) * 512], op0=ALU.add, op1=ALU.mult),
                    reads=bufs(sg, ygT), writes=bufs(sg))
                S.op("dve", lambda e, sg=sg, sz=sz, blk=blk, p0=p0: e.scalar_tensor_tensor(
                    out=yT[:, blk, p0:p0 + 512], in0=sg[:], scalar=0.5, in1=sz[:], op0=ALU.mult, op1=ALU.mult),
                    reads=bufs(sg, sz), writes=bufs(yT))

    def std_tiles(tok0, n):
        return [(rows_std(tok0 + i * 128), i * 128) for i in range(n)]

    units = [(0, 8, 0), (1024, 8, 1), (2048, 8, 1)]
    src = cfg.get("src", None) and inp("xsrc", [NTOK, D]) or xin
    for li in layers:
        last = final and (li == layers[-1])
        dst = xres
        k.areset()
        phase_a(li)
        if li == 1:
            L["yT"] = k.at([128, 16, 1024], BF16)
            c = gmlp_consts()
            for (tok0, nt, cond) in units:
                tiles = std_tiles(tok0, nt)
                phase_b(src, tiles, cond)
                gmlp_unit(c, nt)
                load_wout(li)
                phase_d(src, dst, tiles, cond, last)
        if li == 0:
            c = ssd_consts()
            m0 = k.amark()
            for (tok0, nt, nseq, cond) in ((0, 8, 4, 0), (1024, 16, 1, 1)):
                tiles = std_tiles(tok0, nt)
                phase_b(src, tiles, cond)
                rstd = ssd_unit(c, tok0, nt, nseq, cond == 1)
                S.op("act", lambda e, rstd=rstd, nt=nt: e.activation(out=rstd_keep[:, 0:nt], in_=rstd[:], func=AF.Copy),
                     reads=bufs(rstd), writes=bufs(rstd_keep))
                k.arestore(m0)
                load_wout(li)
                phase_d(src, dst, tiles, cond, last, scale_t=lambda i: (rstd_keep[:, i:i + 1], rstd_keep.b), ytok0=tok0)
                S.barrier()
        if li == 2:
            c = s5_prep()
            m0 = k.amark()
            for (tok0, is_lat, cond) in ((0, False, 0), (1024, True, 1)):
                nsub = 2 if is_lat else 1
                tiles = []
                for sub in range(nsub):
                    tiles += s5_tiles(tok0, sub, is_lat)
                phase_b(src, tiles, cond)
                s5_unit(c, tok0, is_lat)
                k.arestore(m0)
                L["yT"] = k.at([128, 16, 1024 * nsub], BF16)
                m1 = k.amark()
                for sub in range(nsub):
                    s5_glu(c, tok0, sub, L["yT"])
                    k.arestore(m1)
                load_wout(li)
                phase_d(src, dst, tiles, cond, last)
                k.arestore(m0)
        if li == 3:
            L["yT"] = k.at([128, 16, 2048], BF16)
            tiles = std_tiles(0, 8)
            phase_b(src, tiles, 0)
            m_ = k.amark()
            if not cfg.get("skip_ctx"):
                nat_ctx_unit()
            k.arestore(m_)
            load_wout(li)
            phase_d(src, dst, tiles, 0, last)
            S.barrier()
            tiles = std_tiles(1024, 16)
            phase_b(src, tiles, 1)
            m_ = k.amark()
            nat_lat_unit()
            if not cfg.get("skip_d"):
                k.arestore(m_)
            load_wout(li)
            phase_d(src, dst, tiles, 1, last)
        src = xres
    if not final and not cfg.get("skip_d"):
        S.barrier()
        xring = k.aring(3, [128, D], F32)
        for i in range(NTOK // 128):
            xt = xring.next()
            S.dma("sp", xt[:], xres[i * 128:(i + 1) * 128, :], writes=bufs(xt))
            S.dma("sp", y_out[i * 128:(i + 1) * 128, :], xt[:], reads=bufs(xt))
    S.emit(es)
    return nc, es


def host_inputs(inputs, core):
    f = np.ascontiguousarray
    m = {}
    m["xin"] = f(np.concatenate([inputs["x_prompt"][4 * core:4 * core + 4].reshape(NP_TOK, D),
                                 inputs["x_sample"][core % 2]], axis=0))
    m["cvec"] = f(np.stack([inputs["c_ctx"], inputs["c"][core % 2]], axis=0))
    for nm in ["norm_g", "w_mod", "b_mod", "w_out", "final_g"]:
        m[nm] = f(inputs[nm])
    m["mlp_w_in"] = f(inputs["mlp_w_in"][0])
    m["mlp_ln_g"] = f(inputs["mlp_ln_g"][0])
    m["mlp_ln_b"] = f(inputs["mlp_ln_b"][0])
    m["mlp_w_sT"] = f(np.transpose(inputs["mlp_w_s"][0], (0, 2, 1)))
    m["mlp_b_s"] = f(inputs["mlp_b_s"][0])
    m["ssd_w_in"] = f(inputs["ssd_w_in"][0])
    m["ssd_conv_w"] = f(inputs["ssd_conv_w"][0])
    m["ssd_conv_b"] = f(inputs["ssd_conv_b"][0])
    m["ssd_dt_bias"] = f(inputs["ssd_dt_bias"][0].reshape(64))
    m["ssd_a_log"] = f(inputs["ssd_a_log"][0].reshape(64))
    m["ssd_d"] = f(inputs["ssd_d"][0])
    m["ssd_norm_g"] = f(inputs["ssd_norm_g"][0])
    m["state_ssd"] = f(inputs["state_ssd"][core % 2, 0])
    m["s5_w_in"] = f(inputs["s5_w_in"][0])

    def pl(a):
        sh = a.shape[:-2]
        a = a.reshape(sh + (64, 2, 64))
        return np.moveaxis(a, -3, -1).reshape(sh + (128, 64))
    m["s5_lam"] = f(np.stack([pl(inputs["s5_lam_re"][0]), pl(inputs["s5_lam_im"][0])], 0))
    m["s5_lstep"] = f(pl(np.broadcast_to(inputs["s5_log_step"][0][:, :, None], (2, 128, 64))))

    def plj(a):
        a = a.reshape(2, 64, 2, 64, 16)
        return np.transpose(a, (0, 2, 3, 1, 4)).reshape(2, 128, 64, 16)
    m["s5_B"] = f(np.stack([plj(inputs["s5_b_re"][0]), plj(inputs["s5_b_im"][0])], 0))
    m["s5_C"] = f(np.stack([plj(np.transpose(inputs["s5_c_re"][0], (0, 1, 3, 2))),
                            plj(np.transpose(inputs["s5_c_im"][0], (0, 1, 3, 2)))], 0))
    m["s5_h0"] = f(pl(inputs["state_s5"][core % 2, 0]))
    m["s5_d"] = f(inputs["s5_d"][0])
    m["s5_w_glu"] = f(inputs["s5_w_glu"][0])
    m["s5_b_glu"] = f(inputs["s5_b_glu"][0])
    m["nat_w_in"] = f(inputs["nat_w_in"][0])
    m["rpbg"] = rpb_gather(inputs["nat_rpb"][0])
    m["natmask"] = nat_masks()
    m["cache_k"] = f(inputs["cache_k"][core % 2, 0])
    m["cache_v"] = f(inputs["cache_v"][core % 2, 0])
    return m


def rpb_gather(rpb):
    qc = np.arange(64)[:, None]
    kc = np.arange(64)[None, :]
    ci = np.clip(kc - qc + 15, 0, 30)
    out = np.zeros((32, 128, 16, 64), np.float32)
    g = rpb[:, :, ci]
    g = np.transpose(g, (0, 2, 1, 3))
    out[:, 0:64, 0:15, :] = g
    out[:, 64:128, 1:16, :] = g
    return np.ascontiguousarray(out.reshape(32, 128, 1024))


def nat_masks():
    NEG = -30000.0 * 8.0
    qc = np.arange(64)
    cs = np.clip(qc - 8, 0, 48)
    kc = np.arange(64)
    col_ok = (kc[None, :] >= cs[:, None]) & (kc[None, :] < cs[:, None] + 16)
    m = np.zeros((3, 128, 9, 64), np.float32)
    colm = np.where(col_ok, 0.0, NEG).astype(np.float32)
    m[:, 0:64] += colm[None, :, None, :]
    m[:, 64:128] += colm[None, :, None, :]
    m[0, 0:64, 8, :] = NEG
    m[0, 64:128, 0, :] = NEG
    m[1, :, 8, :] = NEG
    return np.ascontiguousarray(m.reshape(3, 128, 576))


def kernel(**inputs):
    inputs = {k_: np.asarray(v) for k_, v in inputs.items()}
    nc, es = build({})
    with es:
        in_maps = [host_inputs(inputs, c) for c in range(8)]
        res = run_bass_kernel_spmd(nc, in_maps, core_ids=list(range(8)))
    r = res.results
    y_prompt = np.concatenate([r[c]["y_out"][:NP_TOK].reshape(4, 256, D) for c in range(8)], axis=0)
    y_sample = np.stack([r[c]["y_out"][NP_TOK:] for c in range(2)], axis=0)
    new_ssd = np.concatenate([r[c]["new_ssd"] for c in range(8)], axis=0)[:, None]
    new_s5 = np.concatenate([r[c]["new_s5"] for c in range(8)], axis=0)[:, None]
    new_k = np.concatenate([r[c]["new_k"] for c in range(8)], axis=0)[:, None]
    new_v = np.concatenate([r[c]["new_v"] for c in range(8)], axis=0)[:, None]
    return (y_prompt.astype(np.float32), y_sample.astype(np.float32), np.ascontiguousarray(new_ssd, dtype=np.float32),
            np.ascontiguousarray(new_s5, dtype=np.float32), np.ascontiguousarray(new_k, dtype=np.float32),
            np.ascontiguousarray(new_v, dtype=np.float32))
```
